# Optimizing a Trainium2 kernel written in Bass

```python
import math
import jax, jax.numpy as jnp
from jax import lax
import numpy as np

D_MODEL = 1024
BATCH = 8
SEQ = 2048
DEPTH = 1
DEC_BATCH = 128
DEC_SEQ = 4
PAST_LEN = 16384
PAGE_SIZE = 128

LRU_WIDTH = D_MODEL
N_LRU_HEADS = 16
LRU_BLOCK = LRU_WIDTH // N_LRU_HEADS
LRU_C = 8.0
SSD_WIDTH = D_MODEL
SSD_HEAD_DIM = 64
N_SSD_HEADS = SSD_WIDTH // SSD_HEAD_DIM
N_SSD_GROUPS = 2
HEADS_PER_GROUP = N_SSD_HEADS // N_SSD_GROUPS
D_STATE = 128
SSD_CHUNK = 128
CONV_WIDTH = 4
SSD_CONV_DIM = SSD_WIDTH + 2 * N_SSD_GROUPS * D_STATE
MIX_WIDTH = LRU_WIDTH + SSD_WIDTH
IN_PROJ_DIM = 2 * LRU_WIDTH + SSD_WIDTH + SSD_CONV_DIM + N_SSD_HEADS
D_FF = 4 * D_MODEL
EPS = 1e-6

kernel_name = "hymba_rglru_ssd_hybrid_step"


def rmsnorm(x, g):
    xf = x.astype(jnp.float32)
    y = xf * lax.rsqrt(jnp.mean(xf * xf, axis=-1, keepdims=True) + EPS)
    return (y * g.astype(jnp.float32)).astype(x.dtype)


def causal_conv(x, buf, w, b):
    t = x.shape[1]
    xp = jnp.concatenate([buf.astype(x.dtype), x], axis=1)
    y = b + sum(xp[:, k:k + t] * w[k] for k in range(CONV_WIDTH))
    return y, xp[:, -(CONV_WIDTH - 1):]


def linear_scan(a, b, h0):
    b = b.at[:, 0].add(a[:, 0] * h0)
    def combine(left, right):
        a1, b1 = left
        a2, b2 = right
        return a1 * a2, a2 * b1 + b2
    _, h = lax.associative_scan(combine, (a, b), axis=1)
    return h


def rg_lru(x, h0, w_a, b_a, w_x, b_x, lam):
    bsz, t, _ = x.shape
    xb = x.reshape(bsz, t, N_LRU_HEADS, LRU_BLOCK)
    r = jax.nn.sigmoid(jnp.einsum('bthi,hij->bthj', xb, w_a) + b_a).reshape(bsz, t, LRU_WIDTH)
    i = jax.nn.sigmoid(jnp.einsum('bthi,hij->bthj', xb, w_x) + b_x).reshape(bsz, t, LRU_WIDTH)
    log_a = -LRU_C * r.astype(jnp.float32) * jax.nn.softplus(-lam.astype(jnp.float32))
    a = jnp.exp(log_a)
    mult = jnp.sqrt(-jnp.expm1(2.0 * log_a))
    h = linear_scan(a, mult * (i * x).astype(jnp.float32), h0.astype(jnp.float32))
    return h, h[:, -1]


def ssd_scan(x, dt, a, bm, cm, h0):
    bsz, t = x.shape[0], x.shape[1]
    q = min(SSD_CHUNK, t)
    pad = (-t) % q
    if pad:
        pw = lambda z: jnp.pad(z, [(0, 0), (0, pad)] + [(0, 0)] * (z.ndim - 2))
        x, dt, bm, cm = pw(x), pw(dt), pw(bm), pw(cm)
    tp = t + pad
    nc = tp // q
    g, e = N_SSD_GROUPS, HEADS_PER_GROUP
    xdt = (x * dt[..., None]).reshape(bsz, nc, q, g, e, SSD_HEAD_DIM)
    da = (dt * a).reshape(bsz, nc, q, g, e)
    bc = bm.reshape(bsz, nc, q, g, D_STATE)
    cc = cm.reshape(bsz, nc, q, g, D_STATE)
    cum = jnp.cumsum(da, axis=2)
    causal = jnp.tril(jnp.ones((q, q), dtype=bool))[None, None, :, :, None, None]
    diff = cum[:, :, :, None] - cum[:, :, None, :]
    lmat = jnp.where(causal, jnp.exp(jnp.where(causal, diff, 0.0)), 0.0)
    cb = jnp.einsum('bctgn,bcsgn->bctsg', cc, bc)
    y_diag = jnp.einsum('bctsg,bctsge,bcsgep->bctgep', cb, lmat, xdt)
    decay_states = jnp.exp(cum[:, :, -1:] - cum)
    states = jnp.einsum('bclgn,bclge,bclgep->bcgepn', bc, decay_states, xdt)
    chunk_decay = jnp.exp(cum[:, :, -1])
    h0g = h0.reshape(bsz, g, e, SSD_HEAD_DIM, D_STATE)

    def step(h, inp):
        dec, st = inp
        return dec[..., None, None] * h + st, h

    h_last, h_in = lax.scan(step, h0g, (jnp.moveaxis(chunk_decay, 1, 0), jnp.moveaxis(states, 1, 0)))
    h_in = jnp.moveaxis(h_in, 0, 1)
    y_off = jnp.einsum('bclgn,bcgepn,bclge->bclgep', cc, h_in, jnp.exp(cum))
    y = (y_diag + y_off).reshape(bsz, tp, N_SSD_HEADS, SSD_HEAD_DIM)[:, :t]
    return y, h_last.reshape(bsz, N_SSD_HEADS, SSD_HEAD_DIM, D_STATE)


def hybrid_layer(x, lru_conv, lru_h, ssd_conv, ssd_h,
                 g_mix, w_in, lru_conv_w, lru_conv_b, w_a, b_a, w_x, b_x, lam, g_lru_out,
                 ssd_conv_w, ssd_conv_b, dt_bias, a_log, d_skip, g_ssd_out, w_out,
                 g_mlp, w_up, w_down):
    bsz, t, _ = x.shape
    h = rmsnorm(x, g_mix)
    proj = h @ w_in
    o1 = LRU_WIDTH
    o2 = o1 + LRU_WIDTH
    o3 = o2 + SSD_WIDTH
    o4 = o3 + SSD_CONV_DIM
    lru_x, lru_gate, ssd_z, ssd_xbc, ssd_dt = jnp.split(proj, [o1, o2, o3, o4], axis=-1)

    u, new_lru_conv = causal_conv(lru_x, lru_conv, lru_conv_w, lru_conv_b)
    hseq, new_lru_h = rg_lru(u, lru_h, w_a, b_a, w_x, b_x, lam)
    y_lru = rmsnorm(hseq.astype(x.dtype) * jax.nn.gelu(lru_gate), g_lru_out)

    xbc, new_ssd_conv = causal_conv(ssd_xbc, ssd_conv, ssd_conv_w, ssd_conv_b)
    xbc = jax.nn.silu(xbc)
    xs, bm, cm = jnp.split(xbc, [SSD_WIDTH, SSD_WIDTH + N_SSD_GROUPS * D_STATE], axis=-1)
    xs = xs.reshape(bsz, t, N_SSD_HEADS, SSD_HEAD_DIM).astype(jnp.float32)
    bm = bm.reshape(bsz, t, N_SSD_GROUPS, D_STATE).astype(jnp.float32)
    cm = cm.reshape(bsz, t, N_SSD_GROUPS, D_STATE).astype(jnp.float32)
    dt = jax.nn.softplus(ssd_dt.astype(jnp.float32) + dt_bias.astype(jnp.float32))
    a = -jnp.exp(a_log.astype(jnp.float32))
    ys, new_ssd_h = ssd_scan(xs, dt, a, bm, cm, ssd_h.astype(jnp.float32))
    ys = ys + d_skip.astype(jnp.float32)[:, None] * xs
    ys = ys.reshape(bsz, t, SSD_WIDTH).astype(x.dtype)
    gated = (ys * jax.nn.silu(ssd_z)).reshape(bsz, t, N_SSD_GROUPS, SSD_WIDTH // N_SSD_GROUPS)
    y_ssd = rmsnorm(gated, g_ssd_out.reshape(N_SSD_GROUPS, -1)).reshape(bsz, t, SSD_WIDTH)

    x = x + jnp.concatenate([y_lru, y_ssd], axis=-1) @ w_out
    m = rmsnorm(x, g_mlp)
    x = x + jnp.square(jax.nn.relu(m @ w_up)) @ w_down
    dt_out = x.dtype
    return (x, new_lru_conv.astype(dt_out), new_lru_h.astype(dt_out),
            new_ssd_conv.astype(dt_out), new_ssd_h.astype(dt_out))


def setup_inputs(seed: int = 0) -> dict:
    key = jax.random.key(seed)
    ks = jax.random.split(key, 32)
    f32 = jnp.float32
    nrm = lambda k, shape, s: jax.random.normal(k, shape, f32) * s
    gain = lambda k, shape: 1.0 + 0.01 * jax.random.normal(k, shape, f32)
    a0 = jax.random.uniform(ks[10], (DEPTH, LRU_WIDTH), f32, 0.9, 0.999)
    lam = jnp.log(a0) - jnp.log1p(-a0)
    dt0 = jnp.exp(jax.random.uniform(ks[13], (DEPTH, N_SSD_HEADS), f32, math.log(1e-3), math.log(1e-1)))
    dt_bias = dt0 + jnp.log(-jnp.expm1(-dt0))
    a_log = jnp.log(jax.random.uniform(ks[14], (DEPTH, N_SSD_HEADS), f32, 1.0, 16.0))
    return {
        "x_prompt": nrm(ks[0], (BATCH, SEQ, D_MODEL), 1.0),
        "x_sample": nrm(ks[1], (DEC_BATCH, DEC_SEQ, D_MODEL), 1.0),
        "state_lru_conv": nrm(ks[2], (DEPTH, DEC_BATCH, CONV_WIDTH - 1, LRU_WIDTH), 1.0),
        "state_lru_h": nrm(ks[3], (DEPTH, DEC_BATCH, LRU_WIDTH), 0.5),
        "state_ssd_conv": nrm(ks[4], (DEPTH, DEC_BATCH, CONV_WIDTH - 1, SSD_CONV_DIM), 1.0),
        "state_ssd_h": nrm(ks[5], (DEPTH, DEC_BATCH, N_SSD_HEADS, SSD_HEAD_DIM, D_STATE), 0.1),
        "g_mix": gain(ks[6], (DEPTH, D_MODEL)),
        "w_in": nrm(ks[7], (DEPTH, D_MODEL, IN_PROJ_DIM), D_MODEL ** -0.5),
        "lru_conv_w": nrm(ks[8], (DEPTH, CONV_WIDTH, LRU_WIDTH), CONV_WIDTH ** -0.5),
        "lru_conv_b": nrm(ks[9], (DEPTH, LRU_WIDTH), 0.01),
        "w_a": nrm(ks[11], (DEPTH, N_LRU_HEADS, LRU_BLOCK, LRU_BLOCK), LRU_BLOCK ** -0.5),
        "b_a": nrm(ks[12], (DEPTH, N_LRU_HEADS, LRU_BLOCK), 0.01),
        "w_x": nrm(ks[15], (DEPTH, N_LRU_HEADS, LRU_BLOCK, LRU_BLOCK), LRU_BLOCK ** -0.5),
        "b_x": nrm(ks[16], (DEPTH, N_LRU_HEADS, LRU_BLOCK), 0.01),
        "lam": lam,
        "g_lru_out": gain(ks[17], (DEPTH, LRU_WIDTH)),
        "ssd_conv_w": nrm(ks[18], (DEPTH, CONV_WIDTH, SSD_CONV_DIM), CONV_WIDTH ** -0.5),
        "ssd_conv_b": nrm(ks[19], (DEPTH, SSD_CONV_DIM), 0.01),
        "dt_bias": dt_bias,
        "a_log": a_log,
        "d_skip": gain(ks[20], (DEPTH, N_SSD_HEADS)),
        "g_ssd_out": gain(ks[21], (DEPTH, SSD_WIDTH)),
        "w_out": nrm(ks[22], (DEPTH, MIX_WIDTH, D_MODEL), MIX_WIDTH ** -0.5),
        "g_mlp": gain(ks[23], (DEPTH, D_MODEL)),
        "w_up": nrm(ks[24], (DEPTH, D_MODEL, D_FF), D_MODEL ** -0.5),
        "w_down": nrm(ks[25], (DEPTH, D_FF, D_MODEL), D_FF ** -0.5),
        "g_final": gain(ks[26], (D_MODEL,)),
    }


def reference(x_prompt, x_sample, state_lru_conv, state_lru_h, state_ssd_conv, state_ssd_h,
              g_mix, w_in, lru_conv_w, lru_conv_b, w_a, b_a, w_x, b_x, lam, g_lru_out,
              ssd_conv_w, ssd_conv_b, dt_bias, a_log, d_skip, g_ssd_out, w_out,
              g_mlp, w_up, w_down, g_final):
    layer_params = (g_mix, w_in, lru_conv_w, lru_conv_b, w_a, b_a, w_x, b_x, lam, g_lru_out,
                    ssd_conv_w, ssd_conv_b, dt_bias, a_log, d_skip, g_ssd_out, w_out,
                    g_mlp, w_up, w_down)
    bp = x_prompt.shape[0]
    xp, xs = x_prompt, x_sample
    p_lc, p_lh, p_sc, p_sh = [], [], [], []
    s_lc, s_lh, s_sc, s_sh = [], [], [], []
    for l in range(DEPTH):
        lp = [p[l] for p in layer_params]
        xp, a1, a2, a3, a4 = hybrid_layer(
            xp,
            jnp.zeros((bp, CONV_WIDTH - 1, LRU_WIDTH), xp.dtype),
            jnp.zeros((bp, LRU_WIDTH), xp.dtype),
            jnp.zeros((bp, CONV_WIDTH - 1, SSD_CONV_DIM), xp.dtype),
            jnp.zeros((bp, N_SSD_HEADS, SSD_HEAD_DIM, D_STATE), xp.dtype),
            *lp)
        p_lc.append(a1); p_lh.append(a2); p_sc.append(a3); p_sh.append(a4)
        xs, b1, b2, b3, b4 = hybrid_layer(
            xs, state_lru_conv[l], state_lru_h[l], state_ssd_conv[l], state_ssd_h[l], *lp)
        s_lc.append(b1); s_lh.append(b2); s_sc.append(b3); s_sh.append(b4)
    y_prompt = rmsnorm(xp, g_final)
    y_sample = rmsnorm(xs, g_final)
    return (y_prompt, y_sample,
            jnp.stack(p_lc), jnp.stack(p_lh), jnp.stack(p_sc), jnp.stack(p_sh),
            jnp.stack(s_lc), jnp.stack(s_lh), jnp.stack(s_sc), jnp.stack(s_sh))
```

```python
import math
import os as _os
from contextlib import ExitStack

import numpy as np
import concourse.bass as bass
import concourse.mybir as mybir
from concourse.bass_utils import run_bass_kernel_spmd

F32 = mybir.dt.float32
BF16 = mybir.dt.bfloat16
AF = mybir.ActivationFunctionType
ALU = mybir.AluOpType

NCORES = 8
D = 1024
SEQ = 2048
NS = 16
TS = 64
XBC = 1536
INP = 4624
DFF = 4096
EPS = 1e-6
NEG = -30000.0

ENGS = ("pe", "act", "dve", "pool", "sp")
SAME_ENGINE_SYNC = {"pe": False, "act": True, "dve": True, "pool": True, "sp": False}


class Op:
    __slots__ = ("eng", "fn", "deps", "marked", "count", "dma_key", "dma_val")

    def __init__(self, eng, fn, dma_key=None):
        self.eng = eng
        self.fn = fn
        self.deps = ()
        self.marked = False
        self.count = 0
        self.dma_key = dma_key
        self.dma_val = 0


class Sched:
    def __init__(self, nc):
        self.nc = nc
        self.ops = {e: [] for e in ENGS}
        self.last_w = {}
        self.readers = {}
        self.dma_cnt = {}
        self.pending = {}
        self.since_bar = []

    ALIAS = {"pC": "b4", "pCx": "b4", "pD": "b5", "pD2": "b5", "pD3": "b5", "pDn": "b5",
             "pDcb0": "b5", "pDcb1": "b5", "pT": "b01", "pO": "b67", "pA0": "b2", "pA1": "b3"}

    PSUM_KEYS = {"b01", "b2", "b3", "b4", "b5", "b67"}

    def add(self, eng, fn, reads=(), writes=(), dma_key=None):
        reads = [self.ALIAS.get(k, k) for k in reads]
        writes = [self.ALIAS.get(k, k) for k in writes]
        op = Op(eng, fn, dma_key)
        deps = []
        seen = set()

        def dep(o):
            if o is not None and o is not op and id(o) not in seen:
                seen.add(id(o))
                deps.append(o)

        if self.pending.get(eng):
            for o in self.pending[eng]:
                dep(o)
            self.pending[eng] = []
        for b in reads:
            dep(self.last_w.get(b))
            if b in self.PSUM_KEYS:
                for r in self.readers.get(b, ()):
                    if r.eng != eng:
                        dep(r)
        for b in writes:
            dep(self.last_w.get(b))
            for r in self.readers.get(b, ()):
                dep(r)
        for b in reads:
            self.readers.setdefault(b, []).append(op)
        for b in writes:
            self.last_w[b] = op
            self.readers[b] = []
        op.deps = deps
        if dma_key is not None:
            self.dma_cnt[dma_key] = self.dma_cnt.get(dma_key, 0) + 16
            op.dma_val = self.dma_cnt[dma_key]
        self.ops[eng].append(op)
        self.since_bar.append(op)
        return op

    def barrier(self):
        ops = []
        for e in ENGS:
            comp = [o for o in self.ops[e] if o.dma_key is None]
            if comp:
                ops.append(comp[-1])
        last_dma = {}
        for o in self.since_bar:
            if o.dma_key is not None:
                last_dma[o.dma_key] = o
        ops.extend(last_dma.values())
        for e in ENGS:
            self.pending.setdefault(e, []).extend(ops)
        self.since_bar = []

    def pe(self, fn, r=(), w=()):
        return self.add("pe", fn, r, w)

    def act(self, fn, r=(), w=()):
        return self.add("act", fn, r, w)

    def dve(self, fn, r=(), w=()):
        return self.add("dve", fn, r, w)

    def pool(self, fn, r=(), w=()):
        return self.add("pool", fn, r, w)

    def dma(self, fn, key, r=(), w=(), q="sp"):
        return self.add(q, fn, r, w, dma_key=key)

    def emit(self):
        nc = self.nc
        for e in ENGS:
            for op in self.ops[e]:
                for d in op.deps:
                    if d.dma_key is None:
                        if d.eng == op.eng and not SAME_ENGINE_SYNC[d.eng]:
                            continue
                        d.marked = True
        for e in ENGS:
            c = 0
            for op in self.ops[e]:
                if op.dma_key is None and op.marked:
                    c += 1
                    op.count = c
        with ExitStack() as st:
            esem = {e: st.enter_context(nc.semaphore("es_" + e)) for e in ENGS}
            dsem = {}
            for k in self.dma_cnt:
                dsem[k] = st.enter_context(nc.semaphore("ds_%d" % len(dsem)))
            block = st.enter_context(nc.Block())

            def run(ename, eng):
                seen = {}
                for op in self.ops[ename]:
                    need = {}
                    for d in op.deps:
                        if d.dma_key is not None:
                            key = ("d", d.dma_key)
                            val = d.dma_val
                            sem = dsem[d.dma_key]
                        else:
                            if d.eng == ename and not SAME_ENGINE_SYNC[ename]:
                                continue
                            key = ("e", d.eng)
                            val = d.count
                            sem = esem[d.eng]
                        if key not in need or need[key][1] < val:
                            need[key] = (sem, val)
                    for key, (sem, val) in need.items():
                        if seen.get(key, 0) >= val:
                            continue
                        seen[key] = val
                        eng.wait_ge(sem, val)
                    ins = op.fn(eng)
                    if op.dma_key is not None:
                        ins.then_inc(dsem[op.dma_key], 16)
                    elif op.marked:
                        ins.then_inc(esem[ename], 1)
                if ename == "sp":
                    for k, v in self.dma_cnt.items():
                        eng.wait_ge(dsem[k], v)

            @block.sync
            def _(e):
                run("sp", e)

            @block.tensor
            def _(e):
                run("pe", e)

            @block.scalar
            def _(e):
                run("act", e)

            @block.vector
            def _(e):
                run("dve", e)

            @block.gpsimd
            def _(e):
                run("pool", e)


PC = {}
_o = 0
for _n, _w in (("LW", 32), ("LB", 8), ("BA", 8), ("BX", 8), ("LAM", 8), ("GL", 8), ("SW", 48),
               ("SB", 12), ("DS", 8), ("GS", 8), ("GM", 8), ("GP", 8)):
    PC[_n] = _o
    _o += _w
NPAR = _o
CI, CU, CN, CO, CUB, CNB, CBM, CBI = 0, 128, 256, 384, 512, 576, 640, 704
NCST = 720


def build_program(NT=SEQ // 128, SAMP=True, MLP=True, DBG=False, STAGE=9):
    nc = bass.Bass("TRN2", target_bir_lowering=False)
    S = Sched(nc)

    def din(name, shape):
        return nc.dram_tensor(name, list(shape), F32, kind="ExternalInput").ap()

    def dout(name, shape):
        return nc.dram_tensor(name, list(shape), F32, kind="ExternalOutput").ap()

    xp = din("xp", (SEQ, D))
    xs = din("xs", (TS, D))
    st_lc = din("st_lc", (NS * 3, D))
    st_lh = din("st_lh", (NS, D))
    st_sc = din("st_sc", (NS * 3, XBC))
    st_sh = din("st_sh", (NS, 1024, 128))
    w_in = din("w_in", (D, INP))
    w_out = din("w_out", (2 * D, D))
    w_up = din("w_up", (D, DFF))
    w_down = din("w_down", (DFF, D))
    w_a = din("w_a", (16, 64, 64))
    w_x = din("w_x", (16, 64, 64))
    pfm_d = din("pfm", (128, NPAR))
    cst_d = din("cst", (128, NCST))
    dtb_d = din("dt_bias", (16,))
    alog_d = din("a_log", (16,))
    gfin_d = din("g_final", (D,))

    y_p = dout("y_p", (SEQ, D))
    y_s = dout("y_s", (TS, D))
    o_plc = dout("o_plc", (3, D))
    o_plh = dout("o_plh", (8, 128))
    o_psc = dout("o_psc", (3, XBC))
    o_psh = dout("o_psh", (1024, 128))
    o_slc = dout("o_slc", (NS, 3, D))
    o_slh = dout("o_slh", (NS, D))
    o_ssc = dout("o_ssc", (NS, 3, XBC))
    o_ssh = dout("o_ssh", (NS, 1024, 128))
    scr = nc.dram_tensor("scr", [SEQ + TS, D], F32, kind=("ExternalOutput" if DBG else "Internal")).ap()

    st = ExitStack()
    with st:
        RW = 53200
        R = st.enter_context(nc.sbuf_tensor("R", [128, RW], F32))
        PS = st.enter_context(nc.psum_tensor("PS", [128, 4096], F32))
        ptr = [0]

        def alloc(nwords):
            a = ptr[0]
            ptr[0] += (nwords + 7) // 8 * 8
            pass
            return a

        def f32(n):
            a = alloc(n)
            return R[:, a:a + n]

        def bf(n):
            w = (n + 1) // 2
            a = alloc(w)
            return R[:, a:a + w].bitcast(BF16)[:, 0:n]

        def f3(c, t):
            return f32(c * t).rearrange("p (c t) -> p c t", c=c)

        def b3(c, t):
            return bf(c * t).rearrange("p (c t) -> p c t", c=c)

        def bank(b, n=512):
            return PS[:, 512 * b:512 * b + n]

        pT = PS[:, 0:1024]
        pTb = pT.bitcast(BF16)
        pA = [bank(2), bank(3)]
        pC = bank(4)
        pCb = pC.bitcast(BF16)
        pD = bank(5)
        pO = PS[:, 3072:4096]

        cst = f32(NCST)
        pfm = f32(NPAR)
        dtb_bc = f32(16)
        a_bc = f32(16)
        identb = bf(128)
        onesb = bf(128)
        Utrib = bf(128)
        mskb = bf(3 * TS)
        dah = bf(16)
        dal = bf(16)
        cfac = f32(8)
        c2fac = f32(8)
        tiny = f32(8)
        mhalf = f32(4)
        eps_t = f32(4)
        nbias = f32(16)
        wa_blk = b3(8, 128)
        wx_blk = b3(8, 128)
        hstate = f32(8)
        hT = f32(1024)
        hTb = bf(1024)

        ident = cst[:, CI:CI + 128]
        Utri = cst[:, CU:CU + 128]
        negm = cst[:, CN:CN + 128]
        onesf = cst[:, CO:CO + 128]
        Ublk = cst[0:TS, CUB:CUB + TS]
        negblk = cst[0:TS, CNB:CNB + TS]
        blkm = cst[0:TS, CBM:CBM + TS]
        blki = cst[0:TS, CBI:CBI + NS]

        S.dma(lambda e: e.dma_start(out=cst, in_=cst_d), "cst", w=["cst"])
        S.dma(lambda e: e.dma_start(out=pfm, in_=pfm_d), "pfm", w=["pfm"])
        S.dma(lambda e: e.dma_start(out=dtb_bc, in_=dtb_d.partition_broadcast(128)), "dtb", w=["dtb"])
        S.dma(lambda e: e.dma_start(out=a_bc, in_=alog_d.partition_broadcast(128)), "alog", w=["a_bc"])
        S.dve(lambda e: e.tensor_copy(out=identb, in_=ident), r=["cst"], w=["identb"])
        S.dve(lambda e: e.tensor_copy(out=onesb, in_=onesf), r=["cst"], w=["onesb"])
        S.dve(lambda e: e.tensor_copy(out=Utrib, in_=Utri), r=["cst"], w=["mskb"])
        S.dve(lambda e: e.tensor_copy(out=mskb[0:TS, 0:TS], in_=Ublk), r=["cst"], w=["mskb"])
        S.dve(lambda e: e.tensor_copy(out=mskb[0:TS, TS:2 * TS], in_=blkm), r=["cst"], w=["mskb"])
        S.pool(lambda e: e.memset(mhalf, -0.5), w=["mhalf"])
        S.pool(lambda e: e.memset(eps_t, EPS), w=["eps_t"])
        S.dve(lambda e: e.tensor_scalar(out=nbias[:, 0:8], in0=pfm[:, PC["BA"]:PC["BA"] + 8], scalar1=-1.0, scalar2=None, op0=ALU.mult), r=["pfm"], w=["nbias"])
        S.dve(lambda e: e.tensor_scalar(out=nbias[:, 8:16], in0=pfm[:, PC["BX"]:PC["BX"] + 8], scalar1=-1.0, scalar2=None, op0=ALU.mult), r=["pfm", "nbias"], w=["nbias"])
        S.pool(lambda e: e.memset(hstate, 0.0), w=["hstate"])
        S.pool(lambda e: e.memset(hT, 0.0), w=["hT"])
        S.pool(lambda e: e.memset(hTb, 0.0), w=["hTb"])
        S.pool(lambda e: e.memset(wa_blk, 0.0), w=["wa"])
        S.pool(lambda e: e.memset(wx_blk, 0.0), w=["wx"])
        S.act(lambda e: e.activation(out=a_bc, in_=a_bc, func=AF.Exp), r=["a_bc"], w=["a_bc"])
        S.dve(lambda e: e.tensor_scalar(out=a_bc, in0=a_bc, scalar1=-1.0, scalar2=None, op0=ALU.mult), r=["a_bc"], w=["a_bc"])
        lam = pfm[:, PC["LAM"]:PC["LAM"] + 8]
        S.act(lambda e: e.activation(out=tiny, in_=lam, func=AF.Exp, scale=-1.0), r=["pfm"], w=["tiny"])
        S.act(lambda e: e.activation(out=tiny, in_=tiny, func=AF.Ln, bias=1.0), r=["tiny"], w=["tiny"])
        S.dve(lambda e: e.tensor_scalar(out=cfac, in0=tiny, scalar1=-8.0, scalar2=None, op0=ALU.mult), r=["tiny"], w=["cfac"])
        S.dve(lambda e: e.tensor_scalar(out=c2fac, in0=tiny, scalar1=-16.0, scalar2=None, op0=ALU.mult), r=["tiny"], w=["cfac2"])
        for (wd, blk, nm) in ((w_a, wa_blk, "wa"), (w_x, wx_blk, "wx")):
            v = wd.rearrange("(c h) i j -> h i c j", h=2)
            for h2 in range(2):
                S.dma(lambda e, v=v, blk=blk, h2=h2: e.dma_start(
                    out=blk[64 * h2:64 * h2 + 64, :, 64 * h2:64 * h2 + 64], in_=v[h2]),
                    nm + str(h2), w=[nm], q="pool")

        base0 = ptr[0]

        def load_w(dst3, src2, nk, ncol, name, step=2048):
            sv = src2.rearrange("(k p) n -> p k n", p=128)
            pieces = [(k, c0, min(ncol, c0 + step)) for k in range(nk) for c0 in range(0, ncol, step)]
            for i, (k, c0, c1) in enumerate(pieces):
                S.dma(lambda e, k=k, c0=c0, c1=c1: e.dma_start(out=dst3[:, k, c0:c1], in_=sv[:, k, c0:c1]),
                      name, w=([name] if i == len(pieces) - 1 else []), q="pool")
            return name

        w_in_sb = b3(8, INP)
        w_out_sb = b3(16, D)
        if STAGE >= 1:
            k_win = load_w(w_in_sb, w_in, 8, INP, "w_in")
            k_wout = load_w(w_out_sb, w_out, 16, D, "w_out")

        xt = f32(D)
        xn = f32(D)
        junk = xn
        ss = f32(4)
        rstd = f32(4)
        hTt = b3(8, 128)
        lxb = f3(8, 131)
        xcb = f3(12, 131)
        lxs = f32(8 * NS * 7).rearrange("p (c s l) -> p c s l", c=8, s=NS)
        xcs = f32(12 * NS * 7).rearrange("p (c s l) -> p c s l", c=12, s=NS)
        gl = f3(2, 128)
        zs = f3(8, 128)
        u = f3(8, 128)
        ub = b3(8, 128)
        gi = f3(4, 128)
        av = f3(2, 128)
        a2 = f3(2, 128)
        tmpb = f3(2, 128)
        hs = f3(8, 128)
        ysq = b3(2, 128)
        rbc = f32(128)
        ynl = b3(8, 128)
        yns = b3(8, 128)
        xsf = f3(8, 128)
        Bb = b3(2, 128)
        Cb = b3(2, 128)
        dtr = f32(16)
        dtt = f32(16)
        da = f32(16)
        ncum = f32(16)
        dte = f32(16)
        cdec = f32(16)
        xdt = bf(1024)
        xdd = bf(1024)
        BT = bf(256)
        cbT = f3(2, 128)
        Dmf = f32(512)
        Emf = f32(512)
        Mmf = bf(512)
        Chf = bf(1024)
        Chp = Chf[:, 0:512].rearrange("p (a t) -> p a t", a=4)
        Chs = Chf.rearrange("p (a t) -> p a t", a=16)
        cvt = f3(2, 128)
        stg = f32(2560)
        stT = stg[:, 0:1024]
        lc_in = stg[:, 0:1024]
        sc_in = stg[:, 1024:2560]
        lh_in = stg[:, 0:1024]
        h0in = lxb.rearrange("p c t -> p (c t)")[:, 0:1024].rearrange("p (c t) -> p c t", c=8)
        h0Tb = hTb
        Bm = bf(256)
        pyo_sb = f3(8, TS)
        damb = bf(2 * NS * 16).rearrange("p (i s h) -> p i s h", i=2, s=NS)
        dtot = f3(NS, 16)
        hnew = hT
        hout = xcb.rearrange("p c t -> p (c t)")[:, 0:1024].rearrange("p (c t) -> p c t", c=8)
        h0s = f3(8, NS)
        hfin = f3(8, NS)

        def P(name, c=None, w=1):
            o = PC[name] + (0 if c is None else c * w)
            return pfm[:, o:o + w]

        def rms_rstd(xtile, T, keyx, junk, ss, rstd, sfx=""):
            S.act(lambda e: e.activation(out=junk[0:T, :], in_=xtile[0:T, :], func=AF.Square, accum_out=ss[0:T, 0:1]),
                  r=[keyx], w=["xn" + sfx, "ss" + sfx])
            S.act(lambda e: e.activation(out=ss[0:T, 0:1], in_=ss[0:T, 0:1], func=AF.Ln, scale=1.0 / D, bias=eps_t[0:T, 0:1]),
                  r=["ss" + sfx, "eps_t"], w=["ss" + sfx])
            S.act(lambda e: e.activation(out=rstd[0:T, 0:1], in_=ss[0:T, 0:1], func=AF.Exp, scale=-0.5),
                  r=["ss" + sfx], w=["rstd" + sfx])

        def to_fm(T, gname, dst, dkey):
            for k in range(8):
                S.pe(lambda e, k=k: e.transpose(out=pT[:, k * 128:k * 128 + T], in_=xn[0:T, k * 128:(k + 1) * 128],
                                                identity=ident[0:T, 0:T]), r=["xn", "cst"], w=["pT"])
            S.dve(lambda e: e.tensor_tensor(
                out=dst[:, :, 0:T], in0=pT.rearrange("p (k t) -> p k t", k=8)[:, :, 0:T],
                in1=P(gname, 0, 8).unsqueeze(2).to_broadcast([128, 8, T]), op=ALU.mult),
                r=["pT", "pfm"], w=[dkey])

        def mixer_tile(mt, samp):
            T = TS if samp else 128
            row0 = SEQ if samp else mt * 128
            xsrc = xs if samp else xp[mt * 128:(mt + 1) * 128, :]
            last = (not samp) and mt == NT - 1
            S.dma(lambda e: e.dma_start(out=xt[0:T, :], in_=xsrc), "xt", w=["xt"])
            rms_rstd(xt, T, "xt", junk, ss, rstd)
            S.act(lambda e: e.activation(out=xn[0:T, :], in_=xt[0:T, :], func=AF.Copy, scale=rstd[0:T, 0:1]),
                  r=["xt", "rstd"], w=["xn"])
            to_fm(T, "GM", hTt, "hTt")

            if samp:
                S.dma(lambda e: e.dma_start(out=lc_in[0:48, :], in_=st_lc), "stg", w=["stg"])
                S.dma(lambda e: e.dma_start(out=sc_in[0:48, :], in_=st_sc), "stg", w=["stg"])
                S.dma(lambda e: e.dma_start(out=lh_in[64:64 + NS, :], in_=st_lh), "stg", w=["stg"])
                for c in range(8):
                    S.pe(lambda e, c=c: e.transpose(out=pC[:, 0:48], in_=lc_in[0:48, c * 128:(c + 1) * 128],
                                                    identity=ident[0:48, 0:48]), r=["stg", "cst"], w=["pC"])
                    S.act(lambda e, c=c: e.activation(out=lxs[:, c, :, 0:3],
                                                      in_=pC[:, 0:48].rearrange("p (s j) -> p s j", s=NS),
                                                      func=AF.Copy), r=["pC"], w=["lx%d" % c])
                    S.pe(lambda e, c=c: e.transpose(out=pD[:, 0:NS], in_=lh_in[64:64 + NS, c * 128:(c + 1) * 128],
                                                    identity=ident[64:64 + NS, 64:64 + NS]), r=["stg", "cst"], w=["pD"])
                    S.dve(lambda e, c=c: e.tensor_copy(out=h0s[:, c, :], in_=pD[:, 0:NS]), r=["pD"], w=["h0s"])
                for c in range(12):
                    S.pe(lambda e, c=c: e.transpose(out=pC[:, 0:48], in_=sc_in[0:48, c * 128:(c + 1) * 128],
                                                    identity=ident[0:48, 0:48]), r=["stg", "cst"], w=["pC"])
                    S.act(lambda e, c=c: e.activation(out=xcs[:, c, :, 0:3],
                                                      in_=pC[:, 0:48].rearrange("p (s j) -> p s j", s=NS),
                                                      func=AF.Copy), r=["pC"], w=["xc%d" % c])

            pcnt = [0]

            def proj(ci):
                i = pcnt[0] % 2
                pcnt[0] += 1
                pa = pA[i]
                for k in range(8):
                    S.pe(lambda e, k=k: e.matmul(pa[:, 0:T], lhsT=w_in_sb[:, k, ci * 128:(ci + 1) * 128],
                                                 rhs=hTt[:, k, 0:T], start=(k == 0), stop=(k == 7)),
                         r=["hTt", "w_in"], w=["pA%d" % i])
                return pa, "pA%d" % i

            def new_cols(buf, sbuf_, c):
                if samp:
                    return sbuf_[:, c, :, 3:7]
                return buf[:, c, 3:131]

            def pa_view(pa):
                if samp:
                    return pa[:, 0:T].rearrange("p (s l) -> p s l", s=NS)
                return pa[:, 0:T]

            def tap(buf, sbuf_, c, k):
                if samp:
                    return sbuf_[:, c, :, k:k + 4]
                return buf[:, c, k:k + 128]

            def fm(t3, c):
                if samp:
                    return t3[:, c, 0:T].rearrange("p (s l) -> p s l", s=NS)
                return t3[:, c, 0:T]

            def conv(buf, sbuf_, c, wname, bname, out_ap, key_in, key_out):
                S.dve(lambda e: e.tensor_scalar(out=out_ap, in0=tap(buf, sbuf_, c, 3), scalar1=P(wname, c, 4)[:, 3:4],
                                                scalar2=P(bname, c), op0=ALU.mult, op1=ALU.add),
                      r=[key_in, "pfm"], w=[key_out])
                for k in (2, 1, 0):
                    S.dve(lambda e, k=k: e.scalar_tensor_tensor(out=out_ap, in0=tap(buf, sbuf_, c, k),
                                                                scalar=P(wname, c, 4)[:, k:k + 1], in1=out_ap,
                                                                op0=ALU.mult, op1=ALU.add),
                          r=[key_in, key_out, "pfm"], w=[key_out])
                if not samp:
                    S.dve(lambda e: e.tensor_copy(out=buf[:, c, 0:3], in_=buf[:, c, 128:131]), r=[key_in], w=[key_in])

            def g_lrux():
                for c in range(8):
                    pa, pk = proj(c)
                    S.act(lambda e, c=c, pa=pa: e.activation(out=new_cols(lxb, lxs, c), in_=pa_view(pa), func=AF.Copy),
                          r=[pk], w=["lx%d" % c])
                    conv(lxb, lxs, c, "LW", "LB", fm(u, c), "lx%d" % c, "u%d" % c)
                    yield

            def g_z():
                for c in range(8):
                    pa, pk = proj(16 + c)
                    S.act(lambda e, c=c, pa=pa: e.activation(out=zs[:, c, 0:T], in_=pa[:, 0:T], func=AF.Silu),
                          r=[pk], w=["zs%d" % c])
                    yield

            def g_xbc():
                for c in range(12):
                    pa, pk = proj(24 + c)
                    S.act(lambda e, c=c, pa=pa: e.activation(out=new_cols(xcb, xcs, c), in_=pa_view(pa), func=AF.Copy),
                          r=[pk], w=["xc%d" % c])
                    if c < 8:
                        conv(xcb, xcs, c, "SW", "SB", fm(cvt, c % 2), "xc%d" % c, "cv_t%d" % (c % 2))
                        S.act(lambda e, c=c: e.activation(out=xsf[:, c, 0:T], in_=cvt[:, c % 2, 0:T], func=AF.Silu),
                              r=["cv_t%d" % (c % 2)], w=["xsf%d" % c])
                    else:
                        g = (c - 8) % 2
                        dstb = Bb if c < 10 else Cb
                        nm = ("Bb%d" if c < 10 else "Cb%d") % g
                        conv(xcb, xcs, c, "SW", "SB", fm(cvt, g), "xc%d" % c, "cv_t%d" % g)
                        S.act(lambda e, g=g, dstb=dstb: e.activation(out=dstb[:, g, 0:T], in_=cvt[:, g, 0:T], func=AF.Silu),
                              r=["cv_t%d" % g], w=[nm])
                    yield
                for k in range(8):
                    S.pe(lambda e, k=k: e.matmul(pD[0:T, 0:16], lhsT=hTt[:, k, 0:T], rhs=w_in_sb[:, k, 4608:4624],
                                                 start=(k == 0), stop=(k == 7)),
                         r=["hTt", "w_in"], w=["pD"])
                S.dve(lambda e: e.tensor_tensor(out=dtr[0:T, :], in0=pD[0:T, 0:16], in1=dtb_bc[0:T, :], op=ALU.add),
                      r=["pD", "dtb"], w=["dtr"])
                S.act(lambda e: e.activation(out=dtr[0:T, :], in_=dtr[0:T, :], func=AF.Exp), r=["dtr"], w=["dtr"])
                S.act(lambda e: e.activation(out=dtt[0:T, :], in_=dtr[0:T, :], func=AF.Ln, bias=1.0), r=["dtr"], w=["dtt"])
                yield

            def g_lru():
                for c in range(8):
                    pp = c % 2
                    S.act(lambda e, c=c: e.activation(out=ub[:, c, 0:T], in_=u[:, c, 0:T], func=AF.Copy),
                          r=["u%d" % c], w=["ub%d" % c])
                    S.pe(lambda e, c=c: e.matmul(pC[:, 0:T], lhsT=wa_blk[:, c, :], rhs=ub[:, c, 0:T], start=True, stop=True),
                         r=["ub%d" % c, "wa"], w=["pC"])
                    S.pe(lambda e, c=c: e.matmul(pC[:, 128:128 + T], lhsT=wx_blk[:, c, :], rhs=ub[:, c, 0:T], start=True, stop=True),
                         r=["ub%d" % c, "wx"], w=["pCx"])
                    S.act(lambda e, c=c, pp=pp: e.activation(out=gi[:, 2 * pp, 0:T], in_=pC[:, 0:T], func=AF.Exp, scale=-1.0, bias=nbias[:, c:c + 1]),
                          r=["pC", "nbias"], w=["rg%d" % pp])
                    S.act(lambda e, c=c, pp=pp: e.activation(out=gi[:, 2 * pp + 1, 0:T], in_=pC[:, 128:128 + T], func=AF.Exp, scale=-1.0, bias=nbias[:, 8 + c:9 + c]),
                          r=["pCx", "nbias"], w=["ig%d" % pp])
                    yield
                    S.act(lambda e, pp=pp: e.activation(out=gi[:, 2 * pp:2 * pp + 2, 0:T], in_=gi[:, 2 * pp:2 * pp + 2, 0:T], func=AF.Ln, bias=1.0),
                          r=["rg%d" % pp, "ig%d" % pp], w=["rg%d" % pp, "ig%d" % pp])
                    S.act(lambda e, pp=pp: e.activation(out=gi[:, 2 * pp:2 * pp + 2, 0:T], in_=gi[:, 2 * pp:2 * pp + 2, 0:T], func=AF.Exp, scale=-1.0),
                          r=["rg%d" % pp, "ig%d" % pp], w=["rg%d" % pp, "ig%d" % pp])
                    S.act(lambda e, c=c, pp=pp: e.activation(out=av[:, pp, 0:T], in_=gi[:, 2 * pp, 0:T], func=AF.Exp, scale=cfac[:, c:c + 1]),
                          r=["rg%d" % pp, "cfac"], w=["av%d" % pp])
                    S.act(lambda e, c=c, pp=pp: e.activation(out=a2[:, pp, 0:T], in_=gi[:, 2 * pp, 0:T], func=AF.Exp, scale=c2fac[:, c:c + 1]),
                          r=["rg%d" % pp, "cfac2"], w=["a2%d" % pp])
                    S.act(lambda e, pp=pp: e.activation(out=a2[:, pp, 0:T], in_=a2[:, pp, 0:T], func=AF.Ln, scale=-1.0, bias=1.0),
                          r=["a2%d" % pp], w=["a2%d" % pp])
                    S.act(lambda e, pp=pp: e.activation(out=a2[:, pp, 0:T], in_=a2[:, pp, 0:T], func=AF.Exp, scale=0.5),
                          r=["a2%d" % pp], w=["a2%d" % pp])
                    S.dve(lambda e, c=c, pp=pp: e.tensor_tensor(out=tmpb[:, pp, 0:T], in0=gi[:, 2 * pp + 1, 0:T], in1=u[:, c, 0:T], op=ALU.mult),
                          r=["ig%d" % pp, "u%d" % c], w=["tb%d" % pp])
                    S.dve(lambda e, pp=pp: e.tensor_tensor(out=tmpb[:, pp, 0:T], in0=tmpb[:, pp, 0:T], in1=a2[:, pp, 0:T], op=ALU.mult),
                          r=["tb%d" % pp, "a2%d" % pp], w=["tb%d" % pp])
                    if samp:
                        a3 = av[:, pp, 0:T].rearrange("p (s l) -> p s l", s=NS)
                        b3v = tmpb[:, pp, 0:T].rearrange("p (s l) -> p s l", s=NS)
                        S.dve(lambda e, c=c, a3=a3: e.tensor_tensor(out=rbc[:, 0:NS], in0=a3[:, :, 0], in1=h0s[:, c, :], op=ALU.mult),
                              r=["av%d" % pp, "h0s"], w=["rbc"])
                        S.dve(lambda e, b3v=b3v: e.tensor_tensor(out=b3v[:, :, 0], in0=b3v[:, :, 0], in1=rbc[:, 0:NS], op=ALU.add),
                              r=["tb%d" % pp, "rbc"], w=["tb%d" % pp])
                        S.dve(lambda e, a3=a3: e.memset(a3[:, :, 0], 0.0), r=["rbc"], w=["av%d" % pp])
                        S.dve(lambda e, c=c, pp=pp: e.tensor_tensor_scan(out=hs[:, c, 0:T], data0=av[:, pp, 0:T], data1=tmpb[:, pp, 0:T],
                                                                         initial=0.0, op0=ALU.mult, op1=ALU.add),
                              r=["av%d" % pp, "tb%d" % pp], w=["hs%d" % c])
                        S.dve(lambda e, c=c: e.tensor_copy(out=hfin[:, c, :], in_=hs[:, c, 0:T].rearrange("p (s l) -> p s l", s=NS)[:, :, 3]),
                              r=["hs%d" % c], w=["hfin"])
                    else:
                        S.dve(lambda e, c=c, pp=pp: e.tensor_tensor_scan(out=hs[:, c, 0:T], data0=av[:, pp, 0:T], data1=tmpb[:, pp, 0:T],
                                                                         initial=hstate[:, c:c + 1], op0=ALU.mult, op1=ALU.add),
                              r=["av%d" % pp, "tb%d" % pp, "hstate"], w=["hs%d" % c])
                        S.dve(lambda e, c=c: e.tensor_copy(out=hstate[:, c:c + 1], in_=hs[:, c, T - 1:T]),
                              r=["hs%d" % c], w=["hstate"])
                    yield

            def g_gate():
                for c in range(8):
                    pp = c % 2
                    pa, pk = proj(8 + c)
                    S.act(lambda e, pp=pp, pa=pa: e.activation(out=gl[:, pp, 0:T], in_=pa[:, 0:T], func=AF.Gelu_apprx_tanh),
                          r=[pk], w=["gl%d" % pp])
                    S.dve(lambda e, c=c, pp=pp: e.tensor_tensor(out=hs[:, c, 0:T], in0=hs[:, c, 0:T], in1=gl[:, pp, 0:T], op=ALU.mult),
                          r=["hs%d" % c, "gl%d" % pp], w=["yl%d" % c, "hs%d" % c])
                    S.act(lambda e, c=c, pp=pp: e.activation(out=ysq[:, pp, 0:T], in_=hs[:, c, 0:T], func=AF.Square),
                          r=["yl%d" % c], w=["ysq%d" % pp])
                    S.pe(lambda e, c=c, pp=pp: e.matmul(pD[:, 128:128 + T], lhsT=onesb, rhs=ysq[:, pp, 0:T], start=(c == 0), stop=(c == 7)),
                         r=["ysq%d" % pp, "onesb"], w=["pDn"])
                    yield

            def interleave(*gens):
                gens = list(gens)
                while gens:
                    for g_ in list(gens):
                        try:
                            next(g_)
                        except StopIteration:
                            gens.remove(g_)

            interleave(g_lrux(), g_z())
            interleave(g_lru(), g_xbc())

            M = T if samp else 3
            t0 = 0 if samp else 125
            if samp or last:
                for blk, col0 in enumerate((0, 512, 3072, 3584, 4096)):
                    for k in range(8):
                        S.pe(lambda e, k=k, col0=col0: e.matmul(pO[0:M, 0:512], lhsT=hTt[:, k, t0:t0 + M],
                                                                rhs=w_in_sb[:, k, col0:col0 + 512], start=(k == 0), stop=(k == 7)),
                             r=["hTt", "w_in"], w=["pO"])
                    S.dve(lambda e, blk=blk: e.tensor_copy(out=stg[0:M, blk * 512:(blk + 1) * 512], in_=pO[0:M, 0:512]),
                          r=["pO"], w=["stg"])
            if last:
                S.dma(lambda e: e.dma_start(out=o_plc, in_=stg[0:3, 0:1024]), "o_plc", r=["stg"])
                S.dma(lambda e: e.dma_start(out=o_psc, in_=stg[0:3, 1024:2560]), "o_psc", r=["stg"])
            if samp:
                for s in range(NS):
                    S.dma(lambda e, s=s: e.dma_start(out=o_slc[s], in_=stg[4 * s + 1:4 * s + 4, 0:1024]), "o_slc", r=["stg"])
                    S.dma(lambda e, s=s: e.dma_start(out=o_ssc[s], in_=stg[4 * s + 1:4 * s + 4, 1024:2560]), "o_ssc", r=["stg"])
            if last:
                S.pe(lambda e: e.transpose(out=pC[0:8, 0:128], in_=hstate, identity=ident), r=["hstate", "cst"], w=["pC", "pCx"])
                S.act(lambda e: e.activation(out=stT[0:8, 0:128], in_=pC[0:8, 0:128], func=AF.Copy), r=["pC"], w=["stg"])
                S.dma(lambda e: e.dma_start(out=o_plh, in_=stT[0:8, 0:128]), "o_plh", r=["stg"])
            if samp:
                for c in range(8):
                    S.pe(lambda e, c=c: e.transpose(out=pT[0:NS, c * 128:(c + 1) * 128], in_=hfin[:, c, :], identity=ident),
                         r=["hfin", "cst"], w=["pT"])
                S.act(lambda e: e.activation(out=lh_in[0:NS, :], in_=pT[0:NS, :], func=AF.Copy), r=["pT"], w=["stg"])
                S.dma(lambda e: e.dma_start(out=o_slh, in_=lh_in[0:NS, :]), "o_slh", r=["stg"])

            def norm_apply(T, eps_, src, skey, gname, dst, dkey, c0, c1):
                S.act(lambda e: e.activation(out=rbc[:, 0:T], in_=pD[:, 128:128 + T], func=AF.Ln,
                                             scale=1.0 / ((c1 - c0) * 128), bias=eps_t[:, 0:1]),
                      r=["pDn", "eps_t"], w=["rbc"])
                S.act(lambda e: e.activation(out=rbc[:, 0:T], in_=rbc[:, 0:T], func=AF.Exp, scale=-0.5), r=["rbc"], w=["rbc"])
                for c in range(c0, c1):
                    S.dve(lambda e, c=c: e.scalar_tensor_tensor(out=dst[:, c, 0:T], in0=src[:, c, 0:T], scalar=P(gname, c),
                                                                in1=rbc[:, 0:T], op0=ALU.mult, op1=ALU.mult),
                          r=[skey % c, "rbc", "pfm"], w=[dkey % c])

            Um = mskb[0:TS, 0:TS] if samp else Utrib
            ngm = negblk if samp else negm
            allm = mskb[0:TS, TS:2 * TS] if samp else onesb
            d4 = lambda ap: ap[:, 0:4 * T].rearrange("p (a t) -> p a t", a=4)
            Em, Dm, Mm, pC4 = d4(Emf), d4(Dmf), d4(Mmf), d4(pC)

            def g_ssd():
                for c in range(8):
                    S.pe(lambda e, c=c: e.transpose(out=pT[0:T, c * 128:(c + 1) * 128], in_=xsf[:, c, 0:T], identity=ident),
                         r=["xsf%d" % c, "cst"], w=["pT"])
                for g in range(2):
                    S.pe(lambda e, g=g: e.transpose(out=pCb[0:T, 128 + g * 128:128 + (g + 1) * 128], in_=Bb[:, g, 0:T], identity=identb),
                         r=["Bb%d" % g, "identb"], w=["pC", "pCx"])
                S.dve(lambda e: e.tensor_tensor(out=xdt[0:T, :].rearrange("p (h q) -> p h q", h=16),
                                                in0=pT[0:T, :].rearrange("p (h q) -> p h q", h=16),
                                                in1=dtt[0:T, :].unsqueeze(2).to_broadcast([T, 16, 64]), op=ALU.mult),
                      r=["pT", "dtt"], w=["xdt"])
                S.dve(lambda e: e.tensor_copy(out=BT[0:T, :], in_=pCb[0:T, 128:384]), r=["pC"], w=["BT"])
                S.dve(lambda e: e.tensor_tensor(out=da[0:T, :], in0=dtt[0:T, :], in1=a_bc[0:T, :], op=ALU.mult),
                      r=["dtt", "a_bc"], w=["da"])
                yield
                S.dve(lambda e: e.tensor_copy(out=dah[0:T, :], in_=da[0:T, :]), r=["da"], w=["dah"])
                S.dve(lambda e: e.tensor_tensor(out=dal[0:T, :], in0=da[0:T, :], in1=dah[0:T, :], op=ALU.subtract),
                      r=["da", "dah"], w=["dal"])
                for i, dx in enumerate((dah, dal)):
                    S.pe(lambda e, dx=dx, i=i: e.matmul(pC[0:T, 0:16], lhsT=Um[0:T, 0:T], rhs=dx[0:T, :], start=(i == 0), stop=(i == 1)),
                         r=["dah", "dal", "mskb"], w=["pC", "pCx"])
                for i, dx in enumerate((dah, dal)):
                    S.pe(lambda e, dx=dx, i=i: e.matmul(pC[0:T, 16:32], lhsT=allm[0:T, 0:T], rhs=dx[0:T, :], start=(i == 0), stop=(i == 1)),
                         r=["dah", "dal", "mskb", "onesb"], w=["pC", "pCx"])
                if not samp:
                    for i, dx in enumerate((dah, dal)):
                        S.pe(lambda e, dx=dx, i=i: e.matmul(pC[:, 32:48], lhsT=onesb, rhs=dx, start=(i == 0), stop=(i == 1)),
                             r=["dah", "dal", "onesb"], w=["pC", "pCx"])
                for g in range(2):
                    S.pe(lambda e, g=g: e.matmul(pC[0:T, 256 + g * 128:256 + g * 128 + T], lhsT=Bb[:, g, 0:T], rhs=Cb[:, g, 0:T],
                                                 start=True, stop=True), r=["Bb%d" % g, "Cb%d" % g], w=["pC", "pCx"])
                yield
                S.dve(lambda e: e.tensor_scalar(out=ncum[0:T, :], in0=pC[0:T, 0:16], scalar1=-1.0, scalar2=None, op0=ALU.mult),
                      r=["pC"], w=["ncum"])
                S.dve(lambda e: e.tensor_tensor(out=dte[0:T, :], in0=pC[0:T, 16:32], in1=ncum[0:T, :], op=ALU.add),
                      r=["pC", "ncum"], w=["dte"])
                if not samp:
                    S.dve(lambda e: e.tensor_copy(out=cdec, in_=pC[:, 32:48]), r=["pC"], w=["cdec"])
                S.dve(lambda e: e.tensor_copy(out=cbT[0:T, :, 0:T], in_=pC[0:T, 256:512].rearrange("p (g t) -> p g t", g=2)[:, :, 0:T]),
                      r=["pC"], w=["cbT0", "cbT1"])
                S.act(lambda e: e.activation(out=dte[0:T, :], in_=dte[0:T, :], func=AF.Exp), r=["dte"], w=["dte"])
                if not samp:
                    S.act(lambda e: e.activation(out=cdec, in_=cdec, func=AF.Exp), r=["cdec"], w=["cdec"])
                S.dve(lambda e: e.tensor_tensor(out=xdd[0:T, :].rearrange("p (h q) -> p h q", h=16),
                                                in0=xdt[0:T, :].rearrange("p (h q) -> p h q", h=16),
                                                in1=dte[0:T, :].unsqueeze(2).to_broadcast([T, 16, 64]), op=ALU.mult),
                      r=["xdt", "dte"], w=["xdd"])
                yield
                if not samp:
                    for g in range(2):
                        S.pe(lambda e, g=g: e.matmul(pO[:, g * 512:(g + 1) * 512], lhsT=BT[:, g * 128:(g + 1) * 128],
                                                     rhs=xdd[:, g * 512:(g + 1) * 512], start=True, stop=True),
                             r=["BT", "xdd"], w=["pO"])
                    S.dve(lambda e: e.tensor_tensor(out=hT.rearrange("p (h q) -> p h q", h=16),
                                                    in0=hT.rearrange("p (h q) -> p h q", h=16),
                                                    in1=cdec.unsqueeze(2).to_broadcast([128, 16, 64]), op=ALU.mult),
                          r=["hT", "cdec"], w=["hT"])
                    S.dve(lambda e: e.tensor_tensor(out=hT, in0=hT, in1=pO, op=ALU.add), r=["hT", "pO"], w=["hT"])
                    yield
                for q4 in range(4):
                    g = q4 // 2
                    Wl = cvt.rearrange("p a t -> p (a t)").bitcast(BF16)
                    for i, (dx, Wf, wk) in enumerate(((dah, Mmf, ["Mm"]), (dal, Wl, ["cv_t0", "cv_t1"]))):
                        S.dve(lambda e, q4=q4, dx=dx, Wf=Wf: e.tensor_tensor(out=d4(Wf)[0:T], in0=Um[0:T, 0:T].unsqueeze(1).to_broadcast([T, 4, T]),
                                                                             in1=dx[0:T, q4 * 4:q4 * 4 + 4].unsqueeze(2).to_broadcast([T, 4, T]),
                                                                             op=ALU.mult), r=["dah", "dal", "mskb"], w=wk)
                        S.pe(lambda e, Wf=Wf, i=i: e.matmul(pC[:, 0:4 * T], lhsT=onesb[0:T, :], rhs=Wf[0:T, 0:4 * T],
                                                            start=(i == 0), stop=(i == 1)), r=wk + ["onesb"], w=["pC", "pCx"])
                    S.dve(lambda e: e.tensor_copy(out=Em, in_=pC4), r=["pC"], w=["Em"])
                    yield
                    for hh in range(4):
                        h = q4 * 4 + hh
                        S.dve(lambda e, h=h, hh=hh: e.scalar_tensor_tensor(out=Dm[0:T, hh, :], in0=Em[0:T, hh, :],
                                                                           scalar=ncum[0:T, h:h + 1], in1=ngm[0:T, 0:T],
                                                                           op0=ALU.add, op1=ALU.add),
                              r=["Em", "ncum", "cst"], w=["Dm"])
                    S.act(lambda e: e.activation(out=Em, in_=Em, func=AF.Exp), r=["Em"], w=["Em"])
                    S.act(lambda e: e.activation(out=Dm[0:T], in_=Dm[0:T], func=AF.Exp), r=["Dm"], w=["Dm"])
                    S.dve(lambda e, g=g: e.tensor_tensor(out=Mm[0:T], in0=Dm[0:T],
                                                         in1=cbT[0:T, g, 0:T].unsqueeze(1).to_broadcast([T, 4, T]), op=ALU.mult),
                          r=["Dm", "cbT%d" % g], w=["Mm"])
                    S.dve(lambda e, g=g, q4=q4: e.tensor_tensor(out=(Chs[:, q4 * 4:q4 * 4 + 4, :] if samp else Chp), in0=Em,
                                                                in1=Cb[:, g, 0:T].unsqueeze(1).to_broadcast([128, 4, T]), op=ALU.mult),
                          r=["Em", "Cb%d" % g], w=["Ch"])
                    yield
                    for hh in range(4):
                        h = q4 * 4 + hh
                        c = h // 2
                        h2 = h % 2
                        po = pT[64 * h2:64 * h2 + 64, c * 128:c * 128 + T]
                        S.pe(lambda e, h=h, hh=hh, po=po: e.matmul(po, lhsT=xdt[0:T, h * 64:(h + 1) * 64], rhs=Mm[0:T, hh, :],
                                                                   start=True, stop=samp), r=["xdt", "Mm"], w=["pT"])
                        if not samp:
                            S.pe(lambda e, h=h, hh=hh, po=po: e.matmul(po, lhsT=hTb[:, h * 64:(h + 1) * 64], rhs=Chp[:, hh, :],
                                                                       start=False, stop=True), r=["hTb", "Ch"], w=["pT"])
                    yield

            interleave(g_gate(), g_ssd())
            norm_apply(T, EPS, hs, "yl%d", "GL", ynl, "ynl%d", 0, 8)
            if samp:
                ssd_sample_states_prep()
            for c in range(8):
                S.dve(lambda e, c=c: e.scalar_tensor_tensor(out=xsf[:, c, 0:T], in0=xsf[:, c, 0:T], scalar=P("DS", c),
                                                            in1=pT[:, c * 128:c * 128 + T], op0=ALU.mult, op1=ALU.add),
                      r=["pT", "xsf%d" % c, "pfm"], w=["xsf%d" % c])
            if samp:
                S.dve(lambda e: e.tensor_tensor(out=xsf[:, :, 0:T], in0=xsf[:, :, 0:T], in1=pyo_sb, op=ALU.add),
                      r=["xsf%d" % c for c in range(8)] + ["pyo_sb"], w=["xsf%d" % c for c in range(8)])
            S.pool(lambda e: e.tensor_tensor(out=xsf[:, :, 0:T], in0=xsf[:, :, 0:T], in1=zs[:, :, 0:T], op=ALU.mult),
                   r=["xsf%d" % c for c in range(8)] + ["zs%d" % c for c in range(8)], w=["yg%d" % c for c in range(8)] + ["xsf%d" % c for c in range(8)])
            if not samp:
                S.act(lambda e: e.activation(out=hTb, in_=hT, func=AF.Copy), r=["hT"], w=["hTb"])
                if last:
                    for c in range(8):
                        S.pe(lambda e, c=c: e.transpose(out=pO[:, c * 128:(c + 1) * 128], in_=hT[:, c * 128:(c + 1) * 128], identity=ident),
                             r=["hT", "cst"], w=["pO"])
                    S.dve(lambda e: e.tensor_copy(out=stT, in_=pO), r=["pO"], w=["stg"])
                    S.dma(lambda e: e.dma_start(out=o_psh.rearrange("(c q) n -> q c n", q=128),
                                                in_=stT.rearrange("p (c n) -> p c n", c=8)), "o_psh", r=["stg"])
            for g in range(2):
                for c in range(4 * g, 4 * g + 4):
                    pp = c % 2
                    S.act(lambda e, c=c, pp=pp: e.activation(out=ysq[:, pp, 0:T], in_=xsf[:, c, 0:T], func=AF.Square),
                          r=["yg%d" % c], w=["ysq%d" % pp])
                    S.pe(lambda e, c=c, pp=pp, g=g: e.matmul(pD[:, 128:128 + T], lhsT=onesb, rhs=ysq[:, pp, 0:T],
                                                             start=(c == 4 * g), stop=(c == 4 * g + 3)),
                         r=["ysq%d" % pp, "onesb"], w=["pDn"])
                norm_apply(T, EPS, xsf, "yg%d", "GS", yns, "yns%d", 4 * g, 4 * g + 4)

            for nb in range(2):
                for kc in range(16):
                    src = ynl if kc < 8 else yns
                    S.pe(lambda e, kc=kc, nb=nb, src=src: e.matmul(pO[0:T, nb * 512:(nb + 1) * 512], lhsT=src[:, kc % 8, 0:T],
                                                                   rhs=w_out_sb[:, kc, nb * 512:(nb + 1) * 512],
                                                                   start=(kc == 0), stop=(kc == 15)),
                         r=[("ynl%d" if kc < 8 else "yns%d") % (kc % 8), "w_out"], w=["pO"])
            S.dve(lambda e: e.tensor_tensor(out=xt[0:T, :], in0=pO[0:T, :], in1=xt[0:T, :], op=ALU.add),
                  r=["pO", "xt"], w=["xt"])
            S.dma(lambda e: e.dma_start(out=scr[row0:row0 + T, :], in_=xt[0:T, :]), "xnew", r=["xt"], w=["scr%d" % mt])

        def ssd_sample_states_prep():
            T = TS
            for i, dx in enumerate((dah, dal)):
                S.dve(lambda e, dx=dx, i=i: e.tensor_tensor(out=damb[0:T, i], in0=dx[0:T, :].unsqueeze(1).to_broadcast([T, NS, 16]),
                                                            in1=blki.unsqueeze(2).to_broadcast([T, NS, 16]), op=ALU.mult),
                      r=["dah", "dal", "cst"], w=["dam%d" % i])
                S.pe(lambda e, i=i: e.matmul(pD[:, 0:256], lhsT=onesb[0:T, :], rhs=damb[0:T, i].rearrange("p s h -> p (s h)"),
                                             start=(i == 0), stop=(i == 1)), r=["dam%d" % i, "onesb"], w=["pD", "pD2", "pD3", "pDn"])
            S.act(lambda e: e.activation(out=dtot.rearrange("p s h -> p (s h)"), in_=pD[:, 0:256], func=AF.Exp),
                  r=["pD"], w=["dtot"])
            for s in range(NS):
                S.dma(lambda e, s=s: e.dma_start(out=h0in, in_=st_sh[s].rearrange("(c q) n -> q c n", q=128)), "h0in", w=["h0in"])
                for c in range(8):
                    S.pe(lambda e, c=c: e.transpose(out=pO[:, c * 128:(c + 1) * 128], in_=h0in[:, c, :], identity=ident),
                         r=["h0in", "cst"], w=["pO"])
                S.act(lambda e: e.activation(out=h0Tb, in_=pO, func=AF.Copy), r=["pO"], w=["h0Tb"])
                for h in range(16):
                    S.pe(lambda e, h=h, s=s: e.matmul(pA[0][64 * (h % 2):64 * (h % 2) + 64, (h // 2) * TS + 4 * s:(h // 2) * TS + 4 * s + 4],
                                                      lhsT=h0Tb[:, h * 64:(h + 1) * 64], rhs=Chs[:, h, 4 * s:4 * s + 4],
                                                      start=True, stop=True),
                         r=["h0Tb", "Ch"], w=["pA0"])
                S.pool(lambda e, s=s: e.tensor_scalar(out=Bm[0:T, :], in0=BT[0:T, :], scalar1=blki[:, s:s + 1], scalar2=None, op0=ALU.mult),
                       r=["BT", "cst"], w=["Bm"])
                S.dve(lambda e, s=s: e.tensor_tensor(out=hnew.rearrange("p (h q) -> p h q", h=16),
                                                     in0=pO.rearrange("p (h q) -> p h q", h=16),
                                                     in1=dtot[:, s, :].unsqueeze(2).to_broadcast([128, 16, 64]), op=ALU.mult),
                      r=["pO", "dtot"], w=["hnew"])
                for g in range(2):
                    S.pe(lambda e, g=g, s=s: e.matmul(pO[:, g * 512:(g + 1) * 512], lhsT=Bm[0:T, g * 128:(g + 1) * 128],
                                                      rhs=xdd[0:T, g * 512:(g + 1) * 512], start=True, stop=True),
                         r=["Bm", "xdd"], w=["pO"])
                S.dve(lambda e: e.tensor_tensor(out=hnew, in0=hnew, in1=pO, op=ALU.add), r=["hnew", "pO"], w=["hnew"])
                for c in range(8):
                    S.pe(lambda e, c=c: e.transpose(out=pO[:, c * 128:(c + 1) * 128], in_=hnew[:, c * 128:(c + 1) * 128], identity=ident),
                         r=["hnew", "cst"], w=["pO"])
                S.act(lambda e: e.activation(out=hout.rearrange("p c n -> p (c n)"), in_=pO, func=AF.Copy), r=["pO"], w=["hout"])
                S.dma(lambda e, s=s: e.dma_start(out=o_ssh[s].rearrange("(c q) n -> q c n", q=128), in_=hout), "hout", r=["hout"])
            S.act(lambda e: e.activation(out=pyo_sb.rearrange("p c t -> p (c t)"), in_=pA[0][:, 0:8 * TS], func=AF.Copy),
                  r=["pA0"], w=["pyo_sb"])

        S.pool(lambda e: e.memset(lxb, 0.0), w=["lx%d" % c for c in range(8)])
        S.pool(lambda e: e.memset(xcb, 0.0), w=["xc%d" % c for c in range(12)])

        for mt in range(NT):
            mixer_tile(mt, False)
        S.barrier()
        if SAMP:
            mixer_tile(SEQ // 128, True)

        S.barrier()
        ptr[0] = base0
        w_up_sb = b3(8, DFF)
        w_dn_sb = b3(32, D)
        if MLP:
            k_wup = load_w(w_up_sb, w_up, 8, DFF, "w_up")
            k_wdn = load_w(w_dn_sb, w_down, 32, D, "w_dn")
        T2 = 256
        xt2 = [f32(D), f32(D)]
        xn2 = f32(D)
        junk2 = xn2
        ss2 = f32(4)
        rstd2 = f32(4)
        mT = b3(8, T2)
        actb = b3(32, T2)
        rl = [f32(T2), f32(T2)]
        yout = f32(D)
        gfin_bc = f32(D)
        S.dma(lambda e: e.dma_start(out=gfin_bc, in_=gfin_d.partition_broadcast(128)), "gfin", w=["gfin"])

        def mlp_tile(r0, T):
            nsub = (T + 127) // 128
            for j in range(nsub):
                Tj = min(128, T - j * 128)
                S.dma(lambda e, j=j, Tj=Tj: e.dma_start(out=xt2[j][0:Tj, :], in_=scr[r0 + j * 128:r0 + j * 128 + Tj, :]),
                      "xt2_%d" % j, r=["scr%d" % ((r0 + j * 128) // 128)], w=["xt2_%d" % j])
                rms_rstd(xt2[j], Tj, "xt2_%d" % j, junk2, ss2, rstd2, "2")
                S.act(lambda e, j=j, Tj=Tj: e.activation(out=xn2[0:Tj, :], in_=xt2[j][0:Tj, :], func=AF.Copy, scale=rstd2[0:Tj, 0:1]),
                      r=["xt2_%d" % j, "rstd2"], w=["xn2"])
                for k in range(8):
                    S.pe(lambda e, k=k, Tj=Tj: e.transpose(out=pT[:, k * 128:k * 128 + Tj], in_=xn2[0:Tj, k * 128:(k + 1) * 128],
                                                           identity=ident[0:Tj, 0:Tj]), r=["xn2", "cst"], w=["pT"])
                S.dve(lambda e, j=j, Tj=Tj: e.tensor_tensor(
                    out=mT[:, :, j * 128:j * 128 + Tj], in0=pT.rearrange("p (k t) -> p k t", k=8)[:, :, 0:Tj],
                    in1=P("GP", 0, 8).unsqueeze(2).to_broadcast([128, 8, Tj]), op=ALU.mult),
                    r=["pT", "pfm"], w=["mT"])
            for f in range(32):
                pa = pA[f % 2]
                for k in range(8):
                    S.pe(lambda e, k=k, f=f, pa=pa: e.matmul(pa[:, 0:T], lhsT=w_up_sb[:, k, f * 128:(f + 1) * 128], rhs=mT[:, k, 0:T],
                                                             start=(k == 0), stop=(k == 7)),
                         r=["mT", "w_up"], w=["pA%d" % (f % 2)])
                S.act(lambda e, f=f, pa=pa: e.activation(out=rl[f % 2][:, 0:T], in_=pa[:, 0:T], func=AF.Relu),
                      r=["pA%d" % (f % 2)], w=["rl%d" % (f % 2)])
                S.pool(lambda e, f=f: e.tensor_tensor(out=actb[:, f, 0:T], in0=rl[f % 2][:, 0:T], in1=rl[f % 2][:, 0:T], op=ALU.mult),
                       r=["rl%d" % (f % 2)], w=["act%d" % f])
            for j in range(nsub):
                Tj = min(128, T - j * 128)
                for nb in range(2):
                    for f in range(32):
                        S.pe(lambda e, f=f, nb=nb, j=j, Tj=Tj: e.matmul(pO[0:Tj, nb * 512:(nb + 1) * 512],
                                                                        lhsT=actb[:, f, j * 128:j * 128 + Tj],
                                                                        rhs=w_dn_sb[:, f, nb * 512:(nb + 1) * 512],
                                                                        start=(f == 0), stop=(f == 31)),
                             r=["act%d" % f, "w_dn"], w=["pO"])
                S.dve(lambda e, j=j, Tj=Tj: e.tensor_tensor(out=xt2[j][0:Tj, :], in0=pO[0:Tj, :], in1=xt2[j][0:Tj, :], op=ALU.add),
                      r=["pO", "xt2_%d" % j], w=["xt2_%d" % j])
                rms_rstd(xt2[j], Tj, "xt2_%d" % j, junk2, ss2, rstd2, "2")
                S.dve(lambda e, j=j, Tj=Tj: e.scalar_tensor_tensor(out=yout[0:Tj, :], in0=xt2[j][0:Tj, :], scalar=rstd2[0:Tj, 0:1],
                                                                   in1=gfin_bc[0:Tj, :], op0=ALU.mult, op1=ALU.mult),
                      r=["xt2_%d" % j, "rstd2", "gfin"], w=["yout"])
                rr = r0 + j * 128
                if rr < SEQ:
                    S.dma(lambda e, rr=rr, Tj=Tj: e.dma_start(out=y_p[rr:rr + Tj, :], in_=yout[0:Tj, :]), "yout", r=["yout"])
                else:
                    S.dma(lambda e, Tj=Tj: e.dma_start(out=y_s, in_=yout[0:Tj, :]), "yout", r=["yout"])

        if MLP:
            for t in range(NT * 128 // T2):
                mlp_tile(t * T2, T2)
            if SAMP:
                mlp_tile(SEQ, TS)

        S.emit()
    return nc


_CACHE = {}


def _consts():
    c = np.zeros((128, NCST), np.float32)
    i = np.arange(128)
    c[:, CI:CI + 128] = np.eye(128, dtype=np.float32)
    c[:, CU:CU + 128] = (i[:, None] <= i[None, :]).astype(np.float32)
    c[:, CN:CN + 128] = np.where(i[:, None] <= i[None, :], 0.0, NEG).astype(np.float32)
    c[:, CO:CO + 128] = 1.0
    j = np.arange(TS)
    same = (j[:, None] // 4) == (j[None, :] // 4)
    caus = j[:, None] <= j[None, :]
    c[0:TS, CUB:CUB + TS] = (same & caus).astype(np.float32)
    c[0:TS, CNB:CNB + TS] = np.where(same & caus, 0.0, NEG).astype(np.float32)
    c[0:TS, CBM:CBM + TS] = same.astype(np.float32)
    c[0:TS, CBI:CBI + NS] = ((j[:, None] // 4) == np.arange(NS)[None, :]).astype(np.float32)
    return c


def _fm(v, nch):
    return np.ascontiguousarray(np.asarray(v, np.float32).reshape(nch, 128).T)


def kernel(x_prompt, x_sample, state_lru_conv, state_lru_h, state_ssd_conv, state_ssd_h,
           g_mix, w_in, lru_conv_w, lru_conv_b, w_a, b_a, w_x, b_x, lam, g_lru_out,
           ssd_conv_w, ssd_conv_b, dt_bias, a_log, d_skip, g_ssd_out, w_out,
           g_mlp, w_up, w_down, g_final):
    f = lambda a: np.ascontiguousarray(np.asarray(a, np.float32))
    if "nc" not in _CACHE:
        _CACHE["nc"] = build_program()
    nc = _CACHE["nc"]
    pfm = np.zeros((128, NPAR), np.float32)
    lw = np.asarray(lru_conv_w[0], np.float32)
    pfm[:, PC["LW"]:PC["LW"] + 32] = lw.reshape(4, 8, 128).transpose(2, 1, 0).reshape(128, 32)
    pfm[:, PC["LB"]:PC["LB"] + 8] = _fm(lru_conv_b[0], 8)
    pfm[:, PC["BA"]:PC["BA"] + 8] = _fm(np.asarray(b_a[0]).reshape(-1), 8)
    pfm[:, PC["BX"]:PC["BX"] + 8] = _fm(np.asarray(b_x[0]).reshape(-1), 8)
    pfm[:, PC["LAM"]:PC["LAM"] + 8] = _fm(lam[0], 8)
    pfm[:, PC["GL"]:PC["GL"] + 8] = _fm(g_lru_out[0], 8)
    sw = np.asarray(ssd_conv_w[0], np.float32)
    pfm[:, PC["SW"]:PC["SW"] + 48] = sw.reshape(4, 12, 128).transpose(2, 1, 0).reshape(128, 48)
    pfm[:, PC["SB"]:PC["SB"] + 12] = _fm(ssd_conv_b[0], 12)
    pfm[:, PC["DS"]:PC["DS"] + 8] = _fm(np.repeat(np.asarray(d_skip[0], np.float32), 64), 8)
    pfm[:, PC["GS"]:PC["GS"] + 8] = _fm(g_ssd_out[0], 8)
    pfm[:, PC["GM"]:PC["GM"] + 8] = _fm(g_mix[0], 8)
    pfm[:, PC["GP"]:PC["GP"] + 8] = _fm(g_mlp[0], 8)
    cst = _consts()
    shared = {
        "w_in": f(w_in[0]), "w_out": f(w_out[0]), "w_up": f(w_up[0]), "w_down": f(w_down[0]),
        "w_a": f(w_a[0]), "w_x": f(w_x[0]), "pfm": pfm, "cst": cst,
        "dt_bias": f(dt_bias[0]), "a_log": f(a_log[0]), "g_final": f(g_final),
    }
    in_maps = []
    for b in range(NCORES):
        sl = slice(NS * b, NS * (b + 1))
        m = dict(shared)
        m["xp"] = f(x_prompt[b])
        m["xs"] = f(np.asarray(x_sample[sl]).reshape(TS, D))
        m["st_lc"] = f(np.asarray(state_lru_conv[0, sl]).reshape(NS * 3, D))
        m["st_lh"] = f(state_lru_h[0, sl])
        m["st_sc"] = f(np.asarray(state_ssd_conv[0, sl]).reshape(NS * 3, XBC))
        m["st_sh"] = f(np.asarray(state_ssd_h[0, sl]).reshape(NS, 1024, 128))
        in_maps.append(m)
    res = run_bass_kernel_spmd(nc, in_maps, core_ids=list(range(NCORES)))
    R = res.results
    cat = lambda k: np.stack([np.asarray(R[b][k], np.float32) for b in range(NCORES)])
    y_prompt = cat("y_p")
    y_sample = cat("y_s").reshape(NCORES * NS, 4, D)
    p_lc = cat("o_plc")[None]
    p_lh = cat("o_plh").reshape(NCORES, D)[None]
    p_sc = cat("o_psc")[None]
    p_sh = cat("o_psh").reshape(NCORES, 16, 64, 128)[None]
    s_lc = cat("o_slc").reshape(NCORES * NS, 3, D)[None]
    s_lh = cat("o_slh").reshape(NCORES * NS, D)[None]
    s_sc = cat("o_ssc").reshape(NCORES * NS, 3, XBC)[None]
    s_sh = cat("o_ssh").reshape(NCORES * NS, 16, 64, 128)[None]
    return (y_prompt, y_sample, p_lc, p_lh, p_sc, p_sh, s_lc, s_lh, s_sc, s_sh)
```

```python
import math
from contextlib import ExitStack

import numpy as np
import concourse.bass as bass
import concourse.mybir as mybir
from concourse.bass_utils import run_bass_kernel_spmd

F32 = mybir.dt.float32
BF16 = mybir.dt.bfloat16
AF = mybir.ActivationFunctionType
ALU = mybir.AluOpType

NCORES = 8
D = 1024
SEQ = 2048
NS = 16
TS = 64
XBC = 1536
INP = 4624
DFF = 4096
EPS = 1e-6
NEG = -30000.0

ENGS = ("pe", "act", "dve", "pool", "sp")
SAME_ENGINE_SYNC = {"pe": False, "act": True, "dve": True, "pool": True, "sp": False}


class Op:
    __slots__ = ("eng", "fn", "deps", "marked", "count", "dma_key", "dma_val")

    def __init__(self, eng, fn, dma_key=None):
        self.eng = eng
        self.fn = fn
        self.deps = ()
        self.marked = False
        self.count = 0
        self.dma_key = dma_key
        self.dma_val = 0


class Sched:
    def __init__(self, nc):
        self.nc = nc
        self.ops = {e: [] for e in ENGS}
        self.last_w = {}
        self.readers = {}
        self.dma_cnt = {}
        self.pending = {}
        self.since_bar = []

    ALIAS = {"pC": "b4", "pCx": "b4", "pD": "b5", "pD2": "b5", "pD3": "b5", "pDn": "b5",
             "pDcb0": "b5", "pDcb1": "b5", "pT": "b01", "pO": "b67", "pA0": "b2", "pA1": "b3"}

    PSUM_KEYS = {"b01", "b2", "b3", "b4", "b5", "b67"}

    def add(self, eng, fn, reads=(), writes=(), dma_key=None):
        reads = [self.ALIAS.get(k, k) for k in reads]
        writes = [self.ALIAS.get(k, k) for k in writes]
        op = Op(eng, fn, dma_key)
        deps = []
        seen = set()

        def dep(o):
            if o is not None and o is not op and id(o) not in seen:
                seen.add(id(o))
                deps.append(o)

        if self.pending.get(eng):
            for o in self.pending[eng]:
                dep(o)
            self.pending[eng] = []
        for b in reads:
            dep(self.last_w.get(b))
            if b in self.PSUM_KEYS:
                for r in self.readers.get(b, ()):
                    if r.eng != eng:
                        dep(r)
        for b in writes:
            dep(self.last_w.get(b))
            for r in self.readers.get(b, ()):
                dep(r)
        for b in reads:
            self.readers.setdefault(b, []).append(op)
        for b in writes:
            self.last_w[b] = op
            self.readers[b] = []
        op.deps = deps
        if dma_key is not None:
            self.dma_cnt[dma_key] = self.dma_cnt.get(dma_key, 0) + 16
            op.dma_val = self.dma_cnt[dma_key]
        self.ops[eng].append(op)
        self.since_bar.append(op)
        return op

    def barrier(self):
        ops = []
        for e in ENGS:
            comp = [o for o in self.ops[e] if o.dma_key is None]
            if comp:
                ops.append(comp[-1])
        last_dma = {}
        for o in self.since_bar:
            if o.dma_key is not None:
                last_dma[o.dma_key] = o
        ops.extend(last_dma.values())
        for e in ENGS:
            self.pending.setdefault(e, []).extend(ops)
        self.since_bar = []

    def pe(self, fn, r=(), w=()):
        return self.add("pe", fn, r, w)

    def act(self, fn, r=(), w=()):
        return self.add("act", fn, r, w)

    def dve(self, fn, r=(), w=()):
        return self.add("dve", fn, r, w)

    def pool(self, fn, r=(), w=()):
        return self.add("pool", fn, r, w)

    def dma(self, fn, key, r=(), w=(), q="sp"):
        return self.add(q, fn, r, w, dma_key=key)

    def emit(self):
        nc = self.nc
        for e in ENGS:
            for op in self.ops[e]:
                for d in op.deps:
                    if d.dma_key is None:
                        if d.eng == op.eng and not SAME_ENGINE_SYNC[d.eng]:
                            continue
                        d.marked = True
        for e in ENGS:
            c = 0
            for op in self.ops[e]:
                if op.dma_key is None and op.marked:
                    c += 1
                    op.count = c
        with ExitStack() as st:
            esem = {e: st.enter_context(nc.semaphore("es_" + e)) for e in ENGS}
            dsem = {}
            for k in self.dma_cnt:
                dsem[k] = st.enter_context(nc.semaphore("ds_%d" % len(dsem)))
            block = st.enter_context(nc.Block())

            def run(ename, eng):
                seen = {}
                for op in self.ops[ename]:
                    need = {}
                    for d in op.deps:
                        if d.dma_key is not None:
                            key = ("d", d.dma_key)
                            val = d.dma_val
                            sem = dsem[d.dma_key]
                        else:
                            if d.eng == ename and not SAME_ENGINE_SYNC[ename]:
                                continue
                            key = ("e", d.eng)
                            val = d.count
                            sem = esem[d.eng]
                        if key not in need or need[key][1] < val:
                            need[key] = (sem, val)
                    for key, (sem, val) in need.items():
                        if seen.get(key, 0) >= val:
                            continue
                        seen[key] = val
                        eng.wait_ge(sem, val)
                    ins = op.fn(eng)
                    if op.dma_key is not None:
                        ins.then_inc(dsem[op.dma_key], 16)
                    elif op.marked:
                        ins.then_inc(esem[ename], 1)
                if ename == "sp":
                    for k, v in self.dma_cnt.items():
                        eng.wait_ge(dsem[k], v)

            @block.sync
            def _(e):
                run("sp", e)

            @block.tensor
            def _(e):
                run("pe", e)

            @block.scalar
            def _(e):
                run("act", e)

            @block.vector
            def _(e):
                run("dve", e)

            @block.gpsimd
            def _(e):
                run("pool", e)


PC = {}
_o = 0
for _n, _w in (("LW", 32), ("LB", 8), ("BA", 8), ("BX", 8), ("LAM", 8), ("GL", 8), ("SW", 48),
               ("SB", 12), ("DS", 8), ("GS", 8), ("GM", 8), ("GP", 8)):
    PC[_n] = _o
    _o += _w
NPAR = _o
CI, CU, CN, CO, CUB, CNB, CBM, CBI = 0, 128, 256, 384, 512, 576, 640, 704
NCST = 720


def build_program(NT=SEQ // 128, SAMP=True, MLP=True, DBG=False, STAGE=9):
    nc = bass.Bass("TRN2", target_bir_lowering=False)
    S = Sched(nc)

    def din(name, shape):
        return nc.dram_tensor(name, list(shape), F32, kind="ExternalInput").ap()

    def dout(name, shape):
        return nc.dram_tensor(name, list(shape), F32, kind="ExternalOutput").ap()

    xp = din("xp", (SEQ, D))
    xs = din("xs", (TS, D))
    st_lc = din("st_lc", (NS * 3, D))
    st_lh = din("st_lh", (NS, D))
    st_sc = din("st_sc", (NS * 3, XBC))
    st_sh = din("st_sh", (NS, 1024, 128))
    w_in = din("w_in", (D, INP))
    w_out = din("w_out", (2 * D, D))
    w_up = din("w_up", (D, DFF))
    w_down = din("w_down", (DFF, D))
    w_a = din("w_a", (16, 64, 64))
    w_x = din("w_x", (16, 64, 64))
    pfm_d = din("pfm", (128, NPAR))
    cst_d = din("cst", (128, NCST))
    dtb_d = din("dt_bias", (16,))
    alog_d = din("a_log", (16,))
    gfin_d = din("g_final", (D,))

    y_p = dout("y_p", (SEQ, D))
    y_s = dout("y_s", (TS, D))
    o_plc = dout("o_plc", (3, D))
    o_plh = dout("o_plh", (8, 128))
    o_psc = dout("o_psc", (3, XBC))
    o_psh = dout("o_psh", (1024, 128))
    o_slc = dout("o_slc", (NS, 3, D))
    o_slh = dout("o_slh", (NS, D))
    o_ssc = dout("o_ssc", (NS, 3, XBC))
    o_ssh = dout("o_ssh", (NS, 1024, 128))
    scr = nc.dram_tensor("scr", [SEQ + TS, D], F32, kind=("ExternalOutput" if DBG else "Internal")).ap()

    st = ExitStack()
    with st:
        RW = 53200
        R = st.enter_context(nc.sbuf_tensor("R", [128, RW], F32))
        PS = st.enter_context(nc.psum_tensor("PS", [128, 4096], F32))
        ptr = [0]

        def alloc(nwords):
            a = ptr[0]
            ptr[0] += (nwords + 7) // 8 * 8
            pass
            return a

        def f32(n):
            a = alloc(n)
            return R[:, a:a + n]

        def bf(n):
            w = (n + 1) // 2
            a = alloc(w)
            return R[:, a:a + w].bitcast(BF16)[:, 0:n]

        def f3(c, t):
            return f32(c * t).rearrange("p (c t) -> p c t", c=c)

        def b3(c, t):
            return bf(c * t).rearrange("p (c t) -> p c t", c=c)

        def bank(b, n=512):
            return PS[:, 512 * b:512 * b + n]

        pT = PS[:, 0:1024]
        pTb = pT.bitcast(BF16)
        pA = [bank(2), bank(3)]
        pC = bank(4)
        pCb = pC.bitcast(BF16)
        pD = bank(5)
        pO = PS[:, 3072:4096]

        cst = f32(NCST)
        pfm = f32(NPAR)
        dtb_bc = f32(16)
        a_bc = f32(16)
        identb = bf(128)
        onesb = bf(128)
        Utrib = bf(128)
        mskb = bf(3 * TS)
        dah = bf(16)
        dal = bf(16)
        cfac = f32(8)
        c2fac = f32(8)
        tiny = f32(8)
        mhalf = f32(4)
        eps_t = f32(4)
        nbias = f32(16)
        wa_blk = b3(8, 128)
        wx_blk = b3(8, 128)
        hstate = f32(8)
        hT = f32(1024)
        hTb = bf(1024)

        ident = cst[:, CI:CI + 128]
        Utri = cst[:, CU:CU + 128]
        negm = cst[:, CN:CN + 128]
        onesf = cst[:, CO:CO + 128]
        Ublk = cst[0:TS, CUB:CUB + TS]
        negblk = cst[0:TS, CNB:CNB + TS]
        blkm = cst[0:TS, CBM:CBM + TS]
        blki = cst[0:TS, CBI:CBI + NS]

        S.dma(lambda e: e.dma_start(out=cst, in_=cst_d), "cst", w=["cst"])
        S.dma(lambda e: e.dma_start(out=pfm, in_=pfm_d), "pfm", w=["pfm"])
        S.dma(lambda e: e.dma_start(out=dtb_bc, in_=dtb_d.partition_broadcast(128)), "dtb", w=["dtb"])
        S.dma(lambda e: e.dma_start(out=a_bc, in_=alog_d.partition_broadcast(128)), "alog", w=["a_bc"])
        S.dve(lambda e: e.tensor_copy(out=identb, in_=ident), r=["cst"], w=["identb"])
        S.dve(lambda e: e.tensor_copy(out=onesb, in_=onesf), r=["cst"], w=["onesb"])
        S.dve(lambda e: e.tensor_copy(out=Utrib, in_=Utri), r=["cst"], w=["mskb"])
        S.dve(lambda e: e.tensor_copy(out=mskb[0:TS, 0:TS], in_=Ublk), r=["cst"], w=["mskb"])
        S.dve(lambda e: e.tensor_copy(out=mskb[0:TS, TS:2 * TS], in_=blkm), r=["cst"], w=["mskb"])
        S.pool(lambda e: e.memset(mhalf, -0.5), w=["mhalf"])
        S.pool(lambda e: e.memset(eps_t, EPS), w=["eps_t"])
        S.dve(lambda e: e.tensor_scalar(out=nbias[:, 0:8], in0=pfm[:, PC["BA"]:PC["BA"] + 8], scalar1=-1.0, scalar2=None, op0=ALU.mult), r=["pfm"], w=["nbias"])
        S.dve(lambda e: e.tensor_scalar(out=nbias[:, 8:16], in0=pfm[:, PC["BX"]:PC["BX"] + 8], scalar1=-1.0, scalar2=None, op0=ALU.mult), r=["pfm", "nbias"], w=["nbias"])
        S.pool(lambda e: e.memset(hstate, 0.0), w=["hstate"])
        S.pool(lambda e: e.memset(hT, 0.0), w=["hT"])
        S.pool(lambda e: e.memset(hTb, 0.0), w=["hTb"])
        S.pool(lambda e: e.memset(wa_blk, 0.0), w=["wa"])
        S.pool(lambda e: e.memset(wx_blk, 0.0), w=["wx"])
        S.act(lambda e: e.activation(out=a_bc, in_=a_bc, func=AF.Exp), r=["a_bc"], w=["a_bc"])
        S.dve(lambda e: e.tensor_scalar(out=a_bc, in0=a_bc, scalar1=-1.0, scalar2=None, op0=ALU.mult), r=["a_bc"], w=["a_bc"])
        lam = pfm[:, PC["LAM"]:PC["LAM"] + 8]
        S.act(lambda e: e.activation(out=tiny, in_=lam, func=AF.Exp, scale=-1.0), r=["pfm"], w=["tiny"])
        S.act(lambda e: e.activation(out=tiny, in_=tiny, func=AF.Ln, bias=1.0), r=["tiny"], w=["tiny"])
        S.dve(lambda e: e.tensor_scalar(out=cfac, in0=tiny, scalar1=-8.0, scalar2=None, op0=ALU.mult), r=["tiny"], w=["cfac"])
        S.dve(lambda e: e.tensor_scalar(out=c2fac, in0=tiny, scalar1=-16.0, scalar2=None, op0=ALU.mult), r=["tiny"], w=["cfac2"])
        for (wd, blk, nm) in ((w_a, wa_blk, "wa"), (w_x, wx_blk, "wx")):
            v = wd.rearrange("(c h) i j -> h i c j", h=2)
            for h2 in range(2):
                S.dma(lambda e, v=v, blk=blk, h2=h2: e.dma_start(
                    out=blk[64 * h2:64 * h2 + 64, :, 64 * h2:64 * h2 + 64], in_=v[h2]),
                    nm + str(h2), w=[nm], q="pool")

        base0 = ptr[0]

        def load_w(dst3, src2, nk, ncol, name, step=2048):
            sv = src2.rearrange("(k p) n -> p k n", p=128)
            pieces = [(k, c0, min(ncol, c0 + step)) for k in range(nk) for c0 in range(0, ncol, step)]
            for i, (k, c0, c1) in enumerate(pieces):
                S.dma(lambda e, k=k, c0=c0, c1=c1: e.dma_start(out=dst3[:, k, c0:c1], in_=sv[:, k, c0:c1]),
                      name, w=([name] if i == len(pieces) - 1 else []), q="pool")
            return name

        w_in_sb = b3(8, INP)
        w_out_sb = b3(16, D)
        if STAGE >= 1:
            k_win = load_w(w_in_sb, w_in, 8, INP, "w_in")
            k_wout = load_w(w_out_sb, w_out, 16, D, "w_out")

        xt = f32(D)
        xn = f32(D)
        junk = xn
        ss = f32(4)
        rstd = f32(4)
        hTt = b3(8, 128)
        lxb = f3(8, 131)
        xcb = f3(12, 131)
        lxs = f32(8 * NS * 7).rearrange("p (c s l) -> p c s l", c=8, s=NS)
        xcs = f32(12 * NS * 7).rearrange("p (c s l) -> p c s l", c=12, s=NS)
        gl = f3(2, 128)
        zs = f3(8, 128)
        u = f3(8, 128)
        ub = b3(8, 128)
        gi = f3(4, 128)
        av = f3(2, 128)
        a2 = f3(2, 128)
        tmpb = f3(2, 128)
        hs = f3(8, 128)
        ysq = b3(2, 128)
        rbc = f32(128)
        ynl = b3(8, 128)
        yns = b3(8, 128)
        xsf = f3(8, 128)
        Bb = b3(2, 128)
        Cb = b3(2, 128)
        dtr = f32(16)
        dtt = f32(16)
        da = f32(16)
        ncum = f32(16)
        dte = f32(16)
        cdec = f32(16)
        xdt = bf(1024)
        xdd = bf(1024)
        BT = bf(256)
        cbT = f3(2, 128)
        Dmf = f32(512)
        Emf = f32(512)
        Mmf = bf(512)
        Chf = bf(1024)
        Chp = Chf[:, 0:512].rearrange("p (a t) -> p a t", a=4)
        Chs = Chf.rearrange("p (a t) -> p a t", a=16)
        cvt = f3(2, 128)
        stg = f32(2560)
        stT = stg[:, 0:1024]
        lc_in = stg[:, 0:1024]
        sc_in = stg[:, 1024:2560]
        lh_in = stg[:, 0:1024]
        h0in = lxb.rearrange("p c t -> p (c t)")[:, 0:1024].rearrange("p (c t) -> p c t", c=8)
        h0Tb = hTb
        Bm = bf(256)
        pyo_sb = f3(8, TS)
        damb = bf(2 * NS * 16).rearrange("p (i s h) -> p i s h", i=2, s=NS)
        dtot = f3(NS, 16)
        hnew = hT
        hout = xcb.rearrange("p c t -> p (c t)")[:, 0:1024].rearrange("p (c t) -> p c t", c=8)
        h0s = f3(8, NS)
        hfin = f3(8, NS)

        def P(name, c=None, w=1):
            o = PC[name] + (0 if c is None else c * w)
            return pfm[:, o:o + w]

        def rms_rstd(xtile, T, keyx, junk, ss, rstd, sfx=""):
            S.act(lambda e: e.activation(out=junk[0:T, :], in_=xtile[0:T, :], func=AF.Square, accum_out=ss[0:T, 0:1]),
                  r=[keyx], w=["xn" + sfx, "ss" + sfx])
            S.act(lambda e: e.activation(out=ss[0:T, 0:1], in_=ss[0:T, 0:1], func=AF.Ln, scale=1.0 / D, bias=eps_t[0:T, 0:1]),
                  r=["ss" + sfx, "eps_t"], w=["ss" + sfx])
            S.act(lambda e: e.activation(out=rstd[0:T, 0:1], in_=ss[0:T, 0:1], func=AF.Exp, scale=-0.5),
                  r=["ss" + sfx], w=["rstd" + sfx])

        def to_fm(T, gname, dst, dkey):
            for k in range(8):
                S.pe(lambda e, k=k: e.transpose(out=pT[:, k * 128:k * 128 + T], in_=xn[0:T, k * 128:(k + 1) * 128],
                                                identity=ident[0:T, 0:T]), r=["xn", "cst"], w=["pT"])
            S.dve(lambda e: e.tensor_tensor(
                out=dst[:, :, 0:T], in0=pT.rearrange("p (k t) -> p k t", k=8)[:, :, 0:T],
                in1=P(gname, 0, 8).unsqueeze(2).to_broadcast([128, 8, T]), op=ALU.mult),
                r=["pT", "pfm"], w=[dkey])

        def mixer_tile(mt, samp):
            T = TS if samp else 128
            row0 = SEQ if samp else mt * 128
            xsrc = xs if samp else xp[mt * 128:(mt + 1) * 128, :]
            last = (not samp) and mt == NT - 1
            S.dma(lambda e: e.dma_start(out=xt[0:T, :], in_=xsrc), "xt", w=["xt"])
            rms_rstd(xt, T, "xt", junk, ss, rstd)
            S.act(lambda e: e.activation(out=xn[0:T, :], in_=xt[0:T, :], func=AF.Copy, scale=rstd[0:T, 0:1]),
                  r=["xt", "rstd"], w=["xn"])
            to_fm(T, "GM", hTt, "hTt")

            if samp:
                S.dma(lambda e: e.dma_start(out=lc_in[0:48, :], in_=st_lc), "stg", w=["stg"])
                S.dma(lambda e: e.dma_start(out=sc_in[0:48, :], in_=st_sc), "stg", w=["stg"])
                S.dma(lambda e: e.dma_start(out=lh_in[64:64 + NS, :], in_=st_lh), "stg", w=["stg"])
                for c in range(8):
                    S.pe(lambda e, c=c: e.transpose(out=pC[:, 0:48], in_=lc_in[0:48, c * 128:(c + 1) * 128],
                                                    identity=ident[0:48, 0:48]), r=["stg", "cst"], w=["pC"])
                    S.act(lambda e, c=c: e.activation(out=lxs[:, c, :, 0:3],
                                                      in_=pC[:, 0:48].rearrange("p (s j) -> p s j", s=NS),
                                                      func=AF.Copy), r=["pC"], w=["lx%d" % c])
                    S.pe(lambda e, c=c: e.transpose(out=pD[:, 0:NS], in_=lh_in[64:64 + NS, c * 128:(c + 1) * 128],
                                                    identity=ident[64:64 + NS, 64:64 + NS]), r=["stg", "cst"], w=["pD"])
                    S.dve(lambda e, c=c: e.tensor_copy(out=h0s[:, c, :], in_=pD[:, 0:NS]), r=["pD"], w=["h0s"])
                for c in range(12):
                    S.pe(lambda e, c=c: e.transpose(out=pC[:, 0:48], in_=sc_in[0:48, c * 128:(c + 1) * 128],
                                                    identity=ident[0:48, 0:48]), r=["stg", "cst"], w=["pC"])
                    S.act(lambda e, c=c: e.activation(out=xcs[:, c, :, 0:3],
                                                      in_=pC[:, 0:48].rearrange("p (s j) -> p s j", s=NS),
                                                      func=AF.Copy), r=["pC"], w=["xc%d" % c])

            pcnt = [0]

            def proj(ci):
                i = pcnt[0] % 2
                pcnt[0] += 1
                pa = pA[i]
                for k in range(8):
                    S.pe(lambda e, k=k: e.matmul(pa[:, 0:T], lhsT=w_in_sb[:, k, ci * 128:(ci + 1) * 128],
                                                 rhs=hTt[:, k, 0:T], start=(k == 0), stop=(k == 7)),
                         r=["hTt", "w_in"], w=["pA%d" % i])
                return pa, "pA%d" % i

            def new_cols(buf, sbuf_, c):
                if samp:
                    return sbuf_[:, c, :, 3:7]
                return buf[:, c, 3:131]

            def pa_view(pa):
                if samp:
                    return pa[:, 0:T].rearrange("p (s l) -> p s l", s=NS)
                return pa[:, 0:T]

            def tap(buf, sbuf_, c, k):
                if samp:
                    return sbuf_[:, c, :, k:k + 4]
                return buf[:, c, k:k + 128]

            def fm(t3, c):
                if samp:
                    return t3[:, c, 0:T].rearrange("p (s l) -> p s l", s=NS)
                return t3[:, c, 0:T]

            def conv(buf, sbuf_, c, wname, bname, out_ap, key_in, key_out):
                S.dve(lambda e: e.tensor_scalar(out=out_ap, in0=tap(buf, sbuf_, c, 3), scalar1=P(wname, c, 4)[:, 3:4],
                                                scalar2=P(bname, c), op0=ALU.mult, op1=ALU.add),
                      r=[key_in, "pfm"], w=[key_out])
                for k in (2, 1, 0):
                    S.dve(lambda e, k=k: e.scalar_tensor_tensor(out=out_ap, in0=tap(buf, sbuf_, c, k),
                                                                scalar=P(wname, c, 4)[:, k:k + 1], in1=out_ap,
                                                                op0=ALU.mult, op1=ALU.add),
                          r=[key_in, key_out, "pfm"], w=[key_out])
                if not samp:
                    S.dve(lambda e: e.tensor_copy(out=buf[:, c, 0:3], in_=buf[:, c, 128:131]), r=[key_in], w=[key_in])

            def g_lrux():
                for c in range(8):
                    pa, pk = proj(c)
                    S.act(lambda e, c=c, pa=pa: e.activation(out=new_cols(lxb, lxs, c), in_=pa_view(pa), func=AF.Copy),
                          r=[pk], w=["lx%d" % c])
                    conv(lxb, lxs, c, "LW", "LB", fm(u, c), "lx%d" % c, "u%d" % c)
                    yield

            def g_z():
                for c in range(8):
                    pa, pk = proj(16 + c)
                    S.act(lambda e, c=c, pa=pa: e.activation(out=zs[:, c, 0:T], in_=pa[:, 0:T], func=AF.Silu),
                          r=[pk], w=["zs%d" % c])
                    yield

            def g_xbc():
                for c in range(12):
                    pa, pk = proj(24 + c)
                    S.act(lambda e, c=c, pa=pa: e.activation(out=new_cols(xcb, xcs, c), in_=pa_view(pa), func=AF.Copy),
                          r=[pk], w=["xc%d" % c])
                    if c < 8:
                        conv(xcb, xcs, c, "SW", "SB", fm(cvt, c % 2), "xc%d" % c, "cv_t%d" % (c % 2))
                        S.act(lambda e, c=c: e.activation(out=xsf[:, c, 0:T], in_=cvt[:, c % 2, 0:T], func=AF.Silu),
                              r=["cv_t%d" % (c % 2)], w=["xsf%d" % c])
                    else:
                        g = (c - 8) % 2
                        dstb = Bb if c < 10 else Cb
                        nm = ("Bb%d" if c < 10 else "Cb%d") % g
                        conv(xcb, xcs, c, "SW", "SB", fm(cvt, g), "xc%d" % c, "cv_t%d" % g)
                        S.act(lambda e, g=g, dstb=dstb: e.activation(out=dstb[:, g, 0:T], in_=cvt[:, g, 0:T], func=AF.Silu),
                              r=["cv_t%d" % g], w=[nm])
                    yield
                for k in range(8):
                    S.pe(lambda e, k=k: e.matmul(pD[0:T, 0:16], lhsT=hTt[:, k, 0:T], rhs=w_in_sb[:, k, 4608:4624],
                                                 start=(k == 0), stop=(k == 7)),
                         r=["hTt", "w_in"], w=["pD"])
                S.dve(lambda e: e.tensor_tensor(out=dtr[0:T, :], in0=pD[0:T, 0:16], in1=dtb_bc[0:T, :], op=ALU.add),
                      r=["pD", "dtb"], w=["dtr"])
                S.act(lambda e: e.activation(out=dtr[0:T, :], in_=dtr[0:T, :], func=AF.Exp), r=["dtr"], w=["dtr"])
                S.act(lambda e: e.activation(out=dtt[0:T, :], in_=dtr[0:T, :], func=AF.Ln, bias=1.0), r=["dtr"], w=["dtt"])
                yield

            def g_lru(chunks, pg, kr, ki):
                for c in chunks:
                    pp = c % 2
                    S.act(lambda e, c=c: e.activation(out=ub[:, c, 0:T], in_=u[:, c, 0:T], func=AF.Copy),
                          r=["u%d" % c], w=["ub%d" % c])
                    yield
                    S.pe(lambda e, c=c: e.matmul(pg[:, 0:T], lhsT=wa_blk[:, c, :], rhs=ub[:, c, 0:T], start=True, stop=True),
                         r=["ub%d" % c, "wa"], w=[kr])
                    S.pe(lambda e, c=c: e.matmul(pg[:, 128:128 + T], lhsT=wx_blk[:, c, :], rhs=ub[:, c, 0:T], start=True, stop=True),
                         r=["ub%d" % c, "wx"], w=[ki])
                    yield
                    S.act(lambda e, c=c, pp=pp: e.activation(out=gi[:, 2 * pp, 0:T], in_=pg[:, 0:T], func=AF.Exp, scale=-1.0, bias=nbias[:, c:c + 1]),
                          r=[kr, "nbias"], w=["rg%d" % pp])
                    S.act(lambda e, c=c, pp=pp: e.activation(out=gi[:, 2 * pp + 1, 0:T], in_=pg[:, 128:128 + T], func=AF.Exp, scale=-1.0, bias=nbias[:, 8 + c:9 + c]),
                          r=[ki, "nbias"], w=["ig%d" % pp])
                    S.act(lambda e, pp=pp: e.activation(out=gi[:, 2 * pp:2 * pp + 2, 0:T], in_=gi[:, 2 * pp:2 * pp + 2, 0:T], func=AF.Ln, bias=1.0),
                          r=["rg%d" % pp, "ig%d" % pp], w=["rg%d" % pp, "ig%d" % pp])
                    S.act(lambda e, pp=pp: e.activation(out=gi[:, 2 * pp:2 * pp + 2, 0:T], in_=gi[:, 2 * pp:2 * pp + 2, 0:T], func=AF.Exp, scale=-1.0),
                          r=["rg%d" % pp, "ig%d" % pp], w=["rg%d" % pp, "ig%d" % pp])
                    S.act(lambda e, c=c, pp=pp: e.activation(out=av[:, pp, 0:T], in_=gi[:, 2 * pp, 0:T], func=AF.Exp, scale=cfac[:, c:c + 1]),
                          r=["rg%d" % pp, "cfac"], w=["av%d" % pp])
                    S.act(lambda e, c=c, pp=pp: e.activation(out=a2[:, pp, 0:T], in_=gi[:, 2 * pp, 0:T], func=AF.Exp, scale=c2fac[:, c:c + 1]),
                          r=["rg%d" % pp, "cfac2"], w=["a2%d" % pp])
                    S.act(lambda e, pp=pp: e.activation(out=a2[:, pp, 0:T], in_=a2[:, pp, 0:T], func=AF.Ln, scale=-1.0, bias=1.0),
                          r=["a2%d" % pp], w=["a2%d" % pp])
                    S.act(lambda e, pp=pp: e.activation(out=a2[:, pp, 0:T], in_=a2[:, pp, 0:T], func=AF.Exp, scale=0.5),
                          r=["a2%d" % pp], w=["a2%d" % pp])
                    yield
                    S.dve(lambda e, c=c, pp=pp: e.tensor_tensor(out=tmpb[:, pp, 0:T], in0=gi[:, 2 * pp + 1, 0:T], in1=u[:, c, 0:T], op=ALU.mult),
                          r=["ig%d" % pp, "u%d" % c], w=["tb%d" % pp])
                    S.dve(lambda e, pp=pp: e.tensor_tensor(out=tmpb[:, pp, 0:T], in0=tmpb[:, pp, 0:T], in1=a2[:, pp, 0:T], op=ALU.mult),
                          r=["tb%d" % pp, "a2%d" % pp], w=["tb%d" % pp])
                    if samp:
                        a3 = av[:, pp, 0:T].rearrange("p (s l) -> p s l", s=NS)
                        b3v = tmpb[:, pp, 0:T].rearrange("p (s l) -> p s l", s=NS)
                        S.dve(lambda e, c=c, a3=a3: e.tensor_tensor(out=rbc[:, 0:NS], in0=a3[:, :, 0], in1=h0s[:, c, :], op=ALU.mult),
                              r=["av%d" % pp, "h0s"], w=["rbc"])
                        S.dve(lambda e, b3v=b3v: e.tensor_tensor(out=b3v[:, :, 0], in0=b3v[:, :, 0], in1=rbc[:, 0:NS], op=ALU.add),
                              r=["tb%d" % pp, "rbc"], w=["tb%d" % pp])
                        S.dve(lambda e, a3=a3: e.memset(a3[:, :, 0], 0.0), r=["rbc"], w=["av%d" % pp])
                        S.dve(lambda e, c=c, pp=pp: e.tensor_tensor_scan(out=hs[:, c, 0:T], data0=av[:, pp, 0:T], data1=tmpb[:, pp, 0:T],
                                                                         initial=0.0, op0=ALU.mult, op1=ALU.add),
                              r=["av%d" % pp, "tb%d" % pp], w=["hs%d" % c])
                        S.dve(lambda e, c=c: e.tensor_copy(out=hfin[:, c, :], in_=hs[:, c, 0:T].rearrange("p (s l) -> p s l", s=NS)[:, :, 3]),
                              r=["hs%d" % c], w=["hfin"])
                    else:
                        S.dve(lambda e, c=c, pp=pp: e.tensor_tensor_scan(out=hs[:, c, 0:T], data0=av[:, pp, 0:T], data1=tmpb[:, pp, 0:T],
                                                                         initial=hstate[:, c:c + 1], op0=ALU.mult, op1=ALU.add),
                              r=["av%d" % pp, "tb%d" % pp, "hstate"], w=["hs%d" % c])
                        S.dve(lambda e, c=c: e.tensor_copy(out=hstate[:, c:c + 1], in_=hs[:, c, T - 1:T]),
                              r=["hs%d" % c], w=["hstate"])
                    yield

            def g_gate():
                for c in range(8):
                    pp = c % 2
                    pa, pk = proj(8 + c)
                    S.act(lambda e, pp=pp, pa=pa: e.activation(out=gl[:, pp, 0:T], in_=pa[:, 0:T], func=AF.Gelu_apprx_tanh),
                          r=[pk], w=["gl%d" % pp])
                    S.dve(lambda e, c=c, pp=pp: e.tensor_tensor(out=hs[:, c, 0:T], in0=hs[:, c, 0:T], in1=gl[:, pp, 0:T], op=ALU.mult),
                          r=["hs%d" % c, "gl%d" % pp], w=["yl%d" % c, "hs%d" % c])
                    S.act(lambda e, c=c, pp=pp: e.activation(out=ysq[:, pp, 0:T], in_=hs[:, c, 0:T], func=AF.Square),
                          r=["yl%d" % c], w=["ysq%d" % pp])
                    S.pe(lambda e, c=c, pp=pp: e.matmul(pD[:, 128:128 + T], lhsT=onesb, rhs=ysq[:, pp, 0:T], start=(c == 0), stop=(c == 7)),
                         r=["ysq%d" % pp, "onesb"], w=["pDn"])
                    yield

            def interleave(*gens):
                gens = list(gens)
                while gens:
                    for g_ in list(gens):
                        try:
                            next(g_)
                        except StopIteration:
                            gens.remove(g_)

            _ord = "A"
            interleave(g_lrux(), g_z())
            interleave(g_xbc())
            if _ord in ("A", "B"):
                interleave(g_lru((0, 2, 4, 6), pC, "pC", "pCx"), g_lru((1, 3, 5, 7), pD, "pD", "pD2"))
            else:
                interleave(g_lru(range(8), pC, "pC", "pCx"))

            M = T if samp else 3
            t0 = 0 if samp else 125
            if samp or last:
                for blk, col0 in enumerate((0, 512, 3072, 3584, 4096)):
                    for k in range(8):
                        S.pe(lambda e, k=k, col0=col0: e.matmul(pO[0:M, 0:512], lhsT=hTt[:, k, t0:t0 + M],
                                                                rhs=w_in_sb[:, k, col0:col0 + 512], start=(k == 0), stop=(k == 7)),
                             r=["hTt", "w_in"], w=["pO"])
                    S.dve(lambda e, blk=blk: e.tensor_copy(out=stg[0:M, blk * 512:(blk + 1) * 512], in_=pO[0:M, 0:512]),
                          r=["pO"], w=["stg"])
            if last:
                S.dma(lambda e: e.dma_start(out=o_plc, in_=stg[0:3, 0:1024]), "o_plc", r=["stg"])
                S.dma(lambda e: e.dma_start(out=o_psc, in_=stg[0:3, 1024:2560]), "o_psc", r=["stg"])
            if samp:
                for s in range(NS):
                    S.dma(lambda e, s=s: e.dma_start(out=o_slc[s], in_=stg[4 * s + 1:4 * s + 4, 0:1024]), "o_slc", r=["stg"])
                    S.dma(lambda e, s=s: e.dma_start(out=o_ssc[s], in_=stg[4 * s + 1:4 * s + 4, 1024:2560]), "o_ssc", r=["stg"])
            if last:
                S.pe(lambda e: e.transpose(out=pC[0:8, 0:128], in_=hstate, identity=ident), r=["hstate", "cst"], w=["pC", "pCx"])
                S.act(lambda e: e.activation(out=stT[0:8, 0:128], in_=pC[0:8, 0:128], func=AF.Copy), r=["pC"], w=["stg"])
                S.dma(lambda e: e.dma_start(out=o_plh, in_=stT[0:8, 0:128]), "o_plh", r=["stg"])
            if samp:
                for c in range(8):
                    S.pe(lambda e, c=c: e.transpose(out=pT[0:NS, c * 128:(c + 1) * 128], in_=hfin[:, c, :], identity=ident),
                         r=["hfin", "cst"], w=["pT"])
                S.act(lambda e: e.activation(out=lh_in[0:NS, :], in_=pT[0:NS, :], func=AF.Copy), r=["pT"], w=["stg"])
                S.dma(lambda e: e.dma_start(out=o_slh, in_=lh_in[0:NS, :]), "o_slh", r=["stg"])

            def norm_apply(T, eps_, src, skey, gname, dst, dkey, c0, c1):
                S.act(lambda e: e.activation(out=rbc[:, 0:T], in_=pD[:, 128:128 + T], func=AF.Ln,
                                             scale=1.0 / ((c1 - c0) * 128), bias=eps_t[:, 0:1]),
                      r=["pDn", "eps_t"], w=["rbc"])
                S.act(lambda e: e.activation(out=rbc[:, 0:T], in_=rbc[:, 0:T], func=AF.Exp, scale=-0.5), r=["rbc"], w=["rbc"])
                for c in range(c0, c1):
                    S.dve(lambda e, c=c: e.scalar_tensor_tensor(out=dst[:, c, 0:T], in0=src[:, c, 0:T], scalar=P(gname, c),
                                                                in1=rbc[:, 0:T], op0=ALU.mult, op1=ALU.mult),
                          r=[skey % c, "rbc", "pfm"], w=[dkey % c])

            Um = mskb[0:TS, 0:TS] if samp else Utrib
            ngm = negblk if samp else negm
            allm = mskb[0:TS, TS:2 * TS] if samp else onesb
            d4 = lambda ap: ap[:, 0:4 * T].rearrange("p (a t) -> p a t", a=4)
            Em, Dm, Mm, pC4 = d4(Emf), d4(Dmf), d4(Mmf), d4(pC)

            def g_ssd():
                for c in range(8):
                    S.pe(lambda e, c=c: e.transpose(out=pT[0:T, c * 128:(c + 1) * 128], in_=xsf[:, c, 0:T], identity=ident),
                         r=["xsf%d" % c, "cst"], w=["pT"])
                for g in range(2):
                    S.pe(lambda e, g=g: e.transpose(out=pCb[0:T, 128 + g * 128:128 + (g + 1) * 128], in_=Bb[:, g, 0:T], identity=identb),
                         r=["Bb%d" % g, "identb"], w=["pC", "pCx"])
                S.dve(lambda e: e.tensor_tensor(out=xdt[0:T, :].rearrange("p (h q) -> p h q", h=16),
                                                in0=pT[0:T, :].rearrange("p (h q) -> p h q", h=16),
                                                in1=dtt[0:T, :].unsqueeze(2).to_broadcast([T, 16, 64]), op=ALU.mult),
                      r=["pT", "dtt"], w=["xdt"])
                S.dve(lambda e: e.tensor_copy(out=BT[0:T, :], in_=pCb[0:T, 128:384]), r=["pC"], w=["BT"])
                S.dve(lambda e: e.tensor_tensor(out=da[0:T, :], in0=dtt[0:T, :], in1=a_bc[0:T, :], op=ALU.mult),
                      r=["dtt", "a_bc"], w=["da"])
                yield
                S.dve(lambda e: e.tensor_copy(out=dah[0:T, :], in_=da[0:T, :]), r=["da"], w=["dah"])
                S.dve(lambda e: e.tensor_tensor(out=dal[0:T, :], in0=da[0:T, :], in1=dah[0:T, :], op=ALU.subtract),
                      r=["da", "dah"], w=["dal"])
                for i, dx in enumerate((dah, dal)):
                    S.pe(lambda e, dx=dx, i=i: e.matmul(pC[0:T, 0:16], lhsT=Um[0:T, 0:T], rhs=dx[0:T, :], start=(i == 0), stop=(i == 1)),
                         r=["dah", "dal", "mskb"], w=["pC", "pCx"])
                for i, dx in enumerate((dah, dal)):
                    S.pe(lambda e, dx=dx, i=i: e.matmul(pC[0:T, 16:32], lhsT=allm[0:T, 0:T], rhs=dx[0:T, :], start=(i == 0), stop=(i == 1)),
                         r=["dah", "dal", "mskb", "onesb"], w=["pC", "pCx"])
                if not samp:
                    for i, dx in enumerate((dah, dal)):
                        S.pe(lambda e, dx=dx, i=i: e.matmul(pC[:, 32:48], lhsT=onesb, rhs=dx, start=(i == 0), stop=(i == 1)),
                             r=["dah", "dal", "onesb"], w=["pC", "pCx"])
                for g in range(2):
                    S.pe(lambda e, g=g: e.matmul(pC[0:T, 256 + g * 128:256 + g * 128 + T], lhsT=Bb[:, g, 0:T], rhs=Cb[:, g, 0:T],
                                                 start=True, stop=True), r=["Bb%d" % g, "Cb%d" % g], w=["pC", "pCx"])
                yield
                S.dve(lambda e: e.tensor_scalar(out=ncum[0:T, :], in0=pC[0:T, 0:16], scalar1=-1.0, scalar2=None, op0=ALU.mult),
                      r=["pC"], w=["ncum"])
                S.dve(lambda e: e.tensor_tensor(out=dte[0:T, :], in0=pC[0:T, 16:32], in1=ncum[0:T, :], op=ALU.add),
                      r=["pC", "ncum"], w=["dte"])
                if not samp:
                    S.dve(lambda e: e.tensor_copy(out=cdec, in_=pC[:, 32:48]), r=["pC"], w=["cdec"])
                S.dve(lambda e: e.tensor_copy(out=cbT[0:T, :, 0:T], in_=pC[0:T, 256:512].rearrange("p (g t) -> p g t", g=2)[:, :, 0:T]),
                      r=["pC"], w=["cbT0", "cbT1"])
                S.act(lambda e: e.activation(out=dte[0:T, :], in_=dte[0:T, :], func=AF.Exp), r=["dte"], w=["dte"])
                if not samp:
                    S.act(lambda e: e.activation(out=cdec, in_=cdec, func=AF.Exp), r=["cdec"], w=["cdec"])
                S.dve(lambda e: e.tensor_tensor(out=xdd[0:T, :].rearrange("p (h q) -> p h q", h=16),
                                                in0=xdt[0:T, :].rearrange("p (h q) -> p h q", h=16),
                                                in1=dte[0:T, :].unsqueeze(2).to_broadcast([T, 16, 64]), op=ALU.mult),
                      r=["xdt", "dte"], w=["xdd"])
                yield
                if not samp:
                    for g in range(2):
                        S.pe(lambda e, g=g: e.matmul(pO[:, g * 512:(g + 1) * 512], lhsT=BT[:, g * 128:(g + 1) * 128],
                                                     rhs=xdd[:, g * 512:(g + 1) * 512], start=True, stop=True),
                             r=["BT", "xdd"], w=["pO"])
                    S.dve(lambda e: e.tensor_tensor(out=hT.rearrange("p (h q) -> p h q", h=16),
                                                    in0=hT.rearrange("p (h q) -> p h q", h=16),
                                                    in1=cdec.unsqueeze(2).to_broadcast([128, 16, 64]), op=ALU.mult),
                          r=["hT", "cdec"], w=["hT"])
                    S.dve(lambda e: e.tensor_tensor(out=hT, in0=hT, in1=pO, op=ALU.add), r=["hT", "pO"], w=["hT"])
                    yield
                for q4 in range(4):
                    g = q4 // 2
                    Wl = cvt.rearrange("p a t -> p (a t)").bitcast(BF16)
                    for i, (dx, Wf, wk) in enumerate(((dah, Mmf, ["Mm"]), (dal, Wl, ["cv_t0", "cv_t1"]))):
                        S.dve(lambda e, q4=q4, dx=dx, Wf=Wf: e.tensor_tensor(out=d4(Wf)[0:T], in0=Um[0:T, 0:T].unsqueeze(1).to_broadcast([T, 4, T]),
                                                                             in1=dx[0:T, q4 * 4:q4 * 4 + 4].unsqueeze(2).to_broadcast([T, 4, T]),
                                                                             op=ALU.mult), r=["dah", "dal", "mskb"], w=wk)
                        S.pe(lambda e, Wf=Wf, i=i: e.matmul(pC[:, 0:4 * T], lhsT=onesb[0:T, :], rhs=Wf[0:T, 0:4 * T],
                                                            start=(i == 0), stop=(i == 1)), r=wk + ["onesb"], w=["pC", "pCx"])
                    S.dve(lambda e: e.tensor_copy(out=Em, in_=pC4), r=["pC"], w=["Em"])
                    yield
                    for hh in range(4):
                        h = q4 * 4 + hh
                        S.dve(lambda e, h=h, hh=hh: e.scalar_tensor_tensor(out=Dm[0:T, hh, :], in0=Em[0:T, hh, :],
                                                                           scalar=ncum[0:T, h:h + 1], in1=ngm[0:T, 0:T],
                                                                           op0=ALU.add, op1=ALU.add),
                              r=["Em", "ncum", "cst"], w=["Dm"])
                    S.act(lambda e: e.activation(out=Em, in_=Em, func=AF.Exp), r=["Em"], w=["Em"])
                    S.act(lambda e: e.activation(out=Dm[0:T], in_=Dm[0:T], func=AF.Exp), r=["Dm"], w=["Dm"])
                    S.dve(lambda e, g=g: e.tensor_tensor(out=Mm[0:T], in0=Dm[0:T],
                                                         in1=cbT[0:T, g, 0:T].unsqueeze(1).to_broadcast([T, 4, T]), op=ALU.mult),
                          r=["Dm", "cbT%d" % g], w=["Mm"])
                    S.dve(lambda e, g=g, q4=q4: e.tensor_tensor(out=(Chs[:, q4 * 4:q4 * 4 + 4, :] if samp else Chp), in0=Em,
                                                                in1=Cb[:, g, 0:T].unsqueeze(1).to_broadcast([128, 4, T]), op=ALU.mult),
                          r=["Em", "Cb%d" % g], w=["Ch"])
                    yield
                    for hh in range(4):
                        h = q4 * 4 + hh
                        c = h // 2
                        h2 = h % 2
                        po = pT[64 * h2:64 * h2 + 64, c * 128:c * 128 + T]
                        S.pe(lambda e, h=h, hh=hh, po=po: e.matmul(po, lhsT=xdt[0:T, h * 64:(h + 1) * 64], rhs=Mm[0:T, hh, :],
                                                                   start=True, stop=samp), r=["xdt", "Mm"], w=["pT"])
                        if not samp:
                            S.pe(lambda e, h=h, hh=hh, po=po: e.matmul(po, lhsT=hTb[:, h * 64:(h + 1) * 64], rhs=Chp[:, hh, :],
                                                                       start=False, stop=True), r=["hTb", "Ch"], w=["pT"])
                    yield

            if _ord in ("A", "C"):
                interleave(g_gate(), g_ssd())
            else:
                interleave(g_gate())
                interleave(g_ssd())
            norm_apply(T, EPS, hs, "yl%d", "GL", ynl, "ynl%d", 0, 8)
            if samp:
                ssd_sample_states_prep()
            for c in range(8):
                S.dve(lambda e, c=c: e.scalar_tensor_tensor(out=xsf[:, c, 0:T], in0=xsf[:, c, 0:T], scalar=P("DS", c),
                                                            in1=pT[:, c * 128:c * 128 + T], op0=ALU.mult, op1=ALU.add),
                      r=["pT", "xsf%d" % c, "pfm"], w=["xsf%d" % c])
            if samp:
                S.dve(lambda e: e.tensor_tensor(out=xsf[:, :, 0:T], in0=xsf[:, :, 0:T], in1=pyo_sb, op=ALU.add),
                      r=["xsf%d" % c for c in range(8)] + ["pyo_sb"], w=["xsf%d" % c for c in range(8)])
            S.pool(lambda e: e.tensor_tensor(out=xsf[:, :, 0:T], in0=xsf[:, :, 0:T], in1=zs[:, :, 0:T], op=ALU.mult),
                   r=["xsf%d" % c for c in range(8)] + ["zs%d" % c for c in range(8)], w=["yg%d" % c for c in range(8)] + ["xsf%d" % c for c in range(8)])
            if not samp:
                S.act(lambda e: e.activation(out=hTb, in_=hT, func=AF.Copy), r=["hT"], w=["hTb"])
                if last:
                    for c in range(8):
                        S.pe(lambda e, c=c: e.transpose(out=pO[:, c * 128:(c + 1) * 128], in_=hT[:, c * 128:(c + 1) * 128], identity=ident),
                             r=["hT", "cst"], w=["pO"])
                    S.dve(lambda e: e.tensor_copy(out=stT, in_=pO), r=["pO"], w=["stg"])
                    S.dma(lambda e: e.dma_start(out=o_psh.rearrange("(c q) n -> q c n", q=128),
                                                in_=stT.rearrange("p (c n) -> p c n", c=8)), "o_psh", r=["stg"])
            for g in range(2):
                for c in range(4 * g, 4 * g + 4):
                    pp = c % 2
                    S.act(lambda e, c=c, pp=pp: e.activation(out=ysq[:, pp, 0:T], in_=xsf[:, c, 0:T], func=AF.Square),
                          r=["yg%d" % c], w=["ysq%d" % pp])
                    S.pe(lambda e, c=c, pp=pp, g=g: e.matmul(pD[:, 128:128 + T], lhsT=onesb, rhs=ysq[:, pp, 0:T],
                                                             start=(c == 4 * g), stop=(c == 4 * g + 3)),
                         r=["ysq%d" % pp, "onesb"], w=["pDn"])
                norm_apply(T, EPS, xsf, "yg%d", "GS", yns, "yns%d", 4 * g, 4 * g + 4)

            for nb in range(2):
                for kc in range(16):
                    src = ynl if kc < 8 else yns
                    S.pe(lambda e, kc=kc, nb=nb, src=src: e.matmul(pO[0:T, nb * 512:(nb + 1) * 512], lhsT=src[:, kc % 8, 0:T],
                                                                   rhs=w_out_sb[:, kc, nb * 512:(nb + 1) * 512],
                                                                   start=(kc == 0), stop=(kc == 15)),
                         r=[("ynl%d" if kc < 8 else "yns%d") % (kc % 8), "w_out"], w=["pO"])
            S.dve(lambda e: e.tensor_tensor(out=xt[0:T, :], in0=pO[0:T, :], in1=xt[0:T, :], op=ALU.add),
                  r=["pO", "xt"], w=["xt"])
            S.dma(lambda e: e.dma_start(out=scr[row0:row0 + T, :], in_=xt[0:T, :]), "xnew", r=["xt"], w=["scr%d" % mt])

        def ssd_sample_states_prep():
            T = TS
            for i, dx in enumerate((dah, dal)):
                S.dve(lambda e, dx=dx, i=i: e.tensor_tensor(out=damb[0:T, i], in0=dx[0:T, :].unsqueeze(1).to_broadcast([T, NS, 16]),
                                                            in1=blki.unsqueeze(2).to_broadcast([T, NS, 16]), op=ALU.mult),
                      r=["dah", "dal", "cst"], w=["dam%d" % i])
                S.pe(lambda e, i=i: e.matmul(pD[:, 0:256], lhsT=onesb[0:T, :], rhs=damb[0:T, i].rearrange("p s h -> p (s h)"),
                                             start=(i == 0), stop=(i == 1)), r=["dam%d" % i, "onesb"], w=["pD", "pD2", "pD3", "pDn"])
            S.act(lambda e: e.activation(out=dtot.rearrange("p s h -> p (s h)"), in_=pD[:, 0:256], func=AF.Exp),
                  r=["pD"], w=["dtot"])
            for s in range(NS):
                S.dma(lambda e, s=s: e.dma_start(out=h0in, in_=st_sh[s].rearrange("(c q) n -> q c n", q=128)), "h0in", w=["h0in"])
                for c in range(8):
                    S.pe(lambda e, c=c: e.transpose(out=pO[:, c * 128:(c + 1) * 128], in_=h0in[:, c, :], identity=ident),
                         r=["h0in", "cst"], w=["pO"])
                S.act(lambda e: e.activation(out=h0Tb, in_=pO, func=AF.Copy), r=["pO"], w=["h0Tb"])
                for h in range(16):
                    S.pe(lambda e, h=h, s=s: e.matmul(pA[0][64 * (h % 2):64 * (h % 2) + 64, (h // 2) * TS + 4 * s:(h // 2) * TS + 4 * s + 4],
                                                      lhsT=h0Tb[:, h * 64:(h + 1) * 64], rhs=Chs[:, h, 4 * s:4 * s + 4],
                                                      start=True, stop=True),
                         r=["h0Tb", "Ch"], w=["pA0"])
                S.pool(lambda e, s=s: e.tensor_scalar(out=Bm[0:T, :], in0=BT[0:T, :], scalar1=blki[:, s:s + 1], scalar2=None, op0=ALU.mult),
                       r=["BT", "cst"], w=["Bm"])
                S.dve(lambda e, s=s: e.tensor_tensor(out=hnew.rearrange("p (h q) -> p h q", h=16),
                                                     in0=pO.rearrange("p (h q) -> p h q", h=16),
                                                     in1=dtot[:, s, :].unsqueeze(2).to_broadcast([128, 16, 64]), op=ALU.mult),
                      r=["pO", "dtot"], w=["hnew"])
                for g in range(2):
                    S.pe(lambda e, g=g, s=s: e.matmul(pO[:, g * 512:(g + 1) * 512], lhsT=Bm[0:T, g * 128:(g + 1) * 128],
                                                      rhs=xdd[0:T, g * 512:(g + 1) * 512], start=True, stop=True),
                         r=["Bm", "xdd"], w=["pO"])
                S.dve(lambda e: e.tensor_tensor(out=hnew, in0=hnew, in1=pO, op=ALU.add), r=["hnew", "pO"], w=["hnew"])
                for c in range(8):
                    S.pe(lambda e, c=c: e.transpose(out=pO[:, c * 128:(c + 1) * 128], in_=hnew[:, c * 128:(c + 1) * 128], identity=ident),
                         r=["hnew", "cst"], w=["pO"])
                S.act(lambda e: e.activation(out=hout.rearrange("p c n -> p (c n)"), in_=pO, func=AF.Copy), r=["pO"], w=["hout"])
                S.dma(lambda e, s=s: e.dma_start(out=o_ssh[s].rearrange("(c q) n -> q c n", q=128), in_=hout), "hout", r=["hout"])
            S.act(lambda e: e.activation(out=pyo_sb.rearrange("p c t -> p (c t)"), in_=pA[0][:, 0:8 * TS], func=AF.Copy),
                  r=["pA0"], w=["pyo_sb"])

        S.pool(lambda e: e.memset(lxb, 0.0), w=["lx%d" % c for c in range(8)])
        S.pool(lambda e: e.memset(xcb, 0.0), w=["xc%d" % c for c in range(12)])

        for mt in range(NT):
            mixer_tile(mt, False)
        S.barrier()
        if SAMP:
            mixer_tile(SEQ // 128, True)

        S.barrier()
        ptr[0] = base0
        w_up_sb = b3(8, DFF)
        w_dn_sb = b3(32, D)
        if MLP:
            k_wup = load_w(w_up_sb, w_up, 8, DFF, "w_up")
            k_wdn = load_w(w_dn_sb, w_down, 32, D, "w_dn")
        T2 = 256
        xt2 = [f32(D), f32(D)]
        xn2 = f32(D)
        junk2 = xn2
        ss2 = f32(4)
        rstd2 = f32(4)
        mT = b3(8, T2)
        actb = b3(32, T2)
        rl = [f32(T2), f32(T2)]
        yout = f32(D)
        gfin_bc = f32(D)
        S.dma(lambda e: e.dma_start(out=gfin_bc, in_=gfin_d.partition_broadcast(128)), "gfin", w=["gfin"])

        def mlp_tile(r0, T):
            nsub = (T + 127) // 128
            for j in range(nsub):
                Tj = min(128, T - j * 128)
                S.dma(lambda e, j=j, Tj=Tj: e.dma_start(out=xt2[j][0:Tj, :], in_=scr[r0 + j * 128:r0 + j * 128 + Tj, :]),
                      "xt2_%d" % j, r=["scr%d" % ((r0 + j * 128) // 128)], w=["xt2_%d" % j])
                rms_rstd(xt2[j], Tj, "xt2_%d" % j, junk2, ss2, rstd2, "2")
                S.act(lambda e, j=j, Tj=Tj: e.activation(out=xn2[0:Tj, :], in_=xt2[j][0:Tj, :], func=AF.Copy, scale=rstd2[0:Tj, 0:1]),
                      r=["xt2_%d" % j, "rstd2"], w=["xn2"])
                for k in range(8):
                    S.pe(lambda e, k=k, Tj=Tj: e.transpose(out=pT[:, k * 128:k * 128 + Tj], in_=xn2[0:Tj, k * 128:(k + 1) * 128],
                                                           identity=ident[0:Tj, 0:Tj]), r=["xn2", "cst"], w=["pT"])
                S.dve(lambda e, j=j, Tj=Tj: e.tensor_tensor(
                    out=mT[:, :, j * 128:j * 128 + Tj], in0=pT.rearrange("p (k t) -> p k t", k=8)[:, :, 0:Tj],
                    in1=P("GP", 0, 8).unsqueeze(2).to_broadcast([128, 8, Tj]), op=ALU.mult),
                    r=["pT", "pfm"], w=["mT"])
            for f in range(32):
                pa = pA[f % 2]
                for k in range(8):
                    S.pe(lambda e, k=k, f=f, pa=pa: e.matmul(pa[:, 0:T], lhsT=w_up_sb[:, k, f * 128:(f + 1) * 128], rhs=mT[:, k, 0:T],
                                                             start=(k == 0), stop=(k == 7)),
                         r=["mT", "w_up"], w=["pA%d" % (f % 2)])
                S.act(lambda e, f=f, pa=pa: e.activation(out=rl[f % 2][:, 0:T], in_=pa[:, 0:T], func=AF.Relu),
                      r=["pA%d" % (f % 2)], w=["rl%d" % (f % 2)])
                S.pool(lambda e, f=f: e.tensor_tensor(out=actb[:, f, 0:T], in0=rl[f % 2][:, 0:T], in1=rl[f % 2][:, 0:T], op=ALU.mult),
                       r=["rl%d" % (f % 2)], w=["act%d" % f])
            for j in range(nsub):
                Tj = min(128, T - j * 128)
                for nb in range(2):
                    for f in range(32):
                        S.pe(lambda e, f=f, nb=nb, j=j, Tj=Tj: e.matmul(pO[0:Tj, nb * 512:(nb + 1) * 512],
                                                                        lhsT=actb[:, f, j * 128:j * 128 + Tj],
                                                                        rhs=w_dn_sb[:, f, nb * 512:(nb + 1) * 512],
                                                                        start=(f == 0), stop=(f == 31)),
                             r=["act%d" % f, "w_dn"], w=["pO"])
                S.dve(lambda e, j=j, Tj=Tj: e.tensor_tensor(out=xt2[j][0:Tj, :], in0=pO[0:Tj, :], in1=xt2[j][0:Tj, :], op=ALU.add),
                      r=["pO", "xt2_%d" % j], w=["xt2_%d" % j])
                rms_rstd(xt2[j], Tj, "xt2_%d" % j, junk2, ss2, rstd2, "2")
                S.dve(lambda e, j=j, Tj=Tj: e.scalar_tensor_tensor(out=yout[0:Tj, :], in0=xt2[j][0:Tj, :], scalar=rstd2[0:Tj, 0:1],
                                                                   in1=gfin_bc[0:Tj, :], op0=ALU.mult, op1=ALU.mult),
                      r=["xt2_%d" % j, "rstd2", "gfin"], w=["yout"])
                rr = r0 + j * 128
                if rr < SEQ:
                    S.dma(lambda e, rr=rr, Tj=Tj: e.dma_start(out=y_p[rr:rr + Tj, :], in_=yout[0:Tj, :]), "yout", r=["yout"])
                else:
                    S.dma(lambda e, Tj=Tj: e.dma_start(out=y_s, in_=yout[0:Tj, :]), "yout", r=["yout"])

        if MLP:
            for t in range(NT * 128 // T2):
                mlp_tile(t * T2, T2)
            if SAMP:
                mlp_tile(SEQ, TS)

        S.emit()
    return nc


_CACHE = {}


def _consts():
    c = np.zeros((128, NCST), np.float32)
    i = np.arange(128)
    c[:, CI:CI + 128] = np.eye(128, dtype=np.float32)
    c[:, CU:CU + 128] = (i[:, None] <= i[None, :]).astype(np.float32)
    c[:, CN:CN + 128] = np.where(i[:, None] <= i[None, :], 0.0, NEG).astype(np.float32)
    c[:, CO:CO + 128] = 1.0
    j = np.arange(TS)
    same = (j[:, None] // 4) == (j[None, :] // 4)
    caus = j[:, None] <= j[None, :]
    c[0:TS, CUB:CUB + TS] = (same & caus).astype(np.float32)
    c[0:TS, CNB:CNB + TS] = np.where(same & caus, 0.0, NEG).astype(np.float32)
    c[0:TS, CBM:CBM + TS] = same.astype(np.float32)
    c[0:TS, CBI:CBI + NS] = ((j[:, None] // 4) == np.arange(NS)[None, :]).astype(np.float32)
    return c


def _fm(v, nch):
    return np.ascontiguousarray(np.asarray(v, np.float32).reshape(nch, 128).T)


def kernel(x_prompt, x_sample, state_lru_conv, state_lru_h, state_ssd_conv, state_ssd_h,
           g_mix, w_in, lru_conv_w, lru_conv_b, w_a, b_a, w_x, b_x, lam, g_lru_out,
           ssd_conv_w, ssd_conv_b, dt_bias, a_log, d_skip, g_ssd_out, w_out,
           g_mlp, w_up, w_down, g_final):
    f = lambda a: np.ascontiguousarray(np.asarray(a, np.float32))
    if "nc" not in _CACHE:
        _CACHE["nc"] = build_program()
    nc = _CACHE["nc"]
    pfm = np.zeros((128, NPAR), np.float32)
    lw = np.asarray(lru_conv_w[0], np.float32)
    pfm[:, PC["LW"]:PC["LW"] + 32] = lw.reshape(4, 8, 128).transpose(2, 1, 0).reshape(128, 32)
    pfm[:, PC["LB"]:PC["LB"] + 8] = _fm(lru_conv_b[0], 8)
    pfm[:, PC["BA"]:PC["BA"] + 8] = _fm(np.asarray(b_a[0]).reshape(-1), 8)
    pfm[:, PC["BX"]:PC["BX"] + 8] = _fm(np.asarray(b_x[0]).reshape(-1), 8)
    pfm[:, PC["LAM"]:PC["LAM"] + 8] = _fm(lam[0], 8)
    pfm[:, PC["GL"]:PC["GL"] + 8] = _fm(g_lru_out[0], 8)
    sw = np.asarray(ssd_conv_w[0], np.float32)
    pfm[:, PC["SW"]:PC["SW"] + 48] = sw.reshape(4, 12, 128).transpose(2, 1, 0).reshape(128, 48)
    pfm[:, PC["SB"]:PC["SB"] + 12] = _fm(ssd_conv_b[0], 12)
    pfm[:, PC["DS"]:PC["DS"] + 8] = _fm(np.repeat(np.asarray(d_skip[0], np.float32), 64), 8)
    pfm[:, PC["GS"]:PC["GS"] + 8] = _fm(g_ssd_out[0], 8)
    pfm[:, PC["GM"]:PC["GM"] + 8] = _fm(g_mix[0], 8)
    pfm[:, PC["GP"]:PC["GP"] + 8] = _fm(g_mlp[0], 8)
    cst = _consts()
    shared = {
        "w_in": f(w_in[0]), "w_out": f(w_out[0]), "w_up": f(w_up[0]), "w_down": f(w_down[0]),
        "w_a": f(w_a[0]), "w_x": f(w_x[0]), "pfm": pfm, "cst": cst,
        "dt_bias": f(dt_bias[0]), "a_log": f(a_log[0]), "g_final": f(g_final),
    }
    in_maps = []
    for b in range(NCORES):
        sl = slice(NS * b, NS * (b + 1))
        m = dict(shared)
        m["xp"] = f(x_prompt[b])
        m["xs"] = f(np.asarray(x_sample[sl]).reshape(TS, D))
        m["st_lc"] = f(np.asarray(state_lru_conv[0, sl]).reshape(NS * 3, D))
        m["st_lh"] = f(state_lru_h[0, sl])
        m["st_sc"] = f(np.asarray(state_ssd_conv[0, sl]).reshape(NS * 3, XBC))
        m["st_sh"] = f(np.asarray(state_ssd_h[0, sl]).reshape(NS, 1024, 128))
        in_maps.append(m)
    res = run_bass_kernel_spmd(nc, in_maps, core_ids=list(range(NCORES)))
    R = res.results
    cat = lambda k: np.stack([np.asarray(R[b][k], np.float32) for b in range(NCORES)])
    y_prompt = cat("y_p")
    y_sample = cat("y_s").reshape(NCORES * NS, 4, D)
    p_lc = cat("o_plc")[None]
    p_lh = cat("o_plh").reshape(NCORES, D)[None]
    p_sc = cat("o_psc")[None]
    p_sh = cat("o_psh").reshape(NCORES, 16, 64, 128)[None]
    s_lc = cat("o_slc").reshape(NCORES * NS, 3, D)[None]
    s_lh = cat("o_slh").reshape(NCORES * NS, D)[None]
    s_sc = cat("o_ssc").reshape(NCORES * NS, 3, XBC)[None]
    s_sh = cat("o_ssh").reshape(NCORES * NS, 16, 64, 128)[None]
    return (y_prompt, y_sample, p_lc, p_lh, p_sc, p_sh, s_lc, s_lh, s_sc, s_sh)
```

```python
import math
from contextlib import ExitStack

import numpy as np
import concourse.bass as bass
import concourse.mybir as mybir
from concourse.bass_utils import run_bass_kernel_spmd

F32 = mybir.dt.float32
BF16 = mybir.dt.bfloat16
AF = mybir.ActivationFunctionType
ALU = mybir.AluOpType

NCORES = 8
D = 1024
SEQ = 2048
NS = 16
TS = 64
XBC = 1536
INP = 4624
DFF = 4096
EPS = 1e-6
NEG = -30000.0

import re as _re

ENGS = ("pe", "act", "dve", "pool", "sp")
SAME_ENGINE_SYNC = {"pe": False, "act": True, "dve": True, "pool": True, "sp": False}


class Op:
    __slots__ = ("eng", "fn", "deps", "marked", "count", "dma_key", "dma_val")

    def __init__(self, eng, fn, dma_key=None):
        self.eng = eng
        self.fn = fn
        self.deps = ()
        self.marked = False
        self.count = 0
        self.dma_key = dma_key
        self.dma_val = 0


class Sched:
    def __init__(self, nc):
        self.nc = nc
        self.ops = {e: [] for e in ENGS}
        self.last_w = {}
        self.readers = {}
        self.dma_cnt = {}
        self.pending = {}
        self.ctx = None
        self.since_bar = []

    ALIAS = {"pC": "b4", "pCx": "b4", "pD": "b5", "pD2": "b5", "pD3": "b5", "pDn": "b5",
             "pDcb0": "b5", "pDcb1": "b5", "pT": "b01", "pO": "b67", "pA0": "b2", "pA1": "b3"}

    PSUM_KEYS = {"b01", "b2", "b3", "b4", "b5", "b67"}

    PAR_RE = _re.compile(r"^(xt|dtt|xnew)$|^(xsf|zs|Bb|Cb|ynl|yg)\d+$")

    def _k(self, k):
        k = self.ALIAS.get(k, k)
        if self.ctx is not None and self.PAR_RE.match(k):
            return k + "#" + str(self.ctx)
        return k

    def add(self, eng, fn, reads=(), writes=(), dma_key=None):
        reads = [self._k(k) for k in reads]
        writes = [self._k(k) for k in writes]
        if dma_key is not None and self.ctx is not None and self.PAR_RE.match(dma_key):
            dma_key = dma_key + "#" + str(self.ctx)
        op = Op(eng, fn, dma_key)
        deps = []
        seen = set()

        def dep(o):
            if o is not None and o is not op and id(o) not in seen:
                seen.add(id(o))
                deps.append(o)

        if self.pending.get(eng):
            for o in self.pending[eng]:
                dep(o)
            self.pending[eng] = []
        for b in reads:
            dep(self.last_w.get(b))
            if b in self.PSUM_KEYS:
                for r in self.readers.get(b, ()):
                    if r.eng != eng:
                        dep(r)
        for b in writes:
            dep(self.last_w.get(b))
            for r in self.readers.get(b, ()):
                dep(r)
        for b in reads:
            self.readers.setdefault(b, []).append(op)
        for b in writes:
            self.last_w[b] = op
            self.readers[b] = []
        op.deps = deps
        if dma_key is not None:
            self.dma_cnt[dma_key] = self.dma_cnt.get(dma_key, 0) + 16
            op.dma_val = self.dma_cnt[dma_key]
        self.ops[eng].append(op)
        self.since_bar.append(op)
        return op

    def barrier(self):
        ops = []
        for e in ENGS:
            comp = [o for o in self.ops[e] if o.dma_key is None]
            if comp:
                ops.append(comp[-1])
        last_dma = {}
        for o in self.since_bar:
            if o.dma_key is not None:
                last_dma[o.dma_key] = o
        ops.extend(last_dma.values())
        for e in ENGS:
            self.pending.setdefault(e, []).extend(ops)
        self.since_bar = []

    def pe(self, fn, r=(), w=()):
        return self.add("pe", fn, r, w)

    def act(self, fn, r=(), w=()):
        return self.add("act", fn, r, w)

    def dve(self, fn, r=(), w=()):
        return self.add("dve", fn, r, w)

    def pool(self, fn, r=(), w=()):
        return self.add("pool", fn, r, w)

    def dma(self, fn, key, r=(), w=(), q="sp"):
        return self.add(q, fn, r, w, dma_key=key)

    def emit(self):
        nc = self.nc
        for e in ENGS:
            for op in self.ops[e]:
                for d in op.deps:
                    if d.dma_key is None:
                        if d.eng == op.eng and not SAME_ENGINE_SYNC[d.eng]:
                            continue
                        d.marked = True
        for e in ENGS:
            c = 0
            for op in self.ops[e]:
                if op.dma_key is None and op.marked:
                    c += 1
                    op.count = c
        with ExitStack() as st:
            esem = {e: st.enter_context(nc.semaphore("es_" + e)) for e in ENGS}
            dsem = {}
            for k in self.dma_cnt:
                dsem[k] = st.enter_context(nc.semaphore("ds_%d" % len(dsem)))
            block = st.enter_context(nc.Block())

            def run(ename, eng):
                seen = {}
                for op in self.ops[ename]:
                    need = {}
                    for d in op.deps:
                        if d.dma_key is not None:
                            key = ("d", d.dma_key)
                            val = d.dma_val
                            sem = dsem[d.dma_key]
                        else:
                            if d.eng == ename and not SAME_ENGINE_SYNC[ename]:
                                continue
                            key = ("e", d.eng)
                            val = d.count
                            sem = esem[d.eng]
                        if key not in need or need[key][1] < val:
                            need[key] = (sem, val)
                    for key, (sem, val) in need.items():
                        if seen.get(key, 0) >= val:
                            continue
                        seen[key] = val
                        eng.wait_ge(sem, val)
                    ins = op.fn(eng)
                    if op.dma_key is not None:
                        ins.then_inc(dsem[op.dma_key], 16)
                    elif op.marked:
                        ins.then_inc(esem[ename], 1)
                if ename == "sp":
                    for k, v in self.dma_cnt.items():
                        eng.wait_ge(dsem[k], v)

            @block.sync
            def _(e):
                run("sp", e)

            @block.tensor
            def _(e):
                run("pe", e)

            @block.scalar
            def _(e):
                run("act", e)

            @block.vector
            def _(e):
                run("dve", e)

            @block.gpsimd
            def _(e):
                run("pool", e)


PC = {}
_o = 0
for _n, _w in (("LW", 32), ("LB", 8), ("BA", 8), ("BX", 8), ("LAM", 8), ("GL", 8), ("SW", 48),
               ("SB", 12), ("DS", 8), ("GS", 8), ("GM", 8), ("GP", 8)):
    PC[_n] = _o
    _o += _w
NPAR = _o
CI, CU, CN, CO, CUB, CNB, CBM, CBI = 0, 128, 256, 384, 512, 576, 640, 704
NCST = 720


def build_program(NT=SEQ // 128, SAMP=True, MLP=True, DBG=False, STAGE=9):
    nc = bass.Bass("TRN2", target_bir_lowering=False)
    S = Sched(nc)

    def din(name, shape):
        return nc.dram_tensor(name, list(shape), F32, kind="ExternalInput").ap()

    def dout(name, shape):
        return nc.dram_tensor(name, list(shape), F32, kind="ExternalOutput").ap()

    xp = din("xp", (SEQ, D))
    xs = din("xs", (TS, D))
    st_lc = din("st_lc", (NS * 3, D))
    st_lh = din("st_lh", (NS, D))
    st_sc = din("st_sc", (NS * 3, XBC))
    st_sh = din("st_sh", (NS, 1024, 128))
    w_in = din("w_in", (D, INP))
    w_out = din("w_out", (2 * D, D))
    w_up = din("w_up", (D, DFF))
    w_down = din("w_down", (DFF, D))
    w_a = din("w_a", (16, 64, 64))
    w_x = din("w_x", (16, 64, 64))
    pfm_d = din("pfm", (128, NPAR))
    cst_d = din("cst", (128, NCST))
    dtb_d = din("dt_bias", (16,))
    alog_d = din("a_log", (16,))
    gfin_d = din("g_final", (D,))

    y_p = dout("y_p", (SEQ, D))
    y_s = dout("y_s", (TS, D))
    o_plc = dout("o_plc", (3, D))
    o_plh = dout("o_plh", (8, 128))
    o_psc = dout("o_psc", (3, XBC))
    o_psh = dout("o_psh", (1024, 128))
    o_slc = dout("o_slc", (NS, 3, D))
    o_slh = dout("o_slh", (NS, D))
    o_ssc = dout("o_ssc", (NS, 3, XBC))
    o_ssh = dout("o_ssh", (NS, 1024, 128))
    scr = nc.dram_tensor("scr", [SEQ + TS, D], F32, kind=("ExternalOutput" if DBG else "Internal")).ap()

    st = ExitStack()
    with st:
        RW = 53200
        R = st.enter_context(nc.sbuf_tensor("R", [128, RW], F32))
        PS = st.enter_context(nc.psum_tensor("PS", [128, 4096], F32))
        ptr = [0]

        def alloc(nwords):
            a = ptr[0]
            ptr[0] += (nwords + 7) // 8 * 8
            pass
            return a

        def f32(n):
            a = alloc(n)
            return R[:, a:a + n]

        def bf(n):
            w = (n + 1) // 2
            a = alloc(w)
            return R[:, a:a + w].bitcast(BF16)[:, 0:n]

        def f3(c, t):
            return f32(c * t).rearrange("p (c t) -> p c t", c=c)

        def b3(c, t):
            return bf(c * t).rearrange("p (c t) -> p c t", c=c)

        def bank(b, n=512):
            return PS[:, 512 * b:512 * b + n]

        pT = PS[:, 0:1024]
        pTb = pT.bitcast(BF16)
        pA = [bank(2), bank(3)]
        pC = bank(4)
        pCb = pC.bitcast(BF16)
        pD = bank(5)
        pO = PS[:, 3072:4096]

        cst = f32(NCST)
        pfm = f32(NPAR)
        dtb_bc = f32(16)
        a_bc = f32(16)
        identb = bf(128)
        onesb = bf(128)
        Utrib = bf(128)
        mskb = bf(3 * TS)
        dah = bf(16)
        dal = bf(16)
        cfac = f32(8)
        c2fac = f32(8)
        tiny = f32(8)
        mhalf = f32(4)
        eps_t = f32(4)
        nbias = f32(16)
        wa_blk = b3(8, 128)
        wx_blk = b3(8, 128)
        hstate = f32(8)
        hT = f32(1024)
        hTb = bf(1024)

        ident = cst[:, CI:CI + 128]
        Utri = cst[:, CU:CU + 128]
        negm = cst[:, CN:CN + 128]
        onesf = cst[:, CO:CO + 128]
        Ublk = cst[0:TS, CUB:CUB + TS]
        negblk = cst[0:TS, CNB:CNB + TS]
        blkm = cst[0:TS, CBM:CBM + TS]
        blki = cst[0:TS, CBI:CBI + NS]

        S.dma(lambda e: e.dma_start(out=cst, in_=cst_d), "cst", w=["cst"])
        S.dma(lambda e: e.dma_start(out=pfm, in_=pfm_d), "pfm", w=["pfm"])
        S.dma(lambda e: e.dma_start(out=dtb_bc, in_=dtb_d.partition_broadcast(128)), "dtb", w=["dtb"])
        S.dma(lambda e: e.dma_start(out=a_bc, in_=alog_d.partition_broadcast(128)), "alog", w=["a_bc"])
        S.dve(lambda e: e.tensor_copy(out=identb, in_=ident), r=["cst"], w=["identb"])
        S.dve(lambda e: e.tensor_copy(out=onesb, in_=onesf), r=["cst"], w=["onesb"])
        S.dve(lambda e: e.tensor_copy(out=Utrib, in_=Utri), r=["cst"], w=["mskb"])
        S.dve(lambda e: e.tensor_copy(out=mskb[0:TS, 0:TS], in_=Ublk), r=["cst"], w=["mskb"])
        S.dve(lambda e: e.tensor_copy(out=mskb[0:TS, TS:2 * TS], in_=blkm), r=["cst"], w=["mskb"])
        S.pool(lambda e: e.memset(mhalf, -0.5), w=["mhalf"])
        S.pool(lambda e: e.memset(eps_t, EPS), w=["eps_t"])
        S.dve(lambda e: e.tensor_scalar(out=nbias[:, 0:8], in0=pfm[:, PC["BA"]:PC["BA"] + 8], scalar1=-1.0, scalar2=None, op0=ALU.mult), r=["pfm"], w=["nbias"])
        S.dve(lambda e: e.tensor_scalar(out=nbias[:, 8:16], in0=pfm[:, PC["BX"]:PC["BX"] + 8], scalar1=-1.0, scalar2=None, op0=ALU.mult), r=["pfm", "nbias"], w=["nbias"])
        S.pool(lambda e: e.memset(hstate, 0.0), w=["hstate"])
        S.pool(lambda e: e.memset(hT, 0.0), w=["hT"])
        S.pool(lambda e: e.memset(hTb, 0.0), w=["hTb"])
        S.pool(lambda e: e.memset(wa_blk, 0.0), w=["wa"])
        S.pool(lambda e: e.memset(wx_blk, 0.0), w=["wx"])
        S.act(lambda e: e.activation(out=a_bc, in_=a_bc, func=AF.Exp), r=["a_bc"], w=["a_bc"])
        S.dve(lambda e: e.tensor_scalar(out=a_bc, in0=a_bc, scalar1=-1.0, scalar2=None, op0=ALU.mult), r=["a_bc"], w=["a_bc"])
        lam = pfm[:, PC["LAM"]:PC["LAM"] + 8]
        S.act(lambda e: e.activation(out=tiny, in_=lam, func=AF.Exp, scale=-1.0), r=["pfm"], w=["tiny"])
        S.act(lambda e: e.activation(out=tiny, in_=tiny, func=AF.Ln, bias=1.0), r=["tiny"], w=["tiny"])
        S.dve(lambda e: e.tensor_scalar(out=cfac, in0=tiny, scalar1=-8.0, scalar2=None, op0=ALU.mult), r=["tiny"], w=["cfac"])
        S.dve(lambda e: e.tensor_scalar(out=c2fac, in0=tiny, scalar1=-16.0, scalar2=None, op0=ALU.mult), r=["tiny"], w=["cfac2"])
        for (wd, blk, nm) in ((w_a, wa_blk, "wa"), (w_x, wx_blk, "wx")):
            v = wd.rearrange("(c h) i j -> h i c j", h=2)
            for h2 in range(2):
                S.dma(lambda e, v=v, blk=blk, h2=h2: e.dma_start(
                    out=blk[64 * h2:64 * h2 + 64, :, 64 * h2:64 * h2 + 64], in_=v[h2]),
                    nm + str(h2), w=[nm], q="pool")

        base0 = ptr[0]

        def load_w(dst3, src2, nk, ncol, name, step=2048):
            sv = src2.rearrange("(k p) n -> p k n", p=128)
            pieces = [(k, c0, min(ncol, c0 + step)) for k in range(nk) for c0 in range(0, ncol, step)]
            for i, (k, c0, c1) in enumerate(pieces):
                S.dma(lambda e, k=k, c0=c0, c1=c1: e.dma_start(out=dst3[:, k, c0:c1], in_=sv[:, k, c0:c1]),
                      name, w=([name] if i == len(pieces) - 1 else []), q="pool")
            return name

        w_in_sb = b3(8, INP)
        w_out_sb = b3(16, D)
        if STAGE >= 1:
            k_win = load_w(w_in_sb, w_in, 8, INP, "w_in")
            k_wout = load_w(w_out_sb, w_out, 16, D, "w_out")

        xt = f32(D)
        xn = f32(D)
        junk = xn
        ss = f32(4)
        rstd = f32(4)
        hTt = b3(8, 128)
        lxb = f3(8, 131)
        xcb = f3(12, 131)
        sreg = f32(20 * NS * 7)
        lxs = sreg[:, 0:8 * NS * 7].rearrange("p (c s l) -> p c s l", c=8, s=NS)
        xcs = sreg[:, 8 * NS * 7:20 * NS * 7].rearrange("p (c s l) -> p c s l", c=12, s=NS)
        gl = f3(2, 128)
        zs = f3(8, 128)
        u = f3(8, 128)
        ub = b3(8, 128)
        gi = f3(4, 128)
        av = f3(2, 128)
        a2 = f3(2, 128)
        tmpb = f3(2, 128)
        hs = f3(8, 128)
        ysq = b3(2, 128)
        rbc = f32(128)
        ynl = b3(8, 128)
        yns = b3(8, 128)
        xsf = f3(8, 128)
        Bb = b3(2, 128)
        Cb = b3(2, 128)
        dtr = f32(16)
        dtt = f32(16)
        da = f32(16)
        ncum = f32(16)
        dte = f32(16)
        cdec = f32(16)
        xdt = bf(1024)
        xdd = bf(1024)
        BT = bf(256)
        cbT = f3(2, 128)
        Dmf = f32(512)
        Emf = f32(512)
        Mmf = bf(512)
        Chf = bf(1024)
        Chp = Chf[:, 0:512].rearrange("p (a t) -> p a t", a=4)
        Chs = Chf.rearrange("p (a t) -> p a t", a=16)
        cvt = f3(2, 128)
        stg = f32(2560)
        stT = stg[:, 0:1024]
        lc_in = stg[:, 0:1024]
        sc_in = stg[:, 1024:2560]
        lh_in = stg[:, 0:1024]
        h0in = lxb.rearrange("p c t -> p (c t)")[:, 0:1024].rearrange("p (c t) -> p c t", c=8)
        h0Tb = hTb
        Bm = bf(256)
        pyo_f = f32(8 * TS)
        pyo_sb = pyo_f.rearrange("p (c t) -> p c t", c=8)
        damb = bf(2 * NS * 16).rearrange("p (i s h) -> p i s h", i=2, s=NS)
        dtot = f3(NS, 16)
        hnew = hT
        hout = xcb.rearrange("p c t -> p (c t)")[:, 0:1024].rearrange("p (c t) -> p c t", c=8)
        h0s = f3(8, NS)
        hfin = f3(8, NS)

        bf3 = lambda ap, c: ap.bitcast(BF16).rearrange("p (c t) -> p c t", c=c)
        xt_b = [xt, stg[:, 0:1024]]
        ynl_b = [ynl, bf3(stg[:, 1024:1536], 8)]
        Bb_b = [Bb, bf3(stg[:, 1536:1664], 2)]
        Cb_b = [Cb, bf3(stg[:, 1664:1792], 2)]
        dtt_b = [dtt, stg[:, 1792:1808]]
        xsf_b = [xsf, sreg[:, 0:1024].rearrange("p (c t) -> p c t", c=8)]
        zs_b = [zs, sreg[:, 1024:2048].rearrange("p (c t) -> p c t", c=8)]
        Wl_S = pyo_f[:, 0:256].bitcast(BF16)
        ysq_S = bf3(pyo_f[:, 256:384], 2)
        rbc_S = pyo_f[:, 384:512]
        STGW = [k + "#1" for k in ["xt", "dtt"] + ["ynl%d" % c for c in range(8)] + ["Bb0", "Bb1", "Cb0", "Cb1"]]
        STG_ALIAS = ["xt", "dtt"] + ["ynl%d" % c for c in range(8)] + ["Bb0", "Bb1", "Cb0", "Cb1"]

        def P(name, c=None, w=1):
            o = PC[name] + (0 if c is None else c * w)
            return pfm[:, o:o + w]

        def rms_rstd(xtile, T, keyx, junk, ss, rstd, sfx=""):
            S.act(lambda e: e.activation(out=junk[0:T, :], in_=xtile[0:T, :], func=AF.Square, accum_out=ss[0:T, 0:1]),
                  r=[keyx], w=["xn" + sfx, "ss" + sfx])
            S.act(lambda e: e.activation(out=ss[0:T, 0:1], in_=ss[0:T, 0:1], func=AF.Ln, scale=1.0 / D, bias=eps_t[0:T, 0:1]),
                  r=["ss" + sfx, "eps_t"], w=["ss" + sfx])
            S.act(lambda e: e.activation(out=rstd[0:T, 0:1], in_=ss[0:T, 0:1], func=AF.Exp, scale=-0.5),
                  r=["ss" + sfx], w=["rstd" + sfx])

        pAA = PS[:, 1024:2048]

        def to_fm(T, gname, dst, dkey):
            for k in range(8):
                S.pe(lambda e, k=k: e.transpose(out=pAA[:, k * 128:k * 128 + T], in_=xn[0:T, k * 128:(k + 1) * 128],
                                                identity=ident[0:T, 0:T]), r=["xn", "cst"], w=["pA0", "pA1"])
            S.dve(lambda e: e.tensor_tensor(
                out=dst[:, :, 0:T], in0=pAA.rearrange("p (k t) -> p k t", k=8)[:, :, 0:T],
                in1=P(gname, 0, 8).unsqueeze(2).to_broadcast([128, 8, T]), op=ALU.mult),
                r=["pA0", "pA1", "pfm"], w=[dkey])

        def mixer_tile(mt, samp):
            T = TS if samp else 128
            row0 = SEQ if samp else mt * 128
            xsrc = xs if samp else xp[mt * 128:(mt + 1) * 128, :]
            last = (not samp) and mt == NT - 1
            par = 0 if samp else (NT - 1 - mt) % 2
            xt, ynl, Bb, Cb, dtt, xsf, zs = (xt_b[par], ynl_b[par], Bb_b[par], Cb_b[par], dtt_b[par], xsf_b[par], zs_b[par])
            Wl = cvt.rearrange("p a t -> p (a t)").bitcast(BF16) if samp else Wl_S
            wlk = ["cv_t0", "cv_t1"] if samp else ["WlS"]
            ysqS = ysq if samp else ysq_S
            rbcS = rbc if samp else rbc_S
            sk = "" if samp else "S"

            def inter(*gens):
                gens = list(gens)
                while gens:
                    for g_ in list(gens):
                        try:
                            next(g_)
                        except StopIteration:
                            gens.remove(g_)
                        yield

            pcnt = [0]

            def proj(ci):
                i = pcnt[0] % 2
                pcnt[0] += 1
                pa = pA[i]
                for k in range(8):
                    S.pe(lambda e, k=k: e.matmul(pa[:, 0:T], lhsT=w_in_sb[:, k, ci * 128:(ci + 1) * 128],
                                                 rhs=hTt[:, k, 0:T], start=(k == 0), stop=(k == 7)),
                         r=["hTt", "w_in"], w=["pA%d" % i])
                return pa, "pA%d" % i

            def new_cols(buf, sbuf_, c):
                if samp:
                    return sbuf_[:, c, :, 3:7]
                return buf[:, c, 3:131]

            def pa_view(pa):
                if samp:
                    return pa[:, 0:T].rearrange("p (s l) -> p s l", s=NS)
                return pa[:, 0:T]

            def tap(buf, sbuf_, c, k):
                if samp:
                    return sbuf_[:, c, :, k:k + 4]
                return buf[:, c, k:k + 128]

            def fm(t3, c):
                if samp:
                    return t3[:, c, 0:T].rearrange("p (s l) -> p s l", s=NS)
                return t3[:, c, 0:T]

            def conv(buf, sbuf_, c, wname, bname, out_ap, key_in, key_out):
                S.dve(lambda e: e.tensor_scalar(out=out_ap, in0=tap(buf, sbuf_, c, 3), scalar1=P(wname, c, 4)[:, 3:4],
                                                scalar2=P(bname, c), op0=ALU.mult, op1=ALU.add),
                      r=[key_in, "pfm"], w=[key_out])
                for k in (2, 1, 0):
                    S.dve(lambda e, k=k: e.scalar_tensor_tensor(out=out_ap, in0=tap(buf, sbuf_, c, k),
                                                                scalar=P(wname, c, 4)[:, k:k + 1], in1=out_ap,
                                                                op0=ALU.mult, op1=ALU.add),
                          r=[key_in, key_out, "pfm"], w=[key_out])
                if not samp:
                    S.dve(lambda e: e.tensor_copy(out=buf[:, c, 0:3], in_=buf[:, c, 128:131]), r=[key_in], w=[key_in])

            def g_lrux():
                for c in range(8):
                    pa, pk = proj(c)
                    S.act(lambda e, c=c, pa=pa: e.activation(out=new_cols(lxb, lxs, c), in_=pa_view(pa), func=AF.Copy),
                          r=[pk], w=["lx%d" % c])
                    conv(lxb, lxs, c, "LW", "LB", fm(u, c), "lx%d" % c, "u%d" % c)
                    yield

            def g_z():
                for c in range(8):
                    pa, pk = proj(16 + c)
                    S.act(lambda e, c=c, pa=pa: e.activation(out=zs[:, c, 0:T], in_=pa[:, 0:T], func=AF.Silu),
                          r=[pk], w=["zs%d" % c])
                    yield

            def g_xbc():
                for c in range(12):
                    pa, pk = proj(24 + c)
                    S.act(lambda e, c=c, pa=pa: e.activation(out=new_cols(xcb, xcs, c), in_=pa_view(pa), func=AF.Copy),
                          r=[pk], w=["xc%d" % c])
                    if c < 8:
                        conv(xcb, xcs, c, "SW", "SB", fm(cvt, c % 2), "xc%d" % c, "cv_t%d" % (c % 2))
                        S.act(lambda e, c=c: e.activation(out=xsf[:, c, 0:T], in_=cvt[:, c % 2, 0:T], func=AF.Silu),
                              r=["cv_t%d" % (c % 2)], w=["xsf%d" % c])
                    else:
                        g = (c - 8) % 2
                        dstb = Bb if c < 10 else Cb
                        nm = ("Bb%d" if c < 10 else "Cb%d") % g
                        conv(xcb, xcs, c, "SW", "SB", fm(cvt, g), "xc%d" % c, "cv_t%d" % g)
                        S.act(lambda e, g=g, dstb=dstb: e.activation(out=dstb[:, g, 0:T], in_=cvt[:, g, 0:T], func=AF.Silu),
                              r=["cv_t%d" % g], w=[nm])
                    yield
                for k in range(8):
                    S.pe(lambda e, k=k: e.matmul(pD[0:T, 0:16], lhsT=hTt[:, k, 0:T], rhs=w_in_sb[:, k, 4608:4624],
                                                 start=(k == 0), stop=(k == 7)),
                         r=["hTt", "w_in"], w=["pD"])
                S.dve(lambda e: e.tensor_tensor(out=dtr[0:T, :], in0=pD[0:T, 0:16], in1=dtb_bc[0:T, :], op=ALU.add),
                      r=["pD", "dtb"], w=["dtr"])
                S.act(lambda e: e.activation(out=dtr[0:T, :], in_=dtr[0:T, :], func=AF.Exp), r=["dtr"], w=["dtr"])
                S.act(lambda e: e.activation(out=dtt[0:T, :], in_=dtr[0:T, :], func=AF.Ln, bias=1.0), r=["dtr"], w=["dtt"])
                yield

            def g_lru(chunks, pg, kr, ki):
                for c in chunks:
                    pp = c % 2
                    S.act(lambda e, c=c: e.activation(out=ub[:, c, 0:T], in_=u[:, c, 0:T], func=AF.Copy),
                          r=["u%d" % c], w=["ub%d" % c])
                    yield
                    S.pe(lambda e, c=c: e.matmul(pg[:, 0:T], lhsT=wa_blk[:, c, :], rhs=ub[:, c, 0:T], start=True, stop=True),
                         r=["ub%d" % c, "wa"], w=[kr])
                    S.pe(lambda e, c=c: e.matmul(pg[:, 128:128 + T], lhsT=wx_blk[:, c, :], rhs=ub[:, c, 0:T], start=True, stop=True),
                         r=["ub%d" % c, "wx"], w=[ki])
                    yield
                    S.act(lambda e, c=c, pp=pp: e.activation(out=gi[:, 2 * pp, 0:T], in_=pg[:, 0:T], func=AF.Exp, scale=-1.0, bias=nbias[:, c:c + 1]),
                          r=[kr, "nbias"], w=["rg%d" % pp])
                    S.act(lambda e, c=c, pp=pp: e.activation(out=gi[:, 2 * pp + 1, 0:T], in_=pg[:, 128:128 + T], func=AF.Exp, scale=-1.0, bias=nbias[:, 8 + c:9 + c]),
                          r=[ki, "nbias"], w=["ig%d" % pp])
                    S.act(lambda e, pp=pp: e.activation(out=gi[:, 2 * pp:2 * pp + 2, 0:T], in_=gi[:, 2 * pp:2 * pp + 2, 0:T], func=AF.Ln, bias=1.0),
                          r=["rg%d" % pp, "ig%d" % pp], w=["rg%d" % pp, "ig%d" % pp])
                    S.act(lambda e, pp=pp: e.activation(out=gi[:, 2 * pp:2 * pp + 2, 0:T], in_=gi[:, 2 * pp:2 * pp + 2, 0:T], func=AF.Exp, scale=-1.0),
                          r=["rg%d" % pp, "ig%d" % pp], w=["rg%d" % pp, "ig%d" % pp])
                    S.act(lambda e, c=c, pp=pp: e.activation(out=av[:, pp, 0:T], in_=gi[:, 2 * pp, 0:T], func=AF.Exp, scale=cfac[:, c:c + 1]),
                          r=["rg%d" % pp, "cfac"], w=["av%d" % pp])
                    S.act(lambda e, c=c, pp=pp: e.activation(out=a2[:, pp, 0:T], in_=gi[:, 2 * pp, 0:T], func=AF.Exp, scale=c2fac[:, c:c + 1]),
                          r=["rg%d" % pp, "cfac2"], w=["a2%d" % pp])
                    S.act(lambda e, pp=pp: e.activation(out=a2[:, pp, 0:T], in_=a2[:, pp, 0:T], func=AF.Ln, scale=-1.0, bias=1.0),
                          r=["a2%d" % pp], w=["a2%d" % pp])
                    S.act(lambda e, pp=pp: e.activation(out=a2[:, pp, 0:T], in_=a2[:, pp, 0:T], func=AF.Exp, scale=0.5),
                          r=["a2%d" % pp], w=["a2%d" % pp])
                    yield
                    S.dve(lambda e, c=c, pp=pp: e.tensor_tensor(out=tmpb[:, pp, 0:T], in0=gi[:, 2 * pp + 1, 0:T], in1=u[:, c, 0:T], op=ALU.mult),
                          r=["ig%d" % pp, "u%d" % c], w=["tb%d" % pp])
                    S.dve(lambda e, pp=pp: e.tensor_tensor(out=tmpb[:, pp, 0:T], in0=tmpb[:, pp, 0:T], in1=a2[:, pp, 0:T], op=ALU.mult),
                          r=["tb%d" % pp, "a2%d" % pp], w=["tb%d" % pp])
                    if samp:
                        a3 = av[:, pp, 0:T].rearrange("p (s l) -> p s l", s=NS)
                        b3v = tmpb[:, pp, 0:T].rearrange("p (s l) -> p s l", s=NS)
                        S.dve(lambda e, c=c, a3=a3: e.tensor_tensor(out=rbc[:, 0:NS], in0=a3[:, :, 0], in1=h0s[:, c, :], op=ALU.mult),
                              r=["av%d" % pp, "h0s"], w=["rbc"])
                        S.dve(lambda e, b3v=b3v: e.tensor_tensor(out=b3v[:, :, 0], in0=b3v[:, :, 0], in1=rbc[:, 0:NS], op=ALU.add),
                              r=["tb%d" % pp, "rbc"], w=["tb%d" % pp])
                        S.dve(lambda e, a3=a3: e.memset(a3[:, :, 0], 0.0), r=["rbc"], w=["av%d" % pp])
                        S.dve(lambda e, c=c, pp=pp: e.tensor_tensor_scan(out=hs[:, c, 0:T], data0=av[:, pp, 0:T], data1=tmpb[:, pp, 0:T],
                                                                         initial=0.0, op0=ALU.mult, op1=ALU.add),
                              r=["av%d" % pp, "tb%d" % pp], w=["hs%d" % c])
                        S.dve(lambda e, c=c: e.tensor_copy(out=hfin[:, c, :], in_=hs[:, c, 0:T].rearrange("p (s l) -> p s l", s=NS)[:, :, 3]),
                              r=["hs%d" % c], w=["hfin"])
                    else:
                        S.dve(lambda e, c=c, pp=pp: e.tensor_tensor_scan(out=hs[:, c, 0:T], data0=av[:, pp, 0:T], data1=tmpb[:, pp, 0:T],
                                                                         initial=hstate[:, c:c + 1], op0=ALU.mult, op1=ALU.add),
                              r=["av%d" % pp, "tb%d" % pp, "hstate"], w=["hs%d" % c])
                        S.dve(lambda e, c=c: e.tensor_copy(out=hstate[:, c:c + 1], in_=hs[:, c, T - 1:T]),
                              r=["hs%d" % c], w=["hstate"])
                    yield

            def g_gate():
                for c in range(8):
                    pp = c % 2
                    pa, pk = proj(8 + c)
                    S.act(lambda e, pp=pp, pa=pa: e.activation(out=gl[:, pp, 0:T], in_=pa[:, 0:T], func=AF.Gelu_apprx_tanh),
                          r=[pk], w=["gl%d" % pp])
                    S.dve(lambda e, c=c, pp=pp: e.tensor_tensor(out=hs[:, c, 0:T], in0=hs[:, c, 0:T], in1=gl[:, pp, 0:T], op=ALU.mult),
                          r=["hs%d" % c, "gl%d" % pp], w=["yl%d" % c, "hs%d" % c])
                    S.act(lambda e, c=c, pp=pp: e.activation(out=ysq[:, pp, 0:T], in_=hs[:, c, 0:T], func=AF.Square),
                          r=["yl%d" % c], w=["ysq%d" % pp])
                    S.pe(lambda e, c=c, pp=pp: e.matmul(pD[:, 128:128 + T], lhsT=onesb, rhs=ysq[:, pp, 0:T], start=(c == 0), stop=(c == 7)),
                         r=["ysq%d" % pp, "onesb"], w=["pDn"])
                    yield

            def norm_apply(T, eps_, src, skey, gname, dst, dkey, c0, c1, rbc, rk, pst, pk):
                S.act(lambda e: e.activation(out=rbc[:, 0:T], in_=pst, func=AF.Ln,
                                             scale=1.0 / ((c1 - c0) * 128), bias=eps_t[:, 0:1]),
                      r=[pk, "eps_t"], w=[rk])
                S.act(lambda e: e.activation(out=rbc[:, 0:T], in_=rbc[:, 0:T], func=AF.Exp, scale=-0.5), r=[rk], w=[rk])
                for c in range(c0, c1):
                    S.dve(lambda e, c=c: e.scalar_tensor_tensor(out=dst[:, c, 0:T], in0=src[:, c, 0:T], scalar=P(gname, c),
                                                                in1=rbc[:, 0:T], op0=ALU.mult, op1=ALU.mult),
                          r=[skey % c, rk, "pfm"], w=[dkey % c])

            Um = mskb[0:TS, 0:TS] if samp else Utrib
            ngm = negblk if samp else negm
            allm = mskb[0:TS, TS:2 * TS] if samp else onesb
            d4 = lambda ap: ap[:, 0:4 * T].rearrange("p (a t) -> p a t", a=4)
            Em, Dm, Mm, pC4 = d4(Emf), d4(Dmf), d4(Mmf), d4(pC)

            def g_ssd():
                for c in range(8):
                    S.pe(lambda e, c=c: e.transpose(out=pT[0:T, c * 128:(c + 1) * 128], in_=xsf[:, c, 0:T], identity=ident),
                         r=["xsf%d" % c, "cst"], w=["pT"])
                for g in range(2):
                    S.pe(lambda e, g=g: e.transpose(out=pCb[0:T, 128 + g * 128:128 + (g + 1) * 128], in_=Bb[:, g, 0:T], identity=identb),
                         r=["Bb%d" % g, "identb"], w=["pC", "pCx"])
                S.dve(lambda e: e.tensor_tensor(out=xdt[0:T, :].rearrange("p (h q) -> p h q", h=16),
                                                in0=pT[0:T, :].rearrange("p (h q) -> p h q", h=16),
                                                in1=dtt[0:T, :].unsqueeze(2).to_broadcast([T, 16, 64]), op=ALU.mult),
                      r=["pT", "dtt"], w=["xdt"])
                S.dve(lambda e: e.tensor_copy(out=BT[0:T, :], in_=pCb[0:T, 128:384]), r=["pC"], w=["BT"])
                S.dve(lambda e: e.tensor_tensor(out=da[0:T, :], in0=dtt[0:T, :], in1=a_bc[0:T, :], op=ALU.mult),
                      r=["dtt", "a_bc"], w=["da"])
                yield
                S.dve(lambda e: e.tensor_copy(out=dah[0:T, :], in_=da[0:T, :]), r=["da"], w=["dah"])
                S.dve(lambda e: e.tensor_tensor(out=dal[0:T, :], in0=da[0:T, :], in1=dah[0:T, :], op=ALU.subtract),
                      r=["da", "dah"], w=["dal"])
                for i, dx in enumerate((dah, dal)):
                    S.pe(lambda e, dx=dx, i=i: e.matmul(pC[0:T, 0:16], lhsT=Um[0:T, 0:T], rhs=dx[0:T, :], start=(i == 0), stop=(i == 1)),
                         r=["dah", "dal", "mskb"], w=["pC", "pCx"])
                for i, dx in enumerate((dah, dal)):
                    S.pe(lambda e, dx=dx, i=i: e.matmul(pC[0:T, 16:32], lhsT=allm[0:T, 0:T], rhs=dx[0:T, :], start=(i == 0), stop=(i == 1)),
                         r=["dah", "dal", "mskb", "onesb"], w=["pC", "pCx"])
                if not samp:
                    for i, dx in enumerate((dah, dal)):
                        S.pe(lambda e, dx=dx, i=i: e.matmul(pC[:, 32:48], lhsT=onesb, rhs=dx, start=(i == 0), stop=(i == 1)),
                             r=["dah", "dal", "onesb"], w=["pC", "pCx"])
                for g in range(2):
                    S.pe(lambda e, g=g: e.matmul(pC[0:T, 256 + g * 128:256 + g * 128 + T], lhsT=Bb[:, g, 0:T], rhs=Cb[:, g, 0:T],
                                                 start=True, stop=True), r=["Bb%d" % g, "Cb%d" % g], w=["pC", "pCx"])
                yield
                S.dve(lambda e: e.tensor_scalar(out=ncum[0:T, :], in0=pC[0:T, 0:16], scalar1=-1.0, scalar2=None, op0=ALU.mult),
                      r=["pC"], w=["ncum"])
                S.dve(lambda e: e.tensor_tensor(out=dte[0:T, :], in0=pC[0:T, 16:32], in1=ncum[0:T, :], op=ALU.add),
                      r=["pC", "ncum"], w=["dte"])
                if not samp:
                    S.dve(lambda e: e.tensor_copy(out=cdec, in_=pC[:, 32:48]), r=["pC"], w=["cdec"])
                S.dve(lambda e: e.tensor_copy(out=cbT[0:T, :, 0:T], in_=pC[0:T, 256:512].rearrange("p (g t) -> p g t", g=2)[:, :, 0:T]),
                      r=["pC"], w=["cbT0", "cbT1"])
                S.act(lambda e: e.activation(out=dte[0:T, :], in_=dte[0:T, :], func=AF.Exp), r=["dte"], w=["dte"])
                if not samp:
                    S.act(lambda e: e.activation(out=cdec, in_=cdec, func=AF.Exp), r=["cdec"], w=["cdec"])
                S.dve(lambda e: e.tensor_tensor(out=xdd[0:T, :].rearrange("p (h q) -> p h q", h=16),
                                                in0=xdt[0:T, :].rearrange("p (h q) -> p h q", h=16),
                                                in1=dte[0:T, :].unsqueeze(2).to_broadcast([T, 16, 64]), op=ALU.mult),
                      r=["xdt", "dte"], w=["xdd"])
                yield
                if not samp:
                    for g in range(2):
                        S.pe(lambda e, g=g: e.matmul(pO[:, g * 512:(g + 1) * 512], lhsT=BT[:, g * 128:(g + 1) * 128],
                                                     rhs=xdd[:, g * 512:(g + 1) * 512], start=True, stop=True),
                             r=["BT", "xdd"], w=["pO"])
                    S.dve(lambda e: e.tensor_tensor(out=hT.rearrange("p (h q) -> p h q", h=16),
                                                    in0=hT.rearrange("p (h q) -> p h q", h=16),
                                                    in1=cdec.unsqueeze(2).to_broadcast([128, 16, 64]), op=ALU.mult),
                          r=["hT", "cdec"], w=["hT"])
                    S.dve(lambda e: e.tensor_tensor(out=hT, in0=hT, in1=pO, op=ALU.add), r=["hT", "pO"], w=["hT"])
                    yield
                for q4 in range(4):
                    g = q4 // 2
                    for i, (dx, Wf, wk) in enumerate(((dah, Mmf, ["Mm"]), (dal, Wl, wlk))):
                        S.pool(lambda e, q4=q4, dx=dx, Wf=Wf: e.tensor_tensor(out=d4(Wf)[0:T], in0=Um[0:T, 0:T].unsqueeze(1).to_broadcast([T, 4, T]),
                                                                             in1=dx[0:T, q4 * 4:q4 * 4 + 4].unsqueeze(2).to_broadcast([T, 4, T]),
                                                                             op=ALU.mult), r=["dah", "dal", "mskb"], w=wk)
                        S.pe(lambda e, Wf=Wf, i=i: e.matmul(pC[:, 0:4 * T], lhsT=onesb[0:T, :], rhs=Wf[0:T, 0:4 * T],
                                                            start=(i == 0), stop=(i == 1)), r=wk + ["onesb"], w=["pC", "pCx"])
                    S.dve(lambda e: e.tensor_copy(out=Em, in_=pC4), r=["pC"], w=["Em"])
                    yield
                    for hh in range(4):
                        h = q4 * 4 + hh
                        S.dve(lambda e, h=h, hh=hh: e.scalar_tensor_tensor(out=Dm[0:T, hh, :], in0=Em[0:T, hh, :],
                                                                           scalar=ncum[0:T, h:h + 1], in1=ngm[0:T, 0:T],
                                                                           op0=ALU.add, op1=ALU.add),
                              r=["Em", "ncum", "cst"], w=["Dm"])
                    S.act(lambda e: e.activation(out=Em, in_=Em, func=AF.Exp), r=["Em"], w=["Em"])
                    S.act(lambda e: e.activation(out=Dm[0:T], in_=Dm[0:T], func=AF.Exp), r=["Dm"], w=["Dm"])
                    S.pool(lambda e, g=g: e.tensor_tensor(out=Mm[0:T], in0=Dm[0:T],
                                                         in1=cbT[0:T, g, 0:T].unsqueeze(1).to_broadcast([T, 4, T]), op=ALU.mult),
                          r=["Dm", "cbT%d" % g], w=["Mm"])
                    S.pool(lambda e, g=g, q4=q4: e.tensor_tensor(out=(Chs[:, q4 * 4:q4 * 4 + 4, :] if samp else Chp), in0=Em,
                                                                in1=Cb[:, g, 0:T].unsqueeze(1).to_broadcast([128, 4, T]), op=ALU.mult),
                          r=["Em", "Cb%d" % g], w=["Ch"])
                    yield
                    for hh in range(4):
                        h = q4 * 4 + hh
                        c = h // 2
                        h2 = h % 2
                        po = pT[64 * h2:64 * h2 + 64, c * 128:c * 128 + T]
                        S.pe(lambda e, h=h, hh=hh, po=po: e.matmul(po, lhsT=xdt[0:T, h * 64:(h + 1) * 64], rhs=Mm[0:T, hh, :],
                                                                   start=True, stop=samp), r=["xdt", "Mm"], w=["pT"])
                        if not samp:
                            S.pe(lambda e, h=h, hh=hh, po=po: e.matmul(po, lhsT=hTb[:, h * 64:(h + 1) * 64], rhs=Chp[:, hh, :],
                                                                       start=False, stop=True), r=["hTb", "Ch"], w=["pT"])
                    yield

            def late_outputs():
                M = T if samp else 3
                t0 = 0 if samp else 125
                if samp or last:
                    for blk, col0 in enumerate((0, 512, 3072, 3584, 4096)):
                        for k in range(8):
                            S.pe(lambda e, k=k, col0=col0: e.matmul(pO[0:M, 0:512], lhsT=hTt[:, k, t0:t0 + M],
                                                                    rhs=w_in_sb[:, k, col0:col0 + 512], start=(k == 0), stop=(k == 7)),
                                 r=["hTt", "w_in"], w=["pO"])
                        S.dve(lambda e, blk=blk: e.tensor_copy(out=stg[0:M, blk * 512:(blk + 1) * 512], in_=pO[0:M, 0:512]),
                              r=["pO"], w=["stg"] + STGW)
                if last:
                    S.dma(lambda e: e.dma_start(out=o_plc, in_=stg[0:3, 0:1024]), "o_plc", r=["stg"])
                    S.dma(lambda e: e.dma_start(out=o_psc, in_=stg[0:3, 1024:2560]), "o_psc", r=["stg"])
                if samp:
                    for s in range(NS):
                        S.dma(lambda e, s=s: e.dma_start(out=o_slc[s], in_=stg[4 * s + 1:4 * s + 4, 0:1024]), "o_slc", r=["stg"])
                        S.dma(lambda e, s=s: e.dma_start(out=o_ssc[s], in_=stg[4 * s + 1:4 * s + 4, 1024:2560]), "o_ssc", r=["stg"])
                if last:
                    S.pe(lambda e: e.transpose(out=pC[0:8, 0:128], in_=hstate, identity=ident), r=["hstate", "cst"], w=["pC", "pCx"])
                    S.act(lambda e: e.activation(out=stT[0:8, 0:128], in_=pC[0:8, 0:128], func=AF.Copy), r=["pC"], w=["stg"] + STGW)
                    S.dma(lambda e: e.dma_start(out=o_plh, in_=stT[0:8, 0:128]), "o_plh", r=["stg"])
                if samp:
                    for c in range(8):
                        S.pe(lambda e, c=c: e.transpose(out=pT[0:NS, c * 128:(c + 1) * 128], in_=hfin[:, c, :], identity=ident),
                             r=["hfin", "cst"], w=["pT"])
                    S.act(lambda e: e.activation(out=lh_in[0:NS, :], in_=pT[0:NS, :], func=AF.Copy), r=["pT"], w=["stg"] + STGW)
                    S.dma(lambda e: e.dma_start(out=o_slh, in_=lh_in[0:NS, :]), "o_slh", r=["stg"])


            def genP():
                S.dma(lambda e: e.dma_start(out=xt[0:T, :], in_=xsrc), "xt", w=["xt"])
                rms_rstd(xt, T, "xt", junk, ss, rstd)
                S.act(lambda e: e.activation(out=xn[0:T, :], in_=xt[0:T, :], func=AF.Copy, scale=rstd[0:T, 0:1]),
                      r=["xt", "rstd"], w=["xn"])
                to_fm(T, "GM", hTt, "hTt")

                if samp:
                    S.dma(lambda e: e.dma_start(out=lc_in[0:48, :], in_=st_lc), "stg", w=["stg"])
                    S.dma(lambda e: e.dma_start(out=sc_in[0:48, :], in_=st_sc), "stg", w=["stg"])
                    S.dma(lambda e: e.dma_start(out=lh_in[64:64 + NS, :], in_=st_lh), "stg", w=["stg"])
                    for c in range(8):
                        S.pe(lambda e, c=c: e.transpose(out=pC[:, 0:48], in_=lc_in[0:48, c * 128:(c + 1) * 128],
                                                        identity=ident[0:48, 0:48]), r=["stg", "cst"], w=["pC"])
                        S.act(lambda e, c=c: e.activation(out=lxs[:, c, :, 0:3],
                                                          in_=pC[:, 0:48].rearrange("p (s j) -> p s j", s=NS),
                                                          func=AF.Copy), r=["pC"], w=["lx%d" % c])
                        S.pe(lambda e, c=c: e.transpose(out=pD[:, 0:NS], in_=lh_in[64:64 + NS, c * 128:(c + 1) * 128],
                                                        identity=ident[64:64 + NS, 64:64 + NS]), r=["stg", "cst"], w=["pD"])
                        S.dve(lambda e, c=c: e.tensor_copy(out=h0s[:, c, :], in_=pD[:, 0:NS]), r=["pD"], w=["h0s"])
                    for c in range(12):
                        S.pe(lambda e, c=c: e.transpose(out=pC[:, 0:48], in_=sc_in[0:48, c * 128:(c + 1) * 128],
                                                        identity=ident[0:48, 0:48]), r=["stg", "cst"], w=["pC"])
                        S.act(lambda e, c=c: e.activation(out=xcs[:, c, :, 0:3],
                                                          in_=pC[:, 0:48].rearrange("p (s j) -> p s j", s=NS),
                                                          func=AF.Copy), r=["pC"], w=["xc%d" % c])

                yield
                yield from inter(g_lrux(), g_z())
                yield from inter(g_xbc())
                yield from inter(g_lru((0, 2, 4, 6), pA[0], "pA0", "pA0"), g_lru((1, 3, 5, 7), pA[1], "pA1", "pA1"))
                yield from inter(g_gate())
                norm_apply(T, EPS, hs, "yl%d", "GL", ynl, "ynl%d", 0, 8, rbc, "rbc", pD[:, 128:128 + T], "pDn")
                yield
                if samp:
                    late_outputs()

            def genS():
                yield from inter(g_ssd())
                if samp:
                    ssd_sample_states_prep()

                for c in range(8):
                    S.dve(lambda e, c=c: e.scalar_tensor_tensor(out=xsf[:, c, 0:T], in0=xsf[:, c, 0:T], scalar=P("DS", c),
                                                                in1=pT[:, c * 128:c * 128 + T], op0=ALU.mult, op1=ALU.add),
                          r=["pT", "xsf%d" % c, "pfm"], w=["xsf%d" % c])
                if samp:
                    S.dve(lambda e: e.tensor_tensor(out=xsf[:, :, 0:T], in0=xsf[:, :, 0:T], in1=pyo_sb, op=ALU.add),
                          r=["xsf%d" % c for c in range(8)] + ["pyo_sb"], w=["xsf%d" % c for c in range(8)])
                S.pool(lambda e: e.tensor_tensor(out=xsf[:, :, 0:T], in0=xsf[:, :, 0:T], in1=zs[:, :, 0:T], op=ALU.mult),
                       r=["xsf%d" % c for c in range(8)] + ["zs%d" % c for c in range(8)], w=["yg%d" % c for c in range(8)] + ["xsf%d" % c for c in range(8)])
                yield
                if not samp:
                    S.act(lambda e: e.activation(out=hTb, in_=hT, func=AF.Copy), r=["hT"], w=["hTb"])
                    if last:
                        for c in range(8):
                            S.pe(lambda e, c=c: e.transpose(out=pO[:, c * 128:(c + 1) * 128], in_=hT[:, c * 128:(c + 1) * 128], identity=ident),
                                 r=["hT", "cst"], w=["pO"])
                        S.dve(lambda e: e.tensor_copy(out=stT, in_=pO), r=["pO"], w=["stg"] + STGW)
                        S.dma(lambda e: e.dma_start(out=o_psh.rearrange("(c q) n -> q c n", q=128),
                                                    in_=stT.rearrange("p (c n) -> p c n", c=8)), "o_psh", r=["stg"])
                yield
                for g in range(2):
                    for c in range(4 * g, 4 * g + 4):
                        pp = c % 2
                        S.act(lambda e, c=c, pp=pp: e.activation(out=ysqS[:, pp, 0:T], in_=xsf[:, c, 0:T], func=AF.Square),
                              r=["yg%d" % c], w=["ysq%s%d" % (sk, pp)])
                        S.pe(lambda e, c=c, pp=pp, g=g: e.matmul(pO[:, 0:T], lhsT=onesb, rhs=ysqS[:, pp, 0:T],
                                                                 start=(c == 4 * g), stop=(c == 4 * g + 3)),
                             r=["ysq%s%d" % (sk, pp), "onesb"], w=["pO"])
                    norm_apply(T, EPS, xsf, "yg%d", "GS", yns, "yns%d", 4 * g, 4 * g + 4, rbcS, "rbc" + sk, pO[:, 0:T], "pO")

                yield
                for nb in range(2):
                    for kc in range(16):
                        src = ynl if kc < 8 else yns
                        S.pe(lambda e, kc=kc, nb=nb, src=src: e.matmul(pO[0:T, nb * 512:(nb + 1) * 512], lhsT=src[:, kc % 8, 0:T],
                                                                       rhs=w_out_sb[:, kc, nb * 512:(nb + 1) * 512],
                                                                       start=(kc == 0), stop=(kc == 15)),
                             r=[("ynl%d" if kc < 8 else "yns%d") % (kc % 8), "w_out"], w=["pO"])
                S.dve(lambda e: e.tensor_tensor(out=xt[0:T, :], in0=pO[0:T, :], in1=xt[0:T, :], op=ALU.add),
                      r=["pO", "xt"], w=["xt"])
                S.dma(lambda e: e.dma_start(out=scr[row0:row0 + T, :], in_=xt[0:T, :]), "xnew", r=["xt"], w=["scr%d" % mt])

                if last:
                    late_outputs()

            return par, genP, genS

        def ssd_sample_states_prep():
            T = TS
            for i, dx in enumerate((dah, dal)):
                S.dve(lambda e, dx=dx, i=i: e.tensor_tensor(out=damb[0:T, i], in0=dx[0:T, :].unsqueeze(1).to_broadcast([T, NS, 16]),
                                                            in1=blki.unsqueeze(2).to_broadcast([T, NS, 16]), op=ALU.mult),
                      r=["dah", "dal", "cst"], w=["dam%d" % i])
                S.pe(lambda e, i=i: e.matmul(pD[:, 0:256], lhsT=onesb[0:T, :], rhs=damb[0:T, i].rearrange("p s h -> p (s h)"),
                                             start=(i == 0), stop=(i == 1)), r=["dam%d" % i, "onesb"], w=["pD", "pD2", "pD3", "pDn"])
            S.act(lambda e: e.activation(out=dtot.rearrange("p s h -> p (s h)"), in_=pD[:, 0:256], func=AF.Exp),
                  r=["pD"], w=["dtot"])
            for s in range(NS):
                S.dma(lambda e, s=s: e.dma_start(out=h0in, in_=st_sh[s].rearrange("(c q) n -> q c n", q=128)), "h0in", w=["h0in"])
                for c in range(8):
                    S.pe(lambda e, c=c: e.transpose(out=pO[:, c * 128:(c + 1) * 128], in_=h0in[:, c, :], identity=ident),
                         r=["h0in", "cst"], w=["pO"])
                S.act(lambda e: e.activation(out=h0Tb, in_=pO, func=AF.Copy), r=["pO"], w=["h0Tb"])
                for h in range(16):
                    S.pe(lambda e, h=h, s=s: e.matmul(pA[0][64 * (h % 2):64 * (h % 2) + 64, (h // 2) * TS + 4 * s:(h // 2) * TS + 4 * s + 4],
                                                      lhsT=h0Tb[:, h * 64:(h + 1) * 64], rhs=Chs[:, h, 4 * s:4 * s + 4],
                                                      start=True, stop=True),
                         r=["h0Tb", "Ch"], w=["pA0"])
                S.pool(lambda e, s=s: e.tensor_scalar(out=Bm[0:T, :], in0=BT[0:T, :], scalar1=blki[:, s:s + 1], scalar2=None, op0=ALU.mult),
                       r=["BT", "cst"], w=["Bm"])
                S.dve(lambda e, s=s: e.tensor_tensor(out=hnew.rearrange("p (h q) -> p h q", h=16),
                                                     in0=pO.rearrange("p (h q) -> p h q", h=16),
                                                     in1=dtot[:, s, :].unsqueeze(2).to_broadcast([128, 16, 64]), op=ALU.mult),
                      r=["pO", "dtot"], w=["hnew"])
                for g in range(2):
                    S.pe(lambda e, g=g, s=s: e.matmul(pO[:, g * 512:(g + 1) * 512], lhsT=Bm[0:T, g * 128:(g + 1) * 128],
                                                      rhs=xdd[0:T, g * 512:(g + 1) * 512], start=True, stop=True),
                         r=["Bm", "xdd"], w=["pO"])
                S.dve(lambda e: e.tensor_tensor(out=hnew, in0=hnew, in1=pO, op=ALU.add), r=["hnew", "pO"], w=["hnew"])
                for c in range(8):
                    S.pe(lambda e, c=c: e.transpose(out=pO[:, c * 128:(c + 1) * 128], in_=hnew[:, c * 128:(c + 1) * 128], identity=ident),
                         r=["hnew", "cst"], w=["pO"])
                S.act(lambda e: e.activation(out=hout.rearrange("p c n -> p (c n)"), in_=pO, func=AF.Copy), r=["pO"], w=["hout"])
                S.dma(lambda e, s=s: e.dma_start(out=o_ssh[s].rearrange("(c q) n -> q c n", q=128), in_=hout), "hout", r=["hout"])
            S.act(lambda e: e.activation(out=pyo_sb.rearrange("p c t -> p (c t)"), in_=pA[0][:, 0:8 * TS], func=AF.Copy),
                  r=["pA0"], w=["pyo_sb"])

        S.pool(lambda e: e.memset(lxb, 0.0), w=["lx%d" % c for c in range(8)])
        S.pool(lambda e: e.memset(xcb, 0.0), w=["xc%d" % c for c in range(12)])

        def drive(g_, par):
            S.ctx = par
            try:
                next(g_)
                return True
            except StopIteration:
                return False
            finally:
                S.ctx = None

        tiles = [mixer_tile(mt, False) for mt in range(NT)]
        RATIO = 3
        par0, gP0, _ = tiles[0]
        g = gP0()
        while drive(g, par0):
            pass
        for n in range(NT):
            par, _, gS = tiles[n]
            gs = gS()
            alive_s = True
            alive_p = False
            if n + 1 < NT:
                parn, gPn, _ = tiles[n + 1]
                gp = gPn()
                alive_p = True
            while alive_s or alive_p:
                for _ in range(RATIO):
                    if alive_p:
                        alive_p = drive(gp, parn)
                if alive_s:
                    alive_s = drive(gs, par)
        S.barrier()
        if SAMP:
            pars, gPs, gSs = mixer_tile(SEQ // 128, True)
            for g in (gPs(), gSs()):
                while drive(g, pars):
                    pass

        S.barrier()
        ptr[0] = base0
        w_up_sb = b3(8, DFF)
        w_dn_sb = b3(32, D)
        if MLP:
            k_wup = load_w(w_up_sb, w_up, 8, DFF, "w_up")
            k_wdn = load_w(w_dn_sb, w_down, 32, D, "w_dn")
        T2 = 256
        xt2 = [f32(D), f32(D)]
        xn2 = f32(D)
        junk2 = xn2
        ss2 = f32(4)
        rstd2 = f32(4)
        mT = b3(8, T2)
        actb = b3(32, T2)
        rl = [f32(T2), f32(T2)]
        yout = f32(D)
        gfin_bc = f32(D)
        S.dma(lambda e: e.dma_start(out=gfin_bc, in_=gfin_d.partition_broadcast(128)), "gfin", w=["gfin"])

        def mlp_tile(r0, T):
            nsub = (T + 127) // 128
            for j in range(nsub):
                Tj = min(128, T - j * 128)
                S.dma(lambda e, j=j, Tj=Tj: e.dma_start(out=xt2[j][0:Tj, :], in_=scr[r0 + j * 128:r0 + j * 128 + Tj, :]),
                      "xt2_%d" % j, r=["scr%d" % ((r0 + j * 128) // 128)], w=["xt2_%d" % j])
                rms_rstd(xt2[j], Tj, "xt2_%d" % j, junk2, ss2, rstd2, "2")
                S.act(lambda e, j=j, Tj=Tj: e.activation(out=xn2[0:Tj, :], in_=xt2[j][0:Tj, :], func=AF.Copy, scale=rstd2[0:Tj, 0:1]),
                      r=["xt2_%d" % j, "rstd2"], w=["xn2"])
                for k in range(8):
                    S.pe(lambda e, k=k, Tj=Tj: e.transpose(out=pT[:, k * 128:k * 128 + Tj], in_=xn2[0:Tj, k * 128:(k + 1) * 128],
                                                           identity=ident[0:Tj, 0:Tj]), r=["xn2", "cst"], w=["pT"])
                S.dve(lambda e, j=j, Tj=Tj: e.tensor_tensor(
                    out=mT[:, :, j * 128:j * 128 + Tj], in0=pT.rearrange("p (k t) -> p k t", k=8)[:, :, 0:Tj],
                    in1=P("GP", 0, 8).unsqueeze(2).to_broadcast([128, 8, Tj]), op=ALU.mult),
                    r=["pT", "pfm"], w=["mT"])
            for f in range(32):
                pa = pA[f % 2]
                for k in range(8):
                    S.pe(lambda e, k=k, f=f, pa=pa: e.matmul(pa[:, 0:T], lhsT=w_up_sb[:, k, f * 128:(f + 1) * 128], rhs=mT[:, k, 0:T],
                                                             start=(k == 0), stop=(k == 7)),
                         r=["mT", "w_up"], w=["pA%d" % (f % 2)])
                S.act(lambda e, f=f, pa=pa: e.activation(out=rl[f % 2][:, 0:T], in_=pa[:, 0:T], func=AF.Relu),
                      r=["pA%d" % (f % 2)], w=["rl%d" % (f % 2)])
                S.pool(lambda e, f=f: e.tensor_tensor(out=actb[:, f, 0:T], in0=rl[f % 2][:, 0:T], in1=rl[f % 2][:, 0:T], op=ALU.mult),
                       r=["rl%d" % (f % 2)], w=["act%d" % f])
            for j in range(nsub):
                Tj = min(128, T - j * 128)
                for nb in range(2):
                    for f in range(32):
                        S.pe(lambda e, f=f, nb=nb, j=j, Tj=Tj: e.matmul(pO[0:Tj, nb * 512:(nb + 1) * 512],
                                                                        lhsT=actb[:, f, j * 128:j * 128 + Tj],
                                                                        rhs=w_dn_sb[:, f, nb * 512:(nb + 1) * 512],
                                                                        start=(f == 0), stop=(f == 31)),
                             r=["act%d" % f, "w_dn"], w=["pO"])
                S.dve(lambda e, j=j, Tj=Tj: e.tensor_tensor(out=xt2[j][0:Tj, :], in0=pO[0:Tj, :], in1=xt2[j][0:Tj, :], op=ALU.add),
                      r=["pO", "xt2_%d" % j], w=["xt2_%d" % j])
                rms_rstd(xt2[j], Tj, "xt2_%d" % j, junk2, ss2, rstd2, "2")
                S.dve(lambda e, j=j, Tj=Tj: e.scalar_tensor_tensor(out=yout[0:Tj, :], in0=xt2[j][0:Tj, :], scalar=rstd2[0:Tj, 0:1],
                                                                   in1=gfin_bc[0:Tj, :], op0=ALU.mult, op1=ALU.mult),
                      r=["xt2_%d" % j, "rstd2", "gfin"], w=["yout"])
                rr = r0 + j * 128
                if rr < SEQ:
                    S.dma(lambda e, rr=rr, Tj=Tj: e.dma_start(out=y_p[rr:rr + Tj, :], in_=yout[0:Tj, :]), "yout", r=["yout"])
                else:
                    S.dma(lambda e, Tj=Tj: e.dma_start(out=y_s, in_=yout[0:Tj, :]), "yout", r=["yout"])

        if MLP:
            for t in range(NT * 128 // T2):
                mlp_tile(t * T2, T2)
            if SAMP:
                mlp_tile(SEQ, TS)

        S.emit()
    return nc


_CACHE = {}


def _consts():
    c = np.zeros((128, NCST), np.float32)
    i = np.arange(128)
    c[:, CI:CI + 128] = np.eye(128, dtype=np.float32)
    c[:, CU:CU + 128] = (i[:, None] <= i[None, :]).astype(np.float32)
    c[:, CN:CN + 128] = np.where(i[:, None] <= i[None, :], 0.0, NEG).astype(np.float32)
    c[:, CO:CO + 128] = 1.0
    j = np.arange(TS)
    same = (j[:, None] // 4) == (j[None, :] // 4)
    caus = j[:, None] <= j[None, :]
    c[0:TS, CUB:CUB + TS] = (same & caus).astype(np.float32)
    c[0:TS, CNB:CNB + TS] = np.where(same & caus, 0.0, NEG).astype(np.float32)
    c[0:TS, CBM:CBM + TS] = same.astype(np.float32)
    c[0:TS, CBI:CBI + NS] = ((j[:, None] // 4) == np.arange(NS)[None, :]).astype(np.float32)
    return c


def _fm(v, nch):
    return np.ascontiguousarray(np.asarray(v, np.float32).reshape(nch, 128).T)


def kernel(x_prompt, x_sample, state_lru_conv, state_lru_h, state_ssd_conv, state_ssd_h,
           g_mix, w_in, lru_conv_w, lru_conv_b, w_a, b_a, w_x, b_x, lam, g_lru_out,
           ssd_conv_w, ssd_conv_b, dt_bias, a_log, d_skip, g_ssd_out, w_out,
           g_mlp, w_up, w_down, g_final):
    f = lambda a: np.ascontiguousarray(np.asarray(a, np.float32))
    if "nc" not in _CACHE:
        _CACHE["nc"] = build_program()
    nc = _CACHE["nc"]
    pfm = np.zeros((128, NPAR), np.float32)
    lw = np.asarray(lru_conv_w[0], np.float32)
    pfm[:, PC["LW"]:PC["LW"] + 32] = lw.reshape(4, 8, 128).transpose(2, 1, 0).reshape(128, 32)
    pfm[:, PC["LB"]:PC["LB"] + 8] = _fm(lru_conv_b[0], 8)
    pfm[:, PC["BA"]:PC["BA"] + 8] = _fm(np.asarray(b_a[0]).reshape(-1), 8)
    pfm[:, PC["BX"]:PC["BX"] + 8] = _fm(np.asarray(b_x[0]).reshape(-1), 8)
    pfm[:, PC["LAM"]:PC["LAM"] + 8] = _fm(lam[0], 8)
    pfm[:, PC["GL"]:PC["GL"] + 8] = _fm(g_lru_out[0], 8)
    sw = np.asarray(ssd_conv_w[0], np.float32)
    pfm[:, PC["SW"]:PC["SW"] + 48] = sw.reshape(4, 12, 128).transpose(2, 1, 0).reshape(128, 48)
    pfm[:, PC["SB"]:PC["SB"] + 12] = _fm(ssd_conv_b[0], 12)
    pfm[:, PC["DS"]:PC["DS"] + 8] = _fm(np.repeat(np.asarray(d_skip[0], np.float32), 64), 8)
    pfm[:, PC["GS"]:PC["GS"] + 8] = _fm(g_ssd_out[0], 8)
    pfm[:, PC["GM"]:PC["GM"] + 8] = _fm(g_mix[0], 8)
    pfm[:, PC["GP"]:PC["GP"] + 8] = _fm(g_mlp[0], 8)
    cst = _consts()
    shared = {
        "w_in": f(w_in[0]), "w_out": f(w_out[0]), "w_up": f(w_up[0]), "w_down": f(w_down[0]),
        "w_a": f(w_a[0]), "w_x": f(w_x[0]), "pfm": pfm, "cst": cst,
        "dt_bias": f(dt_bias[0]), "a_log": f(a_log[0]), "g_final": f(g_final),
    }
    in_maps = []
    for b in range(NCORES):
        sl = slice(NS * b, NS * (b + 1))
        m = dict(shared)
        m["xp"] = f(x_prompt[b])
        m["xs"] = f(np.asarray(x_sample[sl]).reshape(TS, D))
        m["st_lc"] = f(np.asarray(state_lru_conv[0, sl]).reshape(NS * 3, D))
        m["st_lh"] = f(state_lru_h[0, sl])
        m["st_sc"] = f(np.asarray(state_ssd_conv[0, sl]).reshape(NS * 3, XBC))
        m["st_sh"] = f(np.asarray(state_ssd_h[0, sl]).reshape(NS, 1024, 128))
        in_maps.append(m)
    res = run_bass_kernel_spmd(nc, in_maps, core_ids=list(range(NCORES)))
    R = res.results
    cat = lambda k: np.stack([np.asarray(R[b][k], np.float32) for b in range(NCORES)])
    y_prompt = cat("y_p")
    y_sample = cat("y_s").reshape(NCORES * NS, 4, D)
    p_lc = cat("o_plc")[None]
    p_lh = cat("o_plh").reshape(NCORES, D)[None]
    p_sc = cat("o_psc")[None]
    p_sh = cat("o_psh").reshape(NCORES, 16, 64, 128)[None]
    s_lc = cat("o_slc").reshape(NCORES * NS, 3, D)[None]
    s_lh = cat("o_slh").reshape(NCORES * NS, D)[None]
    s_sc = cat("o_ssc").reshape(NCORES * NS, 3, XBC)[None]
    s_sh = cat("o_ssh").reshape(NCORES * NS, 16, 64, 128)[None]
    return (y_prompt, y_sample, p_lc, p_lh, p_sc, p_sh, s_lc, s_lh, s_sc, s_sh)
```

```python
import math
from contextlib import ExitStack

import numpy as np
import concourse.bass as bass
import concourse.mybir as mybir
from concourse.bass_utils import run_bass_kernel_spmd

F32 = mybir.dt.float32
BF16 = mybir.dt.bfloat16
AF = mybir.ActivationFunctionType
ALU = mybir.AluOpType

NCORES = 8
D = 1024
SEQ = 2048
NS = 16
TS = 64
XBC = 1536
INP = 4624
DFF = 4096
EPS = 1e-6
NEG = -30000.0

import re as _re

ENGS = ("pe", "act", "dve", "pool", "sp")
SAME_ENGINE_SYNC = {"pe": False, "act": True, "dve": True, "pool": True, "sp": False}


class Op:
    __slots__ = ("eng", "fn", "deps", "marked", "count", "dma_key", "dma_val")

    def __init__(self, eng, fn, dma_key=None):
        self.eng = eng
        self.fn = fn
        self.deps = ()
        self.marked = False
        self.count = 0
        self.dma_key = dma_key
        self.dma_val = 0


class Sched:
    def __init__(self, nc):
        self.nc = nc
        self.ops = {e: [] for e in ENGS}
        self.last_w = {}
        self.readers = {}
        self.dma_cnt = {}
        self.pending = {}
        self.ctx = None
        self.since_bar = []

    ALIAS = {"pC": "b4", "pCx": "b4", "pD": "b5", "pD2": "b5", "pD3": "b5", "pDn": "b5",
             "pDcb0": "b5", "pDcb1": "b5", "pT": "b01", "pO": "b67", "pA0": "b2", "pA1": "b3"}

    PSUM_KEYS = {"b01", "b2", "b3", "b4", "b5", "b67"}

    PAR_RE = _re.compile(r"^(xt|dtt|xnew)$|^(xsf|zs|Bb|Cb|ynl|yg)\d+$")

    def _k(self, k):
        k = self.ALIAS.get(k, k)
        if self.ctx is not None and self.PAR_RE.match(k):
            return k + "#" + str(self.ctx)
        return k

    def add(self, eng, fn, reads=(), writes=(), dma_key=None):
        reads = [self._k(k) for k in reads]
        writes = [self._k(k) for k in writes]
        if dma_key is not None and self.ctx is not None and self.PAR_RE.match(dma_key):
            dma_key = dma_key + "#" + str(self.ctx)
        op = Op(eng, fn, dma_key)
        deps = []
        seen = set()

        def dep(o):
            if o is not None and o is not op and id(o) not in seen:
                seen.add(id(o))
                deps.append(o)

        if self.pending.get(eng):
            for o in self.pending[eng]:
                dep(o)
            self.pending[eng] = []
        for b in reads:
            dep(self.last_w.get(b))
            if b in self.PSUM_KEYS:
                for r in self.readers.get(b, ()):
                    if r.eng != eng:
                        dep(r)
        for b in writes:
            dep(self.last_w.get(b))
            for r in self.readers.get(b, ()):
                dep(r)
        for b in reads:
            self.readers.setdefault(b, []).append(op)
        for b in writes:
            self.last_w[b] = op
            self.readers[b] = []
        op.deps = deps
        if dma_key is not None:
            self.dma_cnt[dma_key] = self.dma_cnt.get(dma_key, 0) + 16
            op.dma_val = self.dma_cnt[dma_key]
        self.ops[eng].append(op)
        self.since_bar.append(op)
        return op

    def barrier(self):
        ops = []
        for e in ENGS:
            comp = [o for o in self.ops[e] if o.dma_key is None]
            if comp:
                ops.append(comp[-1])
        last_dma = {}
        for o in self.since_bar:
            if o.dma_key is not None:
                last_dma[o.dma_key] = o
        ops.extend(last_dma.values())
        for e in ENGS:
            self.pending.setdefault(e, []).extend(ops)
        self.since_bar = []

    def pe(self, fn, r=(), w=()):
        return self.add("pe", fn, r, w)

    def act(self, fn, r=(), w=()):
        return self.add("act", fn, r, w)

    def dve(self, fn, r=(), w=()):
        return self.add("dve", fn, r, w)

    def pool(self, fn, r=(), w=()):
        return self.add("pool", fn, r, w)

    def dma(self, fn, key, r=(), w=(), q="sp"):
        return self.add(q, fn, r, w, dma_key=key)

    def emit(self):
        nc = self.nc
        for e in ENGS:
            for op in self.ops[e]:
                for d in op.deps:
                    if d.dma_key is None:
                        if d.eng == op.eng and not SAME_ENGINE_SYNC[d.eng]:
                            continue
                        d.marked = True
        for e in ENGS:
            c = 0
            for op in self.ops[e]:
                if op.dma_key is None and op.marked:
                    c += 1
                    op.count = c
        with ExitStack() as st:
            esem = {e: st.enter_context(nc.semaphore("es_" + e)) for e in ENGS}
            dsem = {}
            for k in self.dma_cnt:
                dsem[k] = st.enter_context(nc.semaphore("ds_%d" % len(dsem)))
            block = st.enter_context(nc.Block())

            def run(ename, eng):
                seen = {}
                for op in self.ops[ename]:
                    need = {}
                    for d in op.deps:
                        if d.dma_key is not None:
                            key = ("d", d.dma_key)
                            val = d.dma_val
                            sem = dsem[d.dma_key]
                        else:
                            if d.eng == ename and not SAME_ENGINE_SYNC[ename]:
                                continue
                            key = ("e", d.eng)
                            val = d.count
                            sem = esem[d.eng]
                        if key not in need or need[key][1] < val:
                            need[key] = (sem, val)
                    for key, (sem, val) in need.items():
                        if seen.get(key, 0) >= val:
                            continue
                        seen[key] = val
                        eng.wait_ge(sem, val)
                    ins = op.fn(eng)
                    if op.dma_key is not None:
                        ins.then_inc(dsem[op.dma_key], 16)
                    elif op.marked:
                        ins.then_inc(esem[ename], 1)
                if ename == "sp":
                    for k, v in self.dma_cnt.items():
                        eng.wait_ge(dsem[k], v)

            @block.sync
            def _(e):
                run("sp", e)

            @block.tensor
            def _(e):
                run("pe", e)

            @block.scalar
            def _(e):
                run("act", e)

            @block.vector
            def _(e):
                run("dve", e)

            @block.gpsimd
            def _(e):
                run("pool", e)


PC = {}
_o = 0
for _n, _w in (("LW", 32), ("LB", 8), ("BA", 8), ("BX", 8), ("LAM", 8), ("GL", 8), ("SW", 48),
               ("SB", 12), ("DS", 8), ("GS", 8), ("GM", 8), ("GP", 8)):
    PC[_n] = _o
    _o += _w
NPAR = _o
CI, CU, CN, CO, CUB, CNB, CBM, CBI = 0, 128, 256, 384, 512, 576, 640, 704
NCST = 720


def build_program(NT=SEQ // 128, SAMP=True, MLP=True, DBG=False, STAGE=9):
    nc = bass.Bass("TRN2", target_bir_lowering=False)
    S = Sched(nc)

    def din(name, shape):
        return nc.dram_tensor(name, list(shape), F32, kind="ExternalInput").ap()

    def dout(name, shape):
        return nc.dram_tensor(name, list(shape), F32, kind="ExternalOutput").ap()

    xp = din("xp", (SEQ, D))
    xs = din("xs", (TS, D))
    st_lc = din("st_lc", (NS * 3, D))
    st_lh = din("st_lh", (NS, D))
    st_sc = din("st_sc", (NS * 3, XBC))
    st_sh = din("st_sh", (NS, 1024, 128))
    w_in = din("w_in", (D, INP))
    w_out = din("w_out", (2 * D, D))
    w_up = din("w_up", (D, DFF))
    w_down = din("w_down", (DFF, D))
    w_a = din("w_a", (16, 64, 64))
    w_x = din("w_x", (16, 64, 64))
    pfm_d = din("pfm", (128, NPAR))
    cst_d = din("cst", (128, NCST))
    dtb_d = din("dt_bias", (16,))
    alog_d = din("a_log", (16,))
    gfin_d = din("g_final", (D,))

    y_p = dout("y_p", (SEQ, D))
    y_s = dout("y_s", (TS, D))
    o_plc = dout("o_plc", (3, D))
    o_plh = dout("o_plh", (8, 128))
    o_psc = dout("o_psc", (3, XBC))
    o_psh = dout("o_psh", (1024, 128))
    o_slc = dout("o_slc", (NS, 3, D))
    o_slh = dout("o_slh", (NS, D))
    o_ssc = dout("o_ssc", (NS, 3, XBC))
    o_ssh = dout("o_ssh", (NS, 1024, 128))
    scr = nc.dram_tensor("scr", [SEQ + TS, D], F32, kind=("ExternalOutput" if DBG else "Internal")).ap()

    st = ExitStack()
    with st:
        RW = 53200
        R = st.enter_context(nc.sbuf_tensor("R", [128, RW], F32))
        PS = st.enter_context(nc.psum_tensor("PS", [128, 4096], F32))
        ptr = [0]

        def alloc(nwords):
            a = ptr[0]
            ptr[0] += (nwords + 7) // 8 * 8
            pass
            return a

        def f32(n):
            a = alloc(n)
            return R[:, a:a + n]

        def bf(n):
            w = (n + 1) // 2
            a = alloc(w)
            return R[:, a:a + w].bitcast(BF16)[:, 0:n]

        def f3(c, t):
            return f32(c * t).rearrange("p (c t) -> p c t", c=c)

        def b3(c, t):
            return bf(c * t).rearrange("p (c t) -> p c t", c=c)

        def bank(b, n=512):
            return PS[:, 512 * b:512 * b + n]

        pT = PS[:, 0:1024]
        pTb = pT.bitcast(BF16)
        pA = [bank(2), bank(3)]
        pC = bank(4)
        pCb = pC.bitcast(BF16)
        pD = bank(5)
        pO = PS[:, 3072:4096]

        cst = f32(NCST)
        pfm = f32(NPAR)
        dtb_bc = f32(16)
        a_bc = f32(16)
        identb = bf(128)
        onesb = bf(128)
        Utrib = bf(128)
        mskb = bf(3 * TS)
        dah = bf(16)
        dal = bf(16)
        cfac = f32(8)
        c2fac = f32(8)
        tiny = f32(8)
        mhalf = f32(4)
        eps_t = f32(4)
        nbias = f32(16)
        wa_blk = b3(8, 128)
        wx_blk = b3(8, 128)
        hstate = f32(8)
        hT = f32(1024)
        hTb = bf(1024)

        ident = cst[:, CI:CI + 128]
        Utri = cst[:, CU:CU + 128]
        negm = cst[:, CN:CN + 128]
        onesf = cst[:, CO:CO + 128]
        Ublk = cst[0:TS, CUB:CUB + TS]
        negblk = cst[0:TS, CNB:CNB + TS]
        blkm = cst[0:TS, CBM:CBM + TS]
        blki = cst[0:TS, CBI:CBI + NS]

        S.dma(lambda e: e.dma_start(out=cst, in_=cst_d), "cst", w=["cst"])
        S.dma(lambda e: e.dma_start(out=pfm, in_=pfm_d), "pfm", w=["pfm"])
        S.dma(lambda e: e.dma_start(out=dtb_bc, in_=dtb_d.partition_broadcast(128)), "dtb", w=["dtb"])
        S.dma(lambda e: e.dma_start(out=a_bc, in_=alog_d.partition_broadcast(128)), "alog", w=["a_bc"])
        S.dve(lambda e: e.tensor_copy(out=identb, in_=ident), r=["cst"], w=["identb"])
        S.dve(lambda e: e.tensor_copy(out=onesb, in_=onesf), r=["cst"], w=["onesb"])
        S.dve(lambda e: e.tensor_copy(out=Utrib, in_=Utri), r=["cst"], w=["mskb"])
        S.dve(lambda e: e.tensor_copy(out=mskb[0:TS, 0:TS], in_=Ublk), r=["cst"], w=["mskb"])
        S.dve(lambda e: e.tensor_copy(out=mskb[0:TS, TS:2 * TS], in_=blkm), r=["cst"], w=["mskb"])
        S.pool(lambda e: e.memset(mhalf, -0.5), w=["mhalf"])
        S.pool(lambda e: e.memset(eps_t, EPS), w=["eps_t"])
        S.dve(lambda e: e.tensor_scalar(out=nbias[:, 0:8], in0=pfm[:, PC["BA"]:PC["BA"] + 8], scalar1=-1.0, scalar2=None, op0=ALU.mult), r=["pfm"], w=["nbias"])
        S.dve(lambda e: e.tensor_scalar(out=nbias[:, 8:16], in0=pfm[:, PC["BX"]:PC["BX"] + 8], scalar1=-1.0, scalar2=None, op0=ALU.mult), r=["pfm", "nbias"], w=["nbias"])
        S.pool(lambda e: e.memset(hstate, 0.0), w=["hstate"])
        S.pool(lambda e: e.memset(hT, 0.0), w=["hT"])
        S.pool(lambda e: e.memset(hTb, 0.0), w=["hTb"])
        S.pool(lambda e: e.memset(wa_blk, 0.0), w=["wa"])
        S.pool(lambda e: e.memset(wx_blk, 0.0), w=["wx"])
        S.act(lambda e: e.activation(out=a_bc, in_=a_bc, func=AF.Exp), r=["a_bc"], w=["a_bc"])
        S.dve(lambda e: e.tensor_scalar(out=a_bc, in0=a_bc, scalar1=-1.0, scalar2=None, op0=ALU.mult), r=["a_bc"], w=["a_bc"])
        lam = pfm[:, PC["LAM"]:PC["LAM"] + 8]
        S.act(lambda e: e.activation(out=tiny, in_=lam, func=AF.Exp, scale=-1.0), r=["pfm"], w=["tiny"])
        S.act(lambda e: e.activation(out=tiny, in_=tiny, func=AF.Ln, bias=1.0), r=["tiny"], w=["tiny"])
        S.dve(lambda e: e.tensor_scalar(out=cfac, in0=tiny, scalar1=-8.0, scalar2=None, op0=ALU.mult), r=["tiny"], w=["cfac"])
        S.dve(lambda e: e.tensor_scalar(out=c2fac, in0=tiny, scalar1=-16.0, scalar2=None, op0=ALU.mult), r=["tiny"], w=["cfac2"])
        for (wd, blk, nm) in ((w_a, wa_blk, "wa"), (w_x, wx_blk, "wx")):
            v = wd.rearrange("(c h) i j -> h i c j", h=2)
            for h2 in range(2):
                S.dma(lambda e, v=v, blk=blk, h2=h2: e.dma_start(
                    out=blk[64 * h2:64 * h2 + 64, :, 64 * h2:64 * h2 + 64], in_=v[h2]),
                    nm + str(h2), w=[nm], q="pool")

        base0 = ptr[0]

        def load_w(dst3, src2, nk, ncol, name, step=2048):
            sv = src2.rearrange("(k p) n -> p k n", p=128)
            pieces = [(k, c0, min(ncol, c0 + step)) for k in range(nk) for c0 in range(0, ncol, step)]
            for i, (k, c0, c1) in enumerate(pieces):
                S.dma(lambda e, k=k, c0=c0, c1=c1: e.dma_start(out=dst3[:, k, c0:c1], in_=sv[:, k, c0:c1]),
                      name, w=([name] if i == len(pieces) - 1 else []), q="pool")
            return name

        w_in_sb = b3(8, INP)
        w_out_sb = b3(16, D)
        if STAGE >= 1:
            k_win = load_w(w_in_sb, w_in, 8, INP, "w_in")
            k_wout = load_w(w_out_sb, w_out, 16, D, "w_out")

        xt = f32(D)
        xn = f32(D)
        junk = xn
        ss = f32(4)
        rstd = f32(4)
        hTt = b3(8, 128)
        lxb = f3(8, 131)
        xcb = f3(12, 131)
        sreg = f32(20 * NS * 7)
        lxs = sreg[:, 0:8 * NS * 7].rearrange("p (c s l) -> p c s l", c=8, s=NS)
        xcs = sreg[:, 8 * NS * 7:20 * NS * 7].rearrange("p (c s l) -> p c s l", c=12, s=NS)
        gl = f3(2, 128)
        zs = f3(8, 128)
        u = f3(8, 128)
        ub = b3(8, 128)
        gi = f3(4, 128)
        av = f3(2, 128)
        a2 = f3(2, 128)
        tmpb = f3(2, 128)
        hs = f3(8, 128)
        ysq = b3(2, 128)
        rbc = f32(128)
        ynl = b3(8, 128)
        yns = b3(8, 128)
        xsf = f3(8, 128)
        Bb = b3(2, 128)
        Cb = b3(2, 128)
        dtr = f32(16)
        dtt = f32(16)
        da = f32(16)
        ncum = f32(16)
        dte = f32(16)
        cdec = f32(16)
        xdt = bf(1024)
        xdd = bf(1024)
        BT = bf(256)
        cbT = f3(2, 128)
        Dmf = f32(512)
        Emf = f32(512)
        Mmf = bf(512)
        Chf = bf(1024)
        Chp = Chf[:, 0:512].rearrange("p (a t) -> p a t", a=4)
        Chs = Chf.rearrange("p (a t) -> p a t", a=16)
        cvt = f3(2, 128)
        stg = f32(2560)
        stT = stg[:, 0:1024]
        lc_in = stg[:, 0:1024]
        sc_in = stg[:, 1024:2560]
        lh_in = stg[:, 0:1024]
        h0in = lxb.rearrange("p c t -> p (c t)")[:, 0:1024].rearrange("p (c t) -> p c t", c=8)
        h0Tb = hTb
        Bm = bf(256)
        pyo_f = f32(8 * TS)
        pyo_sb = pyo_f.rearrange("p (c t) -> p c t", c=8)
        damb = bf(2 * NS * 16).rearrange("p (i s h) -> p i s h", i=2, s=NS)
        dtot = f3(NS, 16)
        hnew = hT
        hout = xcb.rearrange("p c t -> p (c t)")[:, 0:1024].rearrange("p (c t) -> p c t", c=8)
        h0s = f3(8, NS)
        hfin = f3(8, NS)

        bf3 = lambda ap, c: ap.bitcast(BF16).rearrange("p (c t) -> p c t", c=c)
        xt_b = [xt, stg[:, 0:1024]]
        ynl_b = [ynl, bf3(stg[:, 1024:1536], 8)]
        Bb_b = [Bb, bf3(stg[:, 1536:1664], 2)]
        Cb_b = [Cb, bf3(stg[:, 1664:1792], 2)]
        dtt_b = [dtt, stg[:, 1792:1808]]
        xsf_b = [xsf, sreg[:, 0:1024].rearrange("p (c t) -> p c t", c=8)]
        zs_b = [zs, sreg[:, 1024:2048].rearrange("p (c t) -> p c t", c=8)]
        Wl_S = pyo_f[:, 0:256].bitcast(BF16)
        ysq_S = bf3(pyo_f[:, 256:384], 2)
        rbc_S = pyo_f[:, 384:512]
        STGW = [k + "#1" for k in ["xt", "dtt"] + ["ynl%d" % c for c in range(8)] + ["Bb0", "Bb1", "Cb0", "Cb1"]]
        STG_ALIAS = ["xt", "dtt"] + ["ynl%d" % c for c in range(8)] + ["Bb0", "Bb1", "Cb0", "Cb1"]

        def P(name, c=None, w=1):
            o = PC[name] + (0 if c is None else c * w)
            return pfm[:, o:o + w]

        def rms_rstd(xtile, T, keyx, junk, ss, rstd, sfx="", jkey=None):
            S.act(lambda e: e.activation(out=junk[0:T, :], in_=xtile[0:T, :], func=AF.Square, accum_out=ss[0:T, 0:1]),
                  r=[keyx], w=[jkey or ("xn" + sfx), "ss" + sfx])
            S.act(lambda e: e.activation(out=ss[0:T, 0:1], in_=ss[0:T, 0:1], func=AF.Ln, scale=1.0 / D, bias=eps_t[0:T, 0:1]),
                  r=["ss" + sfx, "eps_t"], w=["ss" + sfx])
            S.act(lambda e: e.activation(out=rstd[0:T, 0:1], in_=ss[0:T, 0:1], func=AF.Exp, scale=-0.5),
                  r=["ss" + sfx], w=["rstd" + sfx])

        pAA = PS[:, 1024:2048]

        def to_fm(T, gname, dst, dkey):
            for k in range(8):
                S.pe(lambda e, k=k: e.transpose(out=pAA[:, k * 128:k * 128 + T], in_=xn[0:T, k * 128:(k + 1) * 128],
                                                identity=ident[0:T, 0:T]), r=["xn", "cst"], w=["pA0", "pA1"])
            S.dve(lambda e: e.tensor_tensor(
                out=dst[:, :, 0:T], in0=pAA.rearrange("p (k t) -> p k t", k=8)[:, :, 0:T],
                in1=P(gname, 0, 8).unsqueeze(2).to_broadcast([128, 8, T]), op=ALU.mult),
                r=["pA0", "pA1", "pfm"], w=[dkey])

        def mixer_tile(mt, samp):
            T = TS if samp else 128
            row0 = SEQ if samp else mt * 128
            xsrc = xs if samp else xp[mt * 128:(mt + 1) * 128, :]
            last = (not samp) and mt == NT - 1
            par = 0 if samp else (NT - 1 - mt) % 2
            xt, ynl, Bb, Cb, dtt, xsf, zs = (xt_b[par], ynl_b[par], Bb_b[par], Cb_b[par], dtt_b[par], xsf_b[par], zs_b[par])
            Wl = cvt.rearrange("p a t -> p (a t)").bitcast(BF16) if samp else Wl_S
            wlk = ["cv_t0", "cv_t1"] if samp else ["WlS"]
            ysqS = ysq if samp else ysq_S
            rbcS = rbc if samp else rbc_S
            sk = "" if samp else "S"

            def inter(*gens):
                gens = list(gens)
                while gens:
                    for g_ in list(gens):
                        try:
                            next(g_)
                        except StopIteration:
                            gens.remove(g_)
                        yield

            pcnt = [0]

            def proj(ci):
                i = pcnt[0] % 2
                pcnt[0] += 1
                pa = pA[i]
                for k in range(8):
                    S.pe(lambda e, k=k: e.matmul(pa[:, 0:T], lhsT=w_in_sb[:, k, ci * 128:(ci + 1) * 128],
                                                 rhs=hTt[:, k, 0:T], start=(k == 0), stop=(k == 7)),
                         r=["hTt", "w_in"], w=["pA%d" % i])
                return pa, "pA%d" % i

            def new_cols(buf, sbuf_, c):
                if samp:
                    return sbuf_[:, c, :, 3:7]
                return buf[:, c, 3:131]

            def pa_view(pa):
                if samp:
                    return pa[:, 0:T].rearrange("p (s l) -> p s l", s=NS)
                return pa[:, 0:T]

            def tap(buf, sbuf_, c, k):
                if samp:
                    return sbuf_[:, c, :, k:k + 4]
                return buf[:, c, k:k + 128]

            def fm(t3, c):
                if samp:
                    return t3[:, c, 0:T].rearrange("p (s l) -> p s l", s=NS)
                return t3[:, c, 0:T]

            def conv(buf, sbuf_, c, wname, bname, out_ap, key_in, key_out):
                S.dve(lambda e: e.tensor_scalar(out=out_ap, in0=tap(buf, sbuf_, c, 3), scalar1=P(wname, c, 4)[:, 3:4],
                                                scalar2=P(bname, c), op0=ALU.mult, op1=ALU.add),
                      r=[key_in, "pfm"], w=[key_out])
                for k in (2, 1, 0):
                    S.dve(lambda e, k=k: e.scalar_tensor_tensor(out=out_ap, in0=tap(buf, sbuf_, c, k),
                                                                scalar=P(wname, c, 4)[:, k:k + 1], in1=out_ap,
                                                                op0=ALU.mult, op1=ALU.add),
                          r=[key_in, key_out, "pfm"], w=[key_out])
                if not samp:
                    S.dve(lambda e: e.tensor_copy(out=buf[:, c, 0:3], in_=buf[:, c, 128:131]), r=[key_in], w=[key_in])

            def g_lrux():
                for c in range(8):
                    pa, pk = proj(c)
                    S.act(lambda e, c=c, pa=pa: e.activation(out=new_cols(lxb, lxs, c), in_=pa_view(pa), func=AF.Copy),
                          r=[pk], w=["lx%d" % c])
                    conv(lxb, lxs, c, "LW", "LB", fm(u, c), "lx%d" % c, "u%d" % c)
                    yield

            def g_z():
                for c in range(8):
                    pa, pk = proj(16 + c)
                    S.act(lambda e, c=c, pa=pa: e.activation(out=zs[:, c, 0:T], in_=pa[:, 0:T], func=AF.Silu),
                          r=[pk], w=["zs%d" % c])
                    yield

            def g_xbc():
                for c in range(12):
                    pa, pk = proj(24 + c)
                    S.act(lambda e, c=c, pa=pa: e.activation(out=new_cols(xcb, xcs, c), in_=pa_view(pa), func=AF.Copy),
                          r=[pk], w=["xc%d" % c])
                    if c < 8:
                        conv(xcb, xcs, c, "SW", "SB", fm(cvt, c % 2), "xc%d" % c, "cv_t%d" % (c % 2))
                        S.act(lambda e, c=c: e.activation(out=xsf[:, c, 0:T], in_=cvt[:, c % 2, 0:T], func=AF.Silu),
                              r=["cv_t%d" % (c % 2)], w=["xsf%d" % c])
                    else:
                        g = (c - 8) % 2
                        dstb = Bb if c < 10 else Cb
                        nm = ("Bb%d" if c < 10 else "Cb%d") % g
                        conv(xcb, xcs, c, "SW", "SB", fm(cvt, g), "xc%d" % c, "cv_t%d" % g)
                        S.act(lambda e, g=g, dstb=dstb: e.activation(out=dstb[:, g, 0:T], in_=cvt[:, g, 0:T], func=AF.Silu),
                              r=["cv_t%d" % g], w=[nm])
                    yield
                for k in range(8):
                    S.pe(lambda e, k=k: e.matmul(pD[0:T, 0:16], lhsT=hTt[:, k, 0:T], rhs=w_in_sb[:, k, 4608:4624],
                                                 start=(k == 0), stop=(k == 7)),
                         r=["hTt", "w_in"], w=["pD"])
                S.dve(lambda e: e.tensor_tensor(out=dtr[0:T, :], in0=pD[0:T, 0:16], in1=dtb_bc[0:T, :], op=ALU.add),
                      r=["pD", "dtb"], w=["dtr"])
                S.act(lambda e: e.activation(out=dtr[0:T, :], in_=dtr[0:T, :], func=AF.Exp), r=["dtr"], w=["dtr"])
                S.act(lambda e: e.activation(out=dtt[0:T, :], in_=dtr[0:T, :], func=AF.Ln, bias=1.0), r=["dtr"], w=["dtt"])
                yield

            def g_lru(chunks, pg, kr, ki):
                for c in chunks:
                    pp = c % 2
                    S.act(lambda e, c=c: e.activation(out=ub[:, c, 0:T], in_=u[:, c, 0:T], func=AF.Copy),
                          r=["u%d" % c], w=["ub%d" % c])
                    yield
                    S.pe(lambda e, c=c: e.matmul(pg[:, 0:T], lhsT=wa_blk[:, c, :], rhs=ub[:, c, 0:T], start=True, stop=True),
                         r=["ub%d" % c, "wa"], w=[kr])
                    S.pe(lambda e, c=c: e.matmul(pg[:, 128:128 + T], lhsT=wx_blk[:, c, :], rhs=ub[:, c, 0:T], start=True, stop=True),
                         r=["ub%d" % c, "wx"], w=[ki])
                    yield
                    S.act(lambda e, c=c, pp=pp: e.activation(out=gi[:, 2 * pp, 0:T], in_=pg[:, 0:T], func=AF.Exp, scale=-1.0, bias=nbias[:, c:c + 1]),
                          r=[kr, "nbias"], w=["rg%d" % pp])
                    S.act(lambda e, c=c, pp=pp: e.activation(out=gi[:, 2 * pp + 1, 0:T], in_=pg[:, 128:128 + T], func=AF.Exp, scale=-1.0, bias=nbias[:, 8 + c:9 + c]),
                          r=[ki, "nbias"], w=["ig%d" % pp])
                    S.act(lambda e, pp=pp: e.activation(out=gi[:, 2 * pp:2 * pp + 2, 0:T], in_=gi[:, 2 * pp:2 * pp + 2, 0:T], func=AF.Ln, bias=1.0),
                          r=["rg%d" % pp, "ig%d" % pp], w=["rg%d" % pp, "ig%d" % pp])
                    S.act(lambda e, pp=pp: e.activation(out=gi[:, 2 * pp:2 * pp + 2, 0:T], in_=gi[:, 2 * pp:2 * pp + 2, 0:T], func=AF.Exp, scale=-1.0),
                          r=["rg%d" % pp, "ig%d" % pp], w=["rg%d" % pp, "ig%d" % pp])
                    S.act(lambda e, c=c, pp=pp: e.activation(out=av[:, pp, 0:T], in_=gi[:, 2 * pp, 0:T], func=AF.Exp, scale=cfac[:, c:c + 1]),
                          r=["rg%d" % pp, "cfac"], w=["av%d" % pp])
                    S.act(lambda e, c=c, pp=pp: e.activation(out=a2[:, pp, 0:T], in_=gi[:, 2 * pp, 0:T], func=AF.Exp, scale=c2fac[:, c:c + 1]),
                          r=["rg%d" % pp, "cfac2"], w=["a2%d" % pp])
                    S.act(lambda e, pp=pp: e.activation(out=a2[:, pp, 0:T], in_=a2[:, pp, 0:T], func=AF.Ln, scale=-1.0, bias=1.0),
                          r=["a2%d" % pp], w=["a2%d" % pp])
                    S.act(lambda e, pp=pp: e.activation(out=a2[:, pp, 0:T], in_=a2[:, pp, 0:T], func=AF.Exp, scale=0.5),
                          r=["a2%d" % pp], w=["a2%d" % pp])
                    yield
                    S.dve(lambda e, c=c, pp=pp: e.tensor_tensor(out=tmpb[:, pp, 0:T], in0=gi[:, 2 * pp + 1, 0:T], in1=u[:, c, 0:T], op=ALU.mult),
                          r=["ig%d" % pp, "u%d" % c], w=["tb%d" % pp])
                    S.dve(lambda e, pp=pp: e.tensor_tensor(out=tmpb[:, pp, 0:T], in0=tmpb[:, pp, 0:T], in1=a2[:, pp, 0:T], op=ALU.mult),
                          r=["tb%d" % pp, "a2%d" % pp], w=["tb%d" % pp])
                    if samp:
                        a3 = av[:, pp, 0:T].rearrange("p (s l) -> p s l", s=NS)
                        b3v = tmpb[:, pp, 0:T].rearrange("p (s l) -> p s l", s=NS)
                        S.dve(lambda e, c=c, a3=a3: e.tensor_tensor(out=rbc[:, 0:NS], in0=a3[:, :, 0], in1=h0s[:, c, :], op=ALU.mult),
                              r=["av%d" % pp, "h0s"], w=["rbc"])
                        S.dve(lambda e, b3v=b3v: e.tensor_tensor(out=b3v[:, :, 0], in0=b3v[:, :, 0], in1=rbc[:, 0:NS], op=ALU.add),
                              r=["tb%d" % pp, "rbc"], w=["tb%d" % pp])
                        S.dve(lambda e, a3=a3: e.memset(a3[:, :, 0], 0.0), r=["rbc"], w=["av%d" % pp])
                        S.dve(lambda e, c=c, pp=pp: e.tensor_tensor_scan(out=hs[:, c, 0:T], data0=av[:, pp, 0:T], data1=tmpb[:, pp, 0:T],
                                                                         initial=0.0, op0=ALU.mult, op1=ALU.add),
                              r=["av%d" % pp, "tb%d" % pp], w=["hs%d" % c])
                        S.dve(lambda e, c=c: e.tensor_copy(out=hfin[:, c, :], in_=hs[:, c, 0:T].rearrange("p (s l) -> p s l", s=NS)[:, :, 3]),
                              r=["hs%d" % c], w=["hfin"])
                    else:
                        S.dve(lambda e, c=c, pp=pp: e.tensor_tensor_scan(out=hs[:, c, 0:T], data0=av[:, pp, 0:T], data1=tmpb[:, pp, 0:T],
                                                                         initial=hstate[:, c:c + 1], op0=ALU.mult, op1=ALU.add),
                              r=["av%d" % pp, "tb%d" % pp, "hstate"], w=["hs%d" % c])
                        S.dve(lambda e, c=c: e.tensor_copy(out=hstate[:, c:c + 1], in_=hs[:, c, T - 1:T]),
                              r=["hs%d" % c], w=["hstate"])
                    yield

            def g_gate():
                for c in range(8):
                    pp = c % 2
                    pa, pk = proj(8 + c)
                    S.act(lambda e, pp=pp, pa=pa: e.activation(out=gl[:, pp, 0:T], in_=pa[:, 0:T], func=AF.Gelu_apprx_tanh),
                          r=[pk], w=["gl%d" % pp])
                    S.dve(lambda e, c=c, pp=pp: e.tensor_tensor(out=hs[:, c, 0:T], in0=hs[:, c, 0:T], in1=gl[:, pp, 0:T], op=ALU.mult),
                          r=["hs%d" % c, "gl%d" % pp], w=["yl%d" % c, "hs%d" % c])
                    S.act(lambda e, c=c, pp=pp: e.activation(out=ysq[:, pp, 0:T], in_=hs[:, c, 0:T], func=AF.Square),
                          r=["yl%d" % c], w=["ysq%d" % pp])
                    S.pe(lambda e, c=c, pp=pp: e.matmul(pD[:, 128:128 + T], lhsT=onesb, rhs=ysq[:, pp, 0:T], start=(c == 0), stop=(c == 7)),
                         r=["ysq%d" % pp, "onesb"], w=["pDn"])
                    yield

            def norm_apply(T, eps_, src, skey, gname, dst, dkey, c0, c1, rbc, rk, pst, pk):
                S.act(lambda e: e.activation(out=rbc[:, 0:T], in_=pst, func=AF.Ln,
                                             scale=1.0 / ((c1 - c0) * 128), bias=eps_t[:, 0:1]),
                      r=[pk, "eps_t"], w=[rk])
                S.act(lambda e: e.activation(out=rbc[:, 0:T], in_=rbc[:, 0:T], func=AF.Exp, scale=-0.5), r=[rk], w=[rk])
                for c in range(c0, c1):
                    S.dve(lambda e, c=c: e.scalar_tensor_tensor(out=dst[:, c, 0:T], in0=src[:, c, 0:T], scalar=P(gname, c),
                                                                in1=rbc[:, 0:T], op0=ALU.mult, op1=ALU.mult),
                          r=[skey % c, rk, "pfm"], w=[dkey % c])

            Um = mskb[0:TS, 0:TS] if samp else Utrib
            ngm = negblk if samp else negm
            allm = mskb[0:TS, TS:2 * TS] if samp else onesb
            d4 = lambda ap: ap[:, 0:4 * T].rearrange("p (a t) -> p a t", a=4)
            Em, Dm, Mm, pC4 = d4(Emf), d4(Dmf), d4(Mmf), d4(pC)

            def g_ssd():
                for c in range(8):
                    S.pe(lambda e, c=c: e.transpose(out=pT[0:T, c * 128:(c + 1) * 128], in_=xsf[:, c, 0:T], identity=ident),
                         r=["xsf%d" % c, "cst"], w=["pT"])
                for g in range(2):
                    S.pe(lambda e, g=g: e.transpose(out=pCb[0:T, 128 + g * 128:128 + (g + 1) * 128], in_=Bb[:, g, 0:T], identity=identb),
                         r=["Bb%d" % g, "identb"], w=["pC", "pCx"])
                S.dve(lambda e: e.tensor_tensor(out=xdt[0:T, :].rearrange("p (h q) -> p h q", h=16),
                                                in0=pT[0:T, :].rearrange("p (h q) -> p h q", h=16),
                                                in1=dtt[0:T, :].unsqueeze(2).to_broadcast([T, 16, 64]), op=ALU.mult),
                      r=["pT", "dtt"], w=["xdt"])
                S.dve(lambda e: e.tensor_copy(out=BT[0:T, :], in_=pCb[0:T, 128:384]), r=["pC"], w=["BT"])
                S.dve(lambda e: e.tensor_tensor(out=da[0:T, :], in0=dtt[0:T, :], in1=a_bc[0:T, :], op=ALU.mult),
                      r=["dtt", "a_bc"], w=["da"])
                yield
                S.dve(lambda e: e.tensor_copy(out=dah[0:T, :], in_=da[0:T, :]), r=["da"], w=["dah"])
                S.dve(lambda e: e.tensor_tensor(out=dal[0:T, :], in0=da[0:T, :], in1=dah[0:T, :], op=ALU.subtract),
                      r=["da", "dah"], w=["dal"])
                for i, dx in enumerate((dah, dal)):
                    S.pe(lambda e, dx=dx, i=i: e.matmul(pC[0:T, 0:16], lhsT=Um[0:T, 0:T], rhs=dx[0:T, :], start=(i == 0), stop=(i == 1)),
                         r=["dah", "dal", "mskb"], w=["pC", "pCx"])
                for i, dx in enumerate((dah, dal)):
                    S.pe(lambda e, dx=dx, i=i: e.matmul(pC[0:T, 16:32], lhsT=allm[0:T, 0:T], rhs=dx[0:T, :], start=(i == 0), stop=(i == 1)),
                         r=["dah", "dal", "mskb", "onesb"], w=["pC", "pCx"])
                if not samp:
                    for i, dx in enumerate((dah, dal)):
                        S.pe(lambda e, dx=dx, i=i: e.matmul(pC[:, 32:48], lhsT=onesb, rhs=dx, start=(i == 0), stop=(i == 1)),
                             r=["dah", "dal", "onesb"], w=["pC", "pCx"])
                for g in range(2):
                    S.pe(lambda e, g=g: e.matmul(pC[0:T, 256 + g * 128:256 + g * 128 + T], lhsT=Bb[:, g, 0:T], rhs=Cb[:, g, 0:T],
                                                 start=True, stop=True), r=["Bb%d" % g, "Cb%d" % g], w=["pC", "pCx"])
                yield
                S.dve(lambda e: e.tensor_scalar(out=ncum[0:T, :], in0=pC[0:T, 0:16], scalar1=-1.0, scalar2=None, op0=ALU.mult),
                      r=["pC"], w=["ncum"])
                S.dve(lambda e: e.tensor_tensor(out=dte[0:T, :], in0=pC[0:T, 16:32], in1=ncum[0:T, :], op=ALU.add),
                      r=["pC", "ncum"], w=["dte"])
                if not samp:
                    S.dve(lambda e: e.tensor_copy(out=cdec, in_=pC[:, 32:48]), r=["pC"], w=["cdec"])
                S.dve(lambda e: e.tensor_copy(out=cbT[0:T, :, 0:T], in_=pC[0:T, 256:512].rearrange("p (g t) -> p g t", g=2)[:, :, 0:T]),
                      r=["pC"], w=["cbT0", "cbT1"])
                S.act(lambda e: e.activation(out=dte[0:T, :], in_=dte[0:T, :], func=AF.Exp), r=["dte"], w=["dte"])
                if not samp:
                    S.act(lambda e: e.activation(out=cdec, in_=cdec, func=AF.Exp), r=["cdec"], w=["cdec"])
                S.dve(lambda e: e.tensor_tensor(out=xdd[0:T, :].rearrange("p (h q) -> p h q", h=16),
                                                in0=xdt[0:T, :].rearrange("p (h q) -> p h q", h=16),
                                                in1=dte[0:T, :].unsqueeze(2).to_broadcast([T, 16, 64]), op=ALU.mult),
                      r=["xdt", "dte"], w=["xdd"])
                yield
                if not samp:
                    for g in range(2):
                        S.pe(lambda e, g=g: e.matmul(pO[:, g * 512:(g + 1) * 512], lhsT=BT[:, g * 128:(g + 1) * 128],
                                                     rhs=xdd[:, g * 512:(g + 1) * 512], start=True, stop=True),
                             r=["BT", "xdd"], w=["pO"])
                    S.dve(lambda e: e.tensor_tensor(out=hT.rearrange("p (h q) -> p h q", h=16),
                                                    in0=hT.rearrange("p (h q) -> p h q", h=16),
                                                    in1=cdec.unsqueeze(2).to_broadcast([128, 16, 64]), op=ALU.mult),
                          r=["hT", "cdec"], w=["hT"])
                    S.dve(lambda e: e.tensor_tensor(out=hT, in0=hT, in1=pO, op=ALU.add), r=["hT", "pO"], w=["hT"])
                    yield
                for q4 in range(4):
                    g = q4 // 2
                    for i, (dx, Wf, wk) in enumerate(((dah, Mmf, ["Mm"]), (dal, Wl, wlk))):
                        S.pool(lambda e, q4=q4, dx=dx, Wf=Wf: e.tensor_tensor(out=d4(Wf)[0:T], in0=Um[0:T, 0:T].unsqueeze(1).to_broadcast([T, 4, T]),
                                                                             in1=dx[0:T, q4 * 4:q4 * 4 + 4].unsqueeze(2).to_broadcast([T, 4, T]),
                                                                             op=ALU.mult), r=["dah", "dal", "mskb"], w=wk)
                        S.pe(lambda e, Wf=Wf, i=i: e.matmul(pC[:, 0:4 * T], lhsT=onesb[0:T, :], rhs=Wf[0:T, 0:4 * T],
                                                            start=(i == 0), stop=(i == 1)), r=wk + ["onesb"], w=["pC", "pCx"])
                    S.dve(lambda e: e.tensor_copy(out=Em, in_=pC4), r=["pC"], w=["Em"])
                    yield
                    for hh in range(4):
                        h = q4 * 4 + hh
                        S.dve(lambda e, h=h, hh=hh: e.scalar_tensor_tensor(out=Dm[0:T, hh, :], in0=Em[0:T, hh, :],
                                                                           scalar=ncum[0:T, h:h + 1], in1=ngm[0:T, 0:T],
                                                                           op0=ALU.add, op1=ALU.add),
                              r=["Em", "ncum", "cst"], w=["Dm"])
                    S.act(lambda e: e.activation(out=Em, in_=Em, func=AF.Exp), r=["Em"], w=["Em"])
                    S.act(lambda e: e.activation(out=Dm[0:T], in_=Dm[0:T], func=AF.Exp), r=["Dm"], w=["Dm"])
                    S.pool(lambda e, g=g: e.tensor_tensor(out=Mm[0:T], in0=Dm[0:T],
                                                         in1=cbT[0:T, g, 0:T].unsqueeze(1).to_broadcast([T, 4, T]), op=ALU.mult),
                          r=["Dm", "cbT%d" % g], w=["Mm"])
                    S.pool(lambda e, g=g, q4=q4: e.tensor_tensor(out=(Chs[:, q4 * 4:q4 * 4 + 4, :] if samp else Chp), in0=Em,
                                                                in1=Cb[:, g, 0:T].unsqueeze(1).to_broadcast([128, 4, T]), op=ALU.mult),
                          r=["Em", "Cb%d" % g], w=["Ch"])
                    yield
                    for hh in range(4):
                        h = q4 * 4 + hh
                        c = h // 2
                        h2 = h % 2
                        po = pT[64 * h2:64 * h2 + 64, c * 128:c * 128 + T]
                        S.pe(lambda e, h=h, hh=hh, po=po: e.matmul(po, lhsT=xdt[0:T, h * 64:(h + 1) * 64], rhs=Mm[0:T, hh, :],
                                                                   start=True, stop=samp), r=["xdt", "Mm"], w=["pT"])
                        if not samp:
                            S.pe(lambda e, h=h, hh=hh, po=po: e.matmul(po, lhsT=hTb[:, h * 64:(h + 1) * 64], rhs=Chp[:, hh, :],
                                                                       start=False, stop=True), r=["hTb", "Ch"], w=["pT"])
                    yield

            def late_outputs():
                M = T if samp else 3
                t0 = 0 if samp else 125
                if samp or last:
                    for blk, col0 in enumerate((0, 512, 3072, 3584, 4096)):
                        for k in range(8):
                            S.pe(lambda e, k=k, col0=col0: e.matmul(pO[0:M, 0:512], lhsT=hTt[:, k, t0:t0 + M],
                                                                    rhs=w_in_sb[:, k, col0:col0 + 512], start=(k == 0), stop=(k == 7)),
                                 r=["hTt", "w_in"], w=["pO"])
                        S.dve(lambda e, blk=blk: e.tensor_copy(out=stg[0:M, blk * 512:(blk + 1) * 512], in_=pO[0:M, 0:512]),
                              r=["pO"], w=["stg"] + STGW)
                if last:
                    S.dma(lambda e: e.dma_start(out=o_plc, in_=stg[0:3, 0:1024]), "o_plc", r=["stg"])
                    S.dma(lambda e: e.dma_start(out=o_psc, in_=stg[0:3, 1024:2560]), "o_psc", r=["stg"])
                if samp:
                    for s in range(NS):
                        S.dma(lambda e, s=s: e.dma_start(out=o_slc[s], in_=stg[4 * s + 1:4 * s + 4, 0:1024]), "o_slc", r=["stg"])
                        S.dma(lambda e, s=s: e.dma_start(out=o_ssc[s], in_=stg[4 * s + 1:4 * s + 4, 1024:2560]), "o_ssc", r=["stg"])
                if last:
                    S.pe(lambda e: e.transpose(out=pC[0:8, 0:128], in_=hstate, identity=ident), r=["hstate", "cst"], w=["pC", "pCx"])
                    S.act(lambda e: e.activation(out=stT[0:8, 0:128], in_=pC[0:8, 0:128], func=AF.Copy), r=["pC"], w=["stg"] + STGW)
                    S.dma(lambda e: e.dma_start(out=o_plh, in_=stT[0:8, 0:128]), "o_plh", r=["stg"])
                if samp:
                    for c in range(8):
                        S.pe(lambda e, c=c: e.transpose(out=pT[0:NS, c * 128:(c + 1) * 128], in_=hfin[:, c, :], identity=ident),
                             r=["hfin", "cst"], w=["pT"])
                    S.act(lambda e: e.activation(out=lh_in[0:NS, :], in_=pT[0:NS, :], func=AF.Copy), r=["pT"], w=["stg"] + STGW)
                    S.dma(lambda e: e.dma_start(out=o_slh, in_=lh_in[0:NS, :]), "o_slh", r=["stg"])


            def genP():
                S.dma(lambda e: e.dma_start(out=xt[0:T, :], in_=xsrc), "xt", w=["xt"])
                rms_rstd(xt, T, "xt", junk, ss, rstd)
                S.act(lambda e: e.activation(out=xn[0:T, :], in_=xt[0:T, :], func=AF.Copy, scale=rstd[0:T, 0:1]),
                      r=["xt", "rstd"], w=["xn"])
                to_fm(T, "GM", hTt, "hTt")

                if samp:
                    S.dma(lambda e: e.dma_start(out=lc_in[0:48, :], in_=st_lc), "stg", w=["stg"])
                    S.dma(lambda e: e.dma_start(out=sc_in[0:48, :], in_=st_sc), "stg", w=["stg"])
                    S.dma(lambda e: e.dma_start(out=lh_in[64:64 + NS, :], in_=st_lh), "stg", w=["stg"])
                    for c in range(8):
                        S.pe(lambda e, c=c: e.transpose(out=pC[:, 0:48], in_=lc_in[0:48, c * 128:(c + 1) * 128],
                                                        identity=ident[0:48, 0:48]), r=["stg", "cst"], w=["pC"])
                        S.act(lambda e, c=c: e.activation(out=lxs[:, c, :, 0:3],
                                                          in_=pC[:, 0:48].rearrange("p (s j) -> p s j", s=NS),
                                                          func=AF.Copy), r=["pC"], w=["lx%d" % c])
                        S.pe(lambda e, c=c: e.transpose(out=pD[:, 0:NS], in_=lh_in[64:64 + NS, c * 128:(c + 1) * 128],
                                                        identity=ident[64:64 + NS, 64:64 + NS]), r=["stg", "cst"], w=["pD"])
                        S.dve(lambda e, c=c: e.tensor_copy(out=h0s[:, c, :], in_=pD[:, 0:NS]), r=["pD"], w=["h0s"])
                    for c in range(12):
                        S.pe(lambda e, c=c: e.transpose(out=pC[:, 0:48], in_=sc_in[0:48, c * 128:(c + 1) * 128],
                                                        identity=ident[0:48, 0:48]), r=["stg", "cst"], w=["pC"])
                        S.act(lambda e, c=c: e.activation(out=xcs[:, c, :, 0:3],
                                                          in_=pC[:, 0:48].rearrange("p (s j) -> p s j", s=NS),
                                                          func=AF.Copy), r=["pC"], w=["xc%d" % c])

                yield
                yield from inter(g_lrux(), g_z())
                yield from inter(g_xbc())
                yield from inter(g_lru((0, 2, 4, 6), pA[0], "pA0", "pA0"), g_lru((1, 3, 5, 7), pA[1], "pA1", "pA1"))
                yield from inter(g_gate())
                norm_apply(T, EPS, hs, "yl%d", "GL", ynl, "ynl%d", 0, 8, rbc, "rbc", pD[:, 128:128 + T], "pDn")
                yield
                if samp:
                    late_outputs()

            def genS():
                yield from inter(g_ssd())
                if samp:
                    ssd_sample_states_prep()

                for c in range(8):
                    S.dve(lambda e, c=c: e.scalar_tensor_tensor(out=xsf[:, c, 0:T], in0=xsf[:, c, 0:T], scalar=P("DS", c),
                                                                in1=pT[:, c * 128:c * 128 + T], op0=ALU.mult, op1=ALU.add),
                          r=["pT", "xsf%d" % c, "pfm"], w=["xsf%d" % c])
                if samp:
                    S.dve(lambda e: e.tensor_tensor(out=xsf[:, :, 0:T], in0=xsf[:, :, 0:T], in1=pyo_sb, op=ALU.add),
                          r=["xsf%d" % c for c in range(8)] + ["pyo_sb"], w=["xsf%d" % c for c in range(8)])
                S.pool(lambda e: e.tensor_tensor(out=xsf[:, :, 0:T], in0=xsf[:, :, 0:T], in1=zs[:, :, 0:T], op=ALU.mult),
                       r=["xsf%d" % c for c in range(8)] + ["zs%d" % c for c in range(8)], w=["yg%d" % c for c in range(8)] + ["xsf%d" % c for c in range(8)])
                yield
                if not samp:
                    S.act(lambda e: e.activation(out=hTb, in_=hT, func=AF.Copy), r=["hT"], w=["hTb"])
                    if last:
                        for c in range(8):
                            S.pe(lambda e, c=c: e.transpose(out=pO[:, c * 128:(c + 1) * 128], in_=hT[:, c * 128:(c + 1) * 128], identity=ident),
                                 r=["hT", "cst"], w=["pO"])
                        S.dve(lambda e: e.tensor_copy(out=stT, in_=pO), r=["pO"], w=["stg"] + STGW)
                        S.dma(lambda e: e.dma_start(out=o_psh.rearrange("(c q) n -> q c n", q=128),
                                                    in_=stT.rearrange("p (c n) -> p c n", c=8)), "o_psh", r=["stg"])
                yield
                for g in range(2):
                    for c in range(4 * g, 4 * g + 4):
                        pp = c % 2
                        S.act(lambda e, c=c, pp=pp: e.activation(out=ysqS[:, pp, 0:T], in_=xsf[:, c, 0:T], func=AF.Square),
                              r=["yg%d" % c], w=["ysq%s%d" % (sk, pp)])
                        S.pe(lambda e, c=c, pp=pp, g=g: e.matmul(pO[:, 0:T], lhsT=onesb, rhs=ysqS[:, pp, 0:T],
                                                                 start=(c == 4 * g), stop=(c == 4 * g + 3)),
                             r=["ysq%s%d" % (sk, pp), "onesb"], w=["pO"])
                    norm_apply(T, EPS, xsf, "yg%d", "GS", yns, "yns%d", 4 * g, 4 * g + 4, rbcS, "rbc" + sk, pO[:, 0:T], "pO")

                yield
                for nb in range(2):
                    for kc in range(16):
                        src = ynl if kc < 8 else yns
                        S.pe(lambda e, kc=kc, nb=nb, src=src: e.matmul(pO[0:T, nb * 512:(nb + 1) * 512], lhsT=src[:, kc % 8, 0:T],
                                                                       rhs=w_out_sb[:, kc, nb * 512:(nb + 1) * 512],
                                                                       start=(kc == 0), stop=(kc == 15)),
                             r=[("ynl%d" if kc < 8 else "yns%d") % (kc % 8), "w_out"], w=["pO"])
                S.dve(lambda e: e.tensor_tensor(out=xt[0:T, :], in0=pO[0:T, :], in1=xt[0:T, :], op=ALU.add),
                      r=["pO", "xt"], w=["xt"])
                S.dma(lambda e: e.dma_start(out=scr[row0:row0 + T, :], in_=xt[0:T, :]), "xnew", r=["xt"], w=["scr%d" % mt])

                if last:
                    late_outputs()

            return par, genP, genS

        def ssd_sample_states_prep():
            T = TS
            for i, dx in enumerate((dah, dal)):
                S.dve(lambda e, dx=dx, i=i: e.tensor_tensor(out=damb[0:T, i], in0=dx[0:T, :].unsqueeze(1).to_broadcast([T, NS, 16]),
                                                            in1=blki.unsqueeze(2).to_broadcast([T, NS, 16]), op=ALU.mult),
                      r=["dah", "dal", "cst"], w=["dam%d" % i])
                S.pe(lambda e, i=i: e.matmul(pD[:, 0:256], lhsT=onesb[0:T, :], rhs=damb[0:T, i].rearrange("p s h -> p (s h)"),
                                             start=(i == 0), stop=(i == 1)), r=["dam%d" % i, "onesb"], w=["pD", "pD2", "pD3", "pDn"])
            S.act(lambda e: e.activation(out=dtot.rearrange("p s h -> p (s h)"), in_=pD[:, 0:256], func=AF.Exp),
                  r=["pD"], w=["dtot"])
            for s in range(NS):
                S.dma(lambda e, s=s: e.dma_start(out=h0in, in_=st_sh[s].rearrange("(c q) n -> q c n", q=128)), "h0in", w=["h0in"])
                for c in range(8):
                    S.pe(lambda e, c=c: e.transpose(out=pO[:, c * 128:(c + 1) * 128], in_=h0in[:, c, :], identity=ident),
                         r=["h0in", "cst"], w=["pO"])
                S.act(lambda e: e.activation(out=h0Tb, in_=pO, func=AF.Copy), r=["pO"], w=["h0Tb"])
                for h in range(16):
                    S.pe(lambda e, h=h, s=s: e.matmul(pA[0][64 * (h % 2):64 * (h % 2) + 64, (h // 2) * TS + 4 * s:(h // 2) * TS + 4 * s + 4],
                                                      lhsT=h0Tb[:, h * 64:(h + 1) * 64], rhs=Chs[:, h, 4 * s:4 * s + 4],
                                                      start=True, stop=True),
                         r=["h0Tb", "Ch"], w=["pA0"])
                S.pool(lambda e, s=s: e.tensor_scalar(out=Bm[0:T, :], in0=BT[0:T, :], scalar1=blki[:, s:s + 1], scalar2=None, op0=ALU.mult),
                       r=["BT", "cst"], w=["Bm"])
                S.dve(lambda e, s=s: e.tensor_tensor(out=hnew.rearrange("p (h q) -> p h q", h=16),
                                                     in0=pO.rearrange("p (h q) -> p h q", h=16),
                                                     in1=dtot[:, s, :].unsqueeze(2).to_broadcast([128, 16, 64]), op=ALU.mult),
                      r=["pO", "dtot"], w=["hnew"])
                for g in range(2):
                    S.pe(lambda e, g=g, s=s: e.matmul(pO[:, g * 512:(g + 1) * 512], lhsT=Bm[0:T, g * 128:(g + 1) * 128],
                                                      rhs=xdd[0:T, g * 512:(g + 1) * 512], start=True, stop=True),
                         r=["Bm", "xdd"], w=["pO"])
                S.dve(lambda e: e.tensor_tensor(out=hnew, in0=hnew, in1=pO, op=ALU.add), r=["hnew", "pO"], w=["hnew"])
                for c in range(8):
                    S.pe(lambda e, c=c: e.transpose(out=pO[:, c * 128:(c + 1) * 128], in_=hnew[:, c * 128:(c + 1) * 128], identity=ident),
                         r=["hnew", "cst"], w=["pO"])
                S.act(lambda e: e.activation(out=hout.rearrange("p c n -> p (c n)"), in_=pO, func=AF.Copy), r=["pO"], w=["hout"])
                S.dma(lambda e, s=s: e.dma_start(out=o_ssh[s].rearrange("(c q) n -> q c n", q=128), in_=hout), "hout", r=["hout"])
            S.act(lambda e: e.activation(out=pyo_sb.rearrange("p c t -> p (c t)"), in_=pA[0][:, 0:8 * TS], func=AF.Copy),
                  r=["pA0"], w=["pyo_sb"])

        S.pool(lambda e: e.memset(lxb, 0.0), w=["lx%d" % c for c in range(8)])
        S.pool(lambda e: e.memset(xcb, 0.0), w=["xc%d" % c for c in range(12)])

        def drive(g_, par):
            S.ctx = par
            try:
                next(g_)
                return True
            except StopIteration:
                return False
            finally:
                S.ctx = None

        tiles = [mixer_tile(mt, False) for mt in range(NT)]
        RATIO = 3
        par0, gP0, _ = tiles[0]
        g = gP0()
        while drive(g, par0):
            pass
        for n in range(NT):
            par, _, gS = tiles[n]
            gs = gS()
            alive_s = True
            alive_p = False
            if n + 1 < NT:
                parn, gPn, _ = tiles[n + 1]
                gp = gPn()
                alive_p = True
            while alive_s or alive_p:
                for _ in range(RATIO):
                    if alive_p:
                        alive_p = drive(gp, parn)
                if alive_s:
                    alive_s = drive(gs, par)
        S.barrier()
        if SAMP:
            pars, gPs, gSs = mixer_tile(SEQ // 128, True)
            for g in (gPs(), gSs()):
                while drive(g, pars):
                    pass

        S.barrier()
        ptr[0] = base0
        w_up_sb = b3(8, DFF)
        w_dn_sb = b3(32, D)
        if MLP:
            k_wup = load_w(w_up_sb, w_up, 8, DFF, "w_up")
            k_wdn = load_w(w_dn_sb, w_down, 32, D, "w_dn")
        T2 = 256
        xt2 = [[f32(D), f32(D)], [f32(D), f32(D)]]
        xn2 = f32(D)
        ss2 = [f32(4), f32(4)]
        rstd2 = [f32(4), f32(4)]
        mT = [b3(8, T2), b3(8, T2)]
        actb = b3(32, T2)
        rl = [f32(T2), f32(T2)]
        yout = f32(D)
        gfin_bc = f32(D)
        S.dma(lambda e: e.dma_start(out=gfin_bc, in_=gfin_d.partition_broadcast(128)), "gfin", w=["gfin"])
        pDN = [PS[:, 3072:4096], PS[:, 2048:3072]]
        pDNk = [["pO"], ["pC", "pD"]]

        def mlp_front(ti, r0, T):
            q = ti % 2
            nsub = (T + 127) // 128
            for j in range(nsub):
                Tj = min(128, T - j * 128)
                xk = "xt2_%d_%d" % (q, j)
                S.dma(lambda e, j=j, Tj=Tj: e.dma_start(out=xt2[q][j][0:Tj, :], in_=scr[r0 + j * 128:r0 + j * 128 + Tj, :]),
                      xk, r=["scr%d" % ((r0 + j * 128) // 128)], w=[xk])
                rms_rstd(xt2[q][j], Tj, xk, xn2, ss2[0], rstd2[0], "2")
                S.act(lambda e, j=j, Tj=Tj: e.activation(out=xn2[0:Tj, :], in_=xt2[q][j][0:Tj, :], func=AF.Copy, scale=rstd2[0][0:Tj, 0:1]),
                      r=[xk, "rstd2"], w=["xn2"])
                for k in range(8):
                    S.pe(lambda e, k=k, Tj=Tj: e.transpose(out=pT[:, k * 128:k * 128 + Tj], in_=xn2[0:Tj, k * 128:(k + 1) * 128],
                                                           identity=ident[0:Tj, 0:Tj]), r=["xn2", "cst"], w=["pT"])
                S.dve(lambda e, j=j, Tj=Tj: e.tensor_tensor(
                    out=mT[q][:, :, j * 128:j * 128 + Tj], in0=pT.rearrange("p (k t) -> p k t", k=8)[:, :, 0:Tj],
                    in1=P("GP", 0, 8).unsqueeze(2).to_broadcast([128, 8, Tj]), op=ALU.mult),
                    r=["pT", "pfm"], w=["mT%d" % q])
            yield
            for f in range(32):
                pa = pA[f % 2]
                for k in range(8):
                    S.pe(lambda e, k=k, f=f, pa=pa: e.matmul(pa[:, 0:T], lhsT=w_up_sb[:, k, f * 128:(f + 1) * 128], rhs=mT[q][:, k, 0:T],
                                                             start=(k == 0), stop=(k == 7)),
                         r=["mT%d" % q, "w_up"], w=["pA%d" % (f % 2)])
                S.act(lambda e, f=f, pa=pa: e.activation(out=rl[f % 2][:, 0:T], in_=pa[:, 0:T], func=AF.Relu),
                      r=["pA%d" % (f % 2)], w=["rl%d" % (f % 2)])
                S.pool(lambda e, f=f: e.tensor_tensor(out=actb[:, f, 0:T], in0=rl[f % 2][:, 0:T], in1=rl[f % 2][:, 0:T], op=ALU.mult),
                       r=["rl%d" % (f % 2)], w=["act%d" % f])
                yield

        def mlp_back(ti, r0, T):
            q = ti % 2
            nsub = (T + 127) // 128
            for f in range(32):
                for j in range(nsub):
                    Tj = min(128, T - j * 128)
                    for nb in range(2):
                        S.pe(lambda e, f=f, nb=nb, j=j, Tj=Tj: e.matmul(pDN[j][0:Tj, nb * 512:(nb + 1) * 512],
                                                                        lhsT=actb[:, f, j * 128:j * 128 + Tj],
                                                                        rhs=w_dn_sb[:, f, nb * 512:(nb + 1) * 512],
                                                                        start=(f == 0), stop=(f == 31)),
                             r=["act%d" % f, "w_dn"], w=pDNk[j])
                yield
            for j in range(nsub):
                Tj = min(128, T - j * 128)
                xk = "xt2_%d_%d" % (q, j)
                S.dve(lambda e, j=j, Tj=Tj: e.tensor_tensor(out=xt2[q][j][0:Tj, :], in0=pDN[j][0:Tj, :], in1=xt2[q][j][0:Tj, :], op=ALU.add),
                      r=pDNk[j] + [xk], w=[xk])
                rms_rstd(xt2[q][j], Tj, xk, yout, ss2[1], rstd2[1], "2b", jkey="yout")
                S.dve(lambda e, j=j, Tj=Tj: e.scalar_tensor_tensor(out=yout[0:Tj, :], in0=xt2[q][j][0:Tj, :], scalar=rstd2[1][0:Tj, 0:1],
                                                                   in1=gfin_bc[0:Tj, :], op0=ALU.mult, op1=ALU.mult),
                      r=[xk, "rstd2b", "gfin"], w=["yout"])
                rr = r0 + j * 128
                if rr < SEQ:
                    S.dma(lambda e, rr=rr, Tj=Tj: e.dma_start(out=y_p[rr:rr + Tj, :], in_=yout[0:Tj, :]), "yout", r=["yout"])
                else:
                    S.dma(lambda e, Tj=Tj: e.dma_start(out=y_s, in_=yout[0:Tj, :]), "yout", r=["yout"])
                yield

        if MLP:
            jobs = [(t * T2, T2) for t in range(NT * 128 // T2)]
            if SAMP:
                jobs.append((SEQ, TS))
            for _ in mlp_front(0, *jobs[0]):
                pass
            for ti in range(len(jobs)):
                gb = mlp_back(ti, *jobs[ti])
                gf = mlp_front(ti + 1, *jobs[ti + 1]) if ti + 1 < len(jobs) else iter(())
                ab = af = True
                while ab or af:
                    if ab:
                        ab = next(gb, "END") != "END"
                    if af:
                        af = next(gf, "END") != "END"

        S.emit()
    return nc


_CACHE = {}


def _consts():
    c = np.zeros((128, NCST), np.float32)
    i = np.arange(128)
    c[:, CI:CI + 128] = np.eye(128, dtype=np.float32)
    c[:, CU:CU + 128] = (i[:, None] <= i[None, :]).astype(np.float32)
    c[:, CN:CN + 128] = np.where(i[:, None] <= i[None, :], 0.0, NEG).astype(np.float32)
    c[:, CO:CO + 128] = 1.0
    j = np.arange(TS)
    same = (j[:, None] // 4) == (j[None, :] // 4)
    caus = j[:, None] <= j[None, :]
    c[0:TS, CUB:CUB + TS] = (same & caus).astype(np.float32)
    c[0:TS, CNB:CNB + TS] = np.where(same & caus, 0.0, NEG).astype(np.float32)
    c[0:TS, CBM:CBM + TS] = same.astype(np.float32)
    c[0:TS, CBI:CBI + NS] = ((j[:, None] // 4) == np.arange(NS)[None, :]).astype(np.float32)
    return c


def _fm(v, nch):
    return np.ascontiguousarray(np.asarray(v, np.float32).reshape(nch, 128).T)


def kernel(x_prompt, x_sample, state_lru_conv, state_lru_h, state_ssd_conv, state_ssd_h,
           g_mix, w_in, lru_conv_w, lru_conv_b, w_a, b_a, w_x, b_x, lam, g_lru_out,
           ssd_conv_w, ssd_conv_b, dt_bias, a_log, d_skip, g_ssd_out, w_out,
           g_mlp, w_up, w_down, g_final):
    f = lambda a: np.ascontiguousarray(np.asarray(a, np.float32))
    if "nc" not in _CACHE:
        _CACHE["nc"] = build_program()
    nc = _CACHE["nc"]
    pfm = np.zeros((128, NPAR), np.float32)
    lw = np.asarray(lru_conv_w[0], np.float32)
    pfm[:, PC["LW"]:PC["LW"] + 32] = lw.reshape(4, 8, 128).transpose(2, 1, 0).reshape(128, 32)
    pfm[:, PC["LB"]:PC["LB"] + 8] = _fm(lru_conv_b[0], 8)
    pfm[:, PC["BA"]:PC["BA"] + 8] = _fm(np.asarray(b_a[0]).reshape(-1), 8)
    pfm[:, PC["BX"]:PC["BX"] + 8] = _fm(np.asarray(b_x[0]).reshape(-1), 8)
    pfm[:, PC["LAM"]:PC["LAM"] + 8] = _fm(lam[0], 8)
    pfm[:, PC["GL"]:PC["GL"] + 8] = _fm(g_lru_out[0], 8)
    sw = np.asarray(ssd_conv_w[0], np.float32)
    pfm[:, PC["SW"]:PC["SW"] + 48] = sw.reshape(4, 12, 128).transpose(2, 1, 0).reshape(128, 48)
    pfm[:, PC["SB"]:PC["SB"] + 12] = _fm(ssd_conv_b[0], 12)
    pfm[:, PC["DS"]:PC["DS"] + 8] = _fm(np.repeat(np.asarray(d_skip[0], np.float32), 64), 8)
    pfm[:, PC["GS"]:PC["GS"] + 8] = _fm(g_ssd_out[0], 8)
    pfm[:, PC["GM"]:PC["GM"] + 8] = _fm(g_mix[0], 8)
    pfm[:, PC["GP"]:PC["GP"] + 8] = _fm(g_mlp[0], 8)
    cst = _consts()
    shared = {
        "w_in": f(w_in[0]), "w_out": f(w_out[0]), "w_up": f(w_up[0]), "w_down": f(w_down[0]),
        "w_a": f(w_a[0]), "w_x": f(w_x[0]), "pfm": pfm, "cst": cst,
        "dt_bias": f(dt_bias[0]), "a_log": f(a_log[0]), "g_final": f(g_final),
    }
    in_maps = []
    for b in range(NCORES):
        sl = slice(NS * b, NS * (b + 1))
        m = dict(shared)
        m["xp"] = f(x_prompt[b])
        m["xs"] = f(np.asarray(x_sample[sl]).reshape(TS, D))
        m["st_lc"] = f(np.asarray(state_lru_conv[0, sl]).reshape(NS * 3, D))
        m["st_lh"] = f(state_lru_h[0, sl])
        m["st_sc"] = f(np.asarray(state_ssd_conv[0, sl]).reshape(NS * 3, XBC))
        m["st_sh"] = f(np.asarray(state_ssd_h[0, sl]).reshape(NS, 1024, 128))
        in_maps.append(m)
    res = run_bass_kernel_spmd(nc, in_maps, core_ids=list(range(NCORES)))
    R = res.results
    cat = lambda k: np.stack([np.asarray(R[b][k], np.float32) for b in range(NCORES)])
    y_prompt = cat("y_p")
    y_sample = cat("y_s").reshape(NCORES * NS, 4, D)
    p_lc = cat("o_plc")[None]
    p_lh = cat("o_plh").reshape(NCORES, D)[None]
    p_sc = cat("o_psc")[None]
    p_sh = cat("o_psh").reshape(NCORES, 16, 64, 128)[None]
    s_lc = cat("o_slc").reshape(NCORES * NS, 3, D)[None]
    s_lh = cat("o_slh").reshape(NCORES * NS, D)[None]
    s_sc = cat("o_ssc").reshape(NCORES * NS, 3, XBC)[None]
    s_sh = cat("o_ssh").reshape(NCORES * NS, 16, 64, 128)[None]
    return (y_prompt, y_sample, p_lc, p_lh, p_sc, p_sh, s_lc, s_lh, s_sc, s_sh)
```

```python
import math
from contextlib import ExitStack

import numpy as np
import concourse.bass as bass
import concourse.mybir as mybir
from concourse.bass_utils import run_bass_kernel_spmd

F32 = mybir.dt.float32
BF16 = mybir.dt.bfloat16
AF = mybir.ActivationFunctionType
ALU = mybir.AluOpType

NCORES = 8
D = 1024
SEQ = 2048
NS = 16
TS = 64
XBC = 1536
INP = 4624
DFF = 4096
EPS = 1e-6
NEG = -30000.0

import re as _re

ENGS = ("pe", "act", "dve", "pool", "sp")
SAME_ENGINE_SYNC = {"pe": False, "act": True, "dve": True, "pool": True, "sp": False}


class Op:
    __slots__ = ("eng", "fn", "deps", "marked", "count", "dma_key", "dma_val")

    def __init__(self, eng, fn, dma_key=None):
        self.eng = eng
        self.fn = fn
        self.deps = ()
        self.marked = False
        self.count = 0
        self.dma_key = dma_key
        self.dma_val = 0


class Sched:
    def __init__(self, nc):
        self.nc = nc
        self.ops = {e: [] for e in ENGS}
        self.last_w = {}
        self.readers = {}
        self.dma_cnt = {}
        self.pending = {}
        self.ctx = None
        self.since_bar = []

    ALIAS = {"pC": "b4", "pCx": "b4", "pD": "b5", "pD2": "b5", "pD3": "b5", "pDn": "b5",
             "pDcb0": "b5", "pDcb1": "b5", "pT": "b01", "pO": "b67", "pA0": "b2", "pA1": "b3"}

    PSUM_KEYS = {"b01", "b2", "b3", "b4", "b5", "b67"}

    PAR_RE = _re.compile(r"^(xt|dtt|xnew)$|^(xsf|zs|Bb|Cb|ynl|yg)\d+$")

    def _k(self, k):
        k = self.ALIAS.get(k, k)
        if self.ctx is not None and self.PAR_RE.match(k):
            return k + "#" + str(self.ctx)
        return k

    def add(self, eng, fn, reads=(), writes=(), dma_key=None):
        reads = [self._k(k) for k in reads]
        writes = [self._k(k) for k in writes]
        if dma_key is not None and self.ctx is not None and self.PAR_RE.match(dma_key):
            dma_key = dma_key + "#" + str(self.ctx)
        op = Op(eng, fn, dma_key)
        deps = []
        seen = set()

        def dep(o):
            if o is not None and o is not op and id(o) not in seen:
                seen.add(id(o))
                deps.append(o)

        if self.pending.get(eng):
            for o in self.pending[eng]:
                dep(o)
            self.pending[eng] = []
        for b in reads:
            dep(self.last_w.get(b))
            if b in self.PSUM_KEYS:
                for r in self.readers.get(b, ()):
                    if r.eng != eng:
                        dep(r)
        for b in writes:
            dep(self.last_w.get(b))
            for r in self.readers.get(b, ()):
                dep(r)
        for b in reads:
            self.readers.setdefault(b, []).append(op)
        for b in writes:
            self.last_w[b] = op
            self.readers[b] = []
        op.deps = deps
        if dma_key is not None:
            self.dma_cnt[dma_key] = self.dma_cnt.get(dma_key, 0) + 16
            op.dma_val = self.dma_cnt[dma_key]
        self.ops[eng].append(op)
        self.since_bar.append(op)
        return op

    def barrier(self):
        ops = []
        for e in ENGS:
            comp = [o for o in self.ops[e] if o.dma_key is None]
            if comp:
                ops.append(comp[-1])
        last_dma = {}
        for o in self.since_bar:
            if o.dma_key is not None:
                last_dma[o.dma_key] = o
        ops.extend(last_dma.values())
        for e in ENGS:
            self.pending.setdefault(e, []).extend(ops)
        self.since_bar = []

    def pe(self, fn, r=(), w=()):
        return self.add("pe", fn, r, w)

    def act(self, fn, r=(), w=()):
        return self.add("act", fn, r, w)

    def dve(self, fn, r=(), w=()):
        return self.add("dve", fn, r, w)

    def pool(self, fn, r=(), w=()):
        return self.add("pool", fn, r, w)

    def dma(self, fn, key, r=(), w=(), q="sp"):
        return self.add(q, fn, r, w, dma_key=key)

    def emit(self):
        nc = self.nc
        for e in ENGS:
            for op in self.ops[e]:
                for d in op.deps:
                    if d.dma_key is None:
                        if d.eng == op.eng and not SAME_ENGINE_SYNC[d.eng]:
                            continue
                        d.marked = True
        for e in ENGS:
            c = 0
            for op in self.ops[e]:
                if op.dma_key is None and op.marked:
                    c += 1
                    op.count = c
        with ExitStack() as st:
            esem = {e: st.enter_context(nc.semaphore("es_" + e)) for e in ENGS}
            dsem = {}
            for k in self.dma_cnt:
                dsem[k] = st.enter_context(nc.semaphore("ds_%d" % len(dsem)))
            block = st.enter_context(nc.Block())

            def run(ename, eng):
                seen = {}
                for op in self.ops[ename]:
                    need = {}
                    for d in op.deps:
                        if d.dma_key is not None:
                            key = ("d", d.dma_key)
                            val = d.dma_val
                            sem = dsem[d.dma_key]
                        else:
                            if d.eng == ename and not SAME_ENGINE_SYNC[ename]:
                                continue
                            key = ("e", d.eng)
                            val = d.count
                            sem = esem[d.eng]
                        if key not in need or need[key][1] < val:
                            need[key] = (sem, val)
                    for key, (sem, val) in need.items():
                        if seen.get(key, 0) >= val:
                            continue
                        seen[key] = val
                        eng.wait_ge(sem, val)
                    ins = op.fn(eng)
                    if op.dma_key is not None:
                        ins.then_inc(dsem[op.dma_key], 16)
                    elif op.marked:
                        ins.then_inc(esem[ename], 1)
                if ename == "sp":
                    for k, v in self.dma_cnt.items():
                        eng.wait_ge(dsem[k], v)

            @block.sync
            def _(e):
                run("sp", e)

            @block.tensor
            def _(e):
                run("pe", e)

            @block.scalar
            def _(e):
                run("act", e)

            @block.vector
            def _(e):
                run("dve", e)

            @block.gpsimd
            def _(e):
                run("pool", e)


PC = {}
_o = 0
for _n, _w in (("LW", 32), ("LB", 8), ("BA", 8), ("BX", 8), ("LAM", 8), ("GL", 8), ("SW", 48),
               ("SB", 12), ("DS", 8), ("GS", 8), ("GM", 8), ("GP", 8)):
    PC[_n] = _o
    _o += _w
NPAR = _o
CI, CU, CN, CO, CUB, CNB, CBM, CBI = 0, 128, 256, 384, 512, 576, 640, 704
NCST = 720


def build_program(NT=SEQ // 128, SAMP=True, MLP=True, DBG=False, STAGE=9):
    nc = bass.Bass("TRN2", target_bir_lowering=False)
    S = Sched(nc)

    def din(name, shape):
        return nc.dram_tensor(name, list(shape), F32, kind="ExternalInput").ap()

    def dout(name, shape):
        return nc.dram_tensor(name, list(shape), F32, kind="ExternalOutput").ap()

    xp = din("xp", (SEQ, D))
    xs = din("xs", (TS, D))
    st_lc = din("st_lc", (NS * 3, D))
    st_lh = din("st_lh", (NS, D))
    st_sc = din("st_sc", (NS * 3, XBC))
    st_sh = din("st_sh", (NS, 1024, 128))
    w_in = din("w_in", (D, INP))
    w_out = din("w_out", (2 * D, D))
    w_up = din("w_up", (D, DFF))
    w_down = din("w_down", (DFF, D))
    w_a = din("w_a", (16, 64, 64))
    w_x = din("w_x", (16, 64, 64))
    pfm_d = din("pfm", (128, NPAR))
    cst_d = din("cst", (128, NCST))
    dtb_d = din("dt_bias", (16,))
    alog_d = din("a_log", (16,))
    gfin_d = din("g_final", (D,))

    y_p = dout("y_p", (SEQ, D))
    y_s = dout("y_s", (TS, D))
    o_plc = dout("o_plc", (3, D))
    o_plh = dout("o_plh", (8, 128))
    o_psc = dout("o_psc", (3, XBC))
    o_psh = dout("o_psh", (1024, 128))
    o_slc = dout("o_slc", (NS, 3, D))
    o_slh = dout("o_slh", (NS, D))
    o_ssc = dout("o_ssc", (NS, 3, XBC))
    o_ssh = dout("o_ssh", (NS, 1024, 128))
    scr = nc.dram_tensor("scr", [SEQ + TS, D], F32, kind=("ExternalOutput" if DBG else "Internal")).ap()

    st = ExitStack()
    with st:
        RW = 53200
        R = st.enter_context(nc.sbuf_tensor("R", [128, RW], F32))
        PS = st.enter_context(nc.psum_tensor("PS", [128, 4096], F32))
        ptr = [0]

        def alloc(nwords):
            a = ptr[0]
            ptr[0] += (nwords + 7) // 8 * 8
            pass
            return a

        def f32(n):
            a = alloc(n)
            return R[:, a:a + n]

        def bf(n):
            w = (n + 1) // 2
            a = alloc(w)
            return R[:, a:a + w].bitcast(BF16)[:, 0:n]

        def f3(c, t):
            return f32(c * t).rearrange("p (c t) -> p c t", c=c)

        def b3(c, t):
            return bf(c * t).rearrange("p (c t) -> p c t", c=c)

        def bank(b, n=512):
            return PS[:, 512 * b:512 * b + n]

        pT = PS[:, 0:1024]
        pTb = pT.bitcast(BF16)
        pA = [bank(2), bank(3)]
        pC = bank(4)
        pCb = pC.bitcast(BF16)
        pD = bank(5)
        pO = PS[:, 3072:4096]

        cst = f32(NCST)
        pfm = f32(NPAR)
        dtb_bc = f32(16)
        a_bc = f32(16)
        identb = bf(128)
        onesb = bf(128)
        Utrib = bf(128)
        mskb = bf(3 * TS)
        dah = bf(16)
        dal = bf(16)
        cfac = f32(8)
        c2fac = f32(8)
        tiny = f32(8)
        mhalf = f32(4)
        eps_t = f32(4)
        nbias = f32(16)
        wa_blk = b3(8, 128)
        wx_blk = b3(8, 128)
        hstate = f32(8)
        hT = f32(1024)
        hTb = bf(1024)

        ident = cst[:, CI:CI + 128]
        Utri = cst[:, CU:CU + 128]
        negm = cst[:, CN:CN + 128]
        onesf = cst[:, CO:CO + 128]
        Ublk = cst[0:TS, CUB:CUB + TS]
        negblk = cst[0:TS, CNB:CNB + TS]
        blkm = cst[0:TS, CBM:CBM + TS]
        blki = cst[0:TS, CBI:CBI + NS]

        S.dma(lambda e: e.dma_start(out=cst, in_=cst_d), "cst", w=["cst"])
        S.dma(lambda e: e.dma_start(out=pfm, in_=pfm_d), "pfm", w=["pfm"])
        S.dma(lambda e: e.dma_start(out=dtb_bc, in_=dtb_d.partition_broadcast(128)), "dtb", w=["dtb"])
        S.dma(lambda e: e.dma_start(out=a_bc, in_=alog_d.partition_broadcast(128)), "alog", w=["a_bc"])
        S.dve(lambda e: e.tensor_copy(out=identb, in_=ident), r=["cst"], w=["identb"])
        S.dve(lambda e: e.tensor_copy(out=onesb, in_=onesf), r=["cst"], w=["onesb"])
        S.dve(lambda e: e.tensor_copy(out=Utrib, in_=Utri), r=["cst"], w=["mskb"])
        S.dve(lambda e: e.tensor_copy(out=mskb[0:TS, 0:TS], in_=Ublk), r=["cst"], w=["mskb"])
        S.dve(lambda e: e.tensor_copy(out=mskb[0:TS, TS:2 * TS], in_=blkm), r=["cst"], w=["mskb"])
        S.pool(lambda e: e.memset(mhalf, -0.5), w=["mhalf"])
        S.pool(lambda e: e.memset(eps_t, EPS), w=["eps_t"])
        S.dve(lambda e: e.tensor_scalar(out=nbias[:, 0:8], in0=pfm[:, PC["BA"]:PC["BA"] + 8], scalar1=-1.0, scalar2=None, op0=ALU.mult), r=["pfm"], w=["nbias"])
        S.dve(lambda e: e.tensor_scalar(out=nbias[:, 8:16], in0=pfm[:, PC["BX"]:PC["BX"] + 8], scalar1=-1.0, scalar2=None, op0=ALU.mult), r=["pfm", "nbias"], w=["nbias"])
        S.pool(lambda e: e.memset(hstate, 0.0), w=["hstate"])
        S.pool(lambda e: e.memset(hT, 0.0), w=["hT"])
        S.pool(lambda e: e.memset(hTb, 0.0), w=["hTb"])
        S.pool(lambda e: e.memset(wa_blk, 0.0), w=["wa"])
        S.pool(lambda e: e.memset(wx_blk, 0.0), w=["wx"])
        S.act(lambda e: e.activation(out=a_bc, in_=a_bc, func=AF.Exp), r=["a_bc"], w=["a_bc"])
        S.dve(lambda e: e.tensor_scalar(out=a_bc, in0=a_bc, scalar1=-1.0, scalar2=None, op0=ALU.mult), r=["a_bc"], w=["a_bc"])
        lam = pfm[:, PC["LAM"]:PC["LAM"] + 8]
        S.act(lambda e: e.activation(out=tiny, in_=lam, func=AF.Exp, scale=-1.0), r=["pfm"], w=["tiny"])
        S.act(lambda e: e.activation(out=tiny, in_=tiny, func=AF.Ln, bias=1.0), r=["tiny"], w=["tiny"])
        S.dve(lambda e: e.tensor_scalar(out=cfac, in0=tiny, scalar1=-8.0, scalar2=None, op0=ALU.mult), r=["tiny"], w=["cfac"])
        S.dve(lambda e: e.tensor_scalar(out=c2fac, in0=tiny, scalar1=-16.0, scalar2=None, op0=ALU.mult), r=["tiny"], w=["cfac2"])
        for (wd, blk, nm) in ((w_a, wa_blk, "wa"), (w_x, wx_blk, "wx")):
            v = wd.rearrange("(c h) i j -> h i c j", h=2)
            for h2 in range(2):
                S.dma(lambda e, v=v, blk=blk, h2=h2: e.dma_start(
                    out=blk[64 * h2:64 * h2 + 64, :, 64 * h2:64 * h2 + 64], in_=v[h2]),
                    nm + str(h2), w=[nm], q="pool")

        base0 = ptr[0]

        def load_w(dst3, src2, nk, ncol, name, step=2048):
            sv = src2.rearrange("(k p) n -> p k n", p=128)
            pieces = [(k, c0, min(ncol, c0 + step)) for k in range(nk) for c0 in range(0, ncol, step)]
            for i, (k, c0, c1) in enumerate(pieces):
                S.dma(lambda e, k=k, c0=c0, c1=c1: e.dma_start(out=dst3[:, k, c0:c1], in_=sv[:, k, c0:c1]),
                      name, w=([name] if i == len(pieces) - 1 else []), q="pool")
            return name

        w_in_sb = b3(8, INP)
        w_out_sb = b3(16, D)
        if STAGE >= 1:
            k_win = load_w(w_in_sb, w_in, 8, INP, "w_in")
            k_wout = load_w(w_out_sb, w_out, 16, D, "w_out")

        xt = f32(D)
        xn = f32(D)
        junk = xn
        ss = f32(4)
        rstd = f32(4)
        hTt = b3(8, 128)
        lxb = f3(8, 131)
        xcb = f3(12, 131)
        sreg = f32(20 * NS * 7)
        lxs = sreg[:, 0:8 * NS * 7].rearrange("p (c s l) -> p c s l", c=8, s=NS)
        xcs = sreg[:, 8 * NS * 7:20 * NS * 7].rearrange("p (c s l) -> p c s l", c=12, s=NS)
        gl = f3(2, 128)
        zs = f3(8, 128)
        u = f3(8, 128)
        ub = b3(8, 128)
        lrut = f32(1280)
        gi = lrut[:, 0:512].rearrange("p (c t) -> p c t", c=4)
        av = lrut[:, 512:768].rearrange("p (c t) -> p c t", c=2)
        a2 = lrut[:, 768:1024].rearrange("p (c t) -> p c t", c=2)
        tmpb = lrut[:, 1024:1280].rearrange("p (c t) -> p c t", c=2)
        hs = f3(8, 128)
        ysq = b3(2, 128)
        rbc = f32(128)
        ynl = b3(8, 128)
        yns = b3(8, 128)
        xsf = f3(8, 128)
        Bb = b3(2, 128)
        Cb = b3(2, 128)
        dtr = f32(16)
        dtt = f32(16)
        da = f32(16)
        ncum = f32(16)
        dte = f32(16)
        cdec = f32(16)
        xdt = bf(1024)
        xdd = bf(1024)
        BT = bf(256)
        cbT = f3(2, 128)
        Dmf = f32(512)
        Emf = f32(512)
        Mmf = bf(512)
        Chf = bf(1024)
        Chp = Chf[:, 0:512].rearrange("p (a t) -> p a t", a=4)
        Chs = Chf.rearrange("p (a t) -> p a t", a=16)
        cvt = f3(2, 128)
        stg = f32(2560)
        stT = stg[:, 0:1024]
        lc_in = stg[:, 0:1024]
        sc_in = stg[:, 1024:2560]
        lh_in = stg[:, 0:1024]
        h0in = lxb.rearrange("p c t -> p (c t)")[:, 0:1024].rearrange("p (c t) -> p c t", c=8)
        h0Tb = hTb
        Bm = bf(256)
        pyo_f = f32(8 * TS)
        pyo_sb = pyo_f.rearrange("p (c t) -> p c t", c=8)
        damb = bf(2 * NS * 16).rearrange("p (i s h) -> p i s h", i=2, s=NS)
        dtot = f3(NS, 16)
        hnew = hT
        hout = xcb.rearrange("p c t -> p (c t)")[:, 0:1024].rearrange("p (c t) -> p c t", c=8)
        h0s = f3(8, NS)
        hfin = f3(8, NS)

        bf3 = lambda ap, c: ap.bitcast(BF16).rearrange("p (c t) -> p c t", c=c)
        xt_b = [xt, stg[:, 0:1024]]
        ynl_b = [ynl, bf3(stg[:, 1024:1536], 8)]
        Bb_b = [Bb, bf3(stg[:, 1536:1664], 2)]
        Cb_b = [Cb, bf3(stg[:, 1664:1792], 2)]
        dtt_b = [dtt, stg[:, 1792:1808]]
        xsf_b = [xsf, sreg[:, 0:1024].rearrange("p (c t) -> p c t", c=8)]
        zs_b = [zs, sreg[:, 1024:2048].rearrange("p (c t) -> p c t", c=8)]
        Wl_S = pyo_f[:, 0:256].bitcast(BF16)
        ysq_S = bf3(pyo_f[:, 256:384], 2)
        rbc_S = pyo_f[:, 384:512]
        STGW = [k + "#1" for k in ["xt", "dtt"] + ["ynl%d" % c for c in range(8)] + ["Bb0", "Bb1", "Cb0", "Cb1"]]
        STG_ALIAS = ["xt", "dtt"] + ["ynl%d" % c for c in range(8)] + ["Bb0", "Bb1", "Cb0", "Cb1"]

        def P(name, c=None, w=1):
            o = PC[name] + (0 if c is None else c * w)
            return pfm[:, o:o + w]

        def rms_rstd(xtile, T, keyx, junk, ss, rstd, sfx="", jkey=None):
            S.act(lambda e: e.activation(out=junk[0:T, :], in_=xtile[0:T, :], func=AF.Square, accum_out=ss[0:T, 0:1]),
                  r=[keyx], w=[jkey or ("xn" + sfx), "ss" + sfx])
            S.act(lambda e: e.activation(out=ss[0:T, 0:1], in_=ss[0:T, 0:1], func=AF.Ln, scale=1.0 / D, bias=eps_t[0:T, 0:1]),
                  r=["ss" + sfx, "eps_t"], w=["ss" + sfx])
            S.act(lambda e: e.activation(out=rstd[0:T, 0:1], in_=ss[0:T, 0:1], func=AF.Exp, scale=-0.5),
                  r=["ss" + sfx], w=["rstd" + sfx])

        pAA = PS[:, 1024:2048]

        def to_fm(T, gname, dst, dkey):
            for k in range(8):
                S.pe(lambda e, k=k: e.transpose(out=pAA[:, k * 128:k * 128 + T], in_=xn[0:T, k * 128:(k + 1) * 128],
                                                identity=ident[0:T, 0:T]), r=["xn", "cst"], w=["pA0", "pA1"])
            S.dve(lambda e: e.tensor_tensor(
                out=dst[:, :, 0:T], in0=pAA.rearrange("p (k t) -> p k t", k=8)[:, :, 0:T],
                in1=P(gname, 0, 8).unsqueeze(2).to_broadcast([128, 8, T]), op=ALU.mult),
                r=["pA0", "pA1", "pfm"], w=[dkey])

        def mixer_tile(mt, samp):
            T = TS if samp else 128
            row0 = SEQ if samp else mt * 128
            xsrc = xs if samp else xp[mt * 128:(mt + 1) * 128, :]
            last = (not samp) and mt == NT - 1
            par = 0 if samp else (NT - 1 - mt) % 2
            xt, ynl, Bb, Cb, dtt, xsf, zs = (xt_b[par], ynl_b[par], Bb_b[par], Cb_b[par], dtt_b[par], xsf_b[par], zs_b[par])
            Wl = cvt.rearrange("p a t -> p (a t)").bitcast(BF16) if samp else Wl_S
            wlk = ["cv_t0", "cv_t1"] if samp else ["WlS"]
            ysqS = ysq if samp else ysq_S
            rbcS = rbc if samp else rbc_S
            sk = "" if samp else "S"

            def inter(*gens):
                gens = list(gens)
                while gens:
                    for g_ in list(gens):
                        try:
                            next(g_)
                        except StopIteration:
                            gens.remove(g_)
                        yield

            pcnt = [0]

            def proj(ci):
                i = pcnt[0] % 2
                pcnt[0] += 1
                pa = pA[i]
                for k in range(8):
                    S.pe(lambda e, k=k: e.matmul(pa[:, 0:T], lhsT=w_in_sb[:, k, ci * 128:(ci + 1) * 128],
                                                 rhs=hTt[:, k, 0:T], start=(k == 0), stop=(k == 7)),
                         r=["hTt", "w_in"], w=["pA%d" % i])
                return pa, "pA%d" % i

            def new_cols(buf, sbuf_, c):
                if samp:
                    return sbuf_[:, c, :, 3:7]
                return buf[:, c, 3:131]

            def pa_view(pa):
                if samp:
                    return pa[:, 0:T].rearrange("p (s l) -> p s l", s=NS)
                return pa[:, 0:T]

            def tap(buf, sbuf_, c, k):
                if samp:
                    return sbuf_[:, c, :, k:k + 4]
                return buf[:, c, k:k + 128]

            def fm(t3, c):
                if samp:
                    return t3[:, c, 0:T].rearrange("p (s l) -> p s l", s=NS)
                return t3[:, c, 0:T]

            def conv(buf, sbuf_, c, wname, bname, out_ap, key_in, key_out):
                S.dve(lambda e: e.tensor_scalar(out=out_ap, in0=tap(buf, sbuf_, c, 3), scalar1=P(wname, c, 4)[:, 3:4],
                                                scalar2=P(bname, c), op0=ALU.mult, op1=ALU.add),
                      r=[key_in, "pfm"], w=[key_out])
                for k in (2, 1, 0):
                    S.dve(lambda e, k=k: e.scalar_tensor_tensor(out=out_ap, in0=tap(buf, sbuf_, c, k),
                                                                scalar=P(wname, c, 4)[:, k:k + 1], in1=out_ap,
                                                                op0=ALU.mult, op1=ALU.add),
                          r=[key_in, key_out, "pfm"], w=[key_out])
                if not samp:
                    S.dve(lambda e: e.tensor_copy(out=buf[:, c, 0:3], in_=buf[:, c, 128:131]), r=[key_in], w=[key_in])

            def g_lrux():
                for c in range(8):
                    pa, pk = proj(c)
                    S.act(lambda e, c=c, pa=pa: e.activation(out=new_cols(lxb, lxs, c), in_=pa_view(pa), func=AF.Copy),
                          r=[pk], w=["lx%d" % c])
                    conv(lxb, lxs, c, "LW", "LB", fm(u, c), "lx%d" % c, "u%d" % c)
                    yield

            def g_z():
                for c in range(8):
                    pa, pk = proj(16 + c)
                    S.act(lambda e, c=c, pa=pa: e.activation(out=zs[:, c, 0:T], in_=pa[:, 0:T], func=AF.Silu),
                          r=[pk], w=["zs%d" % c])
                    yield

            def g_xbc():
                for c in range(12):
                    pa, pk = proj(24 + c)
                    S.act(lambda e, c=c, pa=pa: e.activation(out=new_cols(xcb, xcs, c), in_=pa_view(pa), func=AF.Copy),
                          r=[pk], w=["xc%d" % c])
                    if c < 8:
                        conv(xcb, xcs, c, "SW", "SB", fm(cvt, c % 2), "xc%d" % c, "cv_t%d" % (c % 2))
                        S.act(lambda e, c=c: e.activation(out=xsf[:, c, 0:T], in_=cvt[:, c % 2, 0:T], func=AF.Silu),
                              r=["cv_t%d" % (c % 2)], w=["xsf%d" % c])
                    else:
                        g = (c - 8) % 2
                        dstb = Bb if c < 10 else Cb
                        nm = ("Bb%d" if c < 10 else "Cb%d") % g
                        conv(xcb, xcs, c, "SW", "SB", fm(cvt, g), "xc%d" % c, "cv_t%d" % g)
                        S.act(lambda e, g=g, dstb=dstb: e.activation(out=dstb[:, g, 0:T], in_=cvt[:, g, 0:T], func=AF.Silu),
                              r=["cv_t%d" % g], w=[nm])
                    yield
                for k in range(8):
                    S.pe(lambda e, k=k: e.matmul(pD[0:T, 0:16], lhsT=hTt[:, k, 0:T], rhs=w_in_sb[:, k, 4608:4624],
                                                 start=(k == 0), stop=(k == 7)),
                         r=["hTt", "w_in"], w=["pD"])
                S.dve(lambda e: e.tensor_tensor(out=dtr[0:T, :], in0=pD[0:T, 0:16], in1=dtb_bc[0:T, :], op=ALU.add),
                      r=["pD", "dtb"], w=["dtr"])
                S.act(lambda e: e.activation(out=dtr[0:T, :], in_=dtr[0:T, :], func=AF.Exp), r=["dtr"], w=["dtr"])
                S.act(lambda e: e.activation(out=dtt[0:T, :], in_=dtr[0:T, :], func=AF.Ln, bias=1.0), r=["dtr"], w=["dtt"])
                yield

            def g_lru(chunks, pg, kr, ki):
                for c in chunks:
                    pp = c % 2
                    S.pool(lambda e, c=c: e.tensor_copy(out=ub[:, c, 0:T], in_=u[:, c, 0:T]),
                           r=["u%d" % c], w=["ub%d" % c])
                    yield
                    S.pe(lambda e, c=c: e.matmul(pg[:, 0:T], lhsT=wa_blk[:, c, :], rhs=ub[:, c, 0:T], start=True, stop=True),
                         r=["ub%d" % c, "wa"], w=[kr])
                    S.pe(lambda e, c=c: e.matmul(pg[:, 128:128 + T], lhsT=wx_blk[:, c, :], rhs=ub[:, c, 0:T], start=True, stop=True),
                         r=["ub%d" % c, "wx"], w=[ki])
                    yield
                    S.act(lambda e, c=c, pp=pp: e.activation(out=gi[:, 2 * pp, 0:T], in_=pg[:, 0:T], func=AF.Exp, scale=-1.0, bias=nbias[:, c:c + 1]),
                          r=[kr, "nbias"], w=["rg%d" % pp])
                    S.act(lambda e, c=c, pp=pp: e.activation(out=gi[:, 2 * pp + 1, 0:T], in_=pg[:, 128:128 + T], func=AF.Exp, scale=-1.0, bias=nbias[:, 8 + c:9 + c]),
                          r=[ki, "nbias"], w=["ig%d" % pp])
                    S.act(lambda e, pp=pp: e.activation(out=gi[:, 2 * pp:2 * pp + 2, 0:T], in_=gi[:, 2 * pp:2 * pp + 2, 0:T], func=AF.Ln, bias=1.0),
                          r=["rg%d" % pp, "ig%d" % pp], w=["rg%d" % pp, "ig%d" % pp])
                    S.act(lambda e, pp=pp: e.activation(out=gi[:, 2 * pp:2 * pp + 2, 0:T], in_=gi[:, 2 * pp:2 * pp + 2, 0:T], func=AF.Exp, scale=-1.0),
                          r=["rg%d" % pp, "ig%d" % pp], w=["rg%d" % pp, "ig%d" % pp])
                    S.act(lambda e, c=c, pp=pp: e.activation(out=av[:, pp, 0:T], in_=gi[:, 2 * pp, 0:T], func=AF.Exp, scale=cfac[:, c:c + 1]),
                          r=["rg%d" % pp, "cfac"], w=["av%d" % pp])
                    S.act(lambda e, c=c, pp=pp: e.activation(out=a2[:, pp, 0:T], in_=gi[:, 2 * pp, 0:T], func=AF.Exp, scale=c2fac[:, c:c + 1]),
                          r=["rg%d" % pp, "cfac2"], w=["a2%d" % pp])
                    S.act(lambda e, pp=pp: e.activation(out=a2[:, pp, 0:T], in_=a2[:, pp, 0:T], func=AF.Ln, scale=-1.0, bias=1.0),
                          r=["a2%d" % pp], w=["a2%d" % pp])
                    S.act(lambda e, pp=pp: e.activation(out=a2[:, pp, 0:T], in_=a2[:, pp, 0:T], func=AF.Exp, scale=0.5),
                          r=["a2%d" % pp], w=["a2%d" % pp])
                    yield
                    S.dve(lambda e, c=c, pp=pp: e.tensor_tensor(out=tmpb[:, pp, 0:T], in0=gi[:, 2 * pp + 1, 0:T], in1=u[:, c, 0:T], op=ALU.mult),
                          r=["ig%d" % pp, "u%d" % c], w=["tb%d" % pp])
                    S.dve(lambda e, pp=pp: e.tensor_tensor(out=tmpb[:, pp, 0:T], in0=tmpb[:, pp, 0:T], in1=a2[:, pp, 0:T], op=ALU.mult),
                          r=["tb%d" % pp, "a2%d" % pp], w=["tb%d" % pp])
                    if samp:
                        a3 = av[:, pp, 0:T].rearrange("p (s l) -> p s l", s=NS)
                        b3v = tmpb[:, pp, 0:T].rearrange("p (s l) -> p s l", s=NS)
                        S.dve(lambda e, c=c, a3=a3: e.tensor_tensor(out=rbc[:, 0:NS], in0=a3[:, :, 0], in1=h0s[:, c, :], op=ALU.mult),
                              r=["av%d" % pp, "h0s"], w=["rbc"])
                        S.dve(lambda e, b3v=b3v: e.tensor_tensor(out=b3v[:, :, 0], in0=b3v[:, :, 0], in1=rbc[:, 0:NS], op=ALU.add),
                              r=["tb%d" % pp, "rbc"], w=["tb%d" % pp])
                        S.dve(lambda e, a3=a3: e.memset(a3[:, :, 0], 0.0), r=["rbc"], w=["av%d" % pp])
                        S.dve(lambda e, c=c, pp=pp: e.tensor_tensor_scan(out=hs[:, c, 0:T], data0=av[:, pp, 0:T], data1=tmpb[:, pp, 0:T],
                                                                         initial=0.0, op0=ALU.mult, op1=ALU.add),
                              r=["av%d" % pp, "tb%d" % pp], w=["hs%d" % c])
                        S.dve(lambda e, c=c: e.tensor_copy(out=hfin[:, c, :], in_=hs[:, c, 0:T].rearrange("p (s l) -> p s l", s=NS)[:, :, 3]),
                              r=["hs%d" % c], w=["hfin"])
                    else:
                        S.dve(lambda e, c=c, pp=pp: e.tensor_tensor_scan(out=hs[:, c, 0:T], data0=av[:, pp, 0:T], data1=tmpb[:, pp, 0:T],
                                                                         initial=hstate[:, c:c + 1], op0=ALU.mult, op1=ALU.add),
                              r=["av%d" % pp, "tb%d" % pp, "hstate"], w=["hs%d" % c])
                        S.dve(lambda e, c=c: e.tensor_copy(out=hstate[:, c:c + 1], in_=hs[:, c, T - 1:T]),
                              r=["hs%d" % c], w=["hstate"])
                    yield

            def g_gate():
                for c in range(8):
                    pp = c % 2
                    pa, pk = proj(8 + c)
                    S.act(lambda e, pp=pp, pa=pa: e.activation(out=gl[:, pp, 0:T], in_=pa[:, 0:T], func=AF.Gelu_apprx_tanh),
                          r=[pk], w=["gl%d" % pp])
                    S.dve(lambda e, c=c, pp=pp: e.tensor_tensor(out=hs[:, c, 0:T], in0=hs[:, c, 0:T], in1=gl[:, pp, 0:T], op=ALU.mult),
                          r=["hs%d" % c, "gl%d" % pp], w=["yl%d" % c, "hs%d" % c])
                    S.pool(lambda e, c=c, pp=pp: e.tensor_tensor(out=ysq[:, pp, 0:T], in0=hs[:, c, 0:T], in1=hs[:, c, 0:T], op=ALU.mult),
                           r=["yl%d" % c], w=["ysq%d" % pp])
                    S.pe(lambda e, c=c, pp=pp: e.matmul(pD[:, 128:128 + T], lhsT=onesb, rhs=ysq[:, pp, 0:T], start=(c == 0), stop=(c == 7)),
                         r=["ysq%d" % pp, "onesb"], w=["pDn"])
                    yield

            def norm_apply(T, eps_, src, skey, gname, dst, dkey, c0, c1, rbc, rk, pst, pk):
                S.act(lambda e: e.activation(out=rbc[:, 0:T], in_=pst, func=AF.Ln,
                                             scale=1.0 / ((c1 - c0) * 128), bias=eps_t[:, 0:1]),
                      r=[pk, "eps_t"], w=[rk])
                S.act(lambda e: e.activation(out=rbc[:, 0:T], in_=rbc[:, 0:T], func=AF.Exp, scale=-0.5), r=[rk], w=[rk])
                for c in range(c0, c1):
                    S.dve(lambda e, c=c: e.scalar_tensor_tensor(out=dst[:, c, 0:T], in0=src[:, c, 0:T], scalar=P(gname, c),
                                                                in1=rbc[:, 0:T], op0=ALU.mult, op1=ALU.mult),
                          r=[skey % c, rk, "pfm"], w=[dkey % c])

            Um = mskb[0:TS, 0:TS] if samp else Utrib
            ngm = negblk if samp else negm
            allm = mskb[0:TS, TS:2 * TS] if samp else onesb
            d4 = lambda ap: ap[:, 0:4 * T].rearrange("p (a t) -> p a t", a=4)
            Em, Dm, Mm, pC4 = d4(Emf), d4(Dmf), d4(Mmf), d4(pC)

            def g_ssd():
                for c in range(8):
                    S.pe(lambda e, c=c: e.transpose(out=pT[0:T, c * 128:(c + 1) * 128], in_=xsf[:, c, 0:T], identity=ident),
                         r=["xsf%d" % c, "cst"], w=["pT"])
                for g in range(2):
                    S.pe(lambda e, g=g: e.transpose(out=pCb[0:T, 128 + g * 128:128 + (g + 1) * 128], in_=Bb[:, g, 0:T], identity=identb),
                         r=["Bb%d" % g, "identb"], w=["pC", "pCx"])
                S.dve(lambda e: e.tensor_tensor(out=xdt[0:T, :].rearrange("p (h q) -> p h q", h=16),
                                                in0=pT[0:T, :].rearrange("p (h q) -> p h q", h=16),
                                                in1=dtt[0:T, :].unsqueeze(2).to_broadcast([T, 16, 64]), op=ALU.mult),
                      r=["pT", "dtt"], w=["xdt"])
                S.dve(lambda e: e.tensor_copy(out=BT[0:T, :], in_=pCb[0:T, 128:384]), r=["pC"], w=["BT"])
                S.dve(lambda e: e.tensor_tensor(out=da[0:T, :], in0=dtt[0:T, :], in1=a_bc[0:T, :], op=ALU.mult),
                      r=["dtt", "a_bc"], w=["da"])
                yield
                S.dve(lambda e: e.tensor_copy(out=dah[0:T, :], in_=da[0:T, :]), r=["da"], w=["dah"])
                S.dve(lambda e: e.tensor_tensor(out=dal[0:T, :], in0=da[0:T, :], in1=dah[0:T, :], op=ALU.subtract),
                      r=["da", "dah"], w=["dal"])
                for i, dx in enumerate((dah, dal)):
                    S.pe(lambda e, dx=dx, i=i: e.matmul(pC[0:T, 0:16], lhsT=Um[0:T, 0:T], rhs=dx[0:T, :], start=(i == 0), stop=(i == 1)),
                         r=["dah", "dal", "mskb"], w=["pC", "pCx"])
                for i, dx in enumerate((dah, dal)):
                    S.pe(lambda e, dx=dx, i=i: e.matmul(pC[0:T, 16:32], lhsT=allm[0:T, 0:T], rhs=dx[0:T, :], start=(i == 0), stop=(i == 1)),
                         r=["dah", "dal", "mskb", "onesb"], w=["pC", "pCx"])
                if not samp:
                    for i, dx in enumerate((dah, dal)):
                        S.pe(lambda e, dx=dx, i=i: e.matmul(pC[:, 32:48], lhsT=onesb, rhs=dx, start=(i == 0), stop=(i == 1)),
                             r=["dah", "dal", "onesb"], w=["pC", "pCx"])
                for g in range(2):
                    S.pe(lambda e, g=g: e.matmul(pC[0:T, 256 + g * 128:256 + g * 128 + T], lhsT=Bb[:, g, 0:T], rhs=Cb[:, g, 0:T],
                                                 start=True, stop=True), r=["Bb%d" % g, "Cb%d" % g], w=["pC", "pCx"])
                yield
                S.dve(lambda e: e.tensor_scalar(out=ncum[0:T, :], in0=pC[0:T, 0:16], scalar1=-1.0, scalar2=None, op0=ALU.mult),
                      r=["pC"], w=["ncum"])
                S.dve(lambda e: e.tensor_tensor(out=dte[0:T, :], in0=pC[0:T, 16:32], in1=ncum[0:T, :], op=ALU.add),
                      r=["pC", "ncum"], w=["dte"])
                if not samp:
                    S.dve(lambda e: e.tensor_copy(out=cdec, in_=pC[:, 32:48]), r=["pC"], w=["cdec"])
                S.dve(lambda e: e.tensor_copy(out=cbT[0:T, :, 0:T], in_=pC[0:T, 256:512].rearrange("p (g t) -> p g t", g=2)[:, :, 0:T]),
                      r=["pC"], w=["cbT0", "cbT1"])
                S.act(lambda e: e.activation(out=dte[0:T, :], in_=dte[0:T, :], func=AF.Exp), r=["dte"], w=["dte"])
                if not samp:
                    S.act(lambda e: e.activation(out=cdec, in_=cdec, func=AF.Exp), r=["cdec"], w=["cdec"])
                S.dve(lambda e: e.tensor_tensor(out=xdd[0:T, :].rearrange("p (h q) -> p h q", h=16),
                                                in0=xdt[0:T, :].rearrange("p (h q) -> p h q", h=16),
                                                in1=dte[0:T, :].unsqueeze(2).to_broadcast([T, 16, 64]), op=ALU.mult),
                      r=["xdt", "dte"], w=["xdd"])
                yield
                if not samp:
                    for g in range(2):
                        S.pe(lambda e, g=g: e.matmul(pO[:, g * 512:(g + 1) * 512], lhsT=BT[:, g * 128:(g + 1) * 128],
                                                     rhs=xdd[:, g * 512:(g + 1) * 512], start=True, stop=True),
                             r=["BT", "xdd"], w=["pO"])
                    S.dve(lambda e: e.tensor_tensor(out=hT.rearrange("p (h q) -> p h q", h=16),
                                                    in0=hT.rearrange("p (h q) -> p h q", h=16),
                                                    in1=cdec.unsqueeze(2).to_broadcast([128, 16, 64]), op=ALU.mult),
                          r=["hT", "cdec"], w=["hT"])
                    S.dve(lambda e: e.tensor_tensor(out=hT, in0=hT, in1=pO, op=ALU.add), r=["hT", "pO"], w=["hT"])
                    yield
                for q4 in range(4):
                    g = q4 // 2
                    for i, (dx, Wf, wk) in enumerate(((dah, Mmf, ["Mm"]), (dal, Wl, wlk))):
                        S.pool(lambda e, q4=q4, dx=dx, Wf=Wf: e.tensor_tensor(out=d4(Wf)[0:T], in0=Um[0:T, 0:T].unsqueeze(1).to_broadcast([T, 4, T]),
                                                                             in1=dx[0:T, q4 * 4:q4 * 4 + 4].unsqueeze(2).to_broadcast([T, 4, T]),
                                                                             op=ALU.mult), r=["dah", "dal", "mskb"], w=wk)
                        S.pe(lambda e, Wf=Wf, i=i: e.matmul(pC[:, 0:4 * T], lhsT=onesb[0:T, :], rhs=Wf[0:T, 0:4 * T],
                                                            start=(i == 0), stop=(i == 1)), r=wk + ["onesb"], w=["pC", "pCx"])
                    S.dve(lambda e: e.tensor_copy(out=Em, in_=pC4), r=["pC"], w=["Em"])
                    yield
                    for hh in range(4):
                        h = q4 * 4 + hh
                        S.dve(lambda e, h=h, hh=hh: e.scalar_tensor_tensor(out=Dm[0:T, hh, :], in0=Em[0:T, hh, :],
                                                                           scalar=ncum[0:T, h:h + 1], in1=ngm[0:T, 0:T],
                                                                           op0=ALU.add, op1=ALU.add),
                              r=["Em", "ncum", "cst"], w=["Dm"])
                    S.act(lambda e: e.activation(out=Em, in_=Em, func=AF.Exp), r=["Em"], w=["Em"])
                    S.act(lambda e: e.activation(out=Dm[0:T], in_=Dm[0:T], func=AF.Exp), r=["Dm"], w=["Dm"])
                    S.pool(lambda e, g=g: e.tensor_tensor(out=Mm[0:T], in0=Dm[0:T],
                                                         in1=cbT[0:T, g, 0:T].unsqueeze(1).to_broadcast([T, 4, T]), op=ALU.mult),
                          r=["Dm", "cbT%d" % g], w=["Mm"])
                    S.pool(lambda e, g=g, q4=q4: e.tensor_tensor(out=(Chs[:, q4 * 4:q4 * 4 + 4, :] if samp else Chp), in0=Em,
                                                                in1=Cb[:, g, 0:T].unsqueeze(1).to_broadcast([128, 4, T]), op=ALU.mult),
                          r=["Em", "Cb%d" % g], w=["Ch"])
                    yield
                    for hh in range(4):
                        h = q4 * 4 + hh
                        c = h // 2
                        h2 = h % 2
                        po = pT[64 * h2:64 * h2 + 64, c * 128:c * 128 + T]
                        S.pe(lambda e, h=h, hh=hh, po=po: e.matmul(po, lhsT=xdt[0:T, h * 64:(h + 1) * 64], rhs=Mm[0:T, hh, :],
                                                                   start=True, stop=samp), r=["xdt", "Mm"], w=["pT"])
                        if not samp:
                            S.pe(lambda e, h=h, hh=hh, po=po: e.matmul(po, lhsT=hTb[:, h * 64:(h + 1) * 64], rhs=Chp[:, hh, :],
                                                                       start=False, stop=True), r=["hTb", "Ch"], w=["pT"])
                    yield

            def late_outputs():
                M = T if samp else 3
                t0 = 0 if samp else 125
                if samp or last:
                    for blk, col0 in enumerate((0, 512, 3072, 3584, 4096)):
                        for k in range(8):
                            S.pe(lambda e, k=k, col0=col0: e.matmul(pO[0:M, 0:512], lhsT=hTt[:, k, t0:t0 + M],
                                                                    rhs=w_in_sb[:, k, col0:col0 + 512], start=(k == 0), stop=(k == 7)),
                                 r=["hTt", "w_in"], w=["pO"])
                        S.dve(lambda e, blk=blk: e.tensor_copy(out=stg[0:M, blk * 512:(blk + 1) * 512], in_=pO[0:M, 0:512]),
                              r=["pO"], w=["stg"] + STGW)
                if last:
                    S.dma(lambda e: e.dma_start(out=o_plc, in_=stg[0:3, 0:1024]), "o_plc", r=["stg"])
                    S.dma(lambda e: e.dma_start(out=o_psc, in_=stg[0:3, 1024:2560]), "o_psc", r=["stg"])
                if samp:
                    for s in range(NS):
                        S.dma(lambda e, s=s: e.dma_start(out=o_slc[s], in_=stg[4 * s + 1:4 * s + 4, 0:1024]), "o_slc", r=["stg"])
                        S.dma(lambda e, s=s: e.dma_start(out=o_ssc[s], in_=stg[4 * s + 1:4 * s + 4, 1024:2560]), "o_ssc", r=["stg"])
                if last:
                    S.pe(lambda e: e.transpose(out=pC[0:8, 0:128], in_=hstate, identity=ident), r=["hstate", "cst"], w=["pC", "pCx"])
                    S.act(lambda e: e.activation(out=stT[0:8, 0:128], in_=pC[0:8, 0:128], func=AF.Copy), r=["pC"], w=["stg"] + STGW)
                    S.dma(lambda e: e.dma_start(out=o_plh, in_=stT[0:8, 0:128]), "o_plh", r=["stg"])
                if samp:
                    for c in range(8):
                        S.pe(lambda e, c=c: e.transpose(out=pT[0:NS, c * 128:(c + 1) * 128], in_=hfin[:, c, :], identity=ident),
                             r=["hfin", "cst"], w=["pT"])
                    S.act(lambda e: e.activation(out=lh_in[0:NS, :], in_=pT[0:NS, :], func=AF.Copy), r=["pT"], w=["stg"] + STGW)
                    S.dma(lambda e: e.dma_start(out=o_slh, in_=lh_in[0:NS, :]), "o_slh", r=["stg"])


            def genP():
                S.dma(lambda e: e.dma_start(out=xt[0:T, :], in_=xsrc), "xt", w=["xt"])
                rms_rstd(xt, T, "xt", junk, ss, rstd)
                S.act(lambda e: e.activation(out=xn[0:T, :], in_=xt[0:T, :], func=AF.Copy, scale=rstd[0:T, 0:1]),
                      r=["xt", "rstd"], w=["xn"])
                to_fm(T, "GM", hTt, "hTt")

                if samp:
                    S.dma(lambda e: e.dma_start(out=lc_in[0:48, :], in_=st_lc), "stg", w=["stg"])
                    S.dma(lambda e: e.dma_start(out=sc_in[0:48, :], in_=st_sc), "stg", w=["stg"])
                    S.dma(lambda e: e.dma_start(out=lh_in[64:64 + NS, :], in_=st_lh), "stg", w=["stg"])
                    for c in range(8):
                        S.pe(lambda e, c=c: e.transpose(out=pC[:, 0:48], in_=lc_in[0:48, c * 128:(c + 1) * 128],
                                                        identity=ident[0:48, 0:48]), r=["stg", "cst"], w=["pC"])
                        S.act(lambda e, c=c: e.activation(out=lxs[:, c, :, 0:3],
                                                          in_=pC[:, 0:48].rearrange("p (s j) -> p s j", s=NS),
                                                          func=AF.Copy), r=["pC"], w=["lx%d" % c])
                        S.pe(lambda e, c=c: e.transpose(out=pD[:, 0:NS], in_=lh_in[64:64 + NS, c * 128:(c + 1) * 128],
                                                        identity=ident[64:64 + NS, 64:64 + NS]), r=["stg", "cst"], w=["pD"])
                        S.dve(lambda e, c=c: e.tensor_copy(out=h0s[:, c, :], in_=pD[:, 0:NS]), r=["pD"], w=["h0s"])
                    for c in range(12):
                        S.pe(lambda e, c=c: e.transpose(out=pC[:, 0:48], in_=sc_in[0:48, c * 128:(c + 1) * 128],
                                                        identity=ident[0:48, 0:48]), r=["stg", "cst"], w=["pC"])
                        S.act(lambda e, c=c: e.activation(out=xcs[:, c, :, 0:3],
                                                          in_=pC[:, 0:48].rearrange("p (s j) -> p s j", s=NS),
                                                          func=AF.Copy), r=["pC"], w=["xc%d" % c])

                yield
                yield from inter(g_lrux(), g_z())
                yield from inter(g_xbc())
                yield from inter(g_lru((0, 2, 4, 6), pA[0], "pA0", "pA0"), g_lru((1, 3, 5, 7), pA[1], "pA1", "pA1"))
                yield from inter(g_gate())
                norm_apply(T, EPS, hs, "yl%d", "GL", ynl, "ynl%d", 0, 8, rbc, "rbc", pD[:, 128:128 + T], "pDn")
                yield
                if samp:
                    late_outputs()

            def genS():
                yield from inter(g_ssd())
                if samp:
                    ssd_sample_states_prep()

                for c in range(8):
                    S.dve(lambda e, c=c: e.scalar_tensor_tensor(out=xsf[:, c, 0:T], in0=xsf[:, c, 0:T], scalar=P("DS", c),
                                                                in1=pT[:, c * 128:c * 128 + T], op0=ALU.mult, op1=ALU.add),
                          r=["pT", "xsf%d" % c, "pfm"], w=["xsf%d" % c])
                if samp:
                    S.dve(lambda e: e.tensor_tensor(out=xsf[:, :, 0:T], in0=xsf[:, :, 0:T], in1=pyo_sb, op=ALU.add),
                          r=["xsf%d" % c for c in range(8)] + ["pyo_sb"], w=["xsf%d" % c for c in range(8)])
                S.pool(lambda e: e.tensor_tensor(out=xsf[:, :, 0:T], in0=xsf[:, :, 0:T], in1=zs[:, :, 0:T], op=ALU.mult),
                       r=["xsf%d" % c for c in range(8)] + ["zs%d" % c for c in range(8)], w=["yg%d" % c for c in range(8)] + ["xsf%d" % c for c in range(8)])
                yield
                if not samp:
                    S.act(lambda e: e.activation(out=hTb, in_=hT, func=AF.Copy), r=["hT"], w=["hTb"])
                    if last:
                        for c in range(8):
                            S.pe(lambda e, c=c: e.transpose(out=pO[:, c * 128:(c + 1) * 128], in_=hT[:, c * 128:(c + 1) * 128], identity=ident),
                                 r=["hT", "cst"], w=["pO"])
                        S.dve(lambda e: e.tensor_copy(out=stT, in_=pO), r=["pO"], w=["stg"] + STGW)
                        S.dma(lambda e: e.dma_start(out=o_psh.rearrange("(c q) n -> q c n", q=128),
                                                    in_=stT.rearrange("p (c n) -> p c n", c=8)), "o_psh", r=["stg"])
                yield
                for g in range(2):
                    for c in range(4 * g, 4 * g + 4):
                        pp = c % 2
                        S.pool(lambda e, c=c, pp=pp: e.tensor_tensor(out=ysqS[:, pp, 0:T], in0=xsf[:, c, 0:T], in1=xsf[:, c, 0:T], op=ALU.mult),
                               r=["yg%d" % c], w=["ysq%s%d" % (sk, pp)])
                        S.pe(lambda e, c=c, pp=pp, g=g: e.matmul(pO[:, 0:T], lhsT=onesb, rhs=ysqS[:, pp, 0:T],
                                                                 start=(c == 4 * g), stop=(c == 4 * g + 3)),
                             r=["ysq%s%d" % (sk, pp), "onesb"], w=["pO"])
                    norm_apply(T, EPS, xsf, "yg%d", "GS", yns, "yns%d", 4 * g, 4 * g + 4, rbcS, "rbc" + sk, pO[:, 0:T], "pO")

                yield
                for nb in range(2):
                    for kc in range(16):
                        src = ynl if kc < 8 else yns
                        S.pe(lambda e, kc=kc, nb=nb, src=src: e.matmul(pO[0:T, nb * 512:(nb + 1) * 512], lhsT=src[:, kc % 8, 0:T],
                                                                       rhs=w_out_sb[:, kc, nb * 512:(nb + 1) * 512],
                                                                       start=(kc == 0), stop=(kc == 15)),
                             r=[("ynl%d" if kc < 8 else "yns%d") % (kc % 8), "w_out"], w=["pO"])
                S.dve(lambda e: e.tensor_tensor(out=xt[0:T, :], in0=pO[0:T, :], in1=xt[0:T, :], op=ALU.add),
                      r=["pO", "xt"], w=["xt"])
                S.dma(lambda e: e.dma_start(out=scr[row0:row0 + T, :], in_=xt[0:T, :]), "xnew", r=["xt"], w=["scr%d" % mt])

                if last:
                    late_outputs()

            return par, genP, genS

        def ssd_sample_states_prep():
            T = TS
            for i, dx in enumerate((dah, dal)):
                S.dve(lambda e, dx=dx, i=i: e.tensor_tensor(out=damb[0:T, i], in0=dx[0:T, :].unsqueeze(1).to_broadcast([T, NS, 16]),
                                                            in1=blki.unsqueeze(2).to_broadcast([T, NS, 16]), op=ALU.mult),
                      r=["dah", "dal", "cst"], w=["dam%d" % i])
                S.pe(lambda e, i=i: e.matmul(pD[:, 0:256], lhsT=onesb[0:T, :], rhs=damb[0:T, i].rearrange("p s h -> p (s h)"),
                                             start=(i == 0), stop=(i == 1)), r=["dam%d" % i, "onesb"], w=["pD", "pD2", "pD3", "pDn"])
            S.act(lambda e: e.activation(out=dtot.rearrange("p s h -> p (s h)"), in_=pD[:, 0:256], func=AF.Exp),
                  r=["pD"], w=["dtot"])
            S.barrier()
            h0in_b = [h0in, u]
            hout_b = [hout, hs]
            hnew_b = [hnew, lrut[:, 0:1024]]
            h0Tb_b = [h0Tb, ub.rearrange("p c t -> p (c t)")]
            pR_b = [pO, PS[:, 2048:3072]]
            pRk = [["pO"], ["pC", "pD"]]
            for s in range(NS):
                q = s % 2
                hi, ho, hn, hb, pR, pk = h0in_b[q], hout_b[q], hnew_b[q], h0Tb_b[q], pR_b[q], pRk[q]
                S.dma(lambda e, s=s, hi=hi: e.dma_start(out=hi, in_=st_sh[s].rearrange("(c q) n -> q c n", q=128)),
                      "h0in%d" % q, w=["h0in%d" % q])
                for c in range(8):
                    S.pe(lambda e, c=c, hi=hi, pR=pR: e.transpose(out=pR[:, c * 128:(c + 1) * 128], in_=hi[:, c, :], identity=ident),
                         r=["h0in%d" % q, "cst"], w=pk)
                S.act(lambda e, hb=hb, pR=pR: e.activation(out=hb, in_=pR, func=AF.Copy), r=pk, w=["h0Tb%d" % q])
                S.dve(lambda e, s=s, hn=hn, pR=pR: e.tensor_tensor(out=hn.rearrange("p (h q) -> p h q", h=16),
                                                                   in0=pR.rearrange("p (h q) -> p h q", h=16),
                                                                   in1=dtot[:, s, :].unsqueeze(2).to_broadcast([128, 16, 64]), op=ALU.mult),
                      r=pk + ["dtot"], w=["hnew%d" % q])
                for h in range(16):
                    S.pe(lambda e, h=h, s=s, hb=hb: e.matmul(pA[0][64 * (h % 2):64 * (h % 2) + 64, (h // 2) * TS + 4 * s:(h // 2) * TS + 4 * s + 4],
                                                             lhsT=hb[:, h * 64:(h + 1) * 64], rhs=Chs[:, h, 4 * s:4 * s + 4],
                                                             start=True, stop=True),
                         r=["h0Tb%d" % q, "Ch"], w=["pA0"])
                S.pool(lambda e, s=s: e.tensor_scalar(out=Bm[0:T, :], in0=BT[0:T, :], scalar1=blki[:, s:s + 1], scalar2=None, op0=ALU.mult),
                       r=["BT", "cst"], w=["Bm"])
                for g in range(2):
                    S.pe(lambda e, g=g, s=s, pR=pR: e.matmul(pR[:, g * 512:(g + 1) * 512], lhsT=Bm[0:T, g * 128:(g + 1) * 128],
                                                             rhs=xdd[0:T, g * 512:(g + 1) * 512], start=True, stop=True),
                         r=["Bm", "xdd"], w=pk)
                S.dve(lambda e, hn=hn, pR=pR: e.tensor_tensor(out=hn, in0=hn, in1=pR, op=ALU.add), r=["hnew%d" % q] + pk, w=["hnew%d" % q])
                for c in range(8):
                    S.pe(lambda e, c=c, hn=hn, pR=pR: e.transpose(out=pR[:, c * 128:(c + 1) * 128], in_=hn[:, c * 128:(c + 1) * 128], identity=ident),
                         r=["hnew%d" % q, "cst"], w=pk)
                S.dve(lambda e, ho=ho, pR=pR: e.tensor_copy(out=ho.rearrange("p c n -> p (c n)"), in_=pR), r=pk, w=["hout%d" % q])
                S.dma(lambda e, s=s, ho=ho: e.dma_start(out=o_ssh[s].rearrange("(c q) n -> q c n", q=128), in_=ho),
                      "hout%d" % q, r=["hout%d" % q])
            S.act(lambda e: e.activation(out=pyo_sb.rearrange("p c t -> p (c t)"), in_=pA[0][:, 0:8 * TS], func=AF.Copy),
                  r=["pA0"], w=["pyo_sb"])

        S.pool(lambda e: e.memset(lxb, 0.0), w=["lx%d" % c for c in range(8)])
        S.pool(lambda e: e.memset(xcb, 0.0), w=["xc%d" % c for c in range(12)])

        def drive(g_, par):
            S.ctx = par
            try:
                next(g_)
                return True
            except StopIteration:
                return False
            finally:
                S.ctx = None

        tiles = [mixer_tile(mt, False) for mt in range(NT)]
        RATIO = 3
        par0, gP0, _ = tiles[0]
        g = gP0()
        while drive(g, par0):
            pass
        for n in range(NT):
            par, _, gS = tiles[n]
            gs = gS()
            alive_s = True
            alive_p = False
            if n + 1 < NT:
                parn, gPn, _ = tiles[n + 1]
                gp = gPn()
                alive_p = True
            while alive_s or alive_p:
                for _ in range(RATIO):
                    if alive_p:
                        alive_p = drive(gp, parn)
                if alive_s:
                    alive_s = drive(gs, par)
        S.barrier()
        if SAMP:
            pars, gPs, gSs = mixer_tile(SEQ // 128, True)
            for g in (gPs(), gSs()):
                while drive(g, pars):
                    pass

        S.barrier()
        ptr[0] = base0
        w_up_sb = b3(8, DFF)
        w_dn_sb = b3(32, D)
        if MLP:
            k_wup = load_w(w_up_sb, w_up, 8, DFF, "w_up")
            k_wdn = load_w(w_dn_sb, w_down, 32, D, "w_dn")
        T2 = 256
        xt2 = [[f32(D), f32(D)], [f32(D), f32(D)]]
        xn2 = f32(D)
        ss2 = [f32(4), f32(4)]
        rstd2 = [f32(4), f32(4)]
        mT = [b3(8, T2), b3(8, T2)]
        actb = b3(32, T2)
        rl = [f32(T2), f32(T2)]
        yout = f32(D)
        gfin_bc = f32(D)
        S.dma(lambda e: e.dma_start(out=gfin_bc, in_=gfin_d.partition_broadcast(128)), "gfin", w=["gfin"])
        pDN = [PS[:, 3072:4096], PS[:, 2048:3072]]
        pDNk = [["pO"], ["pC", "pD"]]

        def mlp_front(ti, r0, T):
            q = ti % 2
            nsub = (T + 127) // 128
            for j in range(nsub):
                Tj = min(128, T - j * 128)
                xk = "xt2_%d_%d" % (q, j)
                S.dma(lambda e, j=j, Tj=Tj: e.dma_start(out=xt2[q][j][0:Tj, :], in_=scr[r0 + j * 128:r0 + j * 128 + Tj, :]),
                      xk, r=["scr%d" % ((r0 + j * 128) // 128)], w=[xk])
                rms_rstd(xt2[q][j], Tj, xk, xn2, ss2[0], rstd2[0], "2")
                S.act(lambda e, j=j, Tj=Tj: e.activation(out=xn2[0:Tj, :], in_=xt2[q][j][0:Tj, :], func=AF.Copy, scale=rstd2[0][0:Tj, 0:1]),
                      r=[xk, "rstd2"], w=["xn2"])
                for k in range(8):
                    S.pe(lambda e, k=k, Tj=Tj: e.transpose(out=pT[:, k * 128:k * 128 + Tj], in_=xn2[0:Tj, k * 128:(k + 1) * 128],
                                                           identity=ident[0:Tj, 0:Tj]), r=["xn2", "cst"], w=["pT"])
                S.dve(lambda e, j=j, Tj=Tj: e.tensor_tensor(
                    out=mT[q][:, :, j * 128:j * 128 + Tj], in0=pT.rearrange("p (k t) -> p k t", k=8)[:, :, 0:Tj],
                    in1=P("GP", 0, 8).unsqueeze(2).to_broadcast([128, 8, Tj]), op=ALU.mult),
                    r=["pT", "pfm"], w=["mT%d" % q])
            yield
            for f in range(32):
                pa = pA[f % 2]
                for k in range(8):
                    S.pe(lambda e, k=k, f=f, pa=pa: e.matmul(pa[:, 0:T], lhsT=w_up_sb[:, k, f * 128:(f + 1) * 128], rhs=mT[q][:, k, 0:T],
                                                             start=(k == 0), stop=(k == 7)),
                         r=["mT%d" % q, "w_up"], w=["pA%d" % (f % 2)])
                S.act(lambda e, f=f, pa=pa: e.activation(out=rl[f % 2][:, 0:T], in_=pa[:, 0:T], func=AF.Relu),
                      r=["pA%d" % (f % 2)], w=["rl%d" % (f % 2)])
                S.pool(lambda e, f=f: e.tensor_tensor(out=actb[:, f, 0:T], in0=rl[f % 2][:, 0:T], in1=rl[f % 2][:, 0:T], op=ALU.mult),
                       r=["rl%d" % (f % 2)], w=["act%d" % f])
                yield

        def mlp_back(ti, r0, T):
            q = ti % 2
            nsub = (T + 127) // 128
            for f in range(32):
                for j in range(nsub):
                    Tj = min(128, T - j * 128)
                    for nb in range(2):
                        S.pe(lambda e, f=f, nb=nb, j=j, Tj=Tj: e.matmul(pDN[j][0:Tj, nb * 512:(nb + 1) * 512],
                                                                        lhsT=actb[:, f, j * 128:j * 128 + Tj],
                                                                        rhs=w_dn_sb[:, f, nb * 512:(nb + 1) * 512],
                                                                        start=(f == 0), stop=(f == 31)),
                             r=["act%d" % f, "w_dn"], w=pDNk[j])
                yield
            for j in range(nsub):
                Tj = min(128, T - j * 128)
                xk = "xt2_%d_%d" % (q, j)
                S.dve(lambda e, j=j, Tj=Tj: e.tensor_tensor(out=xt2[q][j][0:Tj, :], in0=pDN[j][0:Tj, :], in1=xt2[q][j][0:Tj, :], op=ALU.add),
                      r=pDNk[j] + [xk], w=[xk])
                rms_rstd(xt2[q][j], Tj, xk, yout, ss2[1], rstd2[1], "2b", jkey="yout")
                S.dve(lambda e, j=j, Tj=Tj: e.scalar_tensor_tensor(out=yout[0:Tj, :], in0=xt2[q][j][0:Tj, :], scalar=rstd2[1][0:Tj, 0:1],
                                                                   in1=gfin_bc[0:Tj, :], op0=ALU.mult, op1=ALU.mult),
                      r=[xk, "rstd2b", "gfin"], w=["yout"])
                rr = r0 + j * 128
                if rr < SEQ:
                    S.dma(lambda e, rr=rr, Tj=Tj: e.dma_start(out=y_p[rr:rr + Tj, :], in_=yout[0:Tj, :]), "yout", r=["yout"])
                else:
                    S.dma(lambda e, Tj=Tj: e.dma_start(out=y_s, in_=yout[0:Tj, :]), "yout", r=["yout"])
                yield

        if MLP:
            jobs = [(t * T2, T2) for t in range(NT * 128 // T2)]
            if SAMP:
                jobs.append((SEQ, TS))
            for _ in mlp_front(0, *jobs[0]):
                pass
            for ti in range(len(jobs)):
                gb = mlp_back(ti, *jobs[ti])
                gf = mlp_front(ti + 1, *jobs[ti + 1]) if ti + 1 < len(jobs) else iter(())
                ab = af = True
                while ab or af:
                    if ab:
                        ab = next(gb, "END") != "END"
                    if af:
                        af = next(gf, "END") != "END"

        S.emit()
    return nc


_CACHE = {}


def _consts():
    c = np.zeros((128, NCST), np.float32)
    i = np.arange(128)
    c[:, CI:CI + 128] = np.eye(128, dtype=np.float32)
    c[:, CU:CU + 128] = (i[:, None] <= i[None, :]).astype(np.float32)
    c[:, CN:CN + 128] = np.where(i[:, None] <= i[None, :], 0.0, NEG).astype(np.float32)
    c[:, CO:CO + 128] = 1.0
    j = np.arange(TS)
    same = (j[:, None] // 4) == (j[None, :] // 4)
    caus = j[:, None] <= j[None, :]
    c[0:TS, CUB:CUB + TS] = (same & caus).astype(np.float32)
    c[0:TS, CNB:CNB + TS] = np.where(same & caus, 0.0, NEG).astype(np.float32)
    c[0:TS, CBM:CBM + TS] = same.astype(np.float32)
    c[0:TS, CBI:CBI + NS] = ((j[:, None] // 4) == np.arange(NS)[None, :]).astype(np.float32)
    return c


def _fm(v, nch):
    return np.ascontiguousarray(np.asarray(v, np.float32).reshape(nch, 128).T)


def kernel(x_prompt, x_sample, state_lru_conv, state_lru_h, state_ssd_conv, state_ssd_h,
           g_mix, w_in, lru_conv_w, lru_conv_b, w_a, b_a, w_x, b_x, lam, g_lru_out,
           ssd_conv_w, ssd_conv_b, dt_bias, a_log, d_skip, g_ssd_out, w_out,
           g_mlp, w_up, w_down, g_final):
    f = lambda a: np.ascontiguousarray(np.asarray(a, np.float32))
    if "nc" not in _CACHE:
        _CACHE["nc"] = build_program()
    nc = _CACHE["nc"]
    pfm = np.zeros((128, NPAR), np.float32)
    lw = np.asarray(lru_conv_w[0], np.float32)
    pfm[:, PC["LW"]:PC["LW"] + 32] = lw.reshape(4, 8, 128).transpose(2, 1, 0).reshape(128, 32)
    pfm[:, PC["LB"]:PC["LB"] + 8] = _fm(lru_conv_b[0], 8)
    pfm[:, PC["BA"]:PC["BA"] + 8] = _fm(np.asarray(b_a[0]).reshape(-1), 8)
    pfm[:, PC["BX"]:PC["BX"] + 8] = _fm(np.asarray(b_x[0]).reshape(-1), 8)
    pfm[:, PC["LAM"]:PC["LAM"] + 8] = _fm(lam[0], 8)
    pfm[:, PC["GL"]:PC["GL"] + 8] = _fm(g_lru_out[0], 8)
    sw = np.asarray(ssd_conv_w[0], np.float32)
    pfm[:, PC["SW"]:PC["SW"] + 48] = sw.reshape(4, 12, 128).transpose(2, 1, 0).reshape(128, 48)
    pfm[:, PC["SB"]:PC["SB"] + 12] = _fm(ssd_conv_b[0], 12)
    pfm[:, PC["DS"]:PC["DS"] + 8] = _fm(np.repeat(np.asarray(d_skip[0], np.float32), 64), 8)
    pfm[:, PC["GS"]:PC["GS"] + 8] = _fm(g_ssd_out[0], 8)
    pfm[:, PC["GM"]:PC["GM"] + 8] = _fm(g_mix[0], 8)
    pfm[:, PC["GP"]:PC["GP"] + 8] = _fm(g_mlp[0], 8)
    cst = _consts()
    shared = {
        "w_in": f(w_in[0]), "w_out": f(w_out[0]), "w_up": f(w_up[0]), "w_down": f(w_down[0]),
        "w_a": f(w_a[0]), "w_x": f(w_x[0]), "pfm": pfm, "cst": cst,
        "dt_bias": f(dt_bias[0]), "a_log": f(a_log[0]), "g_final": f(g_final),
    }
    in_maps = []
    for b in range(NCORES):
        sl = slice(NS * b, NS * (b + 1))
        m = dict(shared)
        m["xp"] = f(x_prompt[b])
        m["xs"] = f(np.asarray(x_sample[sl]).reshape(TS, D))
        m["st_lc"] = f(np.asarray(state_lru_conv[0, sl]).reshape(NS * 3, D))
        m["st_lh"] = f(state_lru_h[0, sl])
        m["st_sc"] = f(np.asarray(state_ssd_conv[0, sl]).reshape(NS * 3, XBC))
        m["st_sh"] = f(np.asarray(state_ssd_h[0, sl]).reshape(NS, 1024, 128))
        in_maps.append(m)
    res = run_bass_kernel_spmd(nc, in_maps, core_ids=list(range(NCORES)))
    R = res.results
    cat = lambda k: np.stack([np.asarray(R[b][k], np.float32) for b in range(NCORES)])
    y_prompt = cat("y_p")
    y_sample = cat("y_s").reshape(NCORES * NS, 4, D)
    p_lc = cat("o_plc")[None]
    p_lh = cat("o_plh").reshape(NCORES, D)[None]
    p_sc = cat("o_psc")[None]
    p_sh = cat("o_psh").reshape(NCORES, 16, 64, 128)[None]
    s_lc = cat("o_slc").reshape(NCORES * NS, 3, D)[None]
    s_lh = cat("o_slh").reshape(NCORES * NS, D)[None]
    s_sc = cat("o_ssc").reshape(NCORES * NS, 3, XBC)[None]
    s_sh = cat("o_ssh").reshape(NCORES * NS, 16, 64, 128)[None]
    return (y_prompt, y_sample, p_lc, p_lh, p_sc, p_sh, s_lc, s_lh, s_sc, s_sh)
```

```python
import math
from contextlib import ExitStack

import numpy as np
import concourse.bass as bass
import concourse.mybir as mybir
from concourse.bass_utils import run_bass_kernel_spmd

F32 = mybir.dt.float32
BF16 = mybir.dt.bfloat16
AF = mybir.ActivationFunctionType
ALU = mybir.AluOpType

NCORES = 8
D = 1024
SEQ = 2048
NS = 16
TS = 64
XBC = 1536
INP = 4624
DFF = 4096
EPS = 1e-6
NEG = -30000.0

import re as _re

ENGS = ("pe", "act", "dve", "pool", "sp")
SAME_ENGINE_SYNC = {"pe": False, "act": True, "dve": True, "pool": True, "sp": False}


class Op:
    __slots__ = ("eng", "fn", "deps", "marked", "count", "dma_key", "dma_val")

    def __init__(self, eng, fn, dma_key=None):
        self.eng = eng
        self.fn = fn
        self.deps = ()
        self.marked = False
        self.count = 0
        self.dma_key = dma_key
        self.dma_val = 0


class Sched:
    def __init__(self, nc):
        self.nc = nc
        self.ops = {e: [] for e in ENGS}
        self.last_w = {}
        self.readers = {}
        self.dma_cnt = {}
        self.pending = {}
        self.ctx = None
        self.since_bar = []

    ALIAS = {"pC": "b4", "pCx": "b4", "pD": "b5", "pD2": "b5", "pD3": "b5", "pDn": "b5",
             "pDcb0": "b5", "pDcb1": "b5", "pT": "b01", "pO": "b67", "pA0": "b2", "pA1": "b3"}

    PSUM_KEYS = {"b01", "b2", "b3", "b4", "b5", "b67"}

    PAR_RE = _re.compile(r"^(xt|dtt|xnew)$|^(xsf|zs|Bb|Cb|ynl|yg)\d+$")

    def _k(self, k):
        k = self.ALIAS.get(k, k)
        if self.ctx is not None and self.PAR_RE.match(k):
            return k + "#" + str(self.ctx)
        return k

    def add(self, eng, fn, reads=(), writes=(), dma_key=None):
        reads = [self._k(k) for k in reads]
        writes = [self._k(k) for k in writes]
        if dma_key is not None and self.ctx is not None and self.PAR_RE.match(dma_key):
            dma_key = dma_key + "#" + str(self.ctx)
        op = Op(eng, fn, dma_key)
        deps = []
        seen = set()

        def dep(o):
            if o is not None and o is not op and id(o) not in seen:
                seen.add(id(o))
                deps.append(o)

        if self.pending.get(eng):
            for o in self.pending[eng]:
                dep(o)
            self.pending[eng] = []
        for b in reads:
            dep(self.last_w.get(b))
            if b in self.PSUM_KEYS:
                for r in self.readers.get(b, ()):
                    if r.eng != eng:
                        dep(r)
        for b in writes:
            dep(self.last_w.get(b))
            for r in self.readers.get(b, ()):
                dep(r)
        for b in reads:
            self.readers.setdefault(b, []).append(op)
        for b in writes:
            self.last_w[b] = op
            self.readers[b] = []
        op.deps = deps
        if dma_key is not None:
            self.dma_cnt[dma_key] = self.dma_cnt.get(dma_key, 0) + 16
            op.dma_val = self.dma_cnt[dma_key]
        self.ops[eng].append(op)
        self.since_bar.append(op)
        return op

    def barrier(self):
        ops = []
        for e in ENGS:
            comp = [o for o in self.ops[e] if o.dma_key is None]
            if comp:
                ops.append(comp[-1])
        last_dma = {}
        for o in self.since_bar:
            if o.dma_key is not None:
                last_dma[o.dma_key] = o
        ops.extend(last_dma.values())
        for e in ENGS:
            self.pending.setdefault(e, []).extend(ops)
        self.since_bar = []

    def pe(self, fn, r=(), w=()):
        return self.add("pe", fn, r, w)

    def act(self, fn, r=(), w=()):
        return self.add("act", fn, r, w)

    def dve(self, fn, r=(), w=()):
        return self.add("dve", fn, r, w)

    def pool(self, fn, r=(), w=()):
        return self.add("pool", fn, r, w)

    def dma(self, fn, key, r=(), w=(), q="sp"):
        return self.add(q, fn, r, w, dma_key=key)

    def emit(self):
        nc = self.nc
        for e in ENGS:
            for op in self.ops[e]:
                for d in op.deps:
                    if d.dma_key is None:
                        if d.eng == op.eng and not SAME_ENGINE_SYNC[d.eng]:
                            continue
                        d.marked = True
        for e in ENGS:
            c = 0
            for op in self.ops[e]:
                if op.dma_key is None and op.marked:
                    c += 1
                    op.count = c
        with ExitStack() as st:
            esem = {e: st.enter_context(nc.semaphore("es_" + e)) for e in ENGS}
            dsem = {}
            for k in self.dma_cnt:
                dsem[k] = st.enter_context(nc.semaphore("ds_%d" % len(dsem)))
            block = st.enter_context(nc.Block())

            def run(ename, eng):
                seen = {}
                for op in self.ops[ename]:
                    need = {}
                    for d in op.deps:
                        if d.dma_key is not None:
                            key = ("d", d.dma_key)
                            val = d.dma_val
                            sem = dsem[d.dma_key]
                        else:
                            if d.eng == ename and not SAME_ENGINE_SYNC[ename]:
                                continue
                            key = ("e", d.eng)
                            val = d.count
                            sem = esem[d.eng]
                        if key not in need or need[key][1] < val:
                            need[key] = (sem, val)
                    for key, (sem, val) in need.items():
                        if seen.get(key, 0) >= val:
                            continue
                        seen[key] = val
                        eng.wait_ge(sem, val)
                    ins = op.fn(eng)
                    if op.dma_key is not None:
                        ins.then_inc(dsem[op.dma_key], 16)
                    elif op.marked:
                        ins.then_inc(esem[ename], 1)
                if ename == "sp":
                    for k, v in self.dma_cnt.items():
                        eng.wait_ge(dsem[k], v)

            @block.sync
            def _(e):
                run("sp", e)

            @block.tensor
            def _(e):
                run("pe", e)

            @block.scalar
            def _(e):
                run("act", e)

            @block.vector
            def _(e):
                run("dve", e)

            @block.gpsimd
            def _(e):
                run("pool", e)


PC = {}
_o = 0
for _n, _w in (("LW", 32), ("LB", 8), ("BA", 8), ("BX", 8), ("LAM", 8), ("GL", 8), ("SW", 48),
               ("SB", 12), ("DS", 8), ("GS", 8), ("GM", 8), ("GP", 8)):
    PC[_n] = _o
    _o += _w
NPAR = _o
CI, CU, CN, CO, CUB, CNB, CBM, CBI = 0, 128, 256, 384, 512, 576, 640, 704
NCST = 720


def build_program(NT=SEQ // 128, SAMP=True, MLP=True, DBG=False, STAGE=9):
    nc = bass.Bass("TRN2", target_bir_lowering=False)
    S = Sched(nc)

    def din(name, shape):
        return nc.dram_tensor(name, list(shape), F32, kind="ExternalInput").ap()

    def dout(name, shape):
        return nc.dram_tensor(name, list(shape), F32, kind="ExternalOutput").ap()

    xp = din("xp", (SEQ, D))
    xs = din("xs", (TS, D))
    st_lc = din("st_lc", (NS * 3, D))
    st_lh = din("st_lh", (NS, D))
    st_sc = din("st_sc", (NS * 3, XBC))
    st_sh = din("st_sh", (NS, 1024, 128))
    w_in = din("w_in", (D, INP))
    w_out = din("w_out", (2 * D, D))
    w_up = din("w_up", (D, DFF))
    w_down = din("w_down", (DFF, D))
    w_a = din("w_a", (16, 64, 64))
    w_x = din("w_x", (16, 64, 64))
    pfm_d = din("pfm", (128, NPAR))
    cst_d = din("cst", (128, NCST))
    dtb_d = din("dt_bias", (16,))
    alog_d = din("a_log", (16,))
    gfin_d = din("g_final", (D,))

    y_p = dout("y_p", (SEQ, D))
    y_s = dout("y_s", (TS, D))
    o_plc = dout("o_plc", (3, D))
    o_plh = dout("o_plh", (8, 128))
    o_psc = dout("o_psc", (3, XBC))
    o_psh = dout("o_psh", (1024, 128))
    o_slc = dout("o_slc", (NS, 3, D))
    o_slh = dout("o_slh", (NS, D))
    o_ssc = dout("o_ssc", (NS, 3, XBC))
    o_ssh = dout("o_ssh", (NS, 1024, 128))
    scr = nc.dram_tensor("scr", [SEQ + TS, D], F32, kind=("ExternalOutput" if DBG else "Internal")).ap()

    st = ExitStack()
    with st:
        RW = 53200
        R = st.enter_context(nc.sbuf_tensor("R", [128, RW], F32))
        PS = st.enter_context(nc.psum_tensor("PS", [128, 4096], F32))
        ptr = [0]

        def alloc(nwords):
            a = ptr[0]
            ptr[0] += (nwords + 7) // 8 * 8
            pass
            return a

        def f32(n):
            a = alloc(n)
            return R[:, a:a + n]

        def bf(n):
            w = (n + 1) // 2
            a = alloc(w)
            return R[:, a:a + w].bitcast(BF16)[:, 0:n]

        def f3(c, t):
            return f32(c * t).rearrange("p (c t) -> p c t", c=c)

        def b3(c, t):
            return bf(c * t).rearrange("p (c t) -> p c t", c=c)

        def bank(b, n=512):
            return PS[:, 512 * b:512 * b + n]

        pT = PS[:, 0:1024]
        pTb = pT.bitcast(BF16)
        pA = [bank(2), bank(3)]
        pC = bank(4)
        pCb = pC.bitcast(BF16)
        pD = bank(5)
        pO = PS[:, 3072:4096]

        cst = f32(NCST)
        pfm = f32(NPAR)
        dtb_bc = f32(16)
        a_bc = f32(16)
        identb = bf(128)
        onesb = bf(128)
        Utrib = bf(128)
        mskb = bf(3 * TS)
        dah = bf(16)
        dal = bf(16)
        cfac = f32(8)
        c2fac = f32(8)
        tiny = f32(8)
        mhalf = f32(4)
        eps_t = f32(4)
        nbias = f32(16)
        wa_blk = b3(8, 128)
        wx_blk = b3(8, 128)
        hstate = f32(8)
        hT = f32(1024)
        hTb = bf(1024)

        ident = cst[:, CI:CI + 128]
        Utri = cst[:, CU:CU + 128]
        negm = cst[:, CN:CN + 128]
        onesf = cst[:, CO:CO + 128]
        Ublk = cst[0:TS, CUB:CUB + TS]
        negblk = cst[0:TS, CNB:CNB + TS]
        blkm = cst[0:TS, CBM:CBM + TS]
        blki = cst[0:TS, CBI:CBI + NS]

        S.dma(lambda e: e.dma_start(out=cst, in_=cst_d), "cst", w=["cst"])
        S.dma(lambda e: e.dma_start(out=pfm, in_=pfm_d), "pfm", w=["pfm"])
        S.dma(lambda e: e.dma_start(out=dtb_bc, in_=dtb_d.partition_broadcast(128)), "dtb", w=["dtb"])
        S.dma(lambda e: e.dma_start(out=a_bc, in_=alog_d.partition_broadcast(128)), "alog", w=["a_bc"])
        S.dve(lambda e: e.tensor_copy(out=identb, in_=ident), r=["cst"], w=["identb"])
        S.dve(lambda e: e.tensor_copy(out=onesb, in_=onesf), r=["cst"], w=["onesb"])
        S.dve(lambda e: e.tensor_copy(out=Utrib, in_=Utri), r=["cst"], w=["mskb"])
        S.dve(lambda e: e.tensor_copy(out=mskb[0:TS, 0:TS], in_=Ublk), r=["cst"], w=["mskb"])
        S.dve(lambda e: e.tensor_copy(out=mskb[0:TS, TS:2 * TS], in_=blkm), r=["cst"], w=["mskb"])
        S.pool(lambda e: e.memset(mhalf, -0.5), w=["mhalf"])
        S.pool(lambda e: e.memset(eps_t, EPS), w=["eps_t"])
        S.dve(lambda e: e.tensor_scalar(out=nbias[:, 0:8], in0=pfm[:, PC["BA"]:PC["BA"] + 8], scalar1=-1.0, scalar2=None, op0=ALU.mult), r=["pfm"], w=["nbias"])
        S.dve(lambda e: e.tensor_scalar(out=nbias[:, 8:16], in0=pfm[:, PC["BX"]:PC["BX"] + 8], scalar1=-1.0, scalar2=None, op0=ALU.mult), r=["pfm", "nbias"], w=["nbias"])
        S.pool(lambda e: e.memset(hstate, 0.0), w=["hstate"])
        S.pool(lambda e: e.memset(hT, 0.0), w=["hT"])
        S.pool(lambda e: e.memset(hTb, 0.0), w=["hTb"])
        S.pool(lambda e: e.memset(wa_blk, 0.0), w=["wa"])
        S.pool(lambda e: e.memset(wx_blk, 0.0), w=["wx"])
        S.act(lambda e: e.activation(out=a_bc, in_=a_bc, func=AF.Exp), r=["a_bc"], w=["a_bc"])
        S.dve(lambda e: e.tensor_scalar(out=a_bc, in0=a_bc, scalar1=-1.0, scalar2=None, op0=ALU.mult), r=["a_bc"], w=["a_bc"])
        lam = pfm[:, PC["LAM"]:PC["LAM"] + 8]
        S.act(lambda e: e.activation(out=tiny, in_=lam, func=AF.Exp, scale=-1.0), r=["pfm"], w=["tiny"])
        S.act(lambda e: e.activation(out=tiny, in_=tiny, func=AF.Ln, bias=1.0), r=["tiny"], w=["tiny"])
        S.dve(lambda e: e.tensor_scalar(out=cfac, in0=tiny, scalar1=-8.0, scalar2=None, op0=ALU.mult), r=["tiny"], w=["cfac"])
        S.dve(lambda e: e.tensor_scalar(out=c2fac, in0=tiny, scalar1=-16.0, scalar2=None, op0=ALU.mult), r=["tiny"], w=["cfac2"])
        for (wd, blk, nm) in ((w_a, wa_blk, "wa"), (w_x, wx_blk, "wx")):
            v = wd.rearrange("(c h) i j -> h i c j", h=2)
            for h2 in range(2):
                S.dma(lambda e, v=v, blk=blk, h2=h2: e.dma_start(
                    out=blk[64 * h2:64 * h2 + 64, :, 64 * h2:64 * h2 + 64], in_=v[h2]),
                    nm + str(h2), w=[nm], q="pool")

        base0 = ptr[0]

        def load_w(dst3, src2, nk, ncol, name, step=2048):
            sv = src2.rearrange("(k p) n -> p k n", p=128)
            pieces = [(k, c0, min(ncol, c0 + step)) for k in range(nk) for c0 in range(0, ncol, step)]
            for i, (k, c0, c1) in enumerate(pieces):
                S.dma(lambda e, k=k, c0=c0, c1=c1: e.dma_start(out=dst3[:, k, c0:c1], in_=sv[:, k, c0:c1]),
                      name, w=([name] if i == len(pieces) - 1 else []), q="pool")
            return name

        w_in_sb = b3(8, INP)
        w_out_sb = b3(16, D)
        if STAGE >= 1:
            k_win = load_w(w_in_sb, w_in, 8, INP, "w_in")
            k_wout = load_w(w_out_sb, w_out, 16, D, "w_out")

        xt = f32(D)
        xn = f32(D)
        junk = xn
        ss = f32(4)
        rstd = f32(4)
        hTt = b3(8, 128)
        lxb = f3(8, 131)
        xcb = f3(12, 131)
        sreg = f32(20 * NS * 7)
        lxs = sreg[:, 0:8 * NS * 7].rearrange("p (c s l) -> p c s l", c=8, s=NS)
        xcs = sreg[:, 8 * NS * 7:20 * NS * 7].rearrange("p (c s l) -> p c s l", c=12, s=NS)
        gl = f3(2, 128)
        zs = f3(8, 128)
        u = f3(8, 128)
        ub = b3(8, 128)
        lrut = f32(1280)
        gi = lrut[:, 0:512].rearrange("p (c t) -> p c t", c=4)
        av = lrut[:, 512:768].rearrange("p (c t) -> p c t", c=2)
        a2 = lrut[:, 768:1024].rearrange("p (c t) -> p c t", c=2)
        tmpb = lrut[:, 1024:1280].rearrange("p (c t) -> p c t", c=2)
        hs = f3(8, 128)
        ysq = b3(2, 128)
        rbc = f32(128)
        ynl = b3(8, 128)
        yns = b3(8, 128)
        xsf = f3(8, 128)
        Bb = b3(2, 128)
        Cb = b3(2, 128)
        dtr = f32(16)
        dtt = f32(16)
        da = f32(16)
        ncum = f32(16)
        dte = f32(16)
        cdec = f32(16)
        xdt = bf(1024)
        xdd = bf(1024)
        BT = bf(256)
        cbT = f3(2, 128)
        Dmf = f32(512)
        Emf = f32(512)
        Mmf = bf(512)
        Chf = bf(1024)
        Chp = Chf[:, 0:512].rearrange("p (a t) -> p a t", a=4)
        Chs = Chf.rearrange("p (a t) -> p a t", a=16)
        cvt = f3(2, 128)
        stg = f32(2560)
        stT = stg[:, 0:1024]
        lc_in = stg[:, 0:1024]
        sc_in = stg[:, 1024:2560]
        lh_in = stg[:, 0:1024]
        h0in = lxb.rearrange("p c t -> p (c t)")[:, 0:1024].rearrange("p (c t) -> p c t", c=8)
        h0Tb = hTb
        Bm = bf(256)
        pyo_f = f32(8 * TS)
        pyo_sb = pyo_f.rearrange("p (c t) -> p c t", c=8)
        damb = bf(2 * NS * 16).rearrange("p (i s h) -> p i s h", i=2, s=NS)
        dtot = f3(NS, 16)
        hnew = hT
        hout = xcb.rearrange("p c t -> p (c t)")[:, 0:1024].rearrange("p (c t) -> p c t", c=8)
        h0s = f3(8, NS)
        hfin = f3(8, NS)

        bf3 = lambda ap, c: ap.bitcast(BF16).rearrange("p (c t) -> p c t", c=c)
        xt_b = [xt, stg[:, 0:1024]]
        ynl_b = [ynl, bf3(stg[:, 1024:1536], 8)]
        Bb_b = [Bb, bf3(stg[:, 1536:1664], 2)]
        Cb_b = [Cb, bf3(stg[:, 1664:1792], 2)]
        dtt_b = [dtt, stg[:, 1792:1808]]
        xsf_b = [xsf, sreg[:, 0:1024].rearrange("p (c t) -> p c t", c=8)]
        zs_b = [zs, sreg[:, 1024:2048].rearrange("p (c t) -> p c t", c=8)]
        Wl_S = pyo_f[:, 0:256].bitcast(BF16)
        ysq_S = bf3(pyo_f[:, 256:384], 2)
        rbc_S = pyo_f[:, 384:512]
        STGW = [k + "#1" for k in ["xt", "dtt"] + ["ynl%d" % c for c in range(8)] + ["Bb0", "Bb1", "Cb0", "Cb1"]]
        STG_ALIAS = ["xt", "dtt"] + ["ynl%d" % c for c in range(8)] + ["Bb0", "Bb1", "Cb0", "Cb1"]

        def P(name, c=None, w=1):
            o = PC[name] + (0 if c is None else c * w)
            return pfm[:, o:o + w]

        def rms_rstd(xtile, T, keyx, junk, ss, rstd, sfx="", jkey=None):
            S.act(lambda e: e.activation(out=junk[0:T, :], in_=xtile[0:T, :], func=AF.Square, accum_out=ss[0:T, 0:1]),
                  r=[keyx], w=[jkey or ("xn" + sfx), "ss" + sfx])
            S.act(lambda e: e.activation(out=ss[0:T, 0:1], in_=ss[0:T, 0:1], func=AF.Ln, scale=1.0 / D, bias=eps_t[0:T, 0:1]),
                  r=["ss" + sfx, "eps_t"], w=["ss" + sfx])
            S.act(lambda e: e.activation(out=rstd[0:T, 0:1], in_=ss[0:T, 0:1], func=AF.Exp, scale=-0.5),
                  r=["ss" + sfx], w=["rstd" + sfx])

        pAA = PS[:, 1024:2048]

        def to_fm(T, gname, dst, dkey):
            for k in range(8):
                S.pe(lambda e, k=k: e.transpose(out=pAA[:, k * 128:k * 128 + T], in_=xn[0:T, k * 128:(k + 1) * 128],
                                                identity=ident[0:T, 0:T]), r=["xn", "cst"], w=["pA0", "pA1"])
            S.dve(lambda e: e.tensor_tensor(
                out=dst[:, :, 0:T], in0=pAA.rearrange("p (k t) -> p k t", k=8)[:, :, 0:T],
                in1=P(gname, 0, 8).unsqueeze(2).to_broadcast([128, 8, T]), op=ALU.mult),
                r=["pA0", "pA1", "pfm"], w=[dkey])

        def mixer_tile(mt, samp):
            T = TS if samp else 128
            row0 = SEQ if samp else mt * 128
            xsrc = xs if samp else xp[mt * 128:(mt + 1) * 128, :]
            last = (not samp) and mt == NT - 1
            par = 0 if samp else (NT - 1 - mt) % 2
            xt, ynl, Bb, Cb, dtt, xsf, zs = (xt_b[par], ynl_b[par], Bb_b[par], Cb_b[par], dtt_b[par], xsf_b[par], zs_b[par])
            Wl = cvt.rearrange("p a t -> p (a t)").bitcast(BF16) if samp else Wl_S
            wlk = ["cv_t0", "cv_t1"] if samp else ["WlS"]
            ysqS = ysq if samp else ysq_S
            rbcS = rbc if samp else rbc_S
            sk = "" if samp else "S"

            def inter(*gens):
                gens = list(gens)
                while gens:
                    for g_ in list(gens):
                        try:
                            next(g_)
                        except StopIteration:
                            gens.remove(g_)
                        yield

            pcnt = [0]

            def proj(ci):
                i = pcnt[0] % 2
                pcnt[0] += 1
                pa = pA[i]
                for k in range(8):
                    S.pe(lambda e, k=k: e.matmul(pa[:, 0:T], lhsT=w_in_sb[:, k, ci * 128:(ci + 1) * 128],
                                                 rhs=hTt[:, k, 0:T], start=(k == 0), stop=(k == 7)),
                         r=["hTt", "w_in"], w=["pA%d" % i])
                return pa, "pA%d" % i

            def new_cols(buf, sbuf_, c):
                if samp:
                    return sbuf_[:, c, :, 3:7]
                return buf[:, c, 3:131]

            def pa_view(pa):
                if samp:
                    return pa[:, 0:T].rearrange("p (s l) -> p s l", s=NS)
                return pa[:, 0:T]

            def tap(buf, sbuf_, c, k):
                if samp:
                    return sbuf_[:, c, :, k:k + 4]
                return buf[:, c, k:k + 128]

            def fm(t3, c):
                if samp:
                    return t3[:, c, 0:T].rearrange("p (s l) -> p s l", s=NS)
                return t3[:, c, 0:T]

            def conv(buf, sbuf_, c, wname, bname, out_ap, key_in, key_out):
                S.dve(lambda e: e.tensor_scalar(out=out_ap, in0=tap(buf, sbuf_, c, 3), scalar1=P(wname, c, 4)[:, 3:4],
                                                scalar2=P(bname, c), op0=ALU.mult, op1=ALU.add),
                      r=[key_in, "pfm"], w=[key_out])
                for k in (2, 1, 0):
                    S.dve(lambda e, k=k: e.scalar_tensor_tensor(out=out_ap, in0=tap(buf, sbuf_, c, k),
                                                                scalar=P(wname, c, 4)[:, k:k + 1], in1=out_ap,
                                                                op0=ALU.mult, op1=ALU.add),
                          r=[key_in, key_out, "pfm"], w=[key_out])
                if not samp:
                    S.dve(lambda e: e.tensor_copy(out=buf[:, c, 0:3], in_=buf[:, c, 128:131]), r=[key_in], w=[key_in])

            def g_lrux():
                for c in range(8):
                    pa, pk = proj(c)
                    S.act(lambda e, c=c, pa=pa: e.activation(out=new_cols(lxb, lxs, c), in_=pa_view(pa), func=AF.Copy),
                          r=[pk], w=["lx%d" % c])
                    conv(lxb, lxs, c, "LW", "LB", fm(u, c), "lx%d" % c, "u%d" % c)
                    yield

            def g_z():
                for c in range(8):
                    pa, pk = proj(16 + c)
                    S.act(lambda e, c=c, pa=pa: e.activation(out=zs[:, c, 0:T], in_=pa[:, 0:T], func=AF.Silu),
                          r=[pk], w=["zs%d" % c])
                    yield

            def g_xbc():
                for c in range(12):
                    pa, pk = proj(24 + c)
                    S.act(lambda e, c=c, pa=pa: e.activation(out=new_cols(xcb, xcs, c), in_=pa_view(pa), func=AF.Copy),
                          r=[pk], w=["xc%d" % c])
                    if c < 8:
                        conv(xcb, xcs, c, "SW", "SB", fm(cvt, c % 2), "xc%d" % c, "cv_t%d" % (c % 2))
                        S.act(lambda e, c=c: e.activation(out=xsf[:, c, 0:T], in_=cvt[:, c % 2, 0:T], func=AF.Silu),
                              r=["cv_t%d" % (c % 2)], w=["xsf%d" % c])
                    else:
                        g = (c - 8) % 2
                        dstb = Bb if c < 10 else Cb
                        nm = ("Bb%d" if c < 10 else "Cb%d") % g
                        conv(xcb, xcs, c, "SW", "SB", fm(cvt, g), "xc%d" % c, "cv_t%d" % g)
                        S.act(lambda e, g=g, dstb=dstb: e.activation(out=dstb[:, g, 0:T], in_=cvt[:, g, 0:T], func=AF.Silu),
                              r=["cv_t%d" % g], w=[nm])
                    yield
                for k in range(8):
                    S.pe(lambda e, k=k: e.matmul(pD[0:T, 0:16], lhsT=hTt[:, k, 0:T], rhs=w_in_sb[:, k, 4608:4624],
                                                 start=(k == 0), stop=(k == 7)),
                         r=["hTt", "w_in"], w=["pD"])
                S.dve(lambda e: e.tensor_tensor(out=dtr[0:T, :], in0=pD[0:T, 0:16], in1=dtb_bc[0:T, :], op=ALU.add),
                      r=["pD", "dtb"], w=["dtr"])
                S.act(lambda e: e.activation(out=dtr[0:T, :], in_=dtr[0:T, :], func=AF.Exp), r=["dtr"], w=["dtr"])
                S.act(lambda e: e.activation(out=dtt[0:T, :], in_=dtr[0:T, :], func=AF.Ln, bias=1.0), r=["dtr"], w=["dtt"])
                yield

            def g_lru(chunks, pg, kr, ki):
                for c in chunks:
                    pp = c % 2
                    S.pool(lambda e, c=c: e.tensor_copy(out=ub[:, c, 0:T], in_=u[:, c, 0:T]),
                           r=["u%d" % c], w=["ub%d" % c])
                    yield
                    S.pe(lambda e, c=c: e.matmul(pg[:, 0:T], lhsT=wa_blk[:, c, :], rhs=ub[:, c, 0:T], start=True, stop=True),
                         r=["ub%d" % c, "wa"], w=[kr])
                    S.pe(lambda e, c=c: e.matmul(pg[:, 128:128 + T], lhsT=wx_blk[:, c, :], rhs=ub[:, c, 0:T], start=True, stop=True),
                         r=["ub%d" % c, "wx"], w=[ki])
                    yield
                    S.act(lambda e, c=c, pp=pp: e.activation(out=gi[:, 2 * pp, 0:T], in_=pg[:, 0:T], func=AF.Exp, scale=-1.0, bias=nbias[:, c:c + 1]),
                          r=[kr, "nbias"], w=["rg%d" % pp])
                    S.act(lambda e, c=c, pp=pp: e.activation(out=gi[:, 2 * pp + 1, 0:T], in_=pg[:, 128:128 + T], func=AF.Exp, scale=-1.0, bias=nbias[:, 8 + c:9 + c]),
                          r=[ki, "nbias"], w=["ig%d" % pp])
                    S.act(lambda e, pp=pp: e.activation(out=gi[:, 2 * pp:2 * pp + 2, 0:T], in_=gi[:, 2 * pp:2 * pp + 2, 0:T], func=AF.Ln, bias=1.0),
                          r=["rg%d" % pp, "ig%d" % pp], w=["rg%d" % pp, "ig%d" % pp])
                    S.act(lambda e, pp=pp: e.activation(out=gi[:, 2 * pp:2 * pp + 2, 0:T], in_=gi[:, 2 * pp:2 * pp + 2, 0:T], func=AF.Exp, scale=-1.0),
                          r=["rg%d" % pp, "ig%d" % pp], w=["rg%d" % pp, "ig%d" % pp])
                    S.act(lambda e, c=c, pp=pp: e.activation(out=av[:, pp, 0:T], in_=gi[:, 2 * pp, 0:T], func=AF.Exp, scale=cfac[:, c:c + 1]),
                          r=["rg%d" % pp, "cfac"], w=["av%d" % pp])
                    S.act(lambda e, c=c, pp=pp: e.activation(out=a2[:, pp, 0:T], in_=gi[:, 2 * pp, 0:T], func=AF.Exp, scale=c2fac[:, c:c + 1]),
                          r=["rg%d" % pp, "cfac2"], w=["a2%d" % pp])
                    S.act(lambda e, pp=pp: e.activation(out=a2[:, pp, 0:T], in_=a2[:, pp, 0:T], func=AF.Ln, scale=-1.0, bias=1.0),
                          r=["a2%d" % pp], w=["a2%d" % pp])
                    S.act(lambda e, pp=pp: e.activation(out=a2[:, pp, 0:T], in_=a2[:, pp, 0:T], func=AF.Exp, scale=0.5),
                          r=["a2%d" % pp], w=["a2%d" % pp])
                    yield
                    S.dve(lambda e, c=c, pp=pp: e.tensor_tensor(out=tmpb[:, pp, 0:T], in0=gi[:, 2 * pp + 1, 0:T], in1=u[:, c, 0:T], op=ALU.mult),
                          r=["ig%d" % pp, "u%d" % c], w=["tb%d" % pp])
                    S.dve(lambda e, pp=pp: e.tensor_tensor(out=tmpb[:, pp, 0:T], in0=tmpb[:, pp, 0:T], in1=a2[:, pp, 0:T], op=ALU.mult),
                          r=["tb%d" % pp, "a2%d" % pp], w=["tb%d" % pp])
                    if samp:
                        a3 = av[:, pp, 0:T].rearrange("p (s l) -> p s l", s=NS)
                        b3v = tmpb[:, pp, 0:T].rearrange("p (s l) -> p s l", s=NS)
                        S.dve(lambda e, c=c, a3=a3: e.tensor_tensor(out=rbc[:, 0:NS], in0=a3[:, :, 0], in1=h0s[:, c, :], op=ALU.mult),
                              r=["av%d" % pp, "h0s"], w=["rbc"])
                        S.dve(lambda e, b3v=b3v: e.tensor_tensor(out=b3v[:, :, 0], in0=b3v[:, :, 0], in1=rbc[:, 0:NS], op=ALU.add),
                              r=["tb%d" % pp, "rbc"], w=["tb%d" % pp])
                        S.dve(lambda e, a3=a3: e.memset(a3[:, :, 0], 0.0), r=["rbc"], w=["av%d" % pp])
                        S.dve(lambda e, c=c, pp=pp: e.tensor_tensor_scan(out=hs[:, c, 0:T], data0=av[:, pp, 0:T], data1=tmpb[:, pp, 0:T],
                                                                         initial=0.0, op0=ALU.mult, op1=ALU.add),
                              r=["av%d" % pp, "tb%d" % pp], w=["hs%d" % c])
                        S.dve(lambda e, c=c: e.tensor_copy(out=hfin[:, c, :], in_=hs[:, c, 0:T].rearrange("p (s l) -> p s l", s=NS)[:, :, 3]),
                              r=["hs%d" % c], w=["hfin"])
                    else:
                        S.dve(lambda e, c=c, pp=pp: e.tensor_tensor_scan(out=hs[:, c, 0:T], data0=av[:, pp, 0:T], data1=tmpb[:, pp, 0:T],
                                                                         initial=hstate[:, c:c + 1], op0=ALU.mult, op1=ALU.add),
                              r=["av%d" % pp, "tb%d" % pp, "hstate"], w=["hs%d" % c])
                        S.dve(lambda e, c=c: e.tensor_copy(out=hstate[:, c:c + 1], in_=hs[:, c, T - 1:T]),
                              r=["hs%d" % c], w=["hstate"])
                    yield

            def g_gate():
                for c in range(8):
                    pp = c % 2
                    pa, pk = proj(8 + c)
                    S.act(lambda e, pp=pp, pa=pa: e.activation(out=gl[:, pp, 0:T], in_=pa[:, 0:T], func=AF.Gelu_apprx_tanh),
                          r=[pk], w=["gl%d" % pp])
                    S.dve(lambda e, c=c, pp=pp: e.tensor_tensor(out=hs[:, c, 0:T], in0=hs[:, c, 0:T], in1=gl[:, pp, 0:T], op=ALU.mult),
                          r=["hs%d" % c, "gl%d" % pp], w=["yl%d" % c, "hs%d" % c])
                    S.pool(lambda e, c=c, pp=pp: e.tensor_tensor(out=ysq[:, pp, 0:T], in0=hs[:, c, 0:T], in1=hs[:, c, 0:T], op=ALU.mult),
                           r=["yl%d" % c], w=["ysq%d" % pp])
                    S.pe(lambda e, c=c, pp=pp: e.matmul(pD[:, 128:128 + T], lhsT=onesb, rhs=ysq[:, pp, 0:T], start=(c == 0), stop=(c == 7)),
                         r=["ysq%d" % pp, "onesb"], w=["pDn"])
                    yield

            def norm_apply(T, eps_, src, skey, gname, dst, dkey, c0, c1, rbc, rk, pst, pk):
                S.act(lambda e: e.activation(out=rbc[:, 0:T], in_=pst, func=AF.Ln,
                                             scale=1.0 / ((c1 - c0) * 128), bias=eps_t[:, 0:1]),
                      r=[pk, "eps_t"], w=[rk])
                S.act(lambda e: e.activation(out=rbc[:, 0:T], in_=rbc[:, 0:T], func=AF.Exp, scale=-0.5), r=[rk], w=[rk])
                for c in range(c0, c1):
                    S.dve(lambda e, c=c: e.scalar_tensor_tensor(out=dst[:, c, 0:T], in0=src[:, c, 0:T], scalar=P(gname, c),
                                                                in1=rbc[:, 0:T], op0=ALU.mult, op1=ALU.mult),
                          r=[skey % c, rk, "pfm"], w=[dkey % c])

            Um = mskb[0:TS, 0:TS] if samp else Utrib
            ngm = negblk if samp else negm
            allm = mskb[0:TS, TS:2 * TS] if samp else onesb
            d4 = lambda ap: ap[:, 0:4 * T].rearrange("p (a t) -> p a t", a=4)
            Em, Dm, Mm, pC4 = d4(Emf), d4(Dmf), d4(Mmf), d4(pC)

            def g_ssd():
                for c in range(8):
                    S.pe(lambda e, c=c: e.transpose(out=pT[0:T, c * 128:(c + 1) * 128], in_=xsf[:, c, 0:T], identity=ident),
                         r=["xsf%d" % c, "cst"], w=["pT"])
                for g in range(2):
                    S.pe(lambda e, g=g: e.transpose(out=pCb[0:T, 128 + g * 128:128 + (g + 1) * 128], in_=Bb[:, g, 0:T], identity=identb),
                         r=["Bb%d" % g, "identb"], w=["pC", "pCx"])
                S.dve(lambda e: e.tensor_tensor(out=xdt[0:T, :].rearrange("p (h q) -> p h q", h=16),
                                                in0=pT[0:T, :].rearrange("p (h q) -> p h q", h=16),
                                                in1=dtt[0:T, :].unsqueeze(2).to_broadcast([T, 16, 64]), op=ALU.mult),
                      r=["pT", "dtt"], w=["xdt"])
                S.dve(lambda e: e.tensor_copy(out=BT[0:T, :], in_=pCb[0:T, 128:384]), r=["pC"], w=["BT"])
                S.dve(lambda e: e.tensor_tensor(out=da[0:T, :], in0=dtt[0:T, :], in1=a_bc[0:T, :], op=ALU.mult),
                      r=["dtt", "a_bc"], w=["da"])
                yield
                S.dve(lambda e: e.tensor_copy(out=dah[0:T, :], in_=da[0:T, :]), r=["da"], w=["dah"])
                S.dve(lambda e: e.tensor_tensor(out=dal[0:T, :], in0=da[0:T, :], in1=dah[0:T, :], op=ALU.subtract),
                      r=["da", "dah"], w=["dal"])
                for i, dx in enumerate((dah, dal)):
                    S.pe(lambda e, dx=dx, i=i: e.matmul(pC[0:T, 0:16], lhsT=Um[0:T, 0:T], rhs=dx[0:T, :], start=(i == 0), stop=(i == 1)),
                         r=["dah", "dal", "mskb"], w=["pC", "pCx"])
                for i, dx in enumerate((dah, dal)):
                    S.pe(lambda e, dx=dx, i=i: e.matmul(pC[0:T, 16:32], lhsT=allm[0:T, 0:T], rhs=dx[0:T, :], start=(i == 0), stop=(i == 1)),
                         r=["dah", "dal", "mskb", "onesb"], w=["pC", "pCx"])
                if not samp:
                    for i, dx in enumerate((dah, dal)):
                        S.pe(lambda e, dx=dx, i=i: e.matmul(pC[:, 32:48], lhsT=onesb, rhs=dx, start=(i == 0), stop=(i == 1)),
                             r=["dah", "dal", "onesb"], w=["pC", "pCx"])
                for g in range(2):
                    S.pe(lambda e, g=g: e.matmul(pC[0:T, 256 + g * 128:256 + g * 128 + T], lhsT=Bb[:, g, 0:T], rhs=Cb[:, g, 0:T],
                                                 start=True, stop=True), r=["Bb%d" % g, "Cb%d" % g], w=["pC", "pCx"])
                yield
                S.dve(lambda e: e.tensor_scalar(out=ncum[0:T, :], in0=pC[0:T, 0:16], scalar1=-1.0, scalar2=None, op0=ALU.mult),
                      r=["pC"], w=["ncum"])
                S.dve(lambda e: e.tensor_tensor(out=dte[0:T, :], in0=pC[0:T, 16:32], in1=ncum[0:T, :], op=ALU.add),
                      r=["pC", "ncum"], w=["dte"])
                if not samp:
                    S.dve(lambda e: e.tensor_copy(out=cdec, in_=pC[:, 32:48]), r=["pC"], w=["cdec"])
                S.dve(lambda e: e.tensor_copy(out=cbT[0:T, :, 0:T], in_=pC[0:T, 256:512].rearrange("p (g t) -> p g t", g=2)[:, :, 0:T]),
                      r=["pC"], w=["cbT0", "cbT1"])
                S.act(lambda e: e.activation(out=dte[0:T, :], in_=dte[0:T, :], func=AF.Exp), r=["dte"], w=["dte"])
                if not samp:
                    S.act(lambda e: e.activation(out=cdec, in_=cdec, func=AF.Exp), r=["cdec"], w=["cdec"])
                S.dve(lambda e: e.tensor_tensor(out=xdd[0:T, :].rearrange("p (h q) -> p h q", h=16),
                                                in0=xdt[0:T, :].rearrange("p (h q) -> p h q", h=16),
                                                in1=dte[0:T, :].unsqueeze(2).to_broadcast([T, 16, 64]), op=ALU.mult),
                      r=["xdt", "dte"], w=["xdd"])
                yield
                if not samp:
                    for g in range(2):
                        S.pe(lambda e, g=g: e.matmul(pO[:, g * 512:(g + 1) * 512], lhsT=BT[:, g * 128:(g + 1) * 128],
                                                     rhs=xdd[:, g * 512:(g + 1) * 512], start=True, stop=True),
                             r=["BT", "xdd"], w=["pO"])
                    S.dve(lambda e: e.tensor_tensor(out=hT.rearrange("p (h q) -> p h q", h=16),
                                                    in0=hT.rearrange("p (h q) -> p h q", h=16),
                                                    in1=cdec.unsqueeze(2).to_broadcast([128, 16, 64]), op=ALU.mult),
                          r=["hT", "cdec"], w=["hT"])
                    S.dve(lambda e: e.tensor_tensor(out=hT, in0=hT, in1=pO, op=ALU.add), r=["hT", "pO"], w=["hT"])
                    yield
                for q4 in range(4):
                    g = q4 // 2
                    for i, (dx, Wf, wk) in enumerate(((dah, Mmf, ["Mm"]), (dal, Wl, wlk))):
                        S.pool(lambda e, q4=q4, dx=dx, Wf=Wf: e.tensor_tensor(out=d4(Wf)[0:T], in0=Um[0:T, 0:T].unsqueeze(1).to_broadcast([T, 4, T]),
                                                                             in1=dx[0:T, q4 * 4:q4 * 4 + 4].unsqueeze(2).to_broadcast([T, 4, T]),
                                                                             op=ALU.mult), r=["dah", "dal", "mskb"], w=wk)
                        S.pe(lambda e, Wf=Wf, i=i: e.matmul(pC[:, 0:4 * T], lhsT=onesb[0:T, :], rhs=Wf[0:T, 0:4 * T],
                                                            start=(i == 0), stop=(i == 1)), r=wk + ["onesb"], w=["pC", "pCx"])
                    S.dve(lambda e: e.tensor_copy(out=Em, in_=pC4), r=["pC"], w=["Em"])
                    yield
                    for hh in range(4):
                        h = q4 * 4 + hh
                        S.dve(lambda e, h=h, hh=hh: e.scalar_tensor_tensor(out=Dm[0:T, hh, :], in0=Em[0:T, hh, :],
                                                                           scalar=ncum[0:T, h:h + 1], in1=ngm[0:T, 0:T],
                                                                           op0=ALU.add, op1=ALU.add),
                              r=["Em", "ncum", "cst"], w=["Dm"])
                    S.act(lambda e: e.activation(out=Em, in_=Em, func=AF.Exp), r=["Em"], w=["Em"])
                    S.act(lambda e: e.activation(out=Dm[0:T], in_=Dm[0:T], func=AF.Exp), r=["Dm"], w=["Dm"])
                    S.pool(lambda e, g=g: e.tensor_tensor(out=Mm[0:T], in0=Dm[0:T],
                                                         in1=cbT[0:T, g, 0:T].unsqueeze(1).to_broadcast([T, 4, T]), op=ALU.mult),
                          r=["Dm", "cbT%d" % g], w=["Mm"])
                    S.pool(lambda e, g=g, q4=q4: e.tensor_tensor(out=(Chs[:, q4 * 4:q4 * 4 + 4, :] if samp else Chp), in0=Em,
                                                                in1=Cb[:, g, 0:T].unsqueeze(1).to_broadcast([128, 4, T]), op=ALU.mult),
                          r=["Em", "Cb%d" % g], w=["Ch"])
                    yield
                    for hh in range(4):
                        h = q4 * 4 + hh
                        c = h // 2
                        h2 = h % 2
                        po = pT[64 * h2:64 * h2 + 64, c * 128:c * 128 + T]
                        S.pe(lambda e, h=h, hh=hh, po=po: e.matmul(po, lhsT=xdt[0:T, h * 64:(h + 1) * 64], rhs=Mm[0:T, hh, :],
                                                                   start=True, stop=samp), r=["xdt", "Mm"], w=["pT"])
                        if not samp:
                            S.pe(lambda e, h=h, hh=hh, po=po: e.matmul(po, lhsT=hTb[:, h * 64:(h + 1) * 64], rhs=Chp[:, hh, :],
                                                                       start=False, stop=True), r=["hTb", "Ch"], w=["pT"])
                    yield

            def late_outputs():
                M = T if samp else 3
                t0 = 0 if samp else 125
                if samp or last:
                    for blk, col0 in enumerate((0, 512, 3072, 3584, 4096)):
                        for k in range(8):
                            S.pe(lambda e, k=k, col0=col0: e.matmul(pO[0:M, 0:512], lhsT=hTt[:, k, t0:t0 + M],
                                                                    rhs=w_in_sb[:, k, col0:col0 + 512], start=(k == 0), stop=(k == 7)),
                                 r=["hTt", "w_in"], w=["pO"])
                        S.dve(lambda e, blk=blk: e.tensor_copy(out=stg[0:M, blk * 512:(blk + 1) * 512], in_=pO[0:M, 0:512]),
                              r=["pO"], w=["stg"] + STGW)
                if last:
                    S.dma(lambda e: e.dma_start(out=o_plc, in_=stg[0:3, 0:1024]), "o_plc", r=["stg"])
                    S.dma(lambda e: e.dma_start(out=o_psc, in_=stg[0:3, 1024:2560]), "o_psc", r=["stg"])
                if samp:
                    for s in range(NS):
                        S.dma(lambda e, s=s: e.dma_start(out=o_slc[s], in_=stg[4 * s + 1:4 * s + 4, 0:1024]), "o_slc", r=["stg"])
                        S.dma(lambda e, s=s: e.dma_start(out=o_ssc[s], in_=stg[4 * s + 1:4 * s + 4, 1024:2560]), "o_ssc", r=["stg"])
                if last:
                    S.pe(lambda e: e.transpose(out=pC[0:8, 0:128], in_=hstate, identity=ident), r=["hstate", "cst"], w=["pC", "pCx"])
                    S.act(lambda e: e.activation(out=stT[0:8, 0:128], in_=pC[0:8, 0:128], func=AF.Copy), r=["pC"], w=["stg"] + STGW)
                    S.dma(lambda e: e.dma_start(out=o_plh, in_=stT[0:8, 0:128]), "o_plh", r=["stg"])
                if samp:
                    for c in range(8):
                        S.pe(lambda e, c=c: e.transpose(out=pT[0:NS, c * 128:(c + 1) * 128], in_=hfin[:, c, :], identity=ident),
                             r=["hfin", "cst"], w=["pT"])
                    S.act(lambda e: e.activation(out=lh_in[0:NS, :], in_=pT[0:NS, :], func=AF.Copy), r=["pT"], w=["stg"] + STGW)
                    S.dma(lambda e: e.dma_start(out=o_slh, in_=lh_in[0:NS, :]), "o_slh", r=["stg"])


            def genP():
                S.dma(lambda e: e.dma_start(out=xt[0:T, :], in_=xsrc), "xt", w=["xt"])
                rms_rstd(xt, T, "xt", junk, ss, rstd)
                S.act(lambda e: e.activation(out=xn[0:T, :], in_=xt[0:T, :], func=AF.Copy, scale=rstd[0:T, 0:1]),
                      r=["xt", "rstd"], w=["xn"])
                to_fm(T, "GM", hTt, "hTt")

                if samp:
                    S.dma(lambda e: e.dma_start(out=lc_in[0:48, :], in_=st_lc), "stg", w=["stg"])
                    S.dma(lambda e: e.dma_start(out=sc_in[0:48, :], in_=st_sc), "stg", w=["stg"])
                    S.dma(lambda e: e.dma_start(out=lh_in[64:64 + NS, :], in_=st_lh), "stg", w=["stg"])
                    for c in range(8):
                        S.pe(lambda e, c=c: e.transpose(out=pC[:, 0:48], in_=lc_in[0:48, c * 128:(c + 1) * 128],
                                                        identity=ident[0:48, 0:48]), r=["stg", "cst"], w=["pC"])
                        S.act(lambda e, c=c: e.activation(out=lxs[:, c, :, 0:3],
                                                          in_=pC[:, 0:48].rearrange("p (s j) -> p s j", s=NS),
                                                          func=AF.Copy), r=["pC"], w=["lx%d" % c])
                        S.pe(lambda e, c=c: e.transpose(out=pD[:, 0:NS], in_=lh_in[64:64 + NS, c * 128:(c + 1) * 128],
                                                        identity=ident[64:64 + NS, 64:64 + NS]), r=["stg", "cst"], w=["pD"])
                        S.dve(lambda e, c=c: e.tensor_copy(out=h0s[:, c, :], in_=pD[:, 0:NS]), r=["pD"], w=["h0s"])
                    for c in range(12):
                        S.pe(lambda e, c=c: e.transpose(out=pC[:, 0:48], in_=sc_in[0:48, c * 128:(c + 1) * 128],
                                                        identity=ident[0:48, 0:48]), r=["stg", "cst"], w=["pC"])
                        S.act(lambda e, c=c: e.activation(out=xcs[:, c, :, 0:3],
                                                          in_=pC[:, 0:48].rearrange("p (s j) -> p s j", s=NS),
                                                          func=AF.Copy), r=["pC"], w=["xc%d" % c])

                yield
                yield from inter(g_lrux(), g_z())
                yield from inter(g_xbc())
                yield from inter(g_lru((0, 2, 4, 6), pA[0], "pA0", "pA0"), g_lru((1, 3, 5, 7), pA[1], "pA1", "pA1"))
                yield from inter(g_gate())
                norm_apply(T, EPS, hs, "yl%d", "GL", ynl, "ynl%d", 0, 8, rbc, "rbc", pD[:, 128:128 + T], "pDn")
                yield
                if samp:
                    late_outputs()

            def genS():
                yield from inter(g_ssd())
                if samp:
                    ssd_sample_states_prep()

                for c in range(8):
                    S.dve(lambda e, c=c: e.scalar_tensor_tensor(out=xsf[:, c, 0:T], in0=xsf[:, c, 0:T], scalar=P("DS", c),
                                                                in1=pT[:, c * 128:c * 128 + T], op0=ALU.mult, op1=ALU.add),
                          r=["pT", "xsf%d" % c, "pfm"], w=["xsf%d" % c])
                if samp:
                    S.dve(lambda e: e.tensor_tensor(out=xsf[:, :, 0:T], in0=xsf[:, :, 0:T], in1=pyo_sb, op=ALU.add),
                          r=["xsf%d" % c for c in range(8)] + ["pyo_sb"], w=["xsf%d" % c for c in range(8)])
                S.pool(lambda e: e.tensor_tensor(out=xsf[:, :, 0:T], in0=xsf[:, :, 0:T], in1=zs[:, :, 0:T], op=ALU.mult),
                       r=["xsf%d" % c for c in range(8)] + ["zs%d" % c for c in range(8)], w=["yg%d" % c for c in range(8)] + ["xsf%d" % c for c in range(8)])
                yield
                if not samp:
                    S.act(lambda e: e.activation(out=hTb, in_=hT, func=AF.Copy), r=["hT"], w=["hTb"])
                    if last:
                        for c in range(8):
                            S.pe(lambda e, c=c: e.transpose(out=pO[:, c * 128:(c + 1) * 128], in_=hT[:, c * 128:(c + 1) * 128], identity=ident),
                                 r=["hT", "cst"], w=["pO"])
                        S.dve(lambda e: e.tensor_copy(out=stT, in_=pO), r=["pO"], w=["stg"] + STGW)
                        S.dma(lambda e: e.dma_start(out=o_psh.rearrange("(c q) n -> q c n", q=128),
                                                    in_=stT.rearrange("p (c n) -> p c n", c=8)), "o_psh", r=["stg"])
                yield
                for g in range(2):
                    for c in range(4 * g, 4 * g + 4):
                        pp = c % 2
                        S.pool(lambda e, c=c, pp=pp: e.tensor_tensor(out=ysqS[:, pp, 0:T], in0=xsf[:, c, 0:T], in1=xsf[:, c, 0:T], op=ALU.mult),
                               r=["yg%d" % c], w=["ysq%s%d" % (sk, pp)])
                        S.pe(lambda e, c=c, pp=pp, g=g: e.matmul(pO[:, 0:T], lhsT=onesb, rhs=ysqS[:, pp, 0:T],
                                                                 start=(c == 4 * g), stop=(c == 4 * g + 3)),
                             r=["ysq%s%d" % (sk, pp), "onesb"], w=["pO"])
                    norm_apply(T, EPS, xsf, "yg%d", "GS", yns, "yns%d", 4 * g, 4 * g + 4, rbcS, "rbc" + sk, pO[:, 0:T], "pO")

                yield
                for nb in range(2):
                    for kc in range(16):
                        src = ynl if kc < 8 else yns
                        S.pe(lambda e, kc=kc, nb=nb, src=src: e.matmul(pO[0:T, nb * 512:(nb + 1) * 512], lhsT=src[:, kc % 8, 0:T],
                                                                       rhs=w_out_sb[:, kc, nb * 512:(nb + 1) * 512],
                                                                       start=(kc == 0), stop=(kc == 15)),
                             r=[("ynl%d" if kc < 8 else "yns%d") % (kc % 8), "w_out"], w=["pO"])
                S.dve(lambda e: e.tensor_tensor(out=xt[0:T, :], in0=pO[0:T, :], in1=xt[0:T, :], op=ALU.add),
                      r=["pO", "xt"], w=["xt"])
                S.dma(lambda e: e.dma_start(out=scr[row0:row0 + T, :], in_=xt[0:T, :]), "xnew", r=["xt"], w=["scr%d" % mt])

                if last:
                    late_outputs()

            return par, genP, genS

        def ssd_sample_states_prep():
            T = TS
            for i, dx in enumerate((dah, dal)):
                S.dve(lambda e, dx=dx, i=i: e.tensor_tensor(out=damb[0:T, i], in0=dx[0:T, :].unsqueeze(1).to_broadcast([T, NS, 16]),
                                                            in1=blki.unsqueeze(2).to_broadcast([T, NS, 16]), op=ALU.mult),
                      r=["dah", "dal", "cst"], w=["dam%d" % i])
                S.pe(lambda e, i=i: e.matmul(pD[:, 0:256], lhsT=onesb[0:T, :], rhs=damb[0:T, i].rearrange("p s h -> p (s h)"),
                                             start=(i == 0), stop=(i == 1)), r=["dam%d" % i, "onesb"], w=["pD", "pD2", "pD3", "pDn"])
            S.act(lambda e: e.activation(out=dtot.rearrange("p s h -> p (s h)"), in_=pD[:, 0:256], func=AF.Exp),
                  r=["pD"], w=["dtot"])
            S.barrier()
            dtotP = Dmf[:, 0:NS * 8].rearrange("p (s c) -> p s c", s=NS)
            for h2 in range(2):
                S.dve(lambda e, h2=h2: e.tensor_copy(out=dtotP[64 * h2:64 * h2 + 64],
                                                     in_=dtot[64 * h2:64 * h2 + 64].rearrange("p s (c two) -> p s c two", two=2)[:, :, :, h2]),
                      r=["dtot"], w=["dtotP"])
            h0in_b = [h0in, u]
            hout_b = [hout, hs]
            h0b_b = [hTb, ub.rearrange("p c t -> p (c t)")]
            h0Tb_b = [lrut[:, 0:512].bitcast(BF16), lrut[:, 512:1024].bitcast(BF16)]
            xdm_b = [hT[:, 0:512].bitcast(BF16), hT[:, 512:1024].bitcast(BF16)]
            pTr_b = [pC.bitcast(BF16), pD.bitcast(BF16)]
            pTk = [["pC"], ["pD"]]
            for s in range(NS):
                q = s % 2
                hi, ho, h0b, hb, xdm, pTr, tk = h0in_b[q], hout_b[q], h0b_b[q], h0Tb_b[q], xdm_b[q], pTr_b[q], pTk[q]
                S.dma(lambda e, s=s, hi=hi: e.dma_start(out=hi, in_=st_sh[s].rearrange("(c q) n -> q c n", q=128)),
                      "h0in%d" % q, w=["h0in%d" % q], q="pool")
                S.act(lambda e, hi=hi, h0b=h0b: e.activation(out=h0b, in_=hi.rearrange("p c n -> p (c n)"), func=AF.Copy),
                      r=["h0in%d" % q], w=["h0b%d" % q])
                for c in range(8):
                    S.pe(lambda e, c=c, h0b=h0b, pTr=pTr: e.transpose(out=pTr[:, c * 128:(c + 1) * 128], in_=h0b[:, c * 128:(c + 1) * 128],
                                                                      identity=identb), r=["h0b%d" % q, "identb"], w=tk)
                S.dve(lambda e, hb=hb, pTr=pTr: e.tensor_copy(out=hb, in_=pTr[:, 0:1024]), r=tk, w=["h0Tb%d" % q])
                for h in range(16):
                    S.pe(lambda e, h=h, s=s, hb=hb: e.matmul(pA[0][64 * (h % 2):64 * (h % 2) + 64, (h // 2) * TS + 4 * s:(h // 2) * TS + 4 * s + 4],
                                                             lhsT=hb[:, h * 64:(h + 1) * 64], rhs=Chs[:, h, 4 * s:4 * s + 4],
                                                             start=True, stop=True),
                         r=["h0Tb%d" % q, "Ch"], w=["pA0"])
                S.dve(lambda e, s=s, xdm=xdm: e.tensor_scalar(out=xdm[0:T, :], in0=xdd[0:T, :], scalar1=blki[:, s:s + 1], scalar2=None, op0=ALU.mult),
                      r=["xdd", "cst"], w=["xdm%d" % q])
                for c in range(8):
                    S.pe(lambda e, c=c, xdm=xdm: e.matmul(pO[:, c * 128:(c + 1) * 128], lhsT=xdm[0:T, c * 128:(c + 1) * 128],
                                                          rhs=BT[0:T, (c // 4) * 128:(c // 4 + 1) * 128], start=True, stop=True),
                         r=["xdm%d" % q, "BT"], w=["pO"])
                for c in range(8):
                    S.dve(lambda e, c=c, s=s, hi=hi, ho=ho: e.scalar_tensor_tensor(out=ho[:, c, :], in0=hi[:, c, :], scalar=dtotP[:, s, c:c + 1],
                                                                                   in1=pO[:, c * 128:(c + 1) * 128], op0=ALU.mult, op1=ALU.add),
                          r=["h0in%d" % q, "dtotP", "pO"], w=["hout%d" % q])
                S.dma(lambda e, s=s, ho=ho: e.dma_start(out=o_ssh[s].rearrange("(c q) n -> q c n", q=128), in_=ho),
                      "hout%d" % q, r=["hout%d" % q])
            S.act(lambda e: e.activation(out=pyo_sb.rearrange("p c t -> p (c t)"), in_=pA[0][:, 0:8 * TS], func=AF.Copy),
                  r=["pA0"], w=["pyo_sb"])

        S.pool(lambda e: e.memset(lxb, 0.0), w=["lx%d" % c for c in range(8)])
        S.pool(lambda e: e.memset(xcb, 0.0), w=["xc%d" % c for c in range(12)])

        def drive(g_, par):
            S.ctx = par
            try:
                next(g_)
                return True
            except StopIteration:
                return False
            finally:
                S.ctx = None

        tiles = [mixer_tile(mt, False) for mt in range(NT)]
        RATIO = 3
        par0, gP0, _ = tiles[0]
        g = gP0()
        while drive(g, par0):
            pass
        for n in range(NT):
            par, _, gS = tiles[n]
            gs = gS()
            alive_s = True
            alive_p = False
            if n + 1 < NT:
                parn, gPn, _ = tiles[n + 1]
                gp = gPn()
                alive_p = True
            while alive_s or alive_p:
                for _ in range(RATIO):
                    if alive_p:
                        alive_p = drive(gp, parn)
                if alive_s:
                    alive_s = drive(gs, par)
        S.barrier()
        if SAMP:
            pars, gPs, gSs = mixer_tile(SEQ // 128, True)
            for g in (gPs(), gSs()):
                while drive(g, pars):
                    pass

        S.barrier()
        ptr[0] = base0
        w_up_sb = b3(8, DFF)
        w_dn_sb = b3(32, D)
        if MLP:
            k_wup = load_w(w_up_sb, w_up, 8, DFF, "w_up")
            k_wdn = load_w(w_dn_sb, w_down, 32, D, "w_dn")
        T2 = 256
        xt2 = [[f32(D), f32(D)], [f32(D), f32(D)]]
        xn2 = f32(D)
        ss2 = [f32(4), f32(4)]
        rstd2 = [f32(4), f32(4)]
        mT = [b3(8, T2), b3(8, T2)]
        actb = b3(32, T2)
        rl = [f32(T2), f32(T2)]
        yout = f32(D)
        gfin_bc = f32(D)
        S.dma(lambda e: e.dma_start(out=gfin_bc, in_=gfin_d.partition_broadcast(128)), "gfin", w=["gfin"])
        pDN = [PS[:, 3072:4096], PS[:, 2048:3072]]
        pDNk = [["pO"], ["pC", "pD"]]

        def mlp_front(ti, r0, T):
            q = ti % 2
            nsub = (T + 127) // 128
            for j in range(nsub):
                Tj = min(128, T - j * 128)
                xk = "xt2_%d_%d" % (q, j)
                S.dma(lambda e, j=j, Tj=Tj: e.dma_start(out=xt2[q][j][0:Tj, :], in_=scr[r0 + j * 128:r0 + j * 128 + Tj, :]),
                      xk, r=["scr%d" % ((r0 + j * 128) // 128)], w=[xk])
                rms_rstd(xt2[q][j], Tj, xk, xn2, ss2[0], rstd2[0], "2")
                S.act(lambda e, j=j, Tj=Tj: e.activation(out=xn2[0:Tj, :], in_=xt2[q][j][0:Tj, :], func=AF.Copy, scale=rstd2[0][0:Tj, 0:1]),
                      r=[xk, "rstd2"], w=["xn2"])
                for k in range(8):
                    S.pe(lambda e, k=k, Tj=Tj: e.transpose(out=pT[:, k * 128:k * 128 + Tj], in_=xn2[0:Tj, k * 128:(k + 1) * 128],
                                                           identity=ident[0:Tj, 0:Tj]), r=["xn2", "cst"], w=["pT"])
                S.dve(lambda e, j=j, Tj=Tj: e.tensor_tensor(
                    out=mT[q][:, :, j * 128:j * 128 + Tj], in0=pT.rearrange("p (k t) -> p k t", k=8)[:, :, 0:Tj],
                    in1=P("GP", 0, 8).unsqueeze(2).to_broadcast([128, 8, Tj]), op=ALU.mult),
                    r=["pT", "pfm"], w=["mT%d" % q])
            yield
            for f in range(32):
                pa = pA[f % 2]
                for k in range(8):
                    S.pe(lambda e, k=k, f=f, pa=pa: e.matmul(pa[:, 0:T], lhsT=w_up_sb[:, k, f * 128:(f + 1) * 128], rhs=mT[q][:, k, 0:T],
                                                             start=(k == 0), stop=(k == 7)),
                         r=["mT%d" % q, "w_up"], w=["pA%d" % (f % 2)])
                S.act(lambda e, f=f, pa=pa: e.activation(out=rl[f % 2][:, 0:T], in_=pa[:, 0:T], func=AF.Relu),
                      r=["pA%d" % (f % 2)], w=["rl%d" % (f % 2)])
                S.pool(lambda e, f=f: e.tensor_tensor(out=actb[:, f, 0:T], in0=rl[f % 2][:, 0:T], in1=rl[f % 2][:, 0:T], op=ALU.mult),
                       r=["rl%d" % (f % 2)], w=["act%d" % f])
                yield

        def mlp_back(ti, r0, T):
            q = ti % 2
            nsub = (T + 127) // 128
            for f in range(32):
                for j in range(nsub):
                    Tj = min(128, T - j * 128)
                    for nb in range(2):
                        S.pe(lambda e, f=f, nb=nb, j=j, Tj=Tj: e.matmul(pDN[j][0:Tj, nb * 512:(nb + 1) * 512],
                                                                        lhsT=actb[:, f, j * 128:j * 128 + Tj],
                                                                        rhs=w_dn_sb[:, f, nb * 512:(nb + 1) * 512],
                                                                        start=(f == 0), stop=(f == 31)),
                             r=["act%d" % f, "w_dn"], w=pDNk[j])
                yield
            for j in range(nsub):
                Tj = min(128, T - j * 128)
                xk = "xt2_%d_%d" % (q, j)
                S.dve(lambda e, j=j, Tj=Tj: e.tensor_tensor(out=xt2[q][j][0:Tj, :], in0=pDN[j][0:Tj, :], in1=xt2[q][j][0:Tj, :], op=ALU.add),
                      r=pDNk[j] + [xk], w=[xk])
                rms_rstd(xt2[q][j], Tj, xk, yout, ss2[1], rstd2[1], "2b", jkey="yout")
                S.dve(lambda e, j=j, Tj=Tj: e.scalar_tensor_tensor(out=yout[0:Tj, :], in0=xt2[q][j][0:Tj, :], scalar=rstd2[1][0:Tj, 0:1],
                                                                   in1=gfin_bc[0:Tj, :], op0=ALU.mult, op1=ALU.mult),
                      r=[xk, "rstd2b", "gfin"], w=["yout"])
                rr = r0 + j * 128
                if rr < SEQ:
                    S.dma(lambda e, rr=rr, Tj=Tj: e.dma_start(out=y_p[rr:rr + Tj, :], in_=yout[0:Tj, :]), "yout", r=["yout"])
                else:
                    S.dma(lambda e, Tj=Tj: e.dma_start(out=y_s, in_=yout[0:Tj, :]), "yout", r=["yout"])
                yield

        if MLP:
            jobs = [(t * T2, T2) for t in range(NT * 128 // T2)]
            if SAMP:
                jobs.append((SEQ, TS))
            for _ in mlp_front(0, *jobs[0]):
                pass
            for ti in range(len(jobs)):
                gb = mlp_back(ti, *jobs[ti])
                gf = mlp_front(ti + 1, *jobs[ti + 1]) if ti + 1 < len(jobs) else iter(())
                ab = af = True
                while ab or af:
                    if ab:
                        ab = next(gb, "END") != "END"
                    if af:
                        af = next(gf, "END") != "END"

        S.emit()
    return nc


_CACHE = {}


def _consts():
    c = np.zeros((128, NCST), np.float32)
    i = np.arange(128)
    c[:, CI:CI + 128] = np.eye(128, dtype=np.float32)
    c[:, CU:CU + 128] = (i[:, None] <= i[None, :]).astype(np.float32)
    c[:, CN:CN + 128] = np.where(i[:, None] <= i[None, :], 0.0, NEG).astype(np.float32)
    c[:, CO:CO + 128] = 1.0
    j = np.arange(TS)
    same = (j[:, None] // 4) == (j[None, :] // 4)
    caus = j[:, None] <= j[None, :]
    c[0:TS, CUB:CUB + TS] = (same & caus).astype(np.float32)
    c[0:TS, CNB:CNB + TS] = np.where(same & caus, 0.0, NEG).astype(np.float32)
    c[0:TS, CBM:CBM + TS] = same.astype(np.float32)
    c[0:TS, CBI:CBI + NS] = ((j[:, None] // 4) == np.arange(NS)[None, :]).astype(np.float32)
    return c


def _fm(v, nch):
    return np.ascontiguousarray(np.asarray(v, np.float32).reshape(nch, 128).T)


def kernel(x_prompt, x_sample, state_lru_conv, state_lru_h, state_ssd_conv, state_ssd_h,
           g_mix, w_in, lru_conv_w, lru_conv_b, w_a, b_a, w_x, b_x, lam, g_lru_out,
           ssd_conv_w, ssd_conv_b, dt_bias, a_log, d_skip, g_ssd_out, w_out,
           g_mlp, w_up, w_down, g_final):
    f = lambda a: np.ascontiguousarray(np.asarray(a, np.float32))
    if "nc" not in _CACHE:
        _CACHE["nc"] = build_program()
    nc = _CACHE["nc"]
    pfm = np.zeros((128, NPAR), np.float32)
    lw = np.asarray(lru_conv_w[0], np.float32)
    pfm[:, PC["LW"]:PC["LW"] + 32] = lw.reshape(4, 8, 128).transpose(2, 1, 0).reshape(128, 32)
    pfm[:, PC["LB"]:PC["LB"] + 8] = _fm(lru_conv_b[0], 8)
    pfm[:, PC["BA"]:PC["BA"] + 8] = _fm(np.asarray(b_a[0]).reshape(-1), 8)
    pfm[:, PC["BX"]:PC["BX"] + 8] = _fm(np.asarray(b_x[0]).reshape(-1), 8)
    pfm[:, PC["LAM"]:PC["LAM"] + 8] = _fm(lam[0], 8)
    pfm[:, PC["GL"]:PC["GL"] + 8] = _fm(g_lru_out[0], 8)
    sw = np.asarray(ssd_conv_w[0], np.float32)
    pfm[:, PC["SW"]:PC["SW"] + 48] = sw.reshape(4, 12, 128).transpose(2, 1, 0).reshape(128, 48)
    pfm[:, PC["SB"]:PC["SB"] + 12] = _fm(ssd_conv_b[0], 12)
    pfm[:, PC["DS"]:PC["DS"] + 8] = _fm(np.repeat(np.asarray(d_skip[0], np.float32), 64), 8)
    pfm[:, PC["GS"]:PC["GS"] + 8] = _fm(g_ssd_out[0], 8)
    pfm[:, PC["GM"]:PC["GM"] + 8] = _fm(g_mix[0], 8)
    pfm[:, PC["GP"]:PC["GP"] + 8] = _fm(g_mlp[0], 8)
    cst = _consts()
    shared = {
        "w_in": f(w_in[0]), "w_out": f(w_out[0]), "w_up": f(w_up[0]), "w_down": f(w_down[0]),
        "w_a": f(w_a[0]), "w_x": f(w_x[0]), "pfm": pfm, "cst": cst,
        "dt_bias": f(dt_bias[0]), "a_log": f(a_log[0]), "g_final": f(g_final),
    }
    in_maps = []
    for b in range(NCORES):
        sl = slice(NS * b, NS * (b + 1))
        m = dict(shared)
        m["xp"] = f(x_prompt[b])
        m["xs"] = f(np.asarray(x_sample[sl]).reshape(TS, D))
        m["st_lc"] = f(np.asarray(state_lru_conv[0, sl]).reshape(NS * 3, D))
        m["st_lh"] = f(state_lru_h[0, sl])
        m["st_sc"] = f(np.asarray(state_ssd_conv[0, sl]).reshape(NS * 3, XBC))
        m["st_sh"] = f(np.asarray(state_ssd_h[0, sl]).reshape(NS, 1024, 128))
        in_maps.append(m)
    res = run_bass_kernel_spmd(nc, in_maps, core_ids=list(range(NCORES)))
    R = res.results
    cat = lambda k: np.stack([np.asarray(R[b][k], np.float32) for b in range(NCORES)])
    y_prompt = cat("y_p")
    y_sample = cat("y_s").reshape(NCORES * NS, 4, D)
    p_lc = cat("o_plc")[None]
    p_lh = cat("o_plh").reshape(NCORES, D)[None]
    p_sc = cat("o_psc")[None]
    p_sh = cat("o_psh").reshape(NCORES, 16, 64, 128)[None]
    s_lc = cat("o_slc").reshape(NCORES * NS, 3, D)[None]
    s_lh = cat("o_slh").reshape(NCORES * NS, D)[None]
    s_sc = cat("o_ssc").reshape(NCORES * NS, 3, XBC)[None]
    s_sh = cat("o_ssh").reshape(NCORES * NS, 16, 64, 128)[None]
    return (y_prompt, y_sample, p_lc, p_lh, p_sc, p_sh, s_lc, s_lh, s_sc, s_sh)
```

```python
import math
from contextlib import ExitStack

import numpy as np
import concourse.bass as bass
import concourse.mybir as mybir
from concourse.bass_utils import run_bass_kernel_spmd

F32 = mybir.dt.float32
BF16 = mybir.dt.bfloat16
AF = mybir.ActivationFunctionType
ALU = mybir.AluOpType

NCORES = 8
D = 1024
SEQ = 2048
NS = 16
TS = 64
XBC = 1536
INP = 4624
DFF = 4096
EPS = 1e-6
NEG = -30000.0

import re as _re

ENGS = ("pe", "act", "dve", "pool", "sp")
SAME_ENGINE_SYNC = {"pe": False, "act": True, "dve": True, "pool": True, "sp": False}


class Op:
    __slots__ = ("eng", "fn", "deps", "marked", "count", "dma_key", "dma_val")

    def __init__(self, eng, fn, dma_key=None):
        self.eng = eng
        self.fn = fn
        self.deps = ()
        self.marked = False
        self.count = 0
        self.dma_key = dma_key
        self.dma_val = 0


class Sched:
    def __init__(self, nc):
        self.nc = nc
        self.ops = {e: [] for e in ENGS}
        self.last_w = {}
        self.readers = {}
        self.dma_cnt = {}
        self.pending = {}
        self.ctx = None
        self.since_bar = []

    ALIAS = {"pC": "b4", "pCx": "b4", "pD": "b5", "pD2": "b5", "pD3": "b5", "pDn": "b5",
             "pDcb0": "b5", "pDcb1": "b5", "pT": "b01", "pO": "b67", "pA0": "b2", "pA1": "b3"}

    PSUM_KEYS = {"b01", "b2", "b3", "b4", "b5", "b67"}

    PAR_RE = _re.compile(r"^(xt|dtt|xnew)$|^(xsf|zs|Bb|Cb|ynl|yg)\d+$")

    def _k(self, k):
        k = self.ALIAS.get(k, k)
        if self.ctx is not None and self.PAR_RE.match(k):
            return k + "#" + str(self.ctx)
        return k

    def add(self, eng, fn, reads=(), writes=(), dma_key=None):
        reads = [self._k(k) for k in reads]
        writes = [self._k(k) for k in writes]
        if dma_key is not None and self.ctx is not None and self.PAR_RE.match(dma_key):
            dma_key = dma_key + "#" + str(self.ctx)
        op = Op(eng, fn, dma_key)
        deps = []
        seen = set()

        def dep(o):
            if o is not None and o is not op and id(o) not in seen:
                seen.add(id(o))
                deps.append(o)

        if self.pending.get(eng):
            for o in self.pending[eng]:
                dep(o)
            self.pending[eng] = []
        for b in reads:
            dep(self.last_w.get(b))
            if b in self.PSUM_KEYS:
                for r in self.readers.get(b, ()):
                    if r.eng != eng:
                        dep(r)
        for b in writes:
            dep(self.last_w.get(b))
            for r in self.readers.get(b, ()):
                dep(r)
        for b in reads:
            self.readers.setdefault(b, []).append(op)
        for b in writes:
            self.last_w[b] = op
            self.readers[b] = []
        op.deps = deps
        if dma_key is not None:
            self.dma_cnt[dma_key] = self.dma_cnt.get(dma_key, 0) + 16
            op.dma_val = self.dma_cnt[dma_key]
        self.ops[eng].append(op)
        self.since_bar.append(op)
        return op

    def barrier(self):
        ops = []
        for e in ENGS:
            comp = [o for o in self.ops[e] if o.dma_key is None]
            if comp:
                ops.append(comp[-1])
        last_dma = {}
        for o in self.since_bar:
            if o.dma_key is not None:
                last_dma[o.dma_key] = o
        ops.extend(last_dma.values())
        for e in ENGS:
            self.pending.setdefault(e, []).extend(ops)
        self.since_bar = []

    def pe(self, fn, r=(), w=()):
        return self.add("pe", fn, r, w)

    def act(self, fn, r=(), w=()):
        return self.add("act", fn, r, w)

    def dve(self, fn, r=(), w=()):
        return self.add("dve", fn, r, w)

    def pool(self, fn, r=(), w=()):
        return self.add("pool", fn, r, w)

    def dma(self, fn, key, r=(), w=(), q="sp"):
        return self.add(q, fn, r, w, dma_key=key)

    def emit(self):
        nc = self.nc
        for e in ENGS:
            for op in self.ops[e]:
                for d in op.deps:
                    if d.dma_key is None:
                        if d.eng == op.eng and not SAME_ENGINE_SYNC[d.eng]:
                            continue
                        d.marked = True
        for e in ENGS:
            c = 0
            for op in self.ops[e]:
                if op.dma_key is None and op.marked:
                    c += 1
                    op.count = c
        with ExitStack() as st:
            esem = {e: st.enter_context(nc.semaphore("es_" + e)) for e in ENGS}
            dsem = {}
            for k in self.dma_cnt:
                dsem[k] = st.enter_context(nc.semaphore("ds_%d" % len(dsem)))
            block = st.enter_context(nc.Block())

            def run(ename, eng):
                seen = {}
                for op in self.ops[ename]:
                    need = {}
                    for d in op.deps:
                        if d.dma_key is not None:
                            key = ("d", d.dma_key)
                            val = d.dma_val
                            sem = dsem[d.dma_key]
                        else:
                            if d.eng == ename and not SAME_ENGINE_SYNC[ename]:
                                continue
                            key = ("e", d.eng)
                            val = d.count
                            sem = esem[d.eng]
                        if key not in need or need[key][1] < val:
                            need[key] = (sem, val)
                    for key, (sem, val) in need.items():
                        if seen.get(key, 0) >= val:
                            continue
                        seen[key] = val
                        eng.wait_ge(sem, val)
                    ins = op.fn(eng)
                    if op.dma_key is not None:
                        ins.then_inc(dsem[op.dma_key], 16)
                    elif op.marked:
                        ins.then_inc(esem[ename], 1)
                if ename == "sp":
                    for k, v in self.dma_cnt.items():
                        eng.wait_ge(dsem[k], v)

            @block.sync
            def _(e):
                run("sp", e)

            @block.tensor
            def _(e):
                run("pe", e)

            @block.scalar
            def _(e):
                run("act", e)

            @block.vector
            def _(e):
                run("dve", e)

            @block.gpsimd
            def _(e):
                run("pool", e)


PC = {}
_o = 0
for _n, _w in (("LW", 32), ("LB", 8), ("BA", 8), ("BX", 8), ("LAM", 8), ("GL", 8), ("SW", 48),
               ("SB", 12), ("DS", 8), ("GS", 8), ("GM", 8), ("GP", 8)):
    PC[_n] = _o
    _o += _w
NPAR = _o
CI, CU, CN, CO, CUB, CNB, CBM, CBI = 0, 128, 256, 384, 512, 576, 640, 704
NCST = 720


def build_program(NT=SEQ // 128, SAMP=True, MLP=True, DBG=False, STAGE=9):
    nc = bass.Bass("TRN2", target_bir_lowering=False)
    S = Sched(nc)

    def din(name, shape):
        return nc.dram_tensor(name, list(shape), F32, kind="ExternalInput").ap()

    def dout(name, shape):
        return nc.dram_tensor(name, list(shape), F32, kind="ExternalOutput").ap()

    xp = din("xp", (SEQ, D))
    xs = din("xs", (TS, D))
    st_lc = din("st_lc", (NS * 3, D))
    st_lh = din("st_lh", (NS, D))
    st_sc = din("st_sc", (NS * 3, XBC))
    st_sh = din("st_sh", (NS, 1024, 128))
    w_in = din("w_in", (D, INP))
    w_out = din("w_out", (2 * D, D))
    w_up = din("w_up", (D, DFF))
    w_down = din("w_down", (DFF, D))
    w_a = din("w_a", (16, 64, 64))
    w_x = din("w_x", (16, 64, 64))
    pfm_d = din("pfm", (128, NPAR))
    cst_d = din("cst", (128, NCST))
    dtb_d = din("dt_bias", (16,))
    alog_d = din("a_log", (16,))
    gfin_d = din("g_final", (D,))

    y_p = dout("y_p", (SEQ, D))
    y_s = dout("y_s", (TS, D))
    o_plc = dout("o_plc", (3, D))
    o_plh = dout("o_plh", (8, 128))
    o_psc = dout("o_psc", (3, XBC))
    o_psh = dout("o_psh", (1024, 128))
    o_slc = dout("o_slc", (NS, 3, D))
    o_slh = dout("o_slh", (NS, D))
    o_ssc = dout("o_ssc", (NS, 3, XBC))
    o_ssh = dout("o_ssh", (NS, 1024, 128))
    scr = nc.dram_tensor("scr", [SEQ + TS, D], F32, kind=("ExternalOutput" if DBG else "Internal")).ap()

    st = ExitStack()
    with st:
        RW = 53200
        R = st.enter_context(nc.sbuf_tensor("R", [128, RW], F32))
        PS = st.enter_context(nc.psum_tensor("PS", [128, 4096], F32))
        ptr = [0]

        def alloc(nwords):
            a = ptr[0]
            ptr[0] += (nwords + 7) // 8 * 8
            pass
            return a

        def f32(n):
            a = alloc(n)
            return R[:, a:a + n]

        def bf(n):
            w = (n + 1) // 2
            a = alloc(w)
            return R[:, a:a + w].bitcast(BF16)[:, 0:n]

        def f3(c, t):
            return f32(c * t).rearrange("p (c t) -> p c t", c=c)

        def b3(c, t):
            return bf(c * t).rearrange("p (c t) -> p c t", c=c)

        def bank(b, n=512):
            return PS[:, 512 * b:512 * b + n]

        pT = PS[:, 0:1024]
        pTb = pT.bitcast(BF16)
        pA = [bank(2), bank(3)]
        pC = bank(4)
        pCb = pC.bitcast(BF16)
        pD = bank(5)
        pO = PS[:, 3072:4096]

        cst = f32(NCST)
        pfm = f32(NPAR)
        dtb_bc = f32(16)
        a_bc = f32(16)
        identb = bf(128)
        onesb = bf(128)
        Utrib = bf(128)
        mskb = bf(3 * TS)
        dah = bf(16)
        dal = bf(16)
        cfac = f32(8)
        c2fac = f32(8)
        tiny = f32(8)
        mhalf = f32(4)
        eps_t = f32(4)
        nbias = f32(16)
        wa_blk = b3(8, 128)
        wx_blk = b3(8, 128)
        hstate = f32(8)
        hT = f32(1024)
        hTb = bf(1024)

        ident = cst[:, CI:CI + 128]
        Utri = cst[:, CU:CU + 128]
        negm = cst[:, CN:CN + 128]
        onesf = cst[:, CO:CO + 128]
        Ublk = cst[0:TS, CUB:CUB + TS]
        negblk = cst[0:TS, CNB:CNB + TS]
        blkm = cst[0:TS, CBM:CBM + TS]
        blki = cst[0:TS, CBI:CBI + NS]

        S.dma(lambda e: e.dma_start(out=cst, in_=cst_d), "cst", w=["cst"])
        S.dma(lambda e: e.dma_start(out=pfm, in_=pfm_d), "pfm", w=["pfm"])
        S.dma(lambda e: e.dma_start(out=dtb_bc, in_=dtb_d.partition_broadcast(128)), "dtb", w=["dtb"])
        S.dma(lambda e: e.dma_start(out=a_bc, in_=alog_d.partition_broadcast(128)), "alog", w=["a_bc"])
        S.dve(lambda e: e.tensor_copy(out=identb, in_=ident), r=["cst"], w=["identb"])
        S.dve(lambda e: e.tensor_copy(out=onesb, in_=onesf), r=["cst"], w=["onesb"])
        S.dve(lambda e: e.tensor_copy(out=Utrib, in_=Utri), r=["cst"], w=["mskb"])
        S.dve(lambda e: e.tensor_copy(out=mskb[0:TS, 0:TS], in_=Ublk), r=["cst"], w=["mskb"])
        S.dve(lambda e: e.tensor_copy(out=mskb[0:TS, TS:2 * TS], in_=blkm), r=["cst"], w=["mskb"])
        S.pool(lambda e: e.memset(mhalf, -0.5), w=["mhalf"])
        S.pool(lambda e: e.memset(eps_t, EPS), w=["eps_t"])
        S.dve(lambda e: e.tensor_scalar(out=nbias[:, 0:8], in0=pfm[:, PC["BA"]:PC["BA"] + 8], scalar1=-1.0, scalar2=None, op0=ALU.mult), r=["pfm"], w=["nbias"])
        S.dve(lambda e: e.tensor_scalar(out=nbias[:, 8:16], in0=pfm[:, PC["BX"]:PC["BX"] + 8], scalar1=-1.0, scalar2=None, op0=ALU.mult), r=["pfm", "nbias"], w=["nbias"])
        S.pool(lambda e: e.memset(hstate, 0.0), w=["hstate"])
        S.pool(lambda e: e.memset(hT, 0.0), w=["hT"])
        S.pool(lambda e: e.memset(hTb, 0.0), w=["hTb"])
        S.pool(lambda e: e.memset(wa_blk, 0.0), w=["wa"])
        S.pool(lambda e: e.memset(wx_blk, 0.0), w=["wx"])
        S.act(lambda e: e.activation(out=a_bc, in_=a_bc, func=AF.Exp), r=["a_bc"], w=["a_bc"])
        S.dve(lambda e: e.tensor_scalar(out=a_bc, in0=a_bc, scalar1=-1.0, scalar2=None, op0=ALU.mult), r=["a_bc"], w=["a_bc"])
        lam = pfm[:, PC["LAM"]:PC["LAM"] + 8]
        S.act(lambda e: e.activation(out=tiny, in_=lam, func=AF.Exp, scale=-1.0), r=["pfm"], w=["tiny"])
        S.act(lambda e: e.activation(out=tiny, in_=tiny, func=AF.Ln, bias=1.0), r=["tiny"], w=["tiny"])
        S.dve(lambda e: e.tensor_scalar(out=cfac, in0=tiny, scalar1=-8.0, scalar2=None, op0=ALU.mult), r=["tiny"], w=["cfac"])
        S.dve(lambda e: e.tensor_scalar(out=c2fac, in0=tiny, scalar1=-16.0, scalar2=None, op0=ALU.mult), r=["tiny"], w=["cfac2"])
        for (wd, blk, nm) in ((w_a, wa_blk, "wa"), (w_x, wx_blk, "wx")):
            v = wd.rearrange("(c h) i j -> h i c j", h=2)
            for h2 in range(2):
                S.dma(lambda e, v=v, blk=blk, h2=h2: e.dma_start(
                    out=blk[64 * h2:64 * h2 + 64, :, 64 * h2:64 * h2 + 64], in_=v[h2]),
                    nm + str(h2), w=[nm], q="pool")

        base0 = ptr[0]

        def load_w(dst3, src2, nk, ncol, name, step=2048):
            sv = src2.rearrange("(k p) n -> p k n", p=128)
            pieces = [(k, c0, min(ncol, c0 + step)) for k in range(nk) for c0 in range(0, ncol, step)]
            for i, (k, c0, c1) in enumerate(pieces):
                S.dma(lambda e, k=k, c0=c0, c1=c1: e.dma_start(out=dst3[:, k, c0:c1], in_=sv[:, k, c0:c1]),
                      name, w=([name] if i == len(pieces) - 1 else []), q="pool")
            return name

        w_in_sb = b3(8, INP)
        w_out_sb = b3(16, D)
        if STAGE >= 1:
            k_win = load_w(w_in_sb, w_in, 8, INP, "w_in")
            k_wout = load_w(w_out_sb, w_out, 16, D, "w_out")

        xt = f32(D)
        xn = f32(D)
        junk = xn
        ss = f32(4)
        rstd = f32(4)
        hTt = b3(8, 128)
        lxb = f3(8, 131)
        xcb = f3(12, 131)
        sreg = f32(20 * NS * 7)
        lxs = sreg[:, 0:8 * NS * 7].rearrange("p (c s l) -> p c s l", c=8, s=NS)
        xcs = sreg[:, 8 * NS * 7:20 * NS * 7].rearrange("p (c s l) -> p c s l", c=12, s=NS)
        gl = f3(2, 128)
        zs = f3(8, 128)
        u = f3(8, 128)
        ub = b3(8, 128)
        lrut = f32(1280)
        gi = lrut[:, 0:512].rearrange("p (c t) -> p c t", c=4)
        av = lrut[:, 512:768].rearrange("p (c t) -> p c t", c=2)
        a2 = lrut[:, 768:1024].rearrange("p (c t) -> p c t", c=2)
        tmpb = lrut[:, 1024:1280].rearrange("p (c t) -> p c t", c=2)
        hs = f3(8, 128)
        ysq = b3(2, 128)
        rbc = f32(128)
        ynl = b3(8, 128)
        yns = b3(8, 128)
        xsf = f3(8, 128)
        Bb = b3(2, 128)
        Cb = b3(2, 128)
        dtr = f32(16)
        dtt = f32(16)
        da = f32(16)
        ncum = f32(16)
        dte = f32(16)
        cdec = f32(16)
        xdt = bf(1024)
        xdd = bf(1024)
        BT = bf(256)
        cbT = f3(2, 128)
        Dmf = f32(512)
        Emf = f32(512)
        Mmf = bf(512)
        Chf = bf(1024)
        Chp = Chf[:, 0:512].rearrange("p (a t) -> p a t", a=4)
        Chs = Chf.rearrange("p (a t) -> p a t", a=16)
        cvt = f3(2, 128)
        stg = f32(2560)
        stT = stg[:, 0:1024]
        lc_in = stg[:, 0:1024]
        sc_in = stg[:, 1024:2560]
        lh_in = stg[:, 0:1024]
        h0in = lxb.rearrange("p c t -> p (c t)")[:, 0:1024].rearrange("p (c t) -> p c t", c=8)
        h0Tb = hTb
        Bm = bf(256)
        pyo_f = f32(8 * TS)
        pyo_sb = pyo_f.rearrange("p (c t) -> p c t", c=8)
        damb = bf(2 * NS * 16).rearrange("p (i s h) -> p i s h", i=2, s=NS)
        dtot = f3(NS, 16)
        hnew = hT
        hout = xcb.rearrange("p c t -> p (c t)")[:, 0:1024].rearrange("p (c t) -> p c t", c=8)
        h0s = f3(8, NS)
        hfin = f3(8, NS)

        bf3 = lambda ap, c: ap.bitcast(BF16).rearrange("p (c t) -> p c t", c=c)
        xt_b = [xt, stg[:, 0:1024]]
        ynl_b = [ynl, bf3(stg[:, 1024:1536], 8)]
        Bb_b = [Bb, bf3(stg[:, 1536:1664], 2)]
        Cb_b = [Cb, bf3(stg[:, 1664:1792], 2)]
        dtt_b = [dtt, stg[:, 1792:1808]]
        xsf_b = [xsf, sreg[:, 0:1024].rearrange("p (c t) -> p c t", c=8)]
        zs_b = [zs, sreg[:, 1024:2048].rearrange("p (c t) -> p c t", c=8)]
        Wl_S = pyo_f[:, 0:256].bitcast(BF16)
        ysq_S = bf3(pyo_f[:, 256:384], 2)
        rbc_S = pyo_f[:, 384:512]
        STGW = [k + "#1" for k in ["xt", "dtt"] + ["ynl%d" % c for c in range(8)] + ["Bb0", "Bb1", "Cb0", "Cb1"]]
        STG_ALIAS = ["xt", "dtt"] + ["ynl%d" % c for c in range(8)] + ["Bb0", "Bb1", "Cb0", "Cb1"]

        def P(name, c=None, w=1):
            o = PC[name] + (0 if c is None else c * w)
            return pfm[:, o:o + w]

        def rms_rstd(xtile, T, keyx, junk, ss, rstd, sfx="", jkey=None):
            S.act(lambda e: e.activation(out=junk[0:T, :], in_=xtile[0:T, :], func=AF.Square, accum_out=ss[0:T, 0:1]),
                  r=[keyx], w=[jkey or ("xn" + sfx), "ss" + sfx])
            S.act(lambda e: e.activation(out=ss[0:T, 0:1], in_=ss[0:T, 0:1], func=AF.Ln, scale=1.0 / D, bias=eps_t[0:T, 0:1]),
                  r=["ss" + sfx, "eps_t"], w=["ss" + sfx])
            S.act(lambda e: e.activation(out=rstd[0:T, 0:1], in_=ss[0:T, 0:1], func=AF.Exp, scale=-0.5),
                  r=["ss" + sfx], w=["rstd" + sfx])

        pAA = PS[:, 1024:2048]

        def to_fm(T, gname, dst, dkey):
            for k in range(8):
                S.pe(lambda e, k=k: e.transpose(out=pAA[:, k * 128:k * 128 + T], in_=xn[0:T, k * 128:(k + 1) * 128],
                                                identity=ident[0:T, 0:T]), r=["xn", "cst"], w=["pA0", "pA1"])
            S.dve(lambda e: e.tensor_tensor(
                out=dst[:, :, 0:T], in0=pAA.rearrange("p (k t) -> p k t", k=8)[:, :, 0:T],
                in1=P(gname, 0, 8).unsqueeze(2).to_broadcast([128, 8, T]), op=ALU.mult),
                r=["pA0", "pA1", "pfm"], w=[dkey])

        def mixer_tile(mt, samp):
            T = TS if samp else 128
            row0 = SEQ if samp else mt * 128
            xsrc = xs if samp else xp[mt * 128:(mt + 1) * 128, :]
            last = (not samp) and mt == NT - 1
            par = 0 if samp else (NT - 1 - mt) % 2
            xt, ynl, Bb, Cb, dtt, xsf, zs = (xt_b[par], ynl_b[par], Bb_b[par], Cb_b[par], dtt_b[par], xsf_b[par], zs_b[par])
            Wl = cvt.rearrange("p a t -> p (a t)").bitcast(BF16) if samp else Wl_S
            wlk = ["cv_t0", "cv_t1"] if samp else ["WlS"]
            ysqS = ysq if samp else ysq_S
            rbcS = rbc if samp else rbc_S
            sk = "" if samp else "S"

            def inter(*gens):
                gens = list(gens)
                while gens:
                    for g_ in list(gens):
                        try:
                            next(g_)
                        except StopIteration:
                            gens.remove(g_)
                        yield

            pcnt = [0]

            def proj(ci):
                i = pcnt[0] % 2
                pcnt[0] += 1
                pa = pA[i]
                for k in range(8):
                    S.pe(lambda e, k=k: e.matmul(pa[:, 0:T], lhsT=w_in_sb[:, k, ci * 128:(ci + 1) * 128],
                                                 rhs=hTt[:, k, 0:T], start=(k == 0), stop=(k == 7)),
                         r=["hTt", "w_in"], w=["pA%d" % i])
                return pa, "pA%d" % i

            def new_cols(buf, sbuf_, c):
                if samp:
                    return sbuf_[:, c, :, 3:7]
                return buf[:, c, 3:131]

            def pa_view(pa):
                if samp:
                    return pa[:, 0:T].rearrange("p (s l) -> p s l", s=NS)
                return pa[:, 0:T]

            def tap(buf, sbuf_, c, k):
                if samp:
                    return sbuf_[:, c, :, k:k + 4]
                return buf[:, c, k:k + 128]

            def fm(t3, c):
                if samp:
                    return t3[:, c, 0:T].rearrange("p (s l) -> p s l", s=NS)
                return t3[:, c, 0:T]

            def conv(buf, sbuf_, c, wname, bname, out_ap, key_in, key_out):
                S.dve(lambda e: e.tensor_scalar(out=out_ap, in0=tap(buf, sbuf_, c, 3), scalar1=P(wname, c, 4)[:, 3:4],
                                                scalar2=P(bname, c), op0=ALU.mult, op1=ALU.add),
                      r=[key_in, "pfm"], w=[key_out])
                for k in (2, 1, 0):
                    S.dve(lambda e, k=k: e.scalar_tensor_tensor(out=out_ap, in0=tap(buf, sbuf_, c, k),
                                                                scalar=P(wname, c, 4)[:, k:k + 1], in1=out_ap,
                                                                op0=ALU.mult, op1=ALU.add),
                          r=[key_in, key_out, "pfm"], w=[key_out])
                if not samp:
                    S.dve(lambda e: e.tensor_copy(out=buf[:, c, 0:3], in_=buf[:, c, 128:131]), r=[key_in], w=[key_in])

            def g_lrux():
                for c in range(8):
                    pa, pk = proj(c)
                    S.act(lambda e, c=c, pa=pa: e.activation(out=new_cols(lxb, lxs, c), in_=pa_view(pa), func=AF.Copy),
                          r=[pk], w=["lx%d" % c])
                    conv(lxb, lxs, c, "LW", "LB", fm(u, c), "lx%d" % c, "u%d" % c)
                    yield

            def g_z():
                for c in range(8):
                    pa, pk = proj(16 + c)
                    S.act(lambda e, c=c, pa=pa: e.activation(out=zs[:, c, 0:T], in_=pa[:, 0:T], func=AF.Silu),
                          r=[pk], w=["zs%d" % c])
                    yield

            def g_xbc():
                for c in range(12):
                    pa, pk = proj(24 + c)
                    S.act(lambda e, c=c, pa=pa: e.activation(out=new_cols(xcb, xcs, c), in_=pa_view(pa), func=AF.Copy),
                          r=[pk], w=["xc%d" % c])
                    if c < 8:
                        conv(xcb, xcs, c, "SW", "SB", fm(cvt, c % 2), "xc%d" % c, "cv_t%d" % (c % 2))
                        S.act(lambda e, c=c: e.activation(out=xsf[:, c, 0:T], in_=cvt[:, c % 2, 0:T], func=AF.Silu),
                              r=["cv_t%d" % (c % 2)], w=["xsf%d" % c])
                    else:
                        g = (c - 8) % 2
                        dstb = Bb if c < 10 else Cb
                        nm = ("Bb%d" if c < 10 else "Cb%d") % g
                        conv(xcb, xcs, c, "SW", "SB", fm(cvt, g), "xc%d" % c, "cv_t%d" % g)
                        S.act(lambda e, g=g, dstb=dstb: e.activation(out=dstb[:, g, 0:T], in_=cvt[:, g, 0:T], func=AF.Silu),
                              r=["cv_t%d" % g], w=[nm])
                    yield
                for k in range(8):
                    S.pe(lambda e, k=k: e.matmul(pD[0:T, 0:16], lhsT=hTt[:, k, 0:T], rhs=w_in_sb[:, k, 4608:4624],
                                                 start=(k == 0), stop=(k == 7)),
                         r=["hTt", "w_in"], w=["pD"])
                S.dve(lambda e: e.tensor_tensor(out=dtr[0:T, :], in0=pD[0:T, 0:16], in1=dtb_bc[0:T, :], op=ALU.add),
                      r=["pD", "dtb"], w=["dtr"])
                S.act(lambda e: e.activation(out=dtr[0:T, :], in_=dtr[0:T, :], func=AF.Exp), r=["dtr"], w=["dtr"])
                S.act(lambda e: e.activation(out=dtt[0:T, :], in_=dtr[0:T, :], func=AF.Ln, bias=1.0), r=["dtr"], w=["dtt"])
                yield

            def g_lru(chunks, pg, kr, ki):
                for c in chunks:
                    pp = c % 2
                    S.pool(lambda e, c=c: e.tensor_copy(out=ub[:, c, 0:T], in_=u[:, c, 0:T]),
                           r=["u%d" % c], w=["ub%d" % c])
                    yield
                    S.pe(lambda e, c=c: e.matmul(pg[:, 0:T], lhsT=wa_blk[:, c, :], rhs=ub[:, c, 0:T], start=True, stop=True),
                         r=["ub%d" % c, "wa"], w=[kr])
                    S.pe(lambda e, c=c: e.matmul(pg[:, 128:128 + T], lhsT=wx_blk[:, c, :], rhs=ub[:, c, 0:T], start=True, stop=True),
                         r=["ub%d" % c, "wx"], w=[ki])
                    yield
                    S.act(lambda e, c=c, pp=pp: e.activation(out=gi[:, 2 * pp, 0:T], in_=pg[:, 0:T], func=AF.Exp, scale=-1.0, bias=nbias[:, c:c + 1]),
                          r=[kr, "nbias"], w=["rg%d" % pp])
                    S.act(lambda e, c=c, pp=pp: e.activation(out=gi[:, 2 * pp + 1, 0:T], in_=pg[:, 128:128 + T], func=AF.Exp, scale=-1.0, bias=nbias[:, 8 + c:9 + c]),
                          r=[ki, "nbias"], w=["ig%d" % pp])
                    S.act(lambda e, pp=pp: e.activation(out=gi[:, 2 * pp:2 * pp + 2, 0:T], in_=gi[:, 2 * pp:2 * pp + 2, 0:T], func=AF.Ln, bias=1.0),
                          r=["rg%d" % pp, "ig%d" % pp], w=["rg%d" % pp, "ig%d" % pp])
                    S.act(lambda e, pp=pp: e.activation(out=gi[:, 2 * pp:2 * pp + 2, 0:T], in_=gi[:, 2 * pp:2 * pp + 2, 0:T], func=AF.Exp, scale=-1.0),
                          r=["rg%d" % pp, "ig%d" % pp], w=["rg%d" % pp, "ig%d" % pp])
                    S.act(lambda e, c=c, pp=pp: e.activation(out=av[:, pp, 0:T], in_=gi[:, 2 * pp, 0:T], func=AF.Exp, scale=cfac[:, c:c + 1]),
                          r=["rg%d" % pp, "cfac"], w=["av%d" % pp])
                    S.act(lambda e, c=c, pp=pp: e.activation(out=a2[:, pp, 0:T], in_=gi[:, 2 * pp, 0:T], func=AF.Exp, scale=c2fac[:, c:c + 1]),
                          r=["rg%d" % pp, "cfac2"], w=["a2%d" % pp])
                    S.act(lambda e, pp=pp: e.activation(out=a2[:, pp, 0:T], in_=a2[:, pp, 0:T], func=AF.Ln, scale=-1.0, bias=1.0),
                          r=["a2%d" % pp], w=["a2%d" % pp])
                    S.act(lambda e, pp=pp: e.activation(out=a2[:, pp, 0:T], in_=a2[:, pp, 0:T], func=AF.Exp, scale=0.5),
                          r=["a2%d" % pp], w=["a2%d" % pp])
                    yield
                    S.dve(lambda e, c=c, pp=pp: e.tensor_tensor(out=tmpb[:, pp, 0:T], in0=gi[:, 2 * pp + 1, 0:T], in1=u[:, c, 0:T], op=ALU.mult),
                          r=["ig%d" % pp, "u%d" % c], w=["tb%d" % pp])
                    S.dve(lambda e, pp=pp: e.tensor_tensor(out=tmpb[:, pp, 0:T], in0=tmpb[:, pp, 0:T], in1=a2[:, pp, 0:T], op=ALU.mult),
                          r=["tb%d" % pp, "a2%d" % pp], w=["tb%d" % pp])
                    if samp:
                        a3 = av[:, pp, 0:T].rearrange("p (s l) -> p s l", s=NS)
                        b3v = tmpb[:, pp, 0:T].rearrange("p (s l) -> p s l", s=NS)
                        S.dve(lambda e, c=c, a3=a3: e.tensor_tensor(out=rbc[:, 0:NS], in0=a3[:, :, 0], in1=h0s[:, c, :], op=ALU.mult),
                              r=["av%d" % pp, "h0s"], w=["rbc"])
                        S.dve(lambda e, b3v=b3v: e.tensor_tensor(out=b3v[:, :, 0], in0=b3v[:, :, 0], in1=rbc[:, 0:NS], op=ALU.add),
                              r=["tb%d" % pp, "rbc"], w=["tb%d" % pp])
                        S.dve(lambda e, a3=a3: e.memset(a3[:, :, 0], 0.0), r=["rbc"], w=["av%d" % pp])
                        S.dve(lambda e, c=c, pp=pp: e.tensor_tensor_scan(out=hs[:, c, 0:T], data0=av[:, pp, 0:T], data1=tmpb[:, pp, 0:T],
                                                                         initial=0.0, op0=ALU.mult, op1=ALU.add),
                              r=["av%d" % pp, "tb%d" % pp], w=["hs%d" % c])
                        S.dve(lambda e, c=c: e.tensor_copy(out=hfin[:, c, :], in_=hs[:, c, 0:T].rearrange("p (s l) -> p s l", s=NS)[:, :, 3]),
                              r=["hs%d" % c], w=["hfin"])
                    else:
                        S.dve(lambda e, c=c, pp=pp: e.tensor_tensor_scan(out=hs[:, c, 0:T], data0=av[:, pp, 0:T], data1=tmpb[:, pp, 0:T],
                                                                         initial=hstate[:, c:c + 1], op0=ALU.mult, op1=ALU.add),
                              r=["av%d" % pp, "tb%d" % pp, "hstate"], w=["hs%d" % c])
                        S.dve(lambda e, c=c: e.tensor_copy(out=hstate[:, c:c + 1], in_=hs[:, c, T - 1:T]),
                              r=["hs%d" % c], w=["hstate"])
                    yield

            def g_gate():
                for c in range(8):
                    pp = c % 2
                    pa, pk = proj(8 + c)
                    S.act(lambda e, pp=pp, pa=pa: e.activation(out=gl[:, pp, 0:T], in_=pa[:, 0:T], func=AF.Gelu_apprx_tanh),
                          r=[pk], w=["gl%d" % pp])
                    S.dve(lambda e, c=c, pp=pp: e.tensor_tensor(out=hs[:, c, 0:T], in0=hs[:, c, 0:T], in1=gl[:, pp, 0:T], op=ALU.mult),
                          r=["hs%d" % c, "gl%d" % pp], w=["yl%d" % c, "hs%d" % c])
                    S.pool(lambda e, c=c, pp=pp: e.tensor_tensor(out=ysq[:, pp, 0:T], in0=hs[:, c, 0:T], in1=hs[:, c, 0:T], op=ALU.mult),
                           r=["yl%d" % c], w=["ysq%d" % pp])
                    S.pe(lambda e, c=c, pp=pp: e.matmul(pD[:, 128:128 + T], lhsT=onesb, rhs=ysq[:, pp, 0:T], start=(c == 0), stop=(c == 7)),
                         r=["ysq%d" % pp, "onesb"], w=["pDn"])
                    yield

            def norm_apply(T, eps_, src, skey, gname, dst, dkey, c0, c1, rbc, rk, pst, pk):
                S.act(lambda e: e.activation(out=rbc[:, 0:T], in_=pst, func=AF.Ln,
                                             scale=1.0 / ((c1 - c0) * 128), bias=eps_t[:, 0:1]),
                      r=[pk, "eps_t"], w=[rk])
                S.act(lambda e: e.activation(out=rbc[:, 0:T], in_=rbc[:, 0:T], func=AF.Exp, scale=-0.5), r=[rk], w=[rk])
                for c in range(c0, c1):
                    S.dve(lambda e, c=c: e.scalar_tensor_tensor(out=dst[:, c, 0:T], in0=src[:, c, 0:T], scalar=P(gname, c),
                                                                in1=rbc[:, 0:T], op0=ALU.mult, op1=ALU.mult),
                          r=[skey % c, rk, "pfm"], w=[dkey % c])

            Um = mskb[0:TS, 0:TS] if samp else Utrib
            ngm = negblk if samp else negm
            allm = mskb[0:TS, TS:2 * TS] if samp else onesb
            d4 = lambda ap: ap[:, 0:4 * T].rearrange("p (a t) -> p a t", a=4)
            Em, Dm, Mm, pC4 = d4(Emf), d4(Dmf), d4(Mmf), d4(pC)

            def g_ssd():
                for c in range(8):
                    S.pe(lambda e, c=c: e.transpose(out=pT[0:T, c * 128:(c + 1) * 128], in_=xsf[:, c, 0:T], identity=ident),
                         r=["xsf%d" % c, "cst"], w=["pT"])
                for g in range(2):
                    S.pe(lambda e, g=g: e.transpose(out=pCb[0:T, 128 + g * 128:128 + (g + 1) * 128], in_=Bb[:, g, 0:T], identity=identb),
                         r=["Bb%d" % g, "identb"], w=["pC", "pCx"])
                S.dve(lambda e: e.tensor_tensor(out=xdt[0:T, :].rearrange("p (h q) -> p h q", h=16),
                                                in0=pT[0:T, :].rearrange("p (h q) -> p h q", h=16),
                                                in1=dtt[0:T, :].unsqueeze(2).to_broadcast([T, 16, 64]), op=ALU.mult),
                      r=["pT", "dtt"], w=["xdt"])
                S.dve(lambda e: e.tensor_copy(out=BT[0:T, :], in_=pCb[0:T, 128:384]), r=["pC"], w=["BT"])
                S.dve(lambda e: e.tensor_tensor(out=da[0:T, :], in0=dtt[0:T, :], in1=a_bc[0:T, :], op=ALU.mult),
                      r=["dtt", "a_bc"], w=["da"])
                yield
                S.dve(lambda e: e.tensor_copy(out=dah[0:T, :], in_=da[0:T, :]), r=["da"], w=["dah"])
                S.dve(lambda e: e.tensor_tensor(out=dal[0:T, :], in0=da[0:T, :], in1=dah[0:T, :], op=ALU.subtract),
                      r=["da", "dah"], w=["dal"])
                for i, dx in enumerate((dah, dal)):
                    S.pe(lambda e, dx=dx, i=i: e.matmul(pC[0:T, 0:16], lhsT=Um[0:T, 0:T], rhs=dx[0:T, :], start=(i == 0), stop=(i == 1)),
                         r=["dah", "dal", "mskb"], w=["pC", "pCx"])
                for i, dx in enumerate((dah, dal)):
                    S.pe(lambda e, dx=dx, i=i: e.matmul(pC[0:T, 16:32], lhsT=allm[0:T, 0:T], rhs=dx[0:T, :], start=(i == 0), stop=(i == 1)),
                         r=["dah", "dal", "mskb", "onesb"], w=["pC", "pCx"])
                if not samp:
                    for i, dx in enumerate((dah, dal)):
                        S.pe(lambda e, dx=dx, i=i: e.matmul(pC[:, 32:48], lhsT=onesb, rhs=dx, start=(i == 0), stop=(i == 1)),
                             r=["dah", "dal", "onesb"], w=["pC", "pCx"])
                for g in range(2):
                    S.pe(lambda e, g=g: e.matmul(pC[0:T, 256 + g * 128:256 + g * 128 + T], lhsT=Bb[:, g, 0:T], rhs=Cb[:, g, 0:T],
                                                 start=True, stop=True), r=["Bb%d" % g, "Cb%d" % g], w=["pC", "pCx"])
                yield
                S.dve(lambda e: e.tensor_scalar(out=ncum[0:T, :], in0=pC[0:T, 0:16], scalar1=-1.0, scalar2=None, op0=ALU.mult),
                      r=["pC"], w=["ncum"])
                S.dve(lambda e: e.tensor_tensor(out=dte[0:T, :], in0=pC[0:T, 16:32], in1=ncum[0:T, :], op=ALU.add),
                      r=["pC", "ncum"], w=["dte"])
                if not samp:
                    S.dve(lambda e: e.tensor_copy(out=cdec, in_=pC[:, 32:48]), r=["pC"], w=["cdec"])
                S.dve(lambda e: e.tensor_copy(out=cbT[0:T, :, 0:T], in_=pC[0:T, 256:512].rearrange("p (g t) -> p g t", g=2)[:, :, 0:T]),
                      r=["pC"], w=["cbT0", "cbT1"])
                S.act(lambda e: e.activation(out=dte[0:T, :], in_=dte[0:T, :], func=AF.Exp), r=["dte"], w=["dte"])
                if not samp:
                    S.act(lambda e: e.activation(out=cdec, in_=cdec, func=AF.Exp), r=["cdec"], w=["cdec"])
                S.dve(lambda e: e.tensor_tensor(out=xdd[0:T, :].rearrange("p (h q) -> p h q", h=16),
                                                in0=xdt[0:T, :].rearrange("p (h q) -> p h q", h=16),
                                                in1=dte[0:T, :].unsqueeze(2).to_broadcast([T, 16, 64]), op=ALU.mult),
                      r=["xdt", "dte"], w=["xdd"])
                yield
                if not samp:
                    for g in range(2):
                        S.pe(lambda e, g=g: e.matmul(pO[:, g * 512:(g + 1) * 512], lhsT=BT[:, g * 128:(g + 1) * 128],
                                                     rhs=xdd[:, g * 512:(g + 1) * 512], start=True, stop=True),
                             r=["BT", "xdd"], w=["pO"])
                    S.dve(lambda e: e.tensor_tensor(out=hT.rearrange("p (h q) -> p h q", h=16),
                                                    in0=hT.rearrange("p (h q) -> p h q", h=16),
                                                    in1=cdec.unsqueeze(2).to_broadcast([128, 16, 64]), op=ALU.mult),
                          r=["hT", "cdec"], w=["hT"])
                    S.dve(lambda e: e.tensor_tensor(out=hT, in0=hT, in1=pO, op=ALU.add), r=["hT", "pO"], w=["hT"])
                    yield
                for q4 in range(4):
                    g = q4 // 2
                    for i, (dx, Wf, wk) in enumerate(((dah, Mmf, ["Mm"]), (dal, Wl, wlk))):
                        S.pool(lambda e, q4=q4, dx=dx, Wf=Wf: e.tensor_tensor(out=d4(Wf)[0:T], in0=Um[0:T, 0:T].unsqueeze(1).to_broadcast([T, 4, T]),
                                                                             in1=dx[0:T, q4 * 4:q4 * 4 + 4].unsqueeze(2).to_broadcast([T, 4, T]),
                                                                             op=ALU.mult), r=["dah", "dal", "mskb"], w=wk)
                        S.pe(lambda e, Wf=Wf, i=i: e.matmul(pC[:, 0:4 * T], lhsT=onesb[0:T, :], rhs=Wf[0:T, 0:4 * T],
                                                            start=(i == 0), stop=(i == 1)), r=wk + ["onesb"], w=["pC", "pCx"])
                    S.dve(lambda e: e.tensor_copy(out=Em, in_=pC4), r=["pC"], w=["Em"])
                    yield
                    for hh in range(4):
                        h = q4 * 4 + hh
                        S.dve(lambda e, h=h, hh=hh: e.scalar_tensor_tensor(out=Dm[0:T, hh, :], in0=Em[0:T, hh, :],
                                                                           scalar=ncum[0:T, h:h + 1], in1=ngm[0:T, 0:T],
                                                                           op0=ALU.add, op1=ALU.add),
                              r=["Em", "ncum", "cst"], w=["Dm"])
                    S.act(lambda e: e.activation(out=Em, in_=Em, func=AF.Exp), r=["Em"], w=["Em"])
                    S.act(lambda e: e.activation(out=Dm[0:T], in_=Dm[0:T], func=AF.Exp), r=["Dm"], w=["Dm"])
                    S.pool(lambda e, g=g: e.tensor_tensor(out=Mm[0:T], in0=Dm[0:T],
                                                         in1=cbT[0:T, g, 0:T].unsqueeze(1).to_broadcast([T, 4, T]), op=ALU.mult),
                          r=["Dm", "cbT%d" % g], w=["Mm"])
                    S.pool(lambda e, g=g, q4=q4: e.tensor_tensor(out=(Chs[:, q4 * 4:q4 * 4 + 4, :] if samp else Chp), in0=Em,
                                                                in1=Cb[:, g, 0:T].unsqueeze(1).to_broadcast([128, 4, T]), op=ALU.mult),
                          r=["Em", "Cb%d" % g], w=["Ch"])
                    yield
                    for hh in range(4):
                        h = q4 * 4 + hh
                        c = h // 2
                        h2 = h % 2
                        po = pT[64 * h2:64 * h2 + 64, c * 128:c * 128 + T]
                        S.pe(lambda e, h=h, hh=hh, po=po: e.matmul(po, lhsT=xdt[0:T, h * 64:(h + 1) * 64], rhs=Mm[0:T, hh, :],
                                                                   start=True, stop=samp), r=["xdt", "Mm"], w=["pT"])
                        if not samp:
                            S.pe(lambda e, h=h, hh=hh, po=po: e.matmul(po, lhsT=hTb[:, h * 64:(h + 1) * 64], rhs=Chp[:, hh, :],
                                                                       start=False, stop=True), r=["hTb", "Ch"], w=["pT"])
                    yield

            def late_outputs():
                M = T if samp else 3
                t0 = 0 if samp else 125
                if samp or last:
                    for blk, col0 in enumerate((0, 512, 3072, 3584, 4096)):
                        for k in range(8):
                            S.pe(lambda e, k=k, col0=col0: e.matmul(pO[0:M, 0:512], lhsT=hTt[:, k, t0:t0 + M],
                                                                    rhs=w_in_sb[:, k, col0:col0 + 512], start=(k == 0), stop=(k == 7)),
                                 r=["hTt", "w_in"], w=["pO"])
                        S.dve(lambda e, blk=blk: e.tensor_copy(out=stg[0:M, blk * 512:(blk + 1) * 512], in_=pO[0:M, 0:512]),
                              r=["pO"], w=["stg"] + STGW)
                if last:
                    S.dma(lambda e: e.dma_start(out=o_plc, in_=stg[0:3, 0:1024]), "o_plc", r=["stg"])
                    S.dma(lambda e: e.dma_start(out=o_psc, in_=stg[0:3, 1024:2560]), "o_psc", r=["stg"])
                if samp:
                    for s in range(NS):
                        S.dma(lambda e, s=s: e.dma_start(out=o_slc[s], in_=stg[4 * s + 1:4 * s + 4, 0:1024]), "o_slc", r=["stg"])
                        S.dma(lambda e, s=s: e.dma_start(out=o_ssc[s], in_=stg[4 * s + 1:4 * s + 4, 1024:2560]), "o_ssc", r=["stg"])
                if last:
                    S.pe(lambda e: e.transpose(out=pC[0:8, 0:128], in_=hstate, identity=ident), r=["hstate", "cst"], w=["pC", "pCx"])
                    S.act(lambda e: e.activation(out=stT[0:8, 0:128], in_=pC[0:8, 0:128], func=AF.Copy), r=["pC"], w=["stg"] + STGW)
                    S.dma(lambda e: e.dma_start(out=o_plh, in_=stT[0:8, 0:128]), "o_plh", r=["stg"])
                if samp:
                    for c in range(8):
                        S.pe(lambda e, c=c: e.transpose(out=pT[0:NS, c * 128:(c + 1) * 128], in_=hfin[:, c, :], identity=ident),
                             r=["hfin", "cst"], w=["pT"])
                    S.act(lambda e: e.activation(out=lh_in[0:NS, :], in_=pT[0:NS, :], func=AF.Copy), r=["pT"], w=["stg"] + STGW)
                    S.dma(lambda e: e.dma_start(out=o_slh, in_=lh_in[0:NS, :]), "o_slh", r=["stg"])


            def genP():
                S.dma(lambda e: e.dma_start(out=xt[0:T, :], in_=xsrc), "xt", w=["xt"])
                rms_rstd(xt, T, "xt", junk, ss, rstd)
                S.act(lambda e: e.activation(out=xn[0:T, :], in_=xt[0:T, :], func=AF.Copy, scale=rstd[0:T, 0:1]),
                      r=["xt", "rstd"], w=["xn"])
                to_fm(T, "GM", hTt, "hTt")

                if samp:
                    S.dma(lambda e: e.dma_start(out=lc_in[0:48, :], in_=st_lc), "stg", w=["stg"])
                    S.dma(lambda e: e.dma_start(out=sc_in[0:48, :], in_=st_sc), "stg", w=["stg"])
                    S.dma(lambda e: e.dma_start(out=lh_in[64:64 + NS, :], in_=st_lh), "stg", w=["stg"])
                    for c in range(8):
                        S.pe(lambda e, c=c: e.transpose(out=pC[:, 0:48], in_=lc_in[0:48, c * 128:(c + 1) * 128],
                                                        identity=ident[0:48, 0:48]), r=["stg", "cst"], w=["pC"])
                        S.act(lambda e, c=c: e.activation(out=lxs[:, c, :, 0:3],
                                                          in_=pC[:, 0:48].rearrange("p (s j) -> p s j", s=NS),
                                                          func=AF.Copy), r=["pC"], w=["lx%d" % c])
                        S.pe(lambda e, c=c: e.transpose(out=pD[:, 0:NS], in_=lh_in[64:64 + NS, c * 128:(c + 1) * 128],
                                                        identity=ident[64:64 + NS, 64:64 + NS]), r=["stg", "cst"], w=["pD"])
                        S.dve(lambda e, c=c: e.tensor_copy(out=h0s[:, c, :], in_=pD[:, 0:NS]), r=["pD"], w=["h0s"])
                    for c in range(12):
                        S.pe(lambda e, c=c: e.transpose(out=pC[:, 0:48], in_=sc_in[0:48, c * 128:(c + 1) * 128],
                                                        identity=ident[0:48, 0:48]), r=["stg", "cst"], w=["pC"])
                        S.act(lambda e, c=c: e.activation(out=xcs[:, c, :, 0:3],
                                                          in_=pC[:, 0:48].rearrange("p (s j) -> p s j", s=NS),
                                                          func=AF.Copy), r=["pC"], w=["xc%d" % c])

                yield
                yield from inter(g_lrux(), g_z())
                yield from inter(g_xbc())
                yield from inter(g_lru((0, 2, 4, 6), pA[0], "pA0", "pA0"), g_lru((1, 3, 5, 7), pA[1], "pA1", "pA1"))
                yield from inter(g_gate())
                norm_apply(T, EPS, hs, "yl%d", "GL", ynl, "ynl%d", 0, 8, rbc, "rbc", pD[:, 128:128 + T], "pDn")
                yield
                if samp:
                    late_outputs()

            def genS():
                yield from inter(g_ssd())
                if samp:
                    ssd_sample_states_prep()

                for c in range(8):
                    S.dve(lambda e, c=c: e.scalar_tensor_tensor(out=xsf[:, c, 0:T], in0=xsf[:, c, 0:T], scalar=P("DS", c),
                                                                in1=pT[:, c * 128:c * 128 + T], op0=ALU.mult, op1=ALU.add),
                          r=["pT", "xsf%d" % c, "pfm"], w=["xsf%d" % c])
                if samp:
                    S.dve(lambda e: e.tensor_tensor(out=xsf[:, :, 0:T], in0=xsf[:, :, 0:T], in1=pyo_sb, op=ALU.add),
                          r=["xsf%d" % c for c in range(8)] + ["pyo_sb"], w=["xsf%d" % c for c in range(8)])
                S.dve(lambda e: e.tensor_tensor(out=xsf[:, :, 0:T], in0=xsf[:, :, 0:T], in1=zs[:, :, 0:T], op=ALU.mult),
                      r=["xsf%d" % c for c in range(8)] + ["zs%d" % c for c in range(8)], w=["yg%d" % c for c in range(8)] + ["xsf%d" % c for c in range(8)])
                yield
                if not samp:
                    S.act(lambda e: e.activation(out=hTb, in_=hT, func=AF.Copy), r=["hT"], w=["hTb"])
                    if last:
                        for c in range(8):
                            S.pe(lambda e, c=c: e.transpose(out=pO[:, c * 128:(c + 1) * 128], in_=hT[:, c * 128:(c + 1) * 128], identity=ident),
                                 r=["hT", "cst"], w=["pO"])
                        S.dve(lambda e: e.tensor_copy(out=stT, in_=pO), r=["pO"], w=["stg"] + STGW)
                        S.dma(lambda e: e.dma_start(out=o_psh.rearrange("(c q) n -> q c n", q=128),
                                                    in_=stT.rearrange("p (c n) -> p c n", c=8)), "o_psh", r=["stg"])
                yield
                for g in range(2):
                    for c in range(4 * g, 4 * g + 4):
                        pp = c % 2
                        S.pool(lambda e, c=c, pp=pp: e.tensor_tensor(out=ysqS[:, pp, 0:T], in0=xsf[:, c, 0:T], in1=xsf[:, c, 0:T], op=ALU.mult),
                               r=["yg%d" % c], w=["ysq%s%d" % (sk, pp)])
                        S.pe(lambda e, c=c, pp=pp, g=g: e.matmul(pO[:, 0:T], lhsT=onesb, rhs=ysqS[:, pp, 0:T],
                                                                 start=(c == 4 * g), stop=(c == 4 * g + 3)),
                             r=["ysq%s%d" % (sk, pp), "onesb"], w=["pO"])
                    norm_apply(T, EPS, xsf, "yg%d", "GS", yns, "yns%d", 4 * g, 4 * g + 4, rbcS, "rbc" + sk, pO[:, 0:T], "pO")

                yield
                for nb in range(2):
                    for kc in range(16):
                        src = ynl if kc < 8 else yns
                        S.pe(lambda e, kc=kc, nb=nb, src=src: e.matmul(pO[0:T, nb * 512:(nb + 1) * 512], lhsT=src[:, kc % 8, 0:T],
                                                                       rhs=w_out_sb[:, kc, nb * 512:(nb + 1) * 512],
                                                                       start=(kc == 0), stop=(kc == 15)),
                             r=[("ynl%d" if kc < 8 else "yns%d") % (kc % 8), "w_out"], w=["pO"])
                S.dve(lambda e: e.tensor_tensor(out=xt[0:T, :], in0=pO[0:T, :], in1=xt[0:T, :], op=ALU.add),
                      r=["pO", "xt"], w=["xt"])
                S.dma(lambda e: e.dma_start(out=scr[row0:row0 + T, :], in_=xt[0:T, :]), "xnew", r=["xt"], w=["scr%d" % mt])

                if last:
                    late_outputs()

            return par, genP, genS

        def ssd_sample_states_prep():
            T = TS
            for i, dx in enumerate((dah, dal)):
                S.dve(lambda e, dx=dx, i=i: e.tensor_tensor(out=damb[0:T, i], in0=dx[0:T, :].unsqueeze(1).to_broadcast([T, NS, 16]),
                                                            in1=blki.unsqueeze(2).to_broadcast([T, NS, 16]), op=ALU.mult),
                      r=["dah", "dal", "cst"], w=["dam%d" % i])
                S.pe(lambda e, i=i: e.matmul(pD[:, 0:256], lhsT=onesb[0:T, :], rhs=damb[0:T, i].rearrange("p s h -> p (s h)"),
                                             start=(i == 0), stop=(i == 1)), r=["dam%d" % i, "onesb"], w=["pD", "pD2", "pD3", "pDn"])
            S.act(lambda e: e.activation(out=dtot.rearrange("p s h -> p (s h)"), in_=pD[:, 0:256], func=AF.Exp),
                  r=["pD"], w=["dtot"])
            S.barrier()
            dtotP = Dmf[:, 0:NS * 8].rearrange("p (s c) -> p s c", s=NS)
            for h2 in range(2):
                S.dve(lambda e, h2=h2: e.tensor_copy(out=dtotP[64 * h2:64 * h2 + 64],
                                                     in_=dtot[64 * h2:64 * h2 + 64].rearrange("p s (c two) -> p s c two", two=2)[:, :, :, h2]),
                      r=["dtot"], w=["dtotP"])
            h0in_b = [h0in, u]
            hout_b = [hout, hs]
            h0b_b = [hTb, ub.rearrange("p c t -> p (c t)")]
            h0Tb_b = [lrut[:, 0:512].bitcast(BF16), lrut[:, 512:1024].bitcast(BF16)]
            xdm_b = [hT[:, 0:512].bitcast(BF16), hT[:, 512:1024].bitcast(BF16)]
            pTr_b = [pC.bitcast(BF16), pD.bitcast(BF16)]
            pTk = [["pC"], ["pD"]]
            for s in range(NS):
                q = s % 2
                hi, ho, h0b, hb, xdm, pTr, tk = h0in_b[q], hout_b[q], h0b_b[q], h0Tb_b[q], xdm_b[q], pTr_b[q], pTk[q]
                S.dma(lambda e, s=s, hi=hi: e.dma_start(out=hi, in_=st_sh[s].rearrange("(c q) n -> q c n", q=128)),
                      "h0in%d" % q, w=["h0in%d" % q], q="pool")
                S.act(lambda e, hi=hi, h0b=h0b: e.activation(out=h0b, in_=hi.rearrange("p c n -> p (c n)"), func=AF.Copy),
                      r=["h0in%d" % q], w=["h0b%d" % q])
                for c in range(8):
                    S.pe(lambda e, c=c, h0b=h0b, pTr=pTr: e.transpose(out=pTr[:, c * 128:(c + 1) * 128], in_=h0b[:, c * 128:(c + 1) * 128],
                                                                      identity=identb), r=["h0b%d" % q, "identb"], w=tk)
                S.dve(lambda e, hb=hb, pTr=pTr: e.tensor_copy(out=hb, in_=pTr[:, 0:1024]), r=tk, w=["h0Tb%d" % q])
                for h in range(16):
                    S.pe(lambda e, h=h, s=s, hb=hb: e.matmul(pA[0][64 * (h % 2):64 * (h % 2) + 64, (h // 2) * TS + 4 * s:(h // 2) * TS + 4 * s + 4],
                                                             lhsT=hb[:, h * 64:(h + 1) * 64], rhs=Chs[:, h, 4 * s:4 * s + 4],
                                                             start=True, stop=True),
                         r=["h0Tb%d" % q, "Ch"], w=["pA0"])
                S.dve(lambda e, s=s, xdm=xdm: e.tensor_scalar(out=xdm[0:T, :], in0=xdd[0:T, :], scalar1=blki[:, s:s + 1], scalar2=None, op0=ALU.mult),
                      r=["xdd", "cst"], w=["xdm%d" % q])
                for c in range(8):
                    S.pe(lambda e, c=c, xdm=xdm: e.matmul(pO[:, c * 128:(c + 1) * 128], lhsT=xdm[0:T, c * 128:(c + 1) * 128],
                                                          rhs=BT[0:T, (c // 4) * 128:(c // 4 + 1) * 128], start=True, stop=True),
                         r=["xdm%d" % q, "BT"], w=["pO"])
                for c in range(8):
                    S.dve(lambda e, c=c, s=s, hi=hi, ho=ho: e.scalar_tensor_tensor(out=ho[:, c, :], in0=hi[:, c, :], scalar=dtotP[:, s, c:c + 1],
                                                                                   in1=pO[:, c * 128:(c + 1) * 128], op0=ALU.mult, op1=ALU.add),
                          r=["h0in%d" % q, "dtotP", "pO"], w=["hout%d" % q])
                S.dma(lambda e, s=s, ho=ho: e.dma_start(out=o_ssh[s].rearrange("(c q) n -> q c n", q=128), in_=ho),
                      "hout%d" % q, r=["hout%d" % q])
            S.act(lambda e: e.activation(out=pyo_sb.rearrange("p c t -> p (c t)"), in_=pA[0][:, 0:8 * TS], func=AF.Copy),
                  r=["pA0"], w=["pyo_sb"])

        S.pool(lambda e: e.memset(lxb, 0.0), w=["lx%d" % c for c in range(8)])
        S.pool(lambda e: e.memset(xcb, 0.0), w=["xc%d" % c for c in range(12)])

        def drive(g_, par):
            S.ctx = par
            try:
                next(g_)
                return True
            except StopIteration:
                return False
            finally:
                S.ctx = None

        tiles = [mixer_tile(mt, False) for mt in range(NT)]
        RATIO = 3
        par0, gP0, _ = tiles[0]
        g = gP0()
        while drive(g, par0):
            pass
        for n in range(NT):
            par, _, gS = tiles[n]
            gs = gS()
            alive_s = True
            alive_p = False
            if n + 1 < NT:
                parn, gPn, _ = tiles[n + 1]
                gp = gPn()
                alive_p = True
            while alive_s or alive_p:
                for _ in range(RATIO):
                    if alive_p:
                        alive_p = drive(gp, parn)
                if alive_s:
                    alive_s = drive(gs, par)
        S.barrier()
        if SAMP:
            pars, gPs, gSs = mixer_tile(SEQ // 128, True)
            for g in (gPs(), gSs()):
                while drive(g, pars):
                    pass

        S.barrier()
        ptr[0] = base0
        w_up_sb = b3(8, DFF)
        w_dn_sb = b3(32, D)
        if MLP:
            k_wup = load_w(w_up_sb, w_up, 8, DFF, "w_up")
            k_wdn = load_w(w_dn_sb, w_down, 32, D, "w_dn")
        T2 = 256
        xt2 = [[f32(D), f32(D)], [f32(D), f32(D)]]
        xn2 = f32(D)
        ss2 = [f32(4), f32(4)]
        rstd2 = [f32(4), f32(4)]
        mT = [b3(8, T2), b3(8, T2)]
        actb = b3(32, T2)
        rl = [f32(T2), f32(T2)]
        yout = f32(D)
        gfin_bc = f32(D)
        S.dma(lambda e: e.dma_start(out=gfin_bc, in_=gfin_d.partition_broadcast(128)), "gfin", w=["gfin"])
        pDN = [PS[:, 3072:4096], PS[:, 2048:3072]]
        pDNk = [["pO"], ["pC", "pD"]]

        def mlp_front(ti, r0, T):
            q = ti % 2
            nsub = (T + 127) // 128
            for j in range(nsub):
                Tj = min(128, T - j * 128)
                xk = "xt2_%d_%d" % (q, j)
                S.dma(lambda e, j=j, Tj=Tj: e.dma_start(out=xt2[q][j][0:Tj, :], in_=scr[r0 + j * 128:r0 + j * 128 + Tj, :]),
                      xk, r=["scr%d" % ((r0 + j * 128) // 128)], w=[xk])
                rms_rstd(xt2[q][j], Tj, xk, xn2, ss2[0], rstd2[0], "2")
                S.act(lambda e, j=j, Tj=Tj: e.activation(out=xn2[0:Tj, :], in_=xt2[q][j][0:Tj, :], func=AF.Copy, scale=rstd2[0][0:Tj, 0:1]),
                      r=[xk, "rstd2"], w=["xn2"])
                for k in range(8):
                    S.pe(lambda e, k=k, Tj=Tj: e.transpose(out=pT[:, k * 128:k * 128 + Tj], in_=xn2[0:Tj, k * 128:(k + 1) * 128],
                                                           identity=ident[0:Tj, 0:Tj]), r=["xn2", "cst"], w=["pT"])
                S.dve(lambda e, j=j, Tj=Tj: e.tensor_tensor(
                    out=mT[q][:, :, j * 128:j * 128 + Tj], in0=pT.rearrange("p (k t) -> p k t", k=8)[:, :, 0:Tj],
                    in1=P("GP", 0, 8).unsqueeze(2).to_broadcast([128, 8, Tj]), op=ALU.mult),
                    r=["pT", "pfm"], w=["mT%d" % q])
            yield
            for f in range(32):
                pa = pA[f % 2]
                for k in range(8):
                    S.pe(lambda e, k=k, f=f, pa=pa: e.matmul(pa[:, 0:T], lhsT=w_up_sb[:, k, f * 128:(f + 1) * 128], rhs=mT[q][:, k, 0:T],
                                                             start=(k == 0), stop=(k == 7)),
                         r=["mT%d" % q, "w_up"], w=["pA%d" % (f % 2)])
                S.act(lambda e, f=f, pa=pa: e.activation(out=rl[f % 2][:, 0:T], in_=pa[:, 0:T], func=AF.Relu),
                      r=["pA%d" % (f % 2)], w=["rl%d" % (f % 2)])
                S.pool(lambda e, f=f: e.tensor_tensor(out=actb[:, f, 0:T], in0=rl[f % 2][:, 0:T], in1=rl[f % 2][:, 0:T], op=ALU.mult),
                       r=["rl%d" % (f % 2)], w=["act%d" % f])
                yield

        def mlp_back(ti, r0, T):
            q = ti % 2
            nsub = (T + 127) // 128
            for f in range(32):
                for j in range(nsub):
                    Tj = min(128, T - j * 128)
                    for nb in range(2):
                        S.pe(lambda e, f=f, nb=nb, j=j, Tj=Tj: e.matmul(pDN[j][0:Tj, nb * 512:(nb + 1) * 512],
                                                                        lhsT=actb[:, f, j * 128:j * 128 + Tj],
                                                                        rhs=w_dn_sb[:, f, nb * 512:(nb + 1) * 512],
                                                                        start=(f == 0), stop=(f == 31)),
                             r=["act%d" % f, "w_dn"], w=pDNk[j])
                yield
            for j in range(nsub):
                Tj = min(128, T - j * 128)
                xk = "xt2_%d_%d" % (q, j)
                S.dve(lambda e, j=j, Tj=Tj: e.tensor_tensor(out=xt2[q][j][0:Tj, :], in0=pDN[j][0:Tj, :], in1=xt2[q][j][0:Tj, :], op=ALU.add),
                      r=pDNk[j] + [xk], w=[xk])
                rms_rstd(xt2[q][j], Tj, xk, yout, ss2[1], rstd2[1], "2b", jkey="yout")
                S.dve(lambda e, j=j, Tj=Tj: e.scalar_tensor_tensor(out=yout[0:Tj, :], in0=xt2[q][j][0:Tj, :], scalar=rstd2[1][0:Tj, 0:1],
                                                                   in1=gfin_bc[0:Tj, :], op0=ALU.mult, op1=ALU.mult),
                      r=[xk, "rstd2b", "gfin"], w=["yout"])
                rr = r0 + j * 128
                if rr < SEQ:
                    S.dma(lambda e, rr=rr, Tj=Tj: e.dma_start(out=y_p[rr:rr + Tj, :], in_=yout[0:Tj, :]), "yout", r=["yout"])
                else:
                    S.dma(lambda e, Tj=Tj: e.dma_start(out=y_s, in_=yout[0:Tj, :]), "yout", r=["yout"])
                yield

        if MLP:
            jobs = [(t * T2, T2) for t in range(NT * 128 // T2)]
            if SAMP:
                jobs.append((SEQ, TS))
            for _ in mlp_front(0, *jobs[0]):
                pass
            for ti in range(len(jobs)):
                gb = mlp_back(ti, *jobs[ti])
                gf = mlp_front(ti + 1, *jobs[ti + 1]) if ti + 1 < len(jobs) else iter(())
                ab = af = True
                while ab or af:
                    if ab:
                        ab = next(gb, "END") != "END"
                    if af:
                        af = next(gf, "END") != "END"

        S.emit()
    return nc


_CACHE = {}


def _consts():
    c = np.zeros((128, NCST), np.float32)
    i = np.arange(128)
    c[:, CI:CI + 128] = np.eye(128, dtype=np.float32)
    c[:, CU:CU + 128] = (i[:, None] <= i[None, :]).astype(np.float32)
    c[:, CN:CN + 128] = np.where(i[:, None] <= i[None, :], 0.0, NEG).astype(np.float32)
    c[:, CO:CO + 128] = 1.0
    j = np.arange(TS)
    same = (j[:, None] // 4) == (j[None, :] // 4)
    caus = j[:, None] <= j[None, :]
    c[0:TS, CUB:CUB + TS] = (same & caus).astype(np.float32)
    c[0:TS, CNB:CNB + TS] = np.where(same & caus, 0.0, NEG).astype(np.float32)
    c[0:TS, CBM:CBM + TS] = same.astype(np.float32)
    c[0:TS, CBI:CBI + NS] = ((j[:, None] // 4) == np.arange(NS)[None, :]).astype(np.float32)
    return c


def _fm(v, nch):
    return np.ascontiguousarray(np.asarray(v, np.float32).reshape(nch, 128).T)


def kernel(x_prompt, x_sample, state_lru_conv, state_lru_h, state_ssd_conv, state_ssd_h,
           g_mix, w_in, lru_conv_w, lru_conv_b, w_a, b_a, w_x, b_x, lam, g_lru_out,
           ssd_conv_w, ssd_conv_b, dt_bias, a_log, d_skip, g_ssd_out, w_out,
           g_mlp, w_up, w_down, g_final):
    f = lambda a: np.ascontiguousarray(np.asarray(a, np.float32))
    if "nc" not in _CACHE:
        _CACHE["nc"] = build_program()
    nc = _CACHE["nc"]
    pfm = np.zeros((128, NPAR), np.float32)
    lw = np.asarray(lru_conv_w[0], np.float32)
    pfm[:, PC["LW"]:PC["LW"] + 32] = lw.reshape(4, 8, 128).transpose(2, 1, 0).reshape(128, 32)
    pfm[:, PC["LB"]:PC["LB"] + 8] = _fm(lru_conv_b[0], 8)
    pfm[:, PC["BA"]:PC["BA"] + 8] = _fm(np.asarray(b_a[0]).reshape(-1), 8)
    pfm[:, PC["BX"]:PC["BX"] + 8] = _fm(np.asarray(b_x[0]).reshape(-1), 8)
    pfm[:, PC["LAM"]:PC["LAM"] + 8] = _fm(lam[0], 8)
    pfm[:, PC["GL"]:PC["GL"] + 8] = _fm(g_lru_out[0], 8)
    sw = np.asarray(ssd_conv_w[0], np.float32)
    pfm[:, PC["SW"]:PC["SW"] + 48] = sw.reshape(4, 12, 128).transpose(2, 1, 0).reshape(128, 48)
    pfm[:, PC["SB"]:PC["SB"] + 12] = _fm(ssd_conv_b[0], 12)
    pfm[:, PC["DS"]:PC["DS"] + 8] = _fm(np.repeat(np.asarray(d_skip[0], np.float32), 64), 8)
    pfm[:, PC["GS"]:PC["GS"] + 8] = _fm(g_ssd_out[0], 8)
    pfm[:, PC["GM"]:PC["GM"] + 8] = _fm(g_mix[0], 8)
    pfm[:, PC["GP"]:PC["GP"] + 8] = _fm(g_mlp[0], 8)
    cst = _consts()
    shared = {
        "w_in": f(w_in[0]), "w_out": f(w_out[0]), "w_up": f(w_up[0]), "w_down": f(w_down[0]),
        "w_a": f(w_a[0]), "w_x": f(w_x[0]), "pfm": pfm, "cst": cst,
        "dt_bias": f(dt_bias[0]), "a_log": f(a_log[0]), "g_final": f(g_final),
    }
    in_maps = []
    for b in range(NCORES):
        sl = slice(NS * b, NS * (b + 1))
        m = dict(shared)
        m["xp"] = f(x_prompt[b])
        m["xs"] = f(np.asarray(x_sample[sl]).reshape(TS, D))
        m["st_lc"] = f(np.asarray(state_lru_conv[0, sl]).reshape(NS * 3, D))
        m["st_lh"] = f(state_lru_h[0, sl])
        m["st_sc"] = f(np.asarray(state_ssd_conv[0, sl]).reshape(NS * 3, XBC))
        m["st_sh"] = f(np.asarray(state_ssd_h[0, sl]).reshape(NS, 1024, 128))
        in_maps.append(m)
    res = run_bass_kernel_spmd(nc, in_maps, core_ids=list(range(NCORES)))
    R = res.results
    cat = lambda k: np.stack([np.asarray(R[b][k], np.float32) for b in range(NCORES)])
    y_prompt = cat("y_p")
    y_sample = cat("y_s").reshape(NCORES * NS, 4, D)
    p_lc = cat("o_plc")[None]
    p_lh = cat("o_plh").reshape(NCORES, D)[None]
    p_sc = cat("o_psc")[None]
    p_sh = cat("o_psh").reshape(NCORES, 16, 64, 128)[None]
    s_lc = cat("o_slc").reshape(NCORES * NS, 3, D)[None]
    s_lh = cat("o_slh").reshape(NCORES * NS, D)[None]
    s_sc = cat("o_ssc").reshape(NCORES * NS, 3, XBC)[None]
    s_sh = cat("o_ssh").reshape(NCORES * NS, 16, 64, 128)[None]
    return (y_prompt, y_sample, p_lc, p_lh, p_sc, p_sh, s_lc, s_lh, s_sc, s_sh)
```

```python
import math
from contextlib import ExitStack

import numpy as np
import concourse.bass as bass
import concourse.mybir as mybir
from concourse.bass_utils import run_bass_kernel_spmd

F32 = mybir.dt.float32
BF16 = mybir.dt.bfloat16
AF = mybir.ActivationFunctionType
ALU = mybir.AluOpType

NCORES = 8
D = 1024
SEQ = 2048
NS = 16
TS = 64
XBC = 1536
INP = 4624
DFF = 4096
EPS = 1e-6
NEG = -30000.0

import re as _re

ENGS = ("pe", "act", "dve", "pool", "sp")
SAME_ENGINE_SYNC = {"pe": False, "act": True, "dve": True, "pool": True, "sp": False}


class Op:
    __slots__ = ("eng", "fn", "deps", "marked", "count", "dma_key", "dma_val")

    def __init__(self, eng, fn, dma_key=None):
        self.eng = eng
        self.fn = fn
        self.deps = ()
        self.marked = False
        self.count = 0
        self.dma_key = dma_key
        self.dma_val = 0


class Sched:
    def __init__(self, nc):
        self.nc = nc
        self.ops = {e: [] for e in ENGS}
        self.last_w = {}
        self.readers = {}
        self.dma_cnt = {}
        self.pending = {}
        self.ctx = None
        self.since_bar = []

    ALIAS = {"pC": "b4", "pCx": "b4", "pD": "b5", "pD2": "b5", "pD3": "b5", "pDn": "b5",
             "pDcb0": "b5", "pDcb1": "b5", "pT": "b01", "pO": "b67", "pA0": "b2", "pA1": "b3"}

    PSUM_KEYS = {"b01", "b2", "b3", "b4", "b5", "b67"}

    PAR_RE = _re.compile(r"^(xt|dtt|xnew)$|^(xsf|zs|Bb|Cb|ynl|yg)\d+$")

    def _k(self, k):
        k = self.ALIAS.get(k, k)
        if self.ctx is not None and self.PAR_RE.match(k):
            return k + "#" + str(self.ctx)
        return k

    def add(self, eng, fn, reads=(), writes=(), dma_key=None):
        reads = [self._k(k) for k in reads]
        writes = [self._k(k) for k in writes]
        if dma_key is not None and self.ctx is not None and self.PAR_RE.match(dma_key):
            dma_key = dma_key + "#" + str(self.ctx)
        op = Op(eng, fn, dma_key)
        deps = []
        seen = set()

        def dep(o):
            if o is not None and o is not op and id(o) not in seen:
                seen.add(id(o))
                deps.append(o)

        if self.pending.get(eng):
            for o in self.pending[eng]:
                dep(o)
            self.pending[eng] = []
        for b in reads:
            dep(self.last_w.get(b))
            if b in self.PSUM_KEYS:
                for r in self.readers.get(b, ()):
                    if r.eng != eng:
                        dep(r)
        for b in writes:
            dep(self.last_w.get(b))
            for r in self.readers.get(b, ()):
                dep(r)
        for b in reads:
            self.readers.setdefault(b, []).append(op)
        for b in writes:
            self.last_w[b] = op
            self.readers[b] = []
        op.deps = deps
        if dma_key is not None:
            self.dma_cnt[dma_key] = self.dma_cnt.get(dma_key, 0) + 16
            op.dma_val = self.dma_cnt[dma_key]
        self.ops[eng].append(op)
        self.since_bar.append(op)
        return op

    def barrier(self):
        ops = []
        for e in ENGS:
            comp = [o for o in self.ops[e] if o.dma_key is None]
            if comp:
                ops.append(comp[-1])
        last_dma = {}
        for o in self.since_bar:
            if o.dma_key is not None:
                last_dma[o.dma_key] = o
        ops.extend(last_dma.values())
        for e in ENGS:
            self.pending.setdefault(e, []).extend(ops)
        self.since_bar = []

    def pe(self, fn, r=(), w=()):
        return self.add("pe", fn, r, w)

    def act(self, fn, r=(), w=()):
        return self.add("act", fn, r, w)

    def dve(self, fn, r=(), w=()):
        return self.add("dve", fn, r, w)

    def pool(self, fn, r=(), w=()):
        return self.add("pool", fn, r, w)

    def dma(self, fn, key, r=(), w=(), q="sp"):
        return self.add(q, fn, r, w, dma_key=key)

    def emit(self):
        nc = self.nc
        for e in ENGS:
            for op in self.ops[e]:
                for d in op.deps:
                    if d.dma_key is None:
                        if d.eng == op.eng and not SAME_ENGINE_SYNC[d.eng]:
                            continue
                        d.marked = True
        for e in ENGS:
            c = 0
            for op in self.ops[e]:
                if op.dma_key is None and op.marked:
                    c += 1
                    op.count = c
        with ExitStack() as st:
            esem = {e: st.enter_context(nc.semaphore("es_" + e)) for e in ENGS}
            dsem = {}
            for k in self.dma_cnt:
                dsem[k] = st.enter_context(nc.semaphore("ds_%d" % len(dsem)))
            block = st.enter_context(nc.Block())

            def run(ename, eng):
                seen = {}
                for op in self.ops[ename]:
                    need = {}
                    for d in op.deps:
                        if d.dma_key is not None:
                            key = ("d", d.dma_key)
                            val = d.dma_val
                            sem = dsem[d.dma_key]
                        else:
                            if d.eng == ename and not SAME_ENGINE_SYNC[ename]:
                                continue
                            key = ("e", d.eng)
                            val = d.count
                            sem = esem[d.eng]
                        if key not in need or need[key][1] < val:
                            need[key] = (sem, val)
                    for key, (sem, val) in need.items():
                        if seen.get(key, 0) >= val:
                            continue
                        seen[key] = val
                        eng.wait_ge(sem, val)
                    ins = op.fn(eng)
                    if op.dma_key is not None:
                        ins.then_inc(dsem[op.dma_key], 16)
                    elif op.marked:
                        ins.then_inc(esem[ename], 1)
                if ename == "sp":
                    for k, v in self.dma_cnt.items():
                        eng.wait_ge(dsem[k], v)

            @block.sync
            def _(e):
                run("sp", e)

            @block.tensor
            def _(e):
                run("pe", e)

            @block.scalar
            def _(e):
                run("act", e)

            @block.vector
            def _(e):
                run("dve", e)

            @block.gpsimd
            def _(e):
                run("pool", e)


PC = {}
_o = 0
for _n, _w in (("LW", 32), ("LB", 8), ("BA", 8), ("BX", 8), ("LAM", 8), ("GL", 8), ("SW", 48),
               ("SB", 12), ("DS", 8), ("GS", 8), ("GM", 8), ("GP", 8)):
    PC[_n] = _o
    _o += _w
NPAR = _o
CI, CU, CN, CO, CUB, CNB, CBM, CBI = 0, 128, 256, 384, 512, 576, 640, 704
NCST = 720


def build_program(NT=SEQ // 128, SAMP=True, MLP=True, DBG=False, STAGE=9):
    nc = bass.Bass("TRN2", target_bir_lowering=False)
    S = Sched(nc)

    def din(name, shape):
        return nc.dram_tensor(name, list(shape), F32, kind="ExternalInput").ap()

    def dout(name, shape):
        return nc.dram_tensor(name, list(shape), F32, kind="ExternalOutput").ap()

    xp = din("xp", (SEQ, D))
    xs = din("xs", (TS, D))
    st_lc = din("st_lc", (NS * 3, D))
    st_lh = din("st_lh", (NS, D))
    st_sc = din("st_sc", (NS * 3, XBC))
    st_sh = din("st_sh", (NS, 1024, 128))
    w_in = din("w_in", (D, INP))
    w_out = din("w_out", (2 * D, D))
    w_up = din("w_up", (D, DFF))
    w_down = din("w_down", (DFF, D))
    w_a = din("w_a", (16, 64, 64))
    w_x = din("w_x", (16, 64, 64))
    pfm_d = din("pfm", (128, NPAR))
    cst_d = din("cst", (128, NCST))
    dtb_d = din("dt_bias", (16,))
    alog_d = din("a_log", (16,))
    gfin_d = din("g_final", (D,))

    y_p = dout("y_p", (SEQ, D))
    y_s = dout("y_s", (TS, D))
    o_plc = dout("o_plc", (3, D))
    o_plh = dout("o_plh", (8, 128))
    o_psc = dout("o_psc", (3, XBC))
    o_psh = dout("o_psh", (1024, 128))
    o_slc = dout("o_slc", (NS, 3, D))
    o_slh = dout("o_slh", (NS, D))
    o_ssc = dout("o_ssc", (NS, 3, XBC))
    o_ssh = dout("o_ssh", (NS, 1024, 128))
    scr = nc.dram_tensor("scr", [SEQ + TS, D], F32, kind=("ExternalOutput" if DBG else "Internal")).ap()

    st = ExitStack()
    with st:
        RW = 53200
        R = st.enter_context(nc.sbuf_tensor("R", [128, RW], F32))
        PS = st.enter_context(nc.psum_tensor("PS", [128, 4096], F32))
        ptr = [0]

        def alloc(nwords):
            a = ptr[0]
            ptr[0] += (nwords + 7) // 8 * 8
            pass
            return a

        def f32(n):
            a = alloc(n)
            return R[:, a:a + n]

        def bf(n):
            w = (n + 1) // 2
            a = alloc(w)
            return R[:, a:a + w].bitcast(BF16)[:, 0:n]

        def f3(c, t):
            return f32(c * t).rearrange("p (c t) -> p c t", c=c)

        def b3(c, t):
            return bf(c * t).rearrange("p (c t) -> p c t", c=c)

        def bank(b, n=512):
            return PS[:, 512 * b:512 * b + n]

        pT = PS[:, 0:1024]
        pTb = pT.bitcast(BF16)
        pA = [bank(2), bank(3)]
        pC = bank(4)
        pCb = pC.bitcast(BF16)
        pD = bank(5)
        pO = PS[:, 3072:4096]

        cst = f32(NCST)
        pfm = f32(NPAR)
        dtb_bc = f32(16)
        a_bc = f32(16)
        identb = bf(128)
        onesb = bf(128)
        Utrib = bf(128)
        mskb = bf(3 * TS)
        dah = bf(16)
        dal = bf(16)
        cfac = f32(8)
        c2fac = f32(8)
        tiny = f32(8)
        mhalf = f32(4)
        eps_t = f32(4)
        nbias = f32(16)
        wa_blk = b3(8, 128)
        wx_blk = b3(8, 128)
        hstate = f32(8)
        hT = f32(1024)
        hTb = bf(1024)

        ident = cst[:, CI:CI + 128]
        Utri = cst[:, CU:CU + 128]
        negm = cst[:, CN:CN + 128]
        onesf = cst[:, CO:CO + 128]
        Ublk = cst[0:TS, CUB:CUB + TS]
        negblk = cst[0:TS, CNB:CNB + TS]
        blkm = cst[0:TS, CBM:CBM + TS]
        blki = cst[0:TS, CBI:CBI + NS]

        S.dma(lambda e: e.dma_start(out=cst, in_=cst_d), "cst", w=["cst"])
        S.dma(lambda e: e.dma_start(out=pfm, in_=pfm_d), "pfm", w=["pfm"])
        S.dma(lambda e: e.dma_start(out=dtb_bc, in_=dtb_d.partition_broadcast(128)), "dtb", w=["dtb"])
        S.dma(lambda e: e.dma_start(out=a_bc, in_=alog_d.partition_broadcast(128)), "alog", w=["a_bc"])
        S.dve(lambda e: e.tensor_copy(out=identb, in_=ident), r=["cst"], w=["identb"])
        S.dve(lambda e: e.tensor_copy(out=onesb, in_=onesf), r=["cst"], w=["onesb"])
        S.dve(lambda e: e.tensor_copy(out=Utrib, in_=Utri), r=["cst"], w=["mskb"])
        S.dve(lambda e: e.tensor_copy(out=mskb[0:TS, 0:TS], in_=Ublk), r=["cst"], w=["mskb"])
        S.dve(lambda e: e.tensor_copy(out=mskb[0:TS, TS:2 * TS], in_=blkm), r=["cst"], w=["mskb"])
        S.pool(lambda e: e.memset(mhalf, -0.5), w=["mhalf"])
        S.pool(lambda e: e.memset(eps_t, EPS), w=["eps_t"])
        S.dve(lambda e: e.tensor_scalar(out=nbias[:, 0:8], in0=pfm[:, PC["BA"]:PC["BA"] + 8], scalar1=-1.0, scalar2=None, op0=ALU.mult), r=["pfm"], w=["nbias"])
        S.dve(lambda e: e.tensor_scalar(out=nbias[:, 8:16], in0=pfm[:, PC["BX"]:PC["BX"] + 8], scalar1=-1.0, scalar2=None, op0=ALU.mult), r=["pfm", "nbias"], w=["nbias"])
        S.pool(lambda e: e.memset(hstate, 0.0), w=["hstate"])
        S.pool(lambda e: e.memset(hT, 0.0), w=["hT"])
        S.pool(lambda e: e.memset(hTb, 0.0), w=["hTb"])
        S.pool(lambda e: e.memset(wa_blk, 0.0), w=["wa"])
        S.pool(lambda e: e.memset(wx_blk, 0.0), w=["wx"])
        S.act(lambda e: e.activation(out=a_bc, in_=a_bc, func=AF.Exp), r=["a_bc"], w=["a_bc"])
        S.dve(lambda e: e.tensor_scalar(out=a_bc, in0=a_bc, scalar1=-1.0, scalar2=None, op0=ALU.mult), r=["a_bc"], w=["a_bc"])
        lam = pfm[:, PC["LAM"]:PC["LAM"] + 8]
        S.act(lambda e: e.activation(out=tiny, in_=lam, func=AF.Exp, scale=-1.0), r=["pfm"], w=["tiny"])
        S.act(lambda e: e.activation(out=tiny, in_=tiny, func=AF.Ln, bias=1.0), r=["tiny"], w=["tiny"])
        S.dve(lambda e: e.tensor_scalar(out=cfac, in0=tiny, scalar1=-8.0, scalar2=None, op0=ALU.mult), r=["tiny"], w=["cfac"])
        S.dve(lambda e: e.tensor_scalar(out=c2fac, in0=tiny, scalar1=-16.0, scalar2=None, op0=ALU.mult), r=["tiny"], w=["cfac2"])
        for (wd, blk, nm) in ((w_a, wa_blk, "wa"), (w_x, wx_blk, "wx")):
            v = wd.rearrange("(c h) i j -> h i c j", h=2)
            for h2 in range(2):
                S.dma(lambda e, v=v, blk=blk, h2=h2: e.dma_start(
                    out=blk[64 * h2:64 * h2 + 64, :, 64 * h2:64 * h2 + 64], in_=v[h2]),
                    nm + str(h2), w=[nm], q="pool")

        base0 = ptr[0]

        def load_w(dst3, src2, nk, ncol, name, step=2048):
            sv = src2.rearrange("(k p) n -> p k n", p=128)
            pieces = [(k, c0, min(ncol, c0 + step)) for k in range(nk) for c0 in range(0, ncol, step)]
            for i, (k, c0, c1) in enumerate(pieces):
                S.dma(lambda e, k=k, c0=c0, c1=c1: e.dma_start(out=dst3[:, k, c0:c1], in_=sv[:, k, c0:c1]),
                      name, w=([name] if i == len(pieces) - 1 else []), q="pool")
            return name

        w_in_sb = b3(8, INP)
        w_out_sb = b3(16, D)
        if STAGE >= 1:
            k_win = load_w(w_in_sb, w_in, 8, INP, "w_in")
            k_wout = load_w(w_out_sb, w_out, 16, D, "w_out")

        xt = f32(D)
        xn = f32(D)
        junk = xn
        ss = f32(4)
        rstd = f32(4)
        hTt = b3(8, 128)
        lxb = f3(8, 131)
        xcb = f3(12, 131)
        sreg = f32(20 * NS * 7)
        lxs = sreg[:, 0:8 * NS * 7].rearrange("p (c s l) -> p c s l", c=8, s=NS)
        xcs = sreg[:, 8 * NS * 7:20 * NS * 7].rearrange("p (c s l) -> p c s l", c=12, s=NS)
        gl = f3(2, 128)
        zs = f3(8, 128)
        u = f3(8, 128)
        ub = b3(8, 128)
        lrut = f32(1280)
        gi = lrut[:, 0:512].rearrange("p (c t) -> p c t", c=4)
        av = lrut[:, 512:768].rearrange("p (c t) -> p c t", c=2)
        a2 = lrut[:, 768:1024].rearrange("p (c t) -> p c t", c=2)
        tmpb = lrut[:, 1024:1280].rearrange("p (c t) -> p c t", c=2)
        hs = f3(8, 128)
        ysq = b3(2, 128)
        rbc = f32(128)
        ynl = b3(8, 128)
        yns = b3(8, 128)
        xsf = f3(8, 128)
        Bb = b3(2, 128)
        Cb = b3(2, 128)
        dtr = f32(16)
        dtt = f32(16)
        da = f32(16)
        ncum = f32(16)
        dte = f32(16)
        cdec = f32(16)
        xdt = bf(1024)
        xdd = bf(1024)
        BT = bf(256)
        cbT = f3(2, 128)
        Dmf = f32(512)
        Emf = f32(512)
        Mmf = bf(512)
        Chf = bf(1024)
        Chp = Chf[:, 0:512].rearrange("p (a t) -> p a t", a=4)
        Chs = Chf.rearrange("p (a t) -> p a t", a=16)
        cvt = f3(2, 128)
        stg = f32(2560)
        stT = stg[:, 0:1024]
        lc_in = stg[:, 0:1024]
        sc_in = stg[:, 1024:2560]
        lh_in = stg[:, 0:1024]
        h0in = lxb.rearrange("p c t -> p (c t)")[:, 0:1024].rearrange("p (c t) -> p c t", c=8)
        h0Tb = hTb
        Bm = bf(256)
        pyo_f = f32(8 * TS)
        pyo_sb = pyo_f.rearrange("p (c t) -> p c t", c=8)
        damb = bf(2 * NS * 16).rearrange("p (i s h) -> p i s h", i=2, s=NS)
        dtot = f3(NS, 16)
        hnew = hT
        hout = xcb.rearrange("p c t -> p (c t)")[:, 0:1024].rearrange("p (c t) -> p c t", c=8)
        h0s = f3(8, NS)
        hfin = f3(8, NS)

        bf3 = lambda ap, c: ap.bitcast(BF16).rearrange("p (c t) -> p c t", c=c)
        xt_b = [xt, stg[:, 0:1024]]
        ynl_b = [ynl, bf3(stg[:, 1024:1536], 8)]
        Bb_b = [Bb, bf3(stg[:, 1536:1664], 2)]
        Cb_b = [Cb, bf3(stg[:, 1664:1792], 2)]
        dtt_b = [dtt, stg[:, 1792:1808]]
        xsf_b = [xsf, sreg[:, 0:1024].rearrange("p (c t) -> p c t", c=8)]
        zs_b = [zs, sreg[:, 1024:2048].rearrange("p (c t) -> p c t", c=8)]
        Wl_S = pyo_f[:, 0:256].bitcast(BF16)
        ysq_S = bf3(pyo_f[:, 256:384], 2)
        rbc_S = pyo_f[:, 384:512]
        STGW = [k + "#1" for k in ["xt", "dtt"] + ["ynl%d" % c for c in range(8)] + ["Bb0", "Bb1", "Cb0", "Cb1"]]
        STG_ALIAS = ["xt", "dtt"] + ["ynl%d" % c for c in range(8)] + ["Bb0", "Bb1", "Cb0", "Cb1"]

        def P(name, c=None, w=1):
            o = PC[name] + (0 if c is None else c * w)
            return pfm[:, o:o + w]

        def rms_rstd(xtile, T, keyx, junk, ss, rstd, sfx="", jkey=None):
            S.act(lambda e: e.activation(out=junk[0:T, :], in_=xtile[0:T, :], func=AF.Square, accum_out=ss[0:T, 0:1]),
                  r=[keyx], w=[jkey or ("xn" + sfx), "ss" + sfx])
            S.act(lambda e: e.activation(out=ss[0:T, 0:1], in_=ss[0:T, 0:1], func=AF.Ln, scale=1.0 / D, bias=eps_t[0:T, 0:1]),
                  r=["ss" + sfx, "eps_t"], w=["ss" + sfx])
            S.act(lambda e: e.activation(out=rstd[0:T, 0:1], in_=ss[0:T, 0:1], func=AF.Exp, scale=-0.5),
                  r=["ss" + sfx], w=["rstd" + sfx])

        pAA = PS[:, 1024:2048]

        def to_fm(T, gname, dst, dkey):
            for k in range(8):
                S.pe(lambda e, k=k: e.transpose(out=pAA[:, k * 128:k * 128 + T], in_=xn[0:T, k * 128:(k + 1) * 128],
                                                identity=ident[0:T, 0:T]), r=["xn", "cst"], w=["pA0", "pA1"])
            S.dve(lambda e: e.tensor_tensor(
                out=dst[:, :, 0:T], in0=pAA.rearrange("p (k t) -> p k t", k=8)[:, :, 0:T],
                in1=P(gname, 0, 8).unsqueeze(2).to_broadcast([128, 8, T]), op=ALU.mult),
                r=["pA0", "pA1", "pfm"], w=[dkey])

        def mixer_tile(mt, samp):
            T = TS if samp else 128
            row0 = SEQ if samp else mt * 128
            xsrc = xs if samp else xp[mt * 128:(mt + 1) * 128, :]
            last = (not samp) and mt == NT - 1
            par = 0 if samp else (NT - 1 - mt) % 2
            xt, ynl, Bb, Cb, dtt, xsf, zs = (xt_b[par], ynl_b[par], Bb_b[par], Cb_b[par], dtt_b[par], xsf_b[par], zs_b[par])
            Wl = cvt.rearrange("p a t -> p (a t)").bitcast(BF16) if samp else Wl_S
            wlk = ["cv_t0", "cv_t1"] if samp else ["WlS"]
            ysqS = ysq if samp else ysq_S
            rbcS = rbc if samp else rbc_S
            sk = "" if samp else "S"

            def inter(*gens):
                gens = list(gens)
                while gens:
                    for g_ in list(gens):
                        try:
                            next(g_)
                        except StopIteration:
                            gens.remove(g_)
                        yield

            pcnt = [0]

            def proj(ci):
                i = pcnt[0] % 2
                pcnt[0] += 1
                pa = pA[i]
                for k in range(8):
                    S.pe(lambda e, k=k: e.matmul(pa[:, 0:T], lhsT=w_in_sb[:, k, ci * 128:(ci + 1) * 128],
                                                 rhs=hTt[:, k, 0:T], start=(k == 0), stop=(k == 7)),
                         r=["hTt", "w_in"], w=["pA%d" % i])
                return pa, "pA%d" % i

            def new_cols(buf, sbuf_, c):
                if samp:
                    return sbuf_[:, c, :, 3:7]
                return buf[:, c, 3:131]

            def pa_view(pa):
                if samp:
                    return pa[:, 0:T].rearrange("p (s l) -> p s l", s=NS)
                return pa[:, 0:T]

            def tap(buf, sbuf_, c, k):
                if samp:
                    return sbuf_[:, c, :, k:k + 4]
                return buf[:, c, k:k + 128]

            def fm(t3, c):
                if samp:
                    return t3[:, c, 0:T].rearrange("p (s l) -> p s l", s=NS)
                return t3[:, c, 0:T]

            def conv(buf, sbuf_, c, wname, bname, out_ap, key_in, key_out):
                S.dve(lambda e: e.tensor_scalar(out=out_ap, in0=tap(buf, sbuf_, c, 3), scalar1=P(wname, c, 4)[:, 3:4],
                                                scalar2=P(bname, c), op0=ALU.mult, op1=ALU.add),
                      r=[key_in, "pfm"], w=[key_out])
                for k in (2, 1, 0):
                    S.dve(lambda e, k=k: e.scalar_tensor_tensor(out=out_ap, in0=tap(buf, sbuf_, c, k),
                                                                scalar=P(wname, c, 4)[:, k:k + 1], in1=out_ap,
                                                                op0=ALU.mult, op1=ALU.add),
                          r=[key_in, key_out, "pfm"], w=[key_out])
                if not samp:
                    S.dve(lambda e: e.tensor_copy(out=buf[:, c, 0:3], in_=buf[:, c, 128:131]), r=[key_in], w=[key_in])

            def g_lrux():
                for c in range(8):
                    pa, pk = proj(c)
                    S.act(lambda e, c=c, pa=pa: e.activation(out=new_cols(lxb, lxs, c), in_=pa_view(pa), func=AF.Copy),
                          r=[pk], w=["lx%d" % c])
                    conv(lxb, lxs, c, "LW", "LB", fm(u, c), "lx%d" % c, "u%d" % c)
                    yield

            def g_z():
                for c in range(8):
                    pa, pk = proj(16 + c)
                    S.act(lambda e, c=c, pa=pa: e.activation(out=zs[:, c, 0:T], in_=pa[:, 0:T], func=AF.Silu),
                          r=[pk], w=["zs%d" % c])
                    yield

            def g_xbc():
                for c in range(12):
                    pa, pk = proj(24 + c)
                    S.act(lambda e, c=c, pa=pa: e.activation(out=new_cols(xcb, xcs, c), in_=pa_view(pa), func=AF.Copy),
                          r=[pk], w=["xc%d" % c])
                    if c < 8:
                        conv(xcb, xcs, c, "SW", "SB", fm(cvt, c % 2), "xc%d" % c, "cv_t%d" % (c % 2))
                        S.act(lambda e, c=c: e.activation(out=xsf[:, c, 0:T], in_=cvt[:, c % 2, 0:T], func=AF.Silu),
                              r=["cv_t%d" % (c % 2)], w=["xsf%d" % c])
                    else:
                        g = (c - 8) % 2
                        dstb = Bb if c < 10 else Cb
                        nm = ("Bb%d" if c < 10 else "Cb%d") % g
                        conv(xcb, xcs, c, "SW", "SB", fm(cvt, g), "xc%d" % c, "cv_t%d" % g)
                        S.act(lambda e, g=g, dstb=dstb: e.activation(out=dstb[:, g, 0:T], in_=cvt[:, g, 0:T], func=AF.Silu),
                              r=["cv_t%d" % g], w=[nm])
                    yield
                for k in range(8):
                    S.pe(lambda e, k=k: e.matmul(pD[0:T, 0:16], lhsT=hTt[:, k, 0:T], rhs=w_in_sb[:, k, 4608:4624],
                                                 start=(k == 0), stop=(k == 7)),
                         r=["hTt", "w_in"], w=["pD"])
                S.dve(lambda e: e.tensor_tensor(out=dtr[0:T, :], in0=pD[0:T, 0:16], in1=dtb_bc[0:T, :], op=ALU.add),
                      r=["pD", "dtb"], w=["dtr"])
                S.act(lambda e: e.activation(out=dtr[0:T, :], in_=dtr[0:T, :], func=AF.Exp), r=["dtr"], w=["dtr"])
                S.act(lambda e: e.activation(out=dtt[0:T, :], in_=dtr[0:T, :], func=AF.Ln, bias=1.0), r=["dtr"], w=["dtt"])
                yield

            def g_lru(chunks, pg, kr, ki):
                for c in chunks:
                    pp = c % 2
                    S.pool(lambda e, c=c: e.tensor_copy(out=ub[:, c, 0:T], in_=u[:, c, 0:T]),
                           r=["u%d" % c], w=["ub%d" % c])
                    yield
                    S.pe(lambda e, c=c: e.matmul(pg[:, 0:T], lhsT=wa_blk[:, c, :], rhs=ub[:, c, 0:T], start=True, stop=True),
                         r=["ub%d" % c, "wa"], w=[kr])
                    S.pe(lambda e, c=c: e.matmul(pg[:, 128:128 + T], lhsT=wx_blk[:, c, :], rhs=ub[:, c, 0:T], start=True, stop=True),
                         r=["ub%d" % c, "wx"], w=[ki])
                    yield
                    S.act(lambda e, c=c, pp=pp: e.activation(out=gi[:, 2 * pp, 0:T], in_=pg[:, 0:T], func=AF.Exp, scale=-1.0, bias=nbias[:, c:c + 1]),
                          r=[kr, "nbias"], w=["rg%d" % pp])
                    S.act(lambda e, c=c, pp=pp: e.activation(out=gi[:, 2 * pp + 1, 0:T], in_=pg[:, 128:128 + T], func=AF.Exp, scale=-1.0, bias=nbias[:, 8 + c:9 + c]),
                          r=[ki, "nbias"], w=["ig%d" % pp])
                    S.act(lambda e, pp=pp: e.activation(out=gi[:, 2 * pp:2 * pp + 2, 0:T], in_=gi[:, 2 * pp:2 * pp + 2, 0:T], func=AF.Ln, bias=1.0),
                          r=["rg%d" % pp, "ig%d" % pp], w=["rg%d" % pp, "ig%d" % pp])
                    S.act(lambda e, pp=pp: e.activation(out=gi[:, 2 * pp:2 * pp + 2, 0:T], in_=gi[:, 2 * pp:2 * pp + 2, 0:T], func=AF.Exp, scale=-1.0),
                          r=["rg%d" % pp, "ig%d" % pp], w=["rg%d" % pp, "ig%d" % pp])
                    S.act(lambda e, c=c, pp=pp: e.activation(out=av[:, pp, 0:T], in_=gi[:, 2 * pp, 0:T], func=AF.Exp, scale=cfac[:, c:c + 1]),
                          r=["rg%d" % pp, "cfac"], w=["av%d" % pp])
                    S.act(lambda e, c=c, pp=pp: e.activation(out=a2[:, pp, 0:T], in_=gi[:, 2 * pp, 0:T], func=AF.Exp, scale=c2fac[:, c:c + 1]),
                          r=["rg%d" % pp, "cfac2"], w=["a2%d" % pp])
                    S.act(lambda e, pp=pp: e.activation(out=a2[:, pp, 0:T], in_=a2[:, pp, 0:T], func=AF.Ln, scale=-1.0, bias=1.0),
                          r=["a2%d" % pp], w=["a2%d" % pp])
                    S.act(lambda e, pp=pp: e.activation(out=a2[:, pp, 0:T], in_=a2[:, pp, 0:T], func=AF.Exp, scale=0.5),
                          r=["a2%d" % pp], w=["a2%d" % pp])
                    yield
                    S.dve(lambda e, c=c, pp=pp: e.tensor_tensor(out=tmpb[:, pp, 0:T], in0=gi[:, 2 * pp + 1, 0:T], in1=u[:, c, 0:T], op=ALU.mult),
                          r=["ig%d" % pp, "u%d" % c], w=["tb%d" % pp])
                    S.dve(lambda e, pp=pp: e.tensor_tensor(out=tmpb[:, pp, 0:T], in0=tmpb[:, pp, 0:T], in1=a2[:, pp, 0:T], op=ALU.mult),
                          r=["tb%d" % pp, "a2%d" % pp], w=["tb%d" % pp])
                    if samp:
                        a3 = av[:, pp, 0:T].rearrange("p (s l) -> p s l", s=NS)
                        b3v = tmpb[:, pp, 0:T].rearrange("p (s l) -> p s l", s=NS)
                        S.dve(lambda e, c=c, a3=a3: e.tensor_tensor(out=rbc[:, 0:NS], in0=a3[:, :, 0], in1=h0s[:, c, :], op=ALU.mult),
                              r=["av%d" % pp, "h0s"], w=["rbc"])
                        S.dve(lambda e, b3v=b3v: e.tensor_tensor(out=b3v[:, :, 0], in0=b3v[:, :, 0], in1=rbc[:, 0:NS], op=ALU.add),
                              r=["tb%d" % pp, "rbc"], w=["tb%d" % pp])
                        S.dve(lambda e, a3=a3: e.memset(a3[:, :, 0], 0.0), r=["rbc"], w=["av%d" % pp])
                        S.dve(lambda e, c=c, pp=pp: e.tensor_tensor_scan(out=hs[:, c, 0:T], data0=av[:, pp, 0:T], data1=tmpb[:, pp, 0:T],
                                                                         initial=0.0, op0=ALU.mult, op1=ALU.add),
                              r=["av%d" % pp, "tb%d" % pp], w=["hs%d" % c])
                        S.dve(lambda e, c=c: e.tensor_copy(out=hfin[:, c, :], in_=hs[:, c, 0:T].rearrange("p (s l) -> p s l", s=NS)[:, :, 3]),
                              r=["hs%d" % c], w=["hfin"])
                    else:
                        S.dve(lambda e, c=c, pp=pp: e.tensor_tensor_scan(out=hs[:, c, 0:T], data0=av[:, pp, 0:T], data1=tmpb[:, pp, 0:T],
                                                                         initial=hstate[:, c:c + 1], op0=ALU.mult, op1=ALU.add),
                              r=["av%d" % pp, "tb%d" % pp, "hstate"], w=["hs%d" % c])
                        S.dve(lambda e, c=c: e.tensor_copy(out=hstate[:, c:c + 1], in_=hs[:, c, T - 1:T]),
                              r=["hs%d" % c], w=["hstate"])
                    yield

            def g_gate():
                for c in range(8):
                    pp = c % 2
                    pa, pk = proj(8 + c)
                    S.act(lambda e, pp=pp, pa=pa: e.activation(out=gl[:, pp, 0:T], in_=pa[:, 0:T], func=AF.Gelu_apprx_tanh),
                          r=[pk], w=["gl%d" % pp])
                    S.dve(lambda e, c=c, pp=pp: e.tensor_tensor(out=hs[:, c, 0:T], in0=hs[:, c, 0:T], in1=gl[:, pp, 0:T], op=ALU.mult),
                          r=["hs%d" % c, "gl%d" % pp], w=["yl%d" % c, "hs%d" % c])
                    S.pool(lambda e, c=c, pp=pp: e.tensor_tensor(out=ysq[:, pp, 0:T], in0=hs[:, c, 0:T], in1=hs[:, c, 0:T], op=ALU.mult),
                           r=["yl%d" % c], w=["ysq%d" % pp])
                    S.pe(lambda e, c=c, pp=pp: e.matmul(pD[:, 128:128 + T], lhsT=onesb, rhs=ysq[:, pp, 0:T], start=(c == 0), stop=(c == 7)),
                         r=["ysq%d" % pp, "onesb"], w=["pDn"])
                    yield

            def norm_apply(T, eps_, src, skey, gname, dst, dkey, c0, c1, rbc, rk, pst, pk):
                S.act(lambda e: e.activation(out=rbc[:, 0:T], in_=pst, func=AF.Ln,
                                             scale=1.0 / ((c1 - c0) * 128), bias=eps_t[:, 0:1]),
                      r=[pk, "eps_t"], w=[rk])
                S.act(lambda e: e.activation(out=rbc[:, 0:T], in_=rbc[:, 0:T], func=AF.Exp, scale=-0.5), r=[rk], w=[rk])
                for c in range(c0, c1):
                    S.dve(lambda e, c=c: e.scalar_tensor_tensor(out=dst[:, c, 0:T], in0=src[:, c, 0:T], scalar=P(gname, c),
                                                                in1=rbc[:, 0:T], op0=ALU.mult, op1=ALU.mult),
                          r=[skey % c, rk, "pfm"], w=[dkey % c])

            Um = mskb[0:TS, 0:TS] if samp else Utrib
            ngm = negblk if samp else negm
            allm = mskb[0:TS, TS:2 * TS] if samp else onesb
            d4 = lambda ap: ap[:, 0:4 * T].rearrange("p (a t) -> p a t", a=4)
            Em, Dm, Mm, pC4 = d4(Emf), d4(Dmf), d4(Mmf), d4(pC)

            def g_ssd():
                for c in range(8):
                    S.pe(lambda e, c=c: e.transpose(out=pT[0:T, c * 128:(c + 1) * 128], in_=xsf[:, c, 0:T], identity=ident),
                         r=["xsf%d" % c, "cst"], w=["pT"])
                for g in range(2):
                    S.pe(lambda e, g=g: e.transpose(out=pCb[0:T, 128 + g * 128:128 + (g + 1) * 128], in_=Bb[:, g, 0:T], identity=identb),
                         r=["Bb%d" % g, "identb"], w=["pC", "pCx"])
                S.dve(lambda e: e.tensor_tensor(out=xdt[0:T, :].rearrange("p (h q) -> p h q", h=16),
                                                in0=pT[0:T, :].rearrange("p (h q) -> p h q", h=16),
                                                in1=dtt[0:T, :].unsqueeze(2).to_broadcast([T, 16, 64]), op=ALU.mult),
                      r=["pT", "dtt"], w=["xdt"])
                S.dve(lambda e: e.tensor_copy(out=BT[0:T, :], in_=pCb[0:T, 128:384]), r=["pC"], w=["BT"])
                S.dve(lambda e: e.tensor_tensor(out=da[0:T, :], in0=dtt[0:T, :], in1=a_bc[0:T, :], op=ALU.mult),
                      r=["dtt", "a_bc"], w=["da"])
                yield
                S.dve(lambda e: e.tensor_copy(out=dah[0:T, :], in_=da[0:T, :]), r=["da"], w=["dah"])
                S.dve(lambda e: e.tensor_tensor(out=dal[0:T, :], in0=da[0:T, :], in1=dah[0:T, :], op=ALU.subtract),
                      r=["da", "dah"], w=["dal"])
                for i, dx in enumerate((dah, dal)):
                    S.pe(lambda e, dx=dx, i=i: e.matmul(pC[0:T, 0:16], lhsT=Um[0:T, 0:T], rhs=dx[0:T, :], start=(i == 0), stop=(i == 1)),
                         r=["dah", "dal", "mskb"], w=["pC", "pCx"])
                for i, dx in enumerate((dah, dal)):
                    S.pe(lambda e, dx=dx, i=i: e.matmul(pC[0:T, 16:32], lhsT=allm[0:T, 0:T], rhs=dx[0:T, :], start=(i == 0), stop=(i == 1)),
                         r=["dah", "dal", "mskb", "onesb"], w=["pC", "pCx"])
                if not samp:
                    for i, dx in enumerate((dah, dal)):
                        S.pe(lambda e, dx=dx, i=i: e.matmul(pC[:, 32:48], lhsT=onesb, rhs=dx, start=(i == 0), stop=(i == 1)),
                             r=["dah", "dal", "onesb"], w=["pC", "pCx"])
                for g in range(2):
                    S.pe(lambda e, g=g: e.matmul(pC[0:T, 256 + g * 128:256 + g * 128 + T], lhsT=Bb[:, g, 0:T], rhs=Cb[:, g, 0:T],
                                                 start=True, stop=True), r=["Bb%d" % g, "Cb%d" % g], w=["pC", "pCx"])
                yield
                S.dve(lambda e: e.tensor_scalar(out=ncum[0:T, :], in0=pC[0:T, 0:16], scalar1=-1.0, scalar2=None, op0=ALU.mult),
                      r=["pC"], w=["ncum"])
                S.dve(lambda e: e.tensor_tensor(out=dte[0:T, :], in0=pC[0:T, 16:32], in1=ncum[0:T, :], op=ALU.add),
                      r=["pC", "ncum"], w=["dte"])
                if not samp:
                    S.dve(lambda e: e.tensor_copy(out=cdec, in_=pC[:, 32:48]), r=["pC"], w=["cdec"])
                S.dve(lambda e: e.tensor_copy(out=cbT[0:T, :, 0:T], in_=pC[0:T, 256:512].rearrange("p (g t) -> p g t", g=2)[:, :, 0:T]),
                      r=["pC"], w=["cbT0", "cbT1"])
                S.act(lambda e: e.activation(out=dte[0:T, :], in_=dte[0:T, :], func=AF.Exp), r=["dte"], w=["dte"])
                if not samp:
                    S.act(lambda e: e.activation(out=cdec, in_=cdec, func=AF.Exp), r=["cdec"], w=["cdec"])
                S.dve(lambda e: e.tensor_tensor(out=xdd[0:T, :].rearrange("p (h q) -> p h q", h=16),
                                                in0=xdt[0:T, :].rearrange("p (h q) -> p h q", h=16),
                                                in1=dte[0:T, :].unsqueeze(2).to_broadcast([T, 16, 64]), op=ALU.mult),
                      r=["xdt", "dte"], w=["xdd"])
                yield
                if not samp:
                    for g in range(2):
                        S.pe(lambda e, g=g: e.matmul(pO[:, g * 512:(g + 1) * 512], lhsT=BT[:, g * 128:(g + 1) * 128],
                                                     rhs=xdd[:, g * 512:(g + 1) * 512], start=True, stop=True),
                             r=["BT", "xdd"], w=["pO"])
                    S.dve(lambda e: e.tensor_tensor(out=hT.rearrange("p (h q) -> p h q", h=16),
                                                    in0=hT.rearrange("p (h q) -> p h q", h=16),
                                                    in1=cdec.unsqueeze(2).to_broadcast([128, 16, 64]), op=ALU.mult),
                          r=["hT", "cdec"], w=["hT"])
                    S.dve(lambda e: e.tensor_tensor(out=hT, in0=hT, in1=pO, op=ALU.add), r=["hT", "pO"], w=["hT"])
                    yield
                for q4 in range(4):
                    g = q4 // 2
                    for i, (dx, Wf, wk) in enumerate(((dah, Mmf, ["Mm"]), (dal, Wl, wlk))):
                        S.pool(lambda e, q4=q4, dx=dx, Wf=Wf: e.tensor_tensor(out=d4(Wf)[0:T], in0=Um[0:T, 0:T].unsqueeze(1).to_broadcast([T, 4, T]),
                                                                             in1=dx[0:T, q4 * 4:q4 * 4 + 4].unsqueeze(2).to_broadcast([T, 4, T]),
                                                                             op=ALU.mult), r=["dah", "dal", "mskb"], w=wk)
                        S.pe(lambda e, Wf=Wf, i=i: e.matmul(pC[:, 0:4 * T], lhsT=onesb[0:T, :], rhs=Wf[0:T, 0:4 * T],
                                                            start=(i == 0), stop=(i == 1)), r=wk + ["onesb"], w=["pC", "pCx"])
                    S.dve(lambda e: e.tensor_copy(out=Em, in_=pC4), r=["pC"], w=["Em"])
                    yield
                    for hh in range(4):
                        h = q4 * 4 + hh
                        S.dve(lambda e, h=h, hh=hh: e.scalar_tensor_tensor(out=Dm[0:T, hh, :], in0=Em[0:T, hh, :],
                                                                           scalar=ncum[0:T, h:h + 1], in1=ngm[0:T, 0:T],
                                                                           op0=ALU.add, op1=ALU.add),
                              r=["Em", "ncum", "cst"], w=["Dm"])
                    S.act(lambda e: e.activation(out=Em, in_=Em, func=AF.Exp), r=["Em"], w=["Em"])
                    S.act(lambda e: e.activation(out=Dm[0:T], in_=Dm[0:T], func=AF.Exp), r=["Dm"], w=["Dm"])
                    S.dve(lambda e, g=g: e.tensor_tensor(out=Mm[0:T], in0=Dm[0:T],
                                                         in1=cbT[0:T, g, 0:T].unsqueeze(1).to_broadcast([T, 4, T]), op=ALU.mult),
                          r=["Dm", "cbT%d" % g], w=["Mm"])
                    S.pool(lambda e, g=g, q4=q4: e.tensor_tensor(out=(Chs[:, q4 * 4:q4 * 4 + 4, :] if samp else Chp), in0=Em,
                                                                in1=Cb[:, g, 0:T].unsqueeze(1).to_broadcast([128, 4, T]), op=ALU.mult),
                          r=["Em", "Cb%d" % g], w=["Ch"])
                    yield
                    for hh in range(4):
                        h = q4 * 4 + hh
                        c = h // 2
                        h2 = h % 2
                        po = pT[64 * h2:64 * h2 + 64, c * 128:c * 128 + T]
                        S.pe(lambda e, h=h, hh=hh, po=po: e.matmul(po, lhsT=xdt[0:T, h * 64:(h + 1) * 64], rhs=Mm[0:T, hh, :],
                                                                   start=True, stop=samp), r=["xdt", "Mm"], w=["pT"])
                        if not samp:
                            S.pe(lambda e, h=h, hh=hh, po=po: e.matmul(po, lhsT=hTb[:, h * 64:(h + 1) * 64], rhs=Chp[:, hh, :],
                                                                       start=False, stop=True), r=["hTb", "Ch"], w=["pT"])
                    yield

            def late_outputs():
                M = T if samp else 3
                t0 = 0 if samp else 125
                if samp or last:
                    for blk, col0 in enumerate((0, 512, 3072, 3584, 4096)):
                        for k in range(8):
                            S.pe(lambda e, k=k, col0=col0: e.matmul(pO[0:M, 0:512], lhsT=hTt[:, k, t0:t0 + M],
                                                                    rhs=w_in_sb[:, k, col0:col0 + 512], start=(k == 0), stop=(k == 7)),
                                 r=["hTt", "w_in"], w=["pO"])
                        S.dve(lambda e, blk=blk: e.tensor_copy(out=stg[0:M, blk * 512:(blk + 1) * 512], in_=pO[0:M, 0:512]),
                              r=["pO"], w=["stg"] + STGW)
                if last:
                    S.dma(lambda e: e.dma_start(out=o_plc, in_=stg[0:3, 0:1024]), "o_plc", r=["stg"])
                    S.dma(lambda e: e.dma_start(out=o_psc, in_=stg[0:3, 1024:2560]), "o_psc", r=["stg"])
                if samp:
                    for s in range(NS):
                        S.dma(lambda e, s=s: e.dma_start(out=o_slc[s], in_=stg[4 * s + 1:4 * s + 4, 0:1024]), "o_slc", r=["stg"])
                        S.dma(lambda e, s=s: e.dma_start(out=o_ssc[s], in_=stg[4 * s + 1:4 * s + 4, 1024:2560]), "o_ssc", r=["stg"])
                if last:
                    S.pe(lambda e: e.transpose(out=pC[0:8, 0:128], in_=hstate, identity=ident), r=["hstate", "cst"], w=["pC", "pCx"])
                    S.act(lambda e: e.activation(out=stT[0:8, 0:128], in_=pC[0:8, 0:128], func=AF.Copy), r=["pC"], w=["stg"] + STGW)
                    S.dma(lambda e: e.dma_start(out=o_plh, in_=stT[0:8, 0:128]), "o_plh", r=["stg"])
                if samp:
                    for c in range(8):
                        S.pe(lambda e, c=c: e.transpose(out=pT[0:NS, c * 128:(c + 1) * 128], in_=hfin[:, c, :], identity=ident),
                             r=["hfin", "cst"], w=["pT"])
                    S.act(lambda e: e.activation(out=lh_in[0:NS, :], in_=pT[0:NS, :], func=AF.Copy), r=["pT"], w=["stg"] + STGW)
                    S.dma(lambda e: e.dma_start(out=o_slh, in_=lh_in[0:NS, :]), "o_slh", r=["stg"])


            def genP():
                S.dma(lambda e: e.dma_start(out=xt[0:T, :], in_=xsrc), "xt", w=["xt"])
                rms_rstd(xt, T, "xt", junk, ss, rstd)
                S.act(lambda e: e.activation(out=xn[0:T, :], in_=xt[0:T, :], func=AF.Copy, scale=rstd[0:T, 0:1]),
                      r=["xt", "rstd"], w=["xn"])
                to_fm(T, "GM", hTt, "hTt")

                if samp:
                    S.dma(lambda e: e.dma_start(out=lc_in[0:48, :], in_=st_lc), "stg", w=["stg"])
                    S.dma(lambda e: e.dma_start(out=sc_in[0:48, :], in_=st_sc), "stg", w=["stg"])
                    S.dma(lambda e: e.dma_start(out=lh_in[64:64 + NS, :], in_=st_lh), "stg", w=["stg"])
                    for c in range(8):
                        S.pe(lambda e, c=c: e.transpose(out=pC[:, 0:48], in_=lc_in[0:48, c * 128:(c + 1) * 128],
                                                        identity=ident[0:48, 0:48]), r=["stg", "cst"], w=["pC"])
                        S.act(lambda e, c=c: e.activation(out=lxs[:, c, :, 0:3],
                                                          in_=pC[:, 0:48].rearrange("p (s j) -> p s j", s=NS),
                                                          func=AF.Copy), r=["pC"], w=["lx%d" % c])
                        S.pe(lambda e, c=c: e.transpose(out=pD[:, 0:NS], in_=lh_in[64:64 + NS, c * 128:(c + 1) * 128],
                                                        identity=ident[64:64 + NS, 64:64 + NS]), r=["stg", "cst"], w=["pD"])
                        S.dve(lambda e, c=c: e.tensor_copy(out=h0s[:, c, :], in_=pD[:, 0:NS]), r=["pD"], w=["h0s"])
                    for c in range(12):
                        S.pe(lambda e, c=c: e.transpose(out=pC[:, 0:48], in_=sc_in[0:48, c * 128:(c + 1) * 128],
                                                        identity=ident[0:48, 0:48]), r=["stg", "cst"], w=["pC"])
                        S.act(lambda e, c=c: e.activation(out=xcs[:, c, :, 0:3],
                                                          in_=pC[:, 0:48].rearrange("p (s j) -> p s j", s=NS),
                                                          func=AF.Copy), r=["pC"], w=["xc%d" % c])

                yield
                yield from inter(g_lrux(), g_z())
                yield from inter(g_xbc())
                yield from inter(g_lru((0, 2, 4, 6), pA[0], "pA0", "pA0"), g_lru((1, 3, 5, 7), pA[1], "pA1", "pA1"))
                yield from inter(g_gate())
                norm_apply(T, EPS, hs, "yl%d", "GL", ynl, "ynl%d", 0, 8, rbc, "rbc", pD[:, 128:128 + T], "pDn")
                yield
                if samp:
                    late_outputs()

            def genS():
                yield from inter(g_ssd())
                if samp:
                    ssd_sample_states_prep()

                for c in range(8):
                    S.dve(lambda e, c=c: e.scalar_tensor_tensor(out=xsf[:, c, 0:T], in0=xsf[:, c, 0:T], scalar=P("DS", c),
                                                                in1=pT[:, c * 128:c * 128 + T], op0=ALU.mult, op1=ALU.add),
                          r=["pT", "xsf%d" % c, "pfm"], w=["xsf%d" % c])
                if samp:
                    S.dve(lambda e: e.tensor_tensor(out=xsf[:, :, 0:T], in0=xsf[:, :, 0:T], in1=pyo_sb, op=ALU.add),
                          r=["xsf%d" % c for c in range(8)] + ["pyo_sb"], w=["xsf%d" % c for c in range(8)])
                S.dve(lambda e: e.tensor_tensor(out=xsf[:, :, 0:T], in0=xsf[:, :, 0:T], in1=zs[:, :, 0:T], op=ALU.mult),
                      r=["xsf%d" % c for c in range(8)] + ["zs%d" % c for c in range(8)], w=["yg%d" % c for c in range(8)] + ["xsf%d" % c for c in range(8)])
                yield
                if not samp:
                    S.act(lambda e: e.activation(out=hTb, in_=hT, func=AF.Copy), r=["hT"], w=["hTb"])
                    if last:
                        for c in range(8):
                            S.pe(lambda e, c=c: e.transpose(out=pO[:, c * 128:(c + 1) * 128], in_=hT[:, c * 128:(c + 1) * 128], identity=ident),
                                 r=["hT", "cst"], w=["pO"])
                        S.dve(lambda e: e.tensor_copy(out=stT, in_=pO), r=["pO"], w=["stg"] + STGW)
                        S.dma(lambda e: e.dma_start(out=o_psh.rearrange("(c q) n -> q c n", q=128),
                                                    in_=stT.rearrange("p (c n) -> p c n", c=8)), "o_psh", r=["stg"])
                yield
                for g in range(2):
                    for c in range(4 * g, 4 * g + 4):
                        pp = c % 2
                        S.pool(lambda e, c=c, pp=pp: e.tensor_tensor(out=ysqS[:, pp, 0:T], in0=xsf[:, c, 0:T], in1=xsf[:, c, 0:T], op=ALU.mult),
                               r=["yg%d" % c], w=["ysq%s%d" % (sk, pp)])
                        S.pe(lambda e, c=c, pp=pp, g=g: e.matmul(pO[:, 0:T], lhsT=onesb, rhs=ysqS[:, pp, 0:T],
                                                                 start=(c == 4 * g), stop=(c == 4 * g + 3)),
                             r=["ysq%s%d" % (sk, pp), "onesb"], w=["pO"])
                    norm_apply(T, EPS, xsf, "yg%d", "GS", yns, "yns%d", 4 * g, 4 * g + 4, rbcS, "rbc" + sk, pO[:, 0:T], "pO")

                yield
                for nb in range(2):
                    for kc in range(16):
                        src = ynl if kc < 8 else yns
                        S.pe(lambda e, kc=kc, nb=nb, src=src: e.matmul(pO[0:T, nb * 512:(nb + 1) * 512], lhsT=src[:, kc % 8, 0:T],
                                                                       rhs=w_out_sb[:, kc, nb * 512:(nb + 1) * 512],
                                                                       start=(kc == 0), stop=(kc == 15)),
                             r=[("ynl%d" if kc < 8 else "yns%d") % (kc % 8), "w_out"], w=["pO"])
                S.dve(lambda e: e.tensor_tensor(out=xt[0:T, :], in0=pO[0:T, :], in1=xt[0:T, :], op=ALU.add),
                      r=["pO", "xt"], w=["xt"])
                S.dma(lambda e: e.dma_start(out=scr[row0:row0 + T, :], in_=xt[0:T, :]), "xnew", r=["xt"], w=["scr%d" % mt])

                if last:
                    late_outputs()

            return par, genP, genS

        def ssd_sample_states_prep():
            T = TS
            for i, dx in enumerate((dah, dal)):
                S.dve(lambda e, dx=dx, i=i: e.tensor_tensor(out=damb[0:T, i], in0=dx[0:T, :].unsqueeze(1).to_broadcast([T, NS, 16]),
                                                            in1=blki.unsqueeze(2).to_broadcast([T, NS, 16]), op=ALU.mult),
                      r=["dah", "dal", "cst"], w=["dam%d" % i])
                S.pe(lambda e, i=i: e.matmul(pD[:, 0:256], lhsT=onesb[0:T, :], rhs=damb[0:T, i].rearrange("p s h -> p (s h)"),
                                             start=(i == 0), stop=(i == 1)), r=["dam%d" % i, "onesb"], w=["pD", "pD2", "pD3", "pDn"])
            S.act(lambda e: e.activation(out=dtot.rearrange("p s h -> p (s h)"), in_=pD[:, 0:256], func=AF.Exp),
                  r=["pD"], w=["dtot"])
            S.barrier()
            dtotP = Dmf[:, 0:NS * 8].rearrange("p (s c) -> p s c", s=NS)
            for h2 in range(2):
                S.dve(lambda e, h2=h2: e.tensor_copy(out=dtotP[64 * h2:64 * h2 + 64],
                                                     in_=dtot[64 * h2:64 * h2 + 64].rearrange("p s (c two) -> p s c two", two=2)[:, :, :, h2]),
                      r=["dtot"], w=["dtotP"])
            h0in_b = [h0in, u]
            hout_b = [hout, hs]
            h0b_b = [hTb, ub.rearrange("p c t -> p (c t)")]
            h0Tb_b = [lrut[:, 0:512].bitcast(BF16), lrut[:, 512:1024].bitcast(BF16)]
            xdm_b = [hT[:, 0:512].bitcast(BF16), hT[:, 512:1024].bitcast(BF16)]
            pTr_b = [pC.bitcast(BF16), pD.bitcast(BF16)]
            pTk = [["pC"], ["pD"]]
            for s in range(NS):
                q = s % 2
                hi, ho, h0b, hb, xdm, pTr, tk = h0in_b[q], hout_b[q], h0b_b[q], h0Tb_b[q], xdm_b[q], pTr_b[q], pTk[q]
                S.dma(lambda e, s=s, hi=hi: e.dma_start(out=hi, in_=st_sh[s].rearrange("(c q) n -> q c n", q=128)),
                      "h0in%d" % q, w=["h0in%d" % q], q="pool")
                S.act(lambda e, hi=hi, h0b=h0b: e.activation(out=h0b, in_=hi.rearrange("p c n -> p (c n)"), func=AF.Copy),
                      r=["h0in%d" % q], w=["h0b%d" % q])
                for c in range(8):
                    S.pe(lambda e, c=c, h0b=h0b, pTr=pTr: e.transpose(out=pTr[:, c * 128:(c + 1) * 128], in_=h0b[:, c * 128:(c + 1) * 128],
                                                                      identity=identb), r=["h0b%d" % q, "identb"], w=tk)
                S.dve(lambda e, hb=hb, pTr=pTr: e.tensor_copy(out=hb, in_=pTr[:, 0:1024]), r=tk, w=["h0Tb%d" % q])
                for h in range(16):
                    S.pe(lambda e, h=h, s=s, hb=hb: e.matmul(pA[0][64 * (h % 2):64 * (h % 2) + 64, (h // 2) * TS + 4 * s:(h // 2) * TS + 4 * s + 4],
                                                             lhsT=hb[:, h * 64:(h + 1) * 64], rhs=Chs[:, h, 4 * s:4 * s + 4],
                                                             start=True, stop=True),
                         r=["h0Tb%d" % q, "Ch"], w=["pA0"])
                S.dve(lambda e, s=s, xdm=xdm: e.tensor_scalar(out=xdm[0:T, :], in0=xdd[0:T, :], scalar1=blki[:, s:s + 1], scalar2=None, op0=ALU.mult),
                      r=["xdd", "cst"], w=["xdm%d" % q])
                for c in range(8):
                    S.pe(lambda e, c=c, xdm=xdm: e.matmul(pO[:, c * 128:(c + 1) * 128], lhsT=xdm[0:T, c * 128:(c + 1) * 128],
                                                          rhs=BT[0:T, (c // 4) * 128:(c // 4 + 1) * 128], start=True, stop=True),
                         r=["xdm%d" % q, "BT"], w=["pO"])
                for c in range(8):
                    S.dve(lambda e, c=c, s=s, hi=hi, ho=ho: e.scalar_tensor_tensor(out=ho[:, c, :], in0=hi[:, c, :], scalar=dtotP[:, s, c:c + 1],
                                                                                   in1=pO[:, c * 128:(c + 1) * 128], op0=ALU.mult, op1=ALU.add),
                          r=["h0in%d" % q, "dtotP", "pO"], w=["hout%d" % q])
                S.dma(lambda e, s=s, ho=ho: e.dma_start(out=o_ssh[s].rearrange("(c q) n -> q c n", q=128), in_=ho),
                      "hout%d" % q, r=["hout%d" % q])
            S.act(lambda e: e.activation(out=pyo_sb.rearrange("p c t -> p (c t)"), in_=pA[0][:, 0:8 * TS], func=AF.Copy),
                  r=["pA0"], w=["pyo_sb"])

        S.pool(lambda e: e.memset(lxb, 0.0), w=["lx%d" % c for c in range(8)])
        S.pool(lambda e: e.memset(xcb, 0.0), w=["xc%d" % c for c in range(12)])

        def drive(g_, par):
            S.ctx = par
            try:
                next(g_)
                return True
            except StopIteration:
                return False
            finally:
                S.ctx = None

        tiles = [mixer_tile(mt, False) for mt in range(NT)]
        RATIO = 3
        par0, gP0, _ = tiles[0]
        g = gP0()
        while drive(g, par0):
            pass
        for n in range(NT):
            par, _, gS = tiles[n]
            gs = gS()
            alive_s = True
            alive_p = False
            if n + 1 < NT:
                parn, gPn, _ = tiles[n + 1]
                gp = gPn()
                alive_p = True
            while alive_s or alive_p:
                for _ in range(RATIO):
                    if alive_p:
                        alive_p = drive(gp, parn)
                if alive_s:
                    alive_s = drive(gs, par)
        S.barrier()
        if SAMP:
            pars, gPs, gSs = mixer_tile(SEQ // 128, True)
            for g in (gPs(), gSs()):
                while drive(g, pars):
                    pass

        S.barrier()
        ptr[0] = base0
        w_up_sb = b3(8, DFF)
        w_dn_sb = b3(32, D)
        if MLP:
            k_wup = load_w(w_up_sb, w_up, 8, DFF, "w_up")
            k_wdn = load_w(w_dn_sb, w_down, 32, D, "w_dn")
        T2 = 256
        xt2 = [[f32(D), f32(D)], [f32(D), f32(D)]]
        xn2 = f32(D)
        ss2 = [f32(4), f32(4)]
        rstd2 = [f32(4), f32(4)]
        mT = [b3(8, T2), b3(8, T2)]
        actb = b3(32, T2)
        rl = [f32(T2), f32(T2)]
        yout = f32(D)
        gfin_bc = f32(D)
        S.dma(lambda e: e.dma_start(out=gfin_bc, in_=gfin_d.partition_broadcast(128)), "gfin", w=["gfin"])
        pDN = [PS[:, 3072:4096], PS[:, 2048:3072]]
        pDNk = [["pO"], ["pC", "pD"]]

        def mlp_front(ti, r0, T):
            q = ti % 2
            nsub = (T + 127) // 128
            for j in range(nsub):
                Tj = min(128, T - j * 128)
                xk = "xt2_%d_%d" % (q, j)
                S.dma(lambda e, j=j, Tj=Tj: e.dma_start(out=xt2[q][j][0:Tj, :], in_=scr[r0 + j * 128:r0 + j * 128 + Tj, :]),
                      xk, r=["scr%d" % ((r0 + j * 128) // 128)], w=[xk])
                rms_rstd(xt2[q][j], Tj, xk, xn2, ss2[0], rstd2[0], "2")
                S.act(lambda e, j=j, Tj=Tj: e.activation(out=xn2[0:Tj, :], in_=xt2[q][j][0:Tj, :], func=AF.Copy, scale=rstd2[0][0:Tj, 0:1]),
                      r=[xk, "rstd2"], w=["xn2"])
                for k in range(8):
                    S.pe(lambda e, k=k, Tj=Tj: e.transpose(out=pT[:, k * 128:k * 128 + Tj], in_=xn2[0:Tj, k * 128:(k + 1) * 128],
                                                           identity=ident[0:Tj, 0:Tj]), r=["xn2", "cst"], w=["pT"])
                S.dve(lambda e, j=j, Tj=Tj: e.tensor_tensor(
                    out=mT[q][:, :, j * 128:j * 128 + Tj], in0=pT.rearrange("p (k t) -> p k t", k=8)[:, :, 0:Tj],
                    in1=P("GP", 0, 8).unsqueeze(2).to_broadcast([128, 8, Tj]), op=ALU.mult),
                    r=["pT", "pfm"], w=["mT%d" % q])
            yield
            for f in range(32):
                pa = pA[f % 2]
                for k in range(8):
                    S.pe(lambda e, k=k, f=f, pa=pa: e.matmul(pa[:, 0:T], lhsT=w_up_sb[:, k, f * 128:(f + 1) * 128], rhs=mT[q][:, k, 0:T],
                                                             start=(k == 0), stop=(k == 7)),
                         r=["mT%d" % q, "w_up"], w=["pA%d" % (f % 2)])
                S.act(lambda e, f=f, pa=pa: e.activation(out=rl[f % 2][:, 0:T], in_=pa[:, 0:T], func=AF.Relu),
                      r=["pA%d" % (f % 2)], w=["rl%d" % (f % 2)])
                S.pool(lambda e, f=f: e.tensor_tensor(out=actb[:, f, 0:T], in0=rl[f % 2][:, 0:T], in1=rl[f % 2][:, 0:T], op=ALU.mult),
                       r=["rl%d" % (f % 2)], w=["act%d" % f])
                yield

        def mlp_back(ti, r0, T):
            q = ti % 2
            nsub = (T + 127) // 128
            for f in range(32):
                for j in range(nsub):
                    Tj = min(128, T - j * 128)
                    for nb in range(2):
                        S.pe(lambda e, f=f, nb=nb, j=j, Tj=Tj: e.matmul(pDN[j][0:Tj, nb * 512:(nb + 1) * 512],
                                                                        lhsT=actb[:, f, j * 128:j * 128 + Tj],
                                                                        rhs=w_dn_sb[:, f, nb * 512:(nb + 1) * 512],
                                                                        start=(f == 0), stop=(f == 31)),
                             r=["act%d" % f, "w_dn"], w=pDNk[j])
                yield
            for j in range(nsub):
                Tj = min(128, T - j * 128)
                xk = "xt2_%d_%d" % (q, j)
                S.dve(lambda e, j=j, Tj=Tj: e.tensor_tensor(out=xt2[q][j][0:Tj, :], in0=pDN[j][0:Tj, :], in1=xt2[q][j][0:Tj, :], op=ALU.add),
                      r=pDNk[j] + [xk], w=[xk])
                rms_rstd(xt2[q][j], Tj, xk, yout, ss2[1], rstd2[1], "2b", jkey="yout")
                S.dve(lambda e, j=j, Tj=Tj: e.scalar_tensor_tensor(out=yout[0:Tj, :], in0=xt2[q][j][0:Tj, :], scalar=rstd2[1][0:Tj, 0:1],
                                                                   in1=gfin_bc[0:Tj, :], op0=ALU.mult, op1=ALU.mult),
                      r=[xk, "rstd2b", "gfin"], w=["yout"])
                rr = r0 + j * 128
                if rr < SEQ:
                    S.dma(lambda e, rr=rr, Tj=Tj: e.dma_start(out=y_p[rr:rr + Tj, :], in_=yout[0:Tj, :]), "yout", r=["yout"])
                else:
                    S.dma(lambda e, Tj=Tj: e.dma_start(out=y_s, in_=yout[0:Tj, :]), "yout", r=["yout"])
                yield

        if MLP:
            jobs = [(t * T2, T2) for t in range(NT * 128 // T2)]
            if SAMP:
                jobs.append((SEQ, TS))
            for _ in mlp_front(0, *jobs[0]):
                pass
            for ti in range(len(jobs)):
                gb = mlp_back(ti, *jobs[ti])
                gf = mlp_front(ti + 1, *jobs[ti + 1]) if ti + 1 < len(jobs) else iter(())
                ab = af = True
                while ab or af:
                    if ab:
                        ab = next(gb, "END") != "END"
                    if af:
                        af = next(gf, "END") != "END"

        S.emit()
    return nc


_CACHE = {}


def _consts():
    c = np.zeros((128, NCST), np.float32)
    i = np.arange(128)
    c[:, CI:CI + 128] = np.eye(128, dtype=np.float32)
    c[:, CU:CU + 128] = (i[:, None] <= i[None, :]).astype(np.float32)
    c[:, CN:CN + 128] = np.where(i[:, None] <= i[None, :], 0.0, NEG).astype(np.float32)
    c[:, CO:CO + 128] = 1.0
    j = np.arange(TS)
    same = (j[:, None] // 4) == (j[None, :] // 4)
    caus = j[:, None] <= j[None, :]
    c[0:TS, CUB:CUB + TS] = (same & caus).astype(np.float32)
    c[0:TS, CNB:CNB + TS] = np.where(same & caus, 0.0, NEG).astype(np.float32)
    c[0:TS, CBM:CBM + TS] = same.astype(np.float32)
    c[0:TS, CBI:CBI + NS] = ((j[:, None] // 4) == np.arange(NS)[None, :]).astype(np.float32)
    return c


def _fm(v, nch):
    return np.ascontiguousarray(np.asarray(v, np.float32).reshape(nch, 128).T)


def kernel(x_prompt, x_sample, state_lru_conv, state_lru_h, state_ssd_conv, state_ssd_h,
           g_mix, w_in, lru_conv_w, lru_conv_b, w_a, b_a, w_x, b_x, lam, g_lru_out,
           ssd_conv_w, ssd_conv_b, dt_bias, a_log, d_skip, g_ssd_out, w_out,
           g_mlp, w_up, w_down, g_final):
    f = lambda a: np.ascontiguousarray(np.asarray(a, np.float32))
    if "nc" not in _CACHE:
        _CACHE["nc"] = build_program()
    nc = _CACHE["nc"]
    pfm = np.zeros((128, NPAR), np.float32)
    lw = np.asarray(lru_conv_w[0], np.float32)
    pfm[:, PC["LW"]:PC["LW"] + 32] = lw.reshape(4, 8, 128).transpose(2, 1, 0).reshape(128, 32)
    pfm[:, PC["LB"]:PC["LB"] + 8] = _fm(lru_conv_b[0], 8)
    pfm[:, PC["BA"]:PC["BA"] + 8] = _fm(np.asarray(b_a[0]).reshape(-1), 8)
    pfm[:, PC["BX"]:PC["BX"] + 8] = _fm(np.asarray(b_x[0]).reshape(-1), 8)
    pfm[:, PC["LAM"]:PC["LAM"] + 8] = _fm(lam[0], 8)
    pfm[:, PC["GL"]:PC["GL"] + 8] = _fm(g_lru_out[0], 8)
    sw = np.asarray(ssd_conv_w[0], np.float32)
    pfm[:, PC["SW"]:PC["SW"] + 48] = sw.reshape(4, 12, 128).transpose(2, 1, 0).reshape(128, 48)
    pfm[:, PC["SB"]:PC["SB"] + 12] = _fm(ssd_conv_b[0], 12)
    pfm[:, PC["DS"]:PC["DS"] + 8] = _fm(np.repeat(np.asarray(d_skip[0], np.float32), 64), 8)
    pfm[:, PC["GS"]:PC["GS"] + 8] = _fm(g_ssd_out[0], 8)
    pfm[:, PC["GM"]:PC["GM"] + 8] = _fm(g_mix[0], 8)
    pfm[:, PC["GP"]:PC["GP"] + 8] = _fm(g_mlp[0], 8)
    cst = _consts()
    shared = {
        "w_in": f(w_in[0]), "w_out": f(w_out[0]), "w_up": f(w_up[0]), "w_down": f(w_down[0]),
        "w_a": f(w_a[0]), "w_x": f(w_x[0]), "pfm": pfm, "cst": cst,
        "dt_bias": f(dt_bias[0]), "a_log": f(a_log[0]), "g_final": f(g_final),
    }
    in_maps = []
    for b in range(NCORES):
        sl = slice(NS * b, NS * (b + 1))
        m = dict(shared)
        m["xp"] = f(x_prompt[b])
        m["xs"] = f(np.asarray(x_sample[sl]).reshape(TS, D))
        m["st_lc"] = f(np.asarray(state_lru_conv[0, sl]).reshape(NS * 3, D))
        m["st_lh"] = f(state_lru_h[0, sl])
        m["st_sc"] = f(np.asarray(state_ssd_conv[0, sl]).reshape(NS * 3, XBC))
        m["st_sh"] = f(np.asarray(state_ssd_h[0, sl]).reshape(NS, 1024, 128))
        in_maps.append(m)
    res = run_bass_kernel_spmd(nc, in_maps, core_ids=list(range(NCORES)))
    R = res.results
    cat = lambda k: np.stack([np.asarray(R[b][k], np.float32) for b in range(NCORES)])
    y_prompt = cat("y_p")
    y_sample = cat("y_s").reshape(NCORES * NS, 4, D)
    p_lc = cat("o_plc")[None]
    p_lh = cat("o_plh").reshape(NCORES, D)[None]
    p_sc = cat("o_psc")[None]
    p_sh = cat("o_psh").reshape(NCORES, 16, 64, 128)[None]
    s_lc = cat("o_slc").reshape(NCORES * NS, 3, D)[None]
    s_lh = cat("o_slh").reshape(NCORES * NS, D)[None]
    s_sc = cat("o_ssc").reshape(NCORES * NS, 3, XBC)[None]
    s_sh = cat("o_ssh").reshape(NCORES * NS, 16, 64, 128)[None]
    return (y_prompt, y_sample, p_lc, p_lh, p_sc, p_sh, s_lc, s_lh, s_sc, s_sh)
```

```python
import math
from contextlib import ExitStack

import numpy as np
import concourse.bass as bass
import concourse.mybir as mybir
from concourse.bass_utils import run_bass_kernel_spmd

F32 = mybir.dt.float32
BF16 = mybir.dt.bfloat16
AF = mybir.ActivationFunctionType
ALU = mybir.AluOpType

NCORES = 8
D = 1024
SEQ = 2048
NS = 16
TS = 64
XBC = 1536
INP = 4624
DFF = 4096
EPS = 1e-6
NEG = -30000.0

import re as _re

ENGS = ("pe", "act", "dve", "pool", "sp")
SAME_ENGINE_SYNC = {"pe": False, "act": True, "dve": True, "pool": True, "sp": False}


class Op:
    __slots__ = ("eng", "fn", "deps", "marked", "count", "dma_key", "dma_val")

    def __init__(self, eng, fn, dma_key=None):
        self.eng = eng
        self.fn = fn
        self.deps = ()
        self.marked = False
        self.count = 0
        self.dma_key = dma_key
        self.dma_val = 0


class Sched:
    def __init__(self, nc):
        self.nc = nc
        self.ops = {e: [] for e in ENGS}
        self.last_w = {}
        self.readers = {}
        self.dma_cnt = {}
        self.pending = {}
        self.ctx = None
        self.since_bar = []

    ALIAS = {"pC": "b4", "pCx": "b4", "pD": "b5", "pD2": "b5", "pD3": "b5", "pDn": "b5",
             "pDcb0": "b5", "pDcb1": "b5", "pT": "b01", "pO": "b67", "pA0": "b2", "pA1": "b3"}

    PSUM_KEYS = {"b01", "b2", "b3", "b4", "b5", "b67"}

    PAR_RE = _re.compile(r"^(xt|dtt|xnew)$|^(xsf|zs|Bb|Cb|ynl|yg)\d+$")

    def _k(self, k):
        k = self.ALIAS.get(k, k)
        if self.ctx is not None and self.PAR_RE.match(k):
            return k + "#" + str(self.ctx)
        return k

    def add(self, eng, fn, reads=(), writes=(), dma_key=None):
        reads = [self._k(k) for k in reads]
        writes = [self._k(k) for k in writes]
        if dma_key is not None and self.ctx is not None and self.PAR_RE.match(dma_key):
            dma_key = dma_key + "#" + str(self.ctx)
        op = Op(eng, fn, dma_key)
        deps = []
        seen = set()

        def dep(o):
            if o is not None and o is not op and id(o) not in seen:
                seen.add(id(o))
                deps.append(o)

        if self.pending.get(eng):
            for o in self.pending[eng]:
                dep(o)
            self.pending[eng] = []
        for b in reads:
            dep(self.last_w.get(b))
            if b in self.PSUM_KEYS:
                for r in self.readers.get(b, ()):
                    if r.eng != eng:
                        dep(r)
        for b in writes:
            dep(self.last_w.get(b))
            for r in self.readers.get(b, ()):
                dep(r)
        for b in reads:
            self.readers.setdefault(b, []).append(op)
        for b in writes:
            self.last_w[b] = op
            self.readers[b] = []
        op.deps = deps
        if dma_key is not None:
            self.dma_cnt[dma_key] = self.dma_cnt.get(dma_key, 0) + 16
            op.dma_val = self.dma_cnt[dma_key]
        self.ops[eng].append(op)
        self.since_bar.append(op)
        return op

    def barrier(self):
        ops = []
        for e in ENGS:
            comp = [o for o in self.ops[e] if o.dma_key is None]
            if comp:
                ops.append(comp[-1])
        last_dma = {}
        for o in self.since_bar:
            if o.dma_key is not None:
                last_dma[o.dma_key] = o
        ops.extend(last_dma.values())
        for e in ENGS:
            self.pending.setdefault(e, []).extend(ops)
        self.since_bar = []

    def pe(self, fn, r=(), w=()):
        return self.add("pe", fn, r, w)

    def act(self, fn, r=(), w=()):
        return self.add("act", fn, r, w)

    def dve(self, fn, r=(), w=()):
        return self.add("dve", fn, r, w)

    def pool(self, fn, r=(), w=()):
        return self.add("pool", fn, r, w)

    def dma(self, fn, key, r=(), w=(), q="sp"):
        return self.add(q, fn, r, w, dma_key=key)

    def emit(self):
        nc = self.nc
        for e in ENGS:
            for op in self.ops[e]:
                for d in op.deps:
                    if d.dma_key is None:
                        if d.eng == op.eng and not SAME_ENGINE_SYNC[d.eng]:
                            continue
                        d.marked = True
        for e in ENGS:
            c = 0
            for op in self.ops[e]:
                if op.dma_key is None and op.marked:
                    c += 1
                    op.count = c
        with ExitStack() as st:
            esem = {e: st.enter_context(nc.semaphore("es_" + e)) for e in ENGS}
            dsem = {}
            for k in self.dma_cnt:
                dsem[k] = st.enter_context(nc.semaphore("ds_%d" % len(dsem)))
            block = st.enter_context(nc.Block())

            def run(ename, eng):
                seen = {}
                for op in self.ops[ename]:
                    need = {}
                    for d in op.deps:
                        if d.dma_key is not None:
                            key = ("d", d.dma_key)
                            val = d.dma_val
                            sem = dsem[d.dma_key]
                        else:
                            if d.eng == ename and not SAME_ENGINE_SYNC[ename]:
                                continue
                            key = ("e", d.eng)
                            val = d.count
                            sem = esem[d.eng]
                        if key not in need or need[key][1] < val:
                            need[key] = (sem, val)
                    for key, (sem, val) in need.items():
                        if seen.get(key, 0) >= val:
                            continue
                        seen[key] = val
                        eng.wait_ge(sem, val)
                    ins = op.fn(eng)
                    if op.dma_key is not None:
                        ins.then_inc(dsem[op.dma_key], 16)
                    elif op.marked:
                        ins.then_inc(esem[ename], 1)
                if ename == "sp":
                    for k, v in self.dma_cnt.items():
                        eng.wait_ge(dsem[k], v)

            @block.sync
            def _(e):
                run("sp", e)

            @block.tensor
            def _(e):
                run("pe", e)

            @block.scalar
            def _(e):
                run("act", e)

            @block.vector
            def _(e):
                run("dve", e)

            @block.gpsimd
            def _(e):
                run("pool", e)


PC = {}
_o = 0
for _n, _w in (("LW", 32), ("LB", 8), ("BA", 8), ("BX", 8), ("LAM", 8), ("GL", 8), ("SW", 48),
               ("SB", 12), ("DS", 8), ("GS", 8), ("GM", 8), ("GP", 8)):
    PC[_n] = _o
    _o += _w
NPAR = _o
CI, CU, CN, CO, CUB, CNB, CBM, CBI = 0, 128, 256, 384, 512, 576, 640, 704
NCST = 720


def build_program(NT=SEQ // 128, SAMP=True, MLP=True, DBG=False, STAGE=9):
    nc = bass.Bass("TRN2", target_bir_lowering=False)
    S = Sched(nc)

    def din(name, shape):
        return nc.dram_tensor(name, list(shape), F32, kind="ExternalInput").ap()

    def dout(name, shape):
        return nc.dram_tensor(name, list(shape), F32, kind="ExternalOutput").ap()

    xp = din("xp", (SEQ, D))
    xs = din("xs", (TS, D))
    st_lc = din("st_lc", (NS * 3, D))
    st_lh = din("st_lh", (NS, D))
    st_sc = din("st_sc", (NS * 3, XBC))
    st_sh = din("st_sh", (NS, 1024, 128))
    w_in = din("w_in", (D, INP))
    w_out = din("w_out", (2 * D, D))
    w_up = din("w_up", (D, DFF))
    w_down = din("w_down", (DFF, D))
    w_a = din("w_a", (16, 64, 64))
    w_x = din("w_x", (16, 64, 64))
    pfm_d = din("pfm", (128, NPAR))
    cst_d = din("cst", (128, NCST))
    dtb_d = din("dt_bias", (16,))
    alog_d = din("a_log", (16,))
    gfin_d = din("g_final", (D,))

    y_p = dout("y_p", (SEQ, D))
    y_s = dout("y_s", (TS, D))
    o_plc = dout("o_plc", (3, D))
    o_plh = dout("o_plh", (8, 128))
    o_psc = dout("o_psc", (3, XBC))
    o_psh = dout("o_psh", (1024, 128))
    o_slc = dout("o_slc", (NS, 3, D))
    o_slh = dout("o_slh", (NS, D))
    o_ssc = dout("o_ssc", (NS, 3, XBC))
    o_ssh = dout("o_ssh", (NS, 1024, 128))
    scr = nc.dram_tensor("scr", [SEQ + TS, D], F32, kind=("ExternalOutput" if DBG else "Internal")).ap()

    st = ExitStack()
    with st:
        RW = 53200
        R = st.enter_context(nc.sbuf_tensor("R", [128, RW], F32))
        PS = st.enter_context(nc.psum_tensor("PS", [128, 4096], F32))
        ptr = [0]

        def alloc(nwords):
            a = ptr[0]
            ptr[0] += (nwords + 7) // 8 * 8
            pass
            return a

        def f32(n):
            a = alloc(n)
            return R[:, a:a + n]

        def bf(n):
            w = (n + 1) // 2
            a = alloc(w)
            return R[:, a:a + w].bitcast(BF16)[:, 0:n]

        def f3(c, t):
            return f32(c * t).rearrange("p (c t) -> p c t", c=c)

        def b3(c, t):
            return bf(c * t).rearrange("p (c t) -> p c t", c=c)

        def bank(b, n=512):
            return PS[:, 512 * b:512 * b + n]

        pT = PS[:, 0:1024]
        pTb = pT.bitcast(BF16)
        pA = [bank(2), bank(3)]
        pC = bank(4)
        pCb = pC.bitcast(BF16)
        pD = bank(5)
        pO = PS[:, 3072:4096]

        cst = f32(NCST)
        pfm = f32(NPAR)
        dtb_bc = f32(16)
        a_bc = f32(16)
        identb = bf(128)
        onesb = bf(128)
        Utrib = bf(128)
        mskb = bf(3 * TS)
        dah = bf(16)
        dal = bf(16)
        cfac = f32(8)
        c2fac = f32(8)
        tiny = f32(8)
        mhalf = f32(4)
        eps_t = f32(4)
        nbias = f32(16)
        wa_blk = b3(8, 128)
        wx_blk = b3(8, 128)
        hstate = f32(8)
        hT = f32(1024)
        hTb = bf(1024)

        ident = cst[:, CI:CI + 128]
        Utri = cst[:, CU:CU + 128]
        negm = cst[:, CN:CN + 128]
        onesf = cst[:, CO:CO + 128]
        Ublk = cst[0:TS, CUB:CUB + TS]
        negblk = cst[0:TS, CNB:CNB + TS]
        blkm = cst[0:TS, CBM:CBM + TS]
        blki = cst[0:TS, CBI:CBI + NS]

        S.dma(lambda e: e.dma_start(out=cst, in_=cst_d), "cst", w=["cst"])
        S.dma(lambda e: e.dma_start(out=pfm, in_=pfm_d), "pfm", w=["pfm"])
        S.dma(lambda e: e.dma_start(out=dtb_bc, in_=dtb_d.partition_broadcast(128)), "dtb", w=["dtb"])
        S.dma(lambda e: e.dma_start(out=a_bc, in_=alog_d.partition_broadcast(128)), "alog", w=["a_bc"])
        S.dve(lambda e: e.tensor_copy(out=identb, in_=ident), r=["cst"], w=["identb"])
        S.dve(lambda e: e.tensor_copy(out=onesb, in_=onesf), r=["cst"], w=["onesb"])
        S.dve(lambda e: e.tensor_copy(out=Utrib, in_=Utri), r=["cst"], w=["mskb"])
        S.dve(lambda e: e.tensor_copy(out=mskb[0:TS, 0:TS], in_=Ublk), r=["cst"], w=["mskb"])
        S.dve(lambda e: e.tensor_copy(out=mskb[0:TS, TS:2 * TS], in_=blkm), r=["cst"], w=["mskb"])
        S.pool(lambda e: e.memset(mhalf, -0.5), w=["mhalf"])
        S.pool(lambda e: e.memset(eps_t, EPS), w=["eps_t"])
        S.dve(lambda e: e.tensor_scalar(out=nbias[:, 0:8], in0=pfm[:, PC["BA"]:PC["BA"] + 8], scalar1=-1.0, scalar2=None, op0=ALU.mult), r=["pfm"], w=["nbias"])
        S.dve(lambda e: e.tensor_scalar(out=nbias[:, 8:16], in0=pfm[:, PC["BX"]:PC["BX"] + 8], scalar1=-1.0, scalar2=None, op0=ALU.mult), r=["pfm", "nbias"], w=["nbias"])
        S.pool(lambda e: e.memset(hstate, 0.0), w=["hstate"])
        S.pool(lambda e: e.memset(hT, 0.0), w=["hT"])
        S.pool(lambda e: e.memset(hTb, 0.0), w=["hTb"])
        S.pool(lambda e: e.memset(wa_blk, 0.0), w=["wa"])
        S.pool(lambda e: e.memset(wx_blk, 0.0), w=["wx"])
        S.act(lambda e: e.activation(out=a_bc, in_=a_bc, func=AF.Exp), r=["a_bc"], w=["a_bc"])
        S.dve(lambda e: e.tensor_scalar(out=a_bc, in0=a_bc, scalar1=-1.0, scalar2=None, op0=ALU.mult), r=["a_bc"], w=["a_bc"])
        lam = pfm[:, PC["LAM"]:PC["LAM"] + 8]
        S.act(lambda e: e.activation(out=tiny, in_=lam, func=AF.Exp, scale=-1.0), r=["pfm"], w=["tiny"])
        S.act(lambda e: e.activation(out=tiny, in_=tiny, func=AF.Ln, bias=1.0), r=["tiny"], w=["tiny"])
        S.dve(lambda e: e.tensor_scalar(out=cfac, in0=tiny, scalar1=-8.0, scalar2=None, op0=ALU.mult), r=["tiny"], w=["cfac"])
        S.dve(lambda e: e.tensor_scalar(out=c2fac, in0=tiny, scalar1=-16.0, scalar2=None, op0=ALU.mult), r=["tiny"], w=["cfac2"])
        for (wd, blk, nm) in ((w_a, wa_blk, "wa"), (w_x, wx_blk, "wx")):
            v = wd.rearrange("(c h) i j -> h i c j", h=2)
            for h2 in range(2):
                S.dma(lambda e, v=v, blk=blk, h2=h2: e.dma_start(
                    out=blk[64 * h2:64 * h2 + 64, :, 64 * h2:64 * h2 + 64], in_=v[h2]),
                    nm + str(h2), w=[nm], q="pool")

        base0 = ptr[0]

        def load_w(dst3, src2, nk, ncol, name, step=2048):
            sv = src2.rearrange("(k p) n -> p k n", p=128)
            pieces = [(k, c0, min(ncol, c0 + step)) for k in range(nk) for c0 in range(0, ncol, step)]
            for i, (k, c0, c1) in enumerate(pieces):
                S.dma(lambda e, k=k, c0=c0, c1=c1: e.dma_start(out=dst3[:, k, c0:c1], in_=sv[:, k, c0:c1]),
                      name, w=([name] if i == len(pieces) - 1 else []), q="pool")
            return name

        w_in_sb = b3(8, INP)
        w_out_sb = b3(16, D)
        if STAGE >= 1:
            k_win = load_w(w_in_sb, w_in, 8, INP, "w_in")
            k_wout = load_w(w_out_sb, w_out, 16, D, "w_out")

        xt = f32(D)
        xn = f32(D)
        junk = xn
        ss = f32(4)
        rstd = f32(4)
        hTt = b3(8, 128)
        lxb = f3(8, 131)
        xcb = f3(12, 131)
        sreg = f32(20 * NS * 7)
        lxs = sreg[:, 0:8 * NS * 7].rearrange("p (c s l) -> p c s l", c=8, s=NS)
        xcs = sreg[:, 8 * NS * 7:20 * NS * 7].rearrange("p (c s l) -> p c s l", c=12, s=NS)
        gl = f3(2, 128)
        zs = f3(8, 128)
        u = f3(8, 128)
        ub = b3(8, 128)
        lrut = f32(1280)
        gi = lrut[:, 0:512].rearrange("p (c t) -> p c t", c=4)
        av = lrut[:, 512:768].rearrange("p (c t) -> p c t", c=2)
        a2 = lrut[:, 768:1024].rearrange("p (c t) -> p c t", c=2)
        tmpb = lrut[:, 1024:1280].rearrange("p (c t) -> p c t", c=2)
        hs = f3(8, 128)
        ysq = b3(2, 128)
        rbc = f32(128)
        ynl = b3(8, 128)
        yns = b3(8, 128)
        xsf = f3(8, 128)
        Bb = b3(2, 128)
        Cb = b3(2, 128)
        dtr = f32(16)
        dtt = f32(16)
        da = f32(16)
        ncum = f32(16)
        dte = f32(16)
        cdec = f32(16)
        xdt = bf(1024)
        xdd = bf(1024)
        BT = bf(256)
        cbT = f3(2, 128)
        Dmf = f32(512)
        Emf = f32(512)
        Mmf = bf(512)
        Chf = bf(1024)
        Chp = Chf[:, 0:512].rearrange("p (a t) -> p a t", a=4)
        Chs = Chf.rearrange("p (a t) -> p a t", a=16)
        cvt = f3(2, 128)
        stg = f32(2560)
        stT = stg[:, 0:1024]
        lc_in = stg[:, 0:1024]
        sc_in = stg[:, 1024:2560]
        lh_in = stg[:, 0:1024]
        h0in = lxb.rearrange("p c t -> p (c t)")[:, 0:1024].rearrange("p (c t) -> p c t", c=8)
        h0Tb = hTb
        Bm = bf(256)
        pyo_f = f32(8 * TS)
        pyo_sb = pyo_f.rearrange("p (c t) -> p c t", c=8)
        damb = bf(2 * NS * 16).rearrange("p (i s h) -> p i s h", i=2, s=NS)
        dtot = f3(NS, 16)
        hnew = hT
        hout = xcb.rearrange("p c t -> p (c t)")[:, 0:1024].rearrange("p (c t) -> p c t", c=8)
        h0s = f3(8, NS)
        hfin = f3(8, NS)

        bf3 = lambda ap, c: ap.bitcast(BF16).rearrange("p (c t) -> p c t", c=c)
        xt_b = [xt, stg[:, 0:1024]]
        ynl_b = [ynl, bf3(stg[:, 1024:1536], 8)]
        Bb_b = [Bb, bf3(stg[:, 1536:1664], 2)]
        Cb_b = [Cb, bf3(stg[:, 1664:1792], 2)]
        dtt_b = [dtt, stg[:, 1792:1808]]
        xsf_b = [xsf, sreg[:, 0:1024].rearrange("p (c t) -> p c t", c=8)]
        zs_b = [zs, sreg[:, 1024:2048].rearrange("p (c t) -> p c t", c=8)]
        Wl_S = pyo_f[:, 0:256].bitcast(BF16)
        ysq_S = bf3(pyo_f[:, 256:384], 2)
        rbc_S = pyo_f[:, 384:512]
        STGW = [k + "#1" for k in ["xt", "dtt"] + ["ynl%d" % c for c in range(8)] + ["Bb0", "Bb1", "Cb0", "Cb1"]]
        STG_ALIAS = ["xt", "dtt"] + ["ynl%d" % c for c in range(8)] + ["Bb0", "Bb1", "Cb0", "Cb1"]

        def P(name, c=None, w=1):
            o = PC[name] + (0 if c is None else c * w)
            return pfm[:, o:o + w]

        def rms_rstd(xtile, T, keyx, junk, ss, rstd, sfx="", jkey=None):
            S.act(lambda e: e.activation(out=junk[0:T, :], in_=xtile[0:T, :], func=AF.Square, accum_out=ss[0:T, 0:1]),
                  r=[keyx], w=[jkey or ("xn" + sfx), "ss" + sfx])
            S.act(lambda e: e.activation(out=ss[0:T, 0:1], in_=ss[0:T, 0:1], func=AF.Ln, scale=1.0 / D, bias=eps_t[0:T, 0:1]),
                  r=["ss" + sfx, "eps_t"], w=["ss" + sfx])
            S.act(lambda e: e.activation(out=rstd[0:T, 0:1], in_=ss[0:T, 0:1], func=AF.Exp, scale=-0.5),
                  r=["ss" + sfx], w=["rstd" + sfx])

        pAA = PS[:, 1024:2048]

        def to_fm(T, gname, dst, dkey):
            for k in range(8):
                S.pe(lambda e, k=k: e.transpose(out=pAA[:, k * 128:k * 128 + T], in_=xn[0:T, k * 128:(k + 1) * 128],
                                                identity=ident[0:T, 0:T]), r=["xn", "cst"], w=["pA0", "pA1"])
            S.dve(lambda e: e.tensor_tensor(
                out=dst[:, :, 0:T], in0=pAA.rearrange("p (k t) -> p k t", k=8)[:, :, 0:T],
                in1=P(gname, 0, 8).unsqueeze(2).to_broadcast([128, 8, T]), op=ALU.mult),
                r=["pA0", "pA1", "pfm"], w=[dkey])

        def mixer_tile(mt, samp):
            T = TS if samp else 128
            row0 = SEQ if samp else mt * 128
            xsrc = xs if samp else xp[mt * 128:(mt + 1) * 128, :]
            last = (not samp) and mt == NT - 1
            par = 0 if samp else (NT - 1 - mt) % 2
            xt, ynl, Bb, Cb, dtt, xsf, zs = (xt_b[par], ynl_b[par], Bb_b[par], Cb_b[par], dtt_b[par], xsf_b[par], zs_b[par])
            Wl = cvt.rearrange("p a t -> p (a t)").bitcast(BF16) if samp else Wl_S
            wlk = ["cv_t0", "cv_t1"] if samp else ["WlS"]
            ysqS = ysq if samp else ysq_S
            rbcS = rbc if samp else rbc_S
            sk = "" if samp else "S"

            def inter(*gens):
                gens = list(gens)
                while gens:
                    for g_ in list(gens):
                        try:
                            next(g_)
                        except StopIteration:
                            gens.remove(g_)
                        yield

            pcnt = [0]

            def proj(ci):
                i = pcnt[0] % 2
                pcnt[0] += 1
                pa = pA[i]
                for k in range(8):
                    S.pe(lambda e, k=k: e.matmul(pa[:, 0:T], lhsT=w_in_sb[:, k, ci * 128:(ci + 1) * 128],
                                                 rhs=hTt[:, k, 0:T], start=(k == 0), stop=(k == 7)),
                         r=["hTt", "w_in"], w=["pA%d" % i])
                return pa, "pA%d" % i

            def new_cols(buf, sbuf_, c):
                if samp:
                    return sbuf_[:, c, :, 3:7]
                return buf[:, c, 3:131]

            def pa_view(pa):
                if samp:
                    return pa[:, 0:T].rearrange("p (s l) -> p s l", s=NS)
                return pa[:, 0:T]

            def tap(buf, sbuf_, c, k):
                if samp:
                    return sbuf_[:, c, :, k:k + 4]
                return buf[:, c, k:k + 128]

            def fm(t3, c):
                if samp:
                    return t3[:, c, 0:T].rearrange("p (s l) -> p s l", s=NS)
                return t3[:, c, 0:T]

            def conv(buf, sbuf_, c, wname, bname, out_ap, key_in, key_out):
                S.dve(lambda e: e.tensor_scalar(out=out_ap, in0=tap(buf, sbuf_, c, 3), scalar1=P(wname, c, 4)[:, 3:4],
                                                scalar2=P(bname, c), op0=ALU.mult, op1=ALU.add),
                      r=[key_in, "pfm"], w=[key_out])
                for k in (2, 1, 0):
                    S.dve(lambda e, k=k: e.scalar_tensor_tensor(out=out_ap, in0=tap(buf, sbuf_, c, k),
                                                                scalar=P(wname, c, 4)[:, k:k + 1], in1=out_ap,
                                                                op0=ALU.mult, op1=ALU.add),
                          r=[key_in, key_out, "pfm"], w=[key_out])
                if not samp:
                    S.dve(lambda e: e.tensor_copy(out=buf[:, c, 0:3], in_=buf[:, c, 128:131]), r=[key_in], w=[key_in])

            def g_lrux():
                for c in range(8):
                    pa, pk = proj(c)
                    S.act(lambda e, c=c, pa=pa: e.activation(out=new_cols(lxb, lxs, c), in_=pa_view(pa), func=AF.Copy),
                          r=[pk], w=["lx%d" % c])
                    conv(lxb, lxs, c, "LW", "LB", fm(u, c), "lx%d" % c, "u%d" % c)
                    yield

            def g_z():
                for c in range(8):
                    pa, pk = proj(16 + c)
                    S.act(lambda e, c=c, pa=pa: e.activation(out=zs[:, c, 0:T], in_=pa[:, 0:T], func=AF.Silu),
                          r=[pk], w=["zs%d" % c])
                    yield

            def g_xbc():
                for c in range(12):
                    pa, pk = proj(24 + c)
                    S.act(lambda e, c=c, pa=pa: e.activation(out=new_cols(xcb, xcs, c), in_=pa_view(pa), func=AF.Copy),
                          r=[pk], w=["xc%d" % c])
                    if c < 8:
                        conv(xcb, xcs, c, "SW", "SB", fm(cvt, c % 2), "xc%d" % c, "cv_t%d" % (c % 2))
                        S.act(lambda e, c=c: e.activation(out=xsf[:, c, 0:T], in_=cvt[:, c % 2, 0:T], func=AF.Silu),
                              r=["cv_t%d" % (c % 2)], w=["xsf%d" % c])
                    else:
                        g = (c - 8) % 2
                        dstb = Bb if c < 10 else Cb
                        nm = ("Bb%d" if c < 10 else "Cb%d") % g
                        conv(xcb, xcs, c, "SW", "SB", fm(cvt, g), "xc%d" % c, "cv_t%d" % g)
                        S.act(lambda e, g=g, dstb=dstb: e.activation(out=dstb[:, g, 0:T], in_=cvt[:, g, 0:T], func=AF.Silu),
                              r=["cv_t%d" % g], w=[nm])
                    yield
                for k in range(8):
                    S.pe(lambda e, k=k: e.matmul(pD[0:T, 0:16], lhsT=hTt[:, k, 0:T], rhs=w_in_sb[:, k, 4608:4624],
                                                 start=(k == 0), stop=(k == 7)),
                         r=["hTt", "w_in"], w=["pD"])
                S.dve(lambda e: e.tensor_tensor(out=dtr[0:T, :], in0=pD[0:T, 0:16], in1=dtb_bc[0:T, :], op=ALU.add),
                      r=["pD", "dtb"], w=["dtr"])
                S.act(lambda e: e.activation(out=dtr[0:T, :], in_=dtr[0:T, :], func=AF.Exp), r=["dtr"], w=["dtr"])
                S.act(lambda e: e.activation(out=dtt[0:T, :], in_=dtr[0:T, :], func=AF.Ln, bias=1.0), r=["dtr"], w=["dtt"])
                yield

            def g_lru(chunks, pg, kr, ki):
                for c in chunks:
                    pp = c % 2
                    S.pool(lambda e, c=c: e.tensor_copy(out=ub[:, c, 0:T], in_=u[:, c, 0:T]),
                           r=["u%d" % c], w=["ub%d" % c])
                    yield
                    S.pe(lambda e, c=c: e.matmul(pg[:, 0:T], lhsT=wa_blk[:, c, :], rhs=ub[:, c, 0:T], start=True, stop=True),
                         r=["ub%d" % c, "wa"], w=[kr])
                    S.pe(lambda e, c=c: e.matmul(pg[:, 128:128 + T], lhsT=wx_blk[:, c, :], rhs=ub[:, c, 0:T], start=True, stop=True),
                         r=["ub%d" % c, "wx"], w=[ki])
                    yield
                    S.act(lambda e, c=c, pp=pp: e.activation(out=gi[:, 2 * pp, 0:T], in_=pg[:, 0:T], func=AF.Exp, scale=-1.0, bias=nbias[:, c:c + 1]),
                          r=[kr, "nbias"], w=["rg%d" % pp])
                    S.act(lambda e, c=c, pp=pp: e.activation(out=gi[:, 2 * pp + 1, 0:T], in_=pg[:, 128:128 + T], func=AF.Exp, scale=-1.0, bias=nbias[:, 8 + c:9 + c]),
                          r=[ki, "nbias"], w=["ig%d" % pp])
                    S.act(lambda e, pp=pp: e.activation(out=gi[:, 2 * pp:2 * pp + 2, 0:T], in_=gi[:, 2 * pp:2 * pp + 2, 0:T], func=AF.Ln, bias=1.0),
                          r=["rg%d" % pp, "ig%d" % pp], w=["rg%d" % pp, "ig%d" % pp])
                    S.act(lambda e, pp=pp: e.activation(out=gi[:, 2 * pp:2 * pp + 2, 0:T], in_=gi[:, 2 * pp:2 * pp + 2, 0:T], func=AF.Exp, scale=-1.0),
                          r=["rg%d" % pp, "ig%d" % pp], w=["rg%d" % pp, "ig%d" % pp])
                    S.act(lambda e, c=c, pp=pp: e.activation(out=av[:, pp, 0:T], in_=gi[:, 2 * pp, 0:T], func=AF.Exp, scale=cfac[:, c:c + 1]),
                          r=["rg%d" % pp, "cfac"], w=["av%d" % pp])
                    S.act(lambda e, c=c, pp=pp: e.activation(out=a2[:, pp, 0:T], in_=gi[:, 2 * pp, 0:T], func=AF.Exp, scale=c2fac[:, c:c + 1]),
                          r=["rg%d" % pp, "cfac2"], w=["a2%d" % pp])
                    S.act(lambda e, pp=pp: e.activation(out=a2[:, pp, 0:T], in_=a2[:, pp, 0:T], func=AF.Ln, scale=-1.0, bias=1.0),
                          r=["a2%d" % pp], w=["a2%d" % pp])
                    S.act(lambda e, pp=pp: e.activation(out=a2[:, pp, 0:T], in_=a2[:, pp, 0:T], func=AF.Exp, scale=0.5),
                          r=["a2%d" % pp], w=["a2%d" % pp])
                    yield
                    S.dve(lambda e, c=c, pp=pp: e.tensor_tensor(out=tmpb[:, pp, 0:T], in0=gi[:, 2 * pp + 1, 0:T], in1=u[:, c, 0:T], op=ALU.mult),
                          r=["ig%d" % pp, "u%d" % c], w=["tb%d" % pp])
                    S.dve(lambda e, pp=pp: e.tensor_tensor(out=tmpb[:, pp, 0:T], in0=tmpb[:, pp, 0:T], in1=a2[:, pp, 0:T], op=ALU.mult),
                          r=["tb%d" % pp, "a2%d" % pp], w=["tb%d" % pp])
                    if samp:
                        a3 = av[:, pp, 0:T].rearrange("p (s l) -> p s l", s=NS)
                        b3v = tmpb[:, pp, 0:T].rearrange("p (s l) -> p s l", s=NS)
                        S.dve(lambda e, c=c, a3=a3: e.tensor_tensor(out=rbc[:, 0:NS], in0=a3[:, :, 0], in1=h0s[:, c, :], op=ALU.mult),
                              r=["av%d" % pp, "h0s"], w=["rbc"])
                        S.dve(lambda e, b3v=b3v: e.tensor_tensor(out=b3v[:, :, 0], in0=b3v[:, :, 0], in1=rbc[:, 0:NS], op=ALU.add),
                              r=["tb%d" % pp, "rbc"], w=["tb%d" % pp])
                        S.dve(lambda e, a3=a3: e.memset(a3[:, :, 0], 0.0), r=["rbc"], w=["av%d" % pp])
                        S.dve(lambda e, c=c, pp=pp: e.tensor_tensor_scan(out=hs[:, c, 0:T], data0=av[:, pp, 0:T], data1=tmpb[:, pp, 0:T],
                                                                         initial=0.0, op0=ALU.mult, op1=ALU.add),
                              r=["av%d" % pp, "tb%d" % pp], w=["hs%d" % c])
                        S.dve(lambda e, c=c: e.tensor_copy(out=hfin[:, c, :], in_=hs[:, c, 0:T].rearrange("p (s l) -> p s l", s=NS)[:, :, 3]),
                              r=["hs%d" % c], w=["hfin"])
                    else:
                        S.dve(lambda e, c=c, pp=pp: e.tensor_tensor_scan(out=hs[:, c, 0:T], data0=av[:, pp, 0:T], data1=tmpb[:, pp, 0:T],
                                                                         initial=hstate[:, c:c + 1], op0=ALU.mult, op1=ALU.add),
                              r=["av%d" % pp, "tb%d" % pp, "hstate"], w=["hs%d" % c])
                        S.dve(lambda e, c=c: e.tensor_copy(out=hstate[:, c:c + 1], in_=hs[:, c, T - 1:T]),
                              r=["hs%d" % c], w=["hstate"])
                    yield

            def g_gate():
                for c in range(8):
                    pp = c % 2
                    pa, pk = proj(8 + c)
                    S.act(lambda e, pp=pp, pa=pa: e.activation(out=gl[:, pp, 0:T], in_=pa[:, 0:T], func=AF.Gelu_apprx_tanh),
                          r=[pk], w=["gl%d" % pp])
                    S.pool(lambda e, c=c, pp=pp: e.tensor_tensor(out=hs[:, c, 0:T], in0=hs[:, c, 0:T], in1=gl[:, pp, 0:T], op=ALU.mult),
                           r=["hs%d" % c, "gl%d" % pp], w=["yl%d" % c, "hs%d" % c])
                    S.pool(lambda e, c=c, pp=pp: e.tensor_tensor(out=ysq[:, pp, 0:T], in0=hs[:, c, 0:T], in1=hs[:, c, 0:T], op=ALU.mult),
                           r=["yl%d" % c], w=["ysq%d" % pp])
                    S.pe(lambda e, c=c, pp=pp: e.matmul(pD[:, 128:128 + T], lhsT=onesb, rhs=ysq[:, pp, 0:T], start=(c == 0), stop=(c == 7)),
                         r=["ysq%d" % pp, "onesb"], w=["pDn"])
                    yield

            def norm_apply(T, eps_, src, skey, gname, dst, dkey, c0, c1, rbc, rk, pst, pk):
                S.act(lambda e: e.activation(out=rbc[:, 0:T], in_=pst, func=AF.Ln,
                                             scale=1.0 / ((c1 - c0) * 128), bias=eps_t[:, 0:1]),
                      r=[pk, "eps_t"], w=[rk])
                S.act(lambda e: e.activation(out=rbc[:, 0:T], in_=rbc[:, 0:T], func=AF.Exp, scale=-0.5), r=[rk], w=[rk])
                for c in range(c0, c1):
                    S.dve(lambda e, c=c: e.scalar_tensor_tensor(out=dst[:, c, 0:T], in0=src[:, c, 0:T], scalar=P(gname, c),
                                                                in1=rbc[:, 0:T], op0=ALU.mult, op1=ALU.mult),
                          r=[skey % c, rk, "pfm"], w=[dkey % c])

            Um = mskb[0:TS, 0:TS] if samp else Utrib
            ngm = negblk if samp else negm
            allm = mskb[0:TS, TS:2 * TS] if samp else onesb
            d4 = lambda ap: ap[:, 0:4 * T].rearrange("p (a t) -> p a t", a=4)
            Em, Dm, Mm, pC4 = d4(Emf), d4(Dmf), d4(Mmf), d4(pC)

            def g_ssd():
                for c in range(8):
                    S.pe(lambda e, c=c: e.transpose(out=pT[0:T, c * 128:(c + 1) * 128], in_=xsf[:, c, 0:T], identity=ident),
                         r=["xsf%d" % c, "cst"], w=["pT"])
                for g in range(2):
                    S.pe(lambda e, g=g: e.transpose(out=pCb[0:T, 128 + g * 128:128 + (g + 1) * 128], in_=Bb[:, g, 0:T], identity=identb),
                         r=["Bb%d" % g, "identb"], w=["pC", "pCx"])
                S.dve(lambda e: e.tensor_tensor(out=xdt[0:T, :].rearrange("p (h q) -> p h q", h=16),
                                                in0=pT[0:T, :].rearrange("p (h q) -> p h q", h=16),
                                                in1=dtt[0:T, :].unsqueeze(2).to_broadcast([T, 16, 64]), op=ALU.mult),
                      r=["pT", "dtt"], w=["xdt"])
                S.dve(lambda e: e.tensor_copy(out=BT[0:T, :], in_=pCb[0:T, 128:384]), r=["pC"], w=["BT"])
                S.dve(lambda e: e.tensor_tensor(out=da[0:T, :], in0=dtt[0:T, :], in1=a_bc[0:T, :], op=ALU.mult),
                      r=["dtt", "a_bc"], w=["da"])
                yield
                S.dve(lambda e: e.tensor_copy(out=dah[0:T, :], in_=da[0:T, :]), r=["da"], w=["dah"])
                S.dve(lambda e: e.tensor_tensor(out=dal[0:T, :], in0=da[0:T, :], in1=dah[0:T, :], op=ALU.subtract),
                      r=["da", "dah"], w=["dal"])
                for i, dx in enumerate((dah, dal)):
                    S.pe(lambda e, dx=dx, i=i: e.matmul(pC[0:T, 0:16], lhsT=Um[0:T, 0:T], rhs=dx[0:T, :], start=(i == 0), stop=(i == 1)),
                         r=["dah", "dal", "mskb"], w=["pC", "pCx"])
                for i, dx in enumerate((dah, dal)):
                    S.pe(lambda e, dx=dx, i=i: e.matmul(pC[0:T, 16:32], lhsT=allm[0:T, 0:T], rhs=dx[0:T, :], start=(i == 0), stop=(i == 1)),
                         r=["dah", "dal", "mskb", "onesb"], w=["pC", "pCx"])
                if not samp:
                    for i, dx in enumerate((dah, dal)):
                        S.pe(lambda e, dx=dx, i=i: e.matmul(pC[:, 32:48], lhsT=onesb, rhs=dx, start=(i == 0), stop=(i == 1)),
                             r=["dah", "dal", "onesb"], w=["pC", "pCx"])
                for g in range(2):
                    S.pe(lambda e, g=g: e.matmul(pC[0:T, 256 + g * 128:256 + g * 128 + T], lhsT=Bb[:, g, 0:T], rhs=Cb[:, g, 0:T],
                                                 start=True, stop=True), r=["Bb%d" % g, "Cb%d" % g], w=["pC", "pCx"])
                yield
                S.dve(lambda e: e.tensor_scalar(out=ncum[0:T, :], in0=pC[0:T, 0:16], scalar1=-1.0, scalar2=None, op0=ALU.mult),
                      r=["pC"], w=["ncum"])
                S.dve(lambda e: e.tensor_tensor(out=dte[0:T, :], in0=pC[0:T, 16:32], in1=ncum[0:T, :], op=ALU.add),
                      r=["pC", "ncum"], w=["dte"])
                if not samp:
                    S.dve(lambda e: e.tensor_copy(out=cdec, in_=pC[:, 32:48]), r=["pC"], w=["cdec"])
                S.dve(lambda e: e.tensor_copy(out=cbT[0:T, :, 0:T], in_=pC[0:T, 256:512].rearrange("p (g t) -> p g t", g=2)[:, :, 0:T]),
                      r=["pC"], w=["cbT0", "cbT1"])
                S.act(lambda e: e.activation(out=dte[0:T, :], in_=dte[0:T, :], func=AF.Exp), r=["dte"], w=["dte"])
                if not samp:
                    S.act(lambda e: e.activation(out=cdec, in_=cdec, func=AF.Exp), r=["cdec"], w=["cdec"])
                S.dve(lambda e: e.tensor_tensor(out=xdd[0:T, :].rearrange("p (h q) -> p h q", h=16),
                                                in0=xdt[0:T, :].rearrange("p (h q) -> p h q", h=16),
                                                in1=dte[0:T, :].unsqueeze(2).to_broadcast([T, 16, 64]), op=ALU.mult),
                      r=["xdt", "dte"], w=["xdd"])
                yield
                if not samp:
                    for g in range(2):
                        S.pe(lambda e, g=g: e.matmul(pO[:, g * 512:(g + 1) * 512], lhsT=BT[:, g * 128:(g + 1) * 128],
                                                     rhs=xdd[:, g * 512:(g + 1) * 512], start=True, stop=True),
                             r=["BT", "xdd"], w=["pO"])
                    S.dve(lambda e: e.tensor_tensor(out=hT.rearrange("p (h q) -> p h q", h=16),
                                                    in0=hT.rearrange("p (h q) -> p h q", h=16),
                                                    in1=cdec.unsqueeze(2).to_broadcast([128, 16, 64]), op=ALU.mult),
                          r=["hT", "cdec"], w=["hT"])
                    S.dve(lambda e: e.tensor_tensor(out=hT, in0=hT, in1=pO, op=ALU.add), r=["hT", "pO"], w=["hT"])
                    yield
                for q4 in range(4):
                    g = q4 // 2
                    for i, (dx, Wf, wk) in enumerate(((dah, Mmf, ["Mm"]), (dal, Wl, wlk))):
                        S.pool(lambda e, q4=q4, dx=dx, Wf=Wf: e.tensor_tensor(out=d4(Wf)[0:T], in0=Um[0:T, 0:T].unsqueeze(1).to_broadcast([T, 4, T]),
                                                                             in1=dx[0:T, q4 * 4:q4 * 4 + 4].unsqueeze(2).to_broadcast([T, 4, T]),
                                                                             op=ALU.mult), r=["dah", "dal", "mskb"], w=wk)
                        S.pe(lambda e, Wf=Wf, i=i: e.matmul(pC[:, 0:4 * T], lhsT=onesb[0:T, :], rhs=Wf[0:T, 0:4 * T],
                                                            start=(i == 0), stop=(i == 1)), r=wk + ["onesb"], w=["pC", "pCx"])
                    S.dve(lambda e: e.tensor_copy(out=Em, in_=pC4), r=["pC"], w=["Em"])
                    yield
                    for hh in range(4):
                        h = q4 * 4 + hh
                        S.dve(lambda e, h=h, hh=hh: e.scalar_tensor_tensor(out=Dm[0:T, hh, :], in0=Em[0:T, hh, :],
                                                                           scalar=ncum[0:T, h:h + 1], in1=ngm[0:T, 0:T],
                                                                           op0=ALU.add, op1=ALU.add),
                              r=["Em", "ncum", "cst"], w=["Dm"])
                    S.act(lambda e: e.activation(out=Em, in_=Em, func=AF.Exp), r=["Em"], w=["Em"])
                    S.act(lambda e: e.activation(out=Dm[0:T], in_=Dm[0:T], func=AF.Exp), r=["Dm"], w=["Dm"])
                    S.dve(lambda e, g=g: e.tensor_tensor(out=Mm[0:T], in0=Dm[0:T],
                                                         in1=cbT[0:T, g, 0:T].unsqueeze(1).to_broadcast([T, 4, T]), op=ALU.mult),
                          r=["Dm", "cbT%d" % g], w=["Mm"])
                    S.pool(lambda e, g=g, q4=q4: e.tensor_tensor(out=(Chs[:, q4 * 4:q4 * 4 + 4, :] if samp else Chp), in0=Em,
                                                                in1=Cb[:, g, 0:T].unsqueeze(1).to_broadcast([128, 4, T]), op=ALU.mult),
                          r=["Em", "Cb%d" % g], w=["Ch"])
                    yield
                    for hh in range(4):
                        h = q4 * 4 + hh
                        c = h // 2
                        h2 = h % 2
                        po = pT[64 * h2:64 * h2 + 64, c * 128:c * 128 + T]
                        S.pe(lambda e, h=h, hh=hh, po=po: e.matmul(po, lhsT=xdt[0:T, h * 64:(h + 1) * 64], rhs=Mm[0:T, hh, :],
                                                                   start=True, stop=samp), r=["xdt", "Mm"], w=["pT"])
                        if not samp:
                            S.pe(lambda e, h=h, hh=hh, po=po: e.matmul(po, lhsT=hTb[:, h * 64:(h + 1) * 64], rhs=Chp[:, hh, :],
                                                                       start=False, stop=True), r=["hTb", "Ch"], w=["pT"])
                    yield

            def late_outputs():
                M = T if samp else 3
                t0 = 0 if samp else 125
                if samp or last:
                    for blk, col0 in enumerate((0, 512, 3072, 3584, 4096)):
                        for k in range(8):
                            S.pe(lambda e, k=k, col0=col0: e.matmul(pO[0:M, 0:512], lhsT=hTt[:, k, t0:t0 + M],
                                                                    rhs=w_in_sb[:, k, col0:col0 + 512], start=(k == 0), stop=(k == 7)),
                                 r=["hTt", "w_in"], w=["pO"])
                        S.dve(lambda e, blk=blk: e.tensor_copy(out=stg[0:M, blk * 512:(blk + 1) * 512], in_=pO[0:M, 0:512]),
                              r=["pO"], w=["stg"] + STGW)
                if last:
                    S.dma(lambda e: e.dma_start(out=o_plc, in_=stg[0:3, 0:1024]), "o_plc", r=["stg"])
                    S.dma(lambda e: e.dma_start(out=o_psc, in_=stg[0:3, 1024:2560]), "o_psc", r=["stg"])
                if samp:
                    for s in range(NS):
                        S.dma(lambda e, s=s: e.dma_start(out=o_slc[s], in_=stg[4 * s + 1:4 * s + 4, 0:1024]), "o_slc", r=["stg"])
                        S.dma(lambda e, s=s: e.dma_start(out=o_ssc[s], in_=stg[4 * s + 1:4 * s + 4, 1024:2560]), "o_ssc", r=["stg"])
                if last:
                    S.pe(lambda e: e.transpose(out=pC[0:8, 0:128], in_=hstate, identity=ident), r=["hstate", "cst"], w=["pC", "pCx"])
                    S.act(lambda e: e.activation(out=stT[0:8, 0:128], in_=pC[0:8, 0:128], func=AF.Copy), r=["pC"], w=["stg"] + STGW)
                    S.dma(lambda e: e.dma_start(out=o_plh, in_=stT[0:8, 0:128]), "o_plh", r=["stg"])
                if samp:
                    for c in range(8):
                        S.pe(lambda e, c=c: e.transpose(out=pT[0:NS, c * 128:(c + 1) * 128], in_=hfin[:, c, :], identity=ident),
                             r=["hfin", "cst"], w=["pT"])
                    S.act(lambda e: e.activation(out=lh_in[0:NS, :], in_=pT[0:NS, :], func=AF.Copy), r=["pT"], w=["stg"] + STGW)
                    S.dma(lambda e: e.dma_start(out=o_slh, in_=lh_in[0:NS, :]), "o_slh", r=["stg"])


            def genP():
                S.dma(lambda e: e.dma_start(out=xt[0:T, :], in_=xsrc), "xt", w=["xt"])
                rms_rstd(xt, T, "xt", junk, ss, rstd)
                S.act(lambda e: e.activation(out=xn[0:T, :], in_=xt[0:T, :], func=AF.Copy, scale=rstd[0:T, 0:1]),
                      r=["xt", "rstd"], w=["xn"])
                to_fm(T, "GM", hTt, "hTt")

                if samp:
                    S.dma(lambda e: e.dma_start(out=lc_in[0:48, :], in_=st_lc), "stg", w=["stg"])
                    S.dma(lambda e: e.dma_start(out=sc_in[0:48, :], in_=st_sc), "stg", w=["stg"])
                    S.dma(lambda e: e.dma_start(out=lh_in[64:64 + NS, :], in_=st_lh), "stg", w=["stg"])
                    for c in range(8):
                        S.pe(lambda e, c=c: e.transpose(out=pC[:, 0:48], in_=lc_in[0:48, c * 128:(c + 1) * 128],
                                                        identity=ident[0:48, 0:48]), r=["stg", "cst"], w=["pC"])
                        S.act(lambda e, c=c: e.activation(out=lxs[:, c, :, 0:3],
                                                          in_=pC[:, 0:48].rearrange("p (s j) -> p s j", s=NS),
                                                          func=AF.Copy), r=["pC"], w=["lx%d" % c])
                        S.pe(lambda e, c=c: e.transpose(out=pD[:, 0:NS], in_=lh_in[64:64 + NS, c * 128:(c + 1) * 128],
                                                        identity=ident[64:64 + NS, 64:64 + NS]), r=["stg", "cst"], w=["pD"])
                        S.dve(lambda e, c=c: e.tensor_copy(out=h0s[:, c, :], in_=pD[:, 0:NS]), r=["pD"], w=["h0s"])
                    for c in range(12):
                        S.pe(lambda e, c=c: e.transpose(out=pC[:, 0:48], in_=sc_in[0:48, c * 128:(c + 1) * 128],
                                                        identity=ident[0:48, 0:48]), r=["stg", "cst"], w=["pC"])
                        S.act(lambda e, c=c: e.activation(out=xcs[:, c, :, 0:3],
                                                          in_=pC[:, 0:48].rearrange("p (s j) -> p s j", s=NS),
                                                          func=AF.Copy), r=["pC"], w=["xc%d" % c])

                yield
                yield from inter(g_lrux(), g_z())
                yield from inter(g_xbc())
                yield from inter(g_lru((0, 2, 4, 6), pA[0], "pA0", "pA0"), g_lru((1, 3, 5, 7), pA[1], "pA1", "pA1"))
                yield from inter(g_gate())
                norm_apply(T, EPS, hs, "yl%d", "GL", ynl, "ynl%d", 0, 8, rbc, "rbc", pD[:, 128:128 + T], "pDn")
                yield
                if samp:
                    late_outputs()

            def genS():
                yield from inter(g_ssd())
                if samp:
                    ssd_sample_states_prep()

                for c in range(8):
                    S.dve(lambda e, c=c: e.scalar_tensor_tensor(out=xsf[:, c, 0:T], in0=xsf[:, c, 0:T], scalar=P("DS", c),
                                                                in1=pT[:, c * 128:c * 128 + T], op0=ALU.mult, op1=ALU.add),
                          r=["pT", "xsf%d" % c, "pfm"], w=["xsf%d" % c])
                if samp:
                    S.dve(lambda e: e.tensor_tensor(out=xsf[:, :, 0:T], in0=xsf[:, :, 0:T], in1=pyo_sb, op=ALU.add),
                          r=["xsf%d" % c for c in range(8)] + ["pyo_sb"], w=["xsf%d" % c for c in range(8)])
                S.dve(lambda e: e.tensor_tensor(out=xsf[:, :, 0:T], in0=xsf[:, :, 0:T], in1=zs[:, :, 0:T], op=ALU.mult),
                      r=["xsf%d" % c for c in range(8)] + ["zs%d" % c for c in range(8)], w=["yg%d" % c for c in range(8)] + ["xsf%d" % c for c in range(8)])
                yield
                if not samp:
                    S.act(lambda e: e.activation(out=hTb, in_=hT, func=AF.Copy), r=["hT"], w=["hTb"])
                    if last:
                        for c in range(8):
                            S.pe(lambda e, c=c: e.transpose(out=pO[:, c * 128:(c + 1) * 128], in_=hT[:, c * 128:(c + 1) * 128], identity=ident),
                                 r=["hT", "cst"], w=["pO"])
                        S.dve(lambda e: e.tensor_copy(out=stT, in_=pO), r=["pO"], w=["stg"] + STGW)
                        S.dma(lambda e: e.dma_start(out=o_psh.rearrange("(c q) n -> q c n", q=128),
                                                    in_=stT.rearrange("p (c n) -> p c n", c=8)), "o_psh", r=["stg"])
                yield
                for g in range(2):
                    for c in range(4 * g, 4 * g + 4):
                        pp = c % 2
                        S.pool(lambda e, c=c, pp=pp: e.tensor_tensor(out=ysqS[:, pp, 0:T], in0=xsf[:, c, 0:T], in1=xsf[:, c, 0:T], op=ALU.mult),
                               r=["yg%d" % c], w=["ysq%s%d" % (sk, pp)])
                        S.pe(lambda e, c=c, pp=pp, g=g: e.matmul(pO[:, 0:T], lhsT=onesb, rhs=ysqS[:, pp, 0:T],
                                                                 start=(c == 4 * g), stop=(c == 4 * g + 3)),
                             r=["ysq%s%d" % (sk, pp), "onesb"], w=["pO"])
                    norm_apply(T, EPS, xsf, "yg%d", "GS", yns, "yns%d", 4 * g, 4 * g + 4, rbcS, "rbc" + sk, pO[:, 0:T], "pO")

                yield
                for nb in range(2):
                    for kc in range(16):
                        src = ynl if kc < 8 else yns
                        S.pe(lambda e, kc=kc, nb=nb, src=src: e.matmul(pO[0:T, nb * 512:(nb + 1) * 512], lhsT=src[:, kc % 8, 0:T],
                                                                       rhs=w_out_sb[:, kc, nb * 512:(nb + 1) * 512],
                                                                       start=(kc == 0), stop=(kc == 15)),
                             r=[("ynl%d" if kc < 8 else "yns%d") % (kc % 8), "w_out"], w=["pO"])
                S.dve(lambda e: e.tensor_tensor(out=xt[0:T, :], in0=pO[0:T, :], in1=xt[0:T, :], op=ALU.add),
                      r=["pO", "xt"], w=["xt"])
                S.dma(lambda e: e.dma_start(out=scr[row0:row0 + T, :], in_=xt[0:T, :]), "xnew", r=["xt"], w=["scr%d" % mt])

                if last:
                    late_outputs()

            return par, genP, genS

        def ssd_sample_states_prep():
            T = TS
            for i, dx in enumerate((dah, dal)):
                S.dve(lambda e, dx=dx, i=i: e.tensor_tensor(out=damb[0:T, i], in0=dx[0:T, :].unsqueeze(1).to_broadcast([T, NS, 16]),
                                                            in1=blki.unsqueeze(2).to_broadcast([T, NS, 16]), op=ALU.mult),
                      r=["dah", "dal", "cst"], w=["dam%d" % i])
                S.pe(lambda e, i=i: e.matmul(pD[:, 0:256], lhsT=onesb[0:T, :], rhs=damb[0:T, i].rearrange("p s h -> p (s h)"),
                                             start=(i == 0), stop=(i == 1)), r=["dam%d" % i, "onesb"], w=["pD", "pD2", "pD3", "pDn"])
            S.act(lambda e: e.activation(out=dtot.rearrange("p s h -> p (s h)"), in_=pD[:, 0:256], func=AF.Exp),
                  r=["pD"], w=["dtot"])
            S.barrier()
            dtotP = Dmf[:, 0:NS * 8].rearrange("p (s c) -> p s c", s=NS)
            for h2 in range(2):
                S.dve(lambda e, h2=h2: e.tensor_copy(out=dtotP[64 * h2:64 * h2 + 64],
                                                     in_=dtot[64 * h2:64 * h2 + 64].rearrange("p s (c two) -> p s c two", two=2)[:, :, :, h2]),
                      r=["dtot"], w=["dtotP"])
            h0in_b = [h0in, u]
            hout_b = [hout, hs]
            h0b_b = [hTb, ub.rearrange("p c t -> p (c t)")]
            h0Tb_b = [lrut[:, 0:512].bitcast(BF16), lrut[:, 512:1024].bitcast(BF16)]
            xdm_b = [hT[:, 0:512].bitcast(BF16), hT[:, 512:1024].bitcast(BF16)]
            pTr_b = [pC.bitcast(BF16), pD.bitcast(BF16)]
            pTk = [["pC"], ["pD"]]
            for s in range(NS):
                q = s % 2
                hi, ho, h0b, hb, xdm, pTr, tk = h0in_b[q], hout_b[q], h0b_b[q], h0Tb_b[q], xdm_b[q], pTr_b[q], pTk[q]
                S.dma(lambda e, s=s, hi=hi: e.dma_start(out=hi, in_=st_sh[s].rearrange("(c q) n -> q c n", q=128)),
                      "h0in%d" % q, w=["h0in%d" % q], q="pool")
                S.act(lambda e, hi=hi, h0b=h0b: e.activation(out=h0b, in_=hi.rearrange("p c n -> p (c n)"), func=AF.Copy),
                      r=["h0in%d" % q], w=["h0b%d" % q])
                for c in range(8):
                    S.pe(lambda e, c=c, h0b=h0b, pTr=pTr: e.transpose(out=pTr[:, c * 128:(c + 1) * 128], in_=h0b[:, c * 128:(c + 1) * 128],
                                                                      identity=identb), r=["h0b%d" % q, "identb"], w=tk)
                S.dve(lambda e, hb=hb, pTr=pTr: e.tensor_copy(out=hb, in_=pTr[:, 0:1024]), r=tk, w=["h0Tb%d" % q])
                for h in range(16):
                    S.pe(lambda e, h=h, s=s, hb=hb: e.matmul(pA[0][64 * (h % 2):64 * (h % 2) + 64, (h // 2) * TS + 4 * s:(h // 2) * TS + 4 * s + 4],
                                                             lhsT=hb[:, h * 64:(h + 1) * 64], rhs=Chs[:, h, 4 * s:4 * s + 4],
                                                             start=True, stop=True),
                         r=["h0Tb%d" % q, "Ch"], w=["pA0"])
                S.dve(lambda e, s=s, xdm=xdm: e.tensor_scalar(out=xdm[0:T, :], in0=xdd[0:T, :], scalar1=blki[:, s:s + 1], scalar2=None, op0=ALU.mult),
                      r=["xdd", "cst"], w=["xdm%d" % q])
                for c in range(8):
                    S.pe(lambda e, c=c, xdm=xdm: e.matmul(pO[:, c * 128:(c + 1) * 128], lhsT=xdm[0:T, c * 128:(c + 1) * 128],
                                                          rhs=BT[0:T, (c // 4) * 128:(c // 4 + 1) * 128], start=True, stop=True),
                         r=["xdm%d" % q, "BT"], w=["pO"])
                for c in range(8):
                    S.dve(lambda e, c=c, s=s, hi=hi, ho=ho: e.scalar_tensor_tensor(out=ho[:, c, :], in0=hi[:, c, :], scalar=dtotP[:, s, c:c + 1],
                                                                                   in1=pO[:, c * 128:(c + 1) * 128], op0=ALU.mult, op1=ALU.add),
                          r=["h0in%d" % q, "dtotP", "pO"], w=["hout%d" % q])
                S.dma(lambda e, s=s, ho=ho: e.dma_start(out=o_ssh[s].rearrange("(c q) n -> q c n", q=128), in_=ho),
                      "hout%d" % q, r=["hout%d" % q])
            S.act(lambda e: e.activation(out=pyo_sb.rearrange("p c t -> p (c t)"), in_=pA[0][:, 0:8 * TS], func=AF.Copy),
                  r=["pA0"], w=["pyo_sb"])

        S.pool(lambda e: e.memset(lxb, 0.0), w=["lx%d" % c for c in range(8)])
        S.pool(lambda e: e.memset(xcb, 0.0), w=["xc%d" % c for c in range(12)])

        def drive(g_, par):
            S.ctx = par
            try:
                next(g_)
                return True
            except StopIteration:
                return False
            finally:
                S.ctx = None

        tiles = [mixer_tile(mt, False) for mt in range(NT)]
        RATIO = 3
        par0, gP0, _ = tiles[0]
        g = gP0()
        while drive(g, par0):
            pass
        for n in range(NT):
            par, _, gS = tiles[n]
            gs = gS()
            alive_s = True
            alive_p = False
            if n + 1 < NT:
                parn, gPn, _ = tiles[n + 1]
                gp = gPn()
                alive_p = True
            while alive_s or alive_p:
                for _ in range(RATIO):
                    if alive_p:
                        alive_p = drive(gp, parn)
                if alive_s:
                    alive_s = drive(gs, par)
        S.barrier()
        if SAMP:
            pars, gPs, gSs = mixer_tile(SEQ // 128, True)
            for g in (gPs(), gSs()):
                while drive(g, pars):
                    pass

        S.barrier()
        ptr[0] = base0
        w_up_sb = b3(8, DFF)
        w_dn_sb = b3(32, D)
        if MLP:
            k_wup = load_w(w_up_sb, w_up, 8, DFF, "w_up")
            k_wdn = load_w(w_dn_sb, w_down, 32, D, "w_dn")
        T2 = 256
        xt2 = [[f32(D), f32(D)], [f32(D), f32(D)]]
        xn2 = f32(D)
        ss2 = [f32(4), f32(4)]
        rstd2 = [f32(4), f32(4)]
        mT = [b3(8, T2), b3(8, T2)]
        actb = b3(32, T2)
        rl = [f32(T2), f32(T2)]
        yout = f32(D)
        gfin_bc = f32(D)
        S.dma(lambda e: e.dma_start(out=gfin_bc, in_=gfin_d.partition_broadcast(128)), "gfin", w=["gfin"])
        pDN = [PS[:, 3072:4096], PS[:, 2048:3072]]
        pDNk = [["pO"], ["pC", "pD"]]

        def mlp_front(ti, r0, T):
            q = ti % 2
            nsub = (T + 127) // 128
            for j in range(nsub):
                Tj = min(128, T - j * 128)
                xk = "xt2_%d_%d" % (q, j)
                S.dma(lambda e, j=j, Tj=Tj: e.dma_start(out=xt2[q][j][0:Tj, :], in_=scr[r0 + j * 128:r0 + j * 128 + Tj, :]),
                      xk, r=["scr%d" % ((r0 + j * 128) // 128)], w=[xk])
                rms_rstd(xt2[q][j], Tj, xk, xn2, ss2[0], rstd2[0], "2")
                S.act(lambda e, j=j, Tj=Tj: e.activation(out=xn2[0:Tj, :], in_=xt2[q][j][0:Tj, :], func=AF.Copy, scale=rstd2[0][0:Tj, 0:1]),
                      r=[xk, "rstd2"], w=["xn2"])
                for k in range(8):
                    S.pe(lambda e, k=k, Tj=Tj: e.transpose(out=pT[:, k * 128:k * 128 + Tj], in_=xn2[0:Tj, k * 128:(k + 1) * 128],
                                                           identity=ident[0:Tj, 0:Tj]), r=["xn2", "cst"], w=["pT"])
                S.dve(lambda e, j=j, Tj=Tj: e.tensor_tensor(
                    out=mT[q][:, :, j * 128:j * 128 + Tj], in0=pT.rearrange("p (k t) -> p k t", k=8)[:, :, 0:Tj],
                    in1=P("GP", 0, 8).unsqueeze(2).to_broadcast([128, 8, Tj]), op=ALU.mult),
                    r=["pT", "pfm"], w=["mT%d" % q])
            yield
            for f in range(32):
                pa = pA[f % 2]
                for k in range(8):
                    S.pe(lambda e, k=k, f=f, pa=pa: e.matmul(pa[:, 0:T], lhsT=w_up_sb[:, k, f * 128:(f + 1) * 128], rhs=mT[q][:, k, 0:T],
                                                             start=(k == 0), stop=(k == 7)),
                         r=["mT%d" % q, "w_up"], w=["pA%d" % (f % 2)])
                S.act(lambda e, f=f, pa=pa: e.activation(out=rl[f % 2][:, 0:T], in_=pa[:, 0:T], func=AF.Relu),
                      r=["pA%d" % (f % 2)], w=["rl%d" % (f % 2)])
                S.pool(lambda e, f=f: e.tensor_tensor(out=actb[:, f, 0:T], in0=rl[f % 2][:, 0:T], in1=rl[f % 2][:, 0:T], op=ALU.mult),
                       r=["rl%d" % (f % 2)], w=["act%d" % f])
                yield

        def mlp_back(ti, r0, T):
            q = ti % 2
            nsub = (T + 127) // 128
            for f in range(32):
                for j in range(nsub):
                    Tj = min(128, T - j * 128)
                    for nb in range(2):
                        S.pe(lambda e, f=f, nb=nb, j=j, Tj=Tj: e.matmul(pDN[j][0:Tj, nb * 512:(nb + 1) * 512],
                                                                        lhsT=actb[:, f, j * 128:j * 128 + Tj],
                                                                        rhs=w_dn_sb[:, f, nb * 512:(nb + 1) * 512],
                                                                        start=(f == 0), stop=(f == 31)),
                             r=["act%d" % f, "w_dn"], w=pDNk[j])
                yield
            for j in range(nsub):
                Tj = min(128, T - j * 128)
                xk = "xt2_%d_%d" % (q, j)
                S.dve(lambda e, j=j, Tj=Tj: e.tensor_tensor(out=xt2[q][j][0:Tj, :], in0=pDN[j][0:Tj, :], in1=xt2[q][j][0:Tj, :], op=ALU.add),
                      r=pDNk[j] + [xk], w=[xk])
                rms_rstd(xt2[q][j], Tj, xk, yout, ss2[1], rstd2[1], "2b", jkey="yout")
                S.dve(lambda e, j=j, Tj=Tj: e.scalar_tensor_tensor(out=yout[0:Tj, :], in0=xt2[q][j][0:Tj, :], scalar=rstd2[1][0:Tj, 0:1],
                                                                   in1=gfin_bc[0:Tj, :], op0=ALU.mult, op1=ALU.mult),
                      r=[xk, "rstd2b", "gfin"], w=["yout"])
                rr = r0 + j * 128
                if rr < SEQ:
                    S.dma(lambda e, rr=rr, Tj=Tj: e.dma_start(out=y_p[rr:rr + Tj, :], in_=yout[0:Tj, :]), "yout", r=["yout"])
                else:
                    S.dma(lambda e, Tj=Tj: e.dma_start(out=y_s, in_=yout[0:Tj, :]), "yout", r=["yout"])
                yield

        if MLP:
            jobs = [(t * T2, T2) for t in range(NT * 128 // T2)]
            if SAMP:
                jobs.append((SEQ, TS))
            for _ in mlp_front(0, *jobs[0]):
                pass
            for ti in range(len(jobs)):
                gb = mlp_back(ti, *jobs[ti])
                gf = mlp_front(ti + 1, *jobs[ti + 1]) if ti + 1 < len(jobs) else iter(())
                ab = af = True
                while ab or af:
                    if ab:
                        ab = next(gb, "END") != "END"
                    if af:
                        af = next(gf, "END") != "END"

        S.emit()
    return nc


_CACHE = {}


def _consts():
    c = np.zeros((128, NCST), np.float32)
    i = np.arange(128)
    c[:, CI:CI + 128] = np.eye(128, dtype=np.float32)
    c[:, CU:CU + 128] = (i[:, None] <= i[None, :]).astype(np.float32)
    c[:, CN:CN + 128] = np.where(i[:, None] <= i[None, :], 0.0, NEG).astype(np.float32)
    c[:, CO:CO + 128] = 1.0
    j = np.arange(TS)
    same = (j[:, None] // 4) == (j[None, :] // 4)
    caus = j[:, None] <= j[None, :]
    c[0:TS, CUB:CUB + TS] = (same & caus).astype(np.float32)
    c[0:TS, CNB:CNB + TS] = np.where(same & caus, 0.0, NEG).astype(np.float32)
    c[0:TS, CBM:CBM + TS] = same.astype(np.float32)
    c[0:TS, CBI:CBI + NS] = ((j[:, None] // 4) == np.arange(NS)[None, :]).astype(np.float32)
    return c


def _fm(v, nch):
    return np.ascontiguousarray(np.asarray(v, np.float32).reshape(nch, 128).T)


def kernel(x_prompt, x_sample, state_lru_conv, state_lru_h, state_ssd_conv, state_ssd_h,
           g_mix, w_in, lru_conv_w, lru_conv_b, w_a, b_a, w_x, b_x, lam, g_lru_out,
           ssd_conv_w, ssd_conv_b, dt_bias, a_log, d_skip, g_ssd_out, w_out,
           g_mlp, w_up, w_down, g_final):
    f = lambda a: np.ascontiguousarray(np.asarray(a, np.float32))
    if "nc" not in _CACHE:
        _CACHE["nc"] = build_program()
    nc = _CACHE["nc"]
    pfm = np.zeros((128, NPAR), np.float32)
    lw = np.asarray(lru_conv_w[0], np.float32)
    pfm[:, PC["LW"]:PC["LW"] + 32] = lw.reshape(4, 8, 128).transpose(2, 1, 0).reshape(128, 32)
    pfm[:, PC["LB"]:PC["LB"] + 8] = _fm(lru_conv_b[0], 8)
    pfm[:, PC["BA"]:PC["BA"] + 8] = _fm(np.asarray(b_a[0]).reshape(-1), 8)
    pfm[:, PC["BX"]:PC["BX"] + 8] = _fm(np.asarray(b_x[0]).reshape(-1), 8)
    pfm[:, PC["LAM"]:PC["LAM"] + 8] = _fm(lam[0], 8)
    pfm[:, PC["GL"]:PC["GL"] + 8] = _fm(g_lru_out[0], 8)
    sw = np.asarray(ssd_conv_w[0], np.float32)
    pfm[:, PC["SW"]:PC["SW"] + 48] = sw.reshape(4, 12, 128).transpose(2, 1, 0).reshape(128, 48)
    pfm[:, PC["SB"]:PC["SB"] + 12] = _fm(ssd_conv_b[0], 12)
    pfm[:, PC["DS"]:PC["DS"] + 8] = _fm(np.repeat(np.asarray(d_skip[0], np.float32), 64), 8)
    pfm[:, PC["GS"]:PC["GS"] + 8] = _fm(g_ssd_out[0], 8)
    pfm[:, PC["GM"]:PC["GM"] + 8] = _fm(g_mix[0], 8)
    pfm[:, PC["GP"]:PC["GP"] + 8] = _fm(g_mlp[0], 8)
    cst = _consts()
    shared = {
        "w_in": f(w_in[0]), "w_out": f(w_out[0]), "w_up": f(w_up[0]), "w_down": f(w_down[0]),
        "w_a": f(w_a[0]), "w_x": f(w_x[0]), "pfm": pfm, "cst": cst,
        "dt_bias": f(dt_bias[0]), "a_log": f(a_log[0]), "g_final": f(g_final),
    }
    in_maps = []
    for b in range(NCORES):
        sl = slice(NS * b, NS * (b + 1))
        m = dict(shared)
        m["xp"] = f(x_prompt[b])
        m["xs"] = f(np.asarray(x_sample[sl]).reshape(TS, D))
        m["st_lc"] = f(np.asarray(state_lru_conv[0, sl]).reshape(NS * 3, D))
        m["st_lh"] = f(state_lru_h[0, sl])
        m["st_sc"] = f(np.asarray(state_ssd_conv[0, sl]).reshape(NS * 3, XBC))
        m["st_sh"] = f(np.asarray(state_ssd_h[0, sl]).reshape(NS, 1024, 128))
        in_maps.append(m)
    res = run_bass_kernel_spmd(nc, in_maps, core_ids=list(range(NCORES)))
    R = res.results
    cat = lambda k: np.stack([np.asarray(R[b][k], np.float32) for b in range(NCORES)])
    y_prompt = cat("y_p")
    y_sample = cat("y_s").reshape(NCORES * NS, 4, D)
    p_lc = cat("o_plc")[None]
    p_lh = cat("o_plh").reshape(NCORES, D)[None]
    p_sc = cat("o_psc")[None]
    p_sh = cat("o_psh").reshape(NCORES, 16, 64, 128)[None]
    s_lc = cat("o_slc").reshape(NCORES * NS, 3, D)[None]
    s_lh = cat("o_slh").reshape(NCORES * NS, D)[None]
    s_sc = cat("o_ssc").reshape(NCORES * NS, 3, XBC)[None]
    s_sh = cat("o_ssh").reshape(NCORES * NS, 16, 64, 128)[None]
    return (y_prompt, y_sample, p_lc, p_lh, p_sc, p_sh, s_lc, s_lh, s_sc, s_sh)
```

```python
import math
from contextlib import ExitStack

import numpy as np
import concourse.bass as bass
import concourse.mybir as mybir
from concourse.bass_utils import run_bass_kernel_spmd

F32 = mybir.dt.float32
BF16 = mybir.dt.bfloat16
AF = mybir.ActivationFunctionType
ALU = mybir.AluOpType

NCORES = 8
D = 1024
SEQ = 2048
NS = 16
TS = 64
XBC = 1536
INP = 4624
DFF = 4096
EPS = 1e-6
NEG = -30000.0

import re as _re

ENGS = ("pe", "act", "dve", "pool", "sp")
SAME_ENGINE_SYNC = {"pe": False, "act": True, "dve": True, "pool": True, "sp": False}


class Op:
    __slots__ = ("eng", "fn", "deps", "marked", "count", "dma_key", "dma_val")

    def __init__(self, eng, fn, dma_key=None):
        self.eng = eng
        self.fn = fn
        self.deps = ()
        self.marked = False
        self.count = 0
        self.dma_key = dma_key
        self.dma_val = 0


class Sched:
    def __init__(self, nc):
        self.nc = nc
        self.ops = {e: [] for e in ENGS}
        self.last_w = {}
        self.readers = {}
        self.dma_cnt = {}
        self.pending = {}
        self.ctx = None
        self.since_bar = []

    ALIAS = {"pC": "b4", "pCx": "b4", "pD": "b5", "pD2": "b5", "pD3": "b5", "pDn": "b5",
             "pDcb0": "b5", "pDcb1": "b5", "pT": "b01", "pO": "b67", "pA0": "b2", "pA1": "b3"}

    PSUM_KEYS = {"b01", "b2", "b3", "b4", "b5", "b67"}

    PAR_RE = _re.compile(r"^(xt|dtt|xnew)$|^(xsf|zs|Bb|Cb|ynl|yg)\d+$")

    def _k(self, k):
        k = self.ALIAS.get(k, k)
        if self.ctx is not None and self.PAR_RE.match(k):
            return k + "#" + str(self.ctx)
        return k

    def add(self, eng, fn, reads=(), writes=(), dma_key=None):
        reads = [self._k(k) for k in reads]
        writes = [self._k(k) for k in writes]
        if dma_key is not None and self.ctx is not None and self.PAR_RE.match(dma_key):
            dma_key = dma_key + "#" + str(self.ctx)
        op = Op(eng, fn, dma_key)
        deps = []
        seen = set()

        def dep(o):
            if o is not None and o is not op and id(o) not in seen:
                seen.add(id(o))
                deps.append(o)

        if self.pending.get(eng):
            for o in self.pending[eng]:
                dep(o)
            self.pending[eng] = []
        for b in reads:
            dep(self.last_w.get(b))
            if b in self.PSUM_KEYS:
                for r in self.readers.get(b, ()):
                    if r.eng != eng:
                        dep(r)
        for b in writes:
            dep(self.last_w.get(b))
            for r in self.readers.get(b, ()):
                dep(r)
        for b in reads:
            self.readers.setdefault(b, []).append(op)
        for b in writes:
            self.last_w[b] = op
            self.readers[b] = []
        op.deps = deps
        if dma_key is not None:
            self.dma_cnt[dma_key] = self.dma_cnt.get(dma_key, 0) + 16
            op.dma_val = self.dma_cnt[dma_key]
        self.ops[eng].append(op)
        self.since_bar.append(op)
        return op

    def barrier(self):
        ops = []
        for e in ENGS:
            comp = [o for o in self.ops[e] if o.dma_key is None]
            if comp:
                ops.append(comp[-1])
        last_dma = {}
        for o in self.since_bar:
            if o.dma_key is not None:
                last_dma[o.dma_key] = o
        ops.extend(last_dma.values())
        for e in ENGS:
            self.pending.setdefault(e, []).extend(ops)
        self.since_bar = []

    def pe(self, fn, r=(), w=()):
        return self.add("pe", fn, r, w)

    def act(self, fn, r=(), w=()):
        return self.add("act", fn, r, w)

    def dve(self, fn, r=(), w=()):
        return self.add("dve", fn, r, w)

    def pool(self, fn, r=(), w=()):
        return self.add("pool", fn, r, w)

    def dma(self, fn, key, r=(), w=(), q="sp"):
        return self.add(q, fn, r, w, dma_key=key)

    def emit(self):
        nc = self.nc
        for e in ENGS:
            for op in self.ops[e]:
                for d in op.deps:
                    if d.dma_key is None:
                        if d.eng == op.eng and not SAME_ENGINE_SYNC[d.eng]:
                            continue
                        d.marked = True
        for e in ENGS:
            c = 0
            for op in self.ops[e]:
                if op.dma_key is None and op.marked:
                    c += 1
                    op.count = c
        with ExitStack() as st:
            esem = {e: st.enter_context(nc.semaphore("es_" + e)) for e in ENGS}
            dsem = {}
            for k in self.dma_cnt:
                dsem[k] = st.enter_context(nc.semaphore("ds_%d" % len(dsem)))
            block = st.enter_context(nc.Block())

            def run(ename, eng):
                seen = {}
                for op in self.ops[ename]:
                    need = {}
                    for d in op.deps:
                        if d.dma_key is not None:
                            key = ("d", d.dma_key)
                            val = d.dma_val
                            sem = dsem[d.dma_key]
                        else:
                            if d.eng == ename and not SAME_ENGINE_SYNC[ename]:
                                continue
                            key = ("e", d.eng)
                            val = d.count
                            sem = esem[d.eng]
                        if key not in need or need[key][1] < val:
                            need[key] = (sem, val)
                    for key, (sem, val) in need.items():
                        if seen.get(key, 0) >= val:
                            continue
                        seen[key] = val
                        eng.wait_ge(sem, val)
                    ins = op.fn(eng)
                    if op.dma_key is not None:
                        ins.then_inc(dsem[op.dma_key], 16)
                    elif op.marked:
                        ins.then_inc(esem[ename], 1)
                if ename == "sp":
                    for k, v in self.dma_cnt.items():
                        eng.wait_ge(dsem[k], v)

            @block.sync
            def _(e):
                run("sp", e)

            @block.tensor
            def _(e):
                run("pe", e)

            @block.scalar
            def _(e):
                run("act", e)

            @block.vector
            def _(e):
                run("dve", e)

            @block.gpsimd
            def _(e):
                run("pool", e)


PC = {}
_o = 0
for _n, _w in (("LW", 32), ("LB", 8), ("BA", 8), ("BX", 8), ("LAM", 8), ("GL", 8), ("SW", 48),
               ("SB", 12), ("DS", 8), ("GS", 8), ("GM", 8), ("GP", 8)):
    PC[_n] = _o
    _o += _w
NPAR = _o
CI, CU, CN, CO, CUB, CNB, CBM, CBI = 0, 128, 256, 384, 512, 576, 640, 704
NCST = 720


def build_program(NT=SEQ // 128, SAMP=True, MLP=True, DBG=False, STAGE=9):
    nc = bass.Bass("TRN2", target_bir_lowering=False)
    S = Sched(nc)

    def din(name, shape):
        return nc.dram_tensor(name, list(shape), F32, kind="ExternalInput").ap()

    def dout(name, shape):
        return nc.dram_tensor(name, list(shape), F32, kind="ExternalOutput").ap()

    xp = din("xp", (SEQ, D))
    xs = din("xs", (TS, D))
    st_lc = din("st_lc", (NS * 3, D))
    st_lh = din("st_lh", (NS, D))
    st_sc = din("st_sc", (NS * 3, XBC))
    st_sh = din("st_sh", (NS, 1024, 128))
    w_in = din("w_in", (D, INP))
    w_out = din("w_out", (2 * D, D))
    w_up = din("w_up", (D, DFF))
    w_down = din("w_down", (DFF, D))
    w_a = din("w_a", (16, 64, 64))
    w_x = din("w_x", (16, 64, 64))
    pfm_d = din("pfm", (128, NPAR))
    cst_d = din("cst", (128, NCST))
    dtb_d = din("dt_bias", (16,))
    alog_d = din("a_log", (16,))
    gfin_d = din("g_final", (D,))

    y_p = dout("y_p", (SEQ, D))
    y_s = dout("y_s", (TS, D))
    o_plc = dout("o_plc", (3, D))
    o_plh = dout("o_plh", (8, 128))
    o_psc = dout("o_psc", (3, XBC))
    o_psh = dout("o_psh", (1024, 128))
    o_slc = dout("o_slc", (NS, 3, D))
    o_slh = dout("o_slh", (NS, D))
    o_ssc = dout("o_ssc", (NS, 3, XBC))
    o_ssh = dout("o_ssh", (NS, 1024, 128))
    scr = nc.dram_tensor("scr", [SEQ + TS, D], F32, kind=("ExternalOutput" if DBG else "Internal")).ap()

    st = ExitStack()
    with st:
        RW = 53200
        R = st.enter_context(nc.sbuf_tensor("R", [128, RW], F32))
        PS = st.enter_context(nc.psum_tensor("PS", [128, 4096], F32))
        ptr = [0]

        def alloc(nwords):
            a = ptr[0]
            ptr[0] += (nwords + 7) // 8 * 8
            pass
            return a

        def f32(n):
            a = alloc(n)
            return R[:, a:a + n]

        def bf(n):
            w = (n + 1) // 2
            a = alloc(w)
            return R[:, a:a + w].bitcast(BF16)[:, 0:n]

        def f3(c, t):
            return f32(c * t).rearrange("p (c t) -> p c t", c=c)

        def b3(c, t):
            return bf(c * t).rearrange("p (c t) -> p c t", c=c)

        def bank(b, n=512):
            return PS[:, 512 * b:512 * b + n]

        pT = PS[:, 0:1024]
        pTb = pT.bitcast(BF16)
        pA = [bank(2), bank(3)]
        pC = bank(4)
        pCb = pC.bitcast(BF16)
        pD = bank(5)
        pO = PS[:, 3072:4096]

        cst = f32(NCST)
        pfm = f32(NPAR)
        dtb_bc = f32(16)
        a_bc = f32(16)
        identb = bf(128)
        onesb = bf(128)
        Utrib = bf(128)
        mskb = bf(3 * TS)
        dah = bf(16)
        dal = bf(16)
        cfac = f32(8)
        c2fac = f32(8)
        tiny = f32(8)
        mhalf = f32(4)
        eps_t = f32(4)
        nbias = f32(16)
        wa_blk = b3(8, 128)
        wx_blk = b3(8, 128)
        hstate = f32(8)
        hT = f32(1024)
        hTb = bf(1024)

        ident = cst[:, CI:CI + 128]
        Utri = cst[:, CU:CU + 128]
        negm = cst[:, CN:CN + 128]
        onesf = cst[:, CO:CO + 128]
        Ublk = cst[0:TS, CUB:CUB + TS]
        negblk = cst[0:TS, CNB:CNB + TS]
        blkm = cst[0:TS, CBM:CBM + TS]
        blki = cst[0:TS, CBI:CBI + NS]

        S.dma(lambda e: e.dma_start(out=cst, in_=cst_d), "cst", w=["cst"])
        S.dma(lambda e: e.dma_start(out=pfm, in_=pfm_d), "pfm", w=["pfm"])
        S.dma(lambda e: e.dma_start(out=dtb_bc, in_=dtb_d.partition_broadcast(128)), "dtb", w=["dtb"])
        S.dma(lambda e: e.dma_start(out=a_bc, in_=alog_d.partition_broadcast(128)), "alog", w=["a_bc"])
        S.dve(lambda e: e.tensor_copy(out=identb, in_=ident), r=["cst"], w=["identb"])
        S.dve(lambda e: e.tensor_copy(out=onesb, in_=onesf), r=["cst"], w=["onesb"])
        S.dve(lambda e: e.tensor_copy(out=Utrib, in_=Utri), r=["cst"], w=["mskb"])
        S.dve(lambda e: e.tensor_copy(out=mskb[0:TS, 0:TS], in_=Ublk), r=["cst"], w=["mskb"])
        S.dve(lambda e: e.tensor_copy(out=mskb[0:TS, TS:2 * TS], in_=blkm), r=["cst"], w=["mskb"])
        S.pool(lambda e: e.memset(mhalf, -0.5), w=["mhalf"])
        S.pool(lambda e: e.memset(eps_t, EPS), w=["eps_t"])
        S.dve(lambda e: e.tensor_scalar(out=nbias[:, 0:8], in0=pfm[:, PC["BA"]:PC["BA"] + 8], scalar1=-1.0, scalar2=None, op0=ALU.mult), r=["pfm"], w=["nbias"])
        S.dve(lambda e: e.tensor_scalar(out=nbias[:, 8:16], in0=pfm[:, PC["BX"]:PC["BX"] + 8], scalar1=-1.0, scalar2=None, op0=ALU.mult), r=["pfm", "nbias"], w=["nbias"])
        S.pool(lambda e: e.memset(hstate, 0.0), w=["hstate"])
        S.pool(lambda e: e.memset(hT, 0.0), w=["hT"])
        S.pool(lambda e: e.memset(hTb, 0.0), w=["hTb"])
        S.pool(lambda e: e.memset(wa_blk, 0.0), w=["wa"])
        S.pool(lambda e: e.memset(wx_blk, 0.0), w=["wx"])
        S.act(lambda e: e.activation(out=a_bc, in_=a_bc, func=AF.Exp), r=["a_bc"], w=["a_bc"])
        S.dve(lambda e: e.tensor_scalar(out=a_bc, in0=a_bc, scalar1=-1.0, scalar2=None, op0=ALU.mult), r=["a_bc"], w=["a_bc"])
        lam = pfm[:, PC["LAM"]:PC["LAM"] + 8]
        S.act(lambda e: e.activation(out=tiny, in_=lam, func=AF.Exp, scale=-1.0), r=["pfm"], w=["tiny"])
        S.act(lambda e: e.activation(out=tiny, in_=tiny, func=AF.Ln, bias=1.0), r=["tiny"], w=["tiny"])
        S.dve(lambda e: e.tensor_scalar(out=cfac, in0=tiny, scalar1=-8.0, scalar2=None, op0=ALU.mult), r=["tiny"], w=["cfac"])
        S.dve(lambda e: e.tensor_scalar(out=c2fac, in0=tiny, scalar1=-16.0, scalar2=None, op0=ALU.mult), r=["tiny"], w=["cfac2"])
        for (wd, blk, nm) in ((w_a, wa_blk, "wa"), (w_x, wx_blk, "wx")):
            v = wd.rearrange("(c h) i j -> h i c j", h=2)
            for h2 in range(2):
                S.dma(lambda e, v=v, blk=blk, h2=h2: e.dma_start(
                    out=blk[64 * h2:64 * h2 + 64, :, 64 * h2:64 * h2 + 64], in_=v[h2]),
                    nm + str(h2), w=[nm], q="pool")

        base0 = ptr[0]

        def load_w(dst3, src2, nk, ncol, name, step=2048):
            sv = src2.rearrange("(k p) n -> p k n", p=128)
            pieces = [(k, c0, min(ncol, c0 + step)) for k in range(nk) for c0 in range(0, ncol, step)]
            for i, (k, c0, c1) in enumerate(pieces):
                S.dma(lambda e, k=k, c0=c0, c1=c1: e.dma_start(out=dst3[:, k, c0:c1], in_=sv[:, k, c0:c1]),
                      name, w=([name] if i == len(pieces) - 1 else []), q="pool")
            return name

        w_in_sb = b3(8, INP)
        w_out_sb = b3(16, D)
        if STAGE >= 1:
            k_win = load_w(w_in_sb, w_in, 8, INP, "w_in")
            k_wout = load_w(w_out_sb, w_out, 16, D, "w_out")

        xt = f32(D)
        xn = f32(D)
        junk = xn
        ss = f32(4)
        rstd = f32(4)
        hTt = b3(8, 128)
        lxb = f3(8, 131)
        xcb = f3(12, 131)
        sreg = f32(20 * NS * 7)
        lxs = sreg[:, 0:8 * NS * 7].rearrange("p (c s l) -> p c s l", c=8, s=NS)
        xcs = sreg[:, 8 * NS * 7:20 * NS * 7].rearrange("p (c s l) -> p c s l", c=12, s=NS)
        gl = f3(2, 128)
        zs = f3(8, 128)
        u = f3(8, 128)
        ub = b3(8, 128)
        lrut = f32(1280)
        gi = lrut[:, 0:512].rearrange("p (c t) -> p c t", c=4)
        av = lrut[:, 512:768].rearrange("p (c t) -> p c t", c=2)
        a2 = lrut[:, 768:1024].rearrange("p (c t) -> p c t", c=2)
        tmpb = lrut[:, 1024:1280].rearrange("p (c t) -> p c t", c=2)
        hs = f3(8, 128)
        ysq = b3(2, 128)
        rbc = f32(128)
        ynl = b3(8, 128)
        yns = b3(8, 128)
        xsf = f3(8, 128)
        Bb = b3(2, 128)
        Cb = b3(2, 128)
        dtr = f32(16)
        dtt = f32(16)
        da = f32(16)
        ncum = f32(16)
        dte = f32(16)
        cdec = f32(16)
        xdt = bf(1024)
        xdd = bf(1024)
        BT = bf(256)
        cbT = f3(2, 128)
        Dmf = f32(512)
        Emf = f32(512)
        Mmf = bf(512)
        Chf = bf(1024)
        Chp = Chf[:, 0:512].rearrange("p (a t) -> p a t", a=4)
        Chs = Chf.rearrange("p (a t) -> p a t", a=16)
        cvt = f3(2, 128)
        stg = f32(2560)
        stT = stg[:, 0:1024]
        lc_in = stg[:, 0:1024]
        sc_in = stg[:, 1024:2560]
        lh_in = stg[:, 0:1024]
        h0in = lxb.rearrange("p c t -> p (c t)")[:, 0:1024].rearrange("p (c t) -> p c t", c=8)
        h0Tb = hTb
        Bm = bf(256)
        pyo_f = f32(8 * TS)
        pyo_sb = pyo_f.rearrange("p (c t) -> p c t", c=8)
        damb = bf(2 * NS * 16).rearrange("p (i s h) -> p i s h", i=2, s=NS)
        dtot = f3(NS, 16)
        hnew = hT
        hout = xcb.rearrange("p c t -> p (c t)")[:, 0:1024].rearrange("p (c t) -> p c t", c=8)
        h0s = f3(8, NS)
        hfin = f3(8, NS)

        bf3 = lambda ap, c: ap.bitcast(BF16).rearrange("p (c t) -> p c t", c=c)
        xt_b = [xt, stg[:, 0:1024]]
        ynl_b = [ynl, bf3(stg[:, 1024:1536], 8)]
        Bb_b = [Bb, bf3(stg[:, 1536:1664], 2)]
        Cb_b = [Cb, bf3(stg[:, 1664:1792], 2)]
        dtt_b = [dtt, stg[:, 1792:1808]]
        xsf_b = [xsf, sreg[:, 0:1024].rearrange("p (c t) -> p c t", c=8)]
        zs_b = [zs, sreg[:, 1024:2048].rearrange("p (c t) -> p c t", c=8)]
        Wl_S = pyo_f[:, 0:256].bitcast(BF16)
        ysq_S = bf3(pyo_f[:, 256:384], 2)
        rbc_S = pyo_f[:, 384:512]
        STGW = [k + "#1" for k in ["xt", "dtt"] + ["ynl%d" % c for c in range(8)] + ["Bb0", "Bb1", "Cb0", "Cb1"]]
        STG_ALIAS = ["xt", "dtt"] + ["ynl%d" % c for c in range(8)] + ["Bb0", "Bb1", "Cb0", "Cb1"]

        def P(name, c=None, w=1):
            o = PC[name] + (0 if c is None else c * w)
            return pfm[:, o:o + w]

        def rms_rstd(xtile, T, keyx, junk, ss, rstd, sfx="", jkey=None):
            S.act(lambda e: e.activation(out=junk[0:T, :], in_=xtile[0:T, :], func=AF.Square, accum_out=ss[0:T, 0:1]),
                  r=[keyx], w=[jkey or ("xn" + sfx), "ss" + sfx])
            S.act(lambda e: e.activation(out=ss[0:T, 0:1], in_=ss[0:T, 0:1], func=AF.Ln, scale=1.0 / D, bias=eps_t[0:T, 0:1]),
                  r=["ss" + sfx, "eps_t"], w=["ss" + sfx])
            S.act(lambda e: e.activation(out=rstd[0:T, 0:1], in_=ss[0:T, 0:1], func=AF.Exp, scale=-0.5),
                  r=["ss" + sfx], w=["rstd" + sfx])

        pAA = PS[:, 1024:2048]

        def to_fm(T, gname, dst, dkey):
            for k in range(8):
                S.pe(lambda e, k=k: e.transpose(out=pAA[:, k * 128:k * 128 + T], in_=xn[0:T, k * 128:(k + 1) * 128],
                                                identity=ident[0:T, 0:T]), r=["xn", "cst"], w=["pA0", "pA1"])
            S.dve(lambda e: e.tensor_tensor(
                out=dst[:, :, 0:T], in0=pAA.rearrange("p (k t) -> p k t", k=8)[:, :, 0:T],
                in1=P(gname, 0, 8).unsqueeze(2).to_broadcast([128, 8, T]), op=ALU.mult),
                r=["pA0", "pA1", "pfm"], w=[dkey])

        def mixer_tile(mt, samp):
            T = TS if samp else 128
            row0 = SEQ if samp else mt * 128
            xsrc = xs if samp else xp[mt * 128:(mt + 1) * 128, :]
            last = (not samp) and mt == NT - 1
            par = 0 if samp else (NT - 1 - mt) % 2
            xt, ynl, Bb, Cb, dtt, xsf, zs = (xt_b[par], ynl_b[par], Bb_b[par], Cb_b[par], dtt_b[par], xsf_b[par], zs_b[par])
            Wl = cvt.rearrange("p a t -> p (a t)").bitcast(BF16) if samp else Wl_S
            wlk = ["cv_t0", "cv_t1"] if samp else ["WlS"]
            ysqS = ysq if samp else ysq_S
            rbcS = rbc if samp else rbc_S
            sk = "" if samp else "S"

            def inter(*gens):
                gens = list(gens)
                while gens:
                    for g_ in list(gens):
                        try:
                            next(g_)
                        except StopIteration:
                            gens.remove(g_)
                        yield

            pcnt = [0]

            def proj(ci):
                i = pcnt[0] % 2
                pcnt[0] += 1
                pa = pA[i]
                for k in range(8):
                    S.pe(lambda e, k=k: e.matmul(pa[:, 0:T], lhsT=w_in_sb[:, k, ci * 128:(ci + 1) * 128],
                                                 rhs=hTt[:, k, 0:T], start=(k == 0), stop=(k == 7)),
                         r=["hTt", "w_in"], w=["pA%d" % i])
                return pa, "pA%d" % i

            def new_cols(buf, sbuf_, c):
                if samp:
                    return sbuf_[:, c, :, 3:7]
                return buf[:, c, 3:131]

            def pa_view(pa):
                if samp:
                    return pa[:, 0:T].rearrange("p (s l) -> p s l", s=NS)
                return pa[:, 0:T]

            def tap(buf, sbuf_, c, k):
                if samp:
                    return sbuf_[:, c, :, k:k + 4]
                return buf[:, c, k:k + 128]

            def fm(t3, c):
                if samp:
                    return t3[:, c, 0:T].rearrange("p (s l) -> p s l", s=NS)
                return t3[:, c, 0:T]

            def conv(buf, sbuf_, c, wname, bname, out_ap, key_in, key_out):
                S.dve(lambda e: e.tensor_scalar(out=out_ap, in0=tap(buf, sbuf_, c, 3), scalar1=P(wname, c, 4)[:, 3:4],
                                                scalar2=P(bname, c), op0=ALU.mult, op1=ALU.add),
                      r=[key_in, "pfm"], w=[key_out])
                for k in (2, 1, 0):
                    S.dve(lambda e, k=k: e.scalar_tensor_tensor(out=out_ap, in0=tap(buf, sbuf_, c, k),
                                                                scalar=P(wname, c, 4)[:, k:k + 1], in1=out_ap,
                                                                op0=ALU.mult, op1=ALU.add),
                          r=[key_in, key_out, "pfm"], w=[key_out])
                if not samp:
                    S.dve(lambda e: e.tensor_copy(out=buf[:, c, 0:3], in_=buf[:, c, 128:131]), r=[key_in], w=[key_in])

            def g_lrux():
                for c in range(8):
                    pa, pk = proj(c)
                    S.act(lambda e, c=c, pa=pa: e.activation(out=new_cols(lxb, lxs, c), in_=pa_view(pa), func=AF.Copy),
                          r=[pk], w=["lx%d" % c])
                    conv(lxb, lxs, c, "LW", "LB", fm(u, c), "lx%d" % c, "u%d" % c)
                    yield

            def g_z():
                for c in range(8):
                    pa, pk = proj(16 + c)
                    S.act(lambda e, c=c, pa=pa: e.activation(out=zs[:, c, 0:T], in_=pa[:, 0:T], func=AF.Silu),
                          r=[pk], w=["zs%d" % c])
                    yield

            def g_xbc():
                for c in range(12):
                    pa, pk = proj(24 + c)
                    S.act(lambda e, c=c, pa=pa: e.activation(out=new_cols(xcb, xcs, c), in_=pa_view(pa), func=AF.Copy),
                          r=[pk], w=["xc%d" % c])
                    if c < 8:
                        conv(xcb, xcs, c, "SW", "SB", fm(cvt, c % 2), "xc%d" % c, "cv_t%d" % (c % 2))
                        S.act(lambda e, c=c: e.activation(out=xsf[:, c, 0:T], in_=cvt[:, c % 2, 0:T], func=AF.Silu),
                              r=["cv_t%d" % (c % 2)], w=["xsf%d" % c])
                    else:
                        g = (c - 8) % 2
                        dstb = Bb if c < 10 else Cb
                        nm = ("Bb%d" if c < 10 else "Cb%d") % g
                        conv(xcb, xcs, c, "SW", "SB", fm(cvt, g), "xc%d" % c, "cv_t%d" % g)
                        S.act(lambda e, g=g, dstb=dstb: e.activation(out=dstb[:, g, 0:T], in_=cvt[:, g, 0:T], func=AF.Silu),
                              r=["cv_t%d" % g], w=[nm])
                    yield
                for k in range(8):
                    S.pe(lambda e, k=k: e.matmul(pD[0:T, 0:16], lhsT=hTt[:, k, 0:T], rhs=w_in_sb[:, k, 4608:4624],
                                                 start=(k == 0), stop=(k == 7)),
                         r=["hTt", "w_in"], w=["pD"])
                S.dve(lambda e: e.tensor_tensor(out=dtr[0:T, :], in0=pD[0:T, 0:16], in1=dtb_bc[0:T, :], op=ALU.add),
                      r=["pD", "dtb"], w=["dtr"])
                S.act(lambda e: e.activation(out=dtr[0:T, :], in_=dtr[0:T, :], func=AF.Exp), r=["dtr"], w=["dtr"])
                S.act(lambda e: e.activation(out=dtt[0:T, :], in_=dtr[0:T, :], func=AF.Ln, bias=1.0), r=["dtr"], w=["dtt"])
                yield

            def g_lru(chunks, pg, kr, ki):
                for c in chunks:
                    pp = c % 2
                    S.pool(lambda e, c=c: e.tensor_copy(out=ub[:, c, 0:T], in_=u[:, c, 0:T]),
                           r=["u%d" % c], w=["ub%d" % c])
                    yield
                    S.pe(lambda e, c=c: e.matmul(pg[:, 0:T], lhsT=wa_blk[:, c, :], rhs=ub[:, c, 0:T], start=True, stop=True),
                         r=["ub%d" % c, "wa"], w=[kr])
                    S.pe(lambda e, c=c: e.matmul(pg[:, 128:128 + T], lhsT=wx_blk[:, c, :], rhs=ub[:, c, 0:T], start=True, stop=True),
                         r=["ub%d" % c, "wx"], w=[ki])
                    yield
                    S.act(lambda e, c=c, pp=pp: e.activation(out=gi[:, 2 * pp, 0:T], in_=pg[:, 0:T], func=AF.Exp, scale=-1.0, bias=nbias[:, c:c + 1]),
                          r=[kr, "nbias"], w=["rg%d" % pp])
                    S.act(lambda e, c=c, pp=pp: e.activation(out=gi[:, 2 * pp + 1, 0:T], in_=pg[:, 128:128 + T], func=AF.Exp, scale=-1.0, bias=nbias[:, 8 + c:9 + c]),
                          r=[ki, "nbias"], w=["ig%d" % pp])
                    S.act(lambda e, pp=pp: e.activation(out=gi[:, 2 * pp:2 * pp + 2, 0:T], in_=gi[:, 2 * pp:2 * pp + 2, 0:T], func=AF.Ln, bias=1.0),
                          r=["rg%d" % pp, "ig%d" % pp], w=["rg%d" % pp, "ig%d" % pp])
                    S.act(lambda e, pp=pp: e.activation(out=gi[:, 2 * pp:2 * pp + 2, 0:T], in_=gi[:, 2 * pp:2 * pp + 2, 0:T], func=AF.Exp, scale=-1.0),
                          r=["rg%d" % pp, "ig%d" % pp], w=["rg%d" % pp, "ig%d" % pp])
                    S.act(lambda e, c=c, pp=pp: e.activation(out=av[:, pp, 0:T], in_=gi[:, 2 * pp, 0:T], func=AF.Exp, scale=cfac[:, c:c + 1]),
                          r=["rg%d" % pp, "cfac"], w=["av%d" % pp])
                    S.act(lambda e, c=c, pp=pp: e.activation(out=a2[:, pp, 0:T], in_=gi[:, 2 * pp, 0:T], func=AF.Exp, scale=c2fac[:, c:c + 1]),
                          r=["rg%d" % pp, "cfac2"], w=["a2%d" % pp])
                    S.act(lambda e, pp=pp: e.activation(out=a2[:, pp, 0:T], in_=a2[:, pp, 0:T], func=AF.Ln, scale=-1.0, bias=1.0),
                          r=["a2%d" % pp], w=["a2%d" % pp])
                    S.act(lambda e, pp=pp: e.activation(out=a2[:, pp, 0:T], in_=a2[:, pp, 0:T], func=AF.Exp, scale=0.5),
                          r=["a2%d" % pp], w=["a2%d" % pp])
                    yield
                    S.dve(lambda e, c=c, pp=pp: e.tensor_tensor(out=tmpb[:, pp, 0:T], in0=gi[:, 2 * pp + 1, 0:T], in1=u[:, c, 0:T], op=ALU.mult),
                          r=["ig%d" % pp, "u%d" % c], w=["tb%d" % pp])
                    S.dve(lambda e, pp=pp: e.tensor_tensor(out=tmpb[:, pp, 0:T], in0=tmpb[:, pp, 0:T], in1=a2[:, pp, 0:T], op=ALU.mult),
                          r=["tb%d" % pp, "a2%d" % pp], w=["tb%d" % pp])
                    if samp:
                        a3 = av[:, pp, 0:T].rearrange("p (s l) -> p s l", s=NS)
                        b3v = tmpb[:, pp, 0:T].rearrange("p (s l) -> p s l", s=NS)
                        S.dve(lambda e, c=c, a3=a3: e.tensor_tensor(out=rbc[:, 0:NS], in0=a3[:, :, 0], in1=h0s[:, c, :], op=ALU.mult),
                              r=["av%d" % pp, "h0s"], w=["rbc"])
                        S.dve(lambda e, b3v=b3v: e.tensor_tensor(out=b3v[:, :, 0], in0=b3v[:, :, 0], in1=rbc[:, 0:NS], op=ALU.add),
                              r=["tb%d" % pp, "rbc"], w=["tb%d" % pp])
                        S.dve(lambda e, a3=a3: e.memset(a3[:, :, 0], 0.0), r=["rbc"], w=["av%d" % pp])
                        S.dve(lambda e, c=c, pp=pp: e.tensor_tensor_scan(out=hs[:, c, 0:T], data0=av[:, pp, 0:T], data1=tmpb[:, pp, 0:T],
                                                                         initial=0.0, op0=ALU.mult, op1=ALU.add),
                              r=["av%d" % pp, "tb%d" % pp], w=["hs%d" % c])
                        S.dve(lambda e, c=c: e.tensor_copy(out=hfin[:, c, :], in_=hs[:, c, 0:T].rearrange("p (s l) -> p s l", s=NS)[:, :, 3]),
                              r=["hs%d" % c], w=["hfin"])
                    else:
                        S.dve(lambda e, c=c, pp=pp: e.tensor_tensor_scan(out=hs[:, c, 0:T], data0=av[:, pp, 0:T], data1=tmpb[:, pp, 0:T],
                                                                         initial=hstate[:, c:c + 1], op0=ALU.mult, op1=ALU.add),
                              r=["av%d" % pp, "tb%d" % pp, "hstate"], w=["hs%d" % c])
                        S.dve(lambda e, c=c: e.tensor_copy(out=hstate[:, c:c + 1], in_=hs[:, c, T - 1:T]),
                              r=["hs%d" % c], w=["hstate"])
                    yield

            def g_gate():
                for c in range(8):
                    pp = c % 2
                    pa, pk = proj(8 + c)
                    S.act(lambda e, pp=pp, pa=pa: e.activation(out=gl[:, pp, 0:T], in_=pa[:, 0:T], func=AF.Gelu_apprx_tanh),
                          r=[pk], w=["gl%d" % pp])
                    S.pool(lambda e, c=c, pp=pp: e.tensor_tensor(out=hs[:, c, 0:T], in0=hs[:, c, 0:T], in1=gl[:, pp, 0:T], op=ALU.mult),
                           r=["hs%d" % c, "gl%d" % pp], w=["yl%d" % c, "hs%d" % c])
                    S.act(lambda e, c=c, pp=pp: e.activation(out=ysq[:, pp, 0:T], in_=hs[:, c, 0:T], func=AF.Square),
                          r=["yl%d" % c], w=["ysq%d" % pp])
                    S.pe(lambda e, c=c, pp=pp: e.matmul(pD[:, 128:128 + T], lhsT=onesb, rhs=ysq[:, pp, 0:T], start=(c == 0), stop=(c == 7)),
                         r=["ysq%d" % pp, "onesb"], w=["pDn"])
                    yield

            def norm_apply(T, eps_, src, skey, gname, dst, dkey, c0, c1, rbc, rk, pst, pk):
                S.act(lambda e: e.activation(out=rbc[:, 0:T], in_=pst, func=AF.Ln,
                                             scale=1.0 / ((c1 - c0) * 128), bias=eps_t[:, 0:1]),
                      r=[pk, "eps_t"], w=[rk])
                S.act(lambda e: e.activation(out=rbc[:, 0:T], in_=rbc[:, 0:T], func=AF.Exp, scale=-0.5), r=[rk], w=[rk])
                for c in range(c0, c1):
                    S.dve(lambda e, c=c: e.scalar_tensor_tensor(out=dst[:, c, 0:T], in0=src[:, c, 0:T], scalar=P(gname, c),
                                                                in1=rbc[:, 0:T], op0=ALU.mult, op1=ALU.mult),
                          r=[skey % c, rk, "pfm"], w=[dkey % c])

            Um = mskb[0:TS, 0:TS] if samp else Utrib
            ngm = negblk if samp else negm
            allm = mskb[0:TS, TS:2 * TS] if samp else onesb
            d4 = lambda ap: ap[:, 0:4 * T].rearrange("p (a t) -> p a t", a=4)
            Em, Dm, Mm, pC4 = d4(Emf), d4(Dmf), d4(Mmf), d4(pC)

            def g_ssd():
                for c in range(8):
                    S.pe(lambda e, c=c: e.transpose(out=pT[0:T, c * 128:(c + 1) * 128], in_=xsf[:, c, 0:T], identity=ident),
                         r=["xsf%d" % c, "cst"], w=["pT"])
                for g in range(2):
                    S.pe(lambda e, g=g: e.transpose(out=pCb[0:T, 128 + g * 128:128 + (g + 1) * 128], in_=Bb[:, g, 0:T], identity=identb),
                         r=["Bb%d" % g, "identb"], w=["pC", "pCx"])
                S.dve(lambda e: e.tensor_tensor(out=xdt[0:T, :].rearrange("p (h q) -> p h q", h=16),
                                                in0=pT[0:T, :].rearrange("p (h q) -> p h q", h=16),
                                                in1=dtt[0:T, :].unsqueeze(2).to_broadcast([T, 16, 64]), op=ALU.mult),
                      r=["pT", "dtt"], w=["xdt"])
                S.dve(lambda e: e.tensor_copy(out=BT[0:T, :], in_=pCb[0:T, 128:384]), r=["pC"], w=["BT"])
                S.dve(lambda e: e.tensor_tensor(out=da[0:T, :], in0=dtt[0:T, :], in1=a_bc[0:T, :], op=ALU.mult),
                      r=["dtt", "a_bc"], w=["da"])
                yield
                S.dve(lambda e: e.tensor_copy(out=dah[0:T, :], in_=da[0:T, :]), r=["da"], w=["dah"])
                S.dve(lambda e: e.tensor_tensor(out=dal[0:T, :], in0=da[0:T, :], in1=dah[0:T, :], op=ALU.subtract),
                      r=["da", "dah"], w=["dal"])
                for i, dx in enumerate((dah, dal)):
                    S.pe(lambda e, dx=dx, i=i: e.matmul(pC[0:T, 0:16], lhsT=Um[0:T, 0:T], rhs=dx[0:T, :], start=(i == 0), stop=(i == 1)),
                         r=["dah", "dal", "mskb"], w=["pC", "pCx"])
                for i, dx in enumerate((dah, dal)):
                    S.pe(lambda e, dx=dx, i=i: e.matmul(pC[0:T, 16:32], lhsT=allm[0:T, 0:T], rhs=dx[0:T, :], start=(i == 0), stop=(i == 1)),
                         r=["dah", "dal", "mskb", "onesb"], w=["pC", "pCx"])
                if not samp:
                    for i, dx in enumerate((dah, dal)):
                        S.pe(lambda e, dx=dx, i=i: e.matmul(pC[:, 32:48], lhsT=onesb, rhs=dx, start=(i == 0), stop=(i == 1)),
                             r=["dah", "dal", "onesb"], w=["pC", "pCx"])
                for g in range(2):
                    S.pe(lambda e, g=g: e.matmul(pC[0:T, 256 + g * 128:256 + g * 128 + T], lhsT=Bb[:, g, 0:T], rhs=Cb[:, g, 0:T],
                                                 start=True, stop=True), r=["Bb%d" % g, "Cb%d" % g], w=["pC", "pCx"])
                yield
                S.dve(lambda e: e.tensor_scalar(out=ncum[0:T, :], in0=pC[0:T, 0:16], scalar1=-1.0, scalar2=None, op0=ALU.mult),
                      r=["pC"], w=["ncum"])
                S.dve(lambda e: e.tensor_tensor(out=dte[0:T, :], in0=pC[0:T, 16:32], in1=ncum[0:T, :], op=ALU.add),
                      r=["pC", "ncum"], w=["dte"])
                if not samp:
                    S.dve(lambda e: e.tensor_copy(out=cdec, in_=pC[:, 32:48]), r=["pC"], w=["cdec"])
                S.dve(lambda e: e.tensor_copy(out=cbT[0:T, :, 0:T], in_=pC[0:T, 256:512].rearrange("p (g t) -> p g t", g=2)[:, :, 0:T]),
                      r=["pC"], w=["cbT0", "cbT1"])
                S.act(lambda e: e.activation(out=dte[0:T, :], in_=dte[0:T, :], func=AF.Exp), r=["dte"], w=["dte"])
                if not samp:
                    S.act(lambda e: e.activation(out=cdec, in_=cdec, func=AF.Exp), r=["cdec"], w=["cdec"])
                S.dve(lambda e: e.tensor_tensor(out=xdd[0:T, :].rearrange("p (h q) -> p h q", h=16),
                                                in0=xdt[0:T, :].rearrange("p (h q) -> p h q", h=16),
                                                in1=dte[0:T, :].unsqueeze(2).to_broadcast([T, 16, 64]), op=ALU.mult),
                      r=["xdt", "dte"], w=["xdd"])
                yield
                if not samp:
                    for g in range(2):
                        S.pe(lambda e, g=g: e.matmul(pO[:, g * 512:(g + 1) * 512], lhsT=BT[:, g * 128:(g + 1) * 128],
                                                     rhs=xdd[:, g * 512:(g + 1) * 512], start=True, stop=True),
                             r=["BT", "xdd"], w=["pO"])
                    S.dve(lambda e: e.tensor_tensor(out=hT.rearrange("p (h q) -> p h q", h=16),
                                                    in0=hT.rearrange("p (h q) -> p h q", h=16),
                                                    in1=cdec.unsqueeze(2).to_broadcast([128, 16, 64]), op=ALU.mult),
                          r=["hT", "cdec"], w=["hT"])
                    S.dve(lambda e: e.tensor_tensor(out=hT, in0=hT, in1=pO, op=ALU.add), r=["hT", "pO"], w=["hT"])
                    yield
                for q4 in range(4):
                    g = q4 // 2
                    for i, (dx, Wf, wk) in enumerate(((dah, Mmf, ["Mm"]), (dal, Wl, wlk))):
                        S.pool(lambda e, q4=q4, dx=dx, Wf=Wf: e.tensor_tensor(out=d4(Wf)[0:T], in0=Um[0:T, 0:T].unsqueeze(1).to_broadcast([T, 4, T]),
                                                                             in1=dx[0:T, q4 * 4:q4 * 4 + 4].unsqueeze(2).to_broadcast([T, 4, T]),
                                                                             op=ALU.mult), r=["dah", "dal", "mskb"], w=wk)
                        S.pe(lambda e, Wf=Wf, i=i: e.matmul(pC[:, 0:4 * T], lhsT=onesb[0:T, :], rhs=Wf[0:T, 0:4 * T],
                                                            start=(i == 0), stop=(i == 1)), r=wk + ["onesb"], w=["pC", "pCx"])
                    S.dve(lambda e: e.tensor_copy(out=Em, in_=pC4), r=["pC"], w=["Em"])
                    yield
                    for hh in range(4):
                        h = q4 * 4 + hh
                        S.dve(lambda e, h=h, hh=hh: e.scalar_tensor_tensor(out=Dm[0:T, hh, :], in0=Em[0:T, hh, :],
                                                                           scalar=ncum[0:T, h:h + 1], in1=ngm[0:T, 0:T],
                                                                           op0=ALU.add, op1=ALU.add),
                              r=["Em", "ncum", "cst"], w=["Dm"])
                    S.act(lambda e: e.activation(out=Em, in_=Em, func=AF.Exp), r=["Em"], w=["Em"])
                    S.act(lambda e: e.activation(out=Dm[0:T], in_=Dm[0:T], func=AF.Exp), r=["Dm"], w=["Dm"])
                    S.dve(lambda e, g=g: e.tensor_tensor(out=Mm[0:T], in0=Dm[0:T],
                                                         in1=cbT[0:T, g, 0:T].unsqueeze(1).to_broadcast([T, 4, T]), op=ALU.mult),
                          r=["Dm", "cbT%d" % g], w=["Mm"])
                    S.pool(lambda e, g=g, q4=q4: e.tensor_tensor(out=(Chs[:, q4 * 4:q4 * 4 + 4, :] if samp else Chp), in0=Em,
                                                                in1=Cb[:, g, 0:T].unsqueeze(1).to_broadcast([128, 4, T]), op=ALU.mult),
                          r=["Em", "Cb%d" % g], w=["Ch"])
                    yield
                    for hh in range(4):
                        h = q4 * 4 + hh
                        c = h // 2
                        h2 = h % 2
                        po = pT[64 * h2:64 * h2 + 64, c * 128:c * 128 + T]
                        S.pe(lambda e, h=h, hh=hh, po=po: e.matmul(po, lhsT=xdt[0:T, h * 64:(h + 1) * 64], rhs=Mm[0:T, hh, :],
                                                                   start=True, stop=samp), r=["xdt", "Mm"], w=["pT"])
                        if not samp:
                            S.pe(lambda e, h=h, hh=hh, po=po: e.matmul(po, lhsT=hTb[:, h * 64:(h + 1) * 64], rhs=Chp[:, hh, :],
                                                                       start=False, stop=True), r=["hTb", "Ch"], w=["pT"])
                    yield

            def late_outputs():
                M = T if samp else 3
                t0 = 0 if samp else 125
                if samp or last:
                    for blk, col0 in enumerate((0, 512, 3072, 3584, 4096)):
                        for k in range(8):
                            S.pe(lambda e, k=k, col0=col0: e.matmul(pO[0:M, 0:512], lhsT=hTt[:, k, t0:t0 + M],
                                                                    rhs=w_in_sb[:, k, col0:col0 + 512], start=(k == 0), stop=(k == 7)),
                                 r=["hTt", "w_in"], w=["pO"])
                        S.dve(lambda e, blk=blk: e.tensor_copy(out=stg[0:M, blk * 512:(blk + 1) * 512], in_=pO[0:M, 0:512]),
                              r=["pO"], w=["stg"] + STGW)
                if last:
                    S.dma(lambda e: e.dma_start(out=o_plc, in_=stg[0:3, 0:1024]), "o_plc", r=["stg"])
                    S.dma(lambda e: e.dma_start(out=o_psc, in_=stg[0:3, 1024:2560]), "o_psc", r=["stg"])
                if samp:
                    for s in range(NS):
                        S.dma(lambda e, s=s: e.dma_start(out=o_slc[s], in_=stg[4 * s + 1:4 * s + 4, 0:1024]), "o_slc", r=["stg"])
                        S.dma(lambda e, s=s: e.dma_start(out=o_ssc[s], in_=stg[4 * s + 1:4 * s + 4, 1024:2560]), "o_ssc", r=["stg"])
                if last:
                    S.pe(lambda e: e.transpose(out=pC[0:8, 0:128], in_=hstate, identity=ident), r=["hstate", "cst"], w=["pC", "pCx"])
                    S.act(lambda e: e.activation(out=stT[0:8, 0:128], in_=pC[0:8, 0:128], func=AF.Copy), r=["pC"], w=["stg"] + STGW)
                    S.dma(lambda e: e.dma_start(out=o_plh, in_=stT[0:8, 0:128]), "o_plh", r=["stg"])
                if samp:
                    for c in range(8):
                        S.pe(lambda e, c=c: e.transpose(out=pT[0:NS, c * 128:(c + 1) * 128], in_=hfin[:, c, :], identity=ident),
                             r=["hfin", "cst"], w=["pT"])
                    S.act(lambda e: e.activation(out=lh_in[0:NS, :], in_=pT[0:NS, :], func=AF.Copy), r=["pT"], w=["stg"] + STGW)
                    S.dma(lambda e: e.dma_start(out=o_slh, in_=lh_in[0:NS, :]), "o_slh", r=["stg"])


            def genP():
                S.dma(lambda e: e.dma_start(out=xt[0:T, :], in_=xsrc), "xt", w=["xt"])
                rms_rstd(xt, T, "xt", junk, ss, rstd)
                S.act(lambda e: e.activation(out=xn[0:T, :], in_=xt[0:T, :], func=AF.Copy, scale=rstd[0:T, 0:1]),
                      r=["xt", "rstd"], w=["xn"])
                to_fm(T, "GM", hTt, "hTt")

                if samp:
                    S.dma(lambda e: e.dma_start(out=lc_in[0:48, :], in_=st_lc), "stg", w=["stg"])
                    S.dma(lambda e: e.dma_start(out=sc_in[0:48, :], in_=st_sc), "stg", w=["stg"])
                    S.dma(lambda e: e.dma_start(out=lh_in[64:64 + NS, :], in_=st_lh), "stg", w=["stg"])
                    for c in range(8):
                        S.pe(lambda e, c=c: e.transpose(out=pC[:, 0:48], in_=lc_in[0:48, c * 128:(c + 1) * 128],
                                                        identity=ident[0:48, 0:48]), r=["stg", "cst"], w=["pC"])
                        S.act(lambda e, c=c: e.activation(out=lxs[:, c, :, 0:3],
                                                          in_=pC[:, 0:48].rearrange("p (s j) -> p s j", s=NS),
                                                          func=AF.Copy), r=["pC"], w=["lx%d" % c])
                        S.pe(lambda e, c=c: e.transpose(out=pD[:, 0:NS], in_=lh_in[64:64 + NS, c * 128:(c + 1) * 128],
                                                        identity=ident[64:64 + NS, 64:64 + NS]), r=["stg", "cst"], w=["pD"])
                        S.dve(lambda e, c=c: e.tensor_copy(out=h0s[:, c, :], in_=pD[:, 0:NS]), r=["pD"], w=["h0s"])
                    for c in range(12):
                        S.pe(lambda e, c=c: e.transpose(out=pC[:, 0:48], in_=sc_in[0:48, c * 128:(c + 1) * 128],
                                                        identity=ident[0:48, 0:48]), r=["stg", "cst"], w=["pC"])
                        S.act(lambda e, c=c: e.activation(out=xcs[:, c, :, 0:3],
                                                          in_=pC[:, 0:48].rearrange("p (s j) -> p s j", s=NS),
                                                          func=AF.Copy), r=["pC"], w=["xc%d" % c])

                yield
                yield from inter(g_lrux(), g_z())
                yield from inter(g_xbc())
                yield from inter(g_lru((0, 2, 4, 6), pA[0], "pA0", "pA0"), g_lru((1, 3, 5, 7), pA[1], "pA1", "pA1"))
                yield from inter(g_gate())
                norm_apply(T, EPS, hs, "yl%d", "GL", ynl, "ynl%d", 0, 8, rbc, "rbc", pD[:, 128:128 + T], "pDn")
                yield
                if samp:
                    late_outputs()

            def genS():
                yield from inter(g_ssd())
                if samp:
                    ssd_sample_states_prep()

                for c in range(8):
                    S.dve(lambda e, c=c: e.scalar_tensor_tensor(out=xsf[:, c, 0:T], in0=xsf[:, c, 0:T], scalar=P("DS", c),
                                                                in1=pT[:, c * 128:c * 128 + T], op0=ALU.mult, op1=ALU.add),
                          r=["pT", "xsf%d" % c, "pfm"], w=["xsf%d" % c])
                if samp:
                    S.dve(lambda e: e.tensor_tensor(out=xsf[:, :, 0:T], in0=xsf[:, :, 0:T], in1=pyo_sb, op=ALU.add),
                          r=["xsf%d" % c for c in range(8)] + ["pyo_sb"], w=["xsf%d" % c for c in range(8)])
                S.dve(lambda e: e.tensor_tensor(out=xsf[:, :, 0:T], in0=xsf[:, :, 0:T], in1=zs[:, :, 0:T], op=ALU.mult),
                      r=["xsf%d" % c for c in range(8)] + ["zs%d" % c for c in range(8)], w=["yg%d" % c for c in range(8)] + ["xsf%d" % c for c in range(8)])
                yield
                if not samp:
                    S.act(lambda e: e.activation(out=hTb, in_=hT, func=AF.Copy), r=["hT"], w=["hTb"])
                    if last:
                        for c in range(8):
                            S.pe(lambda e, c=c: e.transpose(out=pO[:, c * 128:(c + 1) * 128], in_=hT[:, c * 128:(c + 1) * 128], identity=ident),
                                 r=["hT", "cst"], w=["pO"])
                        S.dve(lambda e: e.tensor_copy(out=stT, in_=pO), r=["pO"], w=["stg"] + STGW)
                        S.dma(lambda e: e.dma_start(out=o_psh.rearrange("(c q) n -> q c n", q=128),
                                                    in_=stT.rearrange("p (c n) -> p c n", c=8)), "o_psh", r=["stg"])
                yield
                for g in range(2):
                    for c in range(4 * g, 4 * g + 4):
                        pp = c % 2
                        S.pool(lambda e, c=c, pp=pp: e.tensor_tensor(out=ysqS[:, pp, 0:T], in0=xsf[:, c, 0:T], in1=xsf[:, c, 0:T], op=ALU.mult),
                               r=["yg%d" % c], w=["ysq%s%d" % (sk, pp)])
                        S.pe(lambda e, c=c, pp=pp, g=g: e.matmul(pO[:, 0:T], lhsT=onesb, rhs=ysqS[:, pp, 0:T],
                                                                 start=(c == 4 * g), stop=(c == 4 * g + 3)),
                             r=["ysq%s%d" % (sk, pp), "onesb"], w=["pO"])
                    norm_apply(T, EPS, xsf, "yg%d", "GS", yns, "yns%d", 4 * g, 4 * g + 4, rbcS, "rbc" + sk, pO[:, 0:T], "pO")

                yield
                for nb in range(2):
                    for kc in range(16):
                        src = ynl if kc < 8 else yns
                        S.pe(lambda e, kc=kc, nb=nb, src=src: e.matmul(pO[0:T, nb * 512:(nb + 1) * 512], lhsT=src[:, kc % 8, 0:T],
                                                                       rhs=w_out_sb[:, kc, nb * 512:(nb + 1) * 512],
                                                                       start=(kc == 0), stop=(kc == 15)),
                             r=[("ynl%d" if kc < 8 else "yns%d") % (kc % 8), "w_out"], w=["pO"])
                S.dve(lambda e: e.tensor_tensor(out=xt[0:T, :], in0=pO[0:T, :], in1=xt[0:T, :], op=ALU.add),
                      r=["pO", "xt"], w=["xt"])
                S.dma(lambda e: e.dma_start(out=scr[row0:row0 + T, :], in_=xt[0:T, :]), "xnew", r=["xt"], w=["scr%d" % mt])

                if last:
                    late_outputs()

            return par, genP, genS

        def ssd_sample_states_prep():
            T = TS
            for i, dx in enumerate((dah, dal)):
                S.dve(lambda e, dx=dx, i=i: e.tensor_tensor(out=damb[0:T, i], in0=dx[0:T, :].unsqueeze(1).to_broadcast([T, NS, 16]),
                                                            in1=blki.unsqueeze(2).to_broadcast([T, NS, 16]), op=ALU.mult),
                      r=["dah", "dal", "cst"], w=["dam%d" % i])
                S.pe(lambda e, i=i: e.matmul(pD[:, 0:256], lhsT=onesb[0:T, :], rhs=damb[0:T, i].rearrange("p s h -> p (s h)"),
                                             start=(i == 0), stop=(i == 1)), r=["dam%d" % i, "onesb"], w=["pD", "pD2", "pD3", "pDn"])
            S.act(lambda e: e.activation(out=dtot.rearrange("p s h -> p (s h)"), in_=pD[:, 0:256], func=AF.Exp),
                  r=["pD"], w=["dtot"])
            S.barrier()
            dtotP = Dmf[:, 0:NS * 8].rearrange("p (s c) -> p s c", s=NS)
            for h2 in range(2):
                S.dve(lambda e, h2=h2: e.tensor_copy(out=dtotP[64 * h2:64 * h2 + 64],
                                                     in_=dtot[64 * h2:64 * h2 + 64].rearrange("p s (c two) -> p s c two", two=2)[:, :, :, h2]),
                      r=["dtot"], w=["dtotP"])
            h0in_b = [h0in, u]
            hout_b = [hout, hs]
            h0b_b = [hTb, ub.rearrange("p c t -> p (c t)")]
            h0Tb_b = [lrut[:, 0:512].bitcast(BF16), lrut[:, 512:1024].bitcast(BF16)]
            xdm_b = [hT[:, 0:512].bitcast(BF16), hT[:, 512:1024].bitcast(BF16)]
            pTr_b = [pC.bitcast(BF16), pD.bitcast(BF16)]
            pTk = [["pC"], ["pD"]]
            for s in range(NS):
                q = s % 2
                hi, ho, h0b, hb, xdm, pTr, tk = h0in_b[q], hout_b[q], h0b_b[q], h0Tb_b[q], xdm_b[q], pTr_b[q], pTk[q]
                S.dma(lambda e, s=s, hi=hi: e.dma_start(out=hi, in_=st_sh[s].rearrange("(c q) n -> q c n", q=128)),
                      "h0in%d" % q, w=["h0in%d" % q], q="pool")
                S.act(lambda e, hi=hi, h0b=h0b: e.activation(out=h0b, in_=hi.rearrange("p c n -> p (c n)"), func=AF.Copy),
                      r=["h0in%d" % q], w=["h0b%d" % q])
                for c in range(8):
                    S.pe(lambda e, c=c, h0b=h0b, pTr=pTr: e.transpose(out=pTr[:, c * 128:(c + 1) * 128], in_=h0b[:, c * 128:(c + 1) * 128],
                                                                      identity=identb), r=["h0b%d" % q, "identb"], w=tk)
                S.dve(lambda e, hb=hb, pTr=pTr: e.tensor_copy(out=hb, in_=pTr[:, 0:1024]), r=tk, w=["h0Tb%d" % q])
                for h in range(16):
                    S.pe(lambda e, h=h, s=s, hb=hb: e.matmul(pA[0][64 * (h % 2):64 * (h % 2) + 64, (h // 2) * TS + 4 * s:(h // 2) * TS + 4 * s + 4],
                                                             lhsT=hb[:, h * 64:(h + 1) * 64], rhs=Chs[:, h, 4 * s:4 * s + 4],
                                                             start=True, stop=True),
                         r=["h0Tb%d" % q, "Ch"], w=["pA0"])
                S.dve(lambda e, s=s, xdm=xdm: e.tensor_scalar(out=xdm[0:T, :], in0=xdd[0:T, :], scalar1=blki[:, s:s + 1], scalar2=None, op0=ALU.mult),
                      r=["xdd", "cst"], w=["xdm%d" % q])
                for c in range(8):
                    S.pe(lambda e, c=c, xdm=xdm: e.matmul(pO[:, c * 128:(c + 1) * 128], lhsT=xdm[0:T, c * 128:(c + 1) * 128],
                                                          rhs=BT[0:T, (c // 4) * 128:(c // 4 + 1) * 128], start=True, stop=True),
                         r=["xdm%d" % q, "BT"], w=["pO"])
                for c in range(8):
                    S.dve(lambda e, c=c, s=s, hi=hi, ho=ho: e.scalar_tensor_tensor(out=ho[:, c, :], in0=hi[:, c, :], scalar=dtotP[:, s, c:c + 1],
                                                                                   in1=pO[:, c * 128:(c + 1) * 128], op0=ALU.mult, op1=ALU.add),
                          r=["h0in%d" % q, "dtotP", "pO"], w=["hout%d" % q])
                S.dma(lambda e, s=s, ho=ho: e.dma_start(out=o_ssh[s].rearrange("(c q) n -> q c n", q=128), in_=ho),
                      "hout%d" % q, r=["hout%d" % q])
            S.act(lambda e: e.activation(out=pyo_sb.rearrange("p c t -> p (c t)"), in_=pA[0][:, 0:8 * TS], func=AF.Copy),
                  r=["pA0"], w=["pyo_sb"])

        S.pool(lambda e: e.memset(lxb, 0.0), w=["lx%d" % c for c in range(8)])
        S.pool(lambda e: e.memset(xcb, 0.0), w=["xc%d" % c for c in range(12)])

        def drive(g_, par):
            S.ctx = par
            try:
                next(g_)
                return True
            except StopIteration:
                return False
            finally:
                S.ctx = None

        tiles = [mixer_tile(mt, False) for mt in range(NT)]
        RATIO = 3
        par0, gP0, _ = tiles[0]
        g = gP0()
        while drive(g, par0):
            pass
        for n in range(NT):
            par, _, gS = tiles[n]
            gs = gS()
            alive_s = True
            alive_p = False
            if n + 1 < NT:
                parn, gPn, _ = tiles[n + 1]
                gp = gPn()
                alive_p = True
            while alive_s or alive_p:
                for _ in range(RATIO):
                    if alive_p:
                        alive_p = drive(gp, parn)
                if alive_s:
                    alive_s = drive(gs, par)
        S.barrier()
        if SAMP:
            pars, gPs, gSs = mixer_tile(SEQ // 128, True)
            for g in (gPs(), gSs()):
                while drive(g, pars):
                    pass

        S.barrier()
        ptr[0] = base0
        w_up_sb = b3(8, DFF)
        w_dn_sb = b3(32, D)
        if MLP:
            k_wup = load_w(w_up_sb, w_up, 8, DFF, "w_up")
            k_wdn = load_w(w_dn_sb, w_down, 32, D, "w_dn")
        T2 = 256
        xt2 = [[f32(D), f32(D)], [f32(D), f32(D)]]
        xn2 = f32(D)
        ss2 = [f32(4), f32(4)]
        rstd2 = [f32(4), f32(4)]
        mT = [b3(8, T2), b3(8, T2)]
        actb = b3(32, T2)
        rl = [f32(T2), f32(T2)]
        yout = f32(D)
        gfin_bc = f32(D)
        S.dma(lambda e: e.dma_start(out=gfin_bc, in_=gfin_d.partition_broadcast(128)), "gfin", w=["gfin"])
        pDN = [PS[:, 3072:4096], PS[:, 2048:3072]]
        pDNk = [["pO"], ["pC", "pD"]]

        def mlp_front(ti, r0, T):
            q = ti % 2
            nsub = (T + 127) // 128
            for j in range(nsub):
                Tj = min(128, T - j * 128)
                xk = "xt2_%d_%d" % (q, j)
                S.dma(lambda e, j=j, Tj=Tj: e.dma_start(out=xt2[q][j][0:Tj, :], in_=scr[r0 + j * 128:r0 + j * 128 + Tj, :]),
                      xk, r=["scr%d" % ((r0 + j * 128) // 128)], w=[xk])
                rms_rstd(xt2[q][j], Tj, xk, xn2, ss2[0], rstd2[0], "2")
                S.act(lambda e, j=j, Tj=Tj: e.activation(out=xn2[0:Tj, :], in_=xt2[q][j][0:Tj, :], func=AF.Copy, scale=rstd2[0][0:Tj, 0:1]),
                      r=[xk, "rstd2"], w=["xn2"])
                for k in range(8):
                    S.pe(lambda e, k=k, Tj=Tj: e.transpose(out=pT[:, k * 128:k * 128 + Tj], in_=xn2[0:Tj, k * 128:(k + 1) * 128],
                                                           identity=ident[0:Tj, 0:Tj]), r=["xn2", "cst"], w=["pT"])
                S.dve(lambda e, j=j, Tj=Tj: e.tensor_tensor(
                    out=mT[q][:, :, j * 128:j * 128 + Tj], in0=pT.rearrange("p (k t) -> p k t", k=8)[:, :, 0:Tj],
                    in1=P("GP", 0, 8).unsqueeze(2).to_broadcast([128, 8, Tj]), op=ALU.mult),
                    r=["pT", "pfm"], w=["mT%d" % q])
            yield
            for f in range(32):
                pa = pA[f % 2]
                for k in range(8):
                    S.pe(lambda e, k=k, f=f, pa=pa: e.matmul(pa[:, 0:T], lhsT=w_up_sb[:, k, f * 128:(f + 1) * 128], rhs=mT[q][:, k, 0:T],
                                                             start=(k == 0), stop=(k == 7)),
                         r=["mT%d" % q, "w_up"], w=["pA%d" % (f % 2)])
                S.act(lambda e, f=f, pa=pa: e.activation(out=rl[f % 2][:, 0:T], in_=pa[:, 0:T], func=AF.Relu),
                      r=["pA%d" % (f % 2)], w=["rl%d" % (f % 2)])
                S.pool(lambda e, f=f: e.tensor_tensor(out=actb[:, f, 0:T], in0=rl[f % 2][:, 0:T], in1=rl[f % 2][:, 0:T], op=ALU.mult),
                       r=["rl%d" % (f % 2)], w=["act%d" % f])
                yield

        def mlp_back(ti, r0, T):
            q = ti % 2
            nsub = (T + 127) // 128
            for f in range(32):
                for j in range(nsub):
                    Tj = min(128, T - j * 128)
                    for nb in range(2):
                        S.pe(lambda e, f=f, nb=nb, j=j, Tj=Tj: e.matmul(pDN[j][0:Tj, nb * 512:(nb + 1) * 512],
                                                                        lhsT=actb[:, f, j * 128:j * 128 + Tj],
                                                                        rhs=w_dn_sb[:, f, nb * 512:(nb + 1) * 512],
                                                                        start=(f == 0), stop=(f == 31)),
                             r=["act%d" % f, "w_dn"], w=pDNk[j])
                yield
            for j in range(nsub):
                Tj = min(128, T - j * 128)
                xk = "xt2_%d_%d" % (q, j)
                S.dve(lambda e, j=j, Tj=Tj: e.tensor_tensor(out=xt2[q][j][0:Tj, :], in0=pDN[j][0:Tj, :], in1=xt2[q][j][0:Tj, :], op=ALU.add),
                      r=pDNk[j] + [xk], w=[xk])
                rms_rstd(xt2[q][j], Tj, xk, yout, ss2[1], rstd2[1], "2b", jkey="yout")
                S.dve(lambda e, j=j, Tj=Tj: e.scalar_tensor_tensor(out=yout[0:Tj, :], in0=xt2[q][j][0:Tj, :], scalar=rstd2[1][0:Tj, 0:1],
                                                                   in1=gfin_bc[0:Tj, :], op0=ALU.mult, op1=ALU.mult),
                      r=[xk, "rstd2b", "gfin"], w=["yout"])
                rr = r0 + j * 128
                if rr < SEQ:
                    S.dma(lambda e, rr=rr, Tj=Tj: e.dma_start(out=y_p[rr:rr + Tj, :], in_=yout[0:Tj, :]), "yout", r=["yout"])
                else:
                    S.dma(lambda e, Tj=Tj: e.dma_start(out=y_s, in_=yout[0:Tj, :]), "yout", r=["yout"])
                yield

        if MLP:
            jobs = [(t * T2, T2) for t in range(NT * 128 // T2)]
            if SAMP:
                jobs.append((SEQ, TS))
            for _ in mlp_front(0, *jobs[0]):
                pass
            for ti in range(len(jobs)):
                gb = mlp_back(ti, *jobs[ti])
                gf = mlp_front(ti + 1, *jobs[ti + 1]) if ti + 1 < len(jobs) else iter(())
                ab = af = True
                while ab or af:
                    if ab:
                        ab = next(gb, "END") != "END"
                    if af:
                        af = next(gf, "END") != "END"

        S.emit()
    return nc


_CACHE = {}


def _consts():
    c = np.zeros((128, NCST), np.float32)
    i = np.arange(128)
    c[:, CI:CI + 128] = np.eye(128, dtype=np.float32)
    c[:, CU:CU + 128] = (i[:, None] <= i[None, :]).astype(np.float32)
    c[:, CN:CN + 128] = np.where(i[:, None] <= i[None, :], 0.0, NEG).astype(np.float32)
    c[:, CO:CO + 128] = 1.0
    j = np.arange(TS)
    same = (j[:, None] // 4) == (j[None, :] // 4)
    caus = j[:, None] <= j[None, :]
    c[0:TS, CUB:CUB + TS] = (same & caus).astype(np.float32)
    c[0:TS, CNB:CNB + TS] = np.where(same & caus, 0.0, NEG).astype(np.float32)
    c[0:TS, CBM:CBM + TS] = same.astype(np.float32)
    c[0:TS, CBI:CBI + NS] = ((j[:, None] // 4) == np.arange(NS)[None, :]).astype(np.float32)
    return c


def _fm(v, nch):
    return np.ascontiguousarray(np.asarray(v, np.float32).reshape(nch, 128).T)


def kernel(x_prompt, x_sample, state_lru_conv, state_lru_h, state_ssd_conv, state_ssd_h,
           g_mix, w_in, lru_conv_w, lru_conv_b, w_a, b_a, w_x, b_x, lam, g_lru_out,
           ssd_conv_w, ssd_conv_b, dt_bias, a_log, d_skip, g_ssd_out, w_out,
           g_mlp, w_up, w_down, g_final):
    f = lambda a: np.ascontiguousarray(np.asarray(a, np.float32))
    if "nc" not in _CACHE:
        _CACHE["nc"] = build_program()
    nc = _CACHE["nc"]
    pfm = np.zeros((128, NPAR), np.float32)
    lw = np.asarray(lru_conv_w[0], np.float32)
    pfm[:, PC["LW"]:PC["LW"] + 32] = lw.reshape(4, 8, 128).transpose(2, 1, 0).reshape(128, 32)
    pfm[:, PC["LB"]:PC["LB"] + 8] = _fm(lru_conv_b[0], 8)
    pfm[:, PC["BA"]:PC["BA"] + 8] = _fm(np.asarray(b_a[0]).reshape(-1), 8)
    pfm[:, PC["BX"]:PC["BX"] + 8] = _fm(np.asarray(b_x[0]).reshape(-1), 8)
    pfm[:, PC["LAM"]:PC["LAM"] + 8] = _fm(lam[0], 8)
    pfm[:, PC["GL"]:PC["GL"] + 8] = _fm(g_lru_out[0], 8)
    sw = np.asarray(ssd_conv_w[0], np.float32)
    pfm[:, PC["SW"]:PC["SW"] + 48] = sw.reshape(4, 12, 128).transpose(2, 1, 0).reshape(128, 48)
    pfm[:, PC["SB"]:PC["SB"] + 12] = _fm(ssd_conv_b[0], 12)
    pfm[:, PC["DS"]:PC["DS"] + 8] = _fm(np.repeat(np.asarray(d_skip[0], np.float32), 64), 8)
    pfm[:, PC["GS"]:PC["GS"] + 8] = _fm(g_ssd_out[0], 8)
    pfm[:, PC["GM"]:PC["GM"] + 8] = _fm(g_mix[0], 8)
    pfm[:, PC["GP"]:PC["GP"] + 8] = _fm(g_mlp[0], 8)
    cst = _consts()
    shared = {
        "w_in": f(w_in[0]), "w_out": f(w_out[0]), "w_up": f(w_up[0]), "w_down": f(w_down[0]),
        "w_a": f(w_a[0]), "w_x": f(w_x[0]), "pfm": pfm, "cst": cst,
        "dt_bias": f(dt_bias[0]), "a_log": f(a_log[0]), "g_final": f(g_final),
    }
    in_maps = []
    for b in range(NCORES):
        sl = slice(NS * b, NS * (b + 1))
        m = dict(shared)
        m["xp"] = f(x_prompt[b])
        m["xs"] = f(np.asarray(x_sample[sl]).reshape(TS, D))
        m["st_lc"] = f(np.asarray(state_lru_conv[0, sl]).reshape(NS * 3, D))
        m["st_lh"] = f(state_lru_h[0, sl])
        m["st_sc"] = f(np.asarray(state_ssd_conv[0, sl]).reshape(NS * 3, XBC))
        m["st_sh"] = f(np.asarray(state_ssd_h[0, sl]).reshape(NS, 1024, 128))
        in_maps.append(m)
    res = run_bass_kernel_spmd(nc, in_maps, core_ids=list(range(NCORES)))
    R = res.results
    cat = lambda k: np.stack([np.asarray(R[b][k], np.float32) for b in range(NCORES)])
    y_prompt = cat("y_p")
    y_sample = cat("y_s").reshape(NCORES * NS, 4, D)
    p_lc = cat("o_plc")[None]
    p_lh = cat("o_plh").reshape(NCORES, D)[None]
    p_sc = cat("o_psc")[None]
    p_sh = cat("o_psh").reshape(NCORES, 16, 64, 128)[None]
    s_lc = cat("o_slc").reshape(NCORES * NS, 3, D)[None]
    s_lh = cat("o_slh").reshape(NCORES * NS, D)[None]
    s_sc = cat("o_ssc").reshape(NCORES * NS, 3, XBC)[None]
    s_sh = cat("o_ssh").reshape(NCORES * NS, 16, 64, 128)[None]
    return (y_prompt, y_sample, p_lc, p_lh, p_sc, p_sh, s_lc, s_lh, s_sc, s_sh)
```

```python
import math
from contextlib import ExitStack

import numpy as np
import concourse.bass as bass
import concourse.mybir as mybir
from concourse.bass_utils import run_bass_kernel_spmd

F32 = mybir.dt.float32
BF16 = mybir.dt.bfloat16
AF = mybir.ActivationFunctionType
ALU = mybir.AluOpType

NCORES = 8
D = 1024
SEQ = 2048
NS = 16
TS = 64
XBC = 1536
INP = 4624
DFF = 4096
EPS = 1e-6
NEG = -30000.0

import re as _re

ENGS = ("pe", "act", "dve", "pool", "sp")
SAME_ENGINE_SYNC = {"pe": False, "act": True, "dve": True, "pool": True, "sp": False}


class Op:
    __slots__ = ("eng", "fn", "deps", "marked", "count", "dma_key", "dma_val")

    def __init__(self, eng, fn, dma_key=None):
        self.eng = eng
        self.fn = fn
        self.deps = ()
        self.marked = False
        self.count = 0
        self.dma_key = dma_key
        self.dma_val = 0


class Sched:
    def __init__(self, nc):
        self.nc = nc
        self.ops = {e: [] for e in ENGS}
        self.last_w = {}
        self.readers = {}
        self.dma_cnt = {}
        self.pending = {}
        self.ctx = None
        self.since_bar = []

    ALIAS = {"pC": "b4", "pCx": "b4", "pD": "b5", "pD2": "b5", "pD3": "b5", "pDn": "b5",
             "pDcb0": "b5", "pDcb1": "b5", "pT": "b01", "pO": "b67", "pA0": "b2", "pA1": "b3"}

    PSUM_KEYS = {"b01", "b2", "b3", "b4", "b5", "b67"}

    PAR_RE = _re.compile(r"^(xt|dtt|xnew)$|^(xsf|zs|Bb|Cb|ynl|yg)\d+$")

    def _k(self, k):
        k = self.ALIAS.get(k, k)
        if self.ctx is not None and self.PAR_RE.match(k):
            return k + "#" + str(self.ctx)
        return k

    def add(self, eng, fn, reads=(), writes=(), dma_key=None):
        reads = [self._k(k) for k in reads]
        writes = [self._k(k) for k in writes]
        if dma_key is not None and self.ctx is not None and self.PAR_RE.match(dma_key):
            dma_key = dma_key + "#" + str(self.ctx)
        op = Op(eng, fn, dma_key)
        deps = []
        seen = set()

        def dep(o):
            if o is not None and o is not op and id(o) not in seen:
                seen.add(id(o))
                deps.append(o)

        if self.pending.get(eng):
            for o in self.pending[eng]:
                dep(o)
            self.pending[eng] = []
        for b in reads:
            dep(self.last_w.get(b))
            if b in self.PSUM_KEYS:
                for r in self.readers.get(b, ()):
                    if r.eng != eng:
                        dep(r)
        for b in writes:
            dep(self.last_w.get(b))
            for r in self.readers.get(b, ()):
                dep(r)
        for b in reads:
            self.readers.setdefault(b, []).append(op)
        for b in writes:
            self.last_w[b] = op
            self.readers[b] = []
        op.deps = deps
        if dma_key is not None:
            self.dma_cnt[dma_key] = self.dma_cnt.get(dma_key, 0) + 16
            op.dma_val = self.dma_cnt[dma_key]
        self.ops[eng].append(op)
        self.since_bar.append(op)
        return op

    def barrier(self):
        ops = []
        for e in ENGS:
            comp = [o for o in self.ops[e] if o.dma_key is None]
            if comp:
                ops.append(comp[-1])
        last_dma = {}
        for o in self.since_bar:
            if o.dma_key is not None:
                last_dma[o.dma_key] = o
        ops.extend(last_dma.values())
        for e in ENGS:
            self.pending.setdefault(e, []).extend(ops)
        self.since_bar = []

    def pe(self, fn, r=(), w=()):
        return self.add("pe", fn, r, w)

    def act(self, fn, r=(), w=()):
        return self.add("act", fn, r, w)

    def dve(self, fn, r=(), w=()):
        return self.add("dve", fn, r, w)

    def pool(self, fn, r=(), w=()):
        return self.add("pool", fn, r, w)

    def dma(self, fn, key, r=(), w=(), q="sp"):
        return self.add(q, fn, r, w, dma_key=key)

    def emit(self):
        nc = self.nc
        for e in ENGS:
            for op in self.ops[e]:
                for d in op.deps:
                    if d.dma_key is None:
                        if d.eng == op.eng and not SAME_ENGINE_SYNC[d.eng]:
                            continue
                        d.marked = True
        for e in ENGS:
            c = 0
            for op in self.ops[e]:
                if op.dma_key is None and op.marked:
                    c += 1
                    op.count = c
        with ExitStack() as st:
            esem = {e: st.enter_context(nc.semaphore("es_" + e)) for e in ENGS}
            dsem = {}
            for k in self.dma_cnt:
                dsem[k] = st.enter_context(nc.semaphore("ds_%d" % len(dsem)))
            block = st.enter_context(nc.Block())

            def run(ename, eng):
                seen = {}
                for op in self.ops[ename]:
                    need = {}
                    for d in op.deps:
                        if d.dma_key is not None:
                            key = ("d", d.dma_key)
                            val = d.dma_val
                            sem = dsem[d.dma_key]
                        else:
                            if d.eng == ename and not SAME_ENGINE_SYNC[ename]:
                                continue
                            key = ("e", d.eng)
                            val = d.count
                            sem = esem[d.eng]
                        if key not in need or need[key][1] < val:
                            need[key] = (sem, val)
                    for key, (sem, val) in need.items():
                        if seen.get(key, 0) >= val:
                            continue
                        seen[key] = val
                        eng.wait_ge(sem, val)
                    ins = op.fn(eng)
                    if op.dma_key is not None:
                        ins.then_inc(dsem[op.dma_key], 16)
                    elif op.marked:
                        ins.then_inc(esem[ename], 1)
                if ename == "sp":
                    for k, v in self.dma_cnt.items():
                        eng.wait_ge(dsem[k], v)

            @block.sync
            def _(e):
                run("sp", e)

            @block.tensor
            def _(e):
                run("pe", e)

            @block.scalar
            def _(e):
                run("act", e)

            @block.vector
            def _(e):
                run("dve", e)

            @block.gpsimd
            def _(e):
                run("pool", e)


PC = {}
_o = 0
for _n, _w in (("LW", 32), ("LB", 8), ("BA", 8), ("BX", 8), ("LAM", 8), ("GL", 8), ("SW", 48),
               ("SB", 12), ("DS", 8), ("GS", 8), ("GM", 8), ("GP", 8)):
    PC[_n] = _o
    _o += _w
NPAR = _o
CI, CU, CN, CO, CUB, CNB, CBM, CBI = 0, 128, 256, 384, 512, 576, 640, 704
NCST = 720


def build_program(NT=SEQ // 128, SAMP=True, MLP=True, DBG=False, STAGE=9):
    nc = bass.Bass("TRN2", target_bir_lowering=False)
    S = Sched(nc)

    def din(name, shape):
        return nc.dram_tensor(name, list(shape), F32, kind="ExternalInput").ap()

    def dout(name, shape):
        return nc.dram_tensor(name, list(shape), F32, kind="ExternalOutput").ap()

    xp = din("xp", (SEQ, D))
    xs = din("xs", (TS, D))
    st_lc = din("st_lc", (NS * 3, D))
    st_lh = din("st_lh", (NS, D))
    st_sc = din("st_sc", (NS * 3, XBC))
    st_sh = din("st_sh", (NS, 1024, 128))
    w_in = din("w_in", (D, INP))
    w_out = din("w_out", (2 * D, D))
    w_up = din("w_up", (D, DFF))
    w_down = din("w_down", (DFF, D))
    w_a = din("w_a", (16, 64, 64))
    w_x = din("w_x", (16, 64, 64))
    pfm_d = din("pfm", (128, NPAR))
    cst_d = din("cst", (128, NCST))
    dtb_d = din("dt_bias", (16,))
    alog_d = din("a_log", (16,))
    gfin_d = din("g_final", (D,))

    y_p = dout("y_p", (SEQ, D))
    y_s = dout("y_s", (TS, D))
    o_plc = dout("o_plc", (3, D))
    o_plh = dout("o_plh", (8, 128))
    o_psc = dout("o_psc", (3, XBC))
    o_psh = dout("o_psh", (1024, 128))
    o_slc = dout("o_slc", (NS, 3, D))
    o_slh = dout("o_slh", (NS, D))
    o_ssc = dout("o_ssc", (NS, 3, XBC))
    o_ssh = dout("o_ssh", (NS, 1024, 128))
    scr = nc.dram_tensor("scr", [SEQ + TS, D], F32, kind=("ExternalOutput" if DBG else "Internal")).ap()

    st = ExitStack()
    with st:
        RW = 53200
        R = st.enter_context(nc.sbuf_tensor("R", [128, RW], F32))
        PS = st.enter_context(nc.psum_tensor("PS", [128, 4096], F32))
        ptr = [0]

        def alloc(nwords):
            a = ptr[0]
            ptr[0] += (nwords + 7) // 8 * 8
            pass
            return a

        def f32(n):
            a = alloc(n)
            return R[:, a:a + n]

        def bf(n):
            w = (n + 1) // 2
            a = alloc(w)
            return R[:, a:a + w].bitcast(BF16)[:, 0:n]

        def f3(c, t):
            return f32(c * t).rearrange("p (c t) -> p c t", c=c)

        def b3(c, t):
            return bf(c * t).rearrange("p (c t) -> p c t", c=c)

        def bank(b, n=512):
            return PS[:, 512 * b:512 * b + n]

        pT = PS[:, 0:1024]
        pTb = pT.bitcast(BF16)
        pA = [bank(2), bank(3)]
        pC = bank(4)
        pCb = pC.bitcast(BF16)
        pD = bank(5)
        pO = PS[:, 3072:4096]

        cst = f32(NCST)
        pfm = f32(NPAR)
        dtb_bc = f32(16)
        a_bc = f32(16)
        identb = bf(128)
        onesb = bf(128)
        Utrib = bf(128)
        mskb = bf(3 * TS)
        dah = bf(16)
        dal = bf(16)
        cfac = f32(8)
        c2fac = f32(8)
        tiny = f32(8)
        mhalf = f32(4)
        eps_t = f32(4)
        nbias = f32(16)
        wa_blk = b3(8, 128)
        wx_blk = b3(8, 128)
        hstate = f32(8)
        hT = f32(1024)
        hTb = bf(1024)

        ident = cst[:, CI:CI + 128]
        Utri = cst[:, CU:CU + 128]
        negm = cst[:, CN:CN + 128]
        onesf = cst[:, CO:CO + 128]
        Ublk = cst[0:TS, CUB:CUB + TS]
        negblk = cst[0:TS, CNB:CNB + TS]
        blkm = cst[0:TS, CBM:CBM + TS]
        blki = cst[0:TS, CBI:CBI + NS]

        S.dma(lambda e: e.dma_start(out=cst, in_=cst_d), "cst", w=["cst"])
        S.dma(lambda e: e.dma_start(out=pfm, in_=pfm_d), "pfm", w=["pfm"])
        S.dma(lambda e: e.dma_start(out=dtb_bc, in_=dtb_d.partition_broadcast(128)), "dtb", w=["dtb"])
        S.dma(lambda e: e.dma_start(out=a_bc, in_=alog_d.partition_broadcast(128)), "alog", w=["a_bc"])
        S.dve(lambda e: e.tensor_copy(out=identb, in_=ident), r=["cst"], w=["identb"])
        S.dve(lambda e: e.tensor_copy(out=onesb, in_=onesf), r=["cst"], w=["onesb"])
        S.dve(lambda e: e.tensor_copy(out=Utrib, in_=Utri), r=["cst"], w=["mskb"])
        S.dve(lambda e: e.tensor_copy(out=mskb[0:TS, 0:TS], in_=Ublk), r=["cst"], w=["mskb"])
        S.dve(lambda e: e.tensor_copy(out=mskb[0:TS, TS:2 * TS], in_=blkm), r=["cst"], w=["mskb"])
        S.pool(lambda e: e.memset(mhalf, -0.5), w=["mhalf"])
        S.pool(lambda e: e.memset(eps_t, EPS), w=["eps_t"])
        S.dve(lambda e: e.tensor_scalar(out=nbias[:, 0:8], in0=pfm[:, PC["BA"]:PC["BA"] + 8], scalar1=-1.0, scalar2=None, op0=ALU.mult), r=["pfm"], w=["nbias"])
        S.dve(lambda e: e.tensor_scalar(out=nbias[:, 8:16], in0=pfm[:, PC["BX"]:PC["BX"] + 8], scalar1=-1.0, scalar2=None, op0=ALU.mult), r=["pfm", "nbias"], w=["nbias"])
        S.pool(lambda e: e.memset(hstate, 0.0), w=["hstate"])
        S.pool(lambda e: e.memset(hT, 0.0), w=["hT"])
        S.pool(lambda e: e.memset(hTb, 0.0), w=["hTb"])
        S.pool(lambda e: e.memset(wa_blk, 0.0), w=["wa"])
        S.pool(lambda e: e.memset(wx_blk, 0.0), w=["wx"])
        S.act(lambda e: e.activation(out=a_bc, in_=a_bc, func=AF.Exp), r=["a_bc"], w=["a_bc"])
        S.dve(lambda e: e.tensor_scalar(out=a_bc, in0=a_bc, scalar1=-1.0, scalar2=None, op0=ALU.mult), r=["a_bc"], w=["a_bc"])
        lam = pfm[:, PC["LAM"]:PC["LAM"] + 8]
        S.act(lambda e: e.activation(out=tiny, in_=lam, func=AF.Exp, scale=-1.0), r=["pfm"], w=["tiny"])
        S.act(lambda e: e.activation(out=tiny, in_=tiny, func=AF.Ln, bias=1.0), r=["tiny"], w=["tiny"])
        S.dve(lambda e: e.tensor_scalar(out=cfac, in0=tiny, scalar1=-8.0, scalar2=None, op0=ALU.mult), r=["tiny"], w=["cfac"])
        S.dve(lambda e: e.tensor_scalar(out=c2fac, in0=tiny, scalar1=-16.0, scalar2=None, op0=ALU.mult), r=["tiny"], w=["cfac2"])
        for (wd, blk, nm) in ((w_a, wa_blk, "wa"), (w_x, wx_blk, "wx")):
            v = wd.rearrange("(c h) i j -> h i c j", h=2)
            for h2 in range(2):
                S.dma(lambda e, v=v, blk=blk, h2=h2: e.dma_start(
                    out=blk[64 * h2:64 * h2 + 64, :, 64 * h2:64 * h2 + 64], in_=v[h2]),
                    nm + str(h2), w=[nm], q="pool")

        base0 = ptr[0]

        def load_w(dst3, src2, nk, ncol, name, step=2048):
            sv = src2.rearrange("(k p) n -> p k n", p=128)
            pieces = [(k, c0, min(ncol, c0 + step)) for k in range(nk) for c0 in range(0, ncol, step)]
            for i, (k, c0, c1) in enumerate(pieces):
                S.dma(lambda e, k=k, c0=c0, c1=c1: e.dma_start(out=dst3[:, k, c0:c1], in_=sv[:, k, c0:c1]),
                      name, w=([name] if i == len(pieces) - 1 else []), q="pool")
            return name

        w_in_sb = b3(8, INP)
        w_out_sb = b3(16, D)
        if STAGE >= 1:
            k_win = load_w(w_in_sb, w_in, 8, INP, "w_in")
            k_wout = load_w(w_out_sb, w_out, 16, D, "w_out")

        xt = f32(D)
        xn = f32(D)
        junk = xn
        ss = f32(4)
        rstd = f32(4)
        hTt = b3(8, 128)
        lxb = f3(8, 131)
        xcb = f3(12, 131)
        sreg = f32(20 * NS * 7)
        lxs = sreg[:, 0:8 * NS * 7].rearrange("p (c s l) -> p c s l", c=8, s=NS)
        xcs = sreg[:, 8 * NS * 7:20 * NS * 7].rearrange("p (c s l) -> p c s l", c=12, s=NS)
        gl = f3(2, 128)
        zs = f3(8, 128)
        u = f3(8, 128)
        ub = b3(8, 128)
        lrut = f32(1280)
        gi = lrut[:, 0:512].rearrange("p (c t) -> p c t", c=4)
        av = lrut[:, 512:768].rearrange("p (c t) -> p c t", c=2)
        a2 = lrut[:, 768:1024].rearrange("p (c t) -> p c t", c=2)
        tmpb = lrut[:, 1024:1280].rearrange("p (c t) -> p c t", c=2)
        hs = f3(8, 128)
        ysq = b3(2, 128)
        rbc = f32(128)
        ynl = b3(8, 128)
        yns = b3(8, 128)
        xsf = f3(8, 128)
        Bb = b3(2, 128)
        Cb = b3(2, 128)
        dtr = f32(16)
        dtt = f32(16)
        da = f32(16)
        ncum = f32(16)
        dte = f32(16)
        cdec = f32(16)
        xdt = bf(1024)
        xdd = bf(1024)
        BT = bf(256)
        cbT = f3(2, 128)
        Dmf = f32(512)
        Emf = f32(512)
        Mmf = bf(512)
        Chf = bf(1024)
        Chp = Chf[:, 0:512].rearrange("p (a t) -> p a t", a=4)
        Chs = Chf.rearrange("p (a t) -> p a t", a=16)
        cvt = f3(2, 128)
        stg = f32(2560)
        stT = stg[:, 0:1024]
        lc_in = stg[:, 0:1024]
        sc_in = stg[:, 1024:2560]
        lh_in = stg[:, 0:1024]
        h0in = lxb.rearrange("p c t -> p (c t)")[:, 0:1024].rearrange("p (c t) -> p c t", c=8)
        h0Tb = hTb
        Bm = bf(256)
        pyo_f = f32(8 * TS)
        pyo_sb = pyo_f.rearrange("p (c t) -> p c t", c=8)
        damb = bf(2 * NS * 16).rearrange("p (i s h) -> p i s h", i=2, s=NS)
        dtot = f3(NS, 16)
        hnew = hT
        hout = xcb.rearrange("p c t -> p (c t)")[:, 0:1024].rearrange("p (c t) -> p c t", c=8)
        h0s = f3(8, NS)
        hfin = f3(8, NS)

        bf3 = lambda ap, c: ap.bitcast(BF16).rearrange("p (c t) -> p c t", c=c)
        xt_b = [xt, stg[:, 0:1024]]
        ynl_b = [ynl, bf3(stg[:, 1024:1536], 8)]
        Bb_b = [Bb, bf3(stg[:, 1536:1664], 2)]
        Cb_b = [Cb, bf3(stg[:, 1664:1792], 2)]
        dtt_b = [dtt, stg[:, 1792:1808]]
        xsf_b = [xsf, sreg[:, 0:1024].rearrange("p (c t) -> p c t", c=8)]
        zs_b = [zs, sreg[:, 1024:2048].rearrange("p (c t) -> p c t", c=8)]
        Wl_S = pyo_f[:, 0:256].bitcast(BF16)
        ysq_S = bf3(pyo_f[:, 256:384], 2)
        rbc_S = pyo_f[:, 384:512]
        STGW = [k + "#1" for k in ["xt", "dtt"] + ["ynl%d" % c for c in range(8)] + ["Bb0", "Bb1", "Cb0", "Cb1"]]
        STG_ALIAS = ["xt", "dtt"] + ["ynl%d" % c for c in range(8)] + ["Bb0", "Bb1", "Cb0", "Cb1"]

        def P(name, c=None, w=1):
            o = PC[name] + (0 if c is None else c * w)
            return pfm[:, o:o + w]

        def rms_rstd(xtile, T, keyx, junk, ss, rstd, sfx="", jkey=None):
            S.act(lambda e: e.activation(out=junk[0:T, :], in_=xtile[0:T, :], func=AF.Square, accum_out=ss[0:T, 0:1]),
                  r=[keyx], w=[jkey or ("xn" + sfx), "ss" + sfx])
            S.act(lambda e: e.activation(out=ss[0:T, 0:1], in_=ss[0:T, 0:1], func=AF.Ln, scale=1.0 / D, bias=eps_t[0:T, 0:1]),
                  r=["ss" + sfx, "eps_t"], w=["ss" + sfx])
            S.act(lambda e: e.activation(out=rstd[0:T, 0:1], in_=ss[0:T, 0:1], func=AF.Exp, scale=-0.5),
                  r=["ss" + sfx], w=["rstd" + sfx])

        pAA = PS[:, 1024:2048]

        def to_fm(T, gname, dst, dkey):
            for k in range(8):
                S.pe(lambda e, k=k: e.transpose(out=pAA[:, k * 128:k * 128 + T], in_=xn[0:T, k * 128:(k + 1) * 128],
                                                identity=ident[0:T, 0:T]), r=["xn", "cst"], w=["pA0", "pA1"])
            S.dve(lambda e: e.tensor_tensor(
                out=dst[:, :, 0:T], in0=pAA.rearrange("p (k t) -> p k t", k=8)[:, :, 0:T],
                in1=P(gname, 0, 8).unsqueeze(2).to_broadcast([128, 8, T]), op=ALU.mult),
                r=["pA0", "pA1", "pfm"], w=[dkey])

        def mixer_tile(mt, samp):
            T = TS if samp else 128
            row0 = SEQ if samp else mt * 128
            xsrc = xs if samp else xp[mt * 128:(mt + 1) * 128, :]
            last = (not samp) and mt == NT - 1
            par = 0 if samp else (NT - 1 - mt) % 2
            xt, ynl, Bb, Cb, dtt, xsf, zs = (xt_b[par], ynl_b[par], Bb_b[par], Cb_b[par], dtt_b[par], xsf_b[par], zs_b[par])
            Wl = cvt.rearrange("p a t -> p (a t)").bitcast(BF16) if samp else Wl_S
            wlk = ["cv_t0", "cv_t1"] if samp else ["WlS"]
            ysqS = ysq if samp else ysq_S
            rbcS = rbc if samp else rbc_S
            sk = "" if samp else "S"

            def inter(*gens):
                gens = list(gens)
                while gens:
                    for g_ in list(gens):
                        try:
                            next(g_)
                        except StopIteration:
                            gens.remove(g_)
                        yield

            pcnt = [0]

            def proj(ci):
                i = pcnt[0] % 2
                pcnt[0] += 1
                pa = pA[i]
                for k in range(8):
                    S.pe(lambda e, k=k: e.matmul(pa[:, 0:T], lhsT=w_in_sb[:, k, ci * 128:(ci + 1) * 128],
                                                 rhs=hTt[:, k, 0:T], start=(k == 0), stop=(k == 7)),
                         r=["hTt", "w_in"], w=["pA%d" % i])
                return pa, "pA%d" % i

            def new_cols(buf, sbuf_, c):
                if samp:
                    return sbuf_[:, c, :, 3:7]
                return buf[:, c, 3:131]

            def pa_view(pa):
                if samp:
                    return pa[:, 0:T].rearrange("p (s l) -> p s l", s=NS)
                return pa[:, 0:T]

            def tap(buf, sbuf_, c, k):
                if samp:
                    return sbuf_[:, c, :, k:k + 4]
                return buf[:, c, k:k + 128]

            def fm(t3, c):
                if samp:
                    return t3[:, c, 0:T].rearrange("p (s l) -> p s l", s=NS)
                return t3[:, c, 0:T]

            def conv(buf, sbuf_, c, wname, bname, out_ap, key_in, key_out):
                S.dve(lambda e: e.tensor_scalar(out=out_ap, in0=tap(buf, sbuf_, c, 3), scalar1=P(wname, c, 4)[:, 3:4],
                                                scalar2=P(bname, c), op0=ALU.mult, op1=ALU.add),
                      r=[key_in, "pfm"], w=[key_out])
                for k in (2, 1, 0):
                    S.dve(lambda e, k=k: e.scalar_tensor_tensor(out=out_ap, in0=tap(buf, sbuf_, c, k),
                                                                scalar=P(wname, c, 4)[:, k:k + 1], in1=out_ap,
                                                                op0=ALU.mult, op1=ALU.add),
                          r=[key_in, key_out, "pfm"], w=[key_out])

            def g_lrux():
                for c in range(8):
                    pa, pk = proj(c)
                    S.act(lambda e, c=c, pa=pa: e.activation(out=new_cols(lxb, lxs, c), in_=pa_view(pa), func=AF.Copy),
                          r=[pk], w=["lx%d" % c])
                    conv(lxb, lxs, c, "LW", "LB", fm(u, c), "lx%d" % c, "u%d" % c)
                    yield

            def g_z():
                for c in range(8):
                    pa, pk = proj(16 + c)
                    S.act(lambda e, c=c, pa=pa: e.activation(out=zs[:, c, 0:T], in_=pa[:, 0:T], func=AF.Silu),
                          r=[pk], w=["zs%d" % c])
                    yield

            def g_xbc():
                for c in range(12):
                    pa, pk = proj(24 + c)
                    S.act(lambda e, c=c, pa=pa: e.activation(out=new_cols(xcb, xcs, c), in_=pa_view(pa), func=AF.Copy),
                          r=[pk], w=["xc%d" % c])
                    if c < 8:
                        conv(xcb, xcs, c, "SW", "SB", fm(cvt, c % 2), "xc%d" % c, "cv_t%d" % (c % 2))
                        S.act(lambda e, c=c: e.activation(out=xsf[:, c, 0:T], in_=cvt[:, c % 2, 0:T], func=AF.Silu),
                              r=["cv_t%d" % (c % 2)], w=["xsf%d" % c])
                    else:
                        g = (c - 8) % 2
                        dstb = Bb if c < 10 else Cb
                        nm = ("Bb%d" if c < 10 else "Cb%d") % g
                        conv(xcb, xcs, c, "SW", "SB", fm(cvt, g), "xc%d" % c, "cv_t%d" % g)
                        S.act(lambda e, g=g, dstb=dstb: e.activation(out=dstb[:, g, 0:T], in_=cvt[:, g, 0:T], func=AF.Silu),
                              r=["cv_t%d" % g], w=[nm])
                    yield
                for k in range(8):
                    S.pe(lambda e, k=k: e.matmul(pD[0:T, 0:16], lhsT=hTt[:, k, 0:T], rhs=w_in_sb[:, k, 4608:4624],
                                                 start=(k == 0), stop=(k == 7)),
                         r=["hTt", "w_in"], w=["pD"])
                S.dve(lambda e: e.tensor_tensor(out=dtr[0:T, :], in0=pD[0:T, 0:16], in1=dtb_bc[0:T, :], op=ALU.add),
                      r=["pD", "dtb"], w=["dtr"])
                S.act(lambda e: e.activation(out=dtr[0:T, :], in_=dtr[0:T, :], func=AF.Exp), r=["dtr"], w=["dtr"])
                S.act(lambda e: e.activation(out=dtt[0:T, :], in_=dtr[0:T, :], func=AF.Ln, bias=1.0), r=["dtr"], w=["dtt"])
                yield

            def g_lru(chunks, pg, kr, ki):
                for c in chunks:
                    pp = c % 2
                    S.pool(lambda e, c=c: e.tensor_copy(out=ub[:, c, 0:T], in_=u[:, c, 0:T]),
                           r=["u%d" % c], w=["ub%d" % c])
                    yield
                    S.pe(lambda e, c=c: e.matmul(pg[:, 0:T], lhsT=wa_blk[:, c, :], rhs=ub[:, c, 0:T], start=True, stop=True),
                         r=["ub%d" % c, "wa"], w=[kr])
                    S.pe(lambda e, c=c: e.matmul(pg[:, 128:128 + T], lhsT=wx_blk[:, c, :], rhs=ub[:, c, 0:T], start=True, stop=True),
                         r=["ub%d" % c, "wx"], w=[ki])
                    yield
                    S.act(lambda e, c=c, pp=pp: e.activation(out=gi[:, 2 * pp, 0:T], in_=pg[:, 0:T], func=AF.Exp, scale=-1.0, bias=nbias[:, c:c + 1]),
                          r=[kr, "nbias"], w=["rg%d" % pp])
                    S.act(lambda e, c=c, pp=pp: e.activation(out=gi[:, 2 * pp + 1, 0:T], in_=pg[:, 128:128 + T], func=AF.Exp, scale=-1.0, bias=nbias[:, 8 + c:9 + c]),
                          r=[ki, "nbias"], w=["ig%d" % pp])
                    S.act(lambda e, pp=pp: e.activation(out=gi[:, 2 * pp:2 * pp + 2, 0:T], in_=gi[:, 2 * pp:2 * pp + 2, 0:T], func=AF.Ln, bias=1.0),
                          r=["rg%d" % pp, "ig%d" % pp], w=["rg%d" % pp, "ig%d" % pp])
                    S.act(lambda e, pp=pp: e.activation(out=gi[:, 2 * pp:2 * pp + 2, 0:T], in_=gi[:, 2 * pp:2 * pp + 2, 0:T], func=AF.Exp, scale=-1.0),
                          r=["rg%d" % pp, "ig%d" % pp], w=["rg%d" % pp, "ig%d" % pp])
                    S.act(lambda e, c=c, pp=pp: e.activation(out=av[:, pp, 0:T], in_=gi[:, 2 * pp, 0:T], func=AF.Exp, scale=cfac[:, c:c + 1]),
                          r=["rg%d" % pp, "cfac"], w=["av%d" % pp])
                    S.act(lambda e, c=c, pp=pp: e.activation(out=a2[:, pp, 0:T], in_=gi[:, 2 * pp, 0:T], func=AF.Exp, scale=c2fac[:, c:c + 1]),
                          r=["rg%d" % pp, "cfac2"], w=["a2%d" % pp])
                    S.act(lambda e, pp=pp: e.activation(out=a2[:, pp, 0:T], in_=a2[:, pp, 0:T], func=AF.Ln, scale=-1.0, bias=1.0),
                          r=["a2%d" % pp], w=["a2%d" % pp])
                    S.act(lambda e, pp=pp: e.activation(out=a2[:, pp, 0:T], in_=a2[:, pp, 0:T], func=AF.Exp, scale=0.5),
                          r=["a2%d" % pp], w=["a2%d" % pp])
                    yield
                    S.dve(lambda e, c=c, pp=pp: e.tensor_tensor(out=tmpb[:, pp, 0:T], in0=gi[:, 2 * pp + 1, 0:T], in1=u[:, c, 0:T], op=ALU.mult),
                          r=["ig%d" % pp, "u%d" % c], w=["tb%d" % pp])
                    S.dve(lambda e, pp=pp: e.tensor_tensor(out=tmpb[:, pp, 0:T], in0=tmpb[:, pp, 0:T], in1=a2[:, pp, 0:T], op=ALU.mult),
                          r=["tb%d" % pp, "a2%d" % pp], w=["tb%d" % pp])
                    if samp:
                        a3 = av[:, pp, 0:T].rearrange("p (s l) -> p s l", s=NS)
                        b3v = tmpb[:, pp, 0:T].rearrange("p (s l) -> p s l", s=NS)
                        S.dve(lambda e, c=c, a3=a3: e.tensor_tensor(out=rbc[:, 0:NS], in0=a3[:, :, 0], in1=h0s[:, c, :], op=ALU.mult),
                              r=["av%d" % pp, "h0s"], w=["rbc"])
                        S.dve(lambda e, b3v=b3v: e.tensor_tensor(out=b3v[:, :, 0], in0=b3v[:, :, 0], in1=rbc[:, 0:NS], op=ALU.add),
                              r=["tb%d" % pp, "rbc"], w=["tb%d" % pp])
                        S.dve(lambda e, a3=a3: e.memset(a3[:, :, 0], 0.0), r=["rbc"], w=["av%d" % pp])
                        S.dve(lambda e, c=c, pp=pp: e.tensor_tensor_scan(out=hs[:, c, 0:T], data0=av[:, pp, 0:T], data1=tmpb[:, pp, 0:T],
                                                                         initial=0.0, op0=ALU.mult, op1=ALU.add),
                              r=["av%d" % pp, "tb%d" % pp], w=["hs%d" % c])
                        S.dve(lambda e, c=c: e.tensor_copy(out=hfin[:, c, :], in_=hs[:, c, 0:T].rearrange("p (s l) -> p s l", s=NS)[:, :, 3]),
                              r=["hs%d" % c], w=["hfin"])
                    else:
                        S.dve(lambda e, c=c, pp=pp: e.tensor_tensor_scan(out=hs[:, c, 0:T], data0=av[:, pp, 0:T], data1=tmpb[:, pp, 0:T],
                                                                         initial=hstate[:, c:c + 1], op0=ALU.mult, op1=ALU.add),
                              r=["av%d" % pp, "tb%d" % pp, "hstate"], w=["hs%d" % c])
                        S.dve(lambda e, c=c: e.tensor_copy(out=hstate[:, c:c + 1], in_=hs[:, c, T - 1:T]),
                              r=["hs%d" % c], w=["hstate"])
                    yield

            def g_gate():
                for c in range(8):
                    pp = c % 2
                    pa, pk = proj(8 + c)
                    S.act(lambda e, pp=pp, pa=pa: e.activation(out=gl[:, pp, 0:T], in_=pa[:, 0:T], func=AF.Gelu_apprx_tanh),
                          r=[pk], w=["gl%d" % pp])
                    S.pool(lambda e, c=c, pp=pp: e.tensor_tensor(out=hs[:, c, 0:T], in0=hs[:, c, 0:T], in1=gl[:, pp, 0:T], op=ALU.mult),
                           r=["hs%d" % c, "gl%d" % pp], w=["yl%d" % c, "hs%d" % c])
                    S.act(lambda e, c=c, pp=pp: e.activation(out=ysq[:, pp, 0:T], in_=hs[:, c, 0:T], func=AF.Square),
                          r=["yl%d" % c], w=["ysq%d" % pp])
                    S.pe(lambda e, c=c, pp=pp: e.matmul(pD[:, 128:128 + T], lhsT=onesb, rhs=ysq[:, pp, 0:T], start=(c == 0), stop=(c == 7)),
                         r=["ysq%d" % pp, "onesb"], w=["pDn"])
                    yield

            def norm_apply(T, eps_, src, skey, gname, dst, dkey, c0, c1, rbc, rk, pst, pk):
                S.act(lambda e: e.activation(out=rbc[:, 0:T], in_=pst, func=AF.Ln,
                                             scale=1.0 / ((c1 - c0) * 128), bias=eps_t[:, 0:1]),
                      r=[pk, "eps_t"], w=[rk])
                S.act(lambda e: e.activation(out=rbc[:, 0:T], in_=rbc[:, 0:T], func=AF.Exp, scale=-0.5), r=[rk], w=[rk])
                for c in range(c0, c1):
                    S.dve(lambda e, c=c: e.scalar_tensor_tensor(out=dst[:, c, 0:T], in0=src[:, c, 0:T], scalar=P(gname, c),
                                                                in1=rbc[:, 0:T], op0=ALU.mult, op1=ALU.mult),
                          r=[skey % c, rk, "pfm"], w=[dkey % c])

            Um = mskb[0:TS, 0:TS] if samp else Utrib
            ngm = negblk if samp else negm
            allm = mskb[0:TS, TS:2 * TS] if samp else onesb
            d4 = lambda ap: ap[:, 0:4 * T].rearrange("p (a t) -> p a t", a=4)
            Em, Dm, Mm, pC4 = d4(Emf), d4(Dmf), d4(Mmf), d4(pC)

            def g_ssd():
                for c in range(8):
                    S.pe(lambda e, c=c: e.transpose(out=pT[0:T, c * 128:(c + 1) * 128], in_=xsf[:, c, 0:T], identity=ident),
                         r=["xsf%d" % c, "cst"], w=["pT"])
                for g in range(2):
                    S.pe(lambda e, g=g: e.transpose(out=pCb[0:T, 128 + g * 128:128 + (g + 1) * 128], in_=Bb[:, g, 0:T], identity=identb),
                         r=["Bb%d" % g, "identb"], w=["pC", "pCx"])
                S.dve(lambda e: e.tensor_tensor(out=xdt[0:T, :].rearrange("p (h q) -> p h q", h=16),
                                                in0=pT[0:T, :].rearrange("p (h q) -> p h q", h=16),
                                                in1=dtt[0:T, :].unsqueeze(2).to_broadcast([T, 16, 64]), op=ALU.mult),
                      r=["pT", "dtt"], w=["xdt"])
                S.dve(lambda e: e.tensor_copy(out=BT[0:T, :], in_=pCb[0:T, 128:384]), r=["pC"], w=["BT"])
                S.dve(lambda e: e.tensor_tensor(out=da[0:T, :], in0=dtt[0:T, :], in1=a_bc[0:T, :], op=ALU.mult),
                      r=["dtt", "a_bc"], w=["da"])
                yield
                S.dve(lambda e: e.tensor_copy(out=dah[0:T, :], in_=da[0:T, :]), r=["da"], w=["dah"])
                S.dve(lambda e: e.tensor_tensor(out=dal[0:T, :], in0=da[0:T, :], in1=dah[0:T, :], op=ALU.subtract),
                      r=["da", "dah"], w=["dal"])
                for i, dx in enumerate((dah, dal)):
                    S.pe(lambda e, dx=dx, i=i: e.matmul(pC[0:T, 0:16], lhsT=Um[0:T, 0:T], rhs=dx[0:T, :], start=(i == 0), stop=(i == 1)),
                         r=["dah", "dal", "mskb"], w=["pC", "pCx"])
                for i, dx in enumerate((dah, dal)):
                    S.pe(lambda e, dx=dx, i=i: e.matmul(pC[0:T, 16:32], lhsT=allm[0:T, 0:T], rhs=dx[0:T, :], start=(i == 0), stop=(i == 1)),
                         r=["dah", "dal", "mskb", "onesb"], w=["pC", "pCx"])
                if not samp:
                    for i, dx in enumerate((dah, dal)):
                        S.pe(lambda e, dx=dx, i=i: e.matmul(pC[:, 32:48], lhsT=onesb, rhs=dx, start=(i == 0), stop=(i == 1)),
                             r=["dah", "dal", "onesb"], w=["pC", "pCx"])
                for g in range(2):
                    S.pe(lambda e, g=g: e.matmul(pC[0:T, 256 + g * 128:256 + g * 128 + T], lhsT=Bb[:, g, 0:T], rhs=Cb[:, g, 0:T],
                                                 start=True, stop=True), r=["Bb%d" % g, "Cb%d" % g], w=["pC", "pCx"])
                yield
                S.dve(lambda e: e.tensor_scalar(out=ncum[0:T, :], in0=pC[0:T, 0:16], scalar1=-1.0, scalar2=None, op0=ALU.mult),
                      r=["pC"], w=["ncum"])
                S.dve(lambda e: e.tensor_tensor(out=dte[0:T, :], in0=pC[0:T, 16:32], in1=ncum[0:T, :], op=ALU.add),
                      r=["pC", "ncum"], w=["dte"])
                if not samp:
                    S.dve(lambda e: e.tensor_copy(out=cdec, in_=pC[:, 32:48]), r=["pC"], w=["cdec"])
                S.dve(lambda e: e.tensor_copy(out=cbT[0:T, :, 0:T], in_=pC[0:T, 256:512].rearrange("p (g t) -> p g t", g=2)[:, :, 0:T]),
                      r=["pC"], w=["cbT0", "cbT1"])
                S.act(lambda e: e.activation(out=dte[0:T, :], in_=dte[0:T, :], func=AF.Exp), r=["dte"], w=["dte"])
                if not samp:
                    S.act(lambda e: e.activation(out=cdec, in_=cdec, func=AF.Exp), r=["cdec"], w=["cdec"])
                S.dve(lambda e: e.tensor_tensor(out=xdd[0:T, :].rearrange("p (h q) -> p h q", h=16),
                                                in0=xdt[0:T, :].rearrange("p (h q) -> p h q", h=16),
                                                in1=dte[0:T, :].unsqueeze(2).to_broadcast([T, 16, 64]), op=ALU.mult),
                      r=["xdt", "dte"], w=["xdd"])
                yield
                if not samp:
                    for g in range(2):
                        S.pe(lambda e, g=g: e.matmul(pO[:, g * 512:(g + 1) * 512], lhsT=BT[:, g * 128:(g + 1) * 128],
                                                     rhs=xdd[:, g * 512:(g + 1) * 512], start=True, stop=True),
                             r=["BT", "xdd"], w=["pO"])
                    S.dve(lambda e: e.tensor_tensor(out=hT.rearrange("p (h q) -> p h q", h=16),
                                                    in0=hT.rearrange("p (h q) -> p h q", h=16),
                                                    in1=cdec.unsqueeze(2).to_broadcast([128, 16, 64]), op=ALU.mult),
                          r=["hT", "cdec"], w=["hT"])
                    S.dve(lambda e: e.tensor_tensor(out=hT, in0=hT, in1=pO, op=ALU.add), r=["hT", "pO"], w=["hT"])
                    yield
                for q4 in range(4):
                    g = q4 // 2
                    for i, (dx, Wf, wk) in enumerate(((dah, Mmf, ["Mm"]), (dal, Wl, wlk))):
                        S.pool(lambda e, q4=q4, dx=dx, Wf=Wf: e.tensor_tensor(out=d4(Wf)[0:T], in0=Um[0:T, 0:T].unsqueeze(1).to_broadcast([T, 4, T]),
                                                                             in1=dx[0:T, q4 * 4:q4 * 4 + 4].unsqueeze(2).to_broadcast([T, 4, T]),
                                                                             op=ALU.mult), r=["dah", "dal", "mskb"], w=wk)
                        S.pe(lambda e, Wf=Wf, i=i: e.matmul(pC[:, 0:4 * T], lhsT=onesb[0:T, :], rhs=Wf[0:T, 0:4 * T],
                                                            start=(i == 0), stop=(i == 1)), r=wk + ["onesb"], w=["pC", "pCx"])
                    S.dve(lambda e: e.tensor_copy(out=Em, in_=pC4), r=["pC"], w=["Em"])
                    yield
                    for hh in range(4):
                        h = q4 * 4 + hh
                        S.dve(lambda e, h=h, hh=hh: e.scalar_tensor_tensor(out=Dm[0:T, hh, :], in0=Em[0:T, hh, :],
                                                                           scalar=ncum[0:T, h:h + 1], in1=ngm[0:T, 0:T],
                                                                           op0=ALU.add, op1=ALU.add),
                              r=["Em", "ncum", "cst"], w=["Dm"])
                    S.act(lambda e: e.activation(out=Em, in_=Em, func=AF.Exp), r=["Em"], w=["Em"])
                    S.act(lambda e: e.activation(out=Dm[0:T], in_=Dm[0:T], func=AF.Exp), r=["Dm"], w=["Dm"])
                    S.dve(lambda e, g=g: e.tensor_tensor(out=Mm[0:T], in0=Dm[0:T],
                                                         in1=cbT[0:T, g, 0:T].unsqueeze(1).to_broadcast([T, 4, T]), op=ALU.mult),
                          r=["Dm", "cbT%d" % g], w=["Mm"])
                    S.pool(lambda e, g=g, q4=q4: e.tensor_tensor(out=(Chs[:, q4 * 4:q4 * 4 + 4, :] if samp else Chp), in0=Em,
                                                                in1=Cb[:, g, 0:T].unsqueeze(1).to_broadcast([128, 4, T]), op=ALU.mult),
                          r=["Em", "Cb%d" % g], w=["Ch"])
                    yield
                    for hh in range(4):
                        h = q4 * 4 + hh
                        c = h // 2
                        h2 = h % 2
                        po = pT[64 * h2:64 * h2 + 64, c * 128:c * 128 + T]
                        S.pe(lambda e, h=h, hh=hh, po=po: e.matmul(po, lhsT=xdt[0:T, h * 64:(h + 1) * 64], rhs=Mm[0:T, hh, :],
                                                                   start=True, stop=samp), r=["xdt", "Mm"], w=["pT"])
                        if not samp:
                            S.pe(lambda e, h=h, hh=hh, po=po: e.matmul(po, lhsT=hTb[:, h * 64:(h + 1) * 64], rhs=Chp[:, hh, :],
                                                                       start=False, stop=True), r=["hTb", "Ch"], w=["pT"])
                    yield

            def late_outputs():
                M = T if samp else 3
                t0 = 0 if samp else 125
                if samp or last:
                    for blk, col0 in enumerate((0, 512, 3072, 3584, 4096)):
                        for k in range(8):
                            S.pe(lambda e, k=k, col0=col0: e.matmul(pO[0:M, 0:512], lhsT=hTt[:, k, t0:t0 + M],
                                                                    rhs=w_in_sb[:, k, col0:col0 + 512], start=(k == 0), stop=(k == 7)),
                                 r=["hTt", "w_in"], w=["pO"])
                        S.dve(lambda e, blk=blk: e.tensor_copy(out=stg[0:M, blk * 512:(blk + 1) * 512], in_=pO[0:M, 0:512]),
                              r=["pO"], w=["stg"] + STGW)
                if last:
                    S.dma(lambda e: e.dma_start(out=o_plc, in_=stg[0:3, 0:1024]), "o_plc", r=["stg"])
                    S.dma(lambda e: e.dma_start(out=o_psc, in_=stg[0:3, 1024:2560]), "o_psc", r=["stg"])
                if samp:
                    for s in range(NS):
                        S.dma(lambda e, s=s: e.dma_start(out=o_slc[s], in_=stg[4 * s + 1:4 * s + 4, 0:1024]), "o_slc", r=["stg"])
                        S.dma(lambda e, s=s: e.dma_start(out=o_ssc[s], in_=stg[4 * s + 1:4 * s + 4, 1024:2560]), "o_ssc", r=["stg"])
                if last:
                    S.pe(lambda e: e.transpose(out=pC[0:8, 0:128], in_=hstate, identity=ident), r=["hstate", "cst"], w=["pC", "pCx"])
                    S.act(lambda e: e.activation(out=stT[0:8, 0:128], in_=pC[0:8, 0:128], func=AF.Copy), r=["pC"], w=["stg"] + STGW)
                    S.dma(lambda e: e.dma_start(out=o_plh, in_=stT[0:8, 0:128]), "o_plh", r=["stg"])
                if samp:
                    for c in range(8):
                        S.pe(lambda e, c=c: e.transpose(out=pT[0:NS, c * 128:(c + 1) * 128], in_=hfin[:, c, :], identity=ident),
                             r=["hfin", "cst"], w=["pT"])
                    S.act(lambda e: e.activation(out=lh_in[0:NS, :], in_=pT[0:NS, :], func=AF.Copy), r=["pT"], w=["stg"] + STGW)
                    S.dma(lambda e: e.dma_start(out=o_slh, in_=lh_in[0:NS, :]), "o_slh", r=["stg"])


            def genP():
                S.dma(lambda e: e.dma_start(out=xt[0:T, :], in_=xsrc), "xt", w=["xt"])
                rms_rstd(xt, T, "xt", junk, ss, rstd)
                S.act(lambda e: e.activation(out=xn[0:T, :], in_=xt[0:T, :], func=AF.Copy, scale=rstd[0:T, 0:1]),
                      r=["xt", "rstd"], w=["xn"])
                to_fm(T, "GM", hTt, "hTt")

                if samp:
                    S.dma(lambda e: e.dma_start(out=lc_in[0:48, :], in_=st_lc), "stg", w=["stg"])
                    S.dma(lambda e: e.dma_start(out=sc_in[0:48, :], in_=st_sc), "stg", w=["stg"])
                    S.dma(lambda e: e.dma_start(out=lh_in[64:64 + NS, :], in_=st_lh), "stg", w=["stg"])
                    for c in range(8):
                        S.pe(lambda e, c=c: e.transpose(out=pC[:, 0:48], in_=lc_in[0:48, c * 128:(c + 1) * 128],
                                                        identity=ident[0:48, 0:48]), r=["stg", "cst"], w=["pC"])
                        S.act(lambda e, c=c: e.activation(out=lxs[:, c, :, 0:3],
                                                          in_=pC[:, 0:48].rearrange("p (s j) -> p s j", s=NS),
                                                          func=AF.Copy), r=["pC"], w=["lx%d" % c])
                        S.pe(lambda e, c=c: e.transpose(out=pD[:, 0:NS], in_=lh_in[64:64 + NS, c * 128:(c + 1) * 128],
                                                        identity=ident[64:64 + NS, 64:64 + NS]), r=["stg", "cst"], w=["pD"])
                        S.dve(lambda e, c=c: e.tensor_copy(out=h0s[:, c, :], in_=pD[:, 0:NS]), r=["pD"], w=["h0s"])
                    for c in range(12):
                        S.pe(lambda e, c=c: e.transpose(out=pC[:, 0:48], in_=sc_in[0:48, c * 128:(c + 1) * 128],
                                                        identity=ident[0:48, 0:48]), r=["stg", "cst"], w=["pC"])
                        S.act(lambda e, c=c: e.activation(out=xcs[:, c, :, 0:3],
                                                          in_=pC[:, 0:48].rearrange("p (s j) -> p s j", s=NS),
                                                          func=AF.Copy), r=["pC"], w=["xc%d" % c])

                yield
                yield from inter(g_lrux(), g_z())
                if not samp:
                    S.dve(lambda e: e.tensor_copy(out=lxb[:, :, 0:3], in_=lxb[:, :, 128:131]),
                          r=["lx%d" % c for c in range(8)], w=["lx%d" % c for c in range(8)])
                yield from inter(g_xbc())
                if not samp:
                    S.dve(lambda e: e.tensor_copy(out=xcb[:, :, 0:3], in_=xcb[:, :, 128:131]),
                          r=["xc%d" % c for c in range(12)], w=["xc%d" % c for c in range(12)])
                yield from inter(g_lru((0, 2, 4, 6), pA[0], "pA0", "pA0"), g_lru((1, 3, 5, 7), pA[1], "pA1", "pA1"))
                yield from inter(g_gate())
                norm_apply(T, EPS, hs, "yl%d", "GL", ynl, "ynl%d", 0, 8, rbc, "rbc", pD[:, 128:128 + T], "pDn")
                yield
                if samp:
                    late_outputs()

            def genS():
                yield from inter(g_ssd())
                if samp:
                    ssd_sample_states_prep()

                for c in range(8):
                    S.dve(lambda e, c=c: e.scalar_tensor_tensor(out=xsf[:, c, 0:T], in0=xsf[:, c, 0:T], scalar=P("DS", c),
                                                                in1=pT[:, c * 128:c * 128 + T], op0=ALU.mult, op1=ALU.add),
                          r=["pT", "xsf%d" % c, "pfm"], w=["xsf%d" % c])
                if samp:
                    S.dve(lambda e: e.tensor_tensor(out=xsf[:, :, 0:T], in0=xsf[:, :, 0:T], in1=pyo_sb, op=ALU.add),
                          r=["xsf%d" % c for c in range(8)] + ["pyo_sb"], w=["xsf%d" % c for c in range(8)])
                S.dve(lambda e: e.tensor_tensor(out=xsf[:, :, 0:T], in0=xsf[:, :, 0:T], in1=zs[:, :, 0:T], op=ALU.mult),
                      r=["xsf%d" % c for c in range(8)] + ["zs%d" % c for c in range(8)], w=["yg%d" % c for c in range(8)] + ["xsf%d" % c for c in range(8)])
                yield
                if not samp:
                    S.act(lambda e: e.activation(out=hTb, in_=hT, func=AF.Copy), r=["hT"], w=["hTb"])
                    if last:
                        for c in range(8):
                            S.pe(lambda e, c=c: e.transpose(out=pO[:, c * 128:(c + 1) * 128], in_=hT[:, c * 128:(c + 1) * 128], identity=ident),
                                 r=["hT", "cst"], w=["pO"])
                        S.dve(lambda e: e.tensor_copy(out=stT, in_=pO), r=["pO"], w=["stg"] + STGW)
                        S.dma(lambda e: e.dma_start(out=o_psh.rearrange("(c q) n -> q c n", q=128),
                                                    in_=stT.rearrange("p (c n) -> p c n", c=8)), "o_psh", r=["stg"])
                yield
                for g in range(2):
                    for c in range(4 * g, 4 * g + 4):
                        pp = c % 2
                        S.pool(lambda e, c=c, pp=pp: e.tensor_tensor(out=ysqS[:, pp, 0:T], in0=xsf[:, c, 0:T], in1=xsf[:, c, 0:T], op=ALU.mult),
                               r=["yg%d" % c], w=["ysq%s%d" % (sk, pp)])
                        S.pe(lambda e, c=c, pp=pp, g=g: e.matmul(pO[:, 0:T], lhsT=onesb, rhs=ysqS[:, pp, 0:T],
                                                                 start=(c == 4 * g), stop=(c == 4 * g + 3)),
                             r=["ysq%s%d" % (sk, pp), "onesb"], w=["pO"])
                    norm_apply(T, EPS, xsf, "yg%d", "GS", yns, "yns%d", 4 * g, 4 * g + 4, rbcS, "rbc" + sk, pO[:, 0:T], "pO")

                yield
                for nb in range(2):
                    for kc in range(16):
                        src = ynl if kc < 8 else yns
                        S.pe(lambda e, kc=kc, nb=nb, src=src: e.matmul(pO[0:T, nb * 512:(nb + 1) * 512], lhsT=src[:, kc % 8, 0:T],
                                                                       rhs=w_out_sb[:, kc, nb * 512:(nb + 1) * 512],
                                                                       start=(kc == 0), stop=(kc == 15)),
                             r=[("ynl%d" if kc < 8 else "yns%d") % (kc % 8), "w_out"], w=["pO"])
                S.dve(lambda e: e.tensor_tensor(out=xt[0:T, :], in0=pO[0:T, :], in1=xt[0:T, :], op=ALU.add),
                      r=["pO", "xt"], w=["xt"])
                S.dma(lambda e: e.dma_start(out=scr[row0:row0 + T, :], in_=xt[0:T, :]), "xnew", r=["xt"], w=["scr%d" % mt])

                if last:
                    late_outputs()

            return par, genP, genS

        def ssd_sample_states_prep():
            T = TS
            for i, dx in enumerate((dah, dal)):
                S.dve(lambda e, dx=dx, i=i: e.tensor_tensor(out=damb[0:T, i], in0=dx[0:T, :].unsqueeze(1).to_broadcast([T, NS, 16]),
                                                            in1=blki.unsqueeze(2).to_broadcast([T, NS, 16]), op=ALU.mult),
                      r=["dah", "dal", "cst"], w=["dam%d" % i])
                S.pe(lambda e, i=i: e.matmul(pD[:, 0:256], lhsT=onesb[0:T, :], rhs=damb[0:T, i].rearrange("p s h -> p (s h)"),
                                             start=(i == 0), stop=(i == 1)), r=["dam%d" % i, "onesb"], w=["pD", "pD2", "pD3", "pDn"])
            S.act(lambda e: e.activation(out=dtot.rearrange("p s h -> p (s h)"), in_=pD[:, 0:256], func=AF.Exp),
                  r=["pD"], w=["dtot"])
            S.barrier()
            dtotP = Dmf[:, 0:NS * 8].rearrange("p (s c) -> p s c", s=NS)
            for h2 in range(2):
                S.dve(lambda e, h2=h2: e.tensor_copy(out=dtotP[64 * h2:64 * h2 + 64],
                                                     in_=dtot[64 * h2:64 * h2 + 64].rearrange("p s (c two) -> p s c two", two=2)[:, :, :, h2]),
                      r=["dtot"], w=["dtotP"])
            h0in_b = [h0in, u]
            hout_b = [hout, hs]
            h0b_b = [hTb, ub.rearrange("p c t -> p (c t)")]
            h0Tb_b = [lrut[:, 0:512].bitcast(BF16), lrut[:, 512:1024].bitcast(BF16)]
            xdm_b = [hT[:, 0:512].bitcast(BF16), hT[:, 512:1024].bitcast(BF16)]
            pTr_b = [pC.bitcast(BF16), pD.bitcast(BF16)]
            pTk = [["pC"], ["pD"]]
            for s in range(NS):
                q = s % 2
                hi, ho, h0b, hb, xdm, pTr, tk = h0in_b[q], hout_b[q], h0b_b[q], h0Tb_b[q], xdm_b[q], pTr_b[q], pTk[q]
                S.dma(lambda e, s=s, hi=hi: e.dma_start(out=hi, in_=st_sh[s].rearrange("(c q) n -> q c n", q=128)),
                      "h0in%d" % q, w=["h0in%d" % q], q="pool")
                S.act(lambda e, hi=hi, h0b=h0b: e.activation(out=h0b, in_=hi.rearrange("p c n -> p (c n)"), func=AF.Copy),
                      r=["h0in%d" % q], w=["h0b%d" % q])
                for c in range(8):
                    S.pe(lambda e, c=c, h0b=h0b, pTr=pTr: e.transpose(out=pTr[:, c * 128:(c + 1) * 128], in_=h0b[:, c * 128:(c + 1) * 128],
                                                                      identity=identb), r=["h0b%d" % q, "identb"], w=tk)
                S.dve(lambda e, hb=hb, pTr=pTr: e.tensor_copy(out=hb, in_=pTr[:, 0:1024]), r=tk, w=["h0Tb%d" % q])
                for h in range(16):
                    S.pe(lambda e, h=h, s=s, hb=hb: e.matmul(pA[0][64 * (h % 2):64 * (h % 2) + 64, (h // 2) * TS + 4 * s:(h // 2) * TS + 4 * s + 4],
                                                             lhsT=hb[:, h * 64:(h + 1) * 64], rhs=Chs[:, h, 4 * s:4 * s + 4],
                                                             start=True, stop=True),
                         r=["h0Tb%d" % q, "Ch"], w=["pA0"])
                S.dve(lambda e, s=s, xdm=xdm: e.tensor_scalar(out=xdm[0:T, :], in0=xdd[0:T, :], scalar1=blki[:, s:s + 1], scalar2=None, op0=ALU.mult),
                      r=["xdd", "cst"], w=["xdm%d" % q])
                for c in range(8):
                    S.pe(lambda e, c=c, xdm=xdm: e.matmul(pO[:, c * 128:(c + 1) * 128], lhsT=xdm[0:T, c * 128:(c + 1) * 128],
                                                          rhs=BT[0:T, (c // 4) * 128:(c // 4 + 1) * 128], start=True, stop=True),
                         r=["xdm%d" % q, "BT"], w=["pO"])
                for c in range(8):
                    S.dve(lambda e, c=c, s=s, hi=hi, ho=ho: e.scalar_tensor_tensor(out=ho[:, c, :], in0=hi[:, c, :], scalar=dtotP[:, s, c:c + 1],
                                                                                   in1=pO[:, c * 128:(c + 1) * 128], op0=ALU.mult, op1=ALU.add),
                          r=["h0in%d" % q, "dtotP", "pO"], w=["hout%d" % q])
                S.dma(lambda e, s=s, ho=ho: e.dma_start(out=o_ssh[s].rearrange("(c q) n -> q c n", q=128), in_=ho),
                      "hout%d" % q, r=["hout%d" % q])
            S.act(lambda e: e.activation(out=pyo_sb.rearrange("p c t -> p (c t)"), in_=pA[0][:, 0:8 * TS], func=AF.Copy),
                  r=["pA0"], w=["pyo_sb"])

        S.pool(lambda e: e.memset(lxb, 0.0), w=["lx%d" % c for c in range(8)])
        S.pool(lambda e: e.memset(xcb, 0.0), w=["xc%d" % c for c in range(12)])

        def drive(g_, par):
            S.ctx = par
            try:
                next(g_)
                return True
            except StopIteration:
                return False
            finally:
                S.ctx = None

        tiles = [mixer_tile(mt, False) for mt in range(NT)]
        RATIO = 3
        par0, gP0, _ = tiles[0]
        g = gP0()
        while drive(g, par0):
            pass
        for n in range(NT):
            par, _, gS = tiles[n]
            gs = gS()
            alive_s = True
            alive_p = False
            if n + 1 < NT:
                parn, gPn, _ = tiles[n + 1]
                gp = gPn()
                alive_p = True
            while alive_s or alive_p:
                for _ in range(RATIO):
                    if alive_p:
                        alive_p = drive(gp, parn)
                if alive_s:
                    alive_s = drive(gs, par)
        S.barrier()
        if SAMP:
            pars, gPs, gSs = mixer_tile(SEQ // 128, True)
            for g in (gPs(), gSs()):
                while drive(g, pars):
                    pass

        S.barrier()
        ptr[0] = base0
        w_up_sb = b3(8, DFF)
        w_dn_sb = b3(32, D)
        if MLP:
            k_wup = load_w(w_up_sb, w_up, 8, DFF, "w_up")
            k_wdn = load_w(w_dn_sb, w_down, 32, D, "w_dn")
        T2 = 256
        xt2 = [[f32(D), f32(D)], [f32(D), f32(D)]]
        xn2 = f32(D)
        ss2 = [f32(4), f32(4)]
        rstd2 = [f32(4), f32(4)]
        mT = [b3(8, T2), b3(8, T2)]
        actb = b3(32, T2)
        rl = [f32(T2), f32(T2)]
        yout = f32(D)
        gfin_bc = f32(D)
        S.dma(lambda e: e.dma_start(out=gfin_bc, in_=gfin_d.partition_broadcast(128)), "gfin", w=["gfin"])
        pDN = [PS[:, 3072:4096], PS[:, 2048:3072]]
        pDNk = [["pO"], ["pC", "pD"]]

        def mlp_front(ti, r0, T):
            q = ti % 2
            nsub = (T + 127) // 128
            for j in range(nsub):
                Tj = min(128, T - j * 128)
                xk = "xt2_%d_%d" % (q, j)
                S.dma(lambda e, j=j, Tj=Tj: e.dma_start(out=xt2[q][j][0:Tj, :], in_=scr[r0 + j * 128:r0 + j * 128 + Tj, :]),
                      xk, r=["scr%d" % ((r0 + j * 128) // 128)], w=[xk])
                rms_rstd(xt2[q][j], Tj, xk, xn2, ss2[0], rstd2[0], "2")
                S.act(lambda e, j=j, Tj=Tj: e.activation(out=xn2[0:Tj, :], in_=xt2[q][j][0:Tj, :], func=AF.Copy, scale=rstd2[0][0:Tj, 0:1]),
                      r=[xk, "rstd2"], w=["xn2"])
                for k in range(8):
                    S.pe(lambda e, k=k, Tj=Tj: e.transpose(out=pT[:, k * 128:k * 128 + Tj], in_=xn2[0:Tj, k * 128:(k + 1) * 128],
                                                           identity=ident[0:Tj, 0:Tj]), r=["xn2", "cst"], w=["pT"])
                S.dve(lambda e, j=j, Tj=Tj: e.tensor_tensor(
                    out=mT[q][:, :, j * 128:j * 128 + Tj], in0=pT.rearrange("p (k t) -> p k t", k=8)[:, :, 0:Tj],
                    in1=P("GP", 0, 8).unsqueeze(2).to_broadcast([128, 8, Tj]), op=ALU.mult),
                    r=["pT", "pfm"], w=["mT%d" % q])
            yield
            for f in range(32):
                pa = pA[f % 2]
                for k in range(8):
                    S.pe(lambda e, k=k, f=f, pa=pa: e.matmul(pa[:, 0:T], lhsT=w_up_sb[:, k, f * 128:(f + 1) * 128], rhs=mT[q][:, k, 0:T],
                                                             start=(k == 0), stop=(k == 7)),
                         r=["mT%d" % q, "w_up"], w=["pA%d" % (f % 2)])
                S.act(lambda e, f=f, pa=pa: e.activation(out=rl[f % 2][:, 0:T], in_=pa[:, 0:T], func=AF.Relu),
                      r=["pA%d" % (f % 2)], w=["rl%d" % (f % 2)])
                S.pool(lambda e, f=f: e.tensor_tensor(out=actb[:, f, 0:T], in0=rl[f % 2][:, 0:T], in1=rl[f % 2][:, 0:T], op=ALU.mult),
                       r=["rl%d" % (f % 2)], w=["act%d" % f])
                yield

        def mlp_back(ti, r0, T):
            q = ti % 2
            nsub = (T + 127) // 128
            for f in range(32):
                for j in range(nsub):
                    Tj = min(128, T - j * 128)
                    for nb in range(2):
                        S.pe(lambda e, f=f, nb=nb, j=j, Tj=Tj: e.matmul(pDN[j][0:Tj, nb * 512:(nb + 1) * 512],
                                                                        lhsT=actb[:, f, j * 128:j * 128 + Tj],
                                                                        rhs=w_dn_sb[:, f, nb * 512:(nb + 1) * 512],
                                                                        start=(f == 0), stop=(f == 31)),
                             r=["act%d" % f, "w_dn"], w=pDNk[j])
                yield
            for j in range(nsub):
                Tj = min(128, T - j * 128)
                xk = "xt2_%d_%d" % (q, j)
                S.dve(lambda e, j=j, Tj=Tj: e.tensor_tensor(out=xt2[q][j][0:Tj, :], in0=pDN[j][0:Tj, :], in1=xt2[q][j][0:Tj, :], op=ALU.add),
                      r=pDNk[j] + [xk], w=[xk])
                rms_rstd(xt2[q][j], Tj, xk, yout, ss2[1], rstd2[1], "2b", jkey="yout")
                S.dve(lambda e, j=j, Tj=Tj: e.scalar_tensor_tensor(out=yout[0:Tj, :], in0=xt2[q][j][0:Tj, :], scalar=rstd2[1][0:Tj, 0:1],
                                                                   in1=gfin_bc[0:Tj, :], op0=ALU.mult, op1=ALU.mult),
                      r=[xk, "rstd2b", "gfin"], w=["yout"])
                rr = r0 + j * 128
                if rr < SEQ:
                    S.dma(lambda e, rr=rr, Tj=Tj: e.dma_start(out=y_p[rr:rr + Tj, :], in_=yout[0:Tj, :]), "yout", r=["yout"])
                else:
                    S.dma(lambda e, Tj=Tj: e.dma_start(out=y_s, in_=yout[0:Tj, :]), "yout", r=["yout"])
                yield

        if MLP:
            jobs = [(t * T2, T2) for t in range(NT * 128 // T2)]
            if SAMP:
                jobs.append((SEQ, TS))
            for _ in mlp_front(0, *jobs[0]):
                pass
            for ti in range(len(jobs)):
                gb = mlp_back(ti, *jobs[ti])
                gf = mlp_front(ti + 1, *jobs[ti + 1]) if ti + 1 < len(jobs) else iter(())
                ab = af = True
                while ab or af:
                    if ab:
                        ab = next(gb, "END") != "END"
                    if af:
                        af = next(gf, "END") != "END"

        S.emit()
    return nc


_CACHE = {}


def _consts():
    c = np.zeros((128, NCST), np.float32)
    i = np.arange(128)
    c[:, CI:CI + 128] = np.eye(128, dtype=np.float32)
    c[:, CU:CU + 128] = (i[:, None] <= i[None, :]).astype(np.float32)
    c[:, CN:CN + 128] = np.where(i[:, None] <= i[None, :], 0.0, NEG).astype(np.float32)
    c[:, CO:CO + 128] = 1.0
    j = np.arange(TS)
    same = (j[:, None] // 4) == (j[None, :] // 4)
    caus = j[:, None] <= j[None, :]
    c[0:TS, CUB:CUB + TS] = (same & caus).astype(np.float32)
    c[0:TS, CNB:CNB + TS] = np.where(same & caus, 0.0, NEG).astype(np.float32)
    c[0:TS, CBM:CBM + TS] = same.astype(np.float32)
    c[0:TS, CBI:CBI + NS] = ((j[:, None] // 4) == np.arange(NS)[None, :]).astype(np.float32)
    return c


def _fm(v, nch):
    return np.ascontiguousarray(np.asarray(v, np.float32).reshape(nch, 128).T)


def kernel(x_prompt, x_sample, state_lru_conv, state_lru_h, state_ssd_conv, state_ssd_h,
           g_mix, w_in, lru_conv_w, lru_conv_b, w_a, b_a, w_x, b_x, lam, g_lru_out,
           ssd_conv_w, ssd_conv_b, dt_bias, a_log, d_skip, g_ssd_out, w_out,
           g_mlp, w_up, w_down, g_final):
    f = lambda a: np.ascontiguousarray(np.asarray(a, np.float32))
    if "nc" not in _CACHE:
        _CACHE["nc"] = build_program()
    nc = _CACHE["nc"]
    pfm = np.zeros((128, NPAR), np.float32)
    lw = np.asarray(lru_conv_w[0], np.float32)
    pfm[:, PC["LW"]:PC["LW"] + 32] = lw.reshape(4, 8, 128).transpose(2, 1, 0).reshape(128, 32)
    pfm[:, PC["LB"]:PC["LB"] + 8] = _fm(lru_conv_b[0], 8)
    pfm[:, PC["BA"]:PC["BA"] + 8] = _fm(np.asarray(b_a[0]).reshape(-1), 8)
    pfm[:, PC["BX"]:PC["BX"] + 8] = _fm(np.asarray(b_x[0]).reshape(-1), 8)
    pfm[:, PC["LAM"]:PC["LAM"] + 8] = _fm(lam[0], 8)
    pfm[:, PC["GL"]:PC["GL"] + 8] = _fm(g_lru_out[0], 8)
    sw = np.asarray(ssd_conv_w[0], np.float32)
    pfm[:, PC["SW"]:PC["SW"] + 48] = sw.reshape(4, 12, 128).transpose(2, 1, 0).reshape(128, 48)
    pfm[:, PC["SB"]:PC["SB"] + 12] = _fm(ssd_conv_b[0], 12)
    pfm[:, PC["DS"]:PC["DS"] + 8] = _fm(np.repeat(np.asarray(d_skip[0], np.float32), 64), 8)
    pfm[:, PC["GS"]:PC["GS"] + 8] = _fm(g_ssd_out[0], 8)
    pfm[:, PC["GM"]:PC["GM"] + 8] = _fm(g_mix[0], 8)
    pfm[:, PC["GP"]:PC["GP"] + 8] = _fm(g_mlp[0], 8)
    cst = _consts()
    shared = {
        "w_in": f(w_in[0]), "w_out": f(w_out[0]), "w_up": f(w_up[0]), "w_down": f(w_down[0]),
        "w_a": f(w_a[0]), "w_x": f(w_x[0]), "pfm": pfm, "cst": cst,
        "dt_bias": f(dt_bias[0]), "a_log": f(a_log[0]), "g_final": f(g_final),
    }
    in_maps = []
    for b in range(NCORES):
        sl = slice(NS * b, NS * (b + 1))
        m = dict(shared)
        m["xp"] = f(x_prompt[b])
        m["xs"] = f(np.asarray(x_sample[sl]).reshape(TS, D))
        m["st_lc"] = f(np.asarray(state_lru_conv[0, sl]).reshape(NS * 3, D))
        m["st_lh"] = f(state_lru_h[0, sl])
        m["st_sc"] = f(np.asarray(state_ssd_conv[0, sl]).reshape(NS * 3, XBC))
        m["st_sh"] = f(np.asarray(state_ssd_h[0, sl]).reshape(NS, 1024, 128))
        in_maps.append(m)
    res = run_bass_kernel_spmd(nc, in_maps, core_ids=list(range(NCORES)))
    R = res.results
    cat = lambda k: np.stack([np.asarray(R[b][k], np.float32) for b in range(NCORES)])
    y_prompt = cat("y_p")
    y_sample = cat("y_s").reshape(NCORES * NS, 4, D)
    p_lc = cat("o_plc")[None]
    p_lh = cat("o_plh").reshape(NCORES, D)[None]
    p_sc = cat("o_psc")[None]
    p_sh = cat("o_psh").reshape(NCORES, 16, 64, 128)[None]
    s_lc = cat("o_slc").reshape(NCORES * NS, 3, D)[None]
    s_lh = cat("o_slh").reshape(NCORES * NS, D)[None]
    s_sc = cat("o_ssc").reshape(NCORES * NS, 3, XBC)[None]
    s_sh = cat("o_ssh").reshape(NCORES * NS, 16, 64, 128)[None]
    return (y_prompt, y_sample, p_lc, p_lh, p_sc, p_sh, s_lc, s_lh, s_sc, s_sh)
```

```python
import math
from contextlib import ExitStack

import numpy as np
import concourse.bass as bass
import concourse.mybir as mybir
from concourse.bass_utils import run_bass_kernel_spmd

F32 = mybir.dt.float32
BF16 = mybir.dt.bfloat16
AF = mybir.ActivationFunctionType
ALU = mybir.AluOpType

NCORES = 8
D = 1024
SEQ = 2048
NS = 16
TS = 64
XBC = 1536
INP = 4624
DFF = 4096
EPS = 1e-6
NEG = -30000.0

import re as _re

ENGS = ("pe", "act", "dve", "pool", "sp")
SAME_ENGINE_SYNC = {"pe": False, "act": True, "dve": True, "pool": True, "sp": False}


class Op:
    __slots__ = ("eng", "fn", "deps", "marked", "count", "dma_key", "dma_val")

    def __init__(self, eng, fn, dma_key=None):
        self.eng = eng
        self.fn = fn
        self.deps = ()
        self.marked = False
        self.count = 0
        self.dma_key = dma_key
        self.dma_val = 0


class Sched:
    def __init__(self, nc):
        self.nc = nc
        self.ops = {e: [] for e in ENGS}
        self.last_w = {}
        self.readers = {}
        self.dma_cnt = {}
        self.pending = {}
        self.ctx = None
        self.since_bar = []

    ALIAS = {"pC": "b4", "pCx": "b4", "pD": "b5", "pD2": "b5", "pD3": "b5", "pDn": "b5",
             "pDcb0": "b5", "pDcb1": "b5", "pT": "b01", "pO": "b67", "pA0": "b2", "pA1": "b3"}

    PSUM_KEYS = {"b01", "b2", "b3", "b4", "b5", "b67"}

    PAR_RE = _re.compile(r"^(xt|dtt|xnew)$|^(xsf|zs|Bb|Cb|ynl|yg)\d+$")

    def _k(self, k):
        k = self.ALIAS.get(k, k)
        if self.ctx is not None and self.PAR_RE.match(k):
            return k + "#" + str(self.ctx)
        return k

    def add(self, eng, fn, reads=(), writes=(), dma_key=None):
        reads = [self._k(k) for k in reads]
        writes = [self._k(k) for k in writes]
        if dma_key is not None and self.ctx is not None and self.PAR_RE.match(dma_key):
            dma_key = dma_key + "#" + str(self.ctx)
        op = Op(eng, fn, dma_key)
        deps = []
        seen = set()

        def dep(o):
            if o is not None and o is not op and id(o) not in seen:
                seen.add(id(o))
                deps.append(o)

        if self.pending.get(eng):
            for o in self.pending[eng]:
                dep(o)
            self.pending[eng] = []
        for b in reads:
            dep(self.last_w.get(b))
            if b in self.PSUM_KEYS:
                for r in self.readers.get(b, ()):
                    if r.eng != eng:
                        dep(r)
        for b in writes:
            dep(self.last_w.get(b))
            for r in self.readers.get(b, ()):
                dep(r)
        for b in reads:
            self.readers.setdefault(b, []).append(op)
        for b in writes:
            self.last_w[b] = op
            self.readers[b] = []
        op.deps = deps
        if dma_key is not None:
            self.dma_cnt[dma_key] = self.dma_cnt.get(dma_key, 0) + 16
            op.dma_val = self.dma_cnt[dma_key]
        self.ops[eng].append(op)
        self.since_bar.append(op)
        return op

    def barrier(self):
        ops = []
        for e in ENGS:
            comp = [o for o in self.ops[e] if o.dma_key is None]
            if comp:
                ops.append(comp[-1])
        last_dma = {}
        for o in self.since_bar:
            if o.dma_key is not None:
                last_dma[o.dma_key] = o
        ops.extend(last_dma.values())
        for e in ENGS:
            self.pending.setdefault(e, []).extend(ops)
        self.since_bar = []

    def pe(self, fn, r=(), w=()):
        return self.add("pe", fn, r, w)

    def act(self, fn, r=(), w=()):
        return self.add("act", fn, r, w)

    def dve(self, fn, r=(), w=()):
        return self.add("dve", fn, r, w)

    def pool(self, fn, r=(), w=()):
        return self.add("pool", fn, r, w)

    def dma(self, fn, key, r=(), w=(), q="sp"):
        return self.add(q, fn, r, w, dma_key=key)

    def emit(self):
        nc = self.nc
        for e in ENGS:
            for op in self.ops[e]:
                for d in op.deps:
                    if d.dma_key is None:
                        if d.eng == op.eng and not SAME_ENGINE_SYNC[d.eng]:
                            continue
                        d.marked = True
        for e in ENGS:
            c = 0
            for op in self.ops[e]:
                if op.dma_key is None and op.marked:
                    c += 1
                    op.count = c
        with ExitStack() as st:
            esem = {e: st.enter_context(nc.semaphore("es_" + e)) for e in ENGS}
            dsem = {}
            for k in self.dma_cnt:
                dsem[k] = st.enter_context(nc.semaphore("ds_%d" % len(dsem)))
            block = st.enter_context(nc.Block())

            def run(ename, eng):
                seen = {}
                for op in self.ops[ename]:
                    need = {}
                    for d in op.deps:
                        if d.dma_key is not None:
                            key = ("d", d.dma_key)
                            val = d.dma_val
                            sem = dsem[d.dma_key]
                        else:
                            if d.eng == ename and not SAME_ENGINE_SYNC[ename]:
                                continue
                            key = ("e", d.eng)
                            val = d.count
                            sem = esem[d.eng]
                        if key not in need or need[key][1] < val:
                            need[key] = (sem, val)
                    for key, (sem, val) in need.items():
                        if seen.get(key, 0) >= val:
                            continue
                        seen[key] = val
                        eng.wait_ge(sem, val)
                    ins = op.fn(eng)
                    if op.dma_key is not None:
                        ins.then_inc(dsem[op.dma_key], 16)
                    elif op.marked:
                        ins.then_inc(esem[ename], 1)
                if ename == "sp":
                    for k, v in self.dma_cnt.items():
                        eng.wait_ge(dsem[k], v)

            @block.sync
            def _(e):
                run("sp", e)

            @block.tensor
            def _(e):
                run("pe", e)

            @block.scalar
            def _(e):
                run("act", e)

            @block.vector
            def _(e):
                run("dve", e)

            @block.gpsimd
            def _(e):
                run("pool", e)


PC = {}
_o = 0
for _n, _w in (("LW", 32), ("LB", 8), ("BA", 8), ("BX", 8), ("LAM", 8), ("GL", 8), ("SW", 48),
               ("SB", 12), ("DS", 8), ("GS", 8), ("GM", 8), ("GP", 8)):
    PC[_n] = _o
    _o += _w
NPAR = _o
CI, CU, CN, CO, CUB, CNB, CBM, CBI = 0, 128, 256, 384, 512, 576, 640, 704
NCST = 720


def build_program(NT=SEQ // 128, SAMP=True, MLP=True, DBG=False, STAGE=9):
    nc = bass.Bass("TRN2", target_bir_lowering=False)
    S = Sched(nc)

    def din(name, shape):
        return nc.dram_tensor(name, list(shape), F32, kind="ExternalInput").ap()

    def dout(name, shape):
        return nc.dram_tensor(name, list(shape), F32, kind="ExternalOutput").ap()

    xp = din("xp", (SEQ, D))
    xs = din("xs", (TS, D))
    st_lc = din("st_lc", (NS * 3, D))
    st_lh = din("st_lh", (NS, D))
    st_sc = din("st_sc", (NS * 3, XBC))
    st_sh = din("st_sh", (NS, 1024, 128))
    w_in = din("w_in", (D, INP))
    w_out = din("w_out", (2 * D, D))
    w_up = din("w_up", (D, DFF))
    w_down = din("w_down", (DFF, D))
    w_a = din("w_a", (16, 64, 64))
    w_x = din("w_x", (16, 64, 64))
    pfm_d = din("pfm", (128, NPAR))
    cst_d = din("cst", (128, NCST))
    dtb_d = din("dt_bias", (16,))
    alog_d = din("a_log", (16,))
    gfin_d = din("g_final", (D,))

    y_p = dout("y_p", (SEQ, D))
    y_s = dout("y_s", (TS, D))
    o_plc = dout("o_plc", (3, D))
    o_plh = dout("o_plh", (8, 128))
    o_psc = dout("o_psc", (3, XBC))
    o_psh = dout("o_psh", (1024, 128))
    o_slc = dout("o_slc", (NS, 3, D))
    o_slh = dout("o_slh", (NS, D))
    o_ssc = dout("o_ssc", (NS, 3, XBC))
    o_ssh = dout("o_ssh", (NS, 1024, 128))
    scr = nc.dram_tensor("scr", [SEQ + TS, D], F32, kind=("ExternalOutput" if DBG else "Internal")).ap()

    st = ExitStack()
    with st:
        RW = 53200
        R = st.enter_context(nc.sbuf_tensor("R", [128, RW], F32))
        PS = st.enter_context(nc.psum_tensor("PS", [128, 4096], F32))
        ptr = [0]

        def alloc(nwords):
            a = ptr[0]
            ptr[0] += (nwords + 7) // 8 * 8
            pass
            return a

        def f32(n):
            a = alloc(n)
            return R[:, a:a + n]

        def bf(n):
            w = (n + 1) // 2
            a = alloc(w)
            return R[:, a:a + w].bitcast(BF16)[:, 0:n]

        def f3(c, t):
            return f32(c * t).rearrange("p (c t) -> p c t", c=c)

        def b3(c, t):
            return bf(c * t).rearrange("p (c t) -> p c t", c=c)

        def bank(b, n=512):
            return PS[:, 512 * b:512 * b + n]

        pT = PS[:, 0:1024]
        pTb = pT.bitcast(BF16)
        pA = [bank(2), bank(3)]
        pC = bank(4)
        pCb = pC.bitcast(BF16)
        pD = bank(5)
        pO = PS[:, 3072:4096]

        cst = f32(NCST)
        pfm = f32(NPAR)
        dtb_bc = f32(16)
        a_bc = f32(16)
        identb = bf(128)
        onesb = bf(128)
        Utrib = bf(128)
        mskb = bf(3 * TS)
        dah = bf(16)
        dal = bf(16)
        cfac = f32(8)
        c2fac = f32(8)
        tiny = f32(8)
        mhalf = f32(4)
        eps_t = f32(4)
        nbias = f32(16)
        wa_blk = b3(8, 128)
        wx_blk = b3(8, 128)
        hstate = f32(8)
        hT = f32(1024)
        hTb = bf(1024)

        ident = cst[:, CI:CI + 128]
        Utri = cst[:, CU:CU + 128]
        negm = cst[:, CN:CN + 128]
        onesf = cst[:, CO:CO + 128]
        Ublk = cst[0:TS, CUB:CUB + TS]
        negblk = cst[0:TS, CNB:CNB + TS]
        blkm = cst[0:TS, CBM:CBM + TS]
        blki = cst[0:TS, CBI:CBI + NS]

        S.dma(lambda e: e.dma_start(out=cst, in_=cst_d), "cst", w=["cst"])
        S.dma(lambda e: e.dma_start(out=pfm, in_=pfm_d), "pfm", w=["pfm"])
        S.dma(lambda e: e.dma_start(out=dtb_bc, in_=dtb_d.partition_broadcast(128)), "dtb", w=["dtb"])
        S.dma(lambda e: e.dma_start(out=a_bc, in_=alog_d.partition_broadcast(128)), "alog", w=["a_bc"])
        S.dve(lambda e: e.tensor_copy(out=identb, in_=ident), r=["cst"], w=["identb"])
        S.dve(lambda e: e.tensor_copy(out=onesb, in_=onesf), r=["cst"], w=["onesb"])
        S.dve(lambda e: e.tensor_copy(out=Utrib, in_=Utri), r=["cst"], w=["mskb"])
        S.dve(lambda e: e.tensor_copy(out=mskb[0:TS, 0:TS], in_=Ublk), r=["cst"], w=["mskb"])
        S.dve(lambda e: e.tensor_copy(out=mskb[0:TS, TS:2 * TS], in_=blkm), r=["cst"], w=["mskb"])
        S.pool(lambda e: e.memset(mhalf, -0.5), w=["mhalf"])
        S.pool(lambda e: e.memset(eps_t, EPS), w=["eps_t"])
        S.dve(lambda e: e.tensor_scalar(out=nbias[:, 0:8], in0=pfm[:, PC["BA"]:PC["BA"] + 8], scalar1=-1.0, scalar2=None, op0=ALU.mult), r=["pfm"], w=["nbias"])
        S.dve(lambda e: e.tensor_scalar(out=nbias[:, 8:16], in0=pfm[:, PC["BX"]:PC["BX"] + 8], scalar1=-1.0, scalar2=None, op0=ALU.mult), r=["pfm", "nbias"], w=["nbias"])
        S.pool(lambda e: e.memset(hstate, 0.0), w=["hstate"])
        S.pool(lambda e: e.memset(hT, 0.0), w=["hT"])
        S.pool(lambda e: e.memset(hTb, 0.0), w=["hTb"])
        S.pool(lambda e: e.memset(wa_blk, 0.0), w=["wa"])
        S.pool(lambda e: e.memset(wx_blk, 0.0), w=["wx"])
        S.act(lambda e: e.activation(out=a_bc, in_=a_bc, func=AF.Exp), r=["a_bc"], w=["a_bc"])
        S.dve(lambda e: e.tensor_scalar(out=a_bc, in0=a_bc, scalar1=-1.0, scalar2=None, op0=ALU.mult), r=["a_bc"], w=["a_bc"])
        lam = pfm[:, PC["LAM"]:PC["LAM"] + 8]
        S.act(lambda e: e.activation(out=tiny, in_=lam, func=AF.Exp, scale=-1.0), r=["pfm"], w=["tiny"])
        S.act(lambda e: e.activation(out=tiny, in_=tiny, func=AF.Ln, bias=1.0), r=["tiny"], w=["tiny"])
        S.dve(lambda e: e.tensor_scalar(out=cfac, in0=tiny, scalar1=-8.0, scalar2=None, op0=ALU.mult), r=["tiny"], w=["cfac"])
        S.dve(lambda e: e.tensor_scalar(out=c2fac, in0=tiny, scalar1=-16.0, scalar2=None, op0=ALU.mult), r=["tiny"], w=["cfac2"])
        for (wd, blk, nm) in ((w_a, wa_blk, "wa"), (w_x, wx_blk, "wx")):
            v = wd.rearrange("(c h) i j -> h i c j", h=2)
            for h2 in range(2):
                S.dma(lambda e, v=v, blk=blk, h2=h2: e.dma_start(
                    out=blk[64 * h2:64 * h2 + 64, :, 64 * h2:64 * h2 + 64], in_=v[h2]),
                    nm + str(h2), w=[nm], q="pool")

        base0 = ptr[0]

        def load_w(dst3, src2, nk, ncol, name, step=2048):
            sv = src2.rearrange("(k p) n -> p k n", p=128)
            for r_, c0 in enumerate(range(0, ncol, step)):
                c1 = min(ncol, c0 + step)
                for k in range(nk):
                    S.dma(lambda e, k=k, c0=c0, c1=c1: e.dma_start(out=dst3[:, k, c0:c1], in_=sv[:, k, c0:c1]),
                          "%s_%d" % (name, r_), w=(["%s_%d" % (name, r_)] if k == nk - 1 else []), q="pool")
            return name

        w_in_sb = b3(8, INP)
        w_out_sb = b3(16, D)
        if STAGE >= 1:
            k_win = load_w(w_in_sb, w_in, 8, INP, "w_in")
            k_wout = load_w(w_out_sb, w_out, 16, D, "w_out")

        xt = f32(D)
        xn = f32(D)
        junk = xn
        ss = f32(4)
        rstd = f32(4)
        hTt = b3(8, 128)
        lxb = f3(8, 131)
        xcb = f3(12, 131)
        sreg = f32(20 * NS * 7)
        lxs = sreg[:, 0:8 * NS * 7].rearrange("p (c s l) -> p c s l", c=8, s=NS)
        xcs = sreg[:, 8 * NS * 7:20 * NS * 7].rearrange("p (c s l) -> p c s l", c=12, s=NS)
        gl = f3(2, 128)
        zs = f3(8, 128)
        u = f3(8, 128)
        ub = b3(8, 128)
        lrut = f32(1280)
        gi = lrut[:, 0:512].rearrange("p (c t) -> p c t", c=4)
        av = lrut[:, 512:768].rearrange("p (c t) -> p c t", c=2)
        a2 = lrut[:, 768:1024].rearrange("p (c t) -> p c t", c=2)
        tmpb = lrut[:, 1024:1280].rearrange("p (c t) -> p c t", c=2)
        hs = f3(8, 128)
        ysq = b3(2, 128)
        rbc = f32(128)
        ynl = b3(8, 128)
        yns = b3(8, 128)
        xsf = f3(8, 128)
        Bb = b3(2, 128)
        Cb = b3(2, 128)
        dtr = f32(16)
        dtt = f32(16)
        da = f32(16)
        ncum = f32(16)
        dte = f32(16)
        cdec = f32(16)
        xdt = bf(1024)
        xdd = bf(1024)
        BT = bf(256)
        cbT = f3(2, 128)
        Dmf = f32(512)
        Emf = f32(512)
        Mmf = bf(512)
        Chf = bf(1024)
        Chp = Chf[:, 0:512].rearrange("p (a t) -> p a t", a=4)
        Chs = Chf.rearrange("p (a t) -> p a t", a=16)
        cvt = f3(2, 128)
        stg = f32(2560)
        stT = stg[:, 0:1024]
        lc_in = stg[:, 0:1024]
        sc_in = stg[:, 1024:2560]
        lh_in = stg[:, 0:1024]
        h0in = lxb.rearrange("p c t -> p (c t)")[:, 0:1024].rearrange("p (c t) -> p c t", c=8)
        h0Tb = hTb
        Bm = bf(256)
        pyo_f = f32(8 * TS)
        pyo_sb = pyo_f.rearrange("p (c t) -> p c t", c=8)
        damb = bf(2 * NS * 16).rearrange("p (i s h) -> p i s h", i=2, s=NS)
        dtot = f3(NS, 16)
        hnew = hT
        hout = xcb.rearrange("p c t -> p (c t)")[:, 0:1024].rearrange("p (c t) -> p c t", c=8)
        h0s = f3(8, NS)
        hfin = f3(8, NS)

        bf3 = lambda ap, c: ap.bitcast(BF16).rearrange("p (c t) -> p c t", c=c)
        xt_b = [xt, stg[:, 0:1024]]
        ynl_b = [ynl, bf3(stg[:, 1024:1536], 8)]
        Bb_b = [Bb, bf3(stg[:, 1536:1664], 2)]
        Cb_b = [Cb, bf3(stg[:, 1664:1792], 2)]
        dtt_b = [dtt, stg[:, 1792:1808]]
        xsf_b = [xsf, sreg[:, 0:1024].rearrange("p (c t) -> p c t", c=8)]
        zs_b = [zs, sreg[:, 1024:2048].rearrange("p (c t) -> p c t", c=8)]
        Wl_S = pyo_f[:, 0:256].bitcast(BF16)
        ysq_S = bf3(pyo_f[:, 256:384], 2)
        rbc_S = pyo_f[:, 384:512]
        STGW = [k + "#1" for k in ["xt", "dtt"] + ["ynl%d" % c for c in range(8)] + ["Bb0", "Bb1", "Cb0", "Cb1"]]
        STG_ALIAS = ["xt", "dtt"] + ["ynl%d" % c for c in range(8)] + ["Bb0", "Bb1", "Cb0", "Cb1"]

        def P(name, c=None, w=1):
            o = PC[name] + (0 if c is None else c * w)
            return pfm[:, o:o + w]

        def rms_rstd(xtile, T, keyx, junk, ss, rstd, sfx="", jkey=None):
            S.act(lambda e: e.activation(out=junk[0:T, :], in_=xtile[0:T, :], func=AF.Square, accum_out=ss[0:T, 0:1]),
                  r=[keyx], w=[jkey or ("xn" + sfx), "ss" + sfx])
            S.act(lambda e: e.activation(out=ss[0:T, 0:1], in_=ss[0:T, 0:1], func=AF.Ln, scale=1.0 / D, bias=eps_t[0:T, 0:1]),
                  r=["ss" + sfx, "eps_t"], w=["ss" + sfx])
            S.act(lambda e: e.activation(out=rstd[0:T, 0:1], in_=ss[0:T, 0:1], func=AF.Exp, scale=-0.5),
                  r=["ss" + sfx], w=["rstd" + sfx])

        pAA = PS[:, 1024:2048]

        def to_fm(T, gname, dst, dkey):
            for k in range(8):
                S.pe(lambda e, k=k: e.transpose(out=pAA[:, k * 128:k * 128 + T], in_=xn[0:T, k * 128:(k + 1) * 128],
                                                identity=ident[0:T, 0:T]), r=["xn", "cst"], w=["pA0", "pA1"])
            S.dve(lambda e: e.tensor_tensor(
                out=dst[:, :, 0:T], in0=pAA.rearrange("p (k t) -> p k t", k=8)[:, :, 0:T],
                in1=P(gname, 0, 8).unsqueeze(2).to_broadcast([128, 8, T]), op=ALU.mult),
                r=["pA0", "pA1", "pfm"], w=[dkey])

        def mixer_tile(mt, samp):
            T = TS if samp else 128
            row0 = SEQ if samp else mt * 128
            xsrc = xs if samp else xp[mt * 128:(mt + 1) * 128, :]
            last = (not samp) and mt == NT - 1
            par = 0 if samp else (NT - 1 - mt) % 2
            xt, ynl, Bb, Cb, dtt, xsf, zs = (xt_b[par], ynl_b[par], Bb_b[par], Cb_b[par], dtt_b[par], xsf_b[par], zs_b[par])
            Wl = cvt.rearrange("p a t -> p (a t)").bitcast(BF16) if samp else Wl_S
            wlk = ["cv_t0", "cv_t1"] if samp else ["WlS"]
            ysqS = ysq if samp else ysq_S
            rbcS = rbc if samp else rbc_S
            sk = "" if samp else "S"

            def inter(*gens):
                gens = list(gens)
                while gens:
                    for g_ in list(gens):
                        try:
                            next(g_)
                        except StopIteration:
                            gens.remove(g_)
                        yield

            pcnt = [0]

            def proj(ci):
                i = pcnt[0] % 2
                pcnt[0] += 1
                pa = pA[i]
                for k in range(8):
                    S.pe(lambda e, k=k: e.matmul(pa[:, 0:T], lhsT=w_in_sb[:, k, ci * 128:(ci + 1) * 128],
                                                 rhs=hTt[:, k, 0:T], start=(k == 0), stop=(k == 7)),
                         r=["hTt", "w_in_%d" % ((ci * 128) // 2048)], w=["pA%d" % i])
                return pa, "pA%d" % i

            def new_cols(buf, sbuf_, c):
                if samp:
                    return sbuf_[:, c, :, 3:7]
                return buf[:, c, 3:131]

            def pa_view(pa):
                if samp:
                    return pa[:, 0:T].rearrange("p (s l) -> p s l", s=NS)
                return pa[:, 0:T]

            def tap(buf, sbuf_, c, k):
                if samp:
                    return sbuf_[:, c, :, k:k + 4]
                return buf[:, c, k:k + 128]

            def fm(t3, c):
                if samp:
                    return t3[:, c, 0:T].rearrange("p (s l) -> p s l", s=NS)
                return t3[:, c, 0:T]

            def conv(buf, sbuf_, c, wname, bname, out_ap, key_in, key_out):
                S.dve(lambda e: e.tensor_scalar(out=out_ap, in0=tap(buf, sbuf_, c, 3), scalar1=P(wname, c, 4)[:, 3:4],
                                                scalar2=P(bname, c), op0=ALU.mult, op1=ALU.add),
                      r=[key_in, "pfm"], w=[key_out])
                for k in (2, 1, 0):
                    S.dve(lambda e, k=k: e.scalar_tensor_tensor(out=out_ap, in0=tap(buf, sbuf_, c, k),
                                                                scalar=P(wname, c, 4)[:, k:k + 1], in1=out_ap,
                                                                op0=ALU.mult, op1=ALU.add),
                          r=[key_in, key_out, "pfm"], w=[key_out])

            def g_lrux():
                for c in range(8):
                    pa, pk = proj(c)
                    S.act(lambda e, c=c, pa=pa: e.activation(out=new_cols(lxb, lxs, c), in_=pa_view(pa), func=AF.Copy),
                          r=[pk], w=["lx%d" % c])
                    conv(lxb, lxs, c, "LW", "LB", fm(u, c), "lx%d" % c, "u%d" % c)
                    yield

            def g_z():
                for c in range(8):
                    pa, pk = proj(16 + c)
                    S.act(lambda e, c=c, pa=pa: e.activation(out=zs[:, c, 0:T], in_=pa[:, 0:T], func=AF.Silu),
                          r=[pk], w=["zs%d" % c])
                    yield

            def g_xbc():
                for c in range(12):
                    pa, pk = proj(24 + c)
                    S.act(lambda e, c=c, pa=pa: e.activation(out=new_cols(xcb, xcs, c), in_=pa_view(pa), func=AF.Copy),
                          r=[pk], w=["xc%d" % c])
                    if c < 8:
                        conv(xcb, xcs, c, "SW", "SB", fm(cvt, c % 2), "xc%d" % c, "cv_t%d" % (c % 2))
                        S.act(lambda e, c=c: e.activation(out=xsf[:, c, 0:T], in_=cvt[:, c % 2, 0:T], func=AF.Silu),
                              r=["cv_t%d" % (c % 2)], w=["xsf%d" % c])
                    else:
                        g = (c - 8) % 2
                        dstb = Bb if c < 10 else Cb
                        nm = ("Bb%d" if c < 10 else "Cb%d") % g
                        conv(xcb, xcs, c, "SW", "SB", fm(cvt, g), "xc%d" % c, "cv_t%d" % g)
                        S.act(lambda e, g=g, dstb=dstb: e.activation(out=dstb[:, g, 0:T], in_=cvt[:, g, 0:T], func=AF.Silu),
                              r=["cv_t%d" % g], w=[nm])
                    yield
                for k in range(8):
                    S.pe(lambda e, k=k: e.matmul(pD[0:T, 0:16], lhsT=hTt[:, k, 0:T], rhs=w_in_sb[:, k, 4608:4624],
                                                 start=(k == 0), stop=(k == 7)),
                         r=["hTt", "w_in_2"], w=["pD"])
                S.dve(lambda e: e.tensor_tensor(out=dtr[0:T, :], in0=pD[0:T, 0:16], in1=dtb_bc[0:T, :], op=ALU.add),
                      r=["pD", "dtb"], w=["dtr"])
                S.act(lambda e: e.activation(out=dtr[0:T, :], in_=dtr[0:T, :], func=AF.Exp), r=["dtr"], w=["dtr"])
                S.act(lambda e: e.activation(out=dtt[0:T, :], in_=dtr[0:T, :], func=AF.Ln, bias=1.0), r=["dtr"], w=["dtt"])
                yield

            def g_lru(chunks, pg, kr, ki):
                for c in chunks:
                    pp = c % 2
                    S.pool(lambda e, c=c: e.tensor_copy(out=ub[:, c, 0:T], in_=u[:, c, 0:T]),
                           r=["u%d" % c], w=["ub%d" % c])
                    yield
                    S.pe(lambda e, c=c: e.matmul(pg[:, 0:T], lhsT=wa_blk[:, c, :], rhs=ub[:, c, 0:T], start=True, stop=True),
                         r=["ub%d" % c, "wa"], w=[kr])
                    S.pe(lambda e, c=c: e.matmul(pg[:, 128:128 + T], lhsT=wx_blk[:, c, :], rhs=ub[:, c, 0:T], start=True, stop=True),
                         r=["ub%d" % c, "wx"], w=[ki])
                    yield
                    S.act(lambda e, c=c, pp=pp: e.activation(out=gi[:, 2 * pp, 0:T], in_=pg[:, 0:T], func=AF.Exp, scale=-1.0, bias=nbias[:, c:c + 1]),
                          r=[kr, "nbias"], w=["rg%d" % pp])
                    S.act(lambda e, c=c, pp=pp: e.activation(out=gi[:, 2 * pp + 1, 0:T], in_=pg[:, 128:128 + T], func=AF.Exp, scale=-1.0, bias=nbias[:, 8 + c:9 + c]),
                          r=[ki, "nbias"], w=["ig%d" % pp])
                    S.act(lambda e, pp=pp: e.activation(out=gi[:, 2 * pp:2 * pp + 2, 0:T], in_=gi[:, 2 * pp:2 * pp + 2, 0:T], func=AF.Ln, bias=1.0),
                          r=["rg%d" % pp, "ig%d" % pp], w=["rg%d" % pp, "ig%d" % pp])
                    S.act(lambda e, pp=pp: e.activation(out=gi[:, 2 * pp:2 * pp + 2, 0:T], in_=gi[:, 2 * pp:2 * pp + 2, 0:T], func=AF.Exp, scale=-1.0),
                          r=["rg%d" % pp, "ig%d" % pp], w=["rg%d" % pp, "ig%d" % pp])
                    S.act(lambda e, c=c, pp=pp: e.activation(out=av[:, pp, 0:T], in_=gi[:, 2 * pp, 0:T], func=AF.Exp, scale=cfac[:, c:c + 1]),
                          r=["rg%d" % pp, "cfac"], w=["av%d" % pp])
                    S.act(lambda e, c=c, pp=pp: e.activation(out=a2[:, pp, 0:T], in_=gi[:, 2 * pp, 0:T], func=AF.Exp, scale=c2fac[:, c:c + 1]),
                          r=["rg%d" % pp, "cfac2"], w=["a2%d" % pp])
                    S.act(lambda e, pp=pp: e.activation(out=a2[:, pp, 0:T], in_=a2[:, pp, 0:T], func=AF.Ln, scale=-1.0, bias=1.0),
                          r=["a2%d" % pp], w=["a2%d" % pp])
                    S.act(lambda e, pp=pp: e.activation(out=a2[:, pp, 0:T], in_=a2[:, pp, 0:T], func=AF.Exp, scale=0.5),
                          r=["a2%d" % pp], w=["a2%d" % pp])
                    yield
                    S.dve(lambda e, c=c, pp=pp: e.tensor_tensor(out=tmpb[:, pp, 0:T], in0=gi[:, 2 * pp + 1, 0:T], in1=u[:, c, 0:T], op=ALU.mult),
                          r=["ig%d" % pp, "u%d" % c], w=["tb%d" % pp])
                    S.dve(lambda e, pp=pp: e.tensor_tensor(out=tmpb[:, pp, 0:T], in0=tmpb[:, pp, 0:T], in1=a2[:, pp, 0:T], op=ALU.mult),
                          r=["tb%d" % pp, "a2%d" % pp], w=["tb%d" % pp])
                    if samp:
                        a3 = av[:, pp, 0:T].rearrange("p (s l) -> p s l", s=NS)
                        b3v = tmpb[:, pp, 0:T].rearrange("p (s l) -> p s l", s=NS)
                        S.dve(lambda e, c=c, a3=a3: e.tensor_tensor(out=rbc[:, 0:NS], in0=a3[:, :, 0], in1=h0s[:, c, :], op=ALU.mult),
                              r=["av%d" % pp, "h0s"], w=["rbc"])
                        S.dve(lambda e, b3v=b3v: e.tensor_tensor(out=b3v[:, :, 0], in0=b3v[:, :, 0], in1=rbc[:, 0:NS], op=ALU.add),
                              r=["tb%d" % pp, "rbc"], w=["tb%d" % pp])
                        S.dve(lambda e, a3=a3: e.memset(a3[:, :, 0], 0.0), r=["rbc"], w=["av%d" % pp])
                        S.dve(lambda e, c=c, pp=pp: e.tensor_tensor_scan(out=hs[:, c, 0:T], data0=av[:, pp, 0:T], data1=tmpb[:, pp, 0:T],
                                                                         initial=0.0, op0=ALU.mult, op1=ALU.add),
                              r=["av%d" % pp, "tb%d" % pp], w=["hs%d" % c])
                        S.dve(lambda e, c=c: e.tensor_copy(out=hfin[:, c, :], in_=hs[:, c, 0:T].rearrange("p (s l) -> p s l", s=NS)[:, :, 3]),
                              r=["hs%d" % c], w=["hfin"])
                    else:
                        S.dve(lambda e, c=c, pp=pp: e.tensor_tensor_scan(out=hs[:, c, 0:T], data0=av[:, pp, 0:T], data1=tmpb[:, pp, 0:T],
                                                                         initial=hstate[:, c:c + 1], op0=ALU.mult, op1=ALU.add),
                              r=["av%d" % pp, "tb%d" % pp, "hstate"], w=["hs%d" % c])
                        S.dve(lambda e, c=c: e.tensor_copy(out=hstate[:, c:c + 1], in_=hs[:, c, T - 1:T]),
                              r=["hs%d" % c], w=["hstate"])
                    yield

            def g_gate():
                for c in range(8):
                    pp = c % 2
                    pa, pk = proj(8 + c)
                    S.act(lambda e, pp=pp, pa=pa: e.activation(out=gl[:, pp, 0:T], in_=pa[:, 0:T], func=AF.Gelu_apprx_tanh),
                          r=[pk], w=["gl%d" % pp])
                    S.pool(lambda e, c=c, pp=pp: e.tensor_tensor(out=hs[:, c, 0:T], in0=hs[:, c, 0:T], in1=gl[:, pp, 0:T], op=ALU.mult),
                           r=["hs%d" % c, "gl%d" % pp], w=["yl%d" % c, "hs%d" % c])
                    S.act(lambda e, c=c, pp=pp: e.activation(out=ysq[:, pp, 0:T], in_=hs[:, c, 0:T], func=AF.Square),
                          r=["yl%d" % c], w=["ysq%d" % pp])
                    S.pe(lambda e, c=c, pp=pp: e.matmul(pD[:, 128:128 + T], lhsT=onesb, rhs=ysq[:, pp, 0:T], start=(c == 0), stop=(c == 7)),
                         r=["ysq%d" % pp, "onesb"], w=["pDn"])
                    yield

            def norm_apply(T, eps_, src, skey, gname, dst, dkey, c0, c1, rbc, rk, pst, pk):
                S.act(lambda e: e.activation(out=rbc[:, 0:T], in_=pst, func=AF.Ln,
                                             scale=1.0 / ((c1 - c0) * 128), bias=eps_t[:, 0:1]),
                      r=[pk, "eps_t"], w=[rk])
                S.act(lambda e: e.activation(out=rbc[:, 0:T], in_=rbc[:, 0:T], func=AF.Exp, scale=-0.5), r=[rk], w=[rk])
                for c in range(c0, c1):
                    S.dve(lambda e, c=c: e.scalar_tensor_tensor(out=dst[:, c, 0:T], in0=src[:, c, 0:T], scalar=P(gname, c),
                                                                in1=rbc[:, 0:T], op0=ALU.mult, op1=ALU.mult),
                          r=[skey % c, rk, "pfm"], w=[dkey % c])

            Um = mskb[0:TS, 0:TS] if samp else Utrib
            ngm = negblk if samp else negm
            allm = mskb[0:TS, TS:2 * TS] if samp else onesb
            d4 = lambda ap: ap[:, 0:4 * T].rearrange("p (a t) -> p a t", a=4)
            Em, Dm, Mm, pC4 = d4(Emf), d4(Dmf), d4(Mmf), d4(pC)

            def g_ssd():
                for c in range(8):
                    S.pe(lambda e, c=c: e.transpose(out=pT[0:T, c * 128:(c + 1) * 128], in_=xsf[:, c, 0:T], identity=ident),
                         r=["xsf%d" % c, "cst"], w=["pT"])
                for g in range(2):
                    S.pe(lambda e, g=g: e.transpose(out=pCb[0:T, 128 + g * 128:128 + (g + 1) * 128], in_=Bb[:, g, 0:T], identity=identb),
                         r=["Bb%d" % g, "identb"], w=["pC", "pCx"])
                S.dve(lambda e: e.tensor_tensor(out=xdt[0:T, :].rearrange("p (h q) -> p h q", h=16),
                                                in0=pT[0:T, :].rearrange("p (h q) -> p h q", h=16),
                                                in1=dtt[0:T, :].unsqueeze(2).to_broadcast([T, 16, 64]), op=ALU.mult),
                      r=["pT", "dtt"], w=["xdt"])
                S.act(lambda e: e.activation(out=BT[0:T, :], in_=pCb[0:T, 128:384], func=AF.Copy), r=["pC"], w=["BT"])
                S.dve(lambda e: e.tensor_tensor(out=da[0:T, :], in0=dtt[0:T, :], in1=a_bc[0:T, :], op=ALU.mult),
                      r=["dtt", "a_bc"], w=["da"])
                yield
                S.dve(lambda e: e.tensor_copy(out=dah[0:T, :], in_=da[0:T, :]), r=["da"], w=["dah"])
                S.dve(lambda e: e.tensor_tensor(out=dal[0:T, :], in0=da[0:T, :], in1=dah[0:T, :], op=ALU.subtract),
                      r=["da", "dah"], w=["dal"])
                for i, dx in enumerate((dah, dal)):
                    S.pe(lambda e, dx=dx, i=i: e.matmul(pC[0:T, 0:16], lhsT=Um[0:T, 0:T], rhs=dx[0:T, :], start=(i == 0), stop=(i == 1)),
                         r=["dah", "dal", "mskb"], w=["pC", "pCx"])
                for i, dx in enumerate((dah, dal)):
                    S.pe(lambda e, dx=dx, i=i: e.matmul(pC[0:T, 16:32], lhsT=allm[0:T, 0:T], rhs=dx[0:T, :], start=(i == 0), stop=(i == 1)),
                         r=["dah", "dal", "mskb", "onesb"], w=["pC", "pCx"])
                if not samp:
                    for i, dx in enumerate((dah, dal)):
                        S.pe(lambda e, dx=dx, i=i: e.matmul(pC[:, 32:48], lhsT=onesb, rhs=dx, start=(i == 0), stop=(i == 1)),
                             r=["dah", "dal", "onesb"], w=["pC", "pCx"])
                for g in range(2):
                    S.pe(lambda e, g=g: e.matmul(pC[0:T, 256 + g * 128:256 + g * 128 + T], lhsT=Bb[:, g, 0:T], rhs=Cb[:, g, 0:T],
                                                 start=True, stop=True), r=["Bb%d" % g, "Cb%d" % g], w=["pC", "pCx"])
                yield
                S.dve(lambda e: e.tensor_scalar(out=ncum[0:T, :], in0=pC[0:T, 0:16], scalar1=-1.0, scalar2=None, op0=ALU.mult),
                      r=["pC"], w=["ncum"])
                S.dve(lambda e: e.tensor_tensor(out=dte[0:T, :], in0=pC[0:T, 16:32], in1=ncum[0:T, :], op=ALU.add),
                      r=["pC", "ncum"], w=["dte"])
                if not samp:
                    S.dve(lambda e: e.tensor_copy(out=cdec, in_=pC[:, 32:48]), r=["pC"], w=["cdec"])
                S.dve(lambda e: e.tensor_copy(out=cbT[0:T, :, 0:T], in_=pC[0:T, 256:512].rearrange("p (g t) -> p g t", g=2)[:, :, 0:T]),
                      r=["pC"], w=["cbT0", "cbT1"])
                S.act(lambda e: e.activation(out=dte[0:T, :], in_=dte[0:T, :], func=AF.Exp), r=["dte"], w=["dte"])
                if not samp:
                    S.act(lambda e: e.activation(out=cdec, in_=cdec, func=AF.Exp), r=["cdec"], w=["cdec"])
                S.dve(lambda e: e.tensor_tensor(out=xdd[0:T, :].rearrange("p (h q) -> p h q", h=16),
                                                in0=xdt[0:T, :].rearrange("p (h q) -> p h q", h=16),
                                                in1=dte[0:T, :].unsqueeze(2).to_broadcast([T, 16, 64]), op=ALU.mult),
                      r=["xdt", "dte"], w=["xdd"])
                yield
                if not samp:
                    for g in range(2):
                        S.pe(lambda e, g=g: e.matmul(pO[:, g * 512:(g + 1) * 512], lhsT=BT[:, g * 128:(g + 1) * 128],
                                                     rhs=xdd[:, g * 512:(g + 1) * 512], start=True, stop=True),
                             r=["BT", "xdd"], w=["pO"])
                    S.dve(lambda e: e.tensor_tensor(out=hT.rearrange("p (h q) -> p h q", h=16),
                                                    in0=hT.rearrange("p (h q) -> p h q", h=16),
                                                    in1=cdec.unsqueeze(2).to_broadcast([128, 16, 64]), op=ALU.mult),
                          r=["hT", "cdec"], w=["hT"])
                    S.dve(lambda e: e.tensor_tensor(out=hT, in0=hT, in1=pO, op=ALU.add), r=["hT", "pO"], w=["hT"])
                    yield
                for q4 in range(4):
                    g = q4 // 2
                    for i, (dx, Wf, wk) in enumerate(((dah, Mmf, ["Mm"]), (dal, Wl, wlk))):
                        S.pool(lambda e, q4=q4, dx=dx, Wf=Wf: e.tensor_tensor(out=d4(Wf)[0:T], in0=Um[0:T, 0:T].unsqueeze(1).to_broadcast([T, 4, T]),
                                                                             in1=dx[0:T, q4 * 4:q4 * 4 + 4].unsqueeze(2).to_broadcast([T, 4, T]),
                                                                             op=ALU.mult), r=["dah", "dal", "mskb"], w=wk)
                        S.pe(lambda e, Wf=Wf, i=i: e.matmul(pC[:, 0:4 * T], lhsT=onesb[0:T, :], rhs=Wf[0:T, 0:4 * T],
                                                            start=(i == 0), stop=(i == 1)), r=wk + ["onesb"], w=["pC", "pCx"])
                    S.dve(lambda e: e.tensor_copy(out=Em, in_=pC4), r=["pC"], w=["Em"])
                    yield
                    for hh in range(4):
                        h = q4 * 4 + hh
                        S.dve(lambda e, h=h, hh=hh: e.scalar_tensor_tensor(out=Dm[0:T, hh, :], in0=Em[0:T, hh, :],
                                                                           scalar=ncum[0:T, h:h + 1], in1=ngm[0:T, 0:T],
                                                                           op0=ALU.add, op1=ALU.add),
                              r=["Em", "ncum", "cst"], w=["Dm"])
                    S.act(lambda e: e.activation(out=Em, in_=Em, func=AF.Exp), r=["Em"], w=["Em"])
                    S.act(lambda e: e.activation(out=Dm[0:T], in_=Dm[0:T], func=AF.Exp), r=["Dm"], w=["Dm"])
                    S.dve(lambda e, g=g: e.tensor_tensor(out=Mm[0:T], in0=Dm[0:T],
                                                         in1=cbT[0:T, g, 0:T].unsqueeze(1).to_broadcast([T, 4, T]), op=ALU.mult),
                          r=["Dm", "cbT%d" % g], w=["Mm"])
                    S.pool(lambda e, g=g, q4=q4: e.tensor_tensor(out=(Chs[:, q4 * 4:q4 * 4 + 4, :] if samp else Chp), in0=Em,
                                                                in1=Cb[:, g, 0:T].unsqueeze(1).to_broadcast([128, 4, T]), op=ALU.mult),
                          r=["Em", "Cb%d" % g], w=["Ch"])
                    yield
                    for hh in range(4):
                        h = q4 * 4 + hh
                        c = h // 2
                        h2 = h % 2
                        po = pT[64 * h2:64 * h2 + 64, c * 128:c * 128 + T]
                        S.pe(lambda e, h=h, hh=hh, po=po: e.matmul(po, lhsT=xdt[0:T, h * 64:(h + 1) * 64], rhs=Mm[0:T, hh, :],
                                                                   start=True, stop=samp), r=["xdt", "Mm"], w=["pT"])
                        if not samp:
                            S.pe(lambda e, h=h, hh=hh, po=po: e.matmul(po, lhsT=hTb[:, h * 64:(h + 1) * 64], rhs=Chp[:, hh, :],
                                                                       start=False, stop=True), r=["hTb", "Ch"], w=["pT"])
                    yield

            def late_outputs():
                M = T if samp else 3
                t0 = 0 if samp else 125
                if samp or last:
                    for blk, col0 in enumerate((0, 512, 3072, 3584, 4096)):
                        for k in range(8):
                            S.pe(lambda e, k=k, col0=col0: e.matmul(pO[0:M, 0:512], lhsT=hTt[:, k, t0:t0 + M],
                                                                    rhs=w_in_sb[:, k, col0:col0 + 512], start=(k == 0), stop=(k == 7)),
                                 r=["hTt", "w_in_0", "w_in_1", "w_in_2"], w=["pO"])
                        S.dve(lambda e, blk=blk: e.tensor_copy(out=stg[0:M, blk * 512:(blk + 1) * 512], in_=pO[0:M, 0:512]),
                              r=["pO"], w=["stg"] + STGW)
                if last:
                    S.dma(lambda e: e.dma_start(out=o_plc, in_=stg[0:3, 0:1024]), "o_plc", r=["stg"])
                    S.dma(lambda e: e.dma_start(out=o_psc, in_=stg[0:3, 1024:2560]), "o_psc", r=["stg"])
                if samp:
                    for s in range(NS):
                        S.dma(lambda e, s=s: e.dma_start(out=o_slc[s], in_=stg[4 * s + 1:4 * s + 4, 0:1024]), "o_slc", r=["stg"])
                        S.dma(lambda e, s=s: e.dma_start(out=o_ssc[s], in_=stg[4 * s + 1:4 * s + 4, 1024:2560]), "o_ssc", r=["stg"])
                if last:
                    S.pe(lambda e: e.transpose(out=pC[0:8, 0:128], in_=hstate, identity=ident), r=["hstate", "cst"], w=["pC", "pCx"])
                    S.act(lambda e: e.activation(out=stT[0:8, 0:128], in_=pC[0:8, 0:128], func=AF.Copy), r=["pC"], w=["stg"] + STGW)
                    S.dma(lambda e: e.dma_start(out=o_plh, in_=stT[0:8, 0:128]), "o_plh", r=["stg"])
                if samp:
                    for c in range(8):
                        S.pe(lambda e, c=c: e.transpose(out=pT[0:NS, c * 128:(c + 1) * 128], in_=hfin[:, c, :], identity=ident),
                             r=["hfin", "cst"], w=["pT"])
                    S.act(lambda e: e.activation(out=lh_in[0:NS, :], in_=pT[0:NS, :], func=AF.Copy), r=["pT"], w=["stg"] + STGW)
                    S.dma(lambda e: e.dma_start(out=o_slh, in_=lh_in[0:NS, :]), "o_slh", r=["stg"])


            def genP():
                S.dma(lambda e: e.dma_start(out=xt[0:T, :], in_=xsrc), "xt", w=["xt"])
                rms_rstd(xt, T, "xt", junk, ss, rstd)
                S.act(lambda e: e.activation(out=xn[0:T, :], in_=xt[0:T, :], func=AF.Copy, scale=rstd[0:T, 0:1]),
                      r=["xt", "rstd"], w=["xn"])
                to_fm(T, "GM", hTt, "hTt")

                if samp:
                    S.dma(lambda e: e.dma_start(out=lc_in[0:48, :], in_=st_lc), "stg", w=["stg"])
                    S.dma(lambda e: e.dma_start(out=sc_in[0:48, :], in_=st_sc), "stg", w=["stg"])
                    S.dma(lambda e: e.dma_start(out=lh_in[64:64 + NS, :], in_=st_lh), "stg", w=["stg"])
                    for c in range(8):
                        S.pe(lambda e, c=c: e.transpose(out=pC[:, 0:48], in_=lc_in[0:48, c * 128:(c + 1) * 128],
                                                        identity=ident[0:48, 0:48]), r=["stg", "cst"], w=["pC"])
                        S.act(lambda e, c=c: e.activation(out=lxs[:, c, :, 0:3],
                                                          in_=pC[:, 0:48].rearrange("p (s j) -> p s j", s=NS),
                                                          func=AF.Copy), r=["pC"], w=["lx%d" % c])
                        S.pe(lambda e, c=c: e.transpose(out=pD[:, 0:NS], in_=lh_in[64:64 + NS, c * 128:(c + 1) * 128],
                                                        identity=ident[64:64 + NS, 64:64 + NS]), r=["stg", "cst"], w=["pD"])
                        S.dve(lambda e, c=c: e.tensor_copy(out=h0s[:, c, :], in_=pD[:, 0:NS]), r=["pD"], w=["h0s"])
                    for c in range(12):
                        S.pe(lambda e, c=c: e.transpose(out=pC[:, 0:48], in_=sc_in[0:48, c * 128:(c + 1) * 128],
                                                        identity=ident[0:48, 0:48]), r=["stg", "cst"], w=["pC"])
                        S.act(lambda e, c=c: e.activation(out=xcs[:, c, :, 0:3],
                                                          in_=pC[:, 0:48].rearrange("p (s j) -> p s j", s=NS),
                                                          func=AF.Copy), r=["pC"], w=["xc%d" % c])

                yield
                yield from inter(g_lrux(), g_z())
                if not samp:
                    S.dve(lambda e: e.tensor_copy(out=lxb[:, :, 0:3], in_=lxb[:, :, 128:131]),
                          r=["lx%d" % c for c in range(8)], w=["lx%d" % c for c in range(8)])
                yield from inter(g_xbc())
                if not samp:
                    S.dve(lambda e: e.tensor_copy(out=xcb[:, :, 0:3], in_=xcb[:, :, 128:131]),
                          r=["xc%d" % c for c in range(12)], w=["xc%d" % c for c in range(12)])
                yield from inter(g_lru((0, 2, 4, 6), pA[0], "pA0", "pA0"), g_lru((1, 3, 5, 7), pA[1], "pA1", "pA1"))
                yield from inter(g_gate())
                norm_apply(T, EPS, hs, "yl%d", "GL", ynl, "ynl%d", 0, 8, rbc, "rbc", pD[:, 128:128 + T], "pDn")
                yield
                if samp:
                    late_outputs()

            def genS():
                yield from inter(g_ssd())
                if samp:
                    ssd_sample_states_prep()

                for c in range(8):
                    S.dve(lambda e, c=c: e.scalar_tensor_tensor(out=xsf[:, c, 0:T], in0=xsf[:, c, 0:T], scalar=P("DS", c),
                                                                in1=pT[:, c * 128:c * 128 + T], op0=ALU.mult, op1=ALU.add),
                          r=["pT", "xsf%d" % c, "pfm"], w=["xsf%d" % c])
                if samp:
                    S.dve(lambda e: e.tensor_tensor(out=xsf[:, :, 0:T], in0=xsf[:, :, 0:T], in1=pyo_sb, op=ALU.add),
                          r=["xsf%d" % c for c in range(8)] + ["pyo_sb"], w=["xsf%d" % c for c in range(8)])
                S.dve(lambda e: e.tensor_tensor(out=xsf[:, :, 0:T], in0=xsf[:, :, 0:T], in1=zs[:, :, 0:T], op=ALU.mult),
                      r=["xsf%d" % c for c in range(8)] + ["zs%d" % c for c in range(8)], w=["yg%d" % c for c in range(8)] + ["xsf%d" % c for c in range(8)])
                yield
                if not samp:
                    S.act(lambda e: e.activation(out=hTb, in_=hT, func=AF.Copy), r=["hT"], w=["hTb"])
                    if last:
                        for c in range(8):
                            S.pe(lambda e, c=c: e.transpose(out=pO[:, c * 128:(c + 1) * 128], in_=hT[:, c * 128:(c + 1) * 128], identity=ident),
                                 r=["hT", "cst"], w=["pO"])
                        S.dve(lambda e: e.tensor_copy(out=stT, in_=pO), r=["pO"], w=["stg"] + STGW)
                        S.dma(lambda e: e.dma_start(out=o_psh.rearrange("(c q) n -> q c n", q=128),
                                                    in_=stT.rearrange("p (c n) -> p c n", c=8)), "o_psh", r=["stg"])
                yield
                for g in range(2):
                    for c in range(4 * g, 4 * g + 4):
                        pp = c % 2
                        S.pool(lambda e, c=c, pp=pp: e.tensor_tensor(out=ysqS[:, pp, 0:T], in0=xsf[:, c, 0:T], in1=xsf[:, c, 0:T], op=ALU.mult),
                               r=["yg%d" % c], w=["ysq%s%d" % (sk, pp)])
                        S.pe(lambda e, c=c, pp=pp, g=g: e.matmul(pO[:, 0:T], lhsT=onesb, rhs=ysqS[:, pp, 0:T],
                                                                 start=(c == 4 * g), stop=(c == 4 * g + 3)),
                             r=["ysq%s%d" % (sk, pp), "onesb"], w=["pO"])
                    norm_apply(T, EPS, xsf, "yg%d", "GS", yns, "yns%d", 4 * g, 4 * g + 4, rbcS, "rbc" + sk, pO[:, 0:T], "pO")

                yield
                for nb in range(2):
                    for kc in range(16):
                        src = ynl if kc < 8 else yns
                        S.pe(lambda e, kc=kc, nb=nb, src=src: e.matmul(pO[0:T, nb * 512:(nb + 1) * 512], lhsT=src[:, kc % 8, 0:T],
                                                                       rhs=w_out_sb[:, kc, nb * 512:(nb + 1) * 512],
                                                                       start=(kc == 0), stop=(kc == 15)),
                             r=[("ynl%d" if kc < 8 else "yns%d") % (kc % 8), "w_out_0"], w=["pO"])
                S.dve(lambda e: e.tensor_tensor(out=xt[0:T, :], in0=pO[0:T, :], in1=xt[0:T, :], op=ALU.add),
                      r=["pO", "xt"], w=["xt"])
                S.dma(lambda e: e.dma_start(out=scr[row0:row0 + T, :], in_=xt[0:T, :]), "xnew", r=["xt"], w=["scr%d" % mt])

                if last:
                    late_outputs()

            return par, genP, genS

        def ssd_sample_states_prep():
            T = TS
            for i, dx in enumerate((dah, dal)):
                S.dve(lambda e, dx=dx, i=i: e.tensor_tensor(out=damb[0:T, i], in0=dx[0:T, :].unsqueeze(1).to_broadcast([T, NS, 16]),
                                                            in1=blki.unsqueeze(2).to_broadcast([T, NS, 16]), op=ALU.mult),
                      r=["dah", "dal", "cst"], w=["dam%d" % i])
                S.pe(lambda e, i=i: e.matmul(pD[:, 0:256], lhsT=onesb[0:T, :], rhs=damb[0:T, i].rearrange("p s h -> p (s h)"),
                                             start=(i == 0), stop=(i == 1)), r=["dam%d" % i, "onesb"], w=["pD", "pD2", "pD3", "pDn"])
            S.act(lambda e: e.activation(out=dtot.rearrange("p s h -> p (s h)"), in_=pD[:, 0:256], func=AF.Exp),
                  r=["pD"], w=["dtot"])
            S.barrier()
            dtotP = Dmf[:, 0:NS * 8].rearrange("p (s c) -> p s c", s=NS)
            for h2 in range(2):
                S.dve(lambda e, h2=h2: e.tensor_copy(out=dtotP[64 * h2:64 * h2 + 64],
                                                     in_=dtot[64 * h2:64 * h2 + 64].rearrange("p s (c two) -> p s c two", two=2)[:, :, :, h2]),
                      r=["dtot"], w=["dtotP"])
            h0in_b = [h0in, u]
            hout_b = [hout, hs]
            h0b_b = [hTb, ub.rearrange("p c t -> p (c t)")]
            h0Tb_b = [lrut[:, 0:512].bitcast(BF16), lrut[:, 512:1024].bitcast(BF16)]
            xdm_b = [hT[:, 0:512].bitcast(BF16), hT[:, 512:1024].bitcast(BF16)]
            pTr_b = [pC.bitcast(BF16), pD.bitcast(BF16)]
            pTk = [["pC"], ["pD"]]
            for s in range(NS):
                q = s % 2
                hi, ho, h0b, hb, xdm, pTr, tk = h0in_b[q], hout_b[q], h0b_b[q], h0Tb_b[q], xdm_b[q], pTr_b[q], pTk[q]
                S.dma(lambda e, s=s, hi=hi: e.dma_start(out=hi, in_=st_sh[s].rearrange("(c q) n -> q c n", q=128)),
                      "h0in%d" % q, w=["h0in%d" % q], q="pool")
                S.act(lambda e, hi=hi, h0b=h0b: e.activation(out=h0b, in_=hi.rearrange("p c n -> p (c n)"), func=AF.Copy),
                      r=["h0in%d" % q], w=["h0b%d" % q])
                for c in range(8):
                    S.pe(lambda e, c=c, h0b=h0b, pTr=pTr: e.transpose(out=pTr[:, c * 128:(c + 1) * 128], in_=h0b[:, c * 128:(c + 1) * 128],
                                                                      identity=identb), r=["h0b%d" % q, "identb"], w=tk)
                S.dve(lambda e, hb=hb, pTr=pTr: e.tensor_copy(out=hb, in_=pTr[:, 0:1024]), r=tk, w=["h0Tb%d" % q])
                for h in range(16):
                    S.pe(lambda e, h=h, s=s, hb=hb: e.matmul(pA[0][64 * (h % 2):64 * (h % 2) + 64, (h // 2) * TS + 4 * s:(h // 2) * TS + 4 * s + 4],
                                                             lhsT=hb[:, h * 64:(h + 1) * 64], rhs=Chs[:, h, 4 * s:4 * s + 4],
                                                             start=True, stop=True),
                         r=["h0Tb%d" % q, "Ch"], w=["pA0"])
                S.dve(lambda e, s=s, xdm=xdm: e.tensor_scalar(out=xdm[0:T, :], in0=xdd[0:T, :], scalar1=blki[:, s:s + 1], scalar2=None, op0=ALU.mult),
                      r=["xdd", "cst"], w=["xdm%d" % q])
                for c in range(8):
                    S.pe(lambda e, c=c, xdm=xdm: e.matmul(pO[:, c * 128:(c + 1) * 128], lhsT=xdm[0:T, c * 128:(c + 1) * 128],
                                                          rhs=BT[0:T, (c // 4) * 128:(c // 4 + 1) * 128], start=True, stop=True),
                         r=["xdm%d" % q, "BT"], w=["pO"])
                for c in range(8):
                    S.dve(lambda e, c=c, s=s, hi=hi, ho=ho: e.scalar_tensor_tensor(out=ho[:, c, :], in0=hi[:, c, :], scalar=dtotP[:, s, c:c + 1],
                                                                                   in1=pO[:, c * 128:(c + 1) * 128], op0=ALU.mult, op1=ALU.add),
                          r=["h0in%d" % q, "dtotP", "pO"], w=["hout%d" % q])
                S.dma(lambda e, s=s, ho=ho: e.dma_start(out=o_ssh[s].rearrange("(c q) n -> q c n", q=128), in_=ho),
                      "hout%d" % q, r=["hout%d" % q])
            S.act(lambda e: e.activation(out=pyo_sb.rearrange("p c t -> p (c t)"), in_=pA[0][:, 0:8 * TS], func=AF.Copy),
                  r=["pA0"], w=["pyo_sb"])

        S.pool(lambda e: e.memset(lxb, 0.0), w=["lx%d" % c for c in range(8)])
        S.pool(lambda e: e.memset(xcb, 0.0), w=["xc%d" % c for c in range(12)])

        def drive(g_, par):
            S.ctx = par
            try:
                next(g_)
                return True
            except StopIteration:
                return False
            finally:
                S.ctx = None

        tiles = [mixer_tile(mt, False) for mt in range(NT)]
        RATIO = 3
        par0, gP0, _ = tiles[0]
        g = gP0()
        while drive(g, par0):
            pass
        for n in range(NT):
            par, _, gS = tiles[n]
            gs = gS()
            alive_s = True
            alive_p = False
            if n + 1 < NT:
                parn, gPn, _ = tiles[n + 1]
                gp = gPn()
                alive_p = True
            while alive_s or alive_p:
                for _ in range(RATIO):
                    if alive_p:
                        alive_p = drive(gp, parn)
                if alive_s:
                    alive_s = drive(gs, par)
        S.barrier()
        if SAMP:
            pars, gPs, gSs = mixer_tile(SEQ // 128, True)
            for g in (gPs(), gSs()):
                while drive(g, pars):
                    pass

        S.barrier()
        ptr[0] = base0
        w_up_sb = b3(8, DFF)
        w_dn_sb = b3(32, D)
        if MLP:
            k_wup = load_w(w_up_sb, w_up, 8, DFF, "w_up")
            k_wdn = load_w(w_dn_sb, w_down, 32, D, "w_dn")
        T2 = 256
        xt2 = [[f32(D), f32(D)], [f32(D), f32(D)]]
        xn2 = f32(D)
        ss2 = [f32(4), f32(4)]
        rstd2 = [f32(4), f32(4)]
        mT = [b3(8, T2), b3(8, T2)]
        actb = b3(32, T2)
        rl = [f32(T2), f32(T2)]
        yout = f32(D)
        gfin_bc = f32(D)
        S.dma(lambda e: e.dma_start(out=gfin_bc, in_=gfin_d.partition_broadcast(128)), "gfin", w=["gfin"])
        pDN = [PS[:, 3072:4096], PS[:, 2048:3072]]
        pDNk = [["pO"], ["pC", "pD"]]

        def mlp_front(ti, r0, T):
            q = ti % 2
            nsub = (T + 127) // 128
            for j in range(nsub):
                Tj = min(128, T - j * 128)
                xk = "xt2_%d_%d" % (q, j)
                S.dma(lambda e, j=j, Tj=Tj: e.dma_start(out=xt2[q][j][0:Tj, :], in_=scr[r0 + j * 128:r0 + j * 128 + Tj, :]),
                      xk, r=["scr%d" % ((r0 + j * 128) // 128)], w=[xk])
                rms_rstd(xt2[q][j], Tj, xk, xn2, ss2[0], rstd2[0], "2")
                S.act(lambda e, j=j, Tj=Tj: e.activation(out=xn2[0:Tj, :], in_=xt2[q][j][0:Tj, :], func=AF.Copy, scale=rstd2[0][0:Tj, 0:1]),
                      r=[xk, "rstd2"], w=["xn2"])
                for k in range(8):
                    S.pe(lambda e, k=k, Tj=Tj: e.transpose(out=pT[:, k * 128:k * 128 + Tj], in_=xn2[0:Tj, k * 128:(k + 1) * 128],
                                                           identity=ident[0:Tj, 0:Tj]), r=["xn2", "cst"], w=["pT"])
                S.dve(lambda e, j=j, Tj=Tj: e.tensor_tensor(
                    out=mT[q][:, :, j * 128:j * 128 + Tj], in0=pT.rearrange("p (k t) -> p k t", k=8)[:, :, 0:Tj],
                    in1=P("GP", 0, 8).unsqueeze(2).to_broadcast([128, 8, Tj]), op=ALU.mult),
                    r=["pT", "pfm"], w=["mT%d" % q])
            yield
            for f in range(32):
                pa = pA[f % 2]
                for k in range(8):
                    S.pe(lambda e, k=k, f=f, pa=pa: e.matmul(pa[:, 0:T], lhsT=w_up_sb[:, k, f * 128:(f + 1) * 128], rhs=mT[q][:, k, 0:T],
                                                             start=(k == 0), stop=(k == 7)),
                         r=["mT%d" % q, "w_up_%d" % ((f * 128) // 2048)], w=["pA%d" % (f % 2)])
                S.act(lambda e, f=f, pa=pa: e.activation(out=rl[f % 2][:, 0:T], in_=pa[:, 0:T], func=AF.Relu),
                      r=["pA%d" % (f % 2)], w=["rl%d" % (f % 2)])
                S.pool(lambda e, f=f: e.tensor_tensor(out=actb[:, f, 0:T], in0=rl[f % 2][:, 0:T], in1=rl[f % 2][:, 0:T], op=ALU.mult),
                       r=["rl%d" % (f % 2)], w=["act%d" % f])
                yield

        def mlp_back(ti, r0, T):
            q = ti % 2
            nsub = (T + 127) // 128
            for f in range(32):
                for j in range(nsub):
                    Tj = min(128, T - j * 128)
                    for nb in range(2):
                        S.pe(lambda e, f=f, nb=nb, j=j, Tj=Tj: e.matmul(pDN[j][0:Tj, nb * 512:(nb + 1) * 512],
                                                                        lhsT=actb[:, f, j * 128:j * 128 + Tj],
                                                                        rhs=w_dn_sb[:, f, nb * 512:(nb + 1) * 512],
                                                                        start=(f == 0), stop=(f == 31)),
                             r=["act%d" % f, "w_dn_0"], w=pDNk[j])
                yield
            for j in range(nsub):
                Tj = min(128, T - j * 128)
                xk = "xt2_%d_%d" % (q, j)
                S.dve(lambda e, j=j, Tj=Tj: e.tensor_tensor(out=xt2[q][j][0:Tj, :], in0=pDN[j][0:Tj, :], in1=xt2[q][j][0:Tj, :], op=ALU.add),
                      r=pDNk[j] + [xk], w=[xk])
                rms_rstd(xt2[q][j], Tj, xk, yout, ss2[1], rstd2[1], "2b", jkey="yout")
                S.dve(lambda e, j=j, Tj=Tj: e.scalar_tensor_tensor(out=yout[0:Tj, :], in0=xt2[q][j][0:Tj, :], scalar=rstd2[1][0:Tj, 0:1],
                                                                   in1=gfin_bc[0:Tj, :], op0=ALU.mult, op1=ALU.mult),
                      r=[xk, "rstd2b", "gfin"], w=["yout"])
                rr = r0 + j * 128
                if rr < SEQ:
                    S.dma(lambda e, rr=rr, Tj=Tj: e.dma_start(out=y_p[rr:rr + Tj, :], in_=yout[0:Tj, :]), "yout", r=["yout"])
                else:
                    S.dma(lambda e, Tj=Tj: e.dma_start(out=y_s, in_=yout[0:Tj, :]), "yout", r=["yout"])
                yield

        if MLP:
            jobs = [(t * T2, T2) for t in range(NT * 128 // T2)]
            if SAMP:
                jobs.append((SEQ, TS))
            for _ in mlp_front(0, *jobs[0]):
                pass
            for ti in range(len(jobs)):
                gb = mlp_back(ti, *jobs[ti])
                gf = mlp_front(ti + 1, *jobs[ti + 1]) if ti + 1 < len(jobs) else iter(())
                ab = af = True
                while ab or af:
                    if ab:
                        ab = next(gb, "END") != "END"
                    if af:
                        af = next(gf, "END") != "END"

        S.emit()
    return nc


_CACHE = {}


def _consts():
    c = np.zeros((128, NCST), np.float32)
    i = np.arange(128)
    c[:, CI:CI + 128] = np.eye(128, dtype=np.float32)
    c[:, CU:CU + 128] = (i[:, None] <= i[None, :]).astype(np.float32)
    c[:, CN:CN + 128] = np.where(i[:, None] <= i[None, :], 0.0, NEG).astype(np.float32)
    c[:, CO:CO + 128] = 1.0
    j = np.arange(TS)
    same = (j[:, None] // 4) == (j[None, :] // 4)
    caus = j[:, None] <= j[None, :]
    c[0:TS, CUB:CUB + TS] = (same & caus).astype(np.float32)
    c[0:TS, CNB:CNB + TS] = np.where(same & caus, 0.0, NEG).astype(np.float32)
    c[0:TS, CBM:CBM + TS] = same.astype(np.float32)
    c[0:TS, CBI:CBI + NS] = ((j[:, None] // 4) == np.arange(NS)[None, :]).astype(np.float32)
    return c


def _fm(v, nch):
    return np.ascontiguousarray(np.asarray(v, np.float32).reshape(nch, 128).T)


def kernel(x_prompt, x_sample, state_lru_conv, state_lru_h, state_ssd_conv, state_ssd_h,
           g_mix, w_in, lru_conv_w, lru_conv_b, w_a, b_a, w_x, b_x, lam, g_lru_out,
           ssd_conv_w, ssd_conv_b, dt_bias, a_log, d_skip, g_ssd_out, w_out,
           g_mlp, w_up, w_down, g_final):
    f = lambda a: np.ascontiguousarray(np.asarray(a, np.float32))
    if "nc" not in _CACHE:
        _CACHE["nc"] = build_program()
    nc = _CACHE["nc"]
    pfm = np.zeros((128, NPAR), np.float32)
    lw = np.asarray(lru_conv_w[0], np.float32)
    pfm[:, PC["LW"]:PC["LW"] + 32] = lw.reshape(4, 8, 128).transpose(2, 1, 0).reshape(128, 32)
    pfm[:, PC["LB"]:PC["LB"] + 8] = _fm(lru_conv_b[0], 8)
    pfm[:, PC["BA"]:PC["BA"] + 8] = _fm(np.asarray(b_a[0]).reshape(-1), 8)
    pfm[:, PC["BX"]:PC["BX"] + 8] = _fm(np.asarray(b_x[0]).reshape(-1), 8)
    pfm[:, PC["LAM"]:PC["LAM"] + 8] = _fm(lam[0], 8)
    pfm[:, PC["GL"]:PC["GL"] + 8] = _fm(g_lru_out[0], 8)
    sw = np.asarray(ssd_conv_w[0], np.float32)
    pfm[:, PC["SW"]:PC["SW"] + 48] = sw.reshape(4, 12, 128).transpose(2, 1, 0).reshape(128, 48)
    pfm[:, PC["SB"]:PC["SB"] + 12] = _fm(ssd_conv_b[0], 12)
    pfm[:, PC["DS"]:PC["DS"] + 8] = _fm(np.repeat(np.asarray(d_skip[0], np.float32), 64), 8)
    pfm[:, PC["GS"]:PC["GS"] + 8] = _fm(g_ssd_out[0], 8)
    pfm[:, PC["GM"]:PC["GM"] + 8] = _fm(g_mix[0], 8)
    pfm[:, PC["GP"]:PC["GP"] + 8] = _fm(g_mlp[0], 8)
    cst = _consts()
    shared = {
        "w_in": f(w_in[0]), "w_out": f(w_out[0]), "w_up": f(w_up[0]), "w_down": f(w_down[0]),
        "w_a": f(w_a[0]), "w_x": f(w_x[0]), "pfm": pfm, "cst": cst,
        "dt_bias": f(dt_bias[0]), "a_log": f(a_log[0]), "g_final": f(g_final),
    }
    in_maps = []
    for b in range(NCORES):
        sl = slice(NS * b, NS * (b + 1))
        m = dict(shared)
        m["xp"] = f(x_prompt[b])
        m["xs"] = f(np.asarray(x_sample[sl]).reshape(TS, D))
        m["st_lc"] = f(np.asarray(state_lru_conv[0, sl]).reshape(NS * 3, D))
        m["st_lh"] = f(state_lru_h[0, sl])
        m["st_sc"] = f(np.asarray(state_ssd_conv[0, sl]).reshape(NS * 3, XBC))
        m["st_sh"] = f(np.asarray(state_ssd_h[0, sl]).reshape(NS, 1024, 128))
        in_maps.append(m)
    res = run_bass_kernel_spmd(nc, in_maps, core_ids=list(range(NCORES)))
    R = res.results
    cat = lambda k: np.stack([np.asarray(R[b][k], np.float32) for b in range(NCORES)])
    y_prompt = cat("y_p")
    y_sample = cat("y_s").reshape(NCORES * NS, 4, D)
    p_lc = cat("o_plc")[None]
    p_lh = cat("o_plh").reshape(NCORES, D)[None]
    p_sc = cat("o_psc")[None]
    p_sh = cat("o_psh").reshape(NCORES, 16, 64, 128)[None]
    s_lc = cat("o_slc").reshape(NCORES * NS, 3, D)[None]
    s_lh = cat("o_slh").reshape(NCORES * NS, D)[None]
    s_sc = cat("o_ssc").reshape(NCORES * NS, 3, XBC)[None]
    s_sh = cat("o_ssh").reshape(NCORES * NS, 16, 64, 128)[None]
    return (y_prompt, y_sample, p_lc, p_lh, p_sc, p_sh, s_lc, s_lh, s_sc, s_sh)
```
